# Optimizing a Trainium2 kernel written in Bass

```python
import math
import jax
import jax.numpy as jnp
from jax import lax
import numpy as np

D_MODEL = 1024
BATCH = 8
SEQ = 2048
DEPTH = 4

GRID_W = 64
CTX_LEN = 256
N_MIXERS = 4
D_FF = 4 * D_MODEL
ALPHA = (2 * DEPTH) ** 0.25
BETA = (8 * DEPTH) ** -0.25
NORM_EPS = 1e-6
ROPE_BASE = 10000.0
QUERY_BLOCK = 128
CHUNK = 64

MLA_HEADS = D_MODEL // 64
MLA_NOPE = 64
MLA_ROPE = 32
MLA_V = 64
MLA_KV_LORA = 4 * MLA_V
MLA_Q_LORA = 3 * MLA_KV_LORA

DIFF_HEAD_DIM = 64
DIFF_HEADS = D_MODEL // (2 * DIFF_HEAD_DIM)
DIFF_WIDTH = DIFF_HEADS * 2 * DIFF_HEAD_DIM

GLA_HEADS = 4
GLA_KEY_DIM = D_MODEL // 2
GLA_VALUE_DIM = D_MODEL
GLA_DK = GLA_KEY_DIM // GLA_HEADS
GLA_DV = GLA_VALUE_DIM // GLA_HEADS
GLA_GATE_RANK = 16
GLA_GATE_NORM = 16.0
GLA_IN_WIDTH = 2 * GLA_KEY_DIM + 2 * GLA_VALUE_DIM + 2 * GLA_GATE_RANK

GDN_HEAD_DIM = 128
GDN_KEY_HEADS = D_MODEL // GDN_HEAD_DIM
GDN_VALUE_HEADS = 2 * GDN_KEY_HEADS
GDN_KEY_DIM = GDN_KEY_HEADS * GDN_HEAD_DIM
GDN_VALUE_DIM = GDN_VALUE_HEADS * GDN_HEAD_DIM
GDN_CONV = 5
GDN_CONV_CH = 2 * GDN_KEY_DIM + GDN_VALUE_DIM
GDN_IN_WIDTH = GDN_CONV_CH + GDN_VALUE_DIM + 4 * GDN_VALUE_HEADS

N_A = (DEPTH + 3) // 4
N_B = (DEPTH + 2) // 4
N_C = (DEPTH + 1) // 4
N_D = DEPTH // 4

kernel_name = "hybrid_interleaved_diffusion_trunk"


def _cuts(*sizes):
    out, acc = [], 0
    for s in sizes:
        acc += s
        out.append(acc)
    return out


def _layer_norm(x, g, b):
    xf = x.astype(jnp.float32)
    mu = jnp.mean(xf, -1, keepdims=True)
    var = jnp.mean(jnp.square(xf - mu), -1, keepdims=True)
    return ((xf - mu) * lax.rsqrt(var + NORM_EPS) * g + b).astype(x.dtype)


def _rms_norm(x, g):
    xf = x.astype(jnp.float32)
    y = xf * lax.rsqrt(jnp.mean(jnp.square(xf), -1, keepdims=True) + NORM_EPS)
    return (y * g).astype(x.dtype)


def _l2_norm(x):
    xf = x.astype(jnp.float32)
    return (xf * lax.rsqrt(jnp.sum(jnp.square(xf), -1, keepdims=True) + NORM_EPS)).astype(x.dtype)


def _heads(t, n_heads):
    b, n, _ = t.shape
    return t.reshape(b, n, n_heads, -1).transpose(0, 2, 1, 3)


def _merge_heads(t):
    b, h, n, d = t.shape
    return t.transpose(0, 2, 1, 3).reshape(b, n, h * d)


def _axial_rope_tables(n, dim):
    rows = n // GRID_W
    row = jnp.repeat(jnp.arange(rows, dtype=jnp.float32), GRID_W)
    col = jnp.tile(jnp.arange(GRID_W, dtype=jnp.float32), rows)
    n_freq = dim // 4
    inv_freq = ROPE_BASE ** (-jnp.arange(n_freq, dtype=jnp.float32) / n_freq)
    ang = jnp.concatenate([row[:, None] * inv_freq, col[:, None] * inv_freq], -1)
    return jnp.cos(ang), jnp.sin(ang)


def _apply_rope(x, cos, sin):
    xf = x.astype(jnp.float32).reshape(x.shape[:-1] + (-1, 2))
    x1, x2 = xf[..., 0], xf[..., 1]
    out = jnp.stack([x1 * cos - x2 * sin, x1 * sin + x2 * cos], -1)
    return out.reshape(x.shape).astype(x.dtype)


def _softmax_probs(q, k, scale):
    s = jnp.einsum("bhqd,bhkd->bhqk", q, k).astype(jnp.float32) * scale
    return jax.nn.softmax(s, axis=-1)


def _softmax_attend(q, k, v, scale):
    p = _softmax_probs(q, k, scale)
    return jnp.einsum("bhqk,bhkd->bhqd", p.astype(v.dtype), v)


def _sweep_query_blocks(fn, qs):
    def split(a):
        b, h, n, d = a.shape
        return a.reshape(b, h, n // QUERY_BLOCK, QUERY_BLOCK, d).transpose(2, 0, 1, 3, 4)
    out = lax.map(fn, tuple(split(a) for a in qs))
    nb, b, h, qb, dv = out.shape
    return out.transpose(1, 2, 0, 3, 4).reshape(b, h, nb * qb, dv)


def _centred_dwconv(t, w):
    pad = w.shape[0] // 2
    return lax.conv_general_dilated(t, w[:, None, :], window_strides=(1,), padding=[(pad, pad)],
                                    dimension_numbers=("NWC", "WIO", "NWC"),
                                    feature_group_count=t.shape[-1])


def _gla_chunked(q, k, v, log_a, s0):
    f32 = jnp.float32
    b, h, n, _ = q.shape
    dv = v.shape[-1]
    nc = n // CHUNK
    q, k, v, log_a = (a.astype(f32).reshape(b, h, nc, CHUNK, a.shape[-1]) for a in (q, k, v, log_a))
    cum = jnp.cumsum(log_a, axis=3)
    last = cum[:, :, :, -1:, :]
    q_dec = q * jnp.exp(cum)
    k_inv = k * jnp.exp(-cum)
    k_end = k * jnp.exp(last - cum)
    idx = jnp.arange(CHUNK)
    a_intra = jnp.where(idx[:, None] >= idx[None, :],
                        jnp.einsum("bhncd,bhnsd->bhncs", q_dec, k_inv), 0.0)
    o_intra = jnp.einsum("bhncs,bhnsv->bhncv", a_intra, v)

    def step(s, inp):
        q_c, k_c, v_c, d_c = inp
        o = jnp.einsum("bhcd,bhdv->bhcv", q_c, s)
        s = s * d_c[..., None] + jnp.einsum("bhcd,bhcv->bhdv", k_c, v_c)
        return s, o

    xs = tuple(jnp.moveaxis(a, 2, 0) for a in (q_dec, k_end, v, jnp.exp(last[:, :, :, 0, :])))
    s_fin, o_inter = lax.scan(step, s0.astype(f32), xs)
    o = o_intra + jnp.moveaxis(o_inter, 0, 2)
    return o.reshape(b, h, n, dv), s_fin


def _gated_delta_chunked(q, k, v, g, beta, s0):
    f32 = jnp.float32
    b, h, n, _ = q.shape
    dv = v.shape[-1]
    nc = n // CHUNK
    q, k, v = (a.astype(f32).reshape(b, h, nc, CHUNK, a.shape[-1]) for a in (q, k, v))
    g, beta = (a.astype(f32).reshape(b, h, nc, CHUNK) for a in (g, beta))
    gc = jnp.cumsum(g, -1)
    idx = jnp.arange(CHUNK)
    incl = idx[:, None] >= idx[None, :]
    strict = idx[:, None] > idx[None, :]
    gamma = jnp.exp(jnp.where(incl, gc[..., :, None] - gc[..., None, :], -jnp.inf))
    kb = k * beta[..., None]
    lmat = jnp.where(strict, jnp.einsum("bhncd,bhnsd->bhncs", kb, k) * gamma, 0.0)
    eye = jnp.broadcast_to(jnp.eye(CHUNK, dtype=f32), lmat.shape)
    t_inv = lax.linalg.triangular_solve(lmat, eye, left_side=True, lower=True,
                                        unit_diagonal=True)
    u_vals = t_inv @ (v * beta[..., None])
    w_vals = t_inv @ (kb * jnp.exp(gc)[..., None])
    q_dec = q * jnp.exp(gc)[..., None]
    a_intra = jnp.einsum("bhncd,bhnsd->bhncs", q, k) * gamma
    g_last = gc[..., -1]
    k_dec = k * jnp.exp(g_last[..., None] - gc)[..., None]
    d_last = jnp.exp(g_last)

    def step(s, inp):
        u_c, w_c, q_c, a_c, k_c, d_c = inp
        v_new = u_c - w_c @ s
        o = q_c @ s + a_c @ v_new
        s = s * d_c[..., None, None] + jnp.swapaxes(k_c, -1, -2) @ v_new
        return s, o

    xs = tuple(jnp.moveaxis(a, 2, 0) for a in (u_vals, w_vals, q_dec, a_intra, k_dec, d_last))
    s_fin, o = lax.scan(step, s0.astype(f32), xs)
    return jnp.moveaxis(o, 0, 2).reshape(b, h, n, dv), s_fin


def _flip(a):
    return jnp.flip(a, axis=2)


def _mla_mixer(u, uc, w_in, q_norm, kv_norm, w_qb, w_kvb, w_out, with_ctx):
    def project(t, positioned):
        b, n, _ = t.shape
        cq, ckv, k_rope = jnp.split(t @ w_in, _cuts(MLA_Q_LORA, MLA_KV_LORA), axis=-1)
        k_rope = k_rope[:, None]
        q = _heads(_rms_norm(cq, q_norm) @ w_qb, MLA_HEADS)
        kv = _heads(_rms_norm(ckv, kv_norm) @ w_kvb, MLA_HEADS)
        q_nope, q_rope = q[..., :MLA_NOPE], q[..., MLA_NOPE:]
        if positioned:
            cos, sin = _axial_rope_tables(n, MLA_ROPE)
            q_rope = _apply_rope(q_rope, cos, sin)
            k_rope = _apply_rope(k_rope, cos, sin)
        q = jnp.concatenate([q_nope, q_rope], -1)
        k = jnp.concatenate([kv[..., :MLA_NOPE],
                             jnp.broadcast_to(k_rope, (b, MLA_HEADS, n, MLA_ROPE))], -1)
        return q, k, kv[..., MLA_NOPE:]

    scale = (MLA_NOPE + MLA_ROPE) ** -0.5
    q, k, v = project(u, True)
    qc, kc, vc = project(uc, False)
    k_all = jnp.concatenate([k, kc], axis=2)
    v_all = jnp.concatenate([v, vc], axis=2)
    o = _sweep_query_blocks(lambda blk: _softmax_attend(blk[0], k_all, v_all, scale), (q,))
    y = _merge_heads(o) @ w_out
    yc = _merge_heads(_softmax_attend(qc, kc, vc, scale)) @ w_out if with_ctx else None
    return y, yc


def _diff_mixer(u, uc, w_in, lam_q1, lam_k1, lam_q2, lam_k2, subln, w_out, lambda_init, with_ctx):
    hd = DIFF_HEAD_DIM

    def project(t, positioned):
        n = t.shape[1]
        q, k, v = (_heads(a, DIFF_HEADS) for a in jnp.split(t @ w_in, 3, axis=-1))
        if positioned:
            cos, sin = _axial_rope_tables(n, hd)
            rot = lambda a: jnp.concatenate([_apply_rope(a[..., :hd], cos, sin),
                                             _apply_rope(a[..., hd:], cos, sin)], -1)
            q, k = rot(q), rot(k)
        return q[..., :hd], q[..., hd:], k[..., :hd], k[..., hd:], v

    lam = (jnp.exp(jnp.sum(lam_q1.astype(jnp.float32) * lam_k1))
           - jnp.exp(jnp.sum(lam_q2.astype(jnp.float32) * lam_k2)) + lambda_init)
    scale = hd ** -0.5

    def attend(q1, q2, k1, k2, v):
        p = _softmax_probs(q1, k1, scale) - lam * _softmax_probs(q2, k2, scale)
        o = jnp.einsum("bhqk,bhkd->bhqd", p.astype(v.dtype), v)
        return _rms_norm(o, subln) * (1.0 - lambda_init)

    q1, q2, k1, k2, v = project(u, True)
    q1c, q2c, k1c, k2c, vc = project(uc, False)
    k1a = jnp.concatenate([k1, k1c], axis=2)
    k2a = jnp.concatenate([k2, k2c], axis=2)
    va = jnp.concatenate([v, vc], axis=2)
    o = _sweep_query_blocks(lambda blk: attend(blk[0], blk[1], k1a, k2a, va), (q1, q2))
    y = _merge_heads(o) @ w_out
    yc = _merge_heads(attend(q1c, q2c, k1c, k2c, vc)) @ w_out if with_ctx else None
    return y, yc


def _gla_mixer(u, uc, w_in, gate_w_fwd, gate_b_fwd, gate_w_bwd, gate_b_bwd, norm_g, w_out, with_ctx):
    cuts = _cuts(GLA_KEY_DIM, GLA_KEY_DIM, GLA_VALUE_DIM, GLA_VALUE_DIM, GLA_GATE_RANK)

    def project(t):
        q, k, v, g, r_f, r_b = jnp.split(t @ w_in, cuts, axis=-1)
        la_f = jax.nn.log_sigmoid((r_f @ gate_w_fwd + gate_b_fwd).astype(jnp.float32)) / GLA_GATE_NORM
        la_b = jax.nn.log_sigmoid((r_b @ gate_w_bwd + gate_b_bwd).astype(jnp.float32)) / GLA_GATE_NORM
        return (_heads(q, GLA_HEADS) * GLA_DK ** -0.5, _heads(k, GLA_HEADS), _heads(v, GLA_HEADS), g,
                _heads(la_f, GLA_HEADS), _heads(la_b, GLA_HEADS))

    q, k, v, g, la_f, la_b = project(u)
    qc, kc, vc, gc, lac_f, lac_b = project(uc)
    zero = jnp.zeros((u.shape[0], GLA_HEADS, GLA_DK, GLA_DV), jnp.float32)
    oc_f, s_f = _gla_chunked(qc, kc, vc, lac_f, zero)
    oc_b, s_b = _gla_chunked(_flip(qc), _flip(kc), _flip(vc), _flip(lac_b), zero)
    o_f, _ = _gla_chunked(q, k, v, la_f, s_f)
    o_b, _ = _gla_chunked(_flip(q), _flip(k), _flip(v), _flip(la_b), s_b)

    def out(o, gate):
        return (_merge_heads(_rms_norm(o, norm_g)).astype(gate.dtype) * jax.nn.silu(gate)) @ w_out

    y = out(o_f + _flip(o_b), g)
    yc = out(oc_f + _flip(oc_b), gc) if with_ctx else None
    return y, yc


def _gdn_mixer(u, uc, w_in, conv_w, a_log_fwd, dt_bias_fwd, a_log_bwd, dt_bias_bwd, norm_g, w_out,
               with_ctx):
    hv = GDN_VALUE_HEADS
    cuts = _cuts(GDN_CONV_CH, GDN_VALUE_DIM, hv, hv, hv)
    rep = GDN_VALUE_HEADS // GDN_KEY_HEADS

    def decay(a, a_log, dt_bias):
        return (-jnp.exp(a_log) * jax.nn.softplus(a.astype(jnp.float32) + dt_bias)).transpose(0, 2, 1)

    def project(t):
        qkv, z, b_f, b_b, a_f, a_b = jnp.split(t @ w_in, cuts, axis=-1)
        qkv = jax.nn.silu(_centred_dwconv(qkv, conv_w))
        q, k, v = jnp.split(qkv, _cuts(GDN_KEY_DIM, GDN_KEY_DIM), axis=-1)
        q = jnp.repeat(_l2_norm(_heads(q, GDN_KEY_HEADS)), rep, axis=1) * GDN_HEAD_DIM ** -0.5
        k = jnp.repeat(_l2_norm(_heads(k, GDN_KEY_HEADS)), rep, axis=1)
        v = _heads(v, GDN_VALUE_HEADS)
        beta_f = jax.nn.sigmoid(b_f.astype(jnp.float32)).transpose(0, 2, 1)
        beta_b = jax.nn.sigmoid(b_b.astype(jnp.float32)).transpose(0, 2, 1)
        return (q, k, v, z, decay(a_f, a_log_fwd, dt_bias_fwd), beta_f,
                decay(a_b, a_log_bwd, dt_bias_bwd), beta_b)

    q, k, v, z, g_f, be_f, g_b, be_b = project(u)
    qc, kc, vc, zc, gc_f, bec_f, gc_b, bec_b = project(uc)
    zero = jnp.zeros((u.shape[0], GDN_VALUE_HEADS, GDN_HEAD_DIM, GDN_HEAD_DIM), jnp.float32)
    oc_f, s_f = _gated_delta_chunked(qc, kc, vc, gc_f, bec_f, zero)
    oc_b, s_b = _gated_delta_chunked(_flip(qc), _flip(kc), _flip(vc), _flip(gc_b), _flip(bec_b), zero)
    o_f, _ = _gated_delta_chunked(q, k, v, g_f, be_f, s_f)
    o_b, _ = _gated_delta_chunked(_flip(q), _flip(k), _flip(v), _flip(g_b), _flip(be_b), s_b)

    def out(o, gate):
        return (_merge_heads(_rms_norm(o, norm_g)).astype(gate.dtype) * jax.nn.silu(gate)) @ w_out

    y = out(o_f + _flip(o_b), z)
    yc = out(oc_f + _flip(oc_b), zc) if with_ctx else None
    return y, yc


def _sq_relu_mlp(u, w1, w2):
    return jnp.square(jax.nn.relu(u @ w1)) @ w2


def setup_inputs(seed: int = 0) -> dict:
    key = jax.random.key(seed)
    ks = iter(jax.random.split(key, 48))
    f32 = jnp.float32
    D = D_MODEL

    def normal(shape, std):
        return std * jax.random.normal(next(ks), shape, f32)

    def gain(shape):
        return 1.0 + normal(shape, 0.02)

    def a_log(n):
        return jnp.log(jax.random.uniform(next(ks), (n, GDN_VALUE_HEADS), f32, 1.0, 16.0))

    def dt_bias(n):
        dt = jnp.exp(jax.random.uniform(next(ks), (n, GDN_VALUE_HEADS), f32,
                                        math.log(1e-3), math.log(1e-1)))
        return dt + jnp.log(-jnp.expm1(-dt))

    return {
        "x": normal((BATCH, SEQ, D), 1.0),
        "c": normal((BATCH, D), 1.0),
        "ctx": normal((BATCH, CTX_LEN, D), 1.0),
        "c_ctx": normal((D,), 1.0),
        "ada_w": normal((DEPTH, D, 6 * D), D ** -0.5),
        "ada_b": normal((DEPTH, 6 * D), 0.02),
        "ln1_g": gain((DEPTH, D)),
        "ln1_b": normal((DEPTH, D), 0.02),
        "ln2_g": gain((DEPTH, D)),
        "ln2_b": normal((DEPTH, D), 0.02),
        "mlp_w1": normal((DEPTH, D, D_FF), D ** -0.5),
        "mlp_w2": normal((DEPTH, D_FF, D), BETA * D_FF ** -0.5),
        "mla_w_in": normal((N_A, D, MLA_Q_LORA + MLA_KV_LORA + MLA_ROPE), D ** -0.5),
        "mla_q_norm": gain((N_A, MLA_Q_LORA)),
        "mla_kv_norm": gain((N_A, MLA_KV_LORA)),
        "mla_w_qb": normal((N_A, MLA_Q_LORA, MLA_HEADS * (MLA_NOPE + MLA_ROPE)), MLA_Q_LORA ** -0.5),
        "mla_w_kvb": normal((N_A, MLA_KV_LORA, MLA_HEADS * (MLA_NOPE + MLA_V)), MLA_KV_LORA ** -0.5),
        "mla_w_out": normal((N_A, MLA_HEADS * MLA_V, D), BETA * (MLA_HEADS * MLA_V) ** -0.5),
        "diff_w_in": normal((N_B, D, 3 * DIFF_WIDTH), D ** -0.5),
        "diff_lambda_q1": normal((N_B, DIFF_HEAD_DIM), 0.1),
        "diff_lambda_k1": normal((N_B, DIFF_HEAD_DIM), 0.1),
        "diff_lambda_q2": normal((N_B, DIFF_HEAD_DIM), 0.1),
        "diff_lambda_k2": normal((N_B, DIFF_HEAD_DIM), 0.1),
        "diff_subln": gain((N_B, 2 * DIFF_HEAD_DIM)),
        "diff_w_out": normal((N_B, DIFF_WIDTH, D), BETA * DIFF_WIDTH ** -0.5),
        "gla_w_in": normal((N_C, D, GLA_IN_WIDTH), D ** -0.5),
        "gla_gate_w_fwd": normal((N_C, GLA_GATE_RANK, GLA_KEY_DIM), GLA_GATE_RANK ** -0.5),
        "gla_gate_b_fwd": normal((N_C, GLA_KEY_DIM), 0.02),
        "gla_gate_w_bwd": normal((N_C, GLA_GATE_RANK, GLA_KEY_DIM), GLA_GATE_RANK ** -0.5),
        "gla_gate_b_bwd": normal((N_C, GLA_KEY_DIM), 0.02),
        "gla_norm": gain((N_C, GLA_DV)),
        "gla_w_out": normal((N_C, GLA_VALUE_DIM, D), BETA * GLA_VALUE_DIM ** -0.5),
        "gdn_w_in": normal((N_D, D, GDN_IN_WIDTH), D ** -0.5),
        "gdn_conv_w": normal((N_D, GDN_CONV, GDN_CONV_CH), GDN_CONV ** -0.5),
        "gdn_a_log_fwd": a_log(N_D),
        "gdn_dt_bias_fwd": dt_bias(N_D),
        "gdn_a_log_bwd": a_log(N_D),
        "gdn_dt_bias_bwd": dt_bias(N_D),
        "gdn_norm": gain((N_D, GDN_HEAD_DIM)),
        "gdn_w_out": normal((N_D, GDN_VALUE_DIM, D), BETA * GDN_VALUE_DIM ** -0.5),
    }


def reference(x, c, ctx, c_ctx, ada_w, ada_b, ln1_g, ln1_b, ln2_g, ln2_b, mlp_w1, mlp_w2,
              mla_w_in, mla_q_norm, mla_kv_norm, mla_w_qb, mla_w_kvb, mla_w_out,
              diff_w_in, diff_lambda_q1, diff_lambda_k1, diff_lambda_q2, diff_lambda_k2, diff_subln,
              diff_w_out,
              gla_w_in, gla_gate_w_fwd, gla_gate_b_fwd, gla_gate_w_bwd, gla_gate_b_bwd, gla_norm,
              gla_w_out,
              gdn_w_in, gdn_conv_w, gdn_a_log_fwd, gdn_dt_bias_fwd, gdn_a_log_bwd, gdn_dt_bias_bwd,
              gdn_norm, gdn_w_out):
    s_lat = jax.nn.silu(c)
    s_ctx = jax.nn.silu(c_ctx)
    h, hc = x, ctx
    for i in range(DEPTH):
        with_ctx = i < DEPTH - 1
        kind, j = i % N_MIXERS, i // N_MIXERS
        sh1, sc1, g1, sh2, sc2, g2 = jnp.split((s_lat @ ada_w[i] + ada_b[i])[:, None, :], 6, axis=-1)
        csh1, csc1, cg1, csh2, csc2, cg2 = jnp.split(s_ctx @ ada_w[i] + ada_b[i], 6, axis=-1)
        u = h * (1.0 + sc1) + sh1
        uc = hc * (1.0 + csc1) + csh1
        if kind == 0:
            y, yc = _mla_mixer(u, uc, mla_w_in[j], mla_q_norm[j], mla_kv_norm[j], mla_w_qb[j],
                               mla_w_kvb[j], mla_w_out[j], with_ctx)
        elif kind == 1:
            lambda_init = 0.8 - 0.6 * math.exp(-0.3 * i)
            y, yc = _diff_mixer(u, uc, diff_w_in[j], diff_lambda_q1[j], diff_lambda_k1[j],
                                diff_lambda_q2[j], diff_lambda_k2[j], diff_subln[j], diff_w_out[j],
                                lambda_init, with_ctx)
        elif kind == 2:
            y, yc = _gla_mixer(u, uc, gla_w_in[j], gla_gate_w_fwd[j], gla_gate_b_fwd[j],
                               gla_gate_w_bwd[j], gla_gate_b_bwd[j], gla_norm[j], gla_w_out[j],
                               with_ctx)
        else:
            y, yc = _gdn_mixer(u, uc, gdn_w_in[j], gdn_conv_w[j], gdn_a_log_fwd[j], gdn_dt_bias_fwd[j],
                               gdn_a_log_bwd[j], gdn_dt_bias_bwd[j], gdn_norm[j], gdn_w_out[j],
                               with_ctx)
        h = _layer_norm(ALPHA * h + g1 * y, ln1_g[i], ln1_b[i])
        h = _layer_norm(ALPHA * h + g2 * _sq_relu_mlp(h * (1.0 + sc2) + sh2, mlp_w1[i], mlp_w2[i]),
                        ln2_g[i], ln2_b[i])
        if with_ctx:
            hc = _layer_norm(ALPHA * hc + cg1 * yc, ln1_g[i], ln1_b[i])
            hc = _layer_norm(ALPHA * hc + cg2 * _sq_relu_mlp(hc * (1.0 + csc2) + csh2, mlp_w1[i],
                                                              mlp_w2[i]), ln2_g[i], ln2_b[i])
    return h
```

```python
import math
import numpy as np
import concourse.bass as bass
import concourse.mybir as mybir
from concourse.bass_utils import run_bass_kernel_spmd
from contextlib import ExitStack

F32 = mybir.dt.float32
BF16 = mybir.dt.bfloat16
ALU = mybir.AluOpType
AF = mybir.ActivationFunctionType

DEPTH = 4
D = 1024
TL = 2048
TC = 256
T = TL + TC
ALPHA = (2 * DEPTH) ** 0.25
EPS = 1e-6
EPS_LN = EPS / (ALPHA * ALPHA)
TT = [(0, 512, 0), (512, 512, 0), (1024, 512, 0), (1536, 512, 0), (2048, 256, 1)]


class Res:
    __slots__ = ("name", "w", "r", "dsem", "dcnt")

    def __init__(self, name=""):
        self.name = name
        self.w = None
        self.r = {}
        self.dsem = None
        self.dcnt = 0


class Sched:
    EPOCH = 30000

    def __init__(self, nc, es):
        self.nc = nc
        self.es = es
        self.eng = {"pe": nc.tensor, "dve": nc.vector, "act": nc.scalar,
                    "pool": nc.gpsimd, "sp": nc.sync}
        self.cnt = {e: 0 for e in self.eng}
        self.cursem = {e: None for e in self.eng}
        self.last = {e: None for e in self.eng}
        self.seen = {e: {} for e in self.eng}
        self.nsem = 0
        self.ninst = 0
        self.owners = []
        self.out_events = []

    def newsem(self, name):
        self.nsem += 1
        return self.es.enter_context(self.nc.semaphore(f"{name}_{self.nsem}"))

    def _wait(self, e, ev):
        sem, val, _ = ev
        k = id(sem)
        if self.seen[e].get(k, 0) >= val:
            return
        self.eng[e].wait_ge(sem, val)
        self.seen[e][k] = val

    def _deps(self, e, reads, writes, same_ok):
        for r in reads:
            if r.w is not None and not (same_ok and r.w[2] == e):
                self._wait(e, r.w)
        for w in writes:
            if w.w is not None and not (same_ok and w.w[2] == e):
                self._wait(e, w.w)
            for ev in w.r.values():
                if not (same_ok and ev[2] == e):
                    self._wait(e, ev)

    def op(self, e, fn, reads=(), writes=(), same_ok=False):
        self._deps(e, reads, writes, same_ok)
        ins = fn(self.eng[e])
        if self.cnt[e] % self.EPOCH == 0:
            self.cursem[e] = self.newsem("c" + e)
        self.cnt[e] += 1
        val = (self.cnt[e] - 1) % self.EPOCH + 1
        sem = self.cursem[e]
        ins.then_inc(sem, 1)
        ev = (sem, val, e)
        self.last[e] = ev
        for r in reads:
            r.r[id(sem)] = ev
        for w in writes:
            w.w = ev
            w.r = {}
        self.ninst += 1
        return ev

    def dma(self, q, pairs, owner, reads=(), writes=(), **kw):
        self._deps(q, reads, writes, False)
        if owner.dsem is None:
            owner.dsem = self.newsem("d")
            self.owners.append(owner)
        if owner.dcnt > 0:
            self._wait(q, (owner.dsem, owner.dcnt, "dma"))
        for (o, i) in pairs:
            self.eng[q].dma_start(out=o, in_=i, **kw).then_inc(owner.dsem, 16)
            owner.dcnt += 16
        ev = (owner.dsem, owner.dcnt, "dma")
        for r in reads:
            r.r[id(owner.dsem)] = ev
        for w in writes:
            w.w = ev
            w.r = {}
        self.ninst += len(pairs)
        return ev

    def barrier(self):
        evs = [ev for ev in self.last.values() if ev is not None]
        evs += [(o.dsem, o.dcnt, "dma") for o in self.owners if o.dcnt > 0]
        for e in ("pe", "dve", "act", "pool", "sp"):
            for ev in evs:
                if ev[2] != e:
                    self._wait(e, ev)

    def finish(self):
        for ev in self.out_events:
            self._wait("sp", ev)


class KB:
    def __init__(self, nc, es, depth_run=DEPTH, mixers=True, dbg=False):
        self.nc, self.es = nc, es
        self.S = Sched(nc, es)
        self.depth_run = depth_run
        self.mixers = mixers
        self.dbg = dbg
        self.din = {}
        self._n = 0

    def sb(self, shape, dt, es=None, name=None):
        self._n += 1
        return (es or self.es).enter_context(self.nc.sbuf_tensor(name or f"t{self._n}", list(shape), dt))

    def dram_in(self, name, shape):
        t = self.nc.dram_tensor(name, list(shape), F32, kind="ExternalInput").ap()
        self.din[name] = t
        return t

    def bank(self):
        i = self.bank_i
        self.bank_i = (i + 1) % self.nrr
        return self.banks[i], self.bank_res[i]

    def declare(self):
        di = self.dram_in
        di("x", [TL, D]); di("ctx", [TC, D]); di("c", [8, 128]); di("c_ctx", [8, 128])
        di("ada_w", [DEPTH, D, 6 * D]); di("ada_b", [DEPTH, 48, 128])
        for n in ("ln1_g", "ln1_b", "ln2_g", "ln2_b"):
            di(n, [DEPTH, 8, 128])
        di("mlp_w1", [DEPTH, D, 4 * D]); di("mlp_w2", [DEPTH, 4 * D, D])
        di("mla_w_in", [D, 1056]); di("mla_q_norm", [6, 128]); di("mla_kv_norm", [2, 128])
        di("mla_w_kvb", [256, 2048]); di("mla_w_out", [D, D])
        di("mla_wqx", [768, 2048]); di("mla_wkr", [D, 256])
        di("mla_qtab", [128, T]); di("mla_kta", [128, T]); di("mla_ktb", [128, T])
        di("diff_wx", [D, 16 * 384]); di("diff_wv", [D, D]); di("diff_w_out", [D, D])
        di("diff_qtab", [128, T]); di("diff_kta", [128, T]); di("diff_ktb", [128, T])
        di("diff_lam", [64, 4]); di("diff_subln", [1, 128])
        di("gla_w_in", [D, 3104]); di("gla_gw", [2, 16, 512]); di("gla_gb", [2, 4, 128])
        di("gla_norm", [2, 128]); di("gla_w_out", [D, D])
        di("gdn_w_in", [D, 6208]); di("gdn_wg", [D, 64]); di("gdn_conv", [5, 32, 128])
        di("gdn_hc", [1, 64]); di("gdn_norm", [1, 128]); di("gdn_w_out", [2 * D, D])
        self.out = self.nc.dram_tensor("out", [TL, D], F32, kind="ExternalOutput").ap()
        if self.dbg:
            self.out_c = self.nc.dram_tensor("out_c", [TC, D], F32, kind="ExternalOutput").ap()

        nc = self.nc
        self.banks = [self.es.enter_context(nc.psum_tensor(f"bank{i}", [128, 512], F32)) for i in range(8)]
        self.bank_res = [Res(f"bank{i}") for i in range(8)]
        self.bank_i = 0
        self.nrr = 6
        self.hT = self.sb([128, 8, T], F32, name="hT")
        self.r_h = [Res(f"h{t}") for t in range(len(TT))]
        self.uT = self.sb([128, 8, T], BF16, name="uT")
        self.r_u = [Res(f"u{t}") for t in range(len(TT))]
        self.ident = self.sb([128, 128], F32, name="ident"); self.r_ident = Res("ident")
        self.identb = self.sb([128, 128], BF16, name="identb")
        self.ones = self.sb([128, 128], F32, name="ones")
        self.identr = self.sb([128, 128], F32, name="identr")
        self.sT = self.sb([128, 8, 2], F32, name="sT"); self.r_sT = Res("sT")
        self.mod = [self.sb([128, 48, 2], F32, name=f"mod{i}") for i in range(DEPTH)]
        self.r_mod = [Res(f"mod{i}") for i in range(DEPTH)]
        self.lnp = self.sb([128, 4, DEPTH, 8], F32, name="lnp"); self.r_lnp = Res("lnp")
        self.adab = self.sb([128, DEPTH, 48], F32, name="adab"); self.r_adab = Res("adab")
        self.vst = self.sb([64, 128], F32, name="vst"); self.r_vst = Res("vst")
        self.NW = 3
        self.wslot = [self.sb([128, 4096], BF16, name=f"wslot{i}") for i in range(self.NW)]
        self.r_wslot = [Res(f"wslot{i}") for i in range(self.NW)]
        self.w_i = 0

    def wnext(self):
        i = self.w_i
        self.w_i = (i + 1) % self.NW
        return self.wslot[i], self.r_wslot[i]

    def load_cols(self, src, n, dst, r_dst):
        S = self.S
        S.dma("sp", [(self.vst[0:n, :], src)], self.r_vst, writes=[self.r_vst])
        bk, rb = self.bank()
        S.op("pe", lambda e: e.transpose(bk[:, 0:n], self.vst[0:n, :], self.ident[0:n, 0:n]),
             reads=[self.r_vst, self.r_ident], writes=[rb], same_ok=True)
        S.op("dve", lambda e: e.tensor_copy(dst, bk[:, 0:n]), reads=[rb], writes=[r_dst])

    def setup(self):
        S, nc = self.S, self.nc
        S.op("pool", lambda e: e.memset(self.ident[:], 0.0), writes=[self.r_ident])
        S.op("pool", lambda e: e.affine_select(out=self.ident[:], in_=self.ident[:], compare_op=ALU.not_equal,
                                              fill=1.0, base=0, pattern=[[-1, 128]], channel_multiplier=1),
             reads=[self.r_ident], writes=[self.r_ident])
        S.op("dve", lambda e: e.tensor_copy(self.identb[:], self.ident[:]), reads=[self.r_ident], writes=[self.r_ident])
        S.op("dve", lambda e: e.memset(self.ones[:], 1.0), writes=[self.r_ident])
        S.op("dve", lambda e: e.tensor_copy(self.identr[:].bitcast(mybir.dt.float32r), self.ident[:]), reads=[self.r_ident], writes=[self.r_ident])
        for k, n in enumerate(("ln1_g", "ln1_b", "ln2_g", "ln2_b")):
            for i in range(DEPTH):
                self.load_cols(self.din[n][i], 8, self.lnp[:, k, i, :], self.r_lnp)
        for i in range(DEPTH):
            self.load_cols(self.din["ada_b"][i], 48, self.adab[:, i, :], self.r_adab)
        self.load_cols(self.din["c"], 8, self.sT[:, :, 0], self.r_sT)
        self.load_cols(self.din["c_ctx"], 8, self.sT[:, :, 1], self.r_sT)
        S.op("act", lambda e: e.activation(out=self.sT[:], in_=self.sT[:], func=AF.Silu), reads=[self.r_sT], writes=[self.r_sT])
        with ExitStack() as ph:
            xs = [self.sb([128, D], F32, es=ph) for _ in range(2)]
            r_xs = [Res("xs0"), Res("xs1")]
            for t in range(18):
                src = self.din["x"][t * 128:(t + 1) * 128, :] if t < 16 else self.din["ctx"][(t - 16) * 128:(t - 15) * 128, :]
                st, rs = xs[t % 2], r_xs[t % 2]
                S.dma("sp", [(st[:], src)], rs, writes=[rs])
                ti = min(t // 4, 4)
                for g in range(2):
                    bk, rb = self.bank()
                    for j in range(4):
                        S.op("pe", lambda e: e.transpose(bk[:, j * 128:(j + 1) * 128], st[:, (g * 4 + j) * 128:(g * 4 + j + 1) * 128], self.ident[:]),
                             reads=[rs, self.r_ident], writes=[rb], same_ok=True)
                    dst = self.hT[:, g * 4:(g + 1) * 4, t * 128:(t + 1) * 128]
                    srcp = bk[:, 0:512].rearrange("p (j n) -> p j n", j=4)
                    if g == 0:
                        S.op("dve", lambda e: e.tensor_copy(dst, srcp), reads=[rb], writes=[self.r_h[ti]])
                    else:
                        S.op("act", lambda e: e.copy(dst, srcp), reads=[rb], writes=[self.r_h[ti]])
            S.barrier()

    def mods(self, i):
        S = self.S
        aw = self.din["ada_w"][i]
        bk, rb = self.bank()
        with ExitStack() as ph:
            stg = [self.sb([128, 8, 256], F32, es=ph) for _ in range(2)]
            r_stg = [Res("as0"), Res("as1")]
            for blk in range(24):
                st, rs = stg[blk % 2], r_stg[blk % 2]
                S.dma("sp", [(st[:], aw[:, blk * 256:(blk + 1) * 256].rearrange("(k p) n -> p k n", p=128))], rs, writes=[rs])
                for cc in range(2):
                    c = blk * 2 + cc
                    for kc in range(8):
                        S.op("pe", lambda e: e.matmul(bk[:, c * 2:(c + 1) * 2], lhsT=st[:, kc, cc * 128:(cc + 1) * 128], rhs=self.sT[:, kc, :],
                                                      start=(kc == 0), stop=(kc == 7)),
                             reads=[rs, self.r_sT], writes=[rb], same_ok=True)
            m, rm = self.mod[i], self.r_mod[i]
            pv = bk[:, 0:96].rearrange("p (c l) -> p c l", l=2)
            for l in range(2):
                S.op("dve", lambda e: e.tensor_tensor(out=m[:, :, l], in0=pv[:, :, l], in1=self.adab[:, i, :], op=ALU.add),
                     reads=[rb, self.r_adab], writes=[rm])
            for c0 in (8, 32):
                S.op("dve", lambda e: e.tensor_scalar(out=m[:, c0:c0 + 8, :], in0=m[:, c0:c0 + 8, :], scalar1=1.0, scalar2=None, op0=ALU.add),
                     reads=[rm], writes=[rm])
            for c0 in (16, 40):
                S.op("dve", lambda e: e.tensor_scalar(out=m[:, c0:c0 + 8, :], in0=m[:, c0:c0 + 8, :], scalar1=1.0 / ALPHA, scalar2=None, op0=ALU.mult),
                     reads=[rm], writes=[rm])
            S.barrier()

    def mcol(self, i, which, kc, lc):
        return self.mod[i][:, which * 8 + kc, lc:lc + 1]

    def modulate_all(self, i, sub):
        S = self.S
        for ti, (t0, n, lc) in enumerate(TT):
            for kc in range(8):
                S.op("dve", lambda e: e.tensor_scalar(out=self.uT[:, kc, t0:t0 + n], in0=self.hT[:, kc, t0:t0 + n],
                                                      scalar1=self.mcol(i, 3 * sub + 1, kc, lc), scalar2=self.mcol(i, 3 * sub, kc, lc),
                                                      op0=ALU.mult, op1=ALU.add),
                     reads=[self.r_h[ti], self.r_mod[i]], writes=[self.r_u[ti]])

    def layer_norm(self, i, sub, nxt):
        S = self.S
        with ExitStack() as ph:
            sq = [self.sb([128, 512], F32, es=ph) for _ in range(2)]; r_sq = [Res(), Res()]
            mean = self.sb([128, 512], F32, es=ph); r_mean = Res()
            msq = self.sb([128, 512], F32, es=ph); r_msq = Res()
            rstd = self.sb([128, 512], F32, es=ph); r_rstd = Res()
            tmp = [self.sb([128, 512], F32, es=ph) for _ in range(2)]; r_tmp = [Res(), Res()]
            epsc = self.sb([128, 1], F32, es=ph); r_eps = Res()
            S.op("pool", lambda e: e.memset(epsc[:], EPS_LN), writes=[r_eps])
            for ti, (t0, n, lc) in enumerate(TT):
                rh = self.r_h[ti]
                b1, rb1 = self.bank()
                b2, rb2 = self.bank()
                for kc in range(8):
                    z = self.hT[:, kc, t0:t0 + n]
                    s_, rs_ = sq[kc % 2], r_sq[kc % 2]
                    S.op("act", lambda e: e.activation(out=s_[:, 0:n], in_=z, func=AF.Square), reads=[rh], writes=[rs_])
                    S.op("pe", lambda e: e.matmul(b1[:, 0:n], lhsT=self.ones[:], rhs=z, start=(kc == 0), stop=(kc == 7)),
                         reads=[rh, self.r_ident], writes=[rb1], same_ok=True)
                    S.op("pe", lambda e: e.matmul(b2[:, 0:n], lhsT=self.ones[:], rhs=s_[:, 0:n], start=(kc == 0), stop=(kc == 7)),
                         reads=[rs_, self.r_ident], writes=[rb2], same_ok=True)
                S.op("act", lambda e: e.activation(out=mean[:, 0:n], in_=b1[:, 0:n], func=AF.Copy, scale=1.0 / D), reads=[rb1], writes=[r_mean])
                S.op("pool", lambda e: e.tensor_tensor(out=msq[:, 0:n], in0=mean[:, 0:n], in1=mean[:, 0:n], op=ALU.mult), reads=[r_mean], writes=[r_msq])
                S.op("dve", lambda e: e.scalar_tensor_tensor(out=rstd[:, 0:n], in0=b2[:, 0:n], scalar=1.0 / D, in1=msq[:, 0:n],
                                                             op0=ALU.mult, op1=ALU.subtract), reads=[rb2, r_msq], writes=[r_rstd])
                S.op("act", lambda e: e.activation(out=rstd[:, 0:n], in_=rstd[:, 0:n], func=AF.Sqrt, bias=epsc[:, 0:1], scale=1.0),
                     reads=[r_rstd, r_eps], writes=[r_rstd])
                S.op("dve", lambda e: e.reciprocal(out=rstd[:, 0:n], in_=rstd[:, 0:n]), reads=[r_rstd], writes=[r_rstd])
                for kc in range(8):
                    z = self.hT[:, kc, t0:t0 + n]
                    tp, rtp = tmp[kc % 2], r_tmp[kc % 2]
                    S.op("pool", lambda e: e.tensor_tensor(out=tp[:, 0:n], in0=z, in1=mean[:, 0:n], op=ALU.subtract), reads=[rh, r_mean], writes=[rtp])
                    S.op("dve", lambda e: e.tensor_tensor(out=tp[:, 0:n], in0=tp[:, 0:n], in1=rstd[:, 0:n], op=ALU.mult), reads=[rtp, r_rstd], writes=[rtp])
                    S.op("act", lambda e: e.activation(out=z, in_=tp[:, 0:n], func=AF.Identity,
                                                       bias=self.lnp[:, 2 * sub + 1, i, kc:kc + 1], scale=self.lnp[:, 2 * sub, i, kc:kc + 1]),
                         reads=[rtp, self.r_lnp], writes=[rh])
                    if nxt is not None:
                        ni, nsub = nxt
                        S.op("dve", lambda e: e.tensor_scalar(out=self.uT[:, kc, t0:t0 + n], in0=z,
                                                              scalar1=self.mcol(ni, 3 * nsub + 1, kc, lc), scalar2=self.mcol(ni, 3 * nsub, kc, lc),
                                                              op0=ALU.mult, op1=ALU.add),
                             reads=[rh, self.r_mod[ni]], writes=[self.r_u[ti]])
            S.barrier()

    def mlp(self, i):
        S = self.S
        w1 = self.din["mlp_w1"][i]
        w2 = self.din["mlp_w2"][i]
        with ExitStack() as ph:
            ab = [self.sb([128, 4, 512], BF16, es=ph) for _ in range(2)]; r_ab = [Res(), Res()]
            rl = [self.sb([128, 512], BF16, es=ph) for _ in range(3)]; r_rl = [Res(), Res(), Res()]
            rli = 0
            step = 0
            for j in range(8):
                wa, r_wa = self.wnext()
                wb, r_wb = self.wnext()
                S.dma("pool", [(wa[:].rearrange("p (k n) -> p k n", k=8), w1[:, j * 512:(j + 1) * 512].rearrange("(k p) n -> p k n", p=128))],
                      r_wa, writes=[r_wa])
                S.dma("pool", [(wb[:].rearrange("p (k n) -> p k n", k=4), w2[j * 512:(j + 1) * 512, :].rearrange("(k p) n -> p k n", p=128))],
                      r_wb, writes=[r_wb])
                wav = wa[:].rearrange("p (k n) -> p k n", k=8)
                wbv = wb[:].rearrange("p (k n) -> p k n", k=4)
                for ti, (t0, n, lc) in enumerate(TT):
                    a_, r_a = ab[step % 2], r_ab[step % 2]
                    step += 1
                    for hc in range(4):
                        bk, rb = self.bank()
                        for kc in range(8):
                            S.op("pe", lambda e: e.matmul(bk[:, 0:n], lhsT=wav[:, kc, hc * 128:(hc + 1) * 128], rhs=self.uT[:, kc, t0:t0 + n],
                                                          start=(kc == 0), stop=(kc == 7)),
                                 reads=[r_wa, self.r_u[ti]], writes=[rb], same_ok=True)
                        r_, rr_ = rl[rli % 3], r_rl[rli % 3]
                        rli += 1
                        S.op("act", lambda e: e.activation(out=r_[:, 0:n], in_=bk[:, 0:n], func=AF.Relu), reads=[rb], writes=[rr_])
                        S.op("pool", lambda e: e.tensor_tensor(out=a_[:, hc, 0:n], in0=r_[:, 0:n], in1=r_[:, 0:n], op=ALU.mult),
                             reads=[rr_], writes=[r_a])
                    for oc in range(8):
                        bk, rb = self.bank()
                        for kc in range(4):
                            S.op("pe", lambda e: e.matmul(bk[:, 0:n], lhsT=wbv[:, kc, oc * 128:(oc + 1) * 128], rhs=a_[:, kc, 0:n],
                                                          start=(kc == 0), stop=(kc == 3)),
                                 reads=[r_wb, r_a], writes=[rb], same_ok=True)
                        hz = self.hT[:, oc, t0:t0 + n]
                        S.op("dve", lambda e: e.scalar_tensor_tensor(out=hz, in0=bk[:, 0:n], scalar=self.mcol(i, 5, oc, lc), in1=hz,
                                                                     op0=ALU.mult, op1=ALU.add),
                             reads=[rb, self.r_h[ti], self.r_mod[i]], writes=[self.r_h[ti]])
            S.barrier()

    def store_out(self):
        S = self.S
        with ExitStack() as ph:
            os_ = [self.sb([128, D], F32, es=ph) for _ in range(2)]
            r_os = [Res("os0"), Res("os1")]
            nt = 18 if self.dbg else 16
            for t in range(nt):
                st, rs = os_[t % 2], r_os[t % 2]
                ti = min(t // 4, 4)
                for g in range(2):
                    bk, rb = self.bank()
                    for j in range(4):
                        S.op("pe", lambda e: e.transpose(bk[:, j * 128:(j + 1) * 128], self.hT[:, g * 4 + j, t * 128:(t + 1) * 128], self.ident[:]),
                             reads=[self.r_h[ti], self.r_ident], writes=[rb], same_ok=True)
                    if g == 0:
                        S.op("dve", lambda e: e.tensor_copy(st[:, 0:512], bk[:, 0:512]), reads=[rb], writes=[rs])
                    else:
                        S.op("act", lambda e: e.copy(st[:, 512:1024], bk[:, 0:512]), reads=[rb], writes=[rs])
                dst = self.out[t * 128:(t + 1) * 128, :] if t < 16 else self.out_c[(t - 16) * 128:(t - 15) * 128, :]
                ev = S.dma("sp", [(dst, st[:])], rs, reads=[rs])
                S.out_events.append(ev)
            S.finish()

    def build(self):
        self.declare()
        self.setup()
        self.mods(0)
        self.modulate_all(0, 0)
        for i in range(self.depth_run):
            if i + 1 < DEPTH:
                self.mods(i + 1)
            if self.mixers:
                self.mixer(i)
            self.layer_norm(i, 0, (i, 1))
            self.mlp(i)
            self.layer_norm(i, 1, (i + 1, 0) if i + 1 < DEPTH else None)
        self.store_out()


    def mixer(self, i):
        if i == 0:
            self.mla(i)
        elif i == 1:
            self.diff(i)
        elif i == 2:
            self.gla(i)
        elif i == 3:
            self.gdn(i)

    def mla(self, i):
        S = self.S
        SCALE = 96.0 ** -0.5
        with ExitStack() as ph:
            kp = [self.sb([128, T], BF16, es=ph) for _ in range(2)]; r_kp = [Res("kp0"), Res("kp1")]
            qtab = self.sb([128, T], F32, es=ph); r_qtab = Res("qtab")
            opad = self.sb([128, 2, 128], BF16, es=ph); r_opad = Res("opad")
            nrm = self.sb([128, 8], F32, es=ph); r_nrm = Res("nrm")
            epsc = self.sb([128, 1], F32, es=ph); r_eps = Res("eps")
            S.dma("sp", [(qtab[:], self.din["mla_qtab"])], r_qtab, writes=[r_qtab])
            S.op("pool", lambda e: e.memset(opad[:], 0.0), writes=[r_opad])
            S.op("pool", lambda e: e.memset(opad[:, 0, 0:64], 1.0), reads=[r_opad], writes=[r_opad])
            S.op("pool", lambda e: e.memset(opad[:, 1, 64:128], 1.0), reads=[r_opad], writes=[r_opad])
            S.op("pool", lambda e: e.memset(epsc[:], EPS), writes=[r_eps])
            self.load_cols(self.din["mla_q_norm"], 6, nrm[:, 0:6], r_nrm)
            self.load_cols(self.din["mla_kv_norm"], 2, nrm[:, 6:8], r_nrm)
            with ExitStack() as p1:
                raw = self.sb([128, 8, 512], F32, es=p1); r_raw = Res("raw")
                sq = [self.sb([128, 512], F32, es=p1) for _ in range(2)]; r_sq = [Res(), Res()]
                rs = self.sb([128, 2, 512], F32, es=p1); r_rs = Res("rs")
                kta = self.sb([128, 512], F32, es=p1); r_kta = Res("kta")
                ktb = self.sb([128, 512], F32, es=p1); r_ktb = Res("ktb")
                t1 = self.sb([128, 512], F32, es=p1); r_t1 = Res("t1")
                t2 = self.sb([128, 512], F32, es=p1); r_t2 = Res("t2")
                wi = []
                for blk in range(2):
                    w_, r_w = self.wnext()
                    S.dma("pool", [(w_[:].rearrange("p (k n) -> p k n", k=8),
                                    self.din["mla_w_in"][:, blk * 512:(blk + 1) * 512].rearrange("(k p) n -> p k n", p=128))], r_w, writes=[r_w])
                    wi.append((w_[:].rearrange("p (k n) -> p k n", k=8), r_w))
                w_, r_wk = self.wnext()
                wkr = w_[:, 0:2048].rearrange("p (k n) -> p k n", k=8)
                S.dma("pool", [(wkr, self.din["mla_wkr"].rearrange("(k p) n -> p k n", p=128))], r_wk, writes=[r_wk])
                for ti, (t0, n, lc) in enumerate(TT):
                    ru = self.r_u[ti]
                    S.dma("sp", [(kta[:, 0:n], self.din["mla_kta"][:, t0:t0 + n])], r_kta, writes=[r_kta])
                    S.dma("sp", [(ktb[:, 0:n], self.din["mla_ktb"][:, t0:t0 + n])], r_ktb, writes=[r_ktb])
                    bA, rbA = self.bank()
                    bB, rbB = self.bank()
                    for kc in range(8):
                        S.op("pe", lambda e: e.matmul(bA[:, 0:n], lhsT=wkr[:, kc, 0:128], rhs=self.uT[:, kc, t0:t0 + n], start=(kc == 0), stop=(kc == 7)),
                             reads=[r_wk, ru], writes=[rbA], same_ok=True)
                    for kc in range(8):
                        S.op("pe", lambda e: e.matmul(bB[:, 0:n], lhsT=wkr[:, kc, 128:256], rhs=self.uT[:, kc, t0:t0 + n], start=(kc == 0), stop=(kc == 7)),
                             reads=[r_wk, ru], writes=[rbB], same_ok=True)
                    S.op("dve", lambda e: e.tensor_tensor(out=t1[64:128, 0:n], in0=bA[64:128, 0:n], in1=kta[64:128, 0:n], op=ALU.mult),
                         reads=[rbA, r_kta], writes=[r_t1])
                    S.op("dve", lambda e: e.tensor_tensor(out=t2[64:128, 0:n], in0=bB[64:128, 0:n], in1=ktb[64:128, 0:n], op=ALU.mult),
                         reads=[rbB, r_ktb], writes=[r_t2])
                    S.op("pool", lambda e: e.tensor_tensor(out=kp[0][64:128, t0:t0 + n], in0=t1[64:128, 0:n], in1=t2[64:128, 0:n], op=ALU.add),
                         reads=[r_t1, r_t2], writes=[r_kp[0]])
                    S.op("pool", lambda e: e.tensor_copy(kp[1][64:128, t0:t0 + n], kp[0][64:128, t0:t0 + n]), reads=[r_kp[0]], writes=[r_kp[1]])
                    for oc in range(8):
                        wv, r_w = wi[oc // 4]
                        bk, rb = self.bank()
                        for kc in range(8):
                            S.op("pe", lambda e: e.matmul(bk[:, 0:n], lhsT=wv[:, kc, (oc % 4) * 128:(oc % 4 + 1) * 128], rhs=self.uT[:, kc, t0:t0 + n],
                                                          start=(kc == 0), stop=(kc == 7)),
                                 reads=[r_w, ru], writes=[rb], same_ok=True)
                        if oc % 2 == 0:
                            S.op("dve", lambda e: e.tensor_copy(raw[:, oc, 0:n], bk[:, 0:n]), reads=[rb], writes=[r_raw])
                        else:
                            S.op("act", lambda e: e.copy(raw[:, oc, 0:n], bk[:, 0:n]), reads=[rb], writes=[r_raw])
                    bq, rbq = self.banks[6], self.bank_res[6]
                    bkv, rbkv = self.banks[7], self.bank_res[7]
                    for oc in range(8):
                        s_, rs_ = sq[oc % 2], r_sq[oc % 2]
                        S.op("act", lambda e: e.activation(out=s_[:, 0:n], in_=raw[:, oc, 0:n], func=AF.Square), reads=[r_raw], writes=[rs_])
                        if oc < 6:
                            S.op("pe", lambda e: e.matmul(bq[:, 0:n], lhsT=self.ones[:], rhs=s_[:, 0:n], start=(oc == 0), stop=(oc == 5)),
                                 reads=[rs_, self.r_ident], writes=[rbq], same_ok=True)
                        else:
                            S.op("pe", lambda e: e.matmul(bkv[:, 0:n], lhsT=self.ones[:], rhs=s_[:, 0:n], start=(oc == 6), stop=(oc == 7)),
                                 reads=[rs_, self.r_ident], writes=[rbkv], same_ok=True)
                    for g, (bb, rbb, dim) in enumerate(((bq, rbq, 768.0), (bkv, rbkv, 256.0))):
                        S.op("act", lambda e: e.activation(out=rs[:, g, 0:n], in_=bb[:, 0:n], func=AF.Sqrt, bias=epsc[:, 0:1], scale=1.0 / dim),
                             reads=[rbb, r_eps], writes=[r_rs])
                        S.op("dve", lambda e: e.reciprocal(out=rs[:, g, 0:n], in_=rs[:, g, 0:n]), reads=[r_rs], writes=[r_rs])
                    for oc in range(8):
                        g = 0 if oc < 6 else 1
                        S.op("dve", lambda e: e.scalar_tensor_tensor(out=self.uT[:, oc, t0:t0 + n], in0=raw[:, oc, 0:n], scalar=nrm[:, oc:oc + 1],
                                                                     in1=rs[:, g, 0:n], op0=ALU.mult, op1=ALU.mult),
                             reads=[r_raw, r_nrm, r_rs], writes=[ru])
                S.barrier()
            with ExitStack() as p2:
                qp = [self.sb([128, T], BF16, es=p2) for _ in range(2)]; r_qp = [Res("qp0"), Res("qp1")]
                vp = [self.sb([128, 18, 128], BF16, es=p2) for _ in range(2)]; r_vp = [Res("vp0"), Res("vp1")]
                pt = [self.sb([128, 512], BF16, es=p2) for _ in range(4)]; r_pt = [Res() for _ in range(4)]
                rden = self.sb([128, 512], F32, es=p2); r_rden = Res("rden")
                opr = [self.sb([128, 512], BF16, es=p2) for _ in range(2)]; r_opr = [Res(), Res()]
                wo = [self.sb([128, D], BF16, es=p2) for _ in range(2)]; r_wo = [Res("wo0"), Res("wo1")]
                for par in range(2):
                    S.op("pool", lambda e: e.memset(vp[par][:], 0.0), writes=[r_vp[par]])
                pti = 0
                num, r_num = self.banks[6], self.bank_res[6]
                den, r_den = self.banks[7], self.bank_res[7]
                wq = wkv = None
                for pair in range(8):
                    S.dma("pool", [(wo[pair % 2][:], self.din["mla_w_out"][pair * 128:(pair + 1) * 128, :])], r_wo[pair % 2], writes=[r_wo[pair % 2]])
                    for par in range(2):
                        h = pair * 2 + par
                        hl = h % 4
                        if hl == 0:
                            w_, r_wq = self.wnext()
                            wq = w_[:, 0:3072].rearrange("p (k n) -> p k n", k=6)
                            wkv = w_[:, 3072:4096].rearrange("p (k n) -> p k n", k=2)
                            S.dma("pool", [(wq, self.din["mla_wqx"][:, h * 128:(h + 4) * 128].rearrange("(k p) n -> p k n", p=128)),
                                           (wkv, self.din["mla_w_kvb"][:, h * 128:(h + 4) * 128].rearrange("(k p) n -> p k n", p=128))],
                                  r_wq, writes=[r_wq])
                        for ti, (t0, n, lc) in enumerate(TT):
                            ru = self.r_u[ti]
                            bk, rb = self.bank()
                            for kc in range(6):
                                S.op("pe", lambda e: e.matmul(bk[:, 0:n], lhsT=wq[:, kc, hl * 128:(hl + 1) * 128], rhs=self.uT[:, kc, t0:t0 + n],
                                                              start=(kc == 0), stop=(kc == 5)),
                                     reads=[r_wq, ru], writes=[rb], same_ok=True)
                            S.op("dve", lambda e: e.tensor_tensor(out=qp[par][:, t0:t0 + n], in0=bk[:, 0:n], in1=qtab[:, t0:t0 + n], op=ALU.mult),
                                 reads=[rb, r_qtab], writes=[r_qp[par]])
                            bk, rb = self.bank()
                            for kc in range(2):
                                S.op("pe", lambda e: e.matmul(bk[0:64, 0:n], lhsT=wkv[:, kc, hl * 128:hl * 128 + 64], rhs=self.uT[:, 6 + kc, t0:t0 + n],
                                                              start=(kc == 0), stop=(kc == 1)),
                                     reads=[r_wq, ru], writes=[rb], same_ok=True)
                            S.op("dve", lambda e: e.tensor_copy(kp[par][0:64, t0:t0 + n], bk[0:64, 0:n]), reads=[rb], writes=[r_kp[par]])
                        for g0 in range(0, 18, 8):
                            ng = min(8, 18 - g0)
                            bk, rb = self.bank()
                            for jt in range(ng):
                                kt = g0 + jt
                                for kc in range(2):
                                    S.op("pe", lambda e: e.matmul(bk[:, jt * 64:(jt + 1) * 64], lhsT=self.uT[:, 6 + kc, kt * 128:(kt + 1) * 128],
                                                                  rhs=wkv[:, kc, hl * 128 + 64:hl * 128 + 128], start=(kc == 0), stop=(kc == 1)),
                                         reads=[r_wq] + self.r_u, writes=[rb], same_ok=True)
                            S.op("dve", lambda e: e.tensor_copy(vp[par][:, g0:g0 + ng, par * 64:par * 64 + 64],
                                                                bk[:, 0:ng * 64].rearrange("p (j d) -> p j d", d=64)), reads=[rb], writes=[r_vp[par]])
                    for ti, (t0, n, lc) in enumerate(TT):
                        kts = list(range(18)) if lc == 0 else [16, 17]
                        for par in range(2):
                            for ki, kt in enumerate(kts):
                                bk, rb = self.bank()
                                S.op("pe", lambda e: e.matmul(bk[:, 0:n], lhsT=kp[par][:, kt * 128:(kt + 1) * 128], rhs=qp[par][:, t0:t0 + n], start=True, stop=True),
                                     reads=[r_kp[par], r_qp[par]], writes=[rb], same_ok=True)
                                p_, rp_ = pt[pti % 4], r_pt[pti % 4]
                                pti += 1
                                S.op("act", lambda e: e.activation(out=p_[:, 0:n], in_=bk[:, 0:n], func=AF.Exp, scale=SCALE), reads=[rb], writes=[rp_])
                                first = (par == 0 and ki == 0)
                                last = (par == 1 and ki == len(kts) - 1)
                                S.op("pe", lambda e: e.matmul(num[:, 0:n], lhsT=vp[par][:, kt, :], rhs=p_[:, 0:n], start=first, stop=last),
                                     reads=[r_vp[par], rp_], writes=[r_num], same_ok=True)
                                S.op("pe", lambda e: e.matmul(den[:, 0:n], lhsT=opad[:, par, :], rhs=p_[:, 0:n], start=first, stop=last),
                                     reads=[r_opad, rp_], writes=[r_den], same_ok=True)
                        S.op("dve", lambda e: e.reciprocal(out=rden[:, 0:n], in_=den[:, 0:n]), reads=[r_den], writes=[r_rden])
                        o_, ro_ = opr[ti % 2], r_opr[ti % 2]
                        S.op("dve", lambda e: e.tensor_tensor(out=o_[:, 0:n], in0=num[:, 0:n], in1=rden[:, 0:n], op=ALU.mult),
                             reads=[r_num, r_rden], writes=[ro_])
                        for oc in range(8):
                            bk, rb = self.bank()
                            S.op("pe", lambda e: e.matmul(bk[:, 0:n], lhsT=wo[pair % 2][:, oc * 128:(oc + 1) * 128], rhs=o_[:, 0:n], start=True, stop=True),
                                 reads=[r_wo[pair % 2], ro_], writes=[rb], same_ok=True)
                            hz = self.hT[:, oc, t0:t0 + n]
                            S.op("dve", lambda e: e.scalar_tensor_tensor(out=hz, in0=bk[:, 0:n], scalar=self.mcol(i, 2, oc, lc), in1=hz,
                                                                         op0=ALU.mult, op1=ALU.add),
                                 reads=[rb, self.r_h[ti], self.r_mod[i]], writes=[self.r_h[ti]])
                S.barrier()

    def diff(self, i):
        S = self.S
        SCALE = 64.0 ** -0.5
        lam_init = 0.8 - 0.6 * math.exp(-0.3 * i)
        self.nrr = 4
        self.bank_i = 0
        with ExitStack() as ph:
            qp = [self.sb([128, T], BF16, es=ph) for _ in range(2)]; r_qp = [Res("qp0"), Res("qp1")]
            kp = [self.sb([128, T], BF16, es=ph) for _ in range(2)]; r_kp = [Res("kp0"), Res("kp1")]
            vp = self.sb([128, 18, 128], BF16, es=ph); r_vp = Res("vp")
            qtab = self.sb([128, 512], F32, es=ph); r_qtab = Res("qtab")
            kta = self.sb([128, 512], F32, es=ph); r_kta = Res("kta")
            ktb = self.sb([128, 512], F32, es=ph); r_ktb = Res("ktb")
            t1 = self.sb([128, 512], F32, es=ph); r_t1 = Res("t1")
            t2 = self.sb([128, 512], F32, es=ph); r_t2 = Res("t2")
            pt = [self.sb([128, 512], BF16, es=ph) for _ in range(4)]; r_pt = [Res() for _ in range(4)]
            rd = self.sb([128, 2, 512], F32, es=ph); r_rd = Res("rd")
            onb = self.sb([128, 128], BF16, es=ph); r_onb = Res("onb")
            on_ = [self.sb([128, 512], BF16, es=ph) for _ in range(2)]; r_on = [Res(), Res()]
            wo = [self.sb([128, D], BF16, es=ph) for _ in range(2)]; r_wo = [Res("wo0"), Res("wo1")]
            lam = self.sb([128, 4], F32, es=ph); r_lam = Res("lam")
            lv = self.sb([64, 4], F32, es=ph); r_lv = Res("lv")
            sub = self.sb([128, 1], F32, es=ph); r_sub = Res("sub")
            epsc = self.sb([128, 1], F32, es=ph); r_eps = Res("eps")
            S.op("pool", lambda e: e.memset(epsc[:], EPS), writes=[r_eps])
            S.op("pool", lambda e: e.memset(onb[:], 1.0), writes=[r_onb])
            S.dma("sp", [(lv[:], self.din["diff_lam"])], r_lv, writes=[r_lv])
            S.op("dve", lambda e: e.tensor_tensor(out=lv[:, 0:1], in0=lv[:, 0:1], in1=lv[:, 1:2], op=ALU.mult), reads=[r_lv], writes=[r_lv])
            S.op("dve", lambda e: e.tensor_tensor(out=lv[:, 1:2], in0=lv[:, 2:3], in1=lv[:, 3:4], op=ALU.mult), reads=[r_lv], writes=[r_lv])
            bk, rb = self.bank()
            S.op("pe", lambda e: e.matmul(bk[:, 0:2], lhsT=self.ones[0:64, :], rhs=lv[:, 0:2], start=True, stop=True),
                 reads=[r_lv, self.r_ident], writes=[rb], same_ok=True)
            S.op("act", lambda e: e.activation(out=lam[:, 0:2], in_=bk[:, 0:2], func=AF.Exp), reads=[rb], writes=[r_lam])
            S.op("dve", lambda e: e.scalar_tensor_tensor(out=lam[:, 2:3], in0=lam[:, 1:2], scalar=-lam_init, in1=lam[:, 0:1], op0=ALU.add, op1=ALU.subtract),
                 reads=[r_lam], writes=[r_lam])
            self.load_cols(self.din["diff_subln"], 1, sub[:, 0:1], r_sub)
            S.op("dve", lambda e: e.tensor_scalar(out=sub[:], in0=sub[:], scalar1=1.0 - lam_init, scalar2=None, op0=ALU.mult), reads=[r_sub], writes=[r_sub])
            nums = [(self.banks[4], self.bank_res[4]), (self.banks[5], self.bank_res[5])]
            dens = [(self.banks[6], self.bank_res[6]), (self.banks[7], self.bank_res[7])]
            pti = 0
            for h in range(8):
                S.dma("pool", [(wo[h % 2][:], self.din["diff_w_out"][h * 128:(h + 1) * 128, :])], r_wo[h % 2], writes=[r_wo[h % 2]])
                wv = None
                for m in range(2):
                    mi = h * 2 + m
                    w_, r_w = self.wnext()
                    wx = w_[:, 0:3072].rearrange("p (k n) -> p k n", k=8)
                    prs = [(wx, self.din["diff_wx"][:, mi * 384:(mi + 1) * 384].rearrange("(k p) n -> p k n", p=128))]
                    if m == 0:
                        wv = w_[:, 3072:4096].rearrange("p (k n) -> p k n", k=8)
                        r_wv = r_w
                        prs.append((wv, self.din["diff_wv"][:, h * 128:(h + 1) * 128].rearrange("(k p) n -> p k n", p=128)))
                    S.dma("pool", prs, r_w, writes=[r_w])
                    for ti, (t0, n, lc) in enumerate(TT):
                        ru = self.r_u[ti]
                        S.dma("sp", [(qtab[:, 0:n], self.din["diff_qtab"][:, t0:t0 + n])], r_qtab, writes=[r_qtab])
                        S.dma("sp", [(kta[:, 0:n], self.din["diff_kta"][:, t0:t0 + n])], r_kta, writes=[r_kta])
                        S.dma("sp", [(ktb[:, 0:n], self.din["diff_ktb"][:, t0:t0 + n])], r_ktb, writes=[r_ktb])
                        bq, rbq = self.bank()
                        for kc in range(8):
                            S.op("pe", lambda e: e.matmul(bq[:, 0:n], lhsT=wx[:, kc, 0:128], rhs=self.uT[:, kc, t0:t0 + n], start=(kc == 0), stop=(kc == 7)),
                                 reads=[r_w, ru], writes=[rbq], same_ok=True)
                        S.op("dve", lambda e: e.tensor_tensor(out=qp[m][:, t0:t0 + n], in0=bq[:, 0:n], in1=qtab[:, 0:n], op=ALU.mult),
                             reads=[rbq, r_qtab], writes=[r_qp[m]])
                        bA, rbA = self.bank()
                        for kc in range(8):
                            S.op("pe", lambda e: e.matmul(bA[:, 0:n], lhsT=wx[:, kc, 128:256], rhs=self.uT[:, kc, t0:t0 + n], start=(kc == 0), stop=(kc == 7)),
                                 reads=[r_w, ru], writes=[rbA], same_ok=True)
                        bB, rbB = self.bank()
                        for kc in range(8):
                            S.op("pe", lambda e: e.matmul(bB[:, 0:n], lhsT=wx[:, kc, 256:384], rhs=self.uT[:, kc, t0:t0 + n], start=(kc == 0), stop=(kc == 7)),
                                 reads=[r_w, ru], writes=[rbB], same_ok=True)
                        S.op("dve", lambda e: e.tensor_tensor(out=t1[:, 0:n], in0=bA[:, 0:n], in1=kta[:, 0:n], op=ALU.mult), reads=[rbA, r_kta], writes=[r_t1])
                        S.op("dve", lambda e: e.tensor_tensor(out=t2[:, 0:n], in0=bB[:, 0:n], in1=ktb[:, 0:n], op=ALU.mult), reads=[rbB, r_ktb], writes=[r_t2])
                        S.op("pool", lambda e: e.tensor_tensor(out=kp[m][:, t0:t0 + n], in0=t1[:, 0:n], in1=t2[:, 0:n], op=ALU.add),
                             reads=[r_t1, r_t2], writes=[r_kp[m]])
                for g0 in range(0, 18, 4):
                    ng = min(4, 18 - g0)
                    bk, rb = self.bank()
                    for jt in range(ng):
                        kt = g0 + jt
                        for kc in range(8):
                            S.op("pe", lambda e: e.matmul(bk[:, jt * 128:(jt + 1) * 128], lhsT=self.uT[:, kc, kt * 128:(kt + 1) * 128],
                                                          rhs=wv[:, kc, :], start=(kc == 0), stop=(kc == 7)),
                                 reads=[r_wv] + self.r_u, writes=[rb], same_ok=True)
                    S.op("act", lambda e: e.copy(vp[:, g0:g0 + ng, :], bk[:, 0:ng * 128].rearrange("p (j d) -> p j d", d=128)), reads=[rb], writes=[r_vp])
                for ti, (t0, n, lc) in enumerate(TT):
                    kts = list(range(18)) if lc == 0 else [16, 17]
                    for m in range(2):
                        num, r_num = nums[m]
                        den, r_den = dens[m]
                        for ki, kt in enumerate(kts):
                            bk, rb = self.bank()
                            S.op("pe", lambda e: e.matmul(bk[:, 0:n], lhsT=kp[m][:, kt * 128:(kt + 1) * 128], rhs=qp[m][:, t0:t0 + n], start=True, stop=True),
                                 reads=[r_kp[m], r_qp[m]], writes=[rb], same_ok=True)
                            p_, rp_ = pt[pti % 4], r_pt[pti % 4]
                            pti += 1
                            S.op("act", lambda e: e.activation(out=p_[:, 0:n], in_=bk[:, 0:n], func=AF.Exp, scale=SCALE), reads=[rb], writes=[rp_])
                            first, last = (ki == 0), (ki == len(kts) - 1)
                            S.op("pe", lambda e: e.matmul(num[:, 0:n], lhsT=vp[:, kt, :], rhs=p_[:, 0:n], start=first, stop=last),
                                 reads=[r_vp, rp_], writes=[r_num], same_ok=True)
                            S.op("pe", lambda e: e.matmul(den[:, 0:n], lhsT=onb[:], rhs=p_[:, 0:n], start=first, stop=last),
                                 reads=[r_onb, rp_], writes=[r_den], same_ok=True)
                    S.op("dve", lambda e: e.reciprocal(out=rd[:, 0, 0:n], in_=dens[0][0][:, 0:n]), reads=[dens[0][1]], writes=[r_rd])
                    S.op("dve", lambda e: e.reciprocal(out=rd[:, 1, 0:n], in_=dens[1][0][:, 0:n]), reads=[dens[1][1]], writes=[r_rd])
                    S.op("dve", lambda e: e.tensor_tensor(out=t1[:, 0:n], in0=nums[0][0][:, 0:n], in1=rd[:, 0, 0:n], op=ALU.mult),
                         reads=[nums[0][1], r_rd], writes=[r_t1])
                    S.op("dve", lambda e: e.scalar_tensor_tensor(out=t2[:, 0:n], in0=nums[1][0][:, 0:n], scalar=lam[:, 2:3], in1=rd[:, 1, 0:n],
                                                                 op0=ALU.mult, op1=ALU.mult), reads=[nums[1][1], r_rd, r_lam], writes=[r_t2])
                    S.op("pool", lambda e: e.tensor_tensor(out=t1[:, 0:n], in0=t1[:, 0:n], in1=t2[:, 0:n], op=ALU.add), reads=[r_t1, r_t2], writes=[r_t1])
                    S.op("act", lambda e: e.activation(out=t2[:, 0:n], in_=t1[:, 0:n], func=AF.Square), reads=[r_t1], writes=[r_t2])
                    bk, rb = self.bank()
                    S.op("pe", lambda e: e.matmul(bk[:, 0:n], lhsT=self.ones[:], rhs=t2[:, 0:n], start=True, stop=True),
                         reads=[r_t2, self.r_ident], writes=[rb], same_ok=True)
                    S.op("act", lambda e: e.activation(out=t2[:, 0:n], in_=bk[:, 0:n], func=AF.Sqrt, bias=epsc[:, 0:1], scale=1.0 / 128.0),
                         reads=[rb, r_eps], writes=[r_t2])
                    S.op("dve", lambda e: e.reciprocal(out=t2[:, 0:n], in_=t2[:, 0:n]), reads=[r_t2], writes=[r_t2])
                    o_, ro_ = on_[ti % 2], r_on[ti % 2]
                    S.op("dve", lambda e: e.scalar_tensor_tensor(out=o_[:, 0:n], in0=t1[:, 0:n], scalar=sub[:, 0:1], in1=t2[:, 0:n],
                                                                 op0=ALU.mult, op1=ALU.mult), reads=[r_t1, r_t2, r_sub], writes=[ro_])
                    for oc in range(8):
                        bk, rb = self.bank()
                        S.op("pe", lambda e: e.matmul(bk[:, 0:n], lhsT=wo[h % 2][:, oc * 128:(oc + 1) * 128], rhs=o_[:, 0:n], start=True, stop=True),
                             reads=[r_wo[h % 2], ro_], writes=[rb], same_ok=True)
                        hz = self.hT[:, oc, t0:t0 + n]
                        S.op("dve", lambda e: e.scalar_tensor_tensor(out=hz, in0=bk[:, 0:n], scalar=self.mcol(i, 2, oc, lc), in1=hz,
                                                                     op0=ALU.mult, op1=ALU.add),
                             reads=[rb, self.r_h[ti], self.r_mod[i]], writes=[self.r_h[ti]])
            S.barrier()
        self.nrr = 6
        self.bank_i = 0

    def gla(self, i):
        S = self.S
        QS = 128.0 ** -0.5
        win = self.din["gla_w_in"]
        with ExitStack() as ph:
            arr = [[self.sb([128, T], BF16, es=ph) for _ in range(3)] for _ in range(2)]
            r_arr = [[Res() for _ in range(3)] for _ in range(2)]
            vh = self.sb([128, 18, 256], BF16, es=ph); r_vh = Res("vh")
            oacc = self.sb([128, 2, T], BF16, es=ph); r_oacc = [Res(f"oacc{c}") for c in range(18)]
            rT = self.sb([32, T], BF16, es=ph); r_rT = Res("rT")
            gw = self.sb([32, 2, 512], BF16, es=ph); r_gw = Res("gw")
            gb = self.sb([128, 2, 4], F32, es=ph); r_gb = Res("gb")
            ng = self.sb([128, 2], F32, es=ph); r_ng = Res("ng")
            dec = self.sb([128, 2, 18], F32, es=ph); r_dec = Res("dec")
            cmask = self.sb([128, 512], F32, es=ph); r_cm = Res("cmask")
            msk = [self.sb([128, 128], F32, es=ph) for _ in range(2)]; r_msk = Res("msk")
            bA = self.sb([128, 512], F32, es=ph); r_bA = Res("bA")
            bB = self.sb([128, 512], F32, es=ph); r_bB = Res("bB")
            bC = self.sb([128, 512], F32, es=ph); r_bC = Res("bC")
            bD = self.sb([128, 512], F32, es=ph); r_bD = Res("bD")
            Sf = [self.sb([128, 256], F32, es=ph) for _ in range(2)]; r_Sf = [Res("Sf0"), Res("Sf1")]
            Sb = [self.sb([128, 256], BF16, es=ph) for _ in range(2)]; r_Sb = [Res("Sb0"), Res("Sb1")]
            Am = [self.sb([128, 128], BF16, es=ph) for _ in range(2)]; r_Am = [Res(), Res()]
            keT = [self.sb([128, 128], BF16, es=ph) for _ in range(2)]; r_keT = [Res(), Res()]
            ogn = [self.sb([128, 2, 512], BF16, es=ph) for _ in range(1)]; r_ogn = [Res()]
            epsc = self.sb([128, 1], F32, es=ph); r_eps = Res("eps")
            S.op("pool", lambda e: e.memset(epsc[:], EPS), writes=[r_eps])
            S.op("pool", lambda e: e.memset(cmask[:], 1.0), writes=[r_cm])
            for c in range(4):
                S.op("pool", lambda e: e.memset(cmask[:, c * 128:c * 128 + 1], 0.0), reads=[r_cm], writes=[r_cm])
            for d in range(2):
                S.op("pool", lambda e: e.memset(msk[d][:], 1.0), reads=[r_msk], writes=[r_msk])
                cm, pat = ((-1, [[1, 128]]) if d == 0 else (1, [[-1, 128]]))
                S.op("pool", lambda e: e.affine_select(out=msk[d][:], in_=msk[d][:], compare_op=ALU.is_ge, fill=0.0, base=0,
                                                      pattern=pat, channel_multiplier=cm), reads=[r_msk], writes=[r_msk])
            S.op("pool", lambda e: e.memset(gw[:], 0.0), writes=[r_gw])
            S.dma("pool", [(gw[0:16, 0, :], self.din["gla_gw"][0]), (gw[16:32, 1, :], self.din["gla_gw"][1])], r_gw, reads=[r_gw], writes=[r_gw])
            for d in range(2):
                self.load_cols(self.din["gla_gb"][d], 4, gb[:, d, :], r_gb)
            S.op("dve", lambda e: e.tensor_scalar(out=gb[:], in0=gb[:], scalar1=-1.0, scalar2=None, op0=ALU.mult), reads=[r_gb], writes=[r_gb])
            self.load_cols(self.din["gla_norm"], 2, ng[:, 0:2], r_ng)
            w_, r_w = self.wnext()
            wr = w_[:, 0:256].rearrange("p (k n) -> p k n", k=8)
            S.dma("pool", [(wr, win[:, 3072:3104].rearrange("(k p) n -> p k n", p=128))], r_w, writes=[r_w])
            for ti, (t0, n, lc) in enumerate(TT):
                bk, rb = self.bank()
                for kc in range(8):
                    S.op("pe", lambda e: e.matmul(bk[0:32, 0:n], lhsT=wr[:, kc, :], rhs=self.uT[:, kc, t0:t0 + n], start=(kc == 0), stop=(kc == 7)),
                         reads=[r_w, self.r_u[ti]], writes=[rb], same_ok=True)
                S.op("act", lambda e: e.copy(rT[:, t0:t0 + n], bk[0:32, 0:n]), reads=[rb], writes=[r_rT])

            for h in range(4):
                wA_, r_wA = self.wnext()
                wqk = wA_[:, 0:2048].rearrange("p (k n) -> p k n", k=8)
                S.dma("pool", [(wqk[:, :, 0:128], win[:, h * 128:(h + 1) * 128].rearrange("(k p) n -> p k n", p=128)),
                               (wqk[:, :, 128:256], win[:, 512 + h * 128:512 + (h + 1) * 128].rearrange("(k p) n -> p k n", p=128))],
                      r_wA, writes=[r_wA])
                wB_, r_wB = self.wnext()
                wv = wB_[:, 0:2048].rearrange("p (k n) -> p k n", k=8)
                wg = wB_[:, 2048:4096].rearrange("p (k n) -> p k n", k=8)
                S.dma("pool", [(wv, win[:, 1024 + h * 256:1024 + (h + 1) * 256].rearrange("(k p) n -> p k n", p=128)),
                               (wg, win[:, 2048 + h * 256:2048 + (h + 1) * 256].rearrange("(k p) n -> p k n", p=128))],
                      r_wB, writes=[r_wB])
                wC_, r_wC = self.wnext()
                wo = wC_[:, 0:2048].rearrange("p (k n) -> p k n", k=2)
                S.dma("pool", [(wo, self.din["gla_w_out"][h * 256:(h + 1) * 256, :].rearrange("(k p) n -> p k n", p=128))], r_wC, writes=[r_wC])
                for g0 in range(0, 18, 2):
                    bk, rb = self.bank()
                    for jt in range(2):
                        kt = g0 + jt
                        for kc in range(8):
                            S.op("pe", lambda e: e.matmul(bk[:, jt * 256:(jt + 1) * 256], lhsT=self.uT[:, kc, kt * 128:(kt + 1) * 128], rhs=wv[:, kc, :],
                                                          start=(kc == 0), stop=(kc == 7)), reads=[r_wB] + self.r_u, writes=[rb], same_ok=True)
                    S.op("act", lambda e: e.copy(vh[:, g0:g0 + 2, :], bk[:, 0:512].rearrange("p (j d) -> p j d", d=256)), reads=[rb], writes=[r_vh])
                for ti, (t0, n, lc) in enumerate(TT):
                    ru = self.r_u[ti]
                    nch = n // 128
                    bq, rbq = self.bank()
                    for kc in range(8):
                        S.op("pe", lambda e: e.matmul(bq[:, 0:n], lhsT=wqk[:, kc, 0:128], rhs=self.uT[:, kc, t0:t0 + n], start=(kc == 0), stop=(kc == 7)),
                             reads=[r_wA, ru], writes=[rbq], same_ok=True)
                    bkk, rbk = self.bank()
                    for kc in range(8):
                        S.op("pe", lambda e: e.matmul(bkk[:, 0:n], lhsT=wqk[:, kc, 128:256], rhs=self.uT[:, kc, t0:t0 + n], start=(kc == 0), stop=(kc == 7)),
                             reads=[r_wA, ru], writes=[rbk], same_ok=True)
                    for d in range(2):
                        bx, rbx = self.bank()
                        S.op("pe", lambda e: e.matmul(bx[:, 0:n], lhsT=gw[:, d, h * 128:(h + 1) * 128], rhs=rT[:, t0:t0 + n], start=True, stop=True),
                             reads=[r_gw, r_rT], writes=[rbx], same_ok=True)
                        S.op("act", lambda e: e.activation(out=bA[:, 0:n], in_=bx[:, 0:n], func=AF.Exp, bias=gb[:, d, h:h + 1], scale=-1.0),
                             reads=[rbx, r_gb], writes=[r_bA])
                        S.op("act", lambda e: e.activation(out=bA[:, 0:n], in_=bA[:, 0:n], func=AF.Ln, bias=1.0, scale=1.0), reads=[r_bA], writes=[r_bA])
                        S.op("dve", lambda e: e.tensor_tensor_scan(out=bB[:, 0:n], data0=cmask[:, 0:n], data1=bA[:, 0:n], initial=0.0,
                                                                   op0=ALU.mult, op1=ALU.add), reads=[r_cm, r_bA], writes=[r_bB])
                        for c in range(nch):
                            gc = t0 // 128 + c
                            ce = c * 128 + 127
                            S.op("act", lambda e: e.activation(out=dec[:, d, gc:gc + 1], in_=bB[:, ce:ce + 1], func=AF.Exp, scale=-1.0 / 16), reads=[r_bB], writes=[r_dec])
                            S.op("dve", lambda e: e.tensor_scalar(out=bD[:, c * 128:(c + 1) * 128], in0=bB[:, c * 128:(c + 1) * 128], scalar1=bB[:, ce:ce + 1],
                                                                  scalar2=None, op0=ALU.subtract), reads=[r_bB], writes=[r_bD])
                        if d == 0:
                            S.op("act", lambda e: e.activation(out=bC[:, 0:n], in_=bB[:, 0:n], func=AF.Exp, scale=-1.0 / 16), reads=[r_bB], writes=[r_bC])
                            S.op("dve", lambda e: e.scalar_tensor_tensor(out=arr[d][0][:, t0:t0 + n], in0=bq[:, 0:n], scalar=QS, in1=bC[:, 0:n], op0=ALU.mult, op1=ALU.mult),
                                 reads=[rbq, r_bC], writes=[r_arr[d][0]])
                            S.op("act", lambda e: e.activation(out=bC[:, 0:n], in_=bB[:, 0:n], func=AF.Exp, scale=1.0 / 16), reads=[r_bB], writes=[r_bC])
                            S.op("dve", lambda e: e.tensor_tensor(out=arr[d][1][:, t0:t0 + n], in0=bkk[:, 0:n], in1=bC[:, 0:n], op=ALU.mult),
                                 reads=[rbk, r_bC], writes=[r_arr[d][1]])
                            S.op("act", lambda e: e.activation(out=bC[:, 0:n], in_=bD[:, 0:n], func=AF.Exp, scale=1.0 / 16), reads=[r_bD], writes=[r_bC])
                            S.op("dve", lambda e: e.tensor_tensor(out=arr[d][2][:, t0:t0 + n], in0=bkk[:, 0:n], in1=bC[:, 0:n], op=ALU.mult),
                                 reads=[rbk, r_bC], writes=[r_arr[d][2]])
                        else:
                            S.op("dve", lambda e: e.tensor_tensor(out=bD[:, 0:n], in0=bA[:, 0:n], in1=bD[:, 0:n], op=ALU.subtract), reads=[r_bA, r_bD], writes=[r_bD])
                            S.op("act", lambda e: e.activation(out=bC[:, 0:n], in_=bD[:, 0:n], func=AF.Exp, scale=-1.0 / 16), reads=[r_bD], writes=[r_bC])
                            S.op("dve", lambda e: e.scalar_tensor_tensor(out=arr[d][0][:, t0:t0 + n], in0=bq[:, 0:n], scalar=QS, in1=bC[:, 0:n], op0=ALU.mult, op1=ALU.mult),
                                 reads=[rbq, r_bC], writes=[r_arr[d][0]])
                            S.op("act", lambda e: e.activation(out=bC[:, 0:n], in_=bD[:, 0:n], func=AF.Exp, scale=1.0 / 16), reads=[r_bD], writes=[r_bC])
                            S.op("dve", lambda e: e.tensor_tensor(out=arr[d][1][:, t0:t0 + n], in0=bkk[:, 0:n], in1=bC[:, 0:n], op=ALU.mult),
                                 reads=[rbk, r_bC], writes=[r_arr[d][1]])
                            S.op("dve", lambda e: e.tensor_tensor(out=bD[:, 0:n], in0=bA[:, 0:n], in1=bB[:, 0:n], op=ALU.subtract), reads=[r_bA, r_bB, r_bC], writes=[r_bD])
                            S.op("act", lambda e: e.activation(out=bC[:, 0:n], in_=bD[:, 0:n], func=AF.Exp, scale=1.0 / 16), reads=[r_bD], writes=[r_bC])
                            S.op("dve", lambda e: e.tensor_tensor(out=arr[d][2][:, t0:t0 + n], in0=bkk[:, 0:n], in1=bC[:, 0:n], op=ALU.mult),
                                 reads=[rbk, r_bC], writes=[r_arr[d][2]])
                for d in range(2):
                    S.op("pool", lambda e: e.memset(Sf[d][:], 0.0), reads=[r_Sf[d]], writes=[r_Sf[d]])
                    S.op("pool", lambda e: e.memset(Sb[d][:], 0.0), reads=[r_Sb[d]], writes=[r_Sb[d]])
                order = [[16, 17] + list(range(16)), [17, 16] + list(range(15, -1, -1))]
                written = set()
                for step in range(18):
                    for d in range(2):
                        c = order[d][step]
                        cs = slice(c * 128, (c + 1) * 128)
                        qd, ki, ke = arr[d]
                        ba, rba = self.bank()
                        S.op("pe", lambda e: e.matmul(ba[:, 0:128], lhsT=ki[:, cs], rhs=qd[:, cs], start=True, stop=True),
                             reads=[r_arr[d][1], r_arr[d][0]], writes=[rba], same_ok=True)
                        S.op("pe", lambda e: e.matmul(ba[:, 128:256], lhsT=ke[:, cs], rhs=self.identb[:], start=True, stop=True),
                             reads=[r_arr[d][2], self.r_ident], writes=[rba], same_ok=True)
                        S.op("dve", lambda e: e.tensor_tensor(out=Am[d][:], in0=ba[:, 0:128], in1=msk[d][:], op=ALU.mult), reads=[rba, r_msk], writes=[r_Am[d]])
                        S.op("act", lambda e: e.copy(keT[d][:], ba[:, 128:256]), reads=[rba], writes=[r_keT[d]])
                        bo, rbo = self.bank()
                        for j in range(2):
                            S.op("pe", lambda e: e.matmul(bo[:, j * 128:(j + 1) * 128], lhsT=Sb[d][:, j * 128:(j + 1) * 128], rhs=qd[:, cs], start=True, stop=False),
                                 reads=[r_Sb[d], r_arr[d][0]], writes=[rbo], same_ok=True)
                            S.op("pe", lambda e: e.matmul(bo[:, j * 128:(j + 1) * 128], lhsT=vh[:, c, j * 128:(j + 1) * 128], rhs=Am[d][:], start=False, stop=True),
                                 reads=[r_vh, r_Am[d]], writes=[rbo], same_ok=True)
                        ov = oacc[:, :, cs]
                        pv = bo[:, 0:256].rearrange("p (j c) -> p j c", j=2)
                        if c not in written:
                            written.add(c)
                            S.op("act", lambda e: e.copy(ov, pv), reads=[rbo], writes=[r_oacc[c]])
                        else:
                            S.op("dve", lambda e: e.tensor_tensor(out=ov, in0=pv, in1=ov, op=ALU.add), reads=[rbo, r_oacc[c]], writes=[r_oacc[c]])
                        bs, rbs = self.bank()
                        S.op("pe", lambda e: e.matmul(bs[:, 0:256], lhsT=keT[d][:], rhs=vh[:, c, :], start=True, stop=True),
                             reads=[r_keT[d], r_vh], writes=[rbs], same_ok=True)
                        S.op("dve", lambda e: e.scalar_tensor_tensor(out=Sf[d][:], in0=Sf[d][:], scalar=dec[:, d, c:c + 1], in1=bs[:, 0:256], op0=ALU.mult, op1=ALU.add),
                             reads=[r_Sf[d], r_dec, rbs], writes=[r_Sf[d]])
                        S.op("act", lambda e: e.copy(Sb[d][:], Sf[d][:]), reads=[r_Sf[d]], writes=[r_Sb[d]])
                for ti, (t0, n, lc) in enumerate(TT):
                    ru = self.r_u[ti]
                    roa = r_oacc[t0 // 128:(t0 + n) // 128]
                    bss = self.banks[6]; rbss = self.bank_res[6]
                    for j in range(2):
                        S.op("act", lambda e: e.activation(out=bA[:, 0:n], in_=oacc[:, j, t0:t0 + n], func=AF.Square), reads=roa + [r_bA], writes=[r_bA])
                        S.op("pe", lambda e: e.matmul(bss[:, 0:n], lhsT=self.ones[:], rhs=bA[:, 0:n], start=(j == 0), stop=(j == 1)),
                             reads=[r_bA, self.r_ident], writes=[rbss], same_ok=True)
                    S.op("act", lambda e: e.activation(out=bB[:, 0:n], in_=bss[:, 0:n], func=AF.Sqrt, bias=epsc[:, 0:1], scale=1.0 / 256.0), reads=[rbss, r_eps], writes=[r_bB])
                    S.op("dve", lambda e: e.reciprocal(out=bB[:, 0:n], in_=bB[:, 0:n]), reads=[r_bB], writes=[r_bB])
                    og, rog = ogn[0], r_ogn[0]
                    for j in range(2):
                        bg, rbg = self.bank()
                        for kc in range(8):
                            S.op("pe", lambda e: e.matmul(bg[:, 0:n], lhsT=wg[:, kc, j * 128:(j + 1) * 128], rhs=self.uT[:, kc, t0:t0 + n], start=(kc == 0), stop=(kc == 7)),
                                 reads=[r_wB, ru], writes=[rbg], same_ok=True)
                        S.op("act", lambda e: e.activation(out=bC[:, 0:n], in_=bg[:, 0:n], func=AF.Silu), reads=[rbg], writes=[r_bC])
                        S.op("dve", lambda e: e.scalar_tensor_tensor(out=bD[:, 0:n], in0=oacc[:, j, t0:t0 + n], scalar=ng[:, j:j + 1], in1=bB[:, 0:n], op0=ALU.mult, op1=ALU.mult),
                             reads=roa + [r_ng, r_bB], writes=[r_bD])
                        S.op("dve", lambda e: e.tensor_tensor(out=og[:, j, 0:n], in0=bD[:, 0:n], in1=bC[:, 0:n], op=ALU.mult), reads=[r_bD, r_bC], writes=[rog])
                    for oc in range(8):
                        bk, rb = self.bank()
                        for j in range(2):
                            S.op("pe", lambda e: e.matmul(bk[:, 0:n], lhsT=wo[:, j, oc * 128:(oc + 1) * 128], rhs=og[:, j, 0:n], start=(j == 0), stop=(j == 1)),
                                 reads=[r_wC, rog], writes=[rb], same_ok=True)
                        hz = self.hT[:, oc, t0:t0 + n]
                        S.op("dve", lambda e: e.scalar_tensor_tensor(out=hz, in0=bk[:, 0:n], scalar=self.mcol(i, 2, oc, lc), in1=hz, op0=ALU.mult, op1=ALU.add),
                             reads=[rb, self.r_h[ti], self.r_mod[i]], writes=[self.r_h[ti]])
            S.barrier()

    def gdn(self, i):
        S = self.S
        R32 = mybir.dt.float32r
        win = self.din["gdn_w_in"]
        rr = lambda ap: ap.bitcast(R32)
        with ExitStack() as ph:
            sbp = lambda shape, dt: self.sb(shape, dt, es=ph)
            cw = sbp([128, 5, 32], F32); r_cw = Res("cw")
            for j in range(5):
                self.load_cols(self.din["gdn_conv"][j], 32, cw[:, j, :], r_cw)
            r_msk = Res("gmsk")
            strict = [sbp([128, 128], F32) for _ in range(2)]
            inclT = [sbp([128, 128], F32) for _ in range(2)]
            specs = [(strict[0], ALU.is_gt, 1, [[-1, 128]]), (strict[1], ALU.is_gt, -1, [[1, 128]]),
                     (inclT[0], ALU.is_ge, -1, [[1, 128]]), (inclT[1], ALU.is_ge, 1, [[-1, 128]])]
            for (t_, cmp_, cm, pat) in specs:
                S.op("pool", lambda e: e.memset(t_[:], 1.0), reads=[r_msk], writes=[r_msk])
                S.op("pool", lambda e: e.affine_select(out=t_[:], in_=t_[:], compare_op=cmp_, fill=0.0, base=0, pattern=pat, channel_multiplier=cm),
                     reads=[r_msk], writes=[r_msk])
            bsel = sbp([8, 128], F32)
            bd = {}
            for b in (16, 32, 64):
                nb = 128 // b
                S.op("pool", lambda e: e.memset(bsel[:], 1.0), reads=[r_msk], writes=[r_msk])
                S.op("pool", lambda e: e.affine_select(out=bsel[:], in_=bsel[:], compare_op=ALU.is_ge, fill=0.0, base=0, pattern=[[1, 128]], channel_multiplier=-b),
                     reads=[r_msk], writes=[r_msk])
                S.op("pool", lambda e: e.affine_select(out=bsel[:], in_=bsel[:], compare_op=ALU.is_ge, fill=0.0, base=b - 1, pattern=[[-1, 128]], channel_multiplier=b),
                     reads=[r_msk], writes=[r_msk])
                bk, rb = self.bank()
                S.op("pe", lambda e: e.matmul(bk[:, 0:128], lhsT=bsel[0:nb, :], rhs=bsel[0:nb, :], start=True, stop=True), reads=[r_msk], writes=[rb], same_ok=True)
                bd[b] = sbp([128, 128], F32)
                S.op("dve", lambda e: e.tensor_copy(bd[b][:], bk[:, 0:128]), reads=[rb, r_msk], writes=[r_msk])
            off64 = sbp([128, 128], F32)
            S.op("dve", lambda e: e.tensor_scalar(out=off64[:], in0=bd[64][:], scalar1=-1.0, scalar2=1.0, op0=ALU.mult, op1=ALU.add), reads=[r_msk], writes=[r_msk])
            S.op("dve", lambda e: e.tensor_tensor(out=bd[64][:], in0=bd[64][:], in1=bd[32][:], op=ALU.subtract), reads=[r_msk], writes=[r_msk])
            S.op("dve", lambda e: e.tensor_tensor(out=bd[32][:], in0=bd[32][:], in1=bd[16][:], op=ALU.subtract), reads=[r_msk], writes=[r_msk])
            bd16, off16, off32 = bd[16], bd[32], bd[64]
            b4 = lambda m_: m_[:].unsqueeze(1).broadcast_to([128, 4, 128])
            hc = sbp([128, 64], F32); r_hc = Res("hc")
            S.dma("sp", [(hc[:], self.din["gdn_hc"].partition_broadcast(128))], r_hc, writes=[r_hc])
            S.op("act", lambda e: e.activation(out=hc[:, 0:32], in_=hc[:, 0:32], func=AF.Exp), reads=[r_hc], writes=[r_hc])
            S.op("dve", lambda e: e.tensor_scalar(out=hc[:, 0:32], in0=hc[:, 0:32], scalar1=-1.0, scalar2=None, op0=ALU.mult), reads=[r_hc], writes=[r_hc])
            ngrep = sbp([128, 128], F32); r_ngr = Res("ngrep")
            S.dma("sp", [(ngrep[:], self.din["gdn_norm"].partition_broadcast(128))], r_ngr, writes=[r_ngr])
            epsc = sbp([128, 1], F32); r_eps = Res("eps")
            S.op("pool", lambda e: e.memset(epsc[:], EPS), writes=[r_eps])
            qkT = sbp([128, 2, T], BF16); r_qkT = Res("qkT")
            kn = sbp([128, 18, 128], BF16); r_kn = Res("kn")
            vt = sbp([128, 18, 256], BF16); r_vt = Res("vt")
            oacc = sbp([128, 18, 256], BF16); r_oacc = [Res(f"go{c}") for c in range(18)]
            sc_names = ("negb", "gc", "e", "ecoef", "negbe", "dl", "g")
            sc = {n_: sbp([128, 18, 4], F32) for n_ in sc_names}
            r_sc = Res("gsc")

            for kh in range(8):
                wA_, r_wA = self.wnext()
                wA = wA_[:].rearrange("p (k n) -> p k n", k=8)
                S.dma("pool", [(wA[:, :, 0:128], win[:, kh * 128:(kh + 1) * 128].rearrange("(k p) n -> p k n", p=128)),
                               (wA[:, :, 128:256], win[:, 1024 + kh * 128:1024 + (kh + 1) * 128].rearrange("(k p) n -> p k n", p=128)),
                               (wA[:, :, 256:512], win[:, 2048 + kh * 256:2048 + (kh + 1) * 256].rearrange("(k p) n -> p k n", p=128))],
                      r_wA, writes=[r_wA])
                wB_, r_wB = self.wnext()
                wz = wB_[:, 0:2048].rearrange("p (k n) -> p k n", k=8)
                wgt = wB_[:, 2048:2112].rearrange("p (k n) -> p k n", k=8)
                S.dma("pool", [(wz, win[:, 4096 + kh * 256:4096 + (kh + 1) * 256].rearrange("(k p) n -> p k n", p=128)),
                               (wgt, self.din["gdn_wg"][:, kh * 8:(kh + 1) * 8].rearrange("(k p) n -> p k n", p=128))], r_wB, writes=[r_wB])
                wC_, r_wC = self.wnext()
                wo = wC_[:, 0:2048].rearrange("p (k n) -> p k n", k=2)
                S.dma("pool", [(wo, self.din["gdn_w_out"][kh * 256:(kh + 1) * 256, :].rearrange("(k p) n -> p k n", p=128))], r_wC, writes=[r_wC])
                with ExitStack() as p1:
                    xpad = self.sb([128, 4, 2312], BF16, es=p1); r_xp = Res("xpad")
                    dg = self.sb([128, 4, 5, 128], BF16, es=p1); r_dg = Res("dg")
                    cv = self.sb([128, 512], F32, es=p1); r_cv = Res("cv")
                    junk = self.sb([128, 128], F32, es=p1); r_junk = Res("junk")
                    ss = self.sb([128, 2], F32, es=p1); r_ss = Res("ss")
                    qn = self.sb([128, 128], BF16, es=p1); r_qn = Res("qn")
                    S.op("pool", lambda e: e.memset(xpad[:], 0.0), writes=[r_xp])
                    gch = [kh, 8 + kh, 16 + 2 * kh, 17 + 2 * kh]
                    for ch in range(4):
                        for j in range(5):
                            S.op("pool", lambda e: e.tensor_scalar(out=dg[:, ch, j, :], in0=self.identb[:], scalar1=cw[:, j, gch[ch]:gch[ch] + 1], scalar2=None, op0=ALU.mult),
                                 reads=[r_cw, self.r_ident], writes=[r_dg])
                    for ch in range(4):
                        for ti, (t0, n, lc) in enumerate(TT):
                            bk, rb = self.bank()
                            for kc in range(8):
                                S.op("pe", lambda e: e.matmul(bk[:, 0:n], lhsT=wA[:, kc, ch * 128:(ch + 1) * 128], rhs=self.uT[:, kc, t0:t0 + n], start=(kc == 0), stop=(kc == 7)),
                                     reads=[r_wA, self.r_u[ti]], writes=[rb], same_ok=True)
                            c0 = t0 + 2 if lc == 0 else 2054
                            if (ch + ti) % 2 == 0:
                                S.op("act", lambda e: e.copy(xpad[:, ch, c0:c0 + n], bk[:, 0:n]), reads=[rb], writes=[r_xp])
                            else:
                                S.op("dve", lambda e: e.tensor_copy(xpad[:, ch, c0:c0 + n], bk[:, 0:n]), reads=[rb], writes=[r_xp])
                    for t in range(18):
                        b0 = t * 128 + 2 if t < 16 else 2054 + (t - 16) * 128
                        bk, rb = self.bank()
                        for ch in range(4):
                            for j in range(5):
                                S.op("pe", lambda e: e.matmul(bk[:, ch * 128:(ch + 1) * 128], lhsT=xpad[:, ch, b0 + j - 2:b0 + j - 2 + 128], rhs=dg[:, ch, j, :],
                                                              start=(j == 0), stop=(j == 4)), reads=[r_xp, r_dg], writes=[rb], same_ok=True)
                        S.op("act", lambda e: e.activation(out=cv[:], in_=bk[:, 0:512], func=AF.Silu), reads=[rb], writes=[r_cv])
                        for q_ in range(2):
                            S.op("act", lambda e: e.activation(out=junk[:], in_=cv[:, q_ * 128:(q_ + 1) * 128], func=AF.Square, accum_out=ss[:, q_:q_ + 1]),
                                 reads=[r_cv], writes=[r_junk, r_ss])
                        S.op("act", lambda e: e.activation(out=ss[:], in_=ss[:], func=AF.Sqrt, bias=epsc[:, 0:1], scale=1.0), reads=[r_ss, r_eps], writes=[r_ss])
                        S.op("dve", lambda e: e.reciprocal(out=ss[:], in_=ss[:]), reads=[r_ss], writes=[r_ss])
                        S.op("dve", lambda e: e.tensor_scalar(out=qn[:], in0=cv[:, 0:128], scalar1=ss[:, 0:1], scalar2=128.0 ** -0.5, op0=ALU.mult, op1=ALU.mult),
                             reads=[r_cv, r_ss], writes=[r_qn])
                        S.op("dve", lambda e: e.tensor_scalar(out=kn[:, t, :], in0=cv[:, 128:256], scalar1=ss[:, 1:2], scalar2=None, op0=ALU.mult),
                             reads=[r_cv, r_ss], writes=[r_kn])
                        S.op("pool", lambda e: e.tensor_copy(vt[:, t, :], cv[:, 256:512]), reads=[r_cv], writes=[r_vt])
                        b2, rb2 = self.bank()
                        S.op("pe", lambda e: e.matmul(b2[:, 0:128], lhsT=qn[:], rhs=self.identb[:], start=True, stop=True), reads=[r_qn, self.r_ident], writes=[rb2], same_ok=True)
                        S.op("pe", lambda e: e.matmul(b2[:, 128:256], lhsT=kn[:, t, :], rhs=self.identb[:], start=True, stop=True), reads=[r_kn, self.r_ident], writes=[rb2], same_ok=True)
                        S.op("act", lambda e: e.copy(qkT[:, :, t * 128:(t + 1) * 128], b2[:, 0:256].rearrange("p (a c) -> p a c", a=2)), reads=[rb2], writes=[r_qkT])
                    S.barrier()
                bk, rb = self.bank()
                for t in range(18):
                    for kc in range(8):
                        S.op("pe", lambda e: e.matmul(bk[:, t * 8:(t + 1) * 8], lhsT=self.uT[:, kc, t * 128:(t + 1) * 128], rhs=wgt[:, kc, :], start=(kc == 0), stop=(kc == 7)),
                             reads=[r_wB] + self.r_u, writes=[rb], same_ok=True)
                graw = bk[:, 0:144].rearrange("p (t c) -> p t c", c=8)
                S.op("act", lambda e: e.activation(out=sc["negb"][:], in_=graw[:, :, 0:4], func=AF.Sigmoid), reads=[rb], writes=[r_sc])
                S.op("dve", lambda e: e.tensor_scalar(out=sc["negb"][:], in0=sc["negb"][:], scalar1=-1.0, scalar2=None, op0=ALU.mult), reads=[r_sc], writes=[r_sc])
                for m in range(4):
                    d_, j_ = m // 2, m % 2
                    hidx = d_ * 16 + 2 * kh + j_
                    S.op("act", lambda e: e.activation(out=sc["g"][:, :, m], in_=graw[:, :, 4 + m], func=AF.Exp, bias=hc[:, 32 + hidx:33 + hidx], scale=1.0),
                         reads=[rb, r_hc], writes=[r_sc])
                S.op("act", lambda e: e.activation(out=sc["g"][:], in_=sc["g"][:], func=AF.Ln, bias=1.0, scale=1.0), reads=[r_sc], writes=[r_sc])
                for m in range(4):
                    d_, j_ = m // 2, m % 2
                    hidx = d_ * 16 + 2 * kh + j_
                    S.op("dve", lambda e: e.tensor_scalar(out=sc["g"][:, :, m], in0=sc["g"][:, :, m], scalar1=hc[:, hidx:hidx + 1], scalar2=None, op0=ALU.mult),
                         reads=[r_sc, r_hc], writes=[r_sc])
                bk, rb = self.bank()
                gv = sc["g"][:]
                S.op("pe", lambda e: e.matmul(bk[:, 0:72].rearrange("p (t c) -> p t c", c=4)[:, :, 0:2], lhsT=inclT[0][:], rhs=gv[:, :, 0:2], start=True, stop=True),
                     reads=[r_sc, r_msk], writes=[rb], same_ok=True)
                S.op("pe", lambda e: e.matmul(bk[:, 0:72].rearrange("p (t c) -> p t c", c=4)[:, :, 2:4], lhsT=inclT[1][:], rhs=gv[:, :, 2:4], start=True, stop=True),
                     reads=[r_sc, r_msk], writes=[rb], same_ok=True)
                S.op("pe", lambda e: e.matmul(bk[:, 128:200], lhsT=self.ones[:], rhs=gv.rearrange("p t c -> p (t c)"), start=True, stop=True),
                     reads=[r_sc, self.r_ident], writes=[rb], same_ok=True)
                gcp = bk[:, 0:72].rearrange("p (t c) -> p t c", c=4)
                glp = bk[:, 128:200].rearrange("p (t c) -> p t c", c=4)
                S.op("dve", lambda e: e.tensor_copy(sc["gc"][:], gcp), reads=[rb], writes=[r_sc])
                S.op("act", lambda e: e.activation(out=sc["e"][:], in_=gcp, func=AF.Exp), reads=[rb], writes=[r_sc])
                S.op("act", lambda e: e.activation(out=sc["dl"][:], in_=glp, func=AF.Exp), reads=[rb], writes=[r_sc])
                S.op("dve", lambda e: e.tensor_tensor(out=sc["ecoef"][:], in0=glp, in1=sc["gc"][:], op=ALU.subtract), reads=[rb, r_sc], writes=[r_sc])
                S.op("act", lambda e: e.activation(out=sc["ecoef"][:], in_=sc["ecoef"][:], func=AF.Exp), reads=[r_sc], writes=[r_sc])
                S.op("dve", lambda e: e.tensor_tensor(out=sc["negbe"][:], in0=sc["negb"][:], in1=sc["e"][:], op=ALU.mult), reads=[r_sc], writes=[r_sc])
                with ExitStack() as p3:
                    f4 = lambda: self.sb([128, 4, 128], F32, es=p3)
                    X = [f4(), f4()]; Y = [f4(), f4()]; W = f4(); Tm = f4(); Y0 = f4()
                    D1 = f4(); D2 = f4(); r_D1 = Res("D1"); r_D2 = Res("D2")
                    r_X = [Res("X0"), Res("X1")]; r_Y = [Res("Y0"), Res("Y1")]; r_W = Res("W"); r_Tm = Res("Tm"); r_Y0 = Res("Y0p")
                    Gm = self.sb([128, 2, 128], F32, es=p3); r_Gm = Res("Gm")
                    QKm = self.sb([128, 2, 128], F32, es=p3); r_QKm = Res("QKm")
                    AT = self.sb([128, 4, 128], BF16, es=p3); r_AT = Res("AT")
                    Wf = W; r_Wf = r_W
                    Rm = f4(); r_Rm = Res("Rm")
                    vn = self.sb([128, 4, 128], BF16, es=p3); r_vn = Res("vn")
                    kdec = self.sb([128, 4, 128], BF16, es=p3); r_kdec = Res("kdec")
                    bv = self.sb([128, 4, 128], BF16, es=p3); r_bv = Res("bv")
                    Sf = f4(); r_Sf = Res("Sf")
                    Sb = self.sb([128, 4, 128], BF16, es=p3); r_Sb = Res("Sb")
                    ot = self.sb([128, 4, 128], BF16, es=p3); r_ot = Res("ot")
                    S.op("pool", lambda e: e.memset(Sf[:], 0.0), writes=[r_Sf])
                    S.op("pool", lambda e: e.memset(Sb[:], 0.0), writes=[r_Sb])
                    order = [[16, 17] + list(range(16)), [17, 16] + list(range(15, -1, -1))]
                    written = set()
                    kT = lambda c: qkT[:, 1, c * 128:(c + 1) * 128]
                    qT = lambda c: qkT[:, 0, c * 128:(c + 1) * 128]
                    for step in range(18):
                        cc = [order[0][step], order[1][step]]
                        cm_ = [cc[m // 2] for m in range(4)]
                        bk, rb = self.bank()
                        for d in range(2):
                            S.op("pe", lambda e: e.matmul(bk[:, d * 128:(d + 1) * 128], lhsT=kT(cc[d]), rhs=kT(cc[d]), start=True, stop=True), reads=[r_qkT], writes=[rb], same_ok=True)
                            S.op("pe", lambda e: e.matmul(bk[:, 256 + d * 128:256 + (d + 1) * 128], lhsT=kT(cc[d]), rhs=qT(cc[d]), start=True, stop=True), reads=[r_qkT], writes=[rb], same_ok=True)
                        for d in range(2):
                            S.op("dve", lambda e: e.tensor_tensor(out=Gm[:, d, :], in0=bk[:, d * 128:(d + 1) * 128], in1=strict[d][:], op=ALU.mult), reads=[rb, r_msk], writes=[r_Gm])
                            S.op("dve", lambda e: e.tensor_tensor(out=QKm[:, d, :], in0=bk[:, 256 + d * 128:256 + (d + 1) * 128], in1=inclT[d][:], op=ALU.mult), reads=[rb, r_msk], writes=[r_QKm])
                        for m in range(4):
                            S.op("pool", lambda e: e.tensor_scalar(out=D1[:, m, :], in0=self.ident[:], scalar1=sc["gc"][:, cm_[m], m:m + 1], scalar2=None, op0=ALU.mult),
                                 reads=[r_sc, self.r_ident], writes=[r_D1])
                        bb, rbb = self.bank()
                        for m in range(4):
                            S.op("pe", lambda e: e.matmul(bb[:, m * 128:(m + 1) * 128], lhsT=self.ones[:], rhs=D1[:, m, :], start=True, stop=True),
                                 reads=[r_D1, self.r_ident], writes=[rbb], same_ok=True)
                        for m in range(4):
                            S.op("dve", lambda e: e.tensor_scalar(out=D1[:, m, :], in0=bb[:, m * 128:(m + 1) * 128], scalar1=sc["gc"][:, cm_[m], m:m + 1], scalar2=0.0,
                                                                  op0=ALU.subtract, op1=ALU.max), reads=[rbb, r_sc], writes=[r_D1])
                            S.op("dve", lambda e: e.tensor_scalar(out=D2[:, m, :], in0=bb[:, m * 128:(m + 1) * 128], scalar1=sc["gc"][:, cm_[m], m:m + 1], scalar2=0.0,
                                                                  op0=ALU.subtract, op1=ALU.min), reads=[rbb, r_sc], writes=[r_D2])
                        S.op("act", lambda e: e.activation(out=D1[:], in_=D1[:], func=AF.Exp, scale=-1.0), reads=[r_D1], writes=[r_D1])
                        S.op("act", lambda e: e.activation(out=D2[:], in_=D2[:], func=AF.Exp, scale=1.0), reads=[r_D2], writes=[r_D2])
                        for m in range(4):
                            d = m // 2
                            S.op("dve", lambda e: e.scalar_tensor_tensor(out=rr(X[0][:, m, :]), in0=D1[:, m, :], scalar=sc["negb"][:, cm_[m], m:m + 1], in1=Gm[:, d, :],
                                                                         op0=ALU.mult, op1=ALU.mult), reads=[r_D1, r_sc, r_Gm], writes=[r_X[0]])
                            S.op("pool", lambda e: e.tensor_tensor(out=AT[:, m, :], in0=QKm[:, d, :], in1=D2[:, m, :], op=ALU.mult), reads=[r_QKm, r_D2], writes=[r_AT])
                            S.op("pool", lambda e: e.tensor_scalar(out=kdec[:, m, :], in0=kn[:, cm_[m], :], scalar1=sc["ecoef"][:, cm_[m], m:m + 1], scalar2=None, op0=ALU.mult),
                                 reads=[r_kn, r_sc], writes=[r_kdec])
                            S.op("pool", lambda e: e.tensor_scalar(out=bv[:, m, :], in0=vt[:, cm_[m], (m % 2) * 128:(m % 2 + 1) * 128], scalar1=sc["negb"][:, cm_[m], m:m + 1],
                                                                   scalar2=-1.0, op0=ALU.mult, op1=ALU.mult), reads=[r_vt, r_sc], writes=[r_bv])
                        bk, rb = self.bank()
                        for m in range(4):
                            S.op("pe", lambda e: e.matmul(bk[:, m * 128:(m + 1) * 128], lhsT=rr(X[0][:, m, :]), rhs=rr(self.identr[:]), start=True, stop=True),
                                 reads=[r_X[0], self.r_ident], writes=[rb], same_ok=True)
                        pv4 = lambda b_: b_[:, 0:512].rearrange("p (m c) -> p m c", m=4)
                        S.op("act", lambda e: e.copy(rr(Y0[:]), pv4(bk)), reads=[rb], writes=[r_Y0])
                        S.op("dve", lambda e: e.tensor_tensor(out=rr(X[1][:]), in0=X[0][:], in1=b4(bd16), op=ALU.mult), reads=[r_X[0], r_msk], writes=[r_X[1]])
                        S.op("pool", lambda e: e.tensor_tensor(out=rr(Y[1][:]), in0=Y0[:], in1=b4(bd16), op=ALU.mult), reads=[r_Y0, r_msk], writes=[r_Y[1]])
                        S.op("dve", lambda e: e.tensor_tensor(out=rr(W[:]), in0=Y[1][:], in1=b4(self.ident), op=ALU.add), reads=[r_Y[1], self.r_ident], writes=[r_W])
                        cur = 1
                        for lev in range(3):
                            nxt = 1 - cur
                            bx, rbx = self.bank()
                            for m in range(4):
                                S.op("pe", lambda e: e.matmul(bx[:, m * 128:(m + 1) * 128], lhsT=rr(Y[cur][:, m, :]), rhs=rr(X[cur][:, m, :]), start=True, stop=True),
                                     reads=[r_Y[cur], r_X[cur]], writes=[rbx], same_ok=True)
                            if lev < 2:
                                by, rby = self.bank()
                                for m in range(4):
                                    S.op("pe", lambda e: e.matmul(by[:, m * 128:(m + 1) * 128], lhsT=rr(X[cur][:, m, :]), rhs=rr(Y[cur][:, m, :]), start=True, stop=True),
                                         reads=[r_Y[cur], r_X[cur]], writes=[rby], same_ok=True)
                            S.op("act", lambda e: e.copy(rr(X[nxt][:]), pv4(bx)), reads=[rbx], writes=[r_X[nxt]])
                            if lev < 2:
                                S.op("dve", lambda e: e.tensor_copy(rr(Y[nxt][:]), pv4(by)), reads=[rby], writes=[r_Y[nxt]])
                            bw, rbw = self.bank()
                            for m in range(4):
                                S.op("pe", lambda e: e.matmul(bw[:, m * 128:(m + 1) * 128], lhsT=rr(X[nxt][:, m, :]), rhs=rr(W[:, m, :]), start=True, stop=True),
                                     reads=[r_X[nxt], r_W], writes=[rbw], same_ok=True)
                            S.op("dve", lambda e: e.tensor_tensor(out=rr(W[:]), in0=pv4(bw), in1=W[:], op=ALU.add), reads=[rbw, r_W], writes=[r_W])
                            cur = nxt
                        bk, rb = self.bank()
                        for m in range(4):
                            S.op("pe", lambda e: e.matmul(bk[:, m * 128:(m + 1) * 128], lhsT=rr(W[:, m, :]), rhs=rr(self.identr[:]), start=True, stop=True),
                                 reads=[r_W, self.r_ident], writes=[rb], same_ok=True)
                        S.op("act", lambda e: e.copy(rr(Tm[:]), pv4(bk)), reads=[rb], writes=[r_Tm])
                        for li, offm in enumerate((off16, off32, off64)):
                            S.op("pool", lambda e: e.tensor_tensor(out=rr(X[0][:]), in0=Y0[:], in1=b4(offm), op=ALU.mult), reads=[r_Y0, r_msk], writes=[r_X[0]])
                            bz, rbz = self.bank()
                            for m in range(4):
                                S.op("pe", lambda e: e.matmul(bz[:, m * 128:(m + 1) * 128], lhsT=rr(X[0][:, m, :]), rhs=rr(Tm[:, m, :]), start=True, stop=True),
                                     reads=[r_X[0], r_Tm], writes=[rbz], same_ok=True)
                            S.op("act", lambda e: e.copy(rr(X[1][:]), pv4(bz)), reads=[rbz], writes=[r_X[1]])
                            if li < 2:
                                bt, rbt = self.bank()
                                for m in range(4):
                                    S.op("pe", lambda e: e.matmul(bt[:, m * 128:(m + 1) * 128], lhsT=rr(W[:, m, :]), rhs=rr(X[1][:, m, :]), start=True, stop=True),
                                         reads=[r_W, r_X[1]], writes=[rbt], same_ok=True)
                            bw, rbw = self.bank()
                            for m in range(4):
                                S.op("pe", lambda e: e.matmul(bw[:, m * 128:(m + 1) * 128], lhsT=rr(X[1][:, m, :]), rhs=rr(W[:, m, :]), start=True, stop=True),
                                     reads=[r_X[1], r_W], writes=[rbw], same_ok=True)
                            if li < 2:
                                S.op("pool" if False else "dve", lambda e: e.tensor_tensor(out=rr(Tm[:]), in0=pv4(bt), in1=Tm[:], op=ALU.add), reads=[rbt, r_Tm], writes=[r_Tm])
                                S.op("dve", lambda e: e.tensor_tensor(out=rr(W[:]), in0=pv4(bw), in1=W[:], op=ALU.add), reads=[rbw, r_W], writes=[r_W])
                            else:
                                S.op("dve", lambda e: e.tensor_tensor(out=rr(Wf[:]), in0=pv4(bw), in1=W[:], op=ALU.add), reads=[rbw, r_W], writes=[r_Wf])
                        bks, rbks = self.bank()
                        for m in range(4):
                            S.op("pe", lambda e: e.matmul(bks[:, m * 128:(m + 1) * 128], lhsT=kT(cm_[m]), rhs=Sb[:, m, :], start=True, stop=True), reads=[r_qkT, r_Sb], writes=[rbks], same_ok=True)
                        bo1, rbo1 = self.bank()
                        for m in range(4):
                            S.op("pe", lambda e: e.matmul(bo1[:, m * 128:(m + 1) * 128], lhsT=qT(cm_[m]), rhs=Sb[:, m, :], start=True, stop=True), reads=[r_qkT, r_Sb], writes=[rbo1], same_ok=True)
                        for m in range(4):
                            S.op("dve", lambda e: e.scalar_tensor_tensor(out=rr(Rm[:, m, :]), in0=bks[:, m * 128:(m + 1) * 128], scalar=sc["negbe"][:, cm_[m], m:m + 1], in1=bv[:, m, :],
                                                                         op0=ALU.mult, op1=ALU.add), reads=[rbks, r_sc, r_bv], writes=[r_Rm])
                            S.op("act", lambda e: e.activation(out=ot[:, m, :], in_=bo1[:, m * 128:(m + 1) * 128], func=AF.Copy, scale=sc["e"][:, cm_[m], m:m + 1]),
                                 reads=[rbo1, r_sc], writes=[r_ot])
                        bvn, rbvn = self.bank()
                        for m in range(4):
                            S.op("pe", lambda e: e.matmul(bvn[:, m * 128:(m + 1) * 128], lhsT=rr(Wf[:, m, :]), rhs=rr(Rm[:, m, :]), start=True, stop=True), reads=[r_Wf, r_Rm], writes=[rbvn], same_ok=True)
                        S.op("act", lambda e: e.copy(vn[:], pv4(bvn)), reads=[rbvn], writes=[r_vn])
                        bo2, rbo2 = self.bank()
                        for m in range(4):
                            S.op("pe", lambda e: e.matmul(bo2[:, m * 128:(m + 1) * 128], lhsT=AT[:, m, :], rhs=vn[:, m, :], start=True, stop=True), reads=[r_AT, r_vn], writes=[rbo2], same_ok=True)
                        bst, rbst = self.bank()
                        for m in range(4):
                            S.op("pe", lambda e: e.matmul(bst[:, m * 128:(m + 1) * 128], lhsT=kdec[:, m, :], rhs=vn[:, m, :], start=True, stop=True), reads=[r_kdec, r_vn], writes=[rbst], same_ok=True)
                        S.op("dve", lambda e: e.tensor_tensor(out=ot[:], in0=pv4(bo2), in1=ot[:], op=ALU.add), reads=[rbo2, r_ot], writes=[r_ot])
                        for d in range(2):
                            c = cc[d]
                            src = ot[:, 2 * d:2 * d + 2, :]
                            dst = oacc[:, c, :].rearrange("p (j v) -> p j v", j=2)
                            if c not in written:
                                written.add(c)
                                S.op("pool", lambda e: e.tensor_copy(dst, src), reads=[r_ot], writes=[r_oacc[c]])
                            else:
                                S.op("pool", lambda e: e.tensor_tensor(out=dst, in0=dst, in1=src, op=ALU.add), reads=[r_ot, r_oacc[c]], writes=[r_oacc[c]])
                        for m in range(4):
                            S.op("dve", lambda e: e.scalar_tensor_tensor(out=Sf[:, m, :], in0=Sf[:, m, :], scalar=sc["dl"][:, cm_[m], m:m + 1], in1=bst[:, m * 128:(m + 1) * 128],
                                                                         op0=ALU.mult, op1=ALU.add), reads=[r_Sf, r_sc, rbst], writes=[r_Sf])
                        S.op("act", lambda e: e.copy(Sb[:], Sf[:]), reads=[r_Sf], writes=[r_Sb])
                    S.barrier()
                with ExitStack() as p4:
                    ogT = self.sb([128, 2, 512], BF16, es=p4); r_ogT = Res("ogT")
                    og = self.sb([128, 256], F32, es=p4); r_og = Res("og")
                    ogb = self.sb([128, 256], BF16, es=p4); r_ogb = Res("ogb")
                    zs = self.sb([128, 256], F32, es=p4); r_zs = Res("zs")
                    junk = self.sb([128, 128], F32, es=p4); r_junk = Res("junk")
                    ss = self.sb([128, 2], F32, es=p4); r_ss = Res("ss")
                    for ti, (t0, n, lc) in enumerate(TT):
                        for tt in range(n // 128):
                            t = t0 // 128 + tt
                            for j in range(2):
                                S.op("act", lambda e: e.activation(out=junk[:], in_=oacc[:, t, j * 128:(j + 1) * 128], func=AF.Square, accum_out=ss[:, j:j + 1]),
                                     reads=[r_oacc[t]], writes=[r_junk, r_ss])
                            S.op("act", lambda e: e.activation(out=ss[:], in_=ss[:], func=AF.Sqrt, bias=epsc[:, 0:1], scale=1.0 / 128.0), reads=[r_ss, r_eps], writes=[r_ss])
                            S.op("dve", lambda e: e.reciprocal(out=ss[:], in_=ss[:]), reads=[r_ss], writes=[r_ss])
                            bz, rbz = self.bank()
                            for kc in range(8):
                                S.op("pe", lambda e: e.matmul(bz[:, 0:256], lhsT=self.uT[:, kc, t * 128:(t + 1) * 128], rhs=wz[:, kc, :], start=(kc == 0), stop=(kc == 7)),
                                     reads=[r_wB, self.r_u[ti]], writes=[rbz], same_ok=True)
                            S.op("act", lambda e: e.activation(out=zs[:], in_=bz[:, 0:256], func=AF.Silu), reads=[rbz], writes=[r_zs])
                            for j in range(2):
                                S.op("dve", lambda e: e.scalar_tensor_tensor(out=og[:, j * 128:(j + 1) * 128], in0=oacc[:, t, j * 128:(j + 1) * 128], scalar=ss[:, j:j + 1], in1=ngrep[:],
                                                                             op0=ALU.mult, op1=ALU.mult), reads=[r_oacc[t], r_ss, r_ngr], writes=[r_og])
                            S.op("dve", lambda e: e.tensor_tensor(out=ogb[:], in0=og[:], in1=zs[:], op=ALU.mult), reads=[r_og, r_zs], writes=[r_ogb])
                            b2, rb2 = self.bank()
                            for j in range(2):
                                S.op("pe", lambda e: e.matmul(b2[:, j * 128:(j + 1) * 128], lhsT=ogb[:, j * 128:(j + 1) * 128], rhs=self.identb[:], start=True, stop=True),
                                     reads=[r_ogb, self.r_ident], writes=[rb2], same_ok=True)
                            S.op("act", lambda e: e.copy(ogT[:, :, tt * 128:(tt + 1) * 128], b2[:, 0:256].rearrange("p (j c) -> p j c", j=2)), reads=[rb2], writes=[r_ogT])
                        for oc in range(8):
                            bk, rb = self.bank()
                            for j in range(2):
                                S.op("pe", lambda e: e.matmul(bk[:, 0:n], lhsT=wo[:, j, oc * 128:(oc + 1) * 128], rhs=ogT[:, j, 0:n], start=(j == 0), stop=(j == 1)),
                                     reads=[r_wC, r_ogT], writes=[rb], same_ok=True)
                            hz = self.hT[:, oc, t0:t0 + n]
                            S.op("dve", lambda e: e.scalar_tensor_tensor(out=hz, in0=bk[:, 0:n], scalar=self.mcol(i, 2, oc, lc), in1=hz, op0=ALU.mult, op1=ALU.add),
                                 reads=[rb, self.r_h[ti], self.r_mod[i]], writes=[self.r_h[ti]])
                    S.barrier()


def build_program(depth_run=DEPTH, mixers=True, dbg=False):
    nc = bass.Bass("TRN2", target_bir_lowering=False)
    es = ExitStack()
    with es:
        kb = KB(nc, es, depth_run, mixers, dbg)
        kb.build()
        print("instructions", kb.S.ninst, "sems", kb.S.nsem, flush=True)
    return nc, kb


def _rope_tables(dim):
    n_freq = dim // 4
    inv = (10000.0 ** (-np.arange(n_freq, dtype=np.float32) / n_freq)).astype(np.float32)
    tok = np.arange(TL)
    row = (tok // 64).astype(np.float32)
    col = (tok % 64).astype(np.float32)
    ang = np.concatenate([row[:, None] * inv, col[:, None] * inv], -1).astype(np.float32)
    c = np.ones((dim // 2, T), np.float32)
    s = np.zeros((dim // 2, T), np.float32)
    c[:, :TL] = np.cos(ang).T
    s[:, :TL] = np.sin(ang).T
    return c, s


def _mla_host(inputs, shared, g):
    w_in = g("mla_w_in")[0]
    w_qb = g("mla_w_qb")[0]
    ev = np.arange(0, 32, 2)
    od = ev + 1
    cols = []
    for h in range(16):
        b = h * 96
        cols += list(range(b, b + 64)) + list(b + 64 + ev) + list(b + 64 + od) + list(b + 64 + ev) + list(b + 64 + od)
    shared["mla_wqx"] = np.ascontiguousarray(w_qb[:, cols])
    ia = list(1024 + ev) * 4
    ib = list(1024 + od) * 4
    shared["mla_wkr"] = np.ascontiguousarray(np.concatenate(
        [w_in[:, 0:64], w_in[:, ia], w_in[:, 0:64], w_in[:, ib]], axis=1))
    c, s = _rope_tables(32)
    one = np.ones((64, T), np.float32)
    shared["mla_qtab"] = np.concatenate([one, c, s, s, c], 0)
    shared["mla_kta"] = np.concatenate([one, c, -c, s, s], 0)
    shared["mla_ktb"] = np.concatenate([one, -s, s, c, c], 0)


def _diff_host(inputs, shared, g):
    w_in = g("diff_w_in")[0]
    ev = np.arange(0, 64, 2)
    od = ev + 1
    cols = []
    for h in range(8):
        for m in range(2):
            bq = h * 128 + m * 64
            bk = 1024 + h * 128 + m * 64
            cols += list(bq + ev) + list(bq + od) + list(bq + ev) + list(bq + od)
            cols += list(bk + ev) * 4
            cols += list(bk + od) * 4
    shared["diff_wx"] = np.ascontiguousarray(w_in[:, cols])
    shared["diff_wv"] = np.ascontiguousarray(w_in[:, 2048:3072])
    shared["diff_w_out"] = g("diff_w_out")[0]
    c, s = _rope_tables(64)
    shared["diff_qtab"] = np.concatenate([c, s, s, c], 0)
    shared["diff_kta"] = np.concatenate([c, -c, s, s], 0)
    shared["diff_ktb"] = np.concatenate([-s, s, c, c], 0)
    shared["diff_lam"] = np.ascontiguousarray(np.stack([g("diff_lambda_q1")[0], g("diff_lambda_k1")[0],
                                                        g("diff_lambda_q2")[0], g("diff_lambda_k2")[0]], axis=1))
    shared["diff_subln"] = g("diff_subln")[0].reshape(1, 128)


def _gla_host(inputs, shared, g):
    shared["gla_w_in"] = g("gla_w_in")[0]
    shared["gla_gw"] = np.ascontiguousarray(np.stack([g("gla_gate_w_fwd")[0], g("gla_gate_w_bwd")[0]], 0))
    shared["gla_gb"] = np.ascontiguousarray(np.stack([g("gla_gate_b_fwd")[0].reshape(4, 128), g("gla_gate_b_bwd")[0].reshape(4, 128)], 0))
    shared["gla_norm"] = g("gla_norm")[0].reshape(2, 128)
    shared["gla_w_out"] = g("gla_w_out")[0]


def _gdn_host(inputs, shared, g):
    w_in = g("gdn_w_in")[0]
    shared["gdn_w_in"] = w_in
    cols = []
    for kh in range(8):
        for base in (6144, 6160, 6176, 6192):
            cols += [base + 2 * kh, base + 2 * kh + 1]
    shared["gdn_wg"] = np.ascontiguousarray(w_in[:, cols])
    shared["gdn_conv"] = g("gdn_conv_w")[0].reshape(5, 32, 128)
    shared["gdn_hc"] = np.ascontiguousarray(np.concatenate([g("gdn_a_log_fwd")[0], g("gdn_a_log_bwd")[0],
                                                            g("gdn_dt_bias_fwd")[0], g("gdn_dt_bias_bwd")[0]]).reshape(1, 64))
    shared["gdn_norm"] = g("gdn_norm")[0].reshape(1, 128)
    shared["gdn_w_out"] = g("gdn_w_out")[0]

def make_in_maps(inputs):
    g = lambda k: np.ascontiguousarray(np.asarray(inputs[k], dtype=np.float32))
    shared = {
        "c_ctx": g("c_ctx").reshape(8, 128),
        "ada_w": g("ada_w"), "ada_b": g("ada_b").reshape(DEPTH, 48, 128),
        "ln1_g": g("ln1_g").reshape(DEPTH, 8, 128), "ln1_b": g("ln1_b").reshape(DEPTH, 8, 128),
        "ln2_g": g("ln2_g").reshape(DEPTH, 8, 128), "ln2_b": g("ln2_b").reshape(DEPTH, 8, 128),
        "mlp_w1": g("mlp_w1"), "mlp_w2": g("mlp_w2"),
        "mla_w_in": g("mla_w_in")[0], "mla_q_norm": g("mla_q_norm")[0].reshape(6, 128),
        "mla_kv_norm": g("mla_kv_norm")[0].reshape(2, 128),
        "mla_w_kvb": g("mla_w_kvb")[0], "mla_w_out": g("mla_w_out")[0],
    }
    _mla_host(inputs, shared, g)
    _diff_host(inputs, shared, g)
    _gla_host(inputs, shared, g)
    _gdn_host(inputs, shared, g)
    x, c, ctx = g("x"), g("c"), g("ctx")
    maps = []
    for b in range(8):
        m = dict(shared)
        m["x"] = x[b]
        m["ctx"] = ctx[b]
        m["c"] = c[b].reshape(8, 128)
        maps.append(m)
    return maps


def kernel(**inputs):
    nc, kb = build_program()
    maps = make_in_maps(inputs)
    res = run_bass_kernel_spmd(nc, maps, core_ids=list(range(8)))
    return np.stack([np.asarray(r["out"], dtype=np.float32) for r in res.results], axis=0)
```

```python
import math
import numpy as np
import concourse.bass as bass
import concourse.mybir as mybir
from concourse.bass_utils import run_bass_kernel_spmd
from contextlib import ExitStack

F32 = mybir.dt.float32
BF16 = mybir.dt.bfloat16
ALU = mybir.AluOpType
AF = mybir.ActivationFunctionType

DEPTH = 4
D = 1024
TL = 2048
TC = 256
T = TL + TC
ALPHA = (2 * DEPTH) ** 0.25
EPS = 1e-6
EPS_LN = EPS / (ALPHA * ALPHA)
TT = [(0, 512, 0), (512, 512, 0), (1024, 512, 0), (1536, 512, 0), (2048, 256, 1)]


class Res:
    __slots__ = ("name", "w", "r", "dsem", "dcnt")

    def __init__(self, name=""):
        self.name = name
        self.w = None
        self.r = {}
        self.dsem = None
        self.dcnt = 0


class Sched:
    EPOCH = 30000

    def __init__(self, nc, es):
        self.nc = nc
        self.es = es
        self.eng = {"pe": nc.tensor, "dve": nc.vector, "act": nc.scalar,
                    "pool": nc.gpsimd, "sp": nc.sync}
        self.cnt = {e: 0 for e in self.eng}
        self.cursem = {e: None for e in self.eng}
        self.last = {e: None for e in self.eng}
        self.seen = {e: {} for e in self.eng}
        self.nsem = 0
        self.ninst = 0
        self.owners = []
        self.out_events = []

    def newsem(self, name):
        self.nsem += 1
        return self.es.enter_context(self.nc.semaphore(f"{name}_{self.nsem}"))

    def _wait(self, e, ev):
        sem, val, _ = ev
        k = id(sem)
        if self.seen[e].get(k, 0) >= val:
            return
        self.eng[e].wait_ge(sem, val)
        self.seen[e][k] = val

    def _deps(self, e, reads, writes, same_ok):
        for r in reads:
            if r.w is not None and not (same_ok and r.w[2] == e):
                self._wait(e, r.w)
        for w in writes:
            if w.w is not None and not (same_ok and w.w[2] == e):
                self._wait(e, w.w)
            for ev in w.r.values():
                if not (same_ok and ev[2] == e):
                    self._wait(e, ev)

    def op(self, e, fn, reads=(), writes=(), same_ok=False):
        self._deps(e, reads, writes, same_ok)
        ins = fn(self.eng[e])
        if self.cnt[e] % self.EPOCH == 0:
            self.cursem[e] = self.newsem("c" + e)
        self.cnt[e] += 1
        val = (self.cnt[e] - 1) % self.EPOCH + 1
        sem = self.cursem[e]
        ins.then_inc(sem, 1)
        ev = (sem, val, e)
        self.last[e] = ev
        for r in reads:
            r.r[id(sem)] = ev
        for w in writes:
            w.w = ev
            w.r = {}
        self.ninst += 1
        return ev

    def dma(self, q, pairs, owner, reads=(), writes=(), **kw):
        self._deps(q, reads, writes, False)
        if owner.dsem is None:
            owner.dsem = self.newsem("d")
            self.owners.append(owner)
        if owner.dcnt > 0:
            self._wait(q, (owner.dsem, owner.dcnt, "dma"))
        for (o, i) in pairs:
            self.eng[q].dma_start(out=o, in_=i, **kw).then_inc(owner.dsem, 16)
            owner.dcnt += 16
        ev = (owner.dsem, owner.dcnt, "dma")
        for r in reads:
            r.r[id(owner.dsem)] = ev
        for w in writes:
            w.w = ev
            w.r = {}
        self.ninst += len(pairs)
        return ev

    def barrier(self):
        evs = [ev for ev in self.last.values() if ev is not None]
        evs += [(o.dsem, o.dcnt, "dma") for o in self.owners if o.dcnt > 0]
        for e in ("pe", "dve", "act", "pool", "sp"):
            for ev in evs:
                if ev[2] != e:
                    self._wait(e, ev)

    def finish(self):
        for ev in self.out_events:
            self._wait("sp", ev)


class KB:
    def __init__(self, nc, es, depth_run=DEPTH, mixers=True, dbg=False):
        self.nc, self.es = nc, es
        self.S = Sched(nc, es)
        self.depth_run = depth_run
        self.mixers = mixers
        self.dbg = dbg
        self.din = {}
        self._n = 0

    def sb(self, shape, dt, es=None, name=None):
        self._n += 1
        return (es or self.es).enter_context(self.nc.sbuf_tensor(name or f"t{self._n}", list(shape), dt))

    def dram_in(self, name, shape):
        t = self.nc.dram_tensor(name, list(shape), F32, kind="ExternalInput").ap()
        self.din[name] = t
        return t

    def bank(self):
        i = self.bank_i
        self.bank_i = (i + 1) % self.nrr
        return self.banks[i], self.bank_res[i]

    def declare(self):
        di = self.dram_in
        di("x", [TL, D]); di("ctx", [TC, D]); di("c", [8, 128]); di("c_ctx", [8, 128])
        di("ada_w", [DEPTH, D, 6 * D]); di("ada_b", [DEPTH, 48, 128])
        for n in ("ln1_g", "ln1_b", "ln2_g", "ln2_b"):
            di(n, [DEPTH, 8, 128])
        di("mlp_w1", [DEPTH, D, 4 * D]); di("mlp_w2", [DEPTH, 4 * D, D])
        di("mla_w_in", [D, 1056]); di("mla_q_norm", [6, 128]); di("mla_kv_norm", [2, 128])
        di("mla_w_kvb", [256, 2048]); di("mla_w_out", [D, D])
        di("mla_wqx", [768, 2048]); di("mla_wkr", [D, 256])
        di("mla_qtab", [128, T]); di("mla_kta", [128, T]); di("mla_ktb", [128, T])
        di("diff_wx", [D, 16 * 384]); di("diff_wv", [D, D]); di("diff_w_out", [D, D])
        di("diff_qtab", [128, T]); di("diff_kta", [128, T]); di("diff_ktb", [128, T])
        di("diff_lam", [64, 4]); di("diff_subln", [1, 128])
        di("gla_w_in", [D, 3104]); di("gla_gw", [2, 16, 512]); di("gla_gb", [2, 4, 128])
        di("gla_norm", [2, 128]); di("gla_w_out", [D, D])
        di("gdn_w_in", [D, 6208]); di("gdn_wg", [D, 64]); di("gdn_conv", [5, 32, 128])
        di("gdn_hc", [1, 64]); di("gdn_norm", [1, 128]); di("gdn_w_out", [2 * D, D])
        self.out = self.nc.dram_tensor("out", [TL, D], F32, kind="ExternalOutput").ap()
        if self.dbg:
            self.out_c = self.nc.dram_tensor("out_c", [TC, D], F32, kind="ExternalOutput").ap()

        nc = self.nc
        self.banks = [self.es.enter_context(nc.psum_tensor(f"bank{i}", [128, 512], F32)) for i in range(8)]
        self.bank_res = [Res(f"bank{i}") for i in range(8)]
        self.bank_i = 0
        self.nrr = 6
        self.hT = self.sb([128, 8, T], F32, name="hT")
        self.r_h = [Res(f"h{t}") for t in range(len(TT))]
        self.uT = self.sb([128, 8, T], BF16, name="uT")
        self.r_u = [Res(f"u{t}") for t in range(len(TT))]
        self.ident = self.sb([128, 128], F32, name="ident"); self.r_ident = Res("ident")
        self.identb = self.sb([128, 128], BF16, name="identb")
        self.ones = self.sb([128, 128], F32, name="ones")
        self.identr = self.sb([128, 128], F32, name="identr")
        self.sT = self.sb([128, 8, 2], F32, name="sT"); self.r_sT = Res("sT")
        self.mod = [self.sb([128, 48, 2], F32, name=f"mod{i}") for i in range(DEPTH)]
        self.r_mod = [Res(f"mod{i}") for i in range(DEPTH)]
        self.lnp = self.sb([128, 4, DEPTH, 8], F32, name="lnp"); self.r_lnp = Res("lnp")
        self.adab = self.sb([128, DEPTH, 48], F32, name="adab"); self.r_adab = Res("adab")
        self.vst = self.sb([64, 128], F32, name="vst"); self.r_vst = Res("vst")
        self.NW = 3
        self.wslot = [self.sb([128, 4096], BF16, name=f"wslot{i}") for i in range(self.NW)]
        self.r_wslot = [Res(f"wslot{i}") for i in range(self.NW)]
        self.w_i = 0

    def wnext(self):
        i = self.w_i
        self.w_i = (i + 1) % self.NW
        return self.wslot[i], self.r_wslot[i]

    def load_cols(self, src, n, dst, r_dst):
        S = self.S
        S.dma("sp", [(self.vst[0:n, :], src)], self.r_vst, writes=[self.r_vst])
        bk, rb = self.bank()
        S.op("pe", lambda e: e.transpose(bk[:, 0:n], self.vst[0:n, :], self.ident[0:n, 0:n]),
             reads=[self.r_vst, self.r_ident], writes=[rb], same_ok=True)
        S.op("dve", lambda e: e.tensor_copy(dst, bk[:, 0:n]), reads=[rb], writes=[r_dst])

    def setup(self):
        S, nc = self.S, self.nc
        S.op("pool", lambda e: e.memset(self.ident[:], 0.0), writes=[self.r_ident])
        S.op("pool", lambda e: e.affine_select(out=self.ident[:], in_=self.ident[:], compare_op=ALU.not_equal,
                                              fill=1.0, base=0, pattern=[[-1, 128]], channel_multiplier=1),
             reads=[self.r_ident], writes=[self.r_ident])
        S.op("dve", lambda e: e.tensor_copy(self.identb[:], self.ident[:]), reads=[self.r_ident], writes=[self.r_ident])
        S.op("dve", lambda e: e.memset(self.ones[:], 1.0), writes=[self.r_ident])
        S.op("dve", lambda e: e.tensor_copy(self.identr[:].bitcast(mybir.dt.float32r), self.ident[:]), reads=[self.r_ident], writes=[self.r_ident])
        for k, n in enumerate(("ln1_g", "ln1_b", "ln2_g", "ln2_b")):
            for i in range(DEPTH):
                self.load_cols(self.din[n][i], 8, self.lnp[:, k, i, :], self.r_lnp)
        for i in range(DEPTH):
            self.load_cols(self.din["ada_b"][i], 48, self.adab[:, i, :], self.r_adab)
        self.load_cols(self.din["c"], 8, self.sT[:, :, 0], self.r_sT)
        self.load_cols(self.din["c_ctx"], 8, self.sT[:, :, 1], self.r_sT)
        S.op("act", lambda e: e.activation(out=self.sT[:], in_=self.sT[:], func=AF.Silu), reads=[self.r_sT], writes=[self.r_sT])
        with ExitStack() as ph:
            xs = [self.sb([128, D], F32, es=ph) for _ in range(2)]
            r_xs = [Res("xs0"), Res("xs1")]
            for t in range(18):
                src = self.din["x"][t * 128:(t + 1) * 128, :] if t < 16 else self.din["ctx"][(t - 16) * 128:(t - 15) * 128, :]
                st, rs = xs[t % 2], r_xs[t % 2]
                S.dma("sp", [(st[:], src)], rs, writes=[rs])
                ti = min(t // 4, 4)
                for g in range(2):
                    bk, rb = self.bank()
                    for j in range(4):
                        S.op("pe", lambda e: e.transpose(bk[:, j * 128:(j + 1) * 128], st[:, (g * 4 + j) * 128:(g * 4 + j + 1) * 128], self.ident[:]),
                             reads=[rs, self.r_ident], writes=[rb], same_ok=True)
                    dst = self.hT[:, g * 4:(g + 1) * 4, t * 128:(t + 1) * 128]
                    srcp = bk[:, 0:512].rearrange("p (j n) -> p j n", j=4)
                    if g == 0:
                        S.op("dve", lambda e: e.tensor_copy(dst, srcp), reads=[rb], writes=[self.r_h[ti]])
                    else:
                        S.op("act", lambda e: e.copy(dst, srcp), reads=[rb], writes=[self.r_h[ti]])
            S.barrier()

    def mods(self, i):
        S = self.S
        aw = self.din["ada_w"][i]
        bk, rb = self.bank()
        with ExitStack() as ph:
            stg = [self.sb([128, 8, 256], F32, es=ph) for _ in range(2)]
            r_stg = [Res("as0"), Res("as1")]
            for blk in range(24):
                st, rs = stg[blk % 2], r_stg[blk % 2]
                S.dma("sp", [(st[:], aw[:, blk * 256:(blk + 1) * 256].rearrange("(k p) n -> p k n", p=128))], rs, writes=[rs])
                for cc in range(2):
                    c = blk * 2 + cc
                    for kc in range(8):
                        S.op("pe", lambda e: e.matmul(bk[:, c * 2:(c + 1) * 2], lhsT=st[:, kc, cc * 128:(cc + 1) * 128], rhs=self.sT[:, kc, :],
                                                      start=(kc == 0), stop=(kc == 7)),
                             reads=[rs, self.r_sT], writes=[rb], same_ok=True)
            m, rm = self.mod[i], self.r_mod[i]
            pv = bk[:, 0:96].rearrange("p (c l) -> p c l", l=2)
            for l in range(2):
                S.op("dve", lambda e: e.tensor_tensor(out=m[:, :, l], in0=pv[:, :, l], in1=self.adab[:, i, :], op=ALU.add),
                     reads=[rb, self.r_adab], writes=[rm])
            for c0 in (8, 32):
                S.op("dve", lambda e: e.tensor_scalar(out=m[:, c0:c0 + 8, :], in0=m[:, c0:c0 + 8, :], scalar1=1.0, scalar2=None, op0=ALU.add),
                     reads=[rm], writes=[rm])
            for c0 in (16, 40):
                S.op("dve", lambda e: e.tensor_scalar(out=m[:, c0:c0 + 8, :], in0=m[:, c0:c0 + 8, :], scalar1=1.0 / ALPHA, scalar2=None, op0=ALU.mult),
                     reads=[rm], writes=[rm])
            S.barrier()

    def mcol(self, i, which, kc, lc):
        return self.mod[i][:, which * 8 + kc, lc:lc + 1]

    def modulate_all(self, i, sub):
        S = self.S
        for ti, (t0, n, lc) in enumerate(TT):
            for kc in range(8):
                S.op("dve", lambda e: e.tensor_scalar(out=self.uT[:, kc, t0:t0 + n], in0=self.hT[:, kc, t0:t0 + n],
                                                      scalar1=self.mcol(i, 3 * sub + 1, kc, lc), scalar2=self.mcol(i, 3 * sub, kc, lc),
                                                      op0=ALU.mult, op1=ALU.add),
                     reads=[self.r_h[ti], self.r_mod[i]], writes=[self.r_u[ti]])

    def layer_norm(self, i, sub, nxt):
        S = self.S
        with ExitStack() as ph:
            sq = [self.sb([128, 512], F32, es=ph) for _ in range(2)]; r_sq = [Res(), Res()]
            mean = self.sb([128, 512], F32, es=ph); r_mean = Res()
            msq = self.sb([128, 512], F32, es=ph); r_msq = Res()
            rstd = self.sb([128, 512], F32, es=ph); r_rstd = Res()
            tmp = [self.sb([128, 512], F32, es=ph) for _ in range(2)]; r_tmp = [Res(), Res()]
            epsc = self.sb([128, 1], F32, es=ph); r_eps = Res()
            S.op("pool", lambda e: e.memset(epsc[:], EPS_LN), writes=[r_eps])
            for ti, (t0, n, lc) in enumerate(TT):
                rh = self.r_h[ti]
                b1, rb1 = self.bank()
                b2, rb2 = self.bank()
                for kc in range(8):
                    z = self.hT[:, kc, t0:t0 + n]
                    s_, rs_ = sq[kc % 2], r_sq[kc % 2]
                    S.op("act", lambda e: e.activation(out=s_[:, 0:n], in_=z, func=AF.Square), reads=[rh], writes=[rs_])
                    S.op("pe", lambda e: e.matmul(b1[:, 0:n], lhsT=self.ones[:], rhs=z, start=(kc == 0), stop=(kc == 7)),
                         reads=[rh, self.r_ident], writes=[rb1], same_ok=True)
                    S.op("pe", lambda e: e.matmul(b2[:, 0:n], lhsT=self.ones[:], rhs=s_[:, 0:n], start=(kc == 0), stop=(kc == 7)),
                         reads=[rs_, self.r_ident], writes=[rb2], same_ok=True)
                S.op("act", lambda e: e.activation(out=mean[:, 0:n], in_=b1[:, 0:n], func=AF.Copy, scale=1.0 / D), reads=[rb1], writes=[r_mean])
                S.op("pool", lambda e: e.tensor_tensor(out=msq[:, 0:n], in0=mean[:, 0:n], in1=mean[:, 0:n], op=ALU.mult), reads=[r_mean], writes=[r_msq])
                S.op("dve", lambda e: e.scalar_tensor_tensor(out=rstd[:, 0:n], in0=b2[:, 0:n], scalar=1.0 / D, in1=msq[:, 0:n],
                                                             op0=ALU.mult, op1=ALU.subtract), reads=[rb2, r_msq], writes=[r_rstd])
                S.op("act", lambda e: e.activation(out=rstd[:, 0:n], in_=rstd[:, 0:n], func=AF.Sqrt, bias=epsc[:, 0:1], scale=1.0),
                     reads=[r_rstd, r_eps], writes=[r_rstd])
                S.op("dve", lambda e: e.reciprocal(out=rstd[:, 0:n], in_=rstd[:, 0:n]), reads=[r_rstd], writes=[r_rstd])
                for kc in range(8):
                    z = self.hT[:, kc, t0:t0 + n]
                    tp, rtp = tmp[kc % 2], r_tmp[kc % 2]
                    S.op("pool", lambda e: e.tensor_tensor(out=tp[:, 0:n], in0=z, in1=mean[:, 0:n], op=ALU.subtract), reads=[rh, r_mean], writes=[rtp])
                    S.op("dve", lambda e: e.tensor_tensor(out=tp[:, 0:n], in0=tp[:, 0:n], in1=rstd[:, 0:n], op=ALU.mult), reads=[rtp, r_rstd], writes=[rtp])
                    S.op("act", lambda e: e.activation(out=z, in_=tp[:, 0:n], func=AF.Identity,
                                                       bias=self.lnp[:, 2 * sub + 1, i, kc:kc + 1], scale=self.lnp[:, 2 * sub, i, kc:kc + 1]),
                         reads=[rtp, self.r_lnp], writes=[rh])
                    if nxt is not None:
                        ni, nsub = nxt
                        S.op("dve", lambda e: e.tensor_scalar(out=self.uT[:, kc, t0:t0 + n], in0=z,
                                                              scalar1=self.mcol(ni, 3 * nsub + 1, kc, lc), scalar2=self.mcol(ni, 3 * nsub, kc, lc),
                                                              op0=ALU.mult, op1=ALU.add),
                             reads=[rh, self.r_mod[ni]], writes=[self.r_u[ti]])
            S.barrier()

    def mlp(self, i):
        S = self.S
        w1 = self.din["mlp_w1"][i]
        w2 = self.din["mlp_w2"][i]
        with ExitStack() as ph:
            ab = [self.sb([128, 4, 512], BF16, es=ph) for _ in range(2)]; r_ab = [Res(), Res()]
            rl = [self.sb([128, 512], BF16, es=ph) for _ in range(3)]; r_rl = [Res(), Res(), Res()]
            rli = 0
            step = 0
            for j in range(8):
                wa, r_wa = self.wnext()
                wb, r_wb = self.wnext()
                S.dma("pool", [(wa[:].rearrange("p (k n) -> p k n", k=8), w1[:, j * 512:(j + 1) * 512].rearrange("(k p) n -> p k n", p=128))],
                      r_wa, writes=[r_wa])
                S.dma("pool", [(wb[:].rearrange("p (k n) -> p k n", k=4), w2[j * 512:(j + 1) * 512, :].rearrange("(k p) n -> p k n", p=128))],
                      r_wb, writes=[r_wb])
                wav = wa[:].rearrange("p (k n) -> p k n", k=8)
                wbv = wb[:].rearrange("p (k n) -> p k n", k=4)
                for ti, (t0, n, lc) in enumerate(TT):
                    a_, r_a = ab[step % 2], r_ab[step % 2]
                    step += 1
                    for hc in range(4):
                        bk, rb = self.bank()
                        for kc in range(8):
                            S.op("pe", lambda e: e.matmul(bk[:, 0:n], lhsT=wav[:, kc, hc * 128:(hc + 1) * 128], rhs=self.uT[:, kc, t0:t0 + n],
                                                          start=(kc == 0), stop=(kc == 7)),
                                 reads=[r_wa, self.r_u[ti]], writes=[rb], same_ok=True)
                        r_, rr_ = rl[rli % 3], r_rl[rli % 3]
                        rli += 1
                        S.op("act", lambda e: e.activation(out=r_[:, 0:n], in_=bk[:, 0:n], func=AF.Relu), reads=[rb], writes=[rr_])
                        S.op("pool", lambda e: e.tensor_tensor(out=a_[:, hc, 0:n], in0=r_[:, 0:n], in1=r_[:, 0:n], op=ALU.mult),
                             reads=[rr_], writes=[r_a])
                    for oc in range(8):
                        bk, rb = self.bank()
                        for kc in range(4):
                            S.op("pe", lambda e: e.matmul(bk[:, 0:n], lhsT=wbv[:, kc, oc * 128:(oc + 1) * 128], rhs=a_[:, kc, 0:n],
                                                          start=(kc == 0), stop=(kc == 3)),
                                 reads=[r_wb, r_a], writes=[rb], same_ok=True)
                        hz = self.hT[:, oc, t0:t0 + n]
                        S.op("dve", lambda e: e.scalar_tensor_tensor(out=hz, in0=bk[:, 0:n], scalar=self.mcol(i, 5, oc, lc), in1=hz,
                                                                     op0=ALU.mult, op1=ALU.add),
                             reads=[rb, self.r_h[ti], self.r_mod[i]], writes=[self.r_h[ti]])
            S.barrier()

    def store_out(self):
        S = self.S
        with ExitStack() as ph:
            os_ = [self.sb([128, D], F32, es=ph) for _ in range(2)]
            r_os = [Res("os0"), Res("os1")]
            nt = 18 if self.dbg else 16
            for t in range(nt):
                st, rs = os_[t % 2], r_os[t % 2]
                ti = min(t // 4, 4)
                for g in range(2):
                    bk, rb = self.bank()
                    for j in range(4):
                        S.op("pe", lambda e: e.transpose(bk[:, j * 128:(j + 1) * 128], self.hT[:, g * 4 + j, t * 128:(t + 1) * 128], self.ident[:]),
                             reads=[self.r_h[ti], self.r_ident], writes=[rb], same_ok=True)
                    if g == 0:
                        S.op("dve", lambda e: e.tensor_copy(st[:, 0:512], bk[:, 0:512]), reads=[rb], writes=[rs])
                    else:
                        S.op("act", lambda e: e.copy(st[:, 512:1024], bk[:, 0:512]), reads=[rb], writes=[rs])
                dst = self.out[t * 128:(t + 1) * 128, :] if t < 16 else self.out_c[(t - 16) * 128:(t - 15) * 128, :]
                ev = S.dma("sp", [(dst, st[:])], rs, reads=[rs])
                S.out_events.append(ev)
            S.finish()

    def build(self):
        self.declare()
        self.setup()
        self.mods(0)
        self.modulate_all(0, 0)
        for i in range(self.depth_run):
            if i + 1 < DEPTH:
                self.mods(i + 1)
            if self.mixers:
                self.mixer(i)
            self.layer_norm(i, 0, (i, 1))
            self.mlp(i)
            self.layer_norm(i, 1, (i + 1, 0) if i + 1 < DEPTH else None)
        self.store_out()


    def mixer(self, i):
        if i == 0:
            self.mla(i)
        elif i == 1:
            self.diff(i)
        elif i == 2:
            self.gla(i)
        elif i == 3:
            self.gdn(i)

    def mla(self, i):
        S = self.S
        SCALE = 96.0 ** -0.5
        with ExitStack() as ph:
            kp = [self.sb([128, T], BF16, es=ph) for _ in range(2)]; r_kp = [Res("kp0"), Res("kp1")]
            qtab = self.sb([128, T], F32, es=ph); r_qtab = Res("qtab")
            opad = self.sb([128, 2, 128], BF16, es=ph); r_opad = Res("opad")
            nrm = self.sb([128, 8], F32, es=ph); r_nrm = Res("nrm")
            epsc = self.sb([128, 1], F32, es=ph); r_eps = Res("eps")
            S.dma("sp", [(qtab[:], self.din["mla_qtab"])], r_qtab, writes=[r_qtab])
            S.op("pool", lambda e: e.memset(opad[:], 0.0), writes=[r_opad])
            S.op("pool", lambda e: e.memset(opad[:, 0, 0:64], 1.0), reads=[r_opad], writes=[r_opad])
            S.op("pool", lambda e: e.memset(opad[:, 1, 64:128], 1.0), reads=[r_opad], writes=[r_opad])
            S.op("pool", lambda e: e.memset(epsc[:], EPS), writes=[r_eps])
            self.load_cols(self.din["mla_q_norm"], 6, nrm[:, 0:6], r_nrm)
            self.load_cols(self.din["mla_kv_norm"], 2, nrm[:, 6:8], r_nrm)
            with ExitStack() as p1:
                raw = self.sb([128, 8, 512], F32, es=p1); r_raw = Res("raw")
                sq = [self.sb([128, 512], F32, es=p1) for _ in range(2)]; r_sq = [Res(), Res()]
                rs = self.sb([128, 2, 512], F32, es=p1); r_rs = Res("rs")
                kta = self.sb([128, 512], F32, es=p1); r_kta = Res("kta")
                ktb = self.sb([128, 512], F32, es=p1); r_ktb = Res("ktb")
                t1 = self.sb([128, 512], F32, es=p1); r_t1 = Res("t1")
                t2 = self.sb([128, 512], F32, es=p1); r_t2 = Res("t2")
                wi = []
                for blk in range(2):
                    w_, r_w = self.wnext()
                    S.dma("pool", [(w_[:].rearrange("p (k n) -> p k n", k=8),
                                    self.din["mla_w_in"][:, blk * 512:(blk + 1) * 512].rearrange("(k p) n -> p k n", p=128))], r_w, writes=[r_w])
                    wi.append((w_[:].rearrange("p (k n) -> p k n", k=8), r_w))
                w_, r_wk = self.wnext()
                wkr = w_[:, 0:2048].rearrange("p (k n) -> p k n", k=8)
                S.dma("pool", [(wkr, self.din["mla_wkr"].rearrange("(k p) n -> p k n", p=128))], r_wk, writes=[r_wk])
                for ti, (t0, n, lc) in enumerate(TT):
                    ru = self.r_u[ti]
                    S.dma("sp", [(kta[:, 0:n], self.din["mla_kta"][:, t0:t0 + n])], r_kta, writes=[r_kta])
                    S.dma("sp", [(ktb[:, 0:n], self.din["mla_ktb"][:, t0:t0 + n])], r_ktb, writes=[r_ktb])
                    bA, rbA = self.bank()
                    bB, rbB = self.bank()
                    for kc in range(8):
                        S.op("pe", lambda e: e.matmul(bA[:, 0:n], lhsT=wkr[:, kc, 0:128], rhs=self.uT[:, kc, t0:t0 + n], start=(kc == 0), stop=(kc == 7)),
                             reads=[r_wk, ru], writes=[rbA], same_ok=True)
                    for kc in range(8):
                        S.op("pe", lambda e: e.matmul(bB[:, 0:n], lhsT=wkr[:, kc, 128:256], rhs=self.uT[:, kc, t0:t0 + n], start=(kc == 0), stop=(kc == 7)),
                             reads=[r_wk, ru], writes=[rbB], same_ok=True)
                    S.op("dve", lambda e: e.tensor_tensor(out=t1[64:128, 0:n], in0=bA[64:128, 0:n], in1=kta[64:128, 0:n], op=ALU.mult),
                         reads=[rbA, r_kta], writes=[r_t1])
                    S.op("dve", lambda e: e.tensor_tensor(out=t2[64:128, 0:n], in0=bB[64:128, 0:n], in1=ktb[64:128, 0:n], op=ALU.mult),
                         reads=[rbB, r_ktb], writes=[r_t2])
                    S.op("pool", lambda e: e.tensor_tensor(out=kp[0][64:128, t0:t0 + n], in0=t1[64:128, 0:n], in1=t2[64:128, 0:n], op=ALU.add),
                         reads=[r_t1, r_t2], writes=[r_kp[0]])
                    S.op("pool", lambda e: e.tensor_copy(kp[1][64:128, t0:t0 + n], kp[0][64:128, t0:t0 + n]), reads=[r_kp[0]], writes=[r_kp[1]])
                    for oc in range(8):
                        wv, r_w = wi[oc // 4]
                        bk, rb = self.bank()
                        for kc in range(8):
                            S.op("pe", lambda e: e.matmul(bk[:, 0:n], lhsT=wv[:, kc, (oc % 4) * 128:(oc % 4 + 1) * 128], rhs=self.uT[:, kc, t0:t0 + n],
                                                          start=(kc == 0), stop=(kc == 7)),
                                 reads=[r_w, ru], writes=[rb], same_ok=True)
                        if oc % 2 == 0:
                            S.op("dve", lambda e: e.tensor_copy(raw[:, oc, 0:n], bk[:, 0:n]), reads=[rb], writes=[r_raw])
                        else:
                            S.op("act", lambda e: e.copy(raw[:, oc, 0:n], bk[:, 0:n]), reads=[rb], writes=[r_raw])
                    bq, rbq = self.banks[6], self.bank_res[6]
                    bkv, rbkv = self.banks[7], self.bank_res[7]
                    for oc in range(8):
                        s_, rs_ = sq[oc % 2], r_sq[oc % 2]
                        S.op("act", lambda e: e.activation(out=s_[:, 0:n], in_=raw[:, oc, 0:n], func=AF.Square), reads=[r_raw], writes=[rs_])
                        if oc < 6:
                            S.op("pe", lambda e: e.matmul(bq[:, 0:n], lhsT=self.ones[:], rhs=s_[:, 0:n], start=(oc == 0), stop=(oc == 5)),
                                 reads=[rs_, self.r_ident], writes=[rbq], same_ok=True)
                        else:
                            S.op("pe", lambda e: e.matmul(bkv[:, 0:n], lhsT=self.ones[:], rhs=s_[:, 0:n], start=(oc == 6), stop=(oc == 7)),
                                 reads=[rs_, self.r_ident], writes=[rbkv], same_ok=True)
                    for g, (bb, rbb, dim) in enumerate(((bq, rbq, 768.0), (bkv, rbkv, 256.0))):
                        S.op("act", lambda e: e.activation(out=rs[:, g, 0:n], in_=bb[:, 0:n], func=AF.Sqrt, bias=epsc[:, 0:1], scale=1.0 / dim),
                             reads=[rbb, r_eps], writes=[r_rs])
                        S.op("dve", lambda e: e.reciprocal(out=rs[:, g, 0:n], in_=rs[:, g, 0:n]), reads=[r_rs], writes=[r_rs])
                    for oc in range(8):
                        g = 0 if oc < 6 else 1
                        S.op("dve", lambda e: e.scalar_tensor_tensor(out=self.uT[:, oc, t0:t0 + n], in0=raw[:, oc, 0:n], scalar=nrm[:, oc:oc + 1],
                                                                     in1=rs[:, g, 0:n], op0=ALU.mult, op1=ALU.mult),
                             reads=[r_raw, r_nrm, r_rs], writes=[ru])
                S.barrier()
            with ExitStack() as p2:
                qp = [self.sb([128, T], BF16, es=p2) for _ in range(2)]; r_qp = [Res("qp0"), Res("qp1")]
                vp = [self.sb([128, 18, 128], BF16, es=p2) for _ in range(2)]; r_vp = [Res("vp0"), Res("vp1")]
                pt = [self.sb([128, 512], BF16, es=p2) for _ in range(4)]; r_pt = [Res() for _ in range(4)]
                rden = [self.sb([128, 512], F32, es=p2) for _ in range(2)]; r_rden = [Res("rden0"), Res("rden1")]
                attn_ctr = [0]
                pending = [None]
                self.nrr = 4
                self.bank_i = 0
                opr = [self.sb([128, 512], BF16, es=p2) for _ in range(2)]; r_opr = [Res(), Res()]
                wo = [self.sb([128, D], BF16, es=p2) for _ in range(2)]; r_wo = [Res("wo0"), Res("wo1")]
                for par in range(2):
                    S.op("pool", lambda e: e.memset(vp[par][:], 0.0), writes=[r_vp[par]])
                pti = 0
                wq = wkv = None
                for pair in range(8):
                    S.dma("pool", [(wo[pair % 2][:], self.din["mla_w_out"][pair * 128:(pair + 1) * 128, :])], r_wo[pair % 2], writes=[r_wo[pair % 2]])
                    for par in range(2):
                        h = pair * 2 + par
                        hl = h % 4
                        if hl == 0:
                            w_, r_wq = self.wnext()
                            wq = w_[:, 0:3072].rearrange("p (k n) -> p k n", k=6)
                            wkv = w_[:, 3072:4096].rearrange("p (k n) -> p k n", k=2)
                            S.dma("pool", [(wq, self.din["mla_wqx"][:, h * 128:(h + 4) * 128].rearrange("(k p) n -> p k n", p=128)),
                                           (wkv, self.din["mla_w_kvb"][:, h * 128:(h + 4) * 128].rearrange("(k p) n -> p k n", p=128))],
                                  r_wq, writes=[r_wq])
                        for ti, (t0, n, lc) in enumerate(TT):
                            ru = self.r_u[ti]
                            bk, rb = self.bank()
                            for kc in range(6):
                                S.op("pe", lambda e: e.matmul(bk[:, 0:n], lhsT=wq[:, kc, hl * 128:(hl + 1) * 128], rhs=self.uT[:, kc, t0:t0 + n],
                                                              start=(kc == 0), stop=(kc == 5)),
                                     reads=[r_wq, ru], writes=[rb], same_ok=True)
                            S.op("dve", lambda e: e.tensor_tensor(out=qp[par][:, t0:t0 + n], in0=bk[:, 0:n], in1=qtab[:, t0:t0 + n], op=ALU.mult),
                                 reads=[rb, r_qtab], writes=[r_qp[par]])
                            bk, rb = self.bank()
                            for kc in range(2):
                                S.op("pe", lambda e: e.matmul(bk[0:64, 0:n], lhsT=wkv[:, kc, hl * 128:hl * 128 + 64], rhs=self.uT[:, 6 + kc, t0:t0 + n],
                                                              start=(kc == 0), stop=(kc == 1)),
                                     reads=[r_wq, ru], writes=[rb], same_ok=True)
                            S.op("dve", lambda e: e.tensor_copy(kp[par][0:64, t0:t0 + n], bk[0:64, 0:n]), reads=[rb], writes=[r_kp[par]])
                        for g0 in range(0, 18, 8):
                            ng = min(8, 18 - g0)
                            bk, rb = self.bank()
                            for jt in range(ng):
                                kt = g0 + jt
                                for kc in range(2):
                                    S.op("pe", lambda e: e.matmul(bk[:, jt * 64:(jt + 1) * 64], lhsT=self.uT[:, 6 + kc, kt * 128:(kt + 1) * 128],
                                                                  rhs=wkv[:, kc, hl * 128 + 64:hl * 128 + 128], start=(kc == 0), stop=(kc == 1)),
                                         reads=[r_wq] + self.r_u, writes=[rb], same_ok=True)
                            S.op("dve", lambda e: e.tensor_copy(vp[par][:, g0:g0 + ng, par * 64:par * 64 + 64],
                                                                bk[:, 0:ng * 64].rearrange("p (j d) -> p j d", d=64)), reads=[rb], writes=[r_vp[par]])
                    for ti, (t0, n, lc) in enumerate(TT):
                        kts = list(range(18)) if lc == 0 else [16, 17]
                        items = [(par, kt) for par in range(2) for kt in kts]
                        nb_ = attn_ctr[0] % 2
                        attn_ctr[0] += 1
                        num, r_num = self.banks[4 + 2 * nb_], self.bank_res[4 + 2 * nb_]
                        den, r_den = self.banks[5 + 2 * nb_], self.bank_res[5 + 2 * nb_]
                        sbanks = {}

                        def issue_score(ix):
                            par, kt = items[ix]
                            bk, rb = self.bank()
                            S.op("pe", lambda e: e.matmul(bk[:, 0:n], lhsT=kp[par][:, kt * 128:(kt + 1) * 128], rhs=qp[par][:, t0:t0 + n], start=True, stop=True),
                                 reads=[r_kp[par], r_qp[par]], writes=[rb], same_ok=True)
                            sbanks[ix] = (bk, rb)
                        for ix in range(min(2, len(items))):
                            issue_score(ix)
                        for ix, (par, kt) in enumerate(items):
                            bk, rb = sbanks.pop(ix)
                            p_, rp_ = pt[pti % 4], r_pt[pti % 4]
                            pti += 1
                            S.op("act", lambda e: e.activation(out=p_[:, 0:n], in_=bk[:, 0:n], func=AF.Exp, scale=SCALE), reads=[rb], writes=[rp_])
                            if ix + 2 < len(items):
                                issue_score(ix + 2)
                            first = (ix == 0)
                            last = (ix == len(items) - 1)
                            S.op("pe", lambda e: e.matmul(num[:, 0:n], lhsT=vp[par][:, kt, :], rhs=p_[:, 0:n], start=first, stop=last),
                                 reads=[r_vp[par], rp_], writes=[r_num], same_ok=True)
                            S.op("pe", lambda e: e.matmul(den[:, 0:n], lhsT=opad[:, par, :], rhs=p_[:, 0:n], start=first, stop=last),
                                 reads=[r_opad, rp_], writes=[r_den], same_ok=True)

                        def epilogue(ti=ti, t0=t0, n=n, lc=lc, num=num, den=den, r_num=r_num, r_den=r_den, pair=pair, k=attn_ctr[0]):
                            rd_, rrd_ = rden[k % 2], r_rden[k % 2]
                            S.op("dve", lambda e: e.reciprocal(out=rd_[:, 0:n], in_=den[:, 0:n]), reads=[r_den], writes=[rrd_])
                            o_, ro_ = opr[k % 2], r_opr[k % 2]
                            S.op("dve", lambda e: e.tensor_tensor(out=o_[:, 0:n], in0=num[:, 0:n], in1=rd_[:, 0:n], op=ALU.mult),
                                 reads=[r_num, rrd_], writes=[ro_])
                            for oc in range(8):
                                bk, rb = self.bank()
                                S.op("pe", lambda e: e.matmul(bk[:, 0:n], lhsT=wo[pair % 2][:, oc * 128:(oc + 1) * 128], rhs=o_[:, 0:n], start=True, stop=True),
                                     reads=[r_wo[pair % 2], ro_], writes=[rb], same_ok=True)
                                hz = self.hT[:, oc, t0:t0 + n]
                                S.op("dve", lambda e: e.scalar_tensor_tensor(out=hz, in0=bk[:, 0:n], scalar=self.mcol(i, 2, oc, lc), in1=hz,
                                                                             op0=ALU.mult, op1=ALU.add),
                                     reads=[rb, self.r_h[ti], self.r_mod[i]], writes=[self.r_h[ti]])
                        if pending[0] is not None:
                            pending[0]()
                        pending[0] = epilogue
                if pending[0] is not None:
                    pending[0]()
                S.barrier()
        self.nrr = 6
        self.bank_i = 0

    def diff(self, i):
        S = self.S
        SCALE = 64.0 ** -0.5
        lam_init = 0.8 - 0.6 * math.exp(-0.3 * i)
        self.nrr = 4
        self.bank_i = 0
        with ExitStack() as ph:
            qp = [self.sb([128, T], BF16, es=ph) for _ in range(2)]; r_qp = [Res("qp0"), Res("qp1")]
            kp = [self.sb([128, T], BF16, es=ph) for _ in range(2)]; r_kp = [Res("kp0"), Res("kp1")]
            vp = self.sb([128, 18, 128], BF16, es=ph); r_vp = Res("vp")
            qtab = self.sb([128, 512], F32, es=ph); r_qtab = Res("qtab")
            kta = self.sb([128, 512], F32, es=ph); r_kta = Res("kta")
            ktb = self.sb([128, 512], F32, es=ph); r_ktb = Res("ktb")
            t1 = self.sb([128, 512], F32, es=ph); r_t1 = Res("t1")
            t2 = self.sb([128, 512], F32, es=ph); r_t2 = Res("t2")
            t1s = [self.sb([128, 512], F32, es=ph) for _ in range(2)]; r_t1s = [Res("t1s0"), Res("t1s1")]
            dctr = [0]
            pend1 = [None]
            pt = [self.sb([128, 512], BF16, es=ph) for _ in range(4)]; r_pt = [Res() for _ in range(4)]
            rd = self.sb([128, 2, 512], F32, es=ph); r_rd0 = Res("rd0"); r_rd1 = Res("rd1")
            onb = self.sb([128, 128], BF16, es=ph); r_onb = Res("onb")
            on_ = [self.sb([128, 512], BF16, es=ph) for _ in range(2)]; r_on = [Res(), Res()]
            wo = [self.sb([128, D], BF16, es=ph) for _ in range(2)]; r_wo = [Res("wo0"), Res("wo1")]
            lam = self.sb([128, 4], F32, es=ph); r_lam = Res("lam")
            lv = self.sb([64, 4], F32, es=ph); r_lv = Res("lv")
            sub = self.sb([128, 1], F32, es=ph); r_sub = Res("sub")
            epsc = self.sb([128, 1], F32, es=ph); r_eps = Res("eps")
            S.op("pool", lambda e: e.memset(epsc[:], EPS), writes=[r_eps])
            S.op("pool", lambda e: e.memset(onb[:], 1.0), writes=[r_onb])
            S.dma("sp", [(lv[:], self.din["diff_lam"])], r_lv, writes=[r_lv])
            S.op("dve", lambda e: e.tensor_tensor(out=lv[:, 0:1], in0=lv[:, 0:1], in1=lv[:, 1:2], op=ALU.mult), reads=[r_lv], writes=[r_lv])
            S.op("dve", lambda e: e.tensor_tensor(out=lv[:, 1:2], in0=lv[:, 2:3], in1=lv[:, 3:4], op=ALU.mult), reads=[r_lv], writes=[r_lv])
            bk, rb = self.bank()
            S.op("pe", lambda e: e.matmul(bk[:, 0:2], lhsT=self.ones[0:64, :], rhs=lv[:, 0:2], start=True, stop=True),
                 reads=[r_lv, self.r_ident], writes=[rb], same_ok=True)
            S.op("act", lambda e: e.activation(out=lam[:, 0:2], in_=bk[:, 0:2], func=AF.Exp), reads=[rb], writes=[r_lam])
            S.op("dve", lambda e: e.scalar_tensor_tensor(out=lam[:, 2:3], in0=lam[:, 1:2], scalar=-lam_init, in1=lam[:, 0:1], op0=ALU.add, op1=ALU.subtract),
                 reads=[r_lam], writes=[r_lam])
            self.load_cols(self.din["diff_subln"], 1, sub[:, 0:1], r_sub)
            S.op("dve", lambda e: e.tensor_scalar(out=sub[:], in0=sub[:], scalar1=1.0 - lam_init, scalar2=None, op0=ALU.mult), reads=[r_sub], writes=[r_sub])
            nums = [(self.banks[4], self.bank_res[4]), (self.banks[5], self.bank_res[5])]
            dens = [(self.banks[6], self.bank_res[6]), (self.banks[7], self.bank_res[7])]
            pti = 0
            pti = 0
            for h in range(8):
                S.dma("pool", [(wo[h % 2][:], self.din["diff_w_out"][h * 128:(h + 1) * 128, :])], r_wo[h % 2], writes=[r_wo[h % 2]])
                wv = None
                for m in range(2):
                    mi = h * 2 + m
                    w_, r_w = self.wnext()
                    wx = w_[:, 0:3072].rearrange("p (k n) -> p k n", k=8)
                    prs = [(wx, self.din["diff_wx"][:, mi * 384:(mi + 1) * 384].rearrange("(k p) n -> p k n", p=128))]
                    if m == 0:
                        wv = w_[:, 3072:4096].rearrange("p (k n) -> p k n", k=8)
                        r_wv = r_w
                        prs.append((wv, self.din["diff_wv"][:, h * 128:(h + 1) * 128].rearrange("(k p) n -> p k n", p=128)))
                    S.dma("pool", prs, r_w, writes=[r_w])
                    for ti, (t0, n, lc) in enumerate(TT):
                        ru = self.r_u[ti]
                        S.dma("sp", [(qtab[:, 0:n], self.din["diff_qtab"][:, t0:t0 + n])], r_qtab, writes=[r_qtab])
                        S.dma("sp", [(kta[:, 0:n], self.din["diff_kta"][:, t0:t0 + n])], r_kta, writes=[r_kta])
                        S.dma("sp", [(ktb[:, 0:n], self.din["diff_ktb"][:, t0:t0 + n])], r_ktb, writes=[r_ktb])
                        bq, rbq = self.bank()
                        for kc in range(8):
                            S.op("pe", lambda e: e.matmul(bq[:, 0:n], lhsT=wx[:, kc, 0:128], rhs=self.uT[:, kc, t0:t0 + n], start=(kc == 0), stop=(kc == 7)),
                                 reads=[r_w, ru], writes=[rbq], same_ok=True)
                        S.op("dve", lambda e: e.tensor_tensor(out=qp[m][:, t0:t0 + n], in0=bq[:, 0:n], in1=qtab[:, 0:n], op=ALU.mult),
                             reads=[rbq, r_qtab], writes=[r_qp[m]])
                        bA, rbA = self.bank()
                        for kc in range(8):
                            S.op("pe", lambda e: e.matmul(bA[:, 0:n], lhsT=wx[:, kc, 128:256], rhs=self.uT[:, kc, t0:t0 + n], start=(kc == 0), stop=(kc == 7)),
                                 reads=[r_w, ru], writes=[rbA], same_ok=True)
                        bB, rbB = self.bank()
                        for kc in range(8):
                            S.op("pe", lambda e: e.matmul(bB[:, 0:n], lhsT=wx[:, kc, 256:384], rhs=self.uT[:, kc, t0:t0 + n], start=(kc == 0), stop=(kc == 7)),
                                 reads=[r_w, ru], writes=[rbB], same_ok=True)
                        S.op("dve", lambda e: e.tensor_tensor(out=t1[:, 0:n], in0=bA[:, 0:n], in1=kta[:, 0:n], op=ALU.mult), reads=[rbA, r_kta], writes=[r_t1])
                        S.op("dve", lambda e: e.tensor_tensor(out=t2[:, 0:n], in0=bB[:, 0:n], in1=ktb[:, 0:n], op=ALU.mult), reads=[rbB, r_ktb], writes=[r_t2])
                        S.op("pool", lambda e: e.tensor_tensor(out=kp[m][:, t0:t0 + n], in0=t1[:, 0:n], in1=t2[:, 0:n], op=ALU.add),
                             reads=[r_t1, r_t2], writes=[r_kp[m]])
                for g0 in range(0, 18, 4):
                    ng = min(4, 18 - g0)
                    bk, rb = self.bank()
                    for jt in range(ng):
                        kt = g0 + jt
                        for kc in range(8):
                            S.op("pe", lambda e: e.matmul(bk[:, jt * 128:(jt + 1) * 128], lhsT=self.uT[:, kc, kt * 128:(kt + 1) * 128],
                                                          rhs=wv[:, kc, :], start=(kc == 0), stop=(kc == 7)),
                                 reads=[r_wv] + self.r_u, writes=[rb], same_ok=True)
                    S.op("act", lambda e: e.copy(vp[:, g0:g0 + ng, :], bk[:, 0:ng * 128].rearrange("p (j d) -> p j d", d=128)), reads=[rb], writes=[r_vp])
                for ti, (t0, n, lc) in enumerate(TT):
                    kts = list(range(18)) if lc == 0 else [16, 17]
                    k_ = dctr[0]
                    dctr[0] += 1
                    t1_, rt1_ = t1s[k_ % 2], r_t1s[k_ % 2]

                    def attend(m):
                        nonlocal pti
                        num, r_num = nums[m]
                        den, r_den = dens[m]
                        sbanks = {}

                        def issue_score(ix):
                            kt = kts[ix]
                            bk, rb = self.bank()
                            S.op("pe", lambda e: e.matmul(bk[:, 0:n], lhsT=kp[m][:, kt * 128:(kt + 1) * 128], rhs=qp[m][:, t0:t0 + n], start=True, stop=True),
                                 reads=[r_kp[m], r_qp[m]], writes=[rb], same_ok=True)
                            sbanks[ix] = (bk, rb)
                        for ix in range(min(2, len(kts))):
                            issue_score(ix)
                        for ix, kt in enumerate(kts):
                            bk, rb = sbanks.pop(ix)
                            p_, rp_ = pt[pti % 4], r_pt[pti % 4]
                            pti += 1
                            S.op("act", lambda e: e.activation(out=p_[:, 0:n], in_=bk[:, 0:n], func=AF.Exp, scale=SCALE), reads=[rb], writes=[rp_])
                            if ix + 2 < len(kts):
                                issue_score(ix + 2)
                            first, last = (ix == 0), (ix == len(kts) - 1)
                            S.op("pe", lambda e: e.matmul(num[:, 0:n], lhsT=vp[:, kt, :], rhs=p_[:, 0:n], start=first, stop=last),
                                 reads=[r_vp, rp_], writes=[r_num], same_ok=True)
                            S.op("pe", lambda e: e.matmul(den[:, 0:n], lhsT=onb[:], rhs=p_[:, 0:n], start=first, stop=last),
                                 reads=[r_onb, rp_], writes=[r_den], same_ok=True)

                    def ep0(n=n, t1_=t1_, rt1_=rt1_):
                        S.op("dve", lambda e: e.reciprocal(out=rd[:, 0, 0:n], in_=dens[0][0][:, 0:n]), reads=[dens[0][1]], writes=[r_rd0])
                        S.op("dve", lambda e: e.tensor_tensor(out=t1_[:, 0:n], in0=nums[0][0][:, 0:n], in1=rd[:, 0, 0:n], op=ALU.mult),
                             reads=[nums[0][1], r_rd0], writes=[rt1_])

                    def ep1(ti=ti, t0=t0, n=n, lc=lc, t1_=t1_, rt1_=rt1_, h=h, k_=k_):
                        S.op("dve", lambda e: e.reciprocal(out=rd[:, 1, 0:n], in_=dens[1][0][:, 0:n]), reads=[dens[1][1]], writes=[r_rd1])
                        S.op("dve", lambda e: e.scalar_tensor_tensor(out=t2[:, 0:n], in0=nums[1][0][:, 0:n], scalar=lam[:, 2:3], in1=rd[:, 1, 0:n],
                                                                     op0=ALU.mult, op1=ALU.mult), reads=[nums[1][1], r_rd1, r_lam], writes=[r_t2])
                        S.op("dve", lambda e: e.tensor_tensor(out=t1_[:, 0:n], in0=t1_[:, 0:n], in1=t2[:, 0:n], op=ALU.add), reads=[rt1_, r_t2], writes=[rt1_])
                        S.op("dve", lambda e: e.tensor_tensor(out=t2[:, 0:n], in0=t1_[:, 0:n], in1=t1_[:, 0:n], op=ALU.mult), reads=[rt1_], writes=[r_t2])
                        bk, rb = self.bank()
                        S.op("pe", lambda e: e.matmul(bk[:, 0:n], lhsT=self.ones[:], rhs=t2[:, 0:n], start=True, stop=True),
                             reads=[r_t2, self.r_ident], writes=[rb], same_ok=True)
                        S.op("act", lambda e: e.activation(out=t2[:, 0:n], in_=bk[:, 0:n], func=AF.Sqrt, bias=epsc[:, 0:1], scale=1.0 / 128.0),
                             reads=[rb, r_eps], writes=[r_t2])
                        S.op("dve", lambda e: e.reciprocal(out=t2[:, 0:n], in_=t2[:, 0:n]), reads=[r_t2], writes=[r_t2])
                        o_, ro_ = on_[k_ % 2], r_on[k_ % 2]
                        S.op("dve", lambda e: e.scalar_tensor_tensor(out=o_[:, 0:n], in0=t1_[:, 0:n], scalar=sub[:, 0:1], in1=t2[:, 0:n],
                                                                     op0=ALU.mult, op1=ALU.mult), reads=[rt1_, r_t2, r_sub], writes=[ro_])
                        for oc in range(8):
                            bk, rb = self.bank()
                            S.op("pe", lambda e: e.matmul(bk[:, 0:n], lhsT=wo[h % 2][:, oc * 128:(oc + 1) * 128], rhs=o_[:, 0:n], start=True, stop=True),
                                 reads=[r_wo[h % 2], ro_], writes=[rb], same_ok=True)
                            hz = self.hT[:, oc, t0:t0 + n]
                            S.op("dve", lambda e: e.scalar_tensor_tensor(out=hz, in0=bk[:, 0:n], scalar=self.mcol(i, 2, oc, lc), in1=hz,
                                                                         op0=ALU.mult, op1=ALU.add),
                                 reads=[rb, self.r_h[ti], self.r_mod[i]], writes=[self.r_h[ti]])
                    attend(0)
                    if pend1[0] is not None:
                        pend1[0]()
                    attend(1)
                    ep0()
                    pend1[0] = ep1
            if pend1[0] is not None:
                pend1[0]()
            S.barrier()
        self.nrr = 6
        self.bank_i = 0

    def gla(self, i):
        S = self.S
        QS = 128.0 ** -0.5
        win = self.din["gla_w_in"]
        with ExitStack() as ph:
            arr = [[self.sb([128, T], BF16, es=ph) for _ in range(3)] for _ in range(2)]
            r_arr = [[Res() for _ in range(3)] for _ in range(2)]
            vh = self.sb([128, 18, 256], BF16, es=ph); r_vh = Res("vh")
            oacc = self.sb([128, 2, T], BF16, es=ph); r_oacc = [Res(f"oacc{c}") for c in range(18)]
            rT = self.sb([32, T], BF16, es=ph); r_rT = Res("rT")
            gw = self.sb([32, 2, 512], BF16, es=ph); r_gw = Res("gw")
            gb = self.sb([128, 2, 4], F32, es=ph); r_gb = Res("gb")
            ng = self.sb([128, 2], F32, es=ph); r_ng = Res("ng")
            dec = self.sb([128, 2, 18], F32, es=ph); r_dec = Res("dec")
            cmask = self.sb([128, 512], F32, es=ph); r_cm = Res("cmask")
            msk = [self.sb([128, 128], F32, es=ph) for _ in range(2)]; r_msk = Res("msk")
            bA = self.sb([128, 512], F32, es=ph); r_bA = Res("bA")
            bB = self.sb([128, 512], F32, es=ph); r_bB = Res("bB")
            bC = self.sb([128, 512], F32, es=ph); r_bC = Res("bC")
            bD = self.sb([128, 512], F32, es=ph); r_bD = Res("bD")
            Sf = [self.sb([128, 256], F32, es=ph) for _ in range(2)]; r_Sf = [Res("Sf0"), Res("Sf1")]
            Sb = [self.sb([128, 256], BF16, es=ph) for _ in range(2)]; r_Sb = [Res("Sb0"), Res("Sb1")]
            Am = [self.sb([128, 128], BF16, es=ph) for _ in range(2)]; r_Am = [Res(), Res()]
            keT = [self.sb([128, 128], BF16, es=ph) for _ in range(2)]; r_keT = [Res(), Res()]
            ogn = [self.sb([128, 2, 512], BF16, es=ph) for _ in range(1)]; r_ogn = [Res()]
            epsc = self.sb([128, 1], F32, es=ph); r_eps = Res("eps")
            S.op("pool", lambda e: e.memset(epsc[:], EPS), writes=[r_eps])
            S.op("pool", lambda e: e.memset(cmask[:], 1.0), writes=[r_cm])
            for c in range(4):
                S.op("pool", lambda e: e.memset(cmask[:, c * 128:c * 128 + 1], 0.0), reads=[r_cm], writes=[r_cm])
            for d in range(2):
                S.op("pool", lambda e: e.memset(msk[d][:], 1.0), reads=[r_msk], writes=[r_msk])
                cm, pat = ((-1, [[1, 128]]) if d == 0 else (1, [[-1, 128]]))
                S.op("pool", lambda e: e.affine_select(out=msk[d][:], in_=msk[d][:], compare_op=ALU.is_ge, fill=0.0, base=0,
                                                      pattern=pat, channel_multiplier=cm), reads=[r_msk], writes=[r_msk])
            S.op("pool", lambda e: e.memset(gw[:], 0.0), writes=[r_gw])
            S.dma("pool", [(gw[0:16, 0, :], self.din["gla_gw"][0]), (gw[16:32, 1, :], self.din["gla_gw"][1])], r_gw, reads=[r_gw], writes=[r_gw])
            for d in range(2):
                self.load_cols(self.din["gla_gb"][d], 4, gb[:, d, :], r_gb)
            S.op("dve", lambda e: e.tensor_scalar(out=gb[:], in0=gb[:], scalar1=-1.0, scalar2=None, op0=ALU.mult), reads=[r_gb], writes=[r_gb])
            self.load_cols(self.din["gla_norm"], 2, ng[:, 0:2], r_ng)
            w_, r_w = self.wnext()
            wr = w_[:, 0:256].rearrange("p (k n) -> p k n", k=8)
            S.dma("pool", [(wr, win[:, 3072:3104].rearrange("(k p) n -> p k n", p=128))], r_w, writes=[r_w])
            for ti, (t0, n, lc) in enumerate(TT):
                bk, rb = self.bank()
                for kc in range(8):
                    S.op("pe", lambda e: e.matmul(bk[0:32, 0:n], lhsT=wr[:, kc, :], rhs=self.uT[:, kc, t0:t0 + n], start=(kc == 0), stop=(kc == 7)),
                         reads=[r_w, self.r_u[ti]], writes=[rb], same_ok=True)
                S.op("act", lambda e: e.copy(rT[:, t0:t0 + n], bk[0:32, 0:n]), reads=[rb], writes=[r_rT])

            for h in range(4):
                wA_, r_wA = self.wnext()
                wqk = wA_[:, 0:2048].rearrange("p (k n) -> p k n", k=8)
                S.dma("pool", [(wqk[:, :, 0:128], win[:, h * 128:(h + 1) * 128].rearrange("(k p) n -> p k n", p=128)),
                               (wqk[:, :, 128:256], win[:, 512 + h * 128:512 + (h + 1) * 128].rearrange("(k p) n -> p k n", p=128))],
                      r_wA, writes=[r_wA])
                wB_, r_wB = self.wnext()
                wv = wB_[:, 0:2048].rearrange("p (k n) -> p k n", k=8)
                wg = wB_[:, 2048:4096].rearrange("p (k n) -> p k n", k=8)
                S.dma("pool", [(wv, win[:, 1024 + h * 256:1024 + (h + 1) * 256].rearrange("(k p) n -> p k n", p=128)),
                               (wg, win[:, 2048 + h * 256:2048 + (h + 1) * 256].rearrange("(k p) n -> p k n", p=128))],
                      r_wB, writes=[r_wB])
                wC_, r_wC = self.wnext()
                wo = wC_[:, 0:2048].rearrange("p (k n) -> p k n", k=2)
                S.dma("pool", [(wo, self.din["gla_w_out"][h * 256:(h + 1) * 256, :].rearrange("(k p) n -> p k n", p=128))], r_wC, writes=[r_wC])
                for g0 in range(0, 18, 2):
                    bk, rb = self.bank()
                    for jt in range(2):
                        kt = g0 + jt
                        for kc in range(8):
                            S.op("pe", lambda e: e.matmul(bk[:, jt * 256:(jt + 1) * 256], lhsT=self.uT[:, kc, kt * 128:(kt + 1) * 128], rhs=wv[:, kc, :],
                                                          start=(kc == 0), stop=(kc == 7)), reads=[r_wB] + self.r_u, writes=[rb], same_ok=True)
                    S.op("act", lambda e: e.copy(vh[:, g0:g0 + 2, :], bk[:, 0:512].rearrange("p (j d) -> p j d", d=256)), reads=[rb], writes=[r_vh])
                for ti, (t0, n, lc) in enumerate(TT):
                    ru = self.r_u[ti]
                    nch = n // 128
                    bq, rbq = self.bank()
                    for kc in range(8):
                        S.op("pe", lambda e: e.matmul(bq[:, 0:n], lhsT=wqk[:, kc, 0:128], rhs=self.uT[:, kc, t0:t0 + n], start=(kc == 0), stop=(kc == 7)),
                             reads=[r_wA, ru], writes=[rbq], same_ok=True)
                    bkk, rbk = self.bank()
                    for kc in range(8):
                        S.op("pe", lambda e: e.matmul(bkk[:, 0:n], lhsT=wqk[:, kc, 128:256], rhs=self.uT[:, kc, t0:t0 + n], start=(kc == 0), stop=(kc == 7)),
                             reads=[r_wA, ru], writes=[rbk], same_ok=True)
                    for d in range(2):
                        bx, rbx = self.bank()
                        S.op("pe", lambda e: e.matmul(bx[:, 0:n], lhsT=gw[:, d, h * 128:(h + 1) * 128], rhs=rT[:, t0:t0 + n], start=True, stop=True),
                             reads=[r_gw, r_rT], writes=[rbx], same_ok=True)
                        S.op("act", lambda e: e.activation(out=bA[:, 0:n], in_=bx[:, 0:n], func=AF.Exp, bias=gb[:, d, h:h + 1], scale=-1.0),
                             reads=[rbx, r_gb], writes=[r_bA])
                        S.op("act", lambda e: e.activation(out=bA[:, 0:n], in_=bA[:, 0:n], func=AF.Ln, bias=1.0, scale=1.0), reads=[r_bA], writes=[r_bA])
                        S.op("dve", lambda e: e.tensor_tensor_scan(out=bB[:, 0:n], data0=cmask[:, 0:n], data1=bA[:, 0:n], initial=0.0,
                                                                   op0=ALU.mult, op1=ALU.add), reads=[r_cm, r_bA], writes=[r_bB])
                        for c in range(nch):
                            gc = t0 // 128 + c
                            ce = c * 128 + 127
                            S.op("act", lambda e: e.activation(out=dec[:, d, gc:gc + 1], in_=bB[:, ce:ce + 1], func=AF.Exp, scale=-1.0 / 16), reads=[r_bB], writes=[r_dec])
                            S.op("dve", lambda e: e.tensor_scalar(out=bD[:, c * 128:(c + 1) * 128], in0=bB[:, c * 128:(c + 1) * 128], scalar1=bB[:, ce:ce + 1],
                                                                  scalar2=None, op0=ALU.subtract), reads=[r_bB], writes=[r_bD])
                        if d == 0:
                            S.op("act", lambda e: e.activation(out=bC[:, 0:n], in_=bB[:, 0:n], func=AF.Exp, scale=-1.0 / 16), reads=[r_bB], writes=[r_bC])
                            S.op("dve", lambda e: e.scalar_tensor_tensor(out=arr[d][0][:, t0:t0 + n], in0=bq[:, 0:n], scalar=QS, in1=bC[:, 0:n], op0=ALU.mult, op1=ALU.mult),
                                 reads=[rbq, r_bC], writes=[r_arr[d][0]])
                            S.op("act", lambda e: e.activation(out=bC[:, 0:n], in_=bB[:, 0:n], func=AF.Exp, scale=1.0 / 16), reads=[r_bB], writes=[r_bC])
                            S.op("dve", lambda e: e.tensor_tensor(out=arr[d][1][:, t0:t0 + n], in0=bkk[:, 0:n], in1=bC[:, 0:n], op=ALU.mult),
                                 reads=[rbk, r_bC], writes=[r_arr[d][1]])
                            S.op("act", lambda e: e.activation(out=bC[:, 0:n], in_=bD[:, 0:n], func=AF.Exp, scale=1.0 / 16), reads=[r_bD], writes=[r_bC])
                            S.op("dve", lambda e: e.tensor_tensor(out=arr[d][2][:, t0:t0 + n], in0=bkk[:, 0:n], in1=bC[:, 0:n], op=ALU.mult),
                                 reads=[rbk, r_bC], writes=[r_arr[d][2]])
                        else:
                            S.op("dve", lambda e: e.tensor_tensor(out=bD[:, 0:n], in0=bA[:, 0:n], in1=bD[:, 0:n], op=ALU.subtract), reads=[r_bA, r_bD], writes=[r_bD])
                            S.op("act", lambda e: e.activation(out=bC[:, 0:n], in_=bD[:, 0:n], func=AF.Exp, scale=-1.0 / 16), reads=[r_bD], writes=[r_bC])
                            S.op("dve", lambda e: e.scalar_tensor_tensor(out=arr[d][0][:, t0:t0 + n], in0=bq[:, 0:n], scalar=QS, in1=bC[:, 0:n], op0=ALU.mult, op1=ALU.mult),
                                 reads=[rbq, r_bC], writes=[r_arr[d][0]])
                            S.op("act", lambda e: e.activation(out=bC[:, 0:n], in_=bD[:, 0:n], func=AF.Exp, scale=1.0 / 16), reads=[r_bD], writes=[r_bC])
                            S.op("dve", lambda e: e.tensor_tensor(out=arr[d][1][:, t0:t0 + n], in0=bkk[:, 0:n], in1=bC[:, 0:n], op=ALU.mult),
                                 reads=[rbk, r_bC], writes=[r_arr[d][1]])
                            S.op("dve", lambda e: e.tensor_tensor(out=bD[:, 0:n], in0=bA[:, 0:n], in1=bB[:, 0:n], op=ALU.subtract), reads=[r_bA, r_bB, r_bC], writes=[r_bD])
                            S.op("act", lambda e: e.activation(out=bC[:, 0:n], in_=bD[:, 0:n], func=AF.Exp, scale=1.0 / 16), reads=[r_bD], writes=[r_bC])
                            S.op("dve", lambda e: e.tensor_tensor(out=arr[d][2][:, t0:t0 + n], in0=bkk[:, 0:n], in1=bC[:, 0:n], op=ALU.mult),
                                 reads=[rbk, r_bC], writes=[r_arr[d][2]])
                for d in range(2):
                    S.op("pool", lambda e: e.memset(Sf[d][:], 0.0), reads=[r_Sf[d]], writes=[r_Sf[d]])
                    S.op("pool", lambda e: e.memset(Sb[d][:], 0.0), reads=[r_Sb[d]], writes=[r_Sb[d]])
                order = [[16, 17] + list(range(16)), [17, 16] + list(range(15, -1, -1))]
                written = set()
                for step in range(18):
                    for d in range(2):
                        c = order[d][step]
                        cs = slice(c * 128, (c + 1) * 128)
                        qd, ki, ke = arr[d]
                        ba, rba = self.bank()
                        S.op("pe", lambda e: e.matmul(ba[:, 0:128], lhsT=ki[:, cs], rhs=qd[:, cs], start=True, stop=True),
                             reads=[r_arr[d][1], r_arr[d][0]], writes=[rba], same_ok=True)
                        S.op("pe", lambda e: e.matmul(ba[:, 128:256], lhsT=ke[:, cs], rhs=self.identb[:], start=True, stop=True),
                             reads=[r_arr[d][2], self.r_ident], writes=[rba], same_ok=True)
                        S.op("dve", lambda e: e.tensor_tensor(out=Am[d][:], in0=ba[:, 0:128], in1=msk[d][:], op=ALU.mult), reads=[rba, r_msk], writes=[r_Am[d]])
                        S.op("act", lambda e: e.copy(keT[d][:], ba[:, 128:256]), reads=[rba], writes=[r_keT[d]])
                        bo, rbo = self.bank()
                        for j in range(2):
                            S.op("pe", lambda e: e.matmul(bo[:, j * 128:(j + 1) * 128], lhsT=Sb[d][:, j * 128:(j + 1) * 128], rhs=qd[:, cs], start=True, stop=False),
                                 reads=[r_Sb[d], r_arr[d][0]], writes=[rbo], same_ok=True)
                            S.op("pe", lambda e: e.matmul(bo[:, j * 128:(j + 1) * 128], lhsT=vh[:, c, j * 128:(j + 1) * 128], rhs=Am[d][:], start=False, stop=True),
                                 reads=[r_vh, r_Am[d]], writes=[rbo], same_ok=True)
                        ov = oacc[:, :, cs]
                        pv = bo[:, 0:256].rearrange("p (j c) -> p j c", j=2)
                        if c not in written:
                            written.add(c)
                            S.op("act", lambda e: e.copy(ov, pv), reads=[rbo], writes=[r_oacc[c]])
                        else:
                            S.op("dve", lambda e: e.tensor_tensor(out=ov, in0=pv, in1=ov, op=ALU.add), reads=[rbo, r_oacc[c]], writes=[r_oacc[c]])
                        bs, rbs = self.bank()
                        S.op("pe", lambda e: e.matmul(bs[:, 0:256], lhsT=keT[d][:], rhs=vh[:, c, :], start=True, stop=True),
                             reads=[r_keT[d], r_vh], writes=[rbs], same_ok=True)
                        S.op("dve", lambda e: e.scalar_tensor_tensor(out=Sf[d][:], in0=Sf[d][:], scalar=dec[:, d, c:c + 1], in1=bs[:, 0:256], op0=ALU.mult, op1=ALU.add),
                             reads=[r_Sf[d], r_dec, rbs], writes=[r_Sf[d]])
                        S.op("act", lambda e: e.copy(Sb[d][:], Sf[d][:]), reads=[r_Sf[d]], writes=[r_Sb[d]])
                for ti, (t0, n, lc) in enumerate(TT):
                    ru = self.r_u[ti]
                    roa = r_oacc[t0 // 128:(t0 + n) // 128]
                    bss = self.banks[6]; rbss = self.bank_res[6]
                    for j in range(2):
                        S.op("act", lambda e: e.activation(out=bA[:, 0:n], in_=oacc[:, j, t0:t0 + n], func=AF.Square), reads=roa + [r_bA], writes=[r_bA])
                        S.op("pe", lambda e: e.matmul(bss[:, 0:n], lhsT=self.ones[:], rhs=bA[:, 0:n], start=(j == 0), stop=(j == 1)),
                             reads=[r_bA, self.r_ident], writes=[rbss], same_ok=True)
                    S.op("act", lambda e: e.activation(out=bB[:, 0:n], in_=bss[:, 0:n], func=AF.Sqrt, bias=epsc[:, 0:1], scale=1.0 / 256.0), reads=[rbss, r_eps], writes=[r_bB])
                    S.op("dve", lambda e: e.reciprocal(out=bB[:, 0:n], in_=bB[:, 0:n]), reads=[r_bB], writes=[r_bB])
                    og, rog = ogn[0], r_ogn[0]
                    for j in range(2):
                        bg, rbg = self.bank()
                        for kc in range(8):
                            S.op("pe", lambda e: e.matmul(bg[:, 0:n], lhsT=wg[:, kc, j * 128:(j + 1) * 128], rhs=self.uT[:, kc, t0:t0 + n], start=(kc == 0), stop=(kc == 7)),
                                 reads=[r_wB, ru], writes=[rbg], same_ok=True)
                        S.op("act", lambda e: e.activation(out=bC[:, 0:n], in_=bg[:, 0:n], func=AF.Silu), reads=[rbg], writes=[r_bC])
                        S.op("dve", lambda e: e.scalar_tensor_tensor(out=bD[:, 0:n], in0=oacc[:, j, t0:t0 + n], scalar=ng[:, j:j + 1], in1=bB[:, 0:n], op0=ALU.mult, op1=ALU.mult),
                             reads=roa + [r_ng, r_bB], writes=[r_bD])
                        S.op("dve", lambda e: e.tensor_tensor(out=og[:, j, 0:n], in0=bD[:, 0:n], in1=bC[:, 0:n], op=ALU.mult), reads=[r_bD, r_bC], writes=[rog])
                    for oc in range(8):
                        bk, rb = self.bank()
                        for j in range(2):
                            S.op("pe", lambda e: e.matmul(bk[:, 0:n], lhsT=wo[:, j, oc * 128:(oc + 1) * 128], rhs=og[:, j, 0:n], start=(j == 0), stop=(j == 1)),
                                 reads=[r_wC, rog], writes=[rb], same_ok=True)
                        hz = self.hT[:, oc, t0:t0 + n]
                        S.op("dve", lambda e: e.scalar_tensor_tensor(out=hz, in0=bk[:, 0:n], scalar=self.mcol(i, 2, oc, lc), in1=hz, op0=ALU.mult, op1=ALU.add),
                             reads=[rb, self.r_h[ti], self.r_mod[i]], writes=[self.r_h[ti]])
            S.barrier()

    def gdn(self, i):
        S = self.S
        R32 = mybir.dt.float32r
        win = self.din["gdn_w_in"]
        rr = lambda ap: ap.bitcast(R32)
        with ExitStack() as ph:
            sbp = lambda shape, dt: self.sb(shape, dt, es=ph)
            cw = sbp([128, 5, 32], F32); r_cw = Res("cw")
            for j in range(5):
                self.load_cols(self.din["gdn_conv"][j], 32, cw[:, j, :], r_cw)
            r_msk = Res("gmsk")
            strict = [sbp([128, 128], F32) for _ in range(2)]
            inclT = [sbp([128, 128], F32) for _ in range(2)]
            specs = [(strict[0], ALU.is_gt, 1, [[-1, 128]]), (strict[1], ALU.is_gt, -1, [[1, 128]]),
                     (inclT[0], ALU.is_ge, -1, [[1, 128]]), (inclT[1], ALU.is_ge, 1, [[-1, 128]])]
            for (t_, cmp_, cm, pat) in specs:
                S.op("pool", lambda e: e.memset(t_[:], 1.0), reads=[r_msk], writes=[r_msk])
                S.op("pool", lambda e: e.affine_select(out=t_[:], in_=t_[:], compare_op=cmp_, fill=0.0, base=0, pattern=pat, channel_multiplier=cm),
                     reads=[r_msk], writes=[r_msk])
            bsel = sbp([8, 128], F32)
            bd = {}
            for b in (16, 32, 64):
                nb = 128 // b
                S.op("pool", lambda e: e.memset(bsel[:], 1.0), reads=[r_msk], writes=[r_msk])
                S.op("pool", lambda e: e.affine_select(out=bsel[:], in_=bsel[:], compare_op=ALU.is_ge, fill=0.0, base=0, pattern=[[1, 128]], channel_multiplier=-b),
                     reads=[r_msk], writes=[r_msk])
                S.op("pool", lambda e: e.affine_select(out=bsel[:], in_=bsel[:], compare_op=ALU.is_ge, fill=0.0, base=b - 1, pattern=[[-1, 128]], channel_multiplier=b),
                     reads=[r_msk], writes=[r_msk])
                bk, rb = self.bank()
                S.op("pe", lambda e: e.matmul(bk[:, 0:128], lhsT=bsel[0:nb, :], rhs=bsel[0:nb, :], start=True, stop=True), reads=[r_msk], writes=[rb], same_ok=True)
                bd[b] = sbp([128, 128], F32)
                S.op("dve", lambda e: e.tensor_copy(bd[b][:], bk[:, 0:128]), reads=[rb, r_msk], writes=[r_msk])
            off64 = sbp([128, 128], F32)
            S.op("dve", lambda e: e.tensor_scalar(out=off64[:], in0=bd[64][:], scalar1=-1.0, scalar2=1.0, op0=ALU.mult, op1=ALU.add), reads=[r_msk], writes=[r_msk])
            S.op("dve", lambda e: e.tensor_tensor(out=bd[64][:], in0=bd[64][:], in1=bd[32][:], op=ALU.subtract), reads=[r_msk], writes=[r_msk])
            S.op("dve", lambda e: e.tensor_tensor(out=bd[32][:], in0=bd[32][:], in1=bd[16][:], op=ALU.subtract), reads=[r_msk], writes=[r_msk])
            bd16, off16, off32 = bd[16], bd[32], bd[64]
            b4 = lambda m_: m_[:].unsqueeze(1).broadcast_to([128, 4, 128])
            hc = sbp([128, 64], F32); r_hc = Res("hc")
            S.dma("sp", [(hc[:], self.din["gdn_hc"].partition_broadcast(128))], r_hc, writes=[r_hc])
            S.op("act", lambda e: e.activation(out=hc[:, 0:32], in_=hc[:, 0:32], func=AF.Exp), reads=[r_hc], writes=[r_hc])
            S.op("dve", lambda e: e.tensor_scalar(out=hc[:, 0:32], in0=hc[:, 0:32], scalar1=-1.0, scalar2=None, op0=ALU.mult), reads=[r_hc], writes=[r_hc])
            ngrep = sbp([128, 128], F32); r_ngr = Res("ngrep")
            S.dma("sp", [(ngrep[:], self.din["gdn_norm"].partition_broadcast(128))], r_ngr, writes=[r_ngr])
            epsc = sbp([128, 1], F32); r_eps = Res("eps")
            S.op("pool", lambda e: e.memset(epsc[:], EPS), writes=[r_eps])
            qkT = sbp([128, 2, T], BF16); r_qkT = Res("qkT")
            kn = sbp([128, 18, 128], BF16); r_kn = Res("kn")
            vt = sbp([128, 18, 256], BF16); r_vt = Res("vt")
            oacc = sbp([128, 18, 256], BF16); r_oacc = [Res(f"go{c}") for c in range(18)]
            sc_names = ("negb", "gc", "e", "ecoef", "negbe", "dl", "g")
            sc = {n_: sbp([128, 18, 4], F32) for n_ in sc_names}
            r_sc = Res("gsc")

            for kh in range(8):
                wA_, r_wA = self.wnext()
                wA = wA_[:].rearrange("p (k n) -> p k n", k=8)
                S.dma("pool", [(wA[:, :, 0:128], win[:, kh * 128:(kh + 1) * 128].rearrange("(k p) n -> p k n", p=128)),
                               (wA[:, :, 128:256], win[:, 1024 + kh * 128:1024 + (kh + 1) * 128].rearrange("(k p) n -> p k n", p=128)),
                               (wA[:, :, 256:512], win[:, 2048 + kh * 256:2048 + (kh + 1) * 256].rearrange("(k p) n -> p k n", p=128))],
                      r_wA, writes=[r_wA])
                wB_, r_wB = self.wnext()
                wz = wB_[:, 0:2048].rearrange("p (k n) -> p k n", k=8)
                wgt = wB_[:, 2048:2112].rearrange("p (k n) -> p k n", k=8)
                S.dma("pool", [(wz, win[:, 4096 + kh * 256:4096 + (kh + 1) * 256].rearrange("(k p) n -> p k n", p=128)),
                               (wgt, self.din["gdn_wg"][:, kh * 8:(kh + 1) * 8].rearrange("(k p) n -> p k n", p=128))], r_wB, writes=[r_wB])
                wC_, r_wC = self.wnext()
                wo = wC_[:, 0:2048].rearrange("p (k n) -> p k n", k=2)
                S.dma("pool", [(wo, self.din["gdn_w_out"][kh * 256:(kh + 1) * 256, :].rearrange("(k p) n -> p k n", p=128))], r_wC, writes=[r_wC])
                with ExitStack() as p1:
                    xpad = self.sb([128, 4, 2312], BF16, es=p1); r_xp = Res("xpad")
                    dg = self.sb([128, 4, 5, 128], BF16, es=p1); r_dg = Res("dg")
                    cv = self.sb([128, 512], F32, es=p1); r_cv = Res("cv")
                    junk = self.sb([128, 128], F32, es=p1); r_junk = Res("junk")
                    ss = self.sb([128, 2], F32, es=p1); r_ss = Res("ss")
                    qn = self.sb([128, 128], BF16, es=p1); r_qn = Res("qn")
                    S.op("pool", lambda e: e.memset(xpad[:], 0.0), writes=[r_xp])
                    gch = [kh, 8 + kh, 16 + 2 * kh, 17 + 2 * kh]
                    for ch in range(4):
                        for j in range(5):
                            S.op("pool", lambda e: e.tensor_scalar(out=dg[:, ch, j, :], in0=self.identb[:], scalar1=cw[:, j, gch[ch]:gch[ch] + 1], scalar2=1.0, op0=ALU.mult, op1=ALU.mult),
                                 reads=[r_cw, self.r_ident], writes=[r_dg])
                    for ch in range(4):
                        for ti, (t0, n, lc) in enumerate(TT):
                            bk, rb = self.bank()
                            for kc in range(8):
                                S.op("pe", lambda e: e.matmul(bk[:, 0:n], lhsT=wA[:, kc, ch * 128:(ch + 1) * 128], rhs=self.uT[:, kc, t0:t0 + n], start=(kc == 0), stop=(kc == 7)),
                                     reads=[r_wA, self.r_u[ti]], writes=[rb], same_ok=True)
                            c0 = t0 + 2 if lc == 0 else 2054
                            if (ch + ti) % 2 == 0:
                                S.op("act", lambda e: e.copy(xpad[:, ch, c0:c0 + n], bk[:, 0:n]), reads=[rb], writes=[r_xp])
                            else:
                                S.op("dve", lambda e: e.tensor_copy(xpad[:, ch, c0:c0 + n], bk[:, 0:n]), reads=[rb], writes=[r_xp])
                    for t in range(18):
                        b0 = t * 128 + 2 if t < 16 else 2054 + (t - 16) * 128
                        bk, rb = self.bank()
                        for ch in range(4):
                            for j in range(5):
                                S.op("pe", lambda e: e.matmul(bk[:, ch * 128:(ch + 1) * 128], lhsT=xpad[:, ch, b0 + j - 2:b0 + j - 2 + 128], rhs=dg[:, ch, j, :],
                                                              start=(j == 0), stop=(j == 4)), reads=[r_xp, r_dg], writes=[rb], same_ok=True)
                        S.op("act", lambda e: e.activation(out=cv[:], in_=bk[:, 0:512], func=AF.Silu), reads=[rb], writes=[r_cv])
                        for q_ in range(2):
                            S.op("act", lambda e: e.activation(out=junk[:], in_=cv[:, q_ * 128:(q_ + 1) * 128], func=AF.Square, accum_out=ss[:, q_:q_ + 1]),
                                 reads=[r_cv], writes=[r_junk, r_ss])
                        S.op("act", lambda e: e.activation(out=ss[:], in_=ss[:], func=AF.Sqrt, bias=epsc[:, 0:1], scale=1.0), reads=[r_ss, r_eps], writes=[r_ss])
                        S.op("dve", lambda e: e.reciprocal(out=ss[:], in_=ss[:]), reads=[r_ss], writes=[r_ss])
                        S.op("dve", lambda e: e.tensor_scalar(out=qn[:], in0=cv[:, 0:128], scalar1=ss[:, 0:1], scalar2=128.0 ** -0.5, op0=ALU.mult, op1=ALU.mult),
                             reads=[r_cv, r_ss], writes=[r_qn])
                        S.op("dve", lambda e: e.tensor_scalar(out=kn[:, t, :], in0=cv[:, 128:256], scalar1=ss[:, 1:2], scalar2=None, op0=ALU.mult),
                             reads=[r_cv, r_ss], writes=[r_kn])
                        S.op("pool", lambda e: e.tensor_copy(vt[:, t, :], cv[:, 256:512]), reads=[r_cv], writes=[r_vt])
                        b2, rb2 = self.bank()
                        S.op("pe", lambda e: e.matmul(b2[:, 0:128], lhsT=qn[:], rhs=self.identb[:], start=True, stop=True), reads=[r_qn, self.r_ident], writes=[rb2], same_ok=True)
                        S.op("pe", lambda e: e.matmul(b2[:, 128:256], lhsT=kn[:, t, :], rhs=self.identb[:], start=True, stop=True), reads=[r_kn, self.r_ident], writes=[rb2], same_ok=True)
                        S.op("act", lambda e: e.copy(qkT[:, :, t * 128:(t + 1) * 128], b2[:, 0:256].rearrange("p (a c) -> p a c", a=2)), reads=[rb2], writes=[r_qkT])
                    S.barrier()
                bk, rb = self.bank()
                for t in range(18):
                    for kc in range(8):
                        S.op("pe", lambda e: e.matmul(bk[:, t * 8:(t + 1) * 8], lhsT=self.uT[:, kc, t * 128:(t + 1) * 128], rhs=wgt[:, kc, :], start=(kc == 0), stop=(kc == 7)),
                             reads=[r_wB] + self.r_u, writes=[rb], same_ok=True)
                graw = bk[:, 0:144].rearrange("p (t c) -> p t c", c=8)
                S.op("act", lambda e: e.activation(out=sc["negb"][:], in_=graw[:, :, 0:4], func=AF.Sigmoid), reads=[rb], writes=[r_sc])
                S.op("dve", lambda e: e.tensor_scalar(out=sc["negb"][:], in0=sc["negb"][:], scalar1=-1.0, scalar2=None, op0=ALU.mult), reads=[r_sc], writes=[r_sc])
                for m in range(4):
                    d_, j_ = m // 2, m % 2
                    hidx = d_ * 16 + 2 * kh + j_
                    S.op("act", lambda e: e.activation(out=sc["g"][:, :, m], in_=graw[:, :, 4 + m], func=AF.Exp, bias=hc[:, 32 + hidx:33 + hidx], scale=1.0),
                         reads=[rb, r_hc], writes=[r_sc])
                S.op("act", lambda e: e.activation(out=sc["g"][:], in_=sc["g"][:], func=AF.Ln, bias=1.0, scale=1.0), reads=[r_sc], writes=[r_sc])
                for m in range(4):
                    d_, j_ = m // 2, m % 2
                    hidx = d_ * 16 + 2 * kh + j_
                    S.op("dve", lambda e: e.tensor_scalar(out=sc["g"][:, :, m], in0=sc["g"][:, :, m], scalar1=hc[:, hidx:hidx + 1], scalar2=None, op0=ALU.mult),
                         reads=[r_sc, r_hc], writes=[r_sc])
                bk, rb = self.bank()
                gv = sc["g"][:]
                S.op("pe", lambda e: e.matmul(bk[:, 0:72].rearrange("p (t c) -> p t c", c=4)[:, :, 0:2], lhsT=inclT[0][:], rhs=gv[:, :, 0:2], start=True, stop=True),
                     reads=[r_sc, r_msk], writes=[rb], same_ok=True)
                S.op("pe", lambda e: e.matmul(bk[:, 0:72].rearrange("p (t c) -> p t c", c=4)[:, :, 2:4], lhsT=inclT[1][:], rhs=gv[:, :, 2:4], start=True, stop=True),
                     reads=[r_sc, r_msk], writes=[rb], same_ok=True)
                S.op("pe", lambda e: e.matmul(bk[:, 128:200], lhsT=self.ones[:], rhs=gv.rearrange("p t c -> p (t c)"), start=True, stop=True),
                     reads=[r_sc, self.r_ident], writes=[rb], same_ok=True)
                gcp = bk[:, 0:72].rearrange("p (t c) -> p t c", c=4)
                glp = bk[:, 128:200].rearrange("p (t c) -> p t c", c=4)
                S.op("dve", lambda e: e.tensor_copy(sc["gc"][:], gcp), reads=[rb], writes=[r_sc])
                S.op("act", lambda e: e.activation(out=sc["e"][:], in_=gcp, func=AF.Exp), reads=[rb], writes=[r_sc])
                S.op("act", lambda e: e.activation(out=sc["dl"][:], in_=glp, func=AF.Exp), reads=[rb], writes=[r_sc])
                S.op("dve", lambda e: e.tensor_tensor(out=sc["ecoef"][:], in0=glp, in1=sc["gc"][:], op=ALU.subtract), reads=[rb, r_sc], writes=[r_sc])
                S.op("act", lambda e: e.activation(out=sc["ecoef"][:], in_=sc["ecoef"][:], func=AF.Exp), reads=[r_sc], writes=[r_sc])
                S.op("dve", lambda e: e.tensor_tensor(out=sc["negbe"][:], in0=sc["negb"][:], in1=sc["e"][:], op=ALU.mult), reads=[r_sc], writes=[r_sc])
                with ExitStack() as p3:
                    f4 = lambda: self.sb([128, 4, 128], F32, es=p3)
                    X = [f4(), f4()]; Y = [f4(), f4()]; W = f4(); Tm = f4(); Y0 = f4()
                    D1 = f4(); D2 = f4(); r_D1 = Res("D1"); r_D2 = Res("D2")
                    r_X = [Res("X0"), Res("X1")]; r_Y = [Res("Y0"), Res("Y1")]; r_W = Res("W"); r_Tm = Res("Tm"); r_Y0 = Res("Y0p")
                    Gm = self.sb([128, 2, 128], F32, es=p3); r_Gm = Res("Gm")
                    QKm = self.sb([128, 2, 128], F32, es=p3); r_QKm = Res("QKm")
                    AT = self.sb([128, 4, 128], BF16, es=p3); r_AT = Res("AT")
                    Wf = W; r_Wf = r_W
                    Rm = f4(); r_Rm = Res("Rm")
                    vn = self.sb([128, 4, 128], BF16, es=p3); r_vn = Res("vn")
                    kdec = self.sb([128, 4, 128], BF16, es=p3); r_kdec = Res("kdec")
                    bv = self.sb([128, 4, 128], BF16, es=p3); r_bv = Res("bv")
                    Sf = f4(); r_Sf = Res("Sf")
                    Sb = self.sb([128, 4, 128], BF16, es=p3); r_Sb = Res("Sb")
                    ot = self.sb([128, 4, 128], BF16, es=p3); r_ot = Res("ot")
                    S.op("pool", lambda e: e.memset(Sf[:], 0.0), writes=[r_Sf])
                    S.op("pool", lambda e: e.memset(Sb[:], 0.0), writes=[r_Sb])
                    order = [[16, 17] + list(range(16)), [17, 16] + list(range(15, -1, -1))]
                    written = set()
                    kT = lambda c: qkT[:, 1, c * 128:(c + 1) * 128]
                    qT = lambda c: qkT[:, 0, c * 128:(c + 1) * 128]
                    for step in range(18):
                        cc = [order[0][step], order[1][step]]
                        cm_ = [cc[m // 2] for m in range(4)]
                        for m in range(4):
                            S.op("pool", lambda e: e.tensor_scalar(out=kdec[:, m, :], in0=kn[:, cm_[m], :], scalar1=sc["ecoef"][:, cm_[m], m:m + 1], scalar2=1.0, op0=ALU.mult, op1=ALU.mult),
                                 reads=[r_kn, r_sc], writes=[r_kdec])
                            S.op("pool", lambda e: e.tensor_scalar(out=bv[:, m, :], in0=vt[:, cm_[m], (m % 2) * 128:(m % 2 + 1) * 128], scalar1=sc["negb"][:, cm_[m], m:m + 1],
                                                                   scalar2=-1.0, op0=ALU.mult, op1=ALU.mult), reads=[r_vt, r_sc], writes=[r_bv])
                        bk, rb = self.bank()
                        for d in range(2):
                            S.op("pe", lambda e: e.matmul(bk[:, d * 128:(d + 1) * 128], lhsT=kT(cc[d]), rhs=kT(cc[d]), start=True, stop=True), reads=[r_qkT], writes=[rb], same_ok=True)
                            S.op("pe", lambda e: e.matmul(bk[:, 256 + d * 128:256 + (d + 1) * 128], lhsT=kT(cc[d]), rhs=qT(cc[d]), start=True, stop=True), reads=[r_qkT], writes=[rb], same_ok=True)
                        for d in range(2):
                            S.op("dve", lambda e: e.tensor_tensor(out=Gm[:, d, :], in0=bk[:, d * 128:(d + 1) * 128], in1=strict[d][:], op=ALU.mult), reads=[rb, r_msk], writes=[r_Gm])
                            S.op("dve", lambda e: e.tensor_tensor(out=QKm[:, d, :], in0=bk[:, 256 + d * 128:256 + (d + 1) * 128], in1=inclT[d][:], op=ALU.mult), reads=[rb, r_msk], writes=[r_QKm])
                        for m in range(4):
                            S.op("pool", lambda e: e.tensor_scalar(out=D1[:, m, :], in0=self.ident[:], scalar1=sc["gc"][:, cm_[m], m:m + 1], scalar2=1.0, op0=ALU.mult, op1=ALU.mult),
                                 reads=[r_sc, self.r_ident], writes=[r_D1])
                        bb, rbb = self.bank()
                        for m in range(4):
                            S.op("pe", lambda e: e.matmul(bb[:, m * 128:(m + 1) * 128], lhsT=self.ones[:], rhs=D1[:, m, :], start=True, stop=True),
                                 reads=[r_D1, self.r_ident], writes=[rbb], same_ok=True)
                        for m in range(4):
                            S.op("dve", lambda e: e.tensor_scalar(out=D1[:, m, :], in0=bb[:, m * 128:(m + 1) * 128], scalar1=sc["gc"][:, cm_[m], m:m + 1], scalar2=0.0,
                                                                  op0=ALU.subtract, op1=ALU.max), reads=[rbb, r_sc], writes=[r_D1])
                            S.op("dve", lambda e: e.tensor_scalar(out=D2[:, m, :], in0=bb[:, m * 128:(m + 1) * 128], scalar1=sc["gc"][:, cm_[m], m:m + 1], scalar2=0.0,
                                                                  op0=ALU.subtract, op1=ALU.min), reads=[rbb, r_sc], writes=[r_D2])
                        S.op("act", lambda e: e.activation(out=D1[:], in_=D1[:], func=AF.Exp, scale=-1.0), reads=[r_D1], writes=[r_D1])
                        S.op("act", lambda e: e.activation(out=D2[:], in_=D2[:], func=AF.Exp, scale=1.0), reads=[r_D2], writes=[r_D2])
                        for m in range(4):
                            d = m // 2
                            S.op("dve", lambda e: e.scalar_tensor_tensor(out=rr(X[0][:, m, :]), in0=D1[:, m, :], scalar=sc["negb"][:, cm_[m], m:m + 1], in1=Gm[:, d, :],
                                                                         op0=ALU.mult, op1=ALU.mult), reads=[r_D1, r_sc, r_Gm], writes=[r_X[0]])
                            S.op("pool", lambda e: e.tensor_tensor(out=AT[:, m, :], in0=QKm[:, d, :], in1=D2[:, m, :], op=ALU.mult), reads=[r_QKm, r_D2], writes=[r_AT])
                        bk, rb = self.bank()
                        for m in range(4):
                            S.op("pe", lambda e: e.matmul(bk[:, m * 128:(m + 1) * 128], lhsT=rr(X[0][:, m, :]), rhs=rr(self.identr[:]), start=True, stop=True),
                                 reads=[r_X[0], self.r_ident], writes=[rb], same_ok=True)
                        pv4 = lambda b_: b_[:, 0:512].rearrange("p (m c) -> p m c", m=4)
                        S.op("act", lambda e: e.copy(rr(Y0[:]), pv4(bk)), reads=[rb], writes=[r_Y0])
                        S.op("dve", lambda e: e.tensor_tensor(out=rr(X[1][:]), in0=X[0][:], in1=b4(bd16), op=ALU.mult), reads=[r_X[0], r_msk], writes=[r_X[1]])
                        S.op("pool", lambda e: e.tensor_tensor(out=rr(Y[1][:]), in0=Y0[:], in1=b4(bd16), op=ALU.mult), reads=[r_Y0, r_msk], writes=[r_Y[1]])
                        S.op("dve", lambda e: e.tensor_tensor(out=rr(W[:]), in0=Y[1][:], in1=b4(self.ident), op=ALU.add), reads=[r_Y[1], self.r_ident], writes=[r_W])
                        cur = 1
                        for lev in range(3):
                            nxt = 1 - cur
                            bx, rbx = self.bank()
                            for m in range(4):
                                S.op("pe", lambda e: e.matmul(bx[:, m * 128:(m + 1) * 128], lhsT=rr(Y[cur][:, m, :]), rhs=rr(X[cur][:, m, :]), start=True, stop=True),
                                     reads=[r_Y[cur], r_X[cur]], writes=[rbx], same_ok=True)
                            if lev < 2:
                                by, rby = self.bank()
                                for m in range(4):
                                    S.op("pe", lambda e: e.matmul(by[:, m * 128:(m + 1) * 128], lhsT=rr(X[cur][:, m, :]), rhs=rr(Y[cur][:, m, :]), start=True, stop=True),
                                         reads=[r_Y[cur], r_X[cur]], writes=[rby], same_ok=True)
                            S.op("act", lambda e: e.copy(rr(X[nxt][:]), pv4(bx)), reads=[rbx], writes=[r_X[nxt]])
                            if lev < 2:
                                S.op("dve", lambda e: e.tensor_copy(rr(Y[nxt][:]), pv4(by)), reads=[rby], writes=[r_Y[nxt]])
                            bw, rbw = self.bank()
                            for m in range(4):
                                S.op("pe", lambda e: e.matmul(bw[:, m * 128:(m + 1) * 128], lhsT=rr(X[nxt][:, m, :]), rhs=rr(W[:, m, :]), start=True, stop=True),
                                     reads=[r_X[nxt], r_W], writes=[rbw], same_ok=True)
                            S.op("dve", lambda e: e.tensor_tensor(out=rr(W[:]), in0=pv4(bw), in1=W[:], op=ALU.add), reads=[rbw, r_W], writes=[r_W])
                            cur = nxt
                        bk, rb = self.bank()
                        for m in range(4):
                            S.op("pe", lambda e: e.matmul(bk[:, m * 128:(m + 1) * 128], lhsT=rr(W[:, m, :]), rhs=rr(self.identr[:]), start=True, stop=True),
                                 reads=[r_W, self.r_ident], writes=[rb], same_ok=True)
                        S.op("act", lambda e: e.copy(rr(Tm[:]), pv4(bk)), reads=[rb], writes=[r_Tm])
                        for li, offm in enumerate((off16, off32, off64)):
                            S.op("pool", lambda e: e.tensor_tensor(out=rr(X[0][:]), in0=Y0[:], in1=b4(offm), op=ALU.mult), reads=[r_Y0, r_msk], writes=[r_X[0]])
                            bz, rbz = self.bank()
                            for m in range(4):
                                S.op("pe", lambda e: e.matmul(bz[:, m * 128:(m + 1) * 128], lhsT=rr(X[0][:, m, :]), rhs=rr(Tm[:, m, :]), start=True, stop=True),
                                     reads=[r_X[0], r_Tm], writes=[rbz], same_ok=True)
                            S.op("act", lambda e: e.copy(rr(X[1][:]), pv4(bz)), reads=[rbz], writes=[r_X[1]])
                            if li < 2:
                                bt, rbt = self.bank()
                                for m in range(4):
                                    S.op("pe", lambda e: e.matmul(bt[:, m * 128:(m + 1) * 128], lhsT=rr(W[:, m, :]), rhs=rr(X[1][:, m, :]), start=True, stop=True),
                                         reads=[r_W, r_X[1]], writes=[rbt], same_ok=True)
                            bw, rbw = self.bank()
                            for m in range(4):
                                S.op("pe", lambda e: e.matmul(bw[:, m * 128:(m + 1) * 128], lhsT=rr(X[1][:, m, :]), rhs=rr(W[:, m, :]), start=True, stop=True),
                                     reads=[r_X[1], r_W], writes=[rbw], same_ok=True)
                            if li < 2:
                                S.op("pool" if False else "dve", lambda e: e.tensor_tensor(out=rr(Tm[:]), in0=pv4(bt), in1=Tm[:], op=ALU.add), reads=[rbt, r_Tm], writes=[r_Tm])
                                S.op("dve", lambda e: e.tensor_tensor(out=rr(W[:]), in0=pv4(bw), in1=W[:], op=ALU.add), reads=[rbw, r_W], writes=[r_W])
                            else:
                                S.op("dve", lambda e: e.tensor_tensor(out=rr(Wf[:]), in0=pv4(bw), in1=W[:], op=ALU.add), reads=[rbw, r_W], writes=[r_Wf])
                        bks, rbks = self.bank()
                        for m in range(4):
                            S.op("pe", lambda e: e.matmul(bks[:, m * 128:(m + 1) * 128], lhsT=kT(cm_[m]), rhs=Sb[:, m, :], start=True, stop=True), reads=[r_qkT, r_Sb], writes=[rbks], same_ok=True)
                        bo1, rbo1 = self.bank()
                        for m in range(4):
                            S.op("pe", lambda e: e.matmul(bo1[:, m * 128:(m + 1) * 128], lhsT=qT(cm_[m]), rhs=Sb[:, m, :], start=True, stop=True), reads=[r_qkT, r_Sb], writes=[rbo1], same_ok=True)
                        for m in range(4):
                            S.op("dve", lambda e: e.scalar_tensor_tensor(out=rr(Rm[:, m, :]), in0=bks[:, m * 128:(m + 1) * 128], scalar=sc["negbe"][:, cm_[m], m:m + 1], in1=bv[:, m, :],
                                                                         op0=ALU.mult, op1=ALU.add), reads=[rbks, r_sc, r_bv], writes=[r_Rm])
                            S.op("act", lambda e: e.activation(out=ot[:, m, :], in_=bo1[:, m * 128:(m + 1) * 128], func=AF.Copy, scale=sc["e"][:, cm_[m], m:m + 1]),
                                 reads=[rbo1, r_sc], writes=[r_ot])
                        bvn, rbvn = self.bank()
                        for m in range(4):
                            S.op("pe", lambda e: e.matmul(bvn[:, m * 128:(m + 1) * 128], lhsT=rr(Wf[:, m, :]), rhs=rr(Rm[:, m, :]), start=True, stop=True), reads=[r_Wf, r_Rm], writes=[rbvn], same_ok=True)
                        S.op("act", lambda e: e.copy(vn[:], pv4(bvn)), reads=[rbvn], writes=[r_vn])
                        bo2, rbo2 = self.bank()
                        for m in range(4):
                            S.op("pe", lambda e: e.matmul(bo2[:, m * 128:(m + 1) * 128], lhsT=AT[:, m, :], rhs=vn[:, m, :], start=True, stop=True), reads=[r_AT, r_vn], writes=[rbo2], same_ok=True)
                        bst, rbst = self.bank()
                        for m in range(4):
                            S.op("pe", lambda e: e.matmul(bst[:, m * 128:(m + 1) * 128], lhsT=kdec[:, m, :], rhs=vn[:, m, :], start=True, stop=True), reads=[r_kdec, r_vn], writes=[rbst], same_ok=True)
                        S.op("dve", lambda e: e.tensor_tensor(out=ot[:], in0=pv4(bo2), in1=ot[:], op=ALU.add), reads=[rbo2, r_ot], writes=[r_ot])
                        for d in range(2):
                            c = cc[d]
                            src = ot[:, 2 * d:2 * d + 2, :]
                            dst = oacc[:, c, :].rearrange("p (j v) -> p j v", j=2)
                            if c not in written:
                                written.add(c)
                                S.op("pool", lambda e: e.tensor_copy(dst, src), reads=[r_ot], writes=[r_oacc[c]])
                            else:
                                S.op("pool", lambda e: e.tensor_tensor(out=dst, in0=dst, in1=src, op=ALU.add), reads=[r_ot, r_oacc[c]], writes=[r_oacc[c]])
                        for m in range(4):
                            S.op("dve", lambda e: e.scalar_tensor_tensor(out=Sf[:, m, :], in0=Sf[:, m, :], scalar=sc["dl"][:, cm_[m], m:m + 1], in1=bst[:, m * 128:(m + 1) * 128],
                                                                         op0=ALU.mult, op1=ALU.add), reads=[r_Sf, r_sc, rbst], writes=[r_Sf])
                        S.op("act", lambda e: e.copy(Sb[:], Sf[:]), reads=[r_Sf], writes=[r_Sb])
                    S.barrier()
                with ExitStack() as p4:
                    ogT = self.sb([128, 2, 512], BF16, es=p4); r_ogT = Res("ogT")
                    og = self.sb([128, 256], F32, es=p4); r_og = Res("og")
                    ogb = self.sb([128, 256], BF16, es=p4); r_ogb = Res("ogb")
                    zs = self.sb([128, 256], F32, es=p4); r_zs = Res("zs")
                    junk = self.sb([128, 128], F32, es=p4); r_junk = Res("junk")
                    ss = self.sb([128, 2], F32, es=p4); r_ss = Res("ss")
                    for ti, (t0, n, lc) in enumerate(TT):
                        for tt in range(n // 128):
                            t = t0 // 128 + tt
                            for j in range(2):
                                S.op("act", lambda e: e.activation(out=junk[:], in_=oacc[:, t, j * 128:(j + 1) * 128], func=AF.Square, accum_out=ss[:, j:j + 1]),
                                     reads=[r_oacc[t]], writes=[r_junk, r_ss])
                            S.op("act", lambda e: e.activation(out=ss[:], in_=ss[:], func=AF.Sqrt, bias=epsc[:, 0:1], scale=1.0 / 128.0), reads=[r_ss, r_eps], writes=[r_ss])
                            S.op("dve", lambda e: e.reciprocal(out=ss[:], in_=ss[:]), reads=[r_ss], writes=[r_ss])
                            bz, rbz = self.bank()
                            for kc in range(8):
                                S.op("pe", lambda e: e.matmul(bz[:, 0:256], lhsT=self.uT[:, kc, t * 128:(t + 1) * 128], rhs=wz[:, kc, :], start=(kc == 0), stop=(kc == 7)),
                                     reads=[r_wB, self.r_u[ti]], writes=[rbz], same_ok=True)
                            S.op("act", lambda e: e.activation(out=zs[:], in_=bz[:, 0:256], func=AF.Silu), reads=[rbz], writes=[r_zs])
                            for j in range(2):
                                S.op("dve", lambda e: e.scalar_tensor_tensor(out=og[:, j * 128:(j + 1) * 128], in0=oacc[:, t, j * 128:(j + 1) * 128], scalar=ss[:, j:j + 1], in1=ngrep[:],
                                                                             op0=ALU.mult, op1=ALU.mult), reads=[r_oacc[t], r_ss, r_ngr], writes=[r_og])
                            S.op("dve", lambda e: e.tensor_tensor(out=ogb[:], in0=og[:], in1=zs[:], op=ALU.mult), reads=[r_og, r_zs], writes=[r_ogb])
                            b2, rb2 = self.bank()
                            for j in range(2):
                                S.op("pe", lambda e: e.matmul(b2[:, j * 128:(j + 1) * 128], lhsT=ogb[:, j * 128:(j + 1) * 128], rhs=self.identb[:], start=True, stop=True),
                                     reads=[r_ogb, self.r_ident], writes=[rb2], same_ok=True)
                            S.op("act", lambda e: e.copy(ogT[:, :, tt * 128:(tt + 1) * 128], b2[:, 0:256].rearrange("p (j c) -> p j c", j=2)), reads=[rb2], writes=[r_ogT])
                        for oc in range(8):
                            bk, rb = self.bank()
                            for j in range(2):
                                S.op("pe", lambda e: e.matmul(bk[:, 0:n], lhsT=wo[:, j, oc * 128:(oc + 1) * 128], rhs=ogT[:, j, 0:n], start=(j == 0), stop=(j == 1)),
                                     reads=[r_wC, r_ogT], writes=[rb], same_ok=True)
                            hz = self.hT[:, oc, t0:t0 + n]
                            S.op("dve", lambda e: e.scalar_tensor_tensor(out=hz, in0=bk[:, 0:n], scalar=self.mcol(i, 2, oc, lc), in1=hz, op0=ALU.mult, op1=ALU.add),
                                 reads=[rb, self.r_h[ti], self.r_mod[i]], writes=[self.r_h[ti]])
                    S.barrier()


def build_program(depth_run=DEPTH, mixers=True, dbg=False):
    nc = bass.Bass("TRN2", target_bir_lowering=False)
    es = ExitStack()
    with es:
        kb = KB(nc, es, depth_run, mixers, dbg)
        kb.build()
        print("instructions", kb.S.ninst, "sems", kb.S.nsem, flush=True)
    return nc, kb


def _rope_tables(dim):
    n_freq = dim // 4
    inv = (10000.0 ** (-np.arange(n_freq, dtype=np.float32) / n_freq)).astype(np.float32)
    tok = np.arange(TL)
    row = (tok // 64).astype(np.float32)
    col = (tok % 64).astype(np.float32)
    ang = np.concatenate([row[:, None] * inv, col[:, None] * inv], -1).astype(np.float32)
    c = np.ones((dim // 2, T), np.float32)
    s = np.zeros((dim // 2, T), np.float32)
    c[:, :TL] = np.cos(ang).T
    s[:, :TL] = np.sin(ang).T
    return c, s


def _mla_host(inputs, shared, g):
    w_in = g("mla_w_in")[0]
    w_qb = g("mla_w_qb")[0]
    ev = np.arange(0, 32, 2)
    od = ev + 1
    cols = []
    for h in range(16):
        b = h * 96
        cols += list(range(b, b + 64)) + list(b + 64 + ev) + list(b + 64 + od) + list(b + 64 + ev) + list(b + 64 + od)
    shared["mla_wqx"] = np.ascontiguousarray(w_qb[:, cols])
    ia = list(1024 + ev) * 4
    ib = list(1024 + od) * 4
    shared["mla_wkr"] = np.ascontiguousarray(np.concatenate(
        [w_in[:, 0:64], w_in[:, ia], w_in[:, 0:64], w_in[:, ib]], axis=1))
    c, s = _rope_tables(32)
    one = np.ones((64, T), np.float32)
    shared["mla_qtab"] = np.concatenate([one, c, s, s, c], 0)
    shared["mla_kta"] = np.concatenate([one, c, -c, s, s], 0)
    shared["mla_ktb"] = np.concatenate([one, -s, s, c, c], 0)


def _diff_host(inputs, shared, g):
    w_in = g("diff_w_in")[0]
    ev = np.arange(0, 64, 2)
    od = ev + 1
    cols = []
    for h in range(8):
        for m in range(2):
            bq = h * 128 + m * 64
            bk = 1024 + h * 128 + m * 64
            cols += list(bq + ev) + list(bq + od) + list(bq + ev) + list(bq + od)
            cols += list(bk + ev) * 4
            cols += list(bk + od) * 4
    shared["diff_wx"] = np.ascontiguousarray(w_in[:, cols])
    shared["diff_wv"] = np.ascontiguousarray(w_in[:, 2048:3072])
    shared["diff_w_out"] = g("diff_w_out")[0]
    c, s = _rope_tables(64)
    shared["diff_qtab"] = np.concatenate([c, s, s, c], 0)
    shared["diff_kta"] = np.concatenate([c, -c, s, s], 0)
    shared["diff_ktb"] = np.concatenate([-s, s, c, c], 0)
    shared["diff_lam"] = np.ascontiguousarray(np.stack([g("diff_lambda_q1")[0], g("diff_lambda_k1")[0],
                                                        g("diff_lambda_q2")[0], g("diff_lambda_k2")[0]], axis=1))
    shared["diff_subln"] = g("diff_subln")[0].reshape(1, 128)


def _gla_host(inputs, shared, g):
    shared["gla_w_in"] = g("gla_w_in")[0]
    shared["gla_gw"] = np.ascontiguousarray(np.stack([g("gla_gate_w_fwd")[0], g("gla_gate_w_bwd")[0]], 0))
    shared["gla_gb"] = np.ascontiguousarray(np.stack([g("gla_gate_b_fwd")[0].reshape(4, 128), g("gla_gate_b_bwd")[0].reshape(4, 128)], 0))
    shared["gla_norm"] = g("gla_norm")[0].reshape(2, 128)
    shared["gla_w_out"] = g("gla_w_out")[0]


def _gdn_host(inputs, shared, g):
    w_in = g("gdn_w_in")[0]
    shared["gdn_w_in"] = w_in
    cols = []
    for kh in range(8):
        for base in (6144, 6160, 6176, 6192):
            cols += [base + 2 * kh, base + 2 * kh + 1]
    shared["gdn_wg"] = np.ascontiguousarray(w_in[:, cols])
    shared["gdn_conv"] = g("gdn_conv_w")[0].reshape(5, 32, 128)
    shared["gdn_hc"] = np.ascontiguousarray(np.concatenate([g("gdn_a_log_fwd")[0], g("gdn_a_log_bwd")[0],
                                                            g("gdn_dt_bias_fwd")[0], g("gdn_dt_bias_bwd")[0]]).reshape(1, 64))
    shared["gdn_norm"] = g("gdn_norm")[0].reshape(1, 128)
    shared["gdn_w_out"] = g("gdn_w_out")[0]

def make_in_maps(inputs):
    g = lambda k: np.ascontiguousarray(np.asarray(inputs[k], dtype=np.float32))
    shared = {
        "c_ctx": g("c_ctx").reshape(8, 128),
        "ada_w": g("ada_w"), "ada_b": g("ada_b").reshape(DEPTH, 48, 128),
        "ln1_g": g("ln1_g").reshape(DEPTH, 8, 128), "ln1_b": g("ln1_b").reshape(DEPTH, 8, 128),
        "ln2_g": g("ln2_g").reshape(DEPTH, 8, 128), "ln2_b": g("ln2_b").reshape(DEPTH, 8, 128),
        "mlp_w1": g("mlp_w1"), "mlp_w2": g("mlp_w2"),
        "mla_w_in": g("mla_w_in")[0], "mla_q_norm": g("mla_q_norm")[0].reshape(6, 128),
        "mla_kv_norm": g("mla_kv_norm")[0].reshape(2, 128),
        "mla_w_kvb": g("mla_w_kvb")[0], "mla_w_out": g("mla_w_out")[0],
    }
    _mla_host(inputs, shared, g)
    _diff_host(inputs, shared, g)
    _gla_host(inputs, shared, g)
    _gdn_host(inputs, shared, g)
    x, c, ctx = g("x"), g("c"), g("ctx")
    maps = []
    for b in range(8):
        m = dict(shared)
        m["x"] = x[b]
        m["ctx"] = ctx[b]
        m["c"] = c[b].reshape(8, 128)
        maps.append(m)
    return maps


def kernel(**inputs):
    nc, kb = build_program()
    maps = make_in_maps(inputs)
    res = run_bass_kernel_spmd(nc, maps, core_ids=list(range(8)))
    return np.stack([np.asarray(r["out"], dtype=np.float32) for r in res.results], axis=0)
```

```python
import math
import numpy as np
import concourse.bass as bass
import concourse.mybir as mybir
from concourse.bass_utils import run_bass_kernel_spmd
from contextlib import ExitStack

F32 = mybir.dt.float32
BF16 = mybir.dt.bfloat16
ALU = mybir.AluOpType
AF = mybir.ActivationFunctionType

DEPTH = 4
D = 1024
TL = 2048
TC = 256
T = TL + TC
ALPHA = (2 * DEPTH) ** 0.25
EPS = 1e-6
EPS_LN = EPS / (ALPHA * ALPHA)
TT = [(0, 512, 0), (512, 512, 0), (1024, 512, 0), (1536, 512, 0), (2048, 256, 1)]


class Res:
    __slots__ = ("name", "w", "r", "dsem", "dcnt")

    def __init__(self, name=""):
        self.name = name
        self.w = None
        self.r = {}
        self.dsem = None
        self.dcnt = 0


class Sched:
    EPOCH = 30000

    def __init__(self, nc, es):
        self.nc = nc
        self.es = es
        self.eng = {"pe": nc.tensor, "dve": nc.vector, "act": nc.scalar,
                    "pool": nc.gpsimd, "sp": nc.sync}
        self.cnt = {e: 0 for e in self.eng}
        self.cursem = {e: None for e in self.eng}
        self.last = {e: None for e in self.eng}
        self.seen = {e: {} for e in self.eng}
        self.nsem = 0
        self.ninst = 0
        self.owners = []
        self.out_events = []

    def newsem(self, name):
        self.nsem += 1
        return self.es.enter_context(self.nc.semaphore(f"{name}_{self.nsem}"))

    def _wait(self, e, ev):
        sem, val, _ = ev
        k = id(sem)
        if self.seen[e].get(k, 0) >= val:
            return
        self.eng[e].wait_ge(sem, val)
        self.seen[e][k] = val

    def _deps(self, e, reads, writes, same_ok):
        for r in reads:
            if r.w is not None and not (same_ok and r.w[2] == e):
                self._wait(e, r.w)
        for w in writes:
            if w.w is not None and not (same_ok and w.w[2] == e):
                self._wait(e, w.w)
            for ev in w.r.values():
                if not (same_ok and ev[2] == e):
                    self._wait(e, ev)

    def op(self, e, fn, reads=(), writes=(), same_ok=False):
        self._deps(e, reads, writes, same_ok)
        ins = fn(self.eng[e])
        if self.cnt[e] % self.EPOCH == 0:
            self.cursem[e] = self.newsem("c" + e)
        self.cnt[e] += 1
        val = (self.cnt[e] - 1) % self.EPOCH + 1
        sem = self.cursem[e]
        ins.then_inc(sem, 1)
        ev = (sem, val, e)
        self.last[e] = ev
        for r in reads:
            r.r[id(sem)] = ev
        for w in writes:
            w.w = ev
            w.r = {}
        self.ninst += 1
        return ev

    def dma(self, q, pairs, owner, reads=(), writes=(), **kw):
        self._deps(q, reads, writes, False)
        if owner.dsem is None:
            owner.dsem = self.newsem("d")
            self.owners.append(owner)
        if owner.dcnt > 0:
            self._wait(q, (owner.dsem, owner.dcnt, "dma"))
        for (o, i) in pairs:
            self.eng[q].dma_start(out=o, in_=i, **kw).then_inc(owner.dsem, 16)
            owner.dcnt += 16
        ev = (owner.dsem, owner.dcnt, "dma")
        for r in reads:
            r.r[id(owner.dsem)] = ev
        for w in writes:
            w.w = ev
            w.r = {}
        self.ninst += len(pairs)
        return ev

    def barrier(self):
        evs = [ev for ev in self.last.values() if ev is not None]
        evs += [(o.dsem, o.dcnt, "dma") for o in self.owners if o.dcnt > 0]
        for e in ("pe", "dve", "act", "pool", "sp"):
            for ev in evs:
                if ev[2] != e:
                    self._wait(e, ev)

    def finish(self):
        for ev in self.out_events:
            self._wait("sp", ev)


class KB:
    def __init__(self, nc, es, depth_run=DEPTH, mixers=True, dbg=False):
        self.nc, self.es = nc, es
        self.S = Sched(nc, es)
        self.depth_run = depth_run
        self.mixers = mixers
        self.dbg = dbg
        self.din = {}
        self._n = 0

    def sb(self, shape, dt, es=None, name=None):
        self._n += 1
        return (es or self.es).enter_context(self.nc.sbuf_tensor(name or f"t{self._n}", list(shape), dt))

    def dram_in(self, name, shape):
        t = self.nc.dram_tensor(name, list(shape), F32, kind="ExternalInput").ap()
        self.din[name] = t
        return t

    def bank(self):
        i = self.bank_i
        self.bank_i = (i + 1) % self.nrr
        return self.banks[i], self.bank_res[i]

    def declare(self):
        di = self.dram_in
        di("x", [TL, D]); di("ctx", [TC, D]); di("c", [8, 128]); di("c_ctx", [8, 128])
        di("ada_w", [DEPTH, D, 6 * D]); di("ada_b", [DEPTH, 48, 128])
        for n in ("ln1_g", "ln1_b", "ln2_g", "ln2_b"):
            di(n, [DEPTH, 8, 128])
        di("mlp_w1", [DEPTH, D, 4 * D]); di("mlp_w2", [DEPTH, 4 * D, D])
        di("mla_w_in", [D, 1056]); di("mla_q_norm", [6, 128]); di("mla_kv_norm", [2, 128])
        di("mla_w_kvb", [256, 2048]); di("mla_w_out", [D, D])
        di("mla_wqx", [768, 2048]); di("mla_wkr", [D, 256])
        di("mla_qtab", [128, T]); di("mla_kta", [128, T]); di("mla_ktb", [128, T])
        di("diff_wx", [D, 16 * 384]); di("diff_wv", [D, D]); di("diff_w_out", [D, D])
        di("diff_qtab", [128, T]); di("diff_kta", [128, T]); di("diff_ktb", [128, T])
        di("diff_lam", [64, 4]); di("diff_subln", [1, 128])
        di("gla_w_in", [D, 3104]); di("gla_gw", [2, 16, 512]); di("gla_gb", [2, 4, 128])
        di("gla_norm", [2, 128]); di("gla_w_out", [D, D])
        di("gdn_w_in", [D, 6208]); di("gdn_wg", [D, 64]); di("gdn_conv", [5, 32, 128])
        di("gdn_hc", [1, 64]); di("gdn_norm", [1, 128]); di("gdn_w_out", [2 * D, D])
        self.out = self.nc.dram_tensor("out", [TL, D], F32, kind="ExternalOutput").ap()
        if self.dbg:
            self.out_c = self.nc.dram_tensor("out_c", [TC, D], F32, kind="ExternalOutput").ap()

        nc = self.nc
        self.banks = [self.es.enter_context(nc.psum_tensor(f"bank{i}", [128, 512], F32)) for i in range(8)]
        self.bank_res = [Res(f"bank{i}") for i in range(8)]
        self.bank_i = 0
        self.nrr = 6
        self.hT = self.sb([128, 8, T], F32, name="hT")
        self.r_h = [Res(f"h{t}") for t in range(len(TT))]
        self.uT = self.sb([128, 8, T], BF16, name="uT")
        self.r_u = [Res(f"u{t}") for t in range(len(TT))]
        self.ident = self.sb([128, 128], F32, name="ident"); self.r_ident = Res("ident")
        self.identb = self.sb([128, 128], BF16, name="identb")
        self.ones = self.sb([128, 128], F32, name="ones")
        self.identr = self.sb([128, 128], F32, name="identr")
        self.onesr = self.sb([128, 128], F32, name="onesr")
        self.sT = self.sb([128, 8, 2], F32, name="sT"); self.r_sT = Res("sT")
        self.sTb = self.sb([128, 8, 2], BF16, name="sTb")
        self.mod = [self.sb([128, 48, 2], F32, name=f"mod{i}") for i in range(DEPTH)]
        self.r_mod = [Res(f"mod{i}") for i in range(DEPTH)]
        self.lnp = self.sb([128, 4, DEPTH, 8], F32, name="lnp"); self.r_lnp = Res("lnp")
        self.adab = self.sb([128, DEPTH, 48], F32, name="adab"); self.r_adab = Res("adab")
        self.vst = self.sb([64, 128], F32, name="vst"); self.r_vst = Res("vst")
        self.NW = 3
        self.wslot = [self.sb([128, 4096], BF16, name=f"wslot{i}") for i in range(self.NW)]
        self.r_wslot = [Res(f"wslot{i}") for i in range(self.NW)]
        self.w_i = 0

    def wnext(self):
        i = self.w_i
        self.w_i = (i + 1) % self.NW
        return self.wslot[i], self.r_wslot[i]

    def load_cols(self, src, n, dst, r_dst):
        S = self.S
        S.dma("sp", [(self.vst[0:n, :], src)], self.r_vst, writes=[self.r_vst])
        bk, rb = self.bank()
        S.op("pe", lambda e: e.transpose(bk[:, 0:n], self.vst[0:n, :], self.ident[0:n, 0:n]),
             reads=[self.r_vst, self.r_ident], writes=[rb], same_ok=True)
        S.op("dve", lambda e: e.tensor_copy(dst, bk[:, 0:n]), reads=[rb], writes=[r_dst])

    def setup(self):
        S, nc = self.S, self.nc
        S.op("pool", lambda e: e.memset(self.ident[:], 0.0), writes=[self.r_ident])
        S.op("pool", lambda e: e.affine_select(out=self.ident[:], in_=self.ident[:], compare_op=ALU.not_equal,
                                              fill=1.0, base=0, pattern=[[-1, 128]], channel_multiplier=1),
             reads=[self.r_ident], writes=[self.r_ident])
        S.op("dve", lambda e: e.tensor_copy(self.identb[:], self.ident[:]), reads=[self.r_ident], writes=[self.r_ident])
        S.op("dve", lambda e: e.memset(self.ones[:], 1.0), writes=[self.r_ident])
        S.op("dve", lambda e: e.tensor_copy(self.identr[:].bitcast(mybir.dt.float32r), self.ident[:]), reads=[self.r_ident], writes=[self.r_ident])
        S.op("dve", lambda e: e.tensor_copy(self.onesr[:].bitcast(mybir.dt.float32r), self.ones[:]), reads=[self.r_ident], writes=[self.r_ident])
        for k, n in enumerate(("ln1_g", "ln1_b", "ln2_g", "ln2_b")):
            for i in range(DEPTH):
                self.load_cols(self.din[n][i], 8, self.lnp[:, k, i, :], self.r_lnp)
        for i in range(DEPTH):
            self.load_cols(self.din["ada_b"][i], 48, self.adab[:, i, :], self.r_adab)
        self.load_cols(self.din["c"], 8, self.sT[:, :, 0], self.r_sT)
        self.load_cols(self.din["c_ctx"], 8, self.sT[:, :, 1], self.r_sT)
        S.op("act", lambda e: e.activation(out=self.sT[:], in_=self.sT[:], func=AF.Silu), reads=[self.r_sT], writes=[self.r_sT])
        S.op("dve", lambda e: e.tensor_copy(self.sTb[:], self.sT[:]), reads=[self.r_sT], writes=[self.r_sT])
        with ExitStack() as ph:
            xs = [self.sb([128, D], F32, es=ph) for _ in range(2)]
            r_xs = [Res("xs0"), Res("xs1")]
            for t in range(18):
                src = self.din["x"][t * 128:(t + 1) * 128, :] if t < 16 else self.din["ctx"][(t - 16) * 128:(t - 15) * 128, :]
                st, rs = xs[t % 2], r_xs[t % 2]
                S.dma("sp", [(st[:], src)], rs, writes=[rs])
                ti = min(t // 4, 4)
                for g in range(2):
                    bk, rb = self.bank()
                    for j in range(4):
                        S.op("pe", lambda e: e.transpose(bk[:, j * 128:(j + 1) * 128], st[:, (g * 4 + j) * 128:(g * 4 + j + 1) * 128], self.ident[:]),
                             reads=[rs, self.r_ident], writes=[rb], same_ok=True)
                    dst = self.hT[:, g * 4:(g + 1) * 4, t * 128:(t + 1) * 128]
                    srcp = bk[:, 0:512].rearrange("p (j n) -> p j n", j=4)
                    if g == 0:
                        S.op("dve", lambda e: e.tensor_copy(dst, srcp), reads=[rb], writes=[self.r_h[ti]])
                    else:
                        S.op("act", lambda e: e.copy(dst, srcp), reads=[rb], writes=[self.r_h[ti]])
            S.barrier()

    def mods_gen(self, i, stg, r_stg):
        S = self.S
        aw = self.din["ada_w"][i]
        bk, rb = self.banks[7], self.bank_res[7]
        for blk in range(24):
            st, rs = stg[blk % 2], r_stg[blk % 2]
            S.dma("pool", [(st[:], aw[:, blk * 256:(blk + 1) * 256].rearrange("(k p) n -> p k n", p=128))], rs, writes=[rs])
            for cc in range(2):
                c = blk * 2 + cc
                for kc in range(8):
                    S.op("pe", lambda e: e.matmul(bk[:, c * 2:(c + 1) * 2], lhsT=st[:, kc, cc * 128:(cc + 1) * 128], rhs=self.sTb[:, kc, :],
                                                  start=(kc == 0), stop=(kc == 7)),
                         reads=[rs, self.r_sT], writes=[rb], same_ok=True)
            yield
        m, rm = self.mod[i], self.r_mod[i]
        pv = bk[:, 0:96].rearrange("p (c l) -> p c l", l=2)
        for l in range(2):
            S.op("dve", lambda e: e.tensor_tensor(out=m[:, :, l], in0=pv[:, :, l], in1=self.adab[:, i, :], op=ALU.add),
                 reads=[rb, self.r_adab], writes=[rm])
        for c0 in (8, 32):
            S.op("dve", lambda e: e.tensor_scalar(out=m[:, c0:c0 + 8, :], in0=m[:, c0:c0 + 8, :], scalar1=1.0, scalar2=None, op0=ALU.add),
                 reads=[rm], writes=[rm])
        for c0 in (16, 40):
            S.op("dve", lambda e: e.tensor_scalar(out=m[:, c0:c0 + 8, :], in0=m[:, c0:c0 + 8, :], scalar1=1.0 / ALPHA, scalar2=None, op0=ALU.mult),
                 reads=[rm], writes=[rm])

    def mods(self, i):
        with ExitStack() as ph:
            stg = [self.sb([128, 8, 256], BF16, es=ph) for _ in range(2)]
            r_stg = [Res("as0"), Res("as1")]
            for _ in self.mods_gen(i, stg, r_stg):
                pass
            self.S.barrier()

    def mcol(self, i, which, kc, lc):
        return self.mod[i][:, which * 8 + kc, lc:lc + 1]

    def modulate_all(self, i, sub):
        S = self.S
        for ti, (t0, n, lc) in enumerate(TT):
            for kc in range(8):
                S.op("dve", lambda e: e.tensor_scalar(out=self.uT[:, kc, t0:t0 + n], in0=self.hT[:, kc, t0:t0 + n],
                                                      scalar1=self.mcol(i, 3 * sub + 1, kc, lc), scalar2=self.mcol(i, 3 * sub, kc, lc),
                                                      op0=ALU.mult, op1=ALU.add),
                     reads=[self.r_h[ti], self.r_mod[i]], writes=[self.r_u[ti]])

    def layer_norm(self, i, sub, nxt):
        S = self.S
        R32 = mybir.dt.float32r
        with ExitStack() as ph:
            sq = [self.sb([128, 512], F32, es=ph) for _ in range(3)]; r_sq = [Res() for _ in range(3)]
            mean = [self.sb([128, 512], F32, es=ph) for _ in range(2)]; r_mean = [Res(), Res()]
            msq = [self.sb([128, 512], F32, es=ph) for _ in range(2)]; r_msq = [Res(), Res()]
            rstd = [self.sb([128, 512], F32, es=ph) for _ in range(2)]; r_rstd = [Res(), Res()]
            tmp = [self.sb([128, 512], F32, es=ph) for _ in range(3)]; r_tmp = [Res() for _ in range(3)]
            epsc = self.sb([128, 1], F32, es=ph); r_eps = Res()
            S.op("pool", lambda e: e.memset(epsc[:], EPS_LN), writes=[r_eps])
            banks = {}
            cnt = [0, 0]

            def stats(ti):
                t0, n, lc = TT[ti]
                rh = self.r_h[ti]
                b1, rb1 = self.bank()
                b2, rb2 = self.bank()
                banks[ti] = (b1, rb1, b2, rb2)
                for kc in range(8):
                    z = self.hT[:, kc, t0:t0 + n]
                    s_, rs_ = sq[cnt[0] % 3], r_sq[cnt[0] % 3]
                    cnt[0] += 1
                    S.op("act", lambda e: e.activation(out=s_[:, 0:n].bitcast(R32), in_=z, func=AF.Square), reads=[rh], writes=[rs_])
                    S.op("pe", lambda e: e.matmul(b1[:, 0:n], lhsT=self.ones[:], rhs=z, start=(kc == 0), stop=(kc == 7)),
                         reads=[rh, self.r_ident], writes=[rb1], same_ok=True)
                    S.op("pe", lambda e: e.matmul(b2[:, 0:n], lhsT=self.onesr[:].bitcast(R32), rhs=s_[:, 0:n].bitcast(R32), start=(kc == 0), stop=(kc == 7)),
                         reads=[rs_, self.r_ident], writes=[rb2], same_ok=True)

            def finish_stats(ti):
                t0, n, lc = TT[ti]
                b1, rb1, b2, rb2 = banks[ti]
                mn, rmn = mean[ti % 2], r_mean[ti % 2]
                ms, rms = msq[ti % 2], r_msq[ti % 2]
                rs, rrs = rstd[ti % 2], r_rstd[ti % 2]
                S.op("act", lambda e: e.activation(out=mn[:, 0:n], in_=b1[:, 0:n], func=AF.Copy, scale=1.0 / D), reads=[rb1], writes=[rmn])
                S.op("dve", lambda e: e.tensor_tensor(out=ms[:, 0:n], in0=mn[:, 0:n], in1=mn[:, 0:n], op=ALU.mult), reads=[rmn], writes=[rms])
                S.op("dve", lambda e: e.scalar_tensor_tensor(out=rs[:, 0:n], in0=b2[:, 0:n], scalar=1.0 / D, in1=ms[:, 0:n],
                                                             op0=ALU.mult, op1=ALU.subtract), reads=[rb2, rms], writes=[rrs])
                S.op("act", lambda e: e.activation(out=rs[:, 0:n], in_=rs[:, 0:n], func=AF.Sqrt, bias=epsc[:, 0:1], scale=1.0),
                     reads=[rrs, r_eps], writes=[rrs])
                S.op("dve", lambda e: e.reciprocal(out=rs[:, 0:n], in_=rs[:, 0:n]), reads=[rrs], writes=[rrs])

            def normalize(ti):
                t0, n, lc = TT[ti]
                rh = self.r_h[ti]
                mn, rmn = mean[ti % 2], r_mean[ti % 2]
                rs, rrs = rstd[ti % 2], r_rstd[ti % 2]
                for kc in range(8):
                    z = self.hT[:, kc, t0:t0 + n]
                    tp, rtp = tmp[cnt[1] % 3], r_tmp[cnt[1] % 3]
                    cnt[1] += 1
                    S.op("pool", lambda e: e.tensor_tensor(out=tp[:, 0:n], in0=z, in1=mn[:, 0:n], op=ALU.subtract), reads=[rh, rmn], writes=[rtp])
                    S.op("dve", lambda e: e.tensor_tensor(out=tp[:, 0:n], in0=tp[:, 0:n], in1=rs[:, 0:n], op=ALU.mult), reads=[rtp, rrs], writes=[rtp])
                    S.op("act", lambda e: e.activation(out=z, in_=tp[:, 0:n], func=AF.Identity,
                                                       bias=self.lnp[:, 2 * sub + 1, i, kc:kc + 1], scale=self.lnp[:, 2 * sub, i, kc:kc + 1]),
                         reads=[rtp, self.r_lnp], writes=[rh])
                    if nxt is not None:
                        ni, nsub = nxt
                        S.op("dve", lambda e: e.tensor_scalar(out=self.uT[:, kc, t0:t0 + n], in0=z,
                                                              scalar1=self.mcol(ni, 3 * nsub + 1, kc, lc), scalar2=self.mcol(ni, 3 * nsub, kc, lc),
                                                              op0=ALU.mult, op1=ALU.add),
                             reads=[rh, self.r_mod[ni]], writes=[self.r_u[ti]])

            nt = len(TT)
            stats(0)
            finish_stats(0)
            for ti in range(nt):
                if ti + 1 < nt:
                    stats(ti + 1)
                normalize(ti)
                if ti + 1 < nt:
                    finish_stats(ti + 1)
            S.barrier()

    def mlp(self, i):
        S = self.S
        w1 = self.din["mlp_w1"][i]
        w2 = self.din["mlp_w2"][i]
        with ExitStack() as ph:
            mg = None
            if i + 1 < DEPTH:
                mstg = [self.sb([128, 8, 256], BF16, es=ph) for _ in range(2)]
                mg = self.mods_gen(i + 1, mstg, [Res("ms0"), Res("ms1")])
            ab = [self.sb([128, 4, 512], BF16, es=ph) for _ in range(2)]; r_ab = [Res(), Res()]
            rl = [self.sb([128, 512], BF16, es=ph) for _ in range(3)]; r_rl = [Res(), Res(), Res()]
            rli = 0
            step = 0
            for j in range(8):
                wa, r_wa = self.wnext()
                wb, r_wb = self.wnext()
                S.dma("pool", [(wa[:].rearrange("p (k n) -> p k n", k=8), w1[:, j * 512:(j + 1) * 512].rearrange("(k p) n -> p k n", p=128))],
                      r_wa, writes=[r_wa])
                S.dma("pool", [(wb[:].rearrange("p (k n) -> p k n", k=4), w2[j * 512:(j + 1) * 512, :].rearrange("(k p) n -> p k n", p=128))],
                      r_wb, writes=[r_wb])
                wav = wa[:].rearrange("p (k n) -> p k n", k=8)
                wbv = wb[:].rearrange("p (k n) -> p k n", k=4)
                for ti, (t0, n, lc) in enumerate(TT):
                    a_, r_a = ab[step % 2], r_ab[step % 2]
                    step += 1
                    if mg is not None:
                        try:
                            next(mg)
                        except StopIteration:
                            mg = None
                    for hc in range(4):
                        bk, rb = self.bank()
                        for kc in range(8):
                            S.op("pe", lambda e: e.matmul(bk[:, 0:n], lhsT=wav[:, kc, hc * 128:(hc + 1) * 128], rhs=self.uT[:, kc, t0:t0 + n],
                                                          start=(kc == 0), stop=(kc == 7)),
                                 reads=[r_wa, self.r_u[ti]], writes=[rb], same_ok=True)
                        r_, rr_ = rl[rli % 3], r_rl[rli % 3]
                        rli += 1
                        S.op("act", lambda e: e.activation(out=r_[:, 0:n], in_=bk[:, 0:n], func=AF.Relu), reads=[rb], writes=[rr_])
                        S.op("pool", lambda e: e.tensor_tensor(out=a_[:, hc, 0:n], in0=r_[:, 0:n], in1=r_[:, 0:n], op=ALU.mult),
                             reads=[rr_], writes=[r_a])
                    for oc in range(8):
                        bk, rb = self.bank()
                        for kc in range(4):
                            S.op("pe", lambda e: e.matmul(bk[:, 0:n], lhsT=wbv[:, kc, oc * 128:(oc + 1) * 128], rhs=a_[:, kc, 0:n],
                                                          start=(kc == 0), stop=(kc == 3)),
                                 reads=[r_wb, r_a], writes=[rb], same_ok=True)
                        hz = self.hT[:, oc, t0:t0 + n]
                        S.op("dve", lambda e: e.scalar_tensor_tensor(out=hz, in0=bk[:, 0:n], scalar=self.mcol(i, 5, oc, lc), in1=hz,
                                                                     op0=ALU.mult, op1=ALU.add),
                             reads=[rb, self.r_h[ti], self.r_mod[i]], writes=[self.r_h[ti]])
            if mg is not None:
                for _ in mg:
                    pass
            S.barrier()

    def store_out(self):
        S = self.S
        with ExitStack() as ph:
            os_ = [self.sb([128, D], F32, es=ph) for _ in range(2)]
            r_os = [Res("os0"), Res("os1")]
            nt = 18 if self.dbg else 16
            for t in range(nt):
                st, rs = os_[t % 2], r_os[t % 2]
                ti = min(t // 4, 4)
                for g in range(2):
                    bk, rb = self.bank()
                    for j in range(4):
                        S.op("pe", lambda e: e.transpose(bk[:, j * 128:(j + 1) * 128], self.hT[:, g * 4 + j, t * 128:(t + 1) * 128], self.ident[:]),
                             reads=[self.r_h[ti], self.r_ident], writes=[rb], same_ok=True)
                    if g == 0:
                        S.op("dve", lambda e: e.tensor_copy(st[:, 0:512], bk[:, 0:512]), reads=[rb], writes=[rs])
                    else:
                        S.op("act", lambda e: e.copy(st[:, 512:1024], bk[:, 0:512]), reads=[rb], writes=[rs])
                dst = self.out[t * 128:(t + 1) * 128, :] if t < 16 else self.out_c[(t - 16) * 128:(t - 15) * 128, :]
                ev = S.dma("sp", [(dst, st[:])], rs, reads=[rs])
                S.out_events.append(ev)
            S.finish()

    def build(self):
        self.declare()
        self.setup()
        self.mods(0)
        self.modulate_all(0, 0)
        for i in range(self.depth_run):
            if self.mixers:
                self.mixer(i)
            self.layer_norm(i, 0, (i, 1))
            self.mlp(i)
            self.layer_norm(i, 1, (i + 1, 0) if i + 1 < DEPTH else None)
        self.store_out()


    def mixer(self, i):
        if i == 0:
            self.mla(i)
        elif i == 1:
            self.diff(i)
        elif i == 2:
            self.gla(i)
        elif i == 3:
            self.gdn(i)

    def mla(self, i):
        S = self.S
        SCALE = 96.0 ** -0.5
        with ExitStack() as ph:
            kp = [self.sb([128, T], BF16, es=ph) for _ in range(2)]; r_kp = [Res("kp0"), Res("kp1")]
            qtab = self.sb([128, T], F32, es=ph); r_qtab = Res("qtab")
            opad = self.sb([128, 2, 128], BF16, es=ph); r_opad = Res("opad")
            nrm = self.sb([128, 8], F32, es=ph); r_nrm = Res("nrm")
            epsc = self.sb([128, 1], F32, es=ph); r_eps = Res("eps")
            S.dma("sp", [(qtab[:], self.din["mla_qtab"])], r_qtab, writes=[r_qtab])
            S.op("pool", lambda e: e.memset(opad[:], 0.0), writes=[r_opad])
            S.op("pool", lambda e: e.memset(opad[:, 0, 0:64], 1.0), reads=[r_opad], writes=[r_opad])
            S.op("pool", lambda e: e.memset(opad[:, 1, 64:128], 1.0), reads=[r_opad], writes=[r_opad])
            S.op("pool", lambda e: e.memset(epsc[:], EPS), writes=[r_eps])
            self.load_cols(self.din["mla_q_norm"], 6, nrm[:, 0:6], r_nrm)
            self.load_cols(self.din["mla_kv_norm"], 2, nrm[:, 6:8], r_nrm)
            with ExitStack() as p1:
                raw = self.sb([128, 8, 512], F32, es=p1); r_raw = Res("raw")
                sq = [self.sb([128, 512], F32, es=p1) for _ in range(2)]; r_sq = [Res(), Res()]
                rs = self.sb([128, 2, 512], F32, es=p1); r_rs = Res("rs")
                kta = self.sb([128, 512], F32, es=p1); r_kta = Res("kta")
                ktb = self.sb([128, 512], F32, es=p1); r_ktb = Res("ktb")
                t1 = self.sb([128, 512], F32, es=p1); r_t1 = Res("t1")
                t2 = self.sb([128, 512], F32, es=p1); r_t2 = Res("t2")
                wi = []
                for blk in range(2):
                    w_, r_w = self.wnext()
                    S.dma("pool", [(w_[:].rearrange("p (k n) -> p k n", k=8),
                                    self.din["mla_w_in"][:, blk * 512:(blk + 1) * 512].rearrange("(k p) n -> p k n", p=128))], r_w, writes=[r_w])
                    wi.append((w_[:].rearrange("p (k n) -> p k n", k=8), r_w))
                w_, r_wk = self.wnext()
                wkr = w_[:, 0:2048].rearrange("p (k n) -> p k n", k=8)
                S.dma("pool", [(wkr, self.din["mla_wkr"].rearrange("(k p) n -> p k n", p=128))], r_wk, writes=[r_wk])
                for ti, (t0, n, lc) in enumerate(TT):
                    ru = self.r_u[ti]
                    S.dma("sp", [(kta[:, 0:n], self.din["mla_kta"][:, t0:t0 + n])], r_kta, writes=[r_kta])
                    S.dma("sp", [(ktb[:, 0:n], self.din["mla_ktb"][:, t0:t0 + n])], r_ktb, writes=[r_ktb])
                    bA, rbA = self.bank()
                    bB, rbB = self.bank()
                    for kc in range(8):
                        S.op("pe", lambda e: e.matmul(bA[:, 0:n], lhsT=wkr[:, kc, 0:128], rhs=self.uT[:, kc, t0:t0 + n], start=(kc == 0), stop=(kc == 7)),
                             reads=[r_wk, ru], writes=[rbA], same_ok=True)
                    for kc in range(8):
                        S.op("pe", lambda e: e.matmul(bB[:, 0:n], lhsT=wkr[:, kc, 128:256], rhs=self.uT[:, kc, t0:t0 + n], start=(kc == 0), stop=(kc == 7)),
                             reads=[r_wk, ru], writes=[rbB], same_ok=True)
                    S.op("dve", lambda e: e.tensor_tensor(out=t1[64:128, 0:n], in0=bA[64:128, 0:n], in1=kta[64:128, 0:n], op=ALU.mult),
                         reads=[rbA, r_kta], writes=[r_t1])
                    S.op("dve", lambda e: e.tensor_tensor(out=t2[64:128, 0:n], in0=bB[64:128, 0:n], in1=ktb[64:128, 0:n], op=ALU.mult),
                         reads=[rbB, r_ktb], writes=[r_t2])
                    S.op("pool", lambda e: e.tensor_tensor(out=kp[0][64:128, t0:t0 + n], in0=t1[64:128, 0:n], in1=t2[64:128, 0:n], op=ALU.add),
                         reads=[r_t1, r_t2], writes=[r_kp[0]])
                    S.op("pool", lambda e: e.tensor_copy(kp[1][64:128, t0:t0 + n], kp[0][64:128, t0:t0 + n]), reads=[r_kp[0]], writes=[r_kp[1]])
                    for oc in range(8):
                        wv, r_w = wi[oc // 4]
                        bk, rb = self.bank()
                        for kc in range(8):
                            S.op("pe", lambda e: e.matmul(bk[:, 0:n], lhsT=wv[:, kc, (oc % 4) * 128:(oc % 4 + 1) * 128], rhs=self.uT[:, kc, t0:t0 + n],
                                                          start=(kc == 0), stop=(kc == 7)),
                                 reads=[r_w, ru], writes=[rb], same_ok=True)
                        if oc % 2 == 0:
                            S.op("dve", lambda e: e.tensor_copy(raw[:, oc, 0:n], bk[:, 0:n]), reads=[rb], writes=[r_raw])
                        else:
                            S.op("act", lambda e: e.copy(raw[:, oc, 0:n], bk[:, 0:n]), reads=[rb], writes=[r_raw])
                    bq, rbq = self.banks[6], self.bank_res[6]
                    bkv, rbkv = self.banks[7], self.bank_res[7]
                    for oc in range(8):
                        s_, rs_ = sq[oc % 2], r_sq[oc % 2]
                        S.op("act", lambda e: e.activation(out=s_[:, 0:n], in_=raw[:, oc, 0:n], func=AF.Square), reads=[r_raw], writes=[rs_])
                        if oc < 6:
                            S.op("pe", lambda e: e.matmul(bq[:, 0:n], lhsT=self.ones[:], rhs=s_[:, 0:n], start=(oc == 0), stop=(oc == 5)),
                                 reads=[rs_, self.r_ident], writes=[rbq], same_ok=True)
                        else:
                            S.op("pe", lambda e: e.matmul(bkv[:, 0:n], lhsT=self.ones[:], rhs=s_[:, 0:n], start=(oc == 6), stop=(oc == 7)),
                                 reads=[rs_, self.r_ident], writes=[rbkv], same_ok=True)
                    for g, (bb, rbb, dim) in enumerate(((bq, rbq, 768.0), (bkv, rbkv, 256.0))):
                        S.op("act", lambda e: e.activation(out=rs[:, g, 0:n], in_=bb[:, 0:n], func=AF.Sqrt, bias=epsc[:, 0:1], scale=1.0 / dim),
                             reads=[rbb, r_eps], writes=[r_rs])
                        S.op("dve", lambda e: e.reciprocal(out=rs[:, g, 0:n], in_=rs[:, g, 0:n]), reads=[r_rs], writes=[r_rs])
                    for oc in range(8):
                        g = 0 if oc < 6 else 1
                        S.op("dve", lambda e: e.scalar_tensor_tensor(out=self.uT[:, oc, t0:t0 + n], in0=raw[:, oc, 0:n], scalar=nrm[:, oc:oc + 1],
                                                                     in1=rs[:, g, 0:n], op0=ALU.mult, op1=ALU.mult),
                             reads=[r_raw, r_nrm, r_rs], writes=[ru])
                S.barrier()
            with ExitStack() as p2:
                qp = [self.sb([128, T], BF16, es=p2) for _ in range(2)]; r_qp = [Res("qp0"), Res("qp1")]
                vp = [self.sb([128, 18, 128], BF16, es=p2) for _ in range(2)]; r_vp = [Res("vp0"), Res("vp1")]
                pt = [self.sb([128, 512], BF16, es=p2) for _ in range(4)]; r_pt = [Res() for _ in range(4)]
                rden = [self.sb([128, 512], F32, es=p2) for _ in range(2)]; r_rden = [Res("rden0"), Res("rden1")]
                attn_ctr = [0]
                pending = [None]
                self.nrr = 4
                self.bank_i = 0
                opr = [self.sb([128, 512], BF16, es=p2) for _ in range(2)]; r_opr = [Res(), Res()]
                wo = [self.sb([128, D], BF16, es=p2) for _ in range(2)]; r_wo = [Res("wo0"), Res("wo1")]
                for par in range(2):
                    S.op("pool", lambda e: e.memset(vp[par][:], 0.0), writes=[r_vp[par]])
                pti = 0
                wq = wkv = None
                for pair in range(8):
                    S.dma("pool", [(wo[pair % 2][:], self.din["mla_w_out"][pair * 128:(pair + 1) * 128, :])], r_wo[pair % 2], writes=[r_wo[pair % 2]])
                    for par in range(2):
                        h = pair * 2 + par
                        hl = h % 4
                        if hl == 0:
                            w_, r_wq = self.wnext()
                            wq = w_[:, 0:3072].rearrange("p (k n) -> p k n", k=6)
                            wkv = w_[:, 3072:4096].rearrange("p (k n) -> p k n", k=2)
                            S.dma("pool", [(wq, self.din["mla_wqx"][:, h * 128:(h + 4) * 128].rearrange("(k p) n -> p k n", p=128)),
                                           (wkv, self.din["mla_w_kvb"][:, h * 128:(h + 4) * 128].rearrange("(k p) n -> p k n", p=128))],
                                  r_wq, writes=[r_wq])
                        for ti, (t0, n, lc) in enumerate(TT):
                            ru = self.r_u[ti]
                            bk, rb = self.bank()
                            for kc in range(6):
                                S.op("pe", lambda e: e.matmul(bk[:, 0:n], lhsT=wq[:, kc, hl * 128:(hl + 1) * 128], rhs=self.uT[:, kc, t0:t0 + n],
                                                              start=(kc == 0), stop=(kc == 5)),
                                     reads=[r_wq, ru], writes=[rb], same_ok=True)
                            S.op("dve", lambda e: e.tensor_tensor(out=qp[par][:, t0:t0 + n], in0=bk[:, 0:n], in1=qtab[:, t0:t0 + n], op=ALU.mult),
                                 reads=[rb, r_qtab], writes=[r_qp[par]])
                            bk, rb = self.bank()
                            for kc in range(2):
                                S.op("pe", lambda e: e.matmul(bk[0:64, 0:n], lhsT=wkv[:, kc, hl * 128:hl * 128 + 64], rhs=self.uT[:, 6 + kc, t0:t0 + n],
                                                              start=(kc == 0), stop=(kc == 1)),
                                     reads=[r_wq, ru], writes=[rb], same_ok=True)
                            S.op("dve", lambda e: e.tensor_copy(kp[par][0:64, t0:t0 + n], bk[0:64, 0:n]), reads=[rb], writes=[r_kp[par]])
                        for g0 in range(0, 18, 8):
                            ng = min(8, 18 - g0)
                            bk, rb = self.bank()
                            for jt in range(ng):
                                kt = g0 + jt
                                for kc in range(2):
                                    S.op("pe", lambda e: e.matmul(bk[:, jt * 64:(jt + 1) * 64], lhsT=self.uT[:, 6 + kc, kt * 128:(kt + 1) * 128],
                                                                  rhs=wkv[:, kc, hl * 128 + 64:hl * 128 + 128], start=(kc == 0), stop=(kc == 1)),
                                         reads=[r_wq] + self.r_u, writes=[rb], same_ok=True)
                            S.op("dve", lambda e: e.tensor_copy(vp[par][:, g0:g0 + ng, par * 64:par * 64 + 64],
                                                                bk[:, 0:ng * 64].rearrange("p (j d) -> p j d", d=64)), reads=[rb], writes=[r_vp[par]])
                    for ti, (t0, n, lc) in enumerate(TT):
                        kts = list(range(18)) if lc == 0 else [16, 17]
                        items = [(par, kt) for par in range(2) for kt in kts]
                        nb_ = attn_ctr[0] % 2
                        attn_ctr[0] += 1
                        num, r_num = self.banks[4 + 2 * nb_], self.bank_res[4 + 2 * nb_]
                        den, r_den = self.banks[5 + 2 * nb_], self.bank_res[5 + 2 * nb_]
                        sbanks = {}

                        def issue_score(ix):
                            par, kt = items[ix]
                            bk, rb = self.bank()
                            S.op("pe", lambda e: e.matmul(bk[:, 0:n], lhsT=kp[par][:, kt * 128:(kt + 1) * 128], rhs=qp[par][:, t0:t0 + n], start=True, stop=True),
                                 reads=[r_kp[par], r_qp[par]], writes=[rb], same_ok=True)
                            sbanks[ix] = (bk, rb)
                        for ix in range(min(2, len(items))):
                            issue_score(ix)
                        for ix, (par, kt) in enumerate(items):
                            bk, rb = sbanks.pop(ix)
                            p_, rp_ = pt[pti % 4], r_pt[pti % 4]
                            pti += 1
                            S.op("act", lambda e: e.activation(out=p_[:, 0:n], in_=bk[:, 0:n], func=AF.Exp, scale=SCALE), reads=[rb], writes=[rp_])
                            if ix + 2 < len(items):
                                issue_score(ix + 2)
                            first = (ix == 0)
                            last = (ix == len(items) - 1)
                            S.op("pe", lambda e: e.matmul(num[:, 0:n], lhsT=vp[par][:, kt, :], rhs=p_[:, 0:n], start=first, stop=last),
                                 reads=[r_vp[par], rp_], writes=[r_num], same_ok=True)
                            S.op("pe", lambda e: e.matmul(den[:, 0:n], lhsT=opad[:, par, :], rhs=p_[:, 0:n], start=first, stop=last),
                                 reads=[r_opad, rp_], writes=[r_den], same_ok=True)

                        def epilogue(ti=ti, t0=t0, n=n, lc=lc, num=num, den=den, r_num=r_num, r_den=r_den, pair=pair, k=attn_ctr[0]):
                            rd_, rrd_ = rden[k % 2], r_rden[k % 2]
                            S.op("dve", lambda e: e.reciprocal(out=rd_[:, 0:n], in_=den[:, 0:n]), reads=[r_den], writes=[rrd_])
                            o_, ro_ = opr[k % 2], r_opr[k % 2]
                            S.op("dve", lambda e: e.tensor_tensor(out=o_[:, 0:n], in0=num[:, 0:n], in1=rd_[:, 0:n], op=ALU.mult),
                                 reads=[r_num, rrd_], writes=[ro_])
                            for oc in range(8):
                                bk, rb = self.bank()
                                S.op("pe", lambda e: e.matmul(bk[:, 0:n], lhsT=wo[pair % 2][:, oc * 128:(oc + 1) * 128], rhs=o_[:, 0:n], start=True, stop=True),
                                     reads=[r_wo[pair % 2], ro_], writes=[rb], same_ok=True)
                                hz = self.hT[:, oc, t0:t0 + n]
                                S.op("dve", lambda e: e.scalar_tensor_tensor(out=hz, in0=bk[:, 0:n], scalar=self.mcol(i, 2, oc, lc), in1=hz,
                                                                             op0=ALU.mult, op1=ALU.add),
                                     reads=[rb, self.r_h[ti], self.r_mod[i]], writes=[self.r_h[ti]])
                        if pending[0] is not None:
                            pending[0]()
                        pending[0] = epilogue
                if pending[0] is not None:
                    pending[0]()
                S.barrier()
        self.nrr = 6
        self.bank_i = 0

    def diff(self, i):
        S = self.S
        SCALE = 64.0 ** -0.5
        lam_init = 0.8 - 0.6 * math.exp(-0.3 * i)
        self.nrr = 4
        self.bank_i = 0
        with ExitStack() as ph:
            qp = [self.sb([128, T], BF16, es=ph) for _ in range(2)]; r_qp = [Res("qp0"), Res("qp1")]
            kp = [self.sb([128, T], BF16, es=ph) for _ in range(2)]; r_kp = [Res("kp0"), Res("kp1")]
            vp = self.sb([128, 18, 128], BF16, es=ph); r_vp = Res("vp")
            qtab = self.sb([128, 512], F32, es=ph); r_qtab = Res("qtab")
            kta = self.sb([128, 512], F32, es=ph); r_kta = Res("kta")
            ktb = self.sb([128, 512], F32, es=ph); r_ktb = Res("ktb")
            t1 = self.sb([128, 512], F32, es=ph); r_t1 = Res("t1")
            t2 = self.sb([128, 512], F32, es=ph); r_t2 = Res("t2")
            t1s = [self.sb([128, 512], F32, es=ph) for _ in range(2)]; r_t1s = [Res("t1s0"), Res("t1s1")]
            dctr = [0]
            pend1 = [None]
            pt = [self.sb([128, 512], BF16, es=ph) for _ in range(4)]; r_pt = [Res() for _ in range(4)]
            rd = self.sb([128, 2, 512], F32, es=ph); r_rd0 = Res("rd0"); r_rd1 = Res("rd1")
            onb = self.sb([128, 128], BF16, es=ph); r_onb = Res("onb")
            on_ = [self.sb([128, 512], BF16, es=ph) for _ in range(2)]; r_on = [Res(), Res()]
            wo = [self.sb([128, D], BF16, es=ph) for _ in range(2)]; r_wo = [Res("wo0"), Res("wo1")]
            lam = self.sb([128, 4], F32, es=ph); r_lam = Res("lam")
            lv = self.sb([64, 4], F32, es=ph); r_lv = Res("lv")
            sub = self.sb([128, 1], F32, es=ph); r_sub = Res("sub")
            epsc = self.sb([128, 1], F32, es=ph); r_eps = Res("eps")
            S.op("pool", lambda e: e.memset(epsc[:], EPS), writes=[r_eps])
            S.op("pool", lambda e: e.memset(onb[:], 1.0), writes=[r_onb])
            S.dma("sp", [(lv[:], self.din["diff_lam"])], r_lv, writes=[r_lv])
            S.op("dve", lambda e: e.tensor_tensor(out=lv[:, 0:1], in0=lv[:, 0:1], in1=lv[:, 1:2], op=ALU.mult), reads=[r_lv], writes=[r_lv])
            S.op("dve", lambda e: e.tensor_tensor(out=lv[:, 1:2], in0=lv[:, 2:3], in1=lv[:, 3:4], op=ALU.mult), reads=[r_lv], writes=[r_lv])
            bk, rb = self.bank()
            S.op("pe", lambda e: e.matmul(bk[:, 0:2], lhsT=self.ones[0:64, :], rhs=lv[:, 0:2], start=True, stop=True),
                 reads=[r_lv, self.r_ident], writes=[rb], same_ok=True)
            S.op("act", lambda e: e.activation(out=lam[:, 0:2], in_=bk[:, 0:2], func=AF.Exp), reads=[rb], writes=[r_lam])
            S.op("dve", lambda e: e.scalar_tensor_tensor(out=lam[:, 2:3], in0=lam[:, 1:2], scalar=-lam_init, in1=lam[:, 0:1], op0=ALU.add, op1=ALU.subtract),
                 reads=[r_lam], writes=[r_lam])
            self.load_cols(self.din["diff_subln"], 1, sub[:, 0:1], r_sub)
            S.op("dve", lambda e: e.tensor_scalar(out=sub[:], in0=sub[:], scalar1=1.0 - lam_init, scalar2=None, op0=ALU.mult), reads=[r_sub], writes=[r_sub])
            nums = [(self.banks[4], self.bank_res[4]), (self.banks[5], self.bank_res[5])]
            dens = [(self.banks[6], self.bank_res[6]), (self.banks[7], self.bank_res[7])]
            pti = 0
            pti = 0
            for h in range(8):
                S.dma("pool", [(wo[h % 2][:], self.din["diff_w_out"][h * 128:(h + 1) * 128, :])], r_wo[h % 2], writes=[r_wo[h % 2]])
                wv = None
                for m in range(2):
                    mi = h * 2 + m
                    w_, r_w = self.wnext()
                    wx = w_[:, 0:3072].rearrange("p (k n) -> p k n", k=8)
                    prs = [(wx, self.din["diff_wx"][:, mi * 384:(mi + 1) * 384].rearrange("(k p) n -> p k n", p=128))]
                    if m == 0:
                        wv = w_[:, 3072:4096].rearrange("p (k n) -> p k n", k=8)
                        r_wv = r_w
                        prs.append((wv, self.din["diff_wv"][:, h * 128:(h + 1) * 128].rearrange("(k p) n -> p k n", p=128)))
                    S.dma("pool", prs, r_w, writes=[r_w])
                    for ti, (t0, n, lc) in enumerate(TT):
                        ru = self.r_u[ti]
                        S.dma("sp", [(qtab[:, 0:n], self.din["diff_qtab"][:, t0:t0 + n])], r_qtab, writes=[r_qtab])
                        S.dma("sp", [(kta[:, 0:n], self.din["diff_kta"][:, t0:t0 + n])], r_kta, writes=[r_kta])
                        S.dma("sp", [(ktb[:, 0:n], self.din["diff_ktb"][:, t0:t0 + n])], r_ktb, writes=[r_ktb])
                        bq, rbq = self.bank()
                        for kc in range(8):
                            S.op("pe", lambda e: e.matmul(bq[:, 0:n], lhsT=wx[:, kc, 0:128], rhs=self.uT[:, kc, t0:t0 + n], start=(kc == 0), stop=(kc == 7)),
                                 reads=[r_w, ru], writes=[rbq], same_ok=True)
                        S.op("dve", lambda e: e.tensor_tensor(out=qp[m][:, t0:t0 + n], in0=bq[:, 0:n], in1=qtab[:, 0:n], op=ALU.mult),
                             reads=[rbq, r_qtab], writes=[r_qp[m]])
                        bA, rbA = self.bank()
                        for kc in range(8):
                            S.op("pe", lambda e: e.matmul(bA[:, 0:n], lhsT=wx[:, kc, 128:256], rhs=self.uT[:, kc, t0:t0 + n], start=(kc == 0), stop=(kc == 7)),
                                 reads=[r_w, ru], writes=[rbA], same_ok=True)
                        bB, rbB = self.bank()
                        for kc in range(8):
                            S.op("pe", lambda e: e.matmul(bB[:, 0:n], lhsT=wx[:, kc, 256:384], rhs=self.uT[:, kc, t0:t0 + n], start=(kc == 0), stop=(kc == 7)),
                                 reads=[r_w, ru], writes=[rbB], same_ok=True)
                        S.op("dve", lambda e: e.tensor_tensor(out=t1[:, 0:n], in0=bA[:, 0:n], in1=kta[:, 0:n], op=ALU.mult), reads=[rbA, r_kta], writes=[r_t1])
                        S.op("dve", lambda e: e.tensor_tensor(out=t2[:, 0:n], in0=bB[:, 0:n], in1=ktb[:, 0:n], op=ALU.mult), reads=[rbB, r_ktb], writes=[r_t2])
                        S.op("pool", lambda e: e.tensor_tensor(out=kp[m][:, t0:t0 + n], in0=t1[:, 0:n], in1=t2[:, 0:n], op=ALU.add),
                             reads=[r_t1, r_t2], writes=[r_kp[m]])
                for g0 in range(0, 18, 4):
                    ng = min(4, 18 - g0)
                    bk, rb = self.bank()
                    for jt in range(ng):
                        kt = g0 + jt
                        for kc in range(8):
                            S.op("pe", lambda e: e.matmul(bk[:, jt * 128:(jt + 1) * 128], lhsT=self.uT[:, kc, kt * 128:(kt + 1) * 128],
                                                          rhs=wv[:, kc, :], start=(kc == 0), stop=(kc == 7)),
                                 reads=[r_wv] + self.r_u, writes=[rb], same_ok=True)
                    S.op("act", lambda e: e.copy(vp[:, g0:g0 + ng, :], bk[:, 0:ng * 128].rearrange("p (j d) -> p j d", d=128)), reads=[rb], writes=[r_vp])
                for ti, (t0, n, lc) in enumerate(TT):
                    kts = list(range(18)) if lc == 0 else [16, 17]
                    k_ = dctr[0]
                    dctr[0] += 1
                    t1_, rt1_ = t1s[k_ % 2], r_t1s[k_ % 2]

                    def attend(m):
                        nonlocal pti
                        num, r_num = nums[m]
                        den, r_den = dens[m]
                        sbanks = {}

                        def issue_score(ix):
                            kt = kts[ix]
                            bk, rb = self.bank()
                            S.op("pe", lambda e: e.matmul(bk[:, 0:n], lhsT=kp[m][:, kt * 128:(kt + 1) * 128], rhs=qp[m][:, t0:t0 + n], start=True, stop=True),
                                 reads=[r_kp[m], r_qp[m]], writes=[rb], same_ok=True)
                            sbanks[ix] = (bk, rb)
                        for ix in range(min(2, len(kts))):
                            issue_score(ix)
                        for ix, kt in enumerate(kts):
                            bk, rb = sbanks.pop(ix)
                            p_, rp_ = pt[pti % 4], r_pt[pti % 4]
                            pti += 1
                            S.op("act", lambda e: e.activation(out=p_[:, 0:n], in_=bk[:, 0:n], func=AF.Exp, scale=SCALE), reads=[rb], writes=[rp_])
                            if ix + 2 < len(kts):
                                issue_score(ix + 2)
                            first, last = (ix == 0), (ix == len(kts) - 1)
                            S.op("pe", lambda e: e.matmul(num[:, 0:n], lhsT=vp[:, kt, :], rhs=p_[:, 0:n], start=first, stop=last),
                                 reads=[r_vp, rp_], writes=[r_num], same_ok=True)
                            S.op("pe", lambda e: e.matmul(den[:, 0:n], lhsT=onb[:], rhs=p_[:, 0:n], start=first, stop=last),
                                 reads=[r_onb, rp_], writes=[r_den], same_ok=True)

                    def ep0(n=n, t1_=t1_, rt1_=rt1_):
                        S.op("dve", lambda e: e.reciprocal(out=rd[:, 0, 0:n], in_=dens[0][0][:, 0:n]), reads=[dens[0][1]], writes=[r_rd0])
                        S.op("dve", lambda e: e.tensor_tensor(out=t1_[:, 0:n], in0=nums[0][0][:, 0:n], in1=rd[:, 0, 0:n], op=ALU.mult),
                             reads=[nums[0][1], r_rd0], writes=[rt1_])

                    def ep1(ti=ti, t0=t0, n=n, lc=lc, t1_=t1_, rt1_=rt1_, h=h, k_=k_):
                        S.op("dve", lambda e: e.reciprocal(out=rd[:, 1, 0:n], in_=dens[1][0][:, 0:n]), reads=[dens[1][1]], writes=[r_rd1])
                        S.op("dve", lambda e: e.scalar_tensor_tensor(out=t2[:, 0:n], in0=nums[1][0][:, 0:n], scalar=lam[:, 2:3], in1=rd[:, 1, 0:n],
                                                                     op0=ALU.mult, op1=ALU.mult), reads=[nums[1][1], r_rd1, r_lam], writes=[r_t2])
                        S.op("dve", lambda e: e.tensor_tensor(out=t1_[:, 0:n], in0=t1_[:, 0:n], in1=t2[:, 0:n], op=ALU.add), reads=[rt1_, r_t2], writes=[rt1_])
                        S.op("dve", lambda e: e.tensor_tensor(out=t2[:, 0:n], in0=t1_[:, 0:n], in1=t1_[:, 0:n], op=ALU.mult), reads=[rt1_], writes=[r_t2])
                        bk, rb = self.bank()
                        S.op("pe", lambda e: e.matmul(bk[:, 0:n], lhsT=self.ones[:], rhs=t2[:, 0:n], start=True, stop=True),
                             reads=[r_t2, self.r_ident], writes=[rb], same_ok=True)
                        S.op("act", lambda e: e.activation(out=t2[:, 0:n], in_=bk[:, 0:n], func=AF.Sqrt, bias=epsc[:, 0:1], scale=1.0 / 128.0),
                             reads=[rb, r_eps], writes=[r_t2])
                        S.op("dve", lambda e: e.reciprocal(out=t2[:, 0:n], in_=t2[:, 0:n]), reads=[r_t2], writes=[r_t2])
                        o_, ro_ = on_[k_ % 2], r_on[k_ % 2]
                        S.op("dve", lambda e: e.scalar_tensor_tensor(out=o_[:, 0:n], in0=t1_[:, 0:n], scalar=sub[:, 0:1], in1=t2[:, 0:n],
                                                                     op0=ALU.mult, op1=ALU.mult), reads=[rt1_, r_t2, r_sub], writes=[ro_])
                        for oc in range(8):
                            bk, rb = self.bank()
                            S.op("pe", lambda e: e.matmul(bk[:, 0:n], lhsT=wo[h % 2][:, oc * 128:(oc + 1) * 128], rhs=o_[:, 0:n], start=True, stop=True),
                                 reads=[r_wo[h % 2], ro_], writes=[rb], same_ok=True)
                            hz = self.hT[:, oc, t0:t0 + n]
                            S.op("dve", lambda e: e.scalar_tensor_tensor(out=hz, in0=bk[:, 0:n], scalar=self.mcol(i, 2, oc, lc), in1=hz,
                                                                         op0=ALU.mult, op1=ALU.add),
                                 reads=[rb, self.r_h[ti], self.r_mod[i]], writes=[self.r_h[ti]])
                    attend(0)
                    if pend1[0] is not None:
                        pend1[0]()
                    attend(1)
                    ep0()
                    pend1[0] = ep1
            if pend1[0] is not None:
                pend1[0]()
            S.barrier()
        self.nrr = 6
        self.bank_i = 0

    def gla(self, i):
        S = self.S
        QS = 128.0 ** -0.5
        win = self.din["gla_w_in"]
        with ExitStack() as ph:
            arr = [[self.sb([128, T], BF16, es=ph) for _ in range(3)] for _ in range(2)]
            r_arr = [[Res() for _ in range(3)] for _ in range(2)]
            vh = self.sb([128, 18, 256], BF16, es=ph); r_vh = Res("vh")
            oacc = self.sb([128, 2, T], BF16, es=ph); r_oacc = [Res(f"oacc{c}") for c in range(18)]
            rT = self.sb([32, T], BF16, es=ph); r_rT = Res("rT")
            gw = self.sb([32, 2, 512], BF16, es=ph); r_gw = Res("gw")
            gb = self.sb([128, 2, 4], F32, es=ph); r_gb = Res("gb")
            ng = self.sb([128, 2], F32, es=ph); r_ng = Res("ng")
            dec = self.sb([128, 2, 18], F32, es=ph); r_dec = Res("dec")
            cmask = self.sb([128, 512], F32, es=ph); r_cm = Res("cmask")
            msk = [self.sb([128, 128], F32, es=ph) for _ in range(2)]; r_msk = Res("msk")
            bA = self.sb([128, 512], F32, es=ph); r_bA = Res("bA")
            bB = self.sb([128, 512], F32, es=ph); r_bB = Res("bB")
            bC = self.sb([128, 512], F32, es=ph); r_bC = Res("bC")
            bD = self.sb([128, 512], F32, es=ph); r_bD = Res("bD")
            Sf = [self.sb([128, 256], F32, es=ph) for _ in range(2)]; r_Sf = [Res("Sf0"), Res("Sf1")]
            Sb = [self.sb([128, 256], BF16, es=ph) for _ in range(2)]; r_Sb = [Res("Sb0"), Res("Sb1")]
            Am = [self.sb([128, 128], BF16, es=ph) for _ in range(2)]; r_Am = [Res(), Res()]
            keT = [self.sb([128, 128], BF16, es=ph) for _ in range(2)]; r_keT = [Res(), Res()]
            ogn = [self.sb([128, 2, 512], BF16, es=ph) for _ in range(1)]; r_ogn = [Res()]
            epsc = self.sb([128, 1], F32, es=ph); r_eps = Res("eps")
            S.op("pool", lambda e: e.memset(epsc[:], EPS), writes=[r_eps])
            S.op("pool", lambda e: e.memset(cmask[:], 1.0), writes=[r_cm])
            for c in range(4):
                S.op("pool", lambda e: e.memset(cmask[:, c * 128:c * 128 + 1], 0.0), reads=[r_cm], writes=[r_cm])
            for d in range(2):
                S.op("pool", lambda e: e.memset(msk[d][:], 1.0), reads=[r_msk], writes=[r_msk])
                cm, pat = ((-1, [[1, 128]]) if d == 0 else (1, [[-1, 128]]))
                S.op("pool", lambda e: e.affine_select(out=msk[d][:], in_=msk[d][:], compare_op=ALU.is_ge, fill=0.0, base=0,
                                                      pattern=pat, channel_multiplier=cm), reads=[r_msk], writes=[r_msk])
            S.op("pool", lambda e: e.memset(gw[:], 0.0), writes=[r_gw])
            S.dma("pool", [(gw[0:16, 0, :], self.din["gla_gw"][0]), (gw[16:32, 1, :], self.din["gla_gw"][1])], r_gw, reads=[r_gw], writes=[r_gw])
            for d in range(2):
                self.load_cols(self.din["gla_gb"][d], 4, gb[:, d, :], r_gb)
            S.op("dve", lambda e: e.tensor_scalar(out=gb[:], in0=gb[:], scalar1=-1.0, scalar2=None, op0=ALU.mult), reads=[r_gb], writes=[r_gb])
            self.load_cols(self.din["gla_norm"], 2, ng[:, 0:2], r_ng)
            w_, r_w = self.wnext()
            wr = w_[:, 0:256].rearrange("p (k n) -> p k n", k=8)
            S.dma("pool", [(wr, win[:, 3072:3104].rearrange("(k p) n -> p k n", p=128))], r_w, writes=[r_w])
            for ti, (t0, n, lc) in enumerate(TT):
                bk, rb = self.bank()
                for kc in range(8):
                    S.op("pe", lambda e: e.matmul(bk[0:32, 0:n], lhsT=wr[:, kc, :], rhs=self.uT[:, kc, t0:t0 + n], start=(kc == 0), stop=(kc == 7)),
                         reads=[r_w, self.r_u[ti]], writes=[rb], same_ok=True)
                S.op("act", lambda e: e.copy(rT[:, t0:t0 + n], bk[0:32, 0:n]), reads=[rb], writes=[r_rT])

            for h in range(4):
                wA_, r_wA = self.wnext()
                wqk = wA_[:, 0:2048].rearrange("p (k n) -> p k n", k=8)
                S.dma("pool", [(wqk[:, :, 0:128], win[:, h * 128:(h + 1) * 128].rearrange("(k p) n -> p k n", p=128)),
                               (wqk[:, :, 128:256], win[:, 512 + h * 128:512 + (h + 1) * 128].rearrange("(k p) n -> p k n", p=128))],
                      r_wA, writes=[r_wA])
                wB_, r_wB = self.wnext()
                wv = wB_[:, 0:2048].rearrange("p (k n) -> p k n", k=8)
                wg = wB_[:, 2048:4096].rearrange("p (k n) -> p k n", k=8)
                S.dma("pool", [(wv, win[:, 1024 + h * 256:1024 + (h + 1) * 256].rearrange("(k p) n -> p k n", p=128)),
                               (wg, win[:, 2048 + h * 256:2048 + (h + 1) * 256].rearrange("(k p) n -> p k n", p=128))],
                      r_wB, writes=[r_wB])
                wC_, r_wC = self.wnext()
                wo = wC_[:, 0:2048].rearrange("p (k n) -> p k n", k=2)
                S.dma("pool", [(wo, self.din["gla_w_out"][h * 256:(h + 1) * 256, :].rearrange("(k p) n -> p k n", p=128))], r_wC, writes=[r_wC])
                for g0 in range(0, 18, 2):
                    bk, rb = self.bank()
                    for jt in range(2):
                        kt = g0 + jt
                        for kc in range(8):
                            S.op("pe", lambda e: e.matmul(bk[:, jt * 256:(jt + 1) * 256], lhsT=self.uT[:, kc, kt * 128:(kt + 1) * 128], rhs=wv[:, kc, :],
                                                          start=(kc == 0), stop=(kc == 7)), reads=[r_wB] + self.r_u, writes=[rb], same_ok=True)
                    S.op("act", lambda e: e.copy(vh[:, g0:g0 + 2, :], bk[:, 0:512].rearrange("p (j d) -> p j d", d=256)), reads=[rb], writes=[r_vh])
                for ti, (t0, n, lc) in enumerate(TT):
                    ru = self.r_u[ti]
                    nch = n // 128
                    bq, rbq = self.bank()
                    for kc in range(8):
                        S.op("pe", lambda e: e.matmul(bq[:, 0:n], lhsT=wqk[:, kc, 0:128], rhs=self.uT[:, kc, t0:t0 + n], start=(kc == 0), stop=(kc == 7)),
                             reads=[r_wA, ru], writes=[rbq], same_ok=True)
                    bkk, rbk = self.bank()
                    for kc in range(8):
                        S.op("pe", lambda e: e.matmul(bkk[:, 0:n], lhsT=wqk[:, kc, 128:256], rhs=self.uT[:, kc, t0:t0 + n], start=(kc == 0), stop=(kc == 7)),
                             reads=[r_wA, ru], writes=[rbk], same_ok=True)
                    for d in range(2):
                        bx, rbx = self.bank()
                        S.op("pe", lambda e: e.matmul(bx[:, 0:n], lhsT=gw[:, d, h * 128:(h + 1) * 128], rhs=rT[:, t0:t0 + n], start=True, stop=True),
                             reads=[r_gw, r_rT], writes=[rbx], same_ok=True)
                        S.op("act", lambda e: e.activation(out=bA[:, 0:n], in_=bx[:, 0:n], func=AF.Exp, bias=gb[:, d, h:h + 1], scale=-1.0),
                             reads=[rbx, r_gb], writes=[r_bA])
                        S.op("act", lambda e: e.activation(out=bA[:, 0:n], in_=bA[:, 0:n], func=AF.Ln, bias=1.0, scale=1.0), reads=[r_bA], writes=[r_bA])
                        S.op("dve", lambda e: e.tensor_tensor_scan(out=bB[:, 0:n], data0=cmask[:, 0:n], data1=bA[:, 0:n], initial=0.0,
                                                                   op0=ALU.mult, op1=ALU.add), reads=[r_cm, r_bA], writes=[r_bB])
                        for c in range(nch):
                            gc = t0 // 128 + c
                            ce = c * 128 + 127
                            S.op("act", lambda e: e.activation(out=dec[:, d, gc:gc + 1], in_=bB[:, ce:ce + 1], func=AF.Exp, scale=-1.0 / 16), reads=[r_bB], writes=[r_dec])
                            S.op("dve", lambda e: e.tensor_scalar(out=bD[:, c * 128:(c + 1) * 128], in0=bB[:, c * 128:(c + 1) * 128], scalar1=bB[:, ce:ce + 1],
                                                                  scalar2=None, op0=ALU.subtract), reads=[r_bB], writes=[r_bD])
                        if d == 0:
                            S.op("act", lambda e: e.activation(out=bC[:, 0:n], in_=bB[:, 0:n], func=AF.Exp, scale=-1.0 / 16), reads=[r_bB], writes=[r_bC])
                            S.op("dve", lambda e: e.scalar_tensor_tensor(out=arr[d][0][:, t0:t0 + n], in0=bq[:, 0:n], scalar=QS, in1=bC[:, 0:n], op0=ALU.mult, op1=ALU.mult),
                                 reads=[rbq, r_bC], writes=[r_arr[d][0]])
                            S.op("act", lambda e: e.activation(out=bC[:, 0:n], in_=bB[:, 0:n], func=AF.Exp, scale=1.0 / 16), reads=[r_bB], writes=[r_bC])
                            S.op("dve", lambda e: e.tensor_tensor(out=arr[d][1][:, t0:t0 + n], in0=bkk[:, 0:n], in1=bC[:, 0:n], op=ALU.mult),
                                 reads=[rbk, r_bC], writes=[r_arr[d][1]])
                            S.op("act", lambda e: e.activation(out=bC[:, 0:n], in_=bD[:, 0:n], func=AF.Exp, scale=1.0 / 16), reads=[r_bD], writes=[r_bC])
                            S.op("dve", lambda e: e.tensor_tensor(out=arr[d][2][:, t0:t0 + n], in0=bkk[:, 0:n], in1=bC[:, 0:n], op=ALU.mult),
                                 reads=[rbk, r_bC], writes=[r_arr[d][2]])
                        else:
                            S.op("dve", lambda e: e.tensor_tensor(out=bD[:, 0:n], in0=bA[:, 0:n], in1=bD[:, 0:n], op=ALU.subtract), reads=[r_bA, r_bD], writes=[r_bD])
                            S.op("act", lambda e: e.activation(out=bC[:, 0:n], in_=bD[:, 0:n], func=AF.Exp, scale=-1.0 / 16), reads=[r_bD], writes=[r_bC])
                            S.op("dve", lambda e: e.scalar_tensor_tensor(out=arr[d][0][:, t0:t0 + n], in0=bq[:, 0:n], scalar=QS, in1=bC[:, 0:n], op0=ALU.mult, op1=ALU.mult),
                                 reads=[rbq, r_bC], writes=[r_arr[d][0]])
                            S.op("act", lambda e: e.activation(out=bC[:, 0:n], in_=bD[:, 0:n], func=AF.Exp, scale=1.0 / 16), reads=[r_bD], writes=[r_bC])
                            S.op("dve", lambda e: e.tensor_tensor(out=arr[d][1][:, t0:t0 + n], in0=bkk[:, 0:n], in1=bC[:, 0:n], op=ALU.mult),
                                 reads=[rbk, r_bC], writes=[r_arr[d][1]])
                            S.op("dve", lambda e: e.tensor_tensor(out=bD[:, 0:n], in0=bA[:, 0:n], in1=bB[:, 0:n], op=ALU.subtract), reads=[r_bA, r_bB, r_bC], writes=[r_bD])
                            S.op("act", lambda e: e.activation(out=bC[:, 0:n], in_=bD[:, 0:n], func=AF.Exp, scale=1.0 / 16), reads=[r_bD], writes=[r_bC])
                            S.op("dve", lambda e: e.tensor_tensor(out=arr[d][2][:, t0:t0 + n], in0=bkk[:, 0:n], in1=bC[:, 0:n], op=ALU.mult),
                                 reads=[rbk, r_bC], writes=[r_arr[d][2]])
                for d in range(2):
                    S.op("pool", lambda e: e.memset(Sf[d][:], 0.0), reads=[r_Sf[d]], writes=[r_Sf[d]])
                    S.op("pool", lambda e: e.memset(Sb[d][:], 0.0), reads=[r_Sb[d]], writes=[r_Sb[d]])
                order = [[16, 17] + list(range(16)), [17, 16] + list(range(15, -1, -1))]
                written = set()
                for step in range(18):
                    for d in range(2):
                        c = order[d][step]
                        cs = slice(c * 128, (c + 1) * 128)
                        qd, ki, ke = arr[d]
                        ba, rba = self.bank()
                        S.op("pe", lambda e: e.matmul(ba[:, 0:128], lhsT=ki[:, cs], rhs=qd[:, cs], start=True, stop=True),
                             reads=[r_arr[d][1], r_arr[d][0]], writes=[rba], same_ok=True)
                        S.op("pe", lambda e: e.matmul(ba[:, 128:256], lhsT=ke[:, cs], rhs=self.identb[:], start=True, stop=True),
                             reads=[r_arr[d][2], self.r_ident], writes=[rba], same_ok=True)
                        S.op("dve", lambda e: e.tensor_tensor(out=Am[d][:], in0=ba[:, 0:128], in1=msk[d][:], op=ALU.mult), reads=[rba, r_msk], writes=[r_Am[d]])
                        S.op("act", lambda e: e.copy(keT[d][:], ba[:, 128:256]), reads=[rba], writes=[r_keT[d]])
                        bo, rbo = self.bank()
                        for j in range(2):
                            S.op("pe", lambda e: e.matmul(bo[:, j * 128:(j + 1) * 128], lhsT=Sb[d][:, j * 128:(j + 1) * 128], rhs=qd[:, cs], start=True, stop=False),
                                 reads=[r_Sb[d], r_arr[d][0]], writes=[rbo], same_ok=True)
                            S.op("pe", lambda e: e.matmul(bo[:, j * 128:(j + 1) * 128], lhsT=vh[:, c, j * 128:(j + 1) * 128], rhs=Am[d][:], start=False, stop=True),
                                 reads=[r_vh, r_Am[d]], writes=[rbo], same_ok=True)
                        ov = oacc[:, :, cs]
                        pv = bo[:, 0:256].rearrange("p (j c) -> p j c", j=2)
                        if c not in written:
                            written.add(c)
                            S.op("act", lambda e: e.copy(ov, pv), reads=[rbo], writes=[r_oacc[c]])
                        else:
                            S.op("dve", lambda e: e.tensor_tensor(out=ov, in0=pv, in1=ov, op=ALU.add), reads=[rbo, r_oacc[c]], writes=[r_oacc[c]])
                        bs, rbs = self.bank()
                        S.op("pe", lambda e: e.matmul(bs[:, 0:256], lhsT=keT[d][:], rhs=vh[:, c, :], start=True, stop=True),
                             reads=[r_keT[d], r_vh], writes=[rbs], same_ok=True)
                        S.op("dve", lambda e: e.scalar_tensor_tensor(out=Sf[d][:], in0=Sf[d][:], scalar=dec[:, d, c:c + 1], in1=bs[:, 0:256], op0=ALU.mult, op1=ALU.add),
                             reads=[r_Sf[d], r_dec, rbs], writes=[r_Sf[d]])
                        S.op("act", lambda e: e.copy(Sb[d][:], Sf[d][:]), reads=[r_Sf[d]], writes=[r_Sb[d]])
                for ti, (t0, n, lc) in enumerate(TT):
                    ru = self.r_u[ti]
                    roa = r_oacc[t0 // 128:(t0 + n) // 128]
                    bss = self.banks[6]; rbss = self.bank_res[6]
                    for j in range(2):
                        S.op("act", lambda e: e.activation(out=bA[:, 0:n], in_=oacc[:, j, t0:t0 + n], func=AF.Square), reads=roa + [r_bA], writes=[r_bA])
                        S.op("pe", lambda e: e.matmul(bss[:, 0:n], lhsT=self.ones[:], rhs=bA[:, 0:n], start=(j == 0), stop=(j == 1)),
                             reads=[r_bA, self.r_ident], writes=[rbss], same_ok=True)
                    S.op("act", lambda e: e.activation(out=bB[:, 0:n], in_=bss[:, 0:n], func=AF.Sqrt, bias=epsc[:, 0:1], scale=1.0 / 256.0), reads=[rbss, r_eps], writes=[r_bB])
                    S.op("dve", lambda e: e.reciprocal(out=bB[:, 0:n], in_=bB[:, 0:n]), reads=[r_bB], writes=[r_bB])
                    og, rog = ogn[0], r_ogn[0]
                    for j in range(2):
                        bg, rbg = self.bank()
                        for kc in range(8):
                            S.op("pe", lambda e: e.matmul(bg[:, 0:n], lhsT=wg[:, kc, j * 128:(j + 1) * 128], rhs=self.uT[:, kc, t0:t0 + n], start=(kc == 0), stop=(kc == 7)),
                                 reads=[r_wB, ru], writes=[rbg], same_ok=True)
                        S.op("act", lambda e: e.activation(out=bC[:, 0:n], in_=bg[:, 0:n], func=AF.Silu), reads=[rbg], writes=[r_bC])
                        S.op("dve", lambda e: e.scalar_tensor_tensor(out=bD[:, 0:n], in0=oacc[:, j, t0:t0 + n], scalar=ng[:, j:j + 1], in1=bB[:, 0:n], op0=ALU.mult, op1=ALU.mult),
                             reads=roa + [r_ng, r_bB], writes=[r_bD])
                        S.op("dve", lambda e: e.tensor_tensor(out=og[:, j, 0:n], in0=bD[:, 0:n], in1=bC[:, 0:n], op=ALU.mult), reads=[r_bD, r_bC], writes=[rog])
                    for oc in range(8):
                        bk, rb = self.bank()
                        for j in range(2):
                            S.op("pe", lambda e: e.matmul(bk[:, 0:n], lhsT=wo[:, j, oc * 128:(oc + 1) * 128], rhs=og[:, j, 0:n], start=(j == 0), stop=(j == 1)),
                                 reads=[r_wC, rog], writes=[rb], same_ok=True)
                        hz = self.hT[:, oc, t0:t0 + n]
                        S.op("dve", lambda e: e.scalar_tensor_tensor(out=hz, in0=bk[:, 0:n], scalar=self.mcol(i, 2, oc, lc), in1=hz, op0=ALU.mult, op1=ALU.add),
                             reads=[rb, self.r_h[ti], self.r_mod[i]], writes=[self.r_h[ti]])
            S.barrier()

    def gdn(self, i):
        S = self.S
        R32 = mybir.dt.float32r
        win = self.din["gdn_w_in"]
        rr = lambda ap: ap.bitcast(R32)
        with ExitStack() as ph:
            sbp = lambda shape, dt: self.sb(shape, dt, es=ph)
            cw = sbp([128, 5, 32], F32); r_cw = Res("cw")
            for j in range(5):
                self.load_cols(self.din["gdn_conv"][j], 32, cw[:, j, :], r_cw)
            r_msk = Res("gmsk")
            inclT = [sbp([128, 128], F32) for _ in range(2)]
            strict2 = sbp([128, 2, 128], BF16)
            inclT2 = sbp([128, 2, 128], BF16)
            bd16 = sbp([128, 128], BF16); off16 = sbp([128, 128], BF16); off32 = sbp([128, 128], BF16); off64 = sbp([128, 128], BF16)
            with ExitStack() as mk:
                strict = [self.sb([128, 128], F32, es=mk) for _ in range(2)]
                specs = [(strict[0], ALU.is_gt, 1, [[-1, 128]]), (strict[1], ALU.is_gt, -1, [[1, 128]]),
                         (inclT[0], ALU.is_ge, -1, [[1, 128]]), (inclT[1], ALU.is_ge, 1, [[-1, 128]])]
                for (t_, cmp_, cm, pat) in specs:
                    S.op("pool", lambda e: e.memset(t_[:], 1.0), reads=[r_msk], writes=[r_msk])
                    S.op("pool", lambda e: e.affine_select(out=t_[:], in_=t_[:], compare_op=cmp_, fill=0.0, base=0, pattern=pat, channel_multiplier=cm),
                         reads=[r_msk], writes=[r_msk])
                for d in range(2):
                    S.op("dve", lambda e: e.tensor_copy(strict2[:, d, :], strict[d][:]), reads=[r_msk], writes=[r_msk])
                    S.op("dve", lambda e: e.tensor_copy(inclT2[:, d, :], inclT[d][:]), reads=[r_msk], writes=[r_msk])
                bsel = self.sb([8, 128], F32, es=mk)
                bd = {}
                for b_ in (16, 32, 64):
                    nb = 128 // b_
                    S.op("pool", lambda e: e.memset(bsel[:], 1.0), reads=[r_msk], writes=[r_msk])
                    S.op("pool", lambda e: e.affine_select(out=bsel[:], in_=bsel[:], compare_op=ALU.is_ge, fill=0.0, base=0, pattern=[[1, 128]], channel_multiplier=-b_),
                         reads=[r_msk], writes=[r_msk])
                    S.op("pool", lambda e: e.affine_select(out=bsel[:], in_=bsel[:], compare_op=ALU.is_ge, fill=0.0, base=b_ - 1, pattern=[[-1, 128]], channel_multiplier=b_),
                         reads=[r_msk], writes=[r_msk])
                    bk, rb = self.bank()
                    S.op("pe", lambda e: e.matmul(bk[:, 0:128], lhsT=bsel[0:nb, :], rhs=bsel[0:nb, :], start=True, stop=True), reads=[r_msk], writes=[rb], same_ok=True)
                    bd[b_] = self.sb([128, 128], F32, es=mk)
                    S.op("dve", lambda e: e.tensor_copy(bd[b_][:], bk[:, 0:128]), reads=[rb, r_msk], writes=[r_msk])
                S.op("dve", lambda e: e.tensor_copy(bd16[:], bd[16][:]), reads=[r_msk], writes=[r_msk])
                S.op("dve", lambda e: e.tensor_tensor(out=off16[:], in0=bd[32][:], in1=bd[16][:], op=ALU.subtract), reads=[r_msk], writes=[r_msk])
                S.op("dve", lambda e: e.tensor_tensor(out=off32[:], in0=bd[64][:], in1=bd[32][:], op=ALU.subtract), reads=[r_msk], writes=[r_msk])
                S.op("dve", lambda e: e.tensor_scalar(out=off64[:], in0=bd[64][:], scalar1=-1.0, scalar2=1.0, op0=ALU.mult, op1=ALU.add), reads=[r_msk], writes=[r_msk])
                S.barrier()
            b4 = lambda m_: m_[:].unsqueeze(1).broadcast_to([128, 4, 128])
            hc = sbp([128, 64], F32); r_hc = Res("hc")
            S.dma("sp", [(hc[:], self.din["gdn_hc"].partition_broadcast(128))], r_hc, writes=[r_hc])
            S.op("act", lambda e: e.activation(out=hc[:, 0:32], in_=hc[:, 0:32], func=AF.Exp), reads=[r_hc], writes=[r_hc])
            S.op("dve", lambda e: e.tensor_scalar(out=hc[:, 0:32], in0=hc[:, 0:32], scalar1=-1.0, scalar2=None, op0=ALU.mult), reads=[r_hc], writes=[r_hc])
            ngrep = sbp([128, 128], F32); r_ngr = Res("ngrep")
            S.dma("sp", [(ngrep[:], self.din["gdn_norm"].partition_broadcast(128))], r_ngr, writes=[r_ngr])
            epsc = sbp([128, 1], F32); r_eps = Res("eps")
            S.op("pool", lambda e: e.memset(epsc[:], EPS), writes=[r_eps])
            qkT = sbp([128, 2, T], BF16); r_qkT = Res("qkT")
            kn = sbp([128, 18, 128], BF16); r_kn = Res("kn")
            vt = sbp([128, 18, 256], BF16); r_vt = Res("vt")
            oacc = sbp([128, 18, 256], BF16); r_oacc = [Res(f"go{c}") for c in range(18)]
            sc_names = ("negb", "gc", "e", "ecoef", "negbe", "dl", "g")
            sc = {n_: sbp([128, 18, 4], F32) for n_ in sc_names}
            r_sc = Res("gsc")

            for kh in range(8):
                wA_, r_wA = self.wnext()
                wA = wA_[:].rearrange("p (k n) -> p k n", k=8)
                S.dma("pool", [(wA[:, :, 0:128], win[:, kh * 128:(kh + 1) * 128].rearrange("(k p) n -> p k n", p=128)),
                               (wA[:, :, 128:256], win[:, 1024 + kh * 128:1024 + (kh + 1) * 128].rearrange("(k p) n -> p k n", p=128)),
                               (wA[:, :, 256:512], win[:, 2048 + kh * 256:2048 + (kh + 1) * 256].rearrange("(k p) n -> p k n", p=128))],
                      r_wA, writes=[r_wA])
                wB_, r_wB = self.wnext()
                wz = wB_[:, 0:2048].rearrange("p (k n) -> p k n", k=8)
                wgt = wB_[:, 2048:2112].rearrange("p (k n) -> p k n", k=8)
                S.dma("pool", [(wz, win[:, 4096 + kh * 256:4096 + (kh + 1) * 256].rearrange("(k p) n -> p k n", p=128)),
                               (wgt, self.din["gdn_wg"][:, kh * 8:(kh + 1) * 8].rearrange("(k p) n -> p k n", p=128))], r_wB, writes=[r_wB])
                wC_, r_wC = self.wnext()
                wo = wC_[:, 0:2048].rearrange("p (k n) -> p k n", k=2)
                S.dma("pool", [(wo, self.din["gdn_w_out"][kh * 256:(kh + 1) * 256, :].rearrange("(k p) n -> p k n", p=128))], r_wC, writes=[r_wC])
                with ExitStack() as p1:
                    xpad = self.sb([128, 4, 2312], BF16, es=p1); r_xp = Res("xpad")
                    dg = self.sb([128, 4, 5, 128], BF16, es=p1); r_dg = Res("dg")
                    cvs = [self.sb([128, 512], F32, es=p1) for _ in range(3)]; r_cvs = [Res() for _ in range(3)]
                    junk = self.sb([128, 128], F32, es=p1); r_junk = Res("junk")
                    sss = [self.sb([128, 2], F32, es=p1) for _ in range(3)]; r_sss = [Res() for _ in range(3)]
                    qns = [self.sb([128, 128], BF16, es=p1) for _ in range(3)]; r_qns = [Res() for _ in range(3)]
                    S.op("pool", lambda e: e.memset(xpad[:], 0.0), writes=[r_xp])
                    gch = [kh, 8 + kh, 16 + 2 * kh, 17 + 2 * kh]
                    for ch in range(4):
                        for j in range(5):
                            S.op("pool", lambda e: e.tensor_scalar(out=dg[:, ch, j, :], in0=self.identb[:], scalar1=cw[:, j, gch[ch]:gch[ch] + 1], scalar2=1.0, op0=ALU.mult, op1=ALU.mult),
                                 reads=[r_cw, self.r_ident], writes=[r_dg])
                    for ch in range(4):
                        for ti, (t0, n, lc) in enumerate(TT):
                            bk, rb = self.bank()
                            for kc in range(8):
                                S.op("pe", lambda e: e.matmul(bk[:, 0:n], lhsT=wA[:, kc, ch * 128:(ch + 1) * 128], rhs=self.uT[:, kc, t0:t0 + n], start=(kc == 0), stop=(kc == 7)),
                                     reads=[r_wA, self.r_u[ti]], writes=[rb], same_ok=True)
                            c0 = t0 + 2 if lc == 0 else 2054
                            if (ch + ti) % 2 == 0:
                                S.op("act", lambda e: e.copy(xpad[:, ch, c0:c0 + n], bk[:, 0:n]), reads=[rb], writes=[r_xp])
                            else:
                                S.op("dve", lambda e: e.tensor_copy(xpad[:, ch, c0:c0 + n], bk[:, 0:n]), reads=[rb], writes=[r_xp])
                    def p1A(t):
                        b0 = t * 128 + 2 if t < 16 else 2054 + (t - 16) * 128
                        cv, r_cv = cvs[t % 3], r_cvs[t % 3]
                        ss, r_ss = sss[t % 3], r_sss[t % 3]
                        bk, rb = self.bank()
                        for ch in range(4):
                            for j in range(5):
                                S.op("pe", lambda e: e.matmul(bk[:, ch * 128:(ch + 1) * 128], lhsT=xpad[:, ch, b0 + j - 2:b0 + j - 2 + 128], rhs=dg[:, ch, j, :],
                                                              start=(j == 0), stop=(j == 4)), reads=[r_xp, r_dg], writes=[rb], same_ok=True)
                        S.op("act", lambda e: e.activation(out=cv[:], in_=bk[:, 0:512], func=AF.Silu), reads=[rb], writes=[r_cv])
                        for q_ in range(2):
                            S.op("act", lambda e: e.activation(out=junk[:], in_=cv[:, q_ * 128:(q_ + 1) * 128], func=AF.Square, accum_out=ss[:, q_:q_ + 1]),
                                 reads=[r_cv], writes=[r_ss])
                        S.op("act", lambda e: e.activation(out=ss[:], in_=ss[:], func=AF.Sqrt, bias=epsc[:, 0:1], scale=1.0), reads=[r_ss, r_eps], writes=[r_ss])

                    def p1B(t):
                        cv, r_cv = cvs[t % 3], r_cvs[t % 3]
                        ss, r_ss = sss[t % 3], r_sss[t % 3]
                        qn, r_qn = qns[t % 3], r_qns[t % 3]
                        S.op("dve", lambda e: e.reciprocal(out=ss[:], in_=ss[:]), reads=[r_ss], writes=[r_ss])
                        S.op("dve", lambda e: e.tensor_scalar(out=qn[:], in0=cv[:, 0:128], scalar1=ss[:, 0:1], scalar2=128.0 ** -0.5, op0=ALU.mult, op1=ALU.mult),
                             reads=[r_cv, r_ss], writes=[r_qn])
                        S.op("dve", lambda e: e.tensor_scalar(out=kn[:, t, :], in0=cv[:, 128:256], scalar1=ss[:, 1:2], scalar2=None, op0=ALU.mult),
                             reads=[r_cv, r_ss], writes=[r_kn])
                        S.op("pool", lambda e: e.tensor_copy(vt[:, t, :], cv[:, 256:512]), reads=[r_cv], writes=[r_vt])
                        b2, rb2 = self.bank()
                        S.op("pe", lambda e: e.matmul(b2[:, 0:128], lhsT=qn[:], rhs=self.identb[:], start=True, stop=True), reads=[r_qn, self.r_ident], writes=[rb2], same_ok=True)
                        S.op("pe", lambda e: e.matmul(b2[:, 128:256], lhsT=kn[:, t, :], rhs=self.identb[:], start=True, stop=True), reads=[r_kn, self.r_ident], writes=[rb2], same_ok=True)
                        S.op("act", lambda e: e.copy(qkT[:, :, t * 128:(t + 1) * 128], b2[:, 0:256].rearrange("p (a c) -> p a c", a=2)), reads=[rb2], writes=[r_qkT])
                    p1A(0)
                    for t in range(18):
                        if t + 1 < 18:
                            p1A(t + 1)
                        p1B(t)
                    S.barrier()
                bk, rb = self.bank()
                for t in range(18):
                    for kc in range(8):
                        S.op("pe", lambda e: e.matmul(bk[:, t * 8:(t + 1) * 8], lhsT=self.uT[:, kc, t * 128:(t + 1) * 128], rhs=wgt[:, kc, :], start=(kc == 0), stop=(kc == 7)),
                             reads=[r_wB] + self.r_u, writes=[rb], same_ok=True)
                graw = bk[:, 0:144].rearrange("p (t c) -> p t c", c=8)
                S.op("act", lambda e: e.activation(out=sc["negb"][:], in_=graw[:, :, 0:4], func=AF.Sigmoid), reads=[rb], writes=[r_sc])
                S.op("dve", lambda e: e.tensor_scalar(out=sc["negb"][:], in0=sc["negb"][:], scalar1=-1.0, scalar2=None, op0=ALU.mult), reads=[r_sc], writes=[r_sc])
                for m in range(4):
                    d_, j_ = m // 2, m % 2
                    hidx = d_ * 16 + 2 * kh + j_
                    S.op("act", lambda e: e.activation(out=sc["g"][:, :, m], in_=graw[:, :, 4 + m], func=AF.Exp, bias=hc[:, 32 + hidx:33 + hidx], scale=1.0),
                         reads=[rb, r_hc], writes=[r_sc])
                S.op("act", lambda e: e.activation(out=sc["g"][:], in_=sc["g"][:], func=AF.Ln, bias=1.0, scale=1.0), reads=[r_sc], writes=[r_sc])
                for m in range(4):
                    d_, j_ = m // 2, m % 2
                    hidx = d_ * 16 + 2 * kh + j_
                    S.op("dve", lambda e: e.tensor_scalar(out=sc["g"][:, :, m], in0=sc["g"][:, :, m], scalar1=hc[:, hidx:hidx + 1], scalar2=None, op0=ALU.mult),
                         reads=[r_sc, r_hc], writes=[r_sc])
                bk, rb = self.bank()
                gv = sc["g"][:]
                S.op("pe", lambda e: e.matmul(bk[:, 0:72].rearrange("p (t c) -> p t c", c=4)[:, :, 0:2], lhsT=inclT[0][:], rhs=gv[:, :, 0:2], start=True, stop=True),
                     reads=[r_sc, r_msk], writes=[rb], same_ok=True)
                S.op("pe", lambda e: e.matmul(bk[:, 0:72].rearrange("p (t c) -> p t c", c=4)[:, :, 2:4], lhsT=inclT[1][:], rhs=gv[:, :, 2:4], start=True, stop=True),
                     reads=[r_sc, r_msk], writes=[rb], same_ok=True)
                S.op("pe", lambda e: e.matmul(bk[:, 128:200], lhsT=self.ones[:], rhs=gv.rearrange("p t c -> p (t c)"), start=True, stop=True),
                     reads=[r_sc, self.r_ident], writes=[rb], same_ok=True)
                gcp = bk[:, 0:72].rearrange("p (t c) -> p t c", c=4)
                glp = bk[:, 128:200].rearrange("p (t c) -> p t c", c=4)
                S.op("dve", lambda e: e.tensor_copy(sc["gc"][:], gcp), reads=[rb], writes=[r_sc])
                S.op("act", lambda e: e.activation(out=sc["e"][:], in_=gcp, func=AF.Exp), reads=[rb], writes=[r_sc])
                S.op("act", lambda e: e.activation(out=sc["dl"][:], in_=glp, func=AF.Exp), reads=[rb], writes=[r_sc])
                S.op("dve", lambda e: e.tensor_tensor(out=sc["ecoef"][:], in0=glp, in1=sc["gc"][:], op=ALU.subtract), reads=[rb, r_sc], writes=[r_sc])
                S.op("act", lambda e: e.activation(out=sc["ecoef"][:], in_=sc["ecoef"][:], func=AF.Exp), reads=[r_sc], writes=[r_sc])
                S.op("dve", lambda e: e.tensor_tensor(out=sc["negbe"][:], in0=sc["negb"][:], in1=sc["e"][:], op=ALU.mult), reads=[r_sc], writes=[r_sc])
                with ExitStack() as p3:
                    h4 = lambda: self.sb([128, 4, 128], BF16, es=p3)
                    f4 = lambda: self.sb([128, 4, 128], F32, es=p3)
                    ST = []
                    for st_ in range(2):
                        B = dict(X=[h4(), h4()], Y=[h4(), h4()], W=h4(), Tm=h4(), Y0=h4(), AT=h4(),
                                 Gm=self.sb([128, 2, 128], BF16, es=p3), QKm=self.sb([128, 2, 128], BF16, es=p3), scr=f4(), rscr=Res(),
                                 rX=[Res(), Res()], rY=[Res(), Res()], rW=Res(), rTm=Res(), rY0=Res(), rAT=Res(), rGm=Res(), rQKm=Res())
                        ST.append(B)
                    Rm = h4(); r_Rm = Res("Rm")
                    vn = h4(); r_vn = Res("vn")
                    kdec = h4(); r_kdec = Res("kdec")
                    bv = h4(); r_bv = Res("bv")
                    Sf = f4(); r_Sf = Res("Sf")
                    Sb = h4(); r_Sb = Res("Sb")
                    ot = h4(); r_ot = Res("ot")
                    S.op("pool", lambda e: e.memset(Sf[:], 0.0), writes=[r_Sf])
                    S.op("pool", lambda e: e.memset(Sb[:], 0.0), writes=[r_Sb])
                    order = [[16, 17] + list(range(16)), [17, 16] + list(range(15, -1, -1))]
                    written = set()
                    kT = lambda c: qkT[:, 1, c * 128:(c + 1) * 128]
                    qT = lambda c: qkT[:, 0, c * 128:(c + 1) * 128]
                    pv4 = lambda b_: b_[:, 0:512].rearrange("p (m c) -> p m c", m=4)

                    def pre(step, B):
                        X, Y, W, Tm, Y0, AT, Gm, QKm = B["X"], B["Y"], B["W"], B["Tm"], B["Y0"], B["AT"], B["Gm"], B["QKm"]
                        rX, rY, rW, rTm, rY0, rAT, rGm, rQKm = B["rX"], B["rY"], B["rW"], B["rTm"], B["rY0"], B["rAT"], B["rGm"], B["rQKm"]
                        cc = [order[0][step], order[1][step]]
                        cm_ = [cc[m // 2] for m in range(4)]
                        scrA = scrB = B["scr"]
                        r_scrA = r_scrB = B["rscr"]
                        bk, rb = self.bank()
                        for d in range(2):
                            S.op("pe", lambda e: e.matmul(bk[:, d * 128:(d + 1) * 128], lhsT=kT(cc[d]), rhs=kT(cc[d]), start=True, stop=True), reads=[r_qkT], writes=[rb], same_ok=True)
                            S.op("pe", lambda e: e.matmul(bk[:, 256 + d * 128:256 + (d + 1) * 128], lhsT=kT(cc[d]), rhs=qT(cc[d]), start=True, stop=True), reads=[r_qkT], writes=[rb], same_ok=True)
                        S.op("dve", lambda e: e.tensor_tensor(out=Gm[:], in0=bk[:, 0:256].rearrange("p (d c) -> p d c", d=2), in1=strict2[:], op=ALU.mult), reads=[rb, r_msk], writes=[rGm])
                        S.op("dve", lambda e: e.tensor_tensor(out=QKm[:], in0=bk[:, 256:512].rearrange("p (d c) -> p d c", d=2), in1=inclT2[:], op=ALU.mult), reads=[rb, r_msk], writes=[rQKm])
                        for m in range(4):
                            S.op("pool", lambda e: e.tensor_scalar(out=scrA[:, m, :], in0=self.ident[:], scalar1=sc["gc"][:, cm_[m], m:m + 1], scalar2=1.0, op0=ALU.mult, op1=ALU.mult),
                                 reads=[r_sc, self.r_ident], writes=[r_scrA])
                        yield
                        bb, rbb = self.bank()
                        for m in range(4):
                            S.op("pe", lambda e: e.matmul(bb[:, m * 128:(m + 1) * 128], lhsT=self.ones[:], rhs=scrA[:, m, :], start=True, stop=True),
                                 reads=[r_scrA, self.r_ident], writes=[rbb], same_ok=True)
                        for m in range(4):
                            S.op("dve", lambda e: e.tensor_scalar(out=scrA[:, m, :], in0=bb[:, m * 128:(m + 1) * 128], scalar1=sc["gc"][:, cm_[m], m:m + 1], scalar2=0.0,
                                                                  op0=ALU.subtract, op1=ALU.max), reads=[rbb, r_sc], writes=[r_scrA])
                        S.op("act", lambda e: e.activation(out=X[1][:], in_=scrA[:], func=AF.Exp, scale=-1.0), reads=[r_scrA], writes=[rX[1]])
                        for m in range(4):
                            S.op("dve", lambda e: e.tensor_scalar(out=scrB[:, m, :], in0=bb[:, m * 128:(m + 1) * 128], scalar1=sc["gc"][:, cm_[m], m:m + 1], scalar2=0.0,
                                                                  op0=ALU.subtract, op1=ALU.min), reads=[rbb, r_sc], writes=[r_scrB])
                        S.op("act", lambda e: e.activation(out=Y[1][:], in_=scrB[:], func=AF.Exp, scale=1.0), reads=[r_scrB], writes=[rY[1]])
                        yield
                        for m in range(4):
                            d = m // 2
                            S.op("dve", lambda e: e.scalar_tensor_tensor(out=X[0][:, m, :], in0=X[1][:, m, :], scalar=sc["negb"][:, cm_[m], m:m + 1], in1=Gm[:, d, :],
                                                                         op0=ALU.mult, op1=ALU.mult), reads=[rX[1], r_sc, rGm], writes=[rX[0]])
                        for d in range(2):
                            S.op("pool", lambda e: e.tensor_tensor(out=AT[:, 2 * d:2 * d + 2, :], in0=Y[1][:, 2 * d:2 * d + 2, :],
                                                                   in1=QKm[:, d:d + 1, :].broadcast_to([128, 2, 128]), op=ALU.mult), reads=[rQKm, rY[1]], writes=[rAT])
                        yield
                        bk, rb = self.bank()
                        for m in range(4):
                            S.op("pe", lambda e: e.matmul(bk[:, m * 128:(m + 1) * 128], lhsT=X[0][:, m, :], rhs=self.identb[:], start=True, stop=True),
                                 reads=[rX[0], self.r_ident], writes=[rb], same_ok=True)
                        S.op("act", lambda e: e.copy(Y0[:], pv4(bk)), reads=[rb], writes=[rY0])
                        yield
                        S.op("dve", lambda e: e.tensor_tensor(out=X[1][:], in0=X[0][:], in1=b4(bd16), op=ALU.mult), reads=[rX[0], r_msk, rAT], writes=[rX[1]])
                        S.op("dve", lambda e: e.tensor_tensor(out=Y[1][:], in0=Y0[:], in1=b4(bd16), op=ALU.mult), reads=[rY0, r_msk, rAT], writes=[rY[1]])
                        S.op("dve", lambda e: e.tensor_tensor(out=W[:], in0=Y[1][:], in1=b4(self.identb), op=ALU.add), reads=[rY[1], self.r_ident], writes=[rW])
                        yield
                        cur = 1
                        for lev in range(3):
                            nxt = 1 - cur
                            bx, rbx = self.bank()
                            for m in range(4):
                                S.op("pe", lambda e: e.matmul(bx[:, m * 128:(m + 1) * 128], lhsT=Y[cur][:, m, :], rhs=X[cur][:, m, :], start=True, stop=True),
                                     reads=[rY[cur], rX[cur]], writes=[rbx], same_ok=True)
                            if lev < 2:
                                by, rby = self.bank()
                                for m in range(4):
                                    S.op("pe", lambda e: e.matmul(by[:, m * 128:(m + 1) * 128], lhsT=X[cur][:, m, :], rhs=Y[cur][:, m, :], start=True, stop=True),
                                         reads=[rY[cur], rX[cur]], writes=[rby], same_ok=True)
                            S.op("act", lambda e: e.copy(X[nxt][:], pv4(bx)), reads=[rbx], writes=[rX[nxt]])
                            if lev < 2:
                                S.op("dve", lambda e: e.tensor_copy(Y[nxt][:], pv4(by)), reads=[rby], writes=[rY[nxt]])
                            yield
                            bw, rbw = self.bank()
                            for m in range(4):
                                S.op("pe", lambda e: e.matmul(bw[:, m * 128:(m + 1) * 128], lhsT=X[nxt][:, m, :], rhs=W[:, m, :], start=True, stop=True),
                                     reads=[rX[nxt], rW], writes=[rbw], same_ok=True)
                            S.op("dve", lambda e: e.tensor_tensor(out=W[:], in0=pv4(bw), in1=W[:], op=ALU.add), reads=[rbw, rW], writes=[rW])
                            cur = nxt
                            yield
                        bk, rb = self.bank()
                        for m in range(4):
                            S.op("pe", lambda e: e.matmul(bk[:, m * 128:(m + 1) * 128], lhsT=W[:, m, :], rhs=self.identb[:], start=True, stop=True),
                                 reads=[rW, self.r_ident], writes=[rb], same_ok=True)
                        S.op("act", lambda e: e.copy(Tm[:], pv4(bk)), reads=[rb], writes=[rTm])
                        yield
                        for li, offm in enumerate((off16, off32, off64)):
                            S.op("dve", lambda e: e.tensor_tensor(out=X[0][:], in0=Y0[:], in1=b4(offm), op=ALU.mult), reads=[rY0, r_msk], writes=[rX[0]])
                            bz, rbz = self.bank()
                            for m in range(4):
                                S.op("pe", lambda e: e.matmul(bz[:, m * 128:(m + 1) * 128], lhsT=X[0][:, m, :], rhs=Tm[:, m, :], start=True, stop=True),
                                     reads=[rX[0], rTm], writes=[rbz], same_ok=True)
                            S.op("act", lambda e: e.copy(X[1][:], pv4(bz)), reads=[rbz], writes=[rX[1]])
                            yield
                            if li < 2:
                                bt, rbt = self.bank()
                                for m in range(4):
                                    S.op("pe", lambda e: e.matmul(bt[:, m * 128:(m + 1) * 128], lhsT=W[:, m, :], rhs=X[1][:, m, :], start=True, stop=True),
                                         reads=[rW, rX[1]], writes=[rbt], same_ok=True)
                            bw, rbw = self.bank()
                            for m in range(4):
                                S.op("pe", lambda e: e.matmul(bw[:, m * 128:(m + 1) * 128], lhsT=X[1][:, m, :], rhs=W[:, m, :], start=True, stop=True),
                                     reads=[rX[1], rW], writes=[rbw], same_ok=True)
                            if li < 2:
                                S.op("dve", lambda e: e.tensor_tensor(out=Tm[:], in0=pv4(bt), in1=Tm[:], op=ALU.add), reads=[rbt, rTm], writes=[rTm])
                            S.op("dve", lambda e: e.tensor_tensor(out=W[:], in0=pv4(bw), in1=W[:], op=ALU.add), reads=[rbw, rW], writes=[rW])
                            yield

                    def chain(step, B):
                        W, AT, rW, rAT = B["W"], B["AT"], B["rW"], B["rAT"]
                        cc = [order[0][step], order[1][step]]
                        cm_ = [cc[m // 2] for m in range(4)]
                        for m in range(4):
                            S.op("pool", lambda e: e.tensor_scalar(out=kdec[:, m, :], in0=kn[:, cm_[m], :], scalar1=sc["ecoef"][:, cm_[m], m:m + 1], scalar2=1.0, op0=ALU.mult, op1=ALU.mult),
                                 reads=[r_kn, r_sc], writes=[r_kdec])
                            S.op("pool", lambda e: e.tensor_scalar(out=bv[:, m, :], in0=vt[:, cm_[m], (m % 2) * 128:(m % 2 + 1) * 128], scalar1=sc["negb"][:, cm_[m], m:m + 1],
                                                                   scalar2=-1.0, op0=ALU.mult, op1=ALU.mult), reads=[r_vt, r_sc], writes=[r_bv])
                        bks, rbks = self.bank()
                        for m in range(4):
                            S.op("pe", lambda e: e.matmul(bks[:, m * 128:(m + 1) * 128], lhsT=kT(cm_[m]), rhs=Sb[:, m, :], start=True, stop=True), reads=[r_qkT, r_Sb], writes=[rbks], same_ok=True)
                        bo1, rbo1 = self.bank()
                        for m in range(4):
                            S.op("pe", lambda e: e.matmul(bo1[:, m * 128:(m + 1) * 128], lhsT=qT(cm_[m]), rhs=Sb[:, m, :], start=True, stop=True), reads=[r_qkT, r_Sb], writes=[rbo1], same_ok=True)
                        for m in range(4):
                            S.op("dve", lambda e: e.scalar_tensor_tensor(out=Rm[:, m, :], in0=bks[:, m * 128:(m + 1) * 128], scalar=sc["negbe"][:, cm_[m], m:m + 1], in1=bv[:, m, :],
                                                                         op0=ALU.mult, op1=ALU.add), reads=[rbks, r_sc, r_bv], writes=[r_Rm])
                            S.op("act", lambda e: e.activation(out=ot[:, m, :], in_=bo1[:, m * 128:(m + 1) * 128], func=AF.Copy, scale=sc["e"][:, cm_[m], m:m + 1]),
                                 reads=[rbo1, r_sc], writes=[r_ot])
                        bvn, rbvn = self.bank()
                        for m in range(4):
                            S.op("pe", lambda e: e.matmul(bvn[:, m * 128:(m + 1) * 128], lhsT=W[:, m, :], rhs=Rm[:, m, :], start=True, stop=True), reads=[rW, r_Rm], writes=[rbvn], same_ok=True)
                        S.op("act", lambda e: e.copy(vn[:], pv4(bvn)), reads=[rbvn], writes=[r_vn])
                        bo2, rbo2 = self.bank()
                        for m in range(4):
                            S.op("pe", lambda e: e.matmul(bo2[:, m * 128:(m + 1) * 128], lhsT=AT[:, m, :], rhs=vn[:, m, :], start=True, stop=True), reads=[rAT, r_vn], writes=[rbo2], same_ok=True)
                        bst, rbst = self.bank()
                        for m in range(4):
                            S.op("pe", lambda e: e.matmul(bst[:, m * 128:(m + 1) * 128], lhsT=kdec[:, m, :], rhs=vn[:, m, :], start=True, stop=True), reads=[r_kdec, r_vn], writes=[rbst], same_ok=True)
                        for m in range(4):
                            S.op("dve", lambda e: e.scalar_tensor_tensor(out=Sf[:, m, :], in0=Sf[:, m, :], scalar=sc["dl"][:, cm_[m], m:m + 1], in1=bst[:, m * 128:(m + 1) * 128],
                                                                         op0=ALU.mult, op1=ALU.add), reads=[r_Sf, r_sc, rbst], writes=[r_Sf])
                        S.op("act", lambda e: e.copy(Sb[:], Sf[:]), reads=[r_Sf], writes=[r_Sb])
                        S.op("dve", lambda e: e.tensor_tensor(out=ot[:], in0=pv4(bo2), in1=ot[:], op=ALU.add), reads=[rbo2, r_ot], writes=[r_ot])
                        for d in range(2):
                            c = cc[d]
                            src = ot[:, 2 * d:2 * d + 2, :]
                            dst = oacc[:, c, :].rearrange("p (j v) -> p j v", j=2)
                            if c not in written:
                                written.add(c)
                                S.op("pool", lambda e: e.tensor_copy(dst, src), reads=[r_ot], writes=[r_oacc[c]])
                            else:
                                S.op("pool", lambda e: e.tensor_tensor(out=dst, in0=dst, in1=src, op=ALU.add), reads=[r_ot, r_oacc[c]], writes=[r_oacc[c]])

                    for p_ in range(9):
                        gens = [pre(2 * p_, ST[0]), pre(2 * p_ + 1, ST[1])]
                        live = [True, True]
                        while any(live):
                            for gi in range(2):
                                if live[gi]:
                                    try:
                                        next(gens[gi])
                                    except StopIteration:
                                        live[gi] = False
                        chain(2 * p_, ST[0])
                        chain(2 * p_ + 1, ST[1])
                    S.barrier()
                with ExitStack() as p4:
                    ogT = self.sb([128, 2, 512], BF16, es=p4); r_ogT = Res("ogT")
                    ogs = [self.sb([128, 256], F32, es=p4) for _ in range(2)]; r_ogs = [Res(), Res()]
                    ogbs = [self.sb([128, 256], BF16, es=p4) for _ in range(2)]; r_ogbs = [Res(), Res()]
                    zss = [self.sb([128, 256], F32, es=p4) for _ in range(2)]; r_zss = [Res(), Res()]
                    junk = self.sb([128, 128], F32, es=p4); r_junk = Res("junk")
                    sss4 = [self.sb([128, 2], F32, es=p4) for _ in range(2)]; r_sss4 = [Res(), Res()]
                    ogTs = [ogT, self.sb([128, 2, 512], BF16, es=p4)]; r_ogTs = [r_ogT, Res("ogT1")]
                    tile_of = [min(t // 4, 4) for t in range(18)]
                    zb = {}

                    def p4A(t):
                        ti = tile_of[t]
                        zs, r_zs = zss[t % 2], r_zss[t % 2]
                        ss, r_ss = sss4[t % 2], r_sss4[t % 2]
                        for j in range(2):
                            S.op("act", lambda e: e.activation(out=junk[:], in_=oacc[:, t, j * 128:(j + 1) * 128], func=AF.Square, accum_out=ss[:, j:j + 1]),
                                 reads=[r_oacc[t]], writes=[r_ss])
                        S.op("act", lambda e: e.activation(out=ss[:], in_=ss[:], func=AF.Sqrt, bias=epsc[:, 0:1], scale=1.0 / 128.0), reads=[r_ss, r_eps], writes=[r_ss])
                        bz, rbz = self.bank()
                        for kc in range(8):
                            S.op("pe", lambda e: e.matmul(bz[:, 0:256], lhsT=self.uT[:, kc, t * 128:(t + 1) * 128], rhs=wz[:, kc, :], start=(kc == 0), stop=(kc == 7)),
                                 reads=[r_wB, self.r_u[ti]], writes=[rbz], same_ok=True)
                        S.op("act", lambda e: e.activation(out=zs[:], in_=bz[:, 0:256], func=AF.Silu), reads=[rbz], writes=[r_zs])

                    def p4B(t):
                        ti = tile_of[t]
                        t0, n, lc = TT[ti]
                        tt = t - t0 // 128
                        og, r_og = ogs[t % 2], r_ogs[t % 2]
                        ogb, r_ogb = ogbs[t % 2], r_ogbs[t % 2]
                        zs, r_zs = zss[t % 2], r_zss[t % 2]
                        ss, r_ss = sss4[t % 2], r_sss4[t % 2]
                        ogT_, r_ogT_ = ogTs[ti % 2], r_ogTs[ti % 2]
                        S.op("dve", lambda e: e.reciprocal(out=ss[:], in_=ss[:]), reads=[r_ss], writes=[r_ss])
                        for j in range(2):
                            S.op("dve", lambda e: e.scalar_tensor_tensor(out=og[:, j * 128:(j + 1) * 128], in0=oacc[:, t, j * 128:(j + 1) * 128], scalar=ss[:, j:j + 1], in1=ngrep[:],
                                                                         op0=ALU.mult, op1=ALU.mult), reads=[r_oacc[t], r_ss, r_ngr], writes=[r_og])
                        S.op("dve", lambda e: e.tensor_tensor(out=ogb[:], in0=og[:], in1=zs[:], op=ALU.mult), reads=[r_og, r_zs], writes=[r_ogb])
                        b2, rb2 = self.bank()
                        for j in range(2):
                            S.op("pe", lambda e: e.matmul(b2[:, j * 128:(j + 1) * 128], lhsT=ogb[:, j * 128:(j + 1) * 128], rhs=self.identb[:], start=True, stop=True),
                                 reads=[r_ogb, self.r_ident], writes=[rb2], same_ok=True)
                        S.op("act", lambda e: e.copy(ogT_[:, :, tt * 128:(tt + 1) * 128], b2[:, 0:256].rearrange("p (j c) -> p j c", j=2)), reads=[rb2], writes=[r_ogT_])
                        if tt == n // 128 - 1:
                            for oc in range(8):
                                bk, rb = self.bank()
                                for j in range(2):
                                    S.op("pe", lambda e: e.matmul(bk[:, 0:n], lhsT=wo[:, j, oc * 128:(oc + 1) * 128], rhs=ogT_[:, j, 0:n], start=(j == 0), stop=(j == 1)),
                                         reads=[r_wC, r_ogT_], writes=[rb], same_ok=True)
                                hz = self.hT[:, oc, t0:t0 + n]
                                S.op("dve", lambda e: e.scalar_tensor_tensor(out=hz, in0=bk[:, 0:n], scalar=self.mcol(i, 2, oc, lc), in1=hz, op0=ALU.mult, op1=ALU.add),
                                     reads=[rb, self.r_h[ti], self.r_mod[i]], writes=[self.r_h[ti]])
                    p4A(0)
                    for t in range(18):
                        if t + 1 < 18:
                            p4A(t + 1)
                        p4B(t)
                    S.barrier()


def build_program(depth_run=DEPTH, mixers=True, dbg=False):
    nc = bass.Bass("TRN2", target_bir_lowering=False)
    es = ExitStack()
    with es:
        kb = KB(nc, es, depth_run, mixers, dbg)
        kb.build()
        print("instructions", kb.S.ninst, "sems", kb.S.nsem, flush=True)
    return nc, kb


def _rope_tables(dim):
    n_freq = dim // 4
    inv = (10000.0 ** (-np.arange(n_freq, dtype=np.float32) / n_freq)).astype(np.float32)
    tok = np.arange(TL)
    row = (tok // 64).astype(np.float32)
    col = (tok % 64).astype(np.float32)
    ang = np.concatenate([row[:, None] * inv, col[:, None] * inv], -1).astype(np.float32)
    c = np.ones((dim // 2, T), np.float32)
    s = np.zeros((dim // 2, T), np.float32)
    c[:, :TL] = np.cos(ang).T
    s[:, :TL] = np.sin(ang).T
    return c, s


def _mla_host(inputs, shared, g):
    w_in = g("mla_w_in")[0]
    w_qb = g("mla_w_qb")[0]
    ev = np.arange(0, 32, 2)
    od = ev + 1
    cols = []
    for h in range(16):
        b = h * 96
        cols += list(range(b, b + 64)) + list(b + 64 + ev) + list(b + 64 + od) + list(b + 64 + ev) + list(b + 64 + od)
    shared["mla_wqx"] = np.ascontiguousarray(w_qb[:, cols])
    ia = list(1024 + ev) * 4
    ib = list(1024 + od) * 4
    shared["mla_wkr"] = np.ascontiguousarray(np.concatenate(
        [w_in[:, 0:64], w_in[:, ia], w_in[:, 0:64], w_in[:, ib]], axis=1))
    c, s = _rope_tables(32)
    one = np.ones((64, T), np.float32)
    shared["mla_qtab"] = np.concatenate([one, c, s, s, c], 0)
    shared["mla_kta"] = np.concatenate([one, c, -c, s, s], 0)
    shared["mla_ktb"] = np.concatenate([one, -s, s, c, c], 0)


def _diff_host(inputs, shared, g):
    w_in = g("diff_w_in")[0]
    ev = np.arange(0, 64, 2)
    od = ev + 1
    cols = []
    for h in range(8):
        for m in range(2):
            bq = h * 128 + m * 64
            bk = 1024 + h * 128 + m * 64
            cols += list(bq + ev) + list(bq + od) + list(bq + ev) + list(bq + od)
            cols += list(bk + ev) * 4
            cols += list(bk + od) * 4
    shared["diff_wx"] = np.ascontiguousarray(w_in[:, cols])
    shared["diff_wv"] = np.ascontiguousarray(w_in[:, 2048:3072])
    shared["diff_w_out"] = g("diff_w_out")[0]
    c, s = _rope_tables(64)
    shared["diff_qtab"] = np.concatenate([c, s, s, c], 0)
    shared["diff_kta"] = np.concatenate([c, -c, s, s], 0)
    shared["diff_ktb"] = np.concatenate([-s, s, c, c], 0)
    shared["diff_lam"] = np.ascontiguousarray(np.stack([g("diff_lambda_q1")[0], g("diff_lambda_k1")[0],
                                                        g("diff_lambda_q2")[0], g("diff_lambda_k2")[0]], axis=1))
    shared["diff_subln"] = g("diff_subln")[0].reshape(1, 128)


def _gla_host(inputs, shared, g):
    shared["gla_w_in"] = g("gla_w_in")[0]
    shared["gla_gw"] = np.ascontiguousarray(np.stack([g("gla_gate_w_fwd")[0], g("gla_gate_w_bwd")[0]], 0))
    shared["gla_gb"] = np.ascontiguousarray(np.stack([g("gla_gate_b_fwd")[0].reshape(4, 128), g("gla_gate_b_bwd")[0].reshape(4, 128)], 0))
    shared["gla_norm"] = g("gla_norm")[0].reshape(2, 128)
    shared["gla_w_out"] = g("gla_w_out")[0]


def _gdn_host(inputs, shared, g):
    w_in = g("gdn_w_in")[0]
    shared["gdn_w_in"] = w_in
    cols = []
    for kh in range(8):
        for base in (6144, 6160, 6176, 6192):
            cols += [base + 2 * kh, base + 2 * kh + 1]
    shared["gdn_wg"] = np.ascontiguousarray(w_in[:, cols])
    shared["gdn_conv"] = g("gdn_conv_w")[0].reshape(5, 32, 128)
    shared["gdn_hc"] = np.ascontiguousarray(np.concatenate([g("gdn_a_log_fwd")[0], g("gdn_a_log_bwd")[0],
                                                            g("gdn_dt_bias_fwd")[0], g("gdn_dt_bias_bwd")[0]]).reshape(1, 64))
    shared["gdn_norm"] = g("gdn_norm")[0].reshape(1, 128)
    shared["gdn_w_out"] = g("gdn_w_out")[0]

def make_in_maps(inputs):
    g = lambda k: np.ascontiguousarray(np.asarray(inputs[k], dtype=np.float32))
    shared = {
        "c_ctx": g("c_ctx").reshape(8, 128),
        "ada_w": g("ada_w"), "ada_b": g("ada_b").reshape(DEPTH, 48, 128),
        "ln1_g": g("ln1_g").reshape(DEPTH, 8, 128), "ln1_b": g("ln1_b").reshape(DEPTH, 8, 128),
        "ln2_g": g("ln2_g").reshape(DEPTH, 8, 128), "ln2_b": g("ln2_b").reshape(DEPTH, 8, 128),
        "mlp_w1": g("mlp_w1"), "mlp_w2": g("mlp_w2"),
        "mla_w_in": g("mla_w_in")[0], "mla_q_norm": g("mla_q_norm")[0].reshape(6, 128),
        "mla_kv_norm": g("mla_kv_norm")[0].reshape(2, 128),
        "mla_w_kvb": g("mla_w_kvb")[0], "mla_w_out": g("mla_w_out")[0],
    }
    _mla_host(inputs, shared, g)
    _diff_host(inputs, shared, g)
    _gla_host(inputs, shared, g)
    _gdn_host(inputs, shared, g)
    x, c, ctx = g("x"), g("c"), g("ctx")
    maps = []
    for b in range(8):
        m = dict(shared)
        m["x"] = x[b]
        m["ctx"] = ctx[b]
        m["c"] = c[b].reshape(8, 128)
        maps.append(m)
    return maps


def kernel(**inputs):
    nc, kb = build_program()
    maps = make_in_maps(inputs)
    res = run_bass_kernel_spmd(nc, maps, core_ids=list(range(8)))
    return np.stack([np.asarray(r["out"], dtype=np.float32) for r in res.results], axis=0)
```

```python
import math
import numpy as np
import concourse.bass as bass
import concourse.mybir as mybir
from concourse.bass_utils import run_bass_kernel_spmd
from contextlib import ExitStack

F32 = mybir.dt.float32
BF16 = mybir.dt.bfloat16
ALU = mybir.AluOpType
AF = mybir.ActivationFunctionType

DEPTH = 4
D = 1024
TL = 2048
TC = 256
T = TL + TC
ALPHA = (2 * DEPTH) ** 0.25
EPS = 1e-6
EPS_LN = EPS / (ALPHA * ALPHA)
TT = [(0, 512, 0), (512, 512, 0), (1024, 512, 0), (1536, 512, 0), (2048, 256, 1)]


class Res:
    __slots__ = ("name", "w", "r", "dsem", "dcnt")

    def __init__(self, name=""):
        self.name = name
        self.w = None
        self.r = {}
        self.dsem = None
        self.dcnt = 0


class Sched:
    EPOCH = 30000

    def __init__(self, nc, es):
        self.nc = nc
        self.es = es
        self.eng = {"pe": nc.tensor, "dve": nc.vector, "act": nc.scalar,
                    "pool": nc.gpsimd, "sp": nc.sync}
        self.cnt = {e: 0 for e in self.eng}
        self.cursem = {e: None for e in self.eng}
        self.last = {e: None for e in self.eng}
        self.seen = {e: {} for e in self.eng}
        self.nsem = 0
        self.ninst = 0
        self.owners = []
        self.out_events = []

    def newsem(self, name):
        self.nsem += 1
        return self.es.enter_context(self.nc.semaphore(f"{name}_{self.nsem}"))

    def _wait(self, e, ev):
        sem, val, _ = ev
        k = id(sem)
        if self.seen[e].get(k, 0) >= val:
            return
        self.eng[e].wait_ge(sem, val)
        self.seen[e][k] = val

    def _deps(self, e, reads, writes, same_ok):
        for r in reads:
            if r.w is not None and not (same_ok and r.w[2] == e):
                self._wait(e, r.w)
        for w in writes:
            if w.w is not None and not (same_ok and w.w[2] == e):
                self._wait(e, w.w)
            for ev in w.r.values():
                if not (same_ok and ev[2] == e):
                    self._wait(e, ev)

    def op(self, e, fn, reads=(), writes=(), same_ok=False):
        self._deps(e, reads, writes, same_ok)
        ins = fn(self.eng[e])
        if self.cnt[e] % self.EPOCH == 0:
            self.cursem[e] = self.newsem("c" + e)
        self.cnt[e] += 1
        val = (self.cnt[e] - 1) % self.EPOCH + 1
        sem = self.cursem[e]
        ins.then_inc(sem, 1)
        ev = (sem, val, e)
        self.last[e] = ev
        for r in reads:
            r.r[id(sem)] = ev
        for w in writes:
            w.w = ev
            w.r = {}
        self.ninst += 1
        return ev

    def dma(self, q, pairs, owner, reads=(), writes=(), **kw):
        self._deps(q, reads, writes, False)
        if owner.dsem is None:
            owner.dsem = self.newsem("d")
            self.owners.append(owner)
        if owner.dcnt > 0:
            self._wait(q, (owner.dsem, owner.dcnt, "dma"))
        for (o, i) in pairs:
            self.eng[q].dma_start(out=o, in_=i, **kw).then_inc(owner.dsem, 16)
            owner.dcnt += 16
        ev = (owner.dsem, owner.dcnt, "dma")
        for r in reads:
            r.r[id(owner.dsem)] = ev
        for w in writes:
            w.w = ev
            w.r = {}
        self.ninst += len(pairs)
        return ev

    def barrier(self):
        evs = [ev for ev in self.last.values() if ev is not None]
        evs += [(o.dsem, o.dcnt, "dma") for o in self.owners if o.dcnt > 0]
        for e in ("pe", "dve", "act", "pool", "sp"):
            for ev in evs:
                if ev[2] != e:
                    self._wait(e, ev)

    def finish(self):
        for ev in self.out_events:
            self._wait("sp", ev)


class KB:
    def __init__(self, nc, es, depth_run=DEPTH, mixers=True, dbg=False):
        self.nc, self.es = nc, es
        self.S = Sched(nc, es)
        self.depth_run = depth_run
        self.mixers = mixers
        self.dbg = dbg
        self.din = {}
        self._n = 0

    def sb(self, shape, dt, es=None, name=None):
        self._n += 1
        return (es or self.es).enter_context(self.nc.sbuf_tensor(name or f"t{self._n}", list(shape), dt))

    def dram_in(self, name, shape):
        t = self.nc.dram_tensor(name, list(shape), F32, kind="ExternalInput").ap()
        self.din[name] = t
        return t

    def bank(self):
        i = self.bank_i
        self.bank_i = (i + 1) % self.nrr
        return self.banks[i], self.bank_res[i]

    def declare(self):
        di = self.dram_in
        di("x", [TL, D]); di("ctx", [TC, D]); di("c", [8, 128]); di("c_ctx", [8, 128])
        di("ada_w", [DEPTH, D, 6 * D]); di("ada_b", [DEPTH, 48, 128])
        for n in ("ln1_g", "ln1_b", "ln2_g", "ln2_b"):
            di(n, [DEPTH, 8, 128])
        di("mlp_w1", [DEPTH, D, 4 * D]); di("mlp_w2", [DEPTH, 4 * D, D])
        di("mla_w_in", [D, 1056]); di("mla_q_norm", [6, 128]); di("mla_kv_norm", [2, 128])
        di("mla_w_kvb", [256, 2048]); di("mla_w_out", [D, D])
        di("mla_wqx", [768, 2048]); di("mla_wkr", [D, 256])
        di("mla_qtab", [128, T]); di("mla_kta", [128, T]); di("mla_ktb", [128, T])
        di("diff_wx", [D, 16 * 384]); di("diff_wv", [D, D]); di("diff_w_out", [D, D])
        di("diff_qtab", [128, T]); di("diff_kta", [128, T]); di("diff_ktb", [128, T])
        di("diff_lam", [64, 4]); di("diff_subln", [1, 128])
        di("gla_w_in", [D, 3104]); di("gla_gw", [2, 16, 512]); di("gla_gb", [2, 4, 128])
        di("gla_norm", [2, 128]); di("gla_w_out", [D, D])
        di("gdn_w_in", [D, 6208]); di("gdn_wg", [D, 64]); di("gdn_conv", [5, 32, 128])
        di("gdn_hc", [1, 64]); di("gdn_norm", [1, 128]); di("gdn_w_out", [2 * D, D])
        self.out = self.nc.dram_tensor("out", [TL, D], F32, kind="ExternalOutput").ap()
        if self.dbg:
            self.out_c = self.nc.dram_tensor("out_c", [TC, D], F32, kind="ExternalOutput").ap()

        nc = self.nc
        self.banks = [self.es.enter_context(nc.psum_tensor(f"bank{i}", [128, 512], F32)) for i in range(8)]
        self.bank_res = [Res(f"bank{i}") for i in range(8)]
        self.bank_i = 0
        self.nrr = 6
        self.hT = self.sb([128, 8, T], F32, name="hT")
        self.r_h = [[Res(f"h{t}_{k}") for k in range(8)] for t in range(len(TT))]
        self.uT = self.sb([128, 8, T], BF16, name="uT")
        self.r_u = [Res(f"u{t}") for t in range(len(TT))]
        self.ident = self.sb([128, 128], F32, name="ident"); self.r_ident = Res("ident")
        self.identb = self.sb([128, 128], BF16, name="identb")
        self.ones = self.sb([128, 128], F32, name="ones")
        self.identr = self.sb([128, 128], F32, name="identr")
        self.onesr = self.sb([128, 128], F32, name="onesr")
        self.sT = self.sb([128, 8, 2], F32, name="sT"); self.r_sT = Res("sT")
        self.sTb = self.sb([128, 8, 2], BF16, name="sTb")
        self.mod = [self.sb([128, 48, 2], F32, name=f"mod{i}") for i in range(DEPTH)]
        self.r_mod = [Res(f"mod{i}") for i in range(DEPTH)]
        self.lnp = self.sb([128, 4, DEPTH, 8], F32, name="lnp"); self.r_lnp = Res("lnp")
        self.adab = self.sb([128, DEPTH, 48], F32, name="adab"); self.r_adab = Res("adab")
        self.vst = self.sb([64, 128], F32, name="vst"); self.r_vst = Res("vst")
        self.NW = 3
        self.wslot = [self.sb([128, 4096], BF16, name=f"wslot{i}") for i in range(self.NW)]
        self.r_wslot = [Res(f"wslot{i}") for i in range(self.NW)]
        self.w_i = 0

    def wnext(self):
        i = self.w_i
        self.w_i = (i + 1) % self.NW
        return self.wslot[i], self.r_wslot[i]

    def load_cols(self, src, n, dst, r_dst):
        S = self.S
        S.dma("sp", [(self.vst[0:n, :], src)], self.r_vst, writes=[self.r_vst])
        bk, rb = self.bank()
        S.op("pe", lambda e: e.transpose(bk[:, 0:n], self.vst[0:n, :], self.ident[0:n, 0:n]),
             reads=[self.r_vst, self.r_ident], writes=[rb], same_ok=True)
        S.op("dve", lambda e: e.tensor_copy(dst, bk[:, 0:n]), reads=[rb], writes=[r_dst])

    def setup(self):
        S, nc = self.S, self.nc
        S.op("pool", lambda e: e.memset(self.ident[:], 0.0), writes=[self.r_ident])
        S.op("pool", lambda e: e.affine_select(out=self.ident[:], in_=self.ident[:], compare_op=ALU.not_equal,
                                              fill=1.0, base=0, pattern=[[-1, 128]], channel_multiplier=1),
             reads=[self.r_ident], writes=[self.r_ident])
        S.op("dve", lambda e: e.tensor_copy(self.identb[:], self.ident[:]), reads=[self.r_ident], writes=[self.r_ident])
        S.op("dve", lambda e: e.memset(self.ones[:], 1.0), writes=[self.r_ident])
        S.op("dve", lambda e: e.tensor_copy(self.identr[:].bitcast(mybir.dt.float32r), self.ident[:]), reads=[self.r_ident], writes=[self.r_ident])
        S.op("dve", lambda e: e.tensor_copy(self.onesr[:].bitcast(mybir.dt.float32r), self.ones[:]), reads=[self.r_ident], writes=[self.r_ident])
        for k, n in enumerate(("ln1_g", "ln1_b", "ln2_g", "ln2_b")):
            for i in range(DEPTH):
                self.load_cols(self.din[n][i], 8, self.lnp[:, k, i, :], self.r_lnp)
        for i in range(DEPTH):
            self.load_cols(self.din["ada_b"][i], 48, self.adab[:, i, :], self.r_adab)
        self.load_cols(self.din["c"], 8, self.sT[:, :, 0], self.r_sT)
        self.load_cols(self.din["c_ctx"], 8, self.sT[:, :, 1], self.r_sT)
        S.op("act", lambda e: e.activation(out=self.sT[:], in_=self.sT[:], func=AF.Silu), reads=[self.r_sT], writes=[self.r_sT])
        S.op("dve", lambda e: e.tensor_copy(self.sTb[:], self.sT[:]), reads=[self.r_sT], writes=[self.r_sT])
        with ExitStack() as ph:
            xs = [self.sb([128, D], F32, es=ph) for _ in range(2)]
            r_xs = [Res("xs0"), Res("xs1")]
            for t in range(18):
                src = self.din["x"][t * 128:(t + 1) * 128, :] if t < 16 else self.din["ctx"][(t - 16) * 128:(t - 15) * 128, :]
                st, rs = xs[t % 2], r_xs[t % 2]
                S.dma("sp", [(st[:], src)], rs, writes=[rs])
                ti = min(t // 4, 4)
                for g in range(2):
                    bk, rb = self.bank()
                    for j in range(4):
                        S.op("pe", lambda e: e.transpose(bk[:, j * 128:(j + 1) * 128], st[:, (g * 4 + j) * 128:(g * 4 + j + 1) * 128], self.ident[:]),
                             reads=[rs, self.r_ident], writes=[rb], same_ok=True)
                    dst = self.hT[:, g * 4:(g + 1) * 4, t * 128:(t + 1) * 128]
                    srcp = bk[:, 0:512].rearrange("p (j n) -> p j n", j=4)
                    if g == 0:
                        S.op("dve", lambda e: e.tensor_copy(dst, srcp), reads=[rb], writes=self.r_h[ti][g * 4:(g + 1) * 4])
                    else:
                        S.op("act", lambda e: e.copy(dst, srcp), reads=[rb], writes=self.r_h[ti][g * 4:(g + 1) * 4])
            S.barrier()

    def mods_gen(self, i, stg, r_stg):
        S = self.S
        aw = self.din["ada_w"][i]
        bk, rb = self.banks[7], self.bank_res[7]
        for blk in range(24):
            st, rs = stg[blk % 2], r_stg[blk % 2]
            S.dma("pool", [(st[:], aw[:, blk * 256:(blk + 1) * 256].rearrange("(k p) n -> p k n", p=128))], rs, writes=[rs])
            for cc in range(2):
                c = blk * 2 + cc
                for kc in range(8):
                    S.op("pe", lambda e: e.matmul(bk[:, c * 2:(c + 1) * 2], lhsT=st[:, kc, cc * 128:(cc + 1) * 128], rhs=self.sTb[:, kc, :],
                                                  start=(kc == 0), stop=(kc == 7)),
                         reads=[rs, self.r_sT], writes=[rb], same_ok=True)
            yield
        m, rm = self.mod[i], self.r_mod[i]
        pv = bk[:, 0:96].rearrange("p (c l) -> p c l", l=2)
        for l in range(2):
            S.op("dve", lambda e: e.tensor_tensor(out=m[:, :, l], in0=pv[:, :, l], in1=self.adab[:, i, :], op=ALU.add),
                 reads=[rb, self.r_adab], writes=[rm])
        for c0 in (8, 32):
            S.op("dve", lambda e: e.tensor_scalar(out=m[:, c0:c0 + 8, :], in0=m[:, c0:c0 + 8, :], scalar1=1.0, scalar2=None, op0=ALU.add),
                 reads=[rm], writes=[rm])
        for c0 in (16, 40):
            S.op("dve", lambda e: e.tensor_scalar(out=m[:, c0:c0 + 8, :], in0=m[:, c0:c0 + 8, :], scalar1=1.0 / ALPHA, scalar2=None, op0=ALU.mult),
                 reads=[rm], writes=[rm])

    def mods(self, i):
        with ExitStack() as ph:
            stg = [self.sb([128, 8, 256], BF16, es=ph) for _ in range(2)]
            r_stg = [Res("as0"), Res("as1")]
            for _ in self.mods_gen(i, stg, r_stg):
                pass
            self.S.barrier()

    def mcol(self, i, which, kc, lc):
        return self.mod[i][:, which * 8 + kc, lc:lc + 1]

    def modulate_all(self, i, sub):
        S = self.S
        for ti, (t0, n, lc) in enumerate(TT):
            for kc in range(8):
                S.op("dve", lambda e: e.tensor_scalar(out=self.uT[:, kc, t0:t0 + n], in0=self.hT[:, kc, t0:t0 + n],
                                                      scalar1=self.mcol(i, 3 * sub + 1, kc, lc), scalar2=self.mcol(i, 3 * sub, kc, lc),
                                                      op0=ALU.mult, op1=ALU.add),
                     reads=[self.r_h[ti][kc], self.r_mod[i]], writes=[self.r_u[ti]])

    def layer_norm(self, i, sub, nxt):
        S = self.S
        R32 = mybir.dt.float32r
        with ExitStack() as ph:
            sq = [self.sb([128, 512], F32, es=ph) for _ in range(3)]; r_sq = [Res() for _ in range(3)]
            mean = [self.sb([128, 512], F32, es=ph) for _ in range(2)]; r_mean = [Res(), Res()]
            msq = [self.sb([128, 512], F32, es=ph) for _ in range(2)]; r_msq = [Res(), Res()]
            rstd = [self.sb([128, 512], F32, es=ph) for _ in range(2)]; r_rstd = [Res(), Res()]
            tmp = [self.sb([128, 512], F32, es=ph) for _ in range(3)]; r_tmp = [Res() for _ in range(3)]
            epsc = self.sb([128, 1], F32, es=ph); r_eps = Res()
            S.op("pool", lambda e: e.memset(epsc[:], EPS_LN), writes=[r_eps])
            banks = {}
            cnt = [0, 0]

            def stats(ti):
                t0, n, lc = TT[ti]
                rh = self.r_h[ti]
                b1, rb1 = self.bank()
                b2, rb2 = self.bank()
                banks[ti] = (b1, rb1, b2, rb2)
                for kc in range(8):
                    z = self.hT[:, kc, t0:t0 + n]
                    s_, rs_ = sq[cnt[0] % 3], r_sq[cnt[0] % 3]
                    cnt[0] += 1
                    S.op("act", lambda e: e.activation(out=s_[:, 0:n].bitcast(R32), in_=z, func=AF.Square), reads=[rh[kc]], writes=[rs_])
                    S.op("pe", lambda e: e.matmul(b1[:, 0:n], lhsT=self.ones[:], rhs=z, start=(kc == 0), stop=(kc == 7)),
                         reads=[rh[kc], self.r_ident], writes=[rb1], same_ok=True)
                    S.op("pe", lambda e: e.matmul(b2[:, 0:n], lhsT=self.onesr[:].bitcast(R32), rhs=s_[:, 0:n].bitcast(R32), start=(kc == 0), stop=(kc == 7)),
                         reads=[rs_, self.r_ident], writes=[rb2], same_ok=True)

            def finish_stats(ti):
                t0, n, lc = TT[ti]
                b1, rb1, b2, rb2 = banks[ti]
                mn, rmn = mean[ti % 2], r_mean[ti % 2]
                ms, rms = msq[ti % 2], r_msq[ti % 2]
                rs, rrs = rstd[ti % 2], r_rstd[ti % 2]
                S.op("act", lambda e: e.activation(out=mn[:, 0:n], in_=b1[:, 0:n], func=AF.Copy, scale=1.0 / D), reads=[rb1], writes=[rmn])
                S.op("dve", lambda e: e.tensor_tensor(out=ms[:, 0:n], in0=mn[:, 0:n], in1=mn[:, 0:n], op=ALU.mult), reads=[rmn], writes=[rms])
                S.op("dve", lambda e: e.scalar_tensor_tensor(out=rs[:, 0:n], in0=b2[:, 0:n], scalar=1.0 / D, in1=ms[:, 0:n],
                                                             op0=ALU.mult, op1=ALU.subtract), reads=[rb2, rms], writes=[rrs])
                S.op("act", lambda e: e.activation(out=rs[:, 0:n], in_=rs[:, 0:n], func=AF.Sqrt, bias=epsc[:, 0:1], scale=1.0),
                     reads=[rrs, r_eps], writes=[rrs])
                S.op("dve", lambda e: e.reciprocal(out=rs[:, 0:n], in_=rs[:, 0:n]), reads=[rrs], writes=[rrs])

            def normalize(ti):
                t0, n, lc = TT[ti]
                rh = self.r_h[ti]
                mn, rmn = mean[ti % 2], r_mean[ti % 2]
                rs, rrs = rstd[ti % 2], r_rstd[ti % 2]
                for kc in range(8):
                    z = self.hT[:, kc, t0:t0 + n]
                    tp, rtp = tmp[cnt[1] % 3], r_tmp[cnt[1] % 3]
                    cnt[1] += 1
                    S.op("pool", lambda e: e.tensor_tensor(out=tp[:, 0:n], in0=z, in1=mn[:, 0:n], op=ALU.subtract), reads=[rh[kc], rmn], writes=[rtp])
                    S.op("dve", lambda e: e.tensor_tensor(out=tp[:, 0:n], in0=tp[:, 0:n], in1=rs[:, 0:n], op=ALU.mult), reads=[rtp, rrs], writes=[rtp])
                    S.op("act", lambda e: e.activation(out=z, in_=tp[:, 0:n], func=AF.Identity,
                                                       bias=self.lnp[:, 2 * sub + 1, i, kc:kc + 1], scale=self.lnp[:, 2 * sub, i, kc:kc + 1]),
                         reads=[rtp, self.r_lnp], writes=[rh[kc]])
                    if nxt is not None:
                        ni, nsub = nxt
                        S.op("dve", lambda e: e.tensor_scalar(out=self.uT[:, kc, t0:t0 + n], in0=z,
                                                              scalar1=self.mcol(ni, 3 * nsub + 1, kc, lc), scalar2=self.mcol(ni, 3 * nsub, kc, lc),
                                                              op0=ALU.mult, op1=ALU.add),
                             reads=[rh[kc], self.r_mod[ni]], writes=[self.r_u[ti]])

            nt = len(TT)
            stats(0)
            finish_stats(0)
            for ti in range(nt):
                if ti + 1 < nt:
                    stats(ti + 1)
                normalize(ti)
                if ti + 1 < nt:
                    finish_stats(ti + 1)
            S.barrier()

    def mlp(self, i):
        S = self.S
        w1 = self.din["mlp_w1"][i]
        w2 = self.din["mlp_w2"][i]
        with ExitStack() as ph:
            mg = None
            if i + 1 < DEPTH:
                mstg = [self.sb([128, 8, 256], BF16, es=ph) for _ in range(2)]
                mg = self.mods_gen(i + 1, mstg, [Res("ms0"), Res("ms1")])
            ab = [self.sb([128, 4, 512], BF16, es=ph) for _ in range(2)]; r_ab = [Res(), Res()]
            rl = [self.sb([128, 512], BF16, es=ph) for _ in range(3)]; r_rl = [Res(), Res(), Res()]
            rli = 0
            step = 0
            for j in range(8):
                wa, r_wa = self.wnext()
                wb, r_wb = self.wnext()
                S.dma("pool", [(wa[:].rearrange("p (k n) -> p k n", k=8), w1[:, j * 512:(j + 1) * 512].rearrange("(k p) n -> p k n", p=128))],
                      r_wa, writes=[r_wa])
                S.dma("pool", [(wb[:].rearrange("p (k n) -> p k n", k=4), w2[j * 512:(j + 1) * 512, :].rearrange("(k p) n -> p k n", p=128))],
                      r_wb, writes=[r_wb])
                wav = wa[:].rearrange("p (k n) -> p k n", k=8)
                wbv = wb[:].rearrange("p (k n) -> p k n", k=4)
                for ti, (t0, n, lc) in enumerate(TT):
                    a_, r_a = ab[step % 2], r_ab[step % 2]
                    step += 1
                    if mg is not None:
                        try:
                            next(mg)
                        except StopIteration:
                            mg = None
                    for hc in range(4):
                        bk, rb = self.bank()
                        for kc in range(8):
                            S.op("pe", lambda e: e.matmul(bk[:, 0:n], lhsT=wav[:, kc, hc * 128:(hc + 1) * 128], rhs=self.uT[:, kc, t0:t0 + n],
                                                          start=(kc == 0), stop=(kc == 7)),
                                 reads=[r_wa, self.r_u[ti]], writes=[rb], same_ok=True)
                        r_, rr_ = rl[rli % 3], r_rl[rli % 3]
                        rli += 1
                        S.op("act", lambda e: e.activation(out=r_[:, 0:n], in_=bk[:, 0:n], func=AF.Relu), reads=[rb], writes=[rr_])
                        S.op("pool", lambda e: e.tensor_tensor(out=a_[:, hc, 0:n], in0=r_[:, 0:n], in1=r_[:, 0:n], op=ALU.mult),
                             reads=[rr_], writes=[r_a])
                    for oc in range(8):
                        bk, rb = self.bank()
                        for kc in range(4):
                            S.op("pe", lambda e: e.matmul(bk[:, 0:n], lhsT=wbv[:, kc, oc * 128:(oc + 1) * 128], rhs=a_[:, kc, 0:n],
                                                          start=(kc == 0), stop=(kc == 3)),
                                 reads=[r_wb, r_a], writes=[rb], same_ok=True)
                        hz = self.hT[:, oc, t0:t0 + n]
                        S.op("dve", lambda e: e.scalar_tensor_tensor(out=hz, in0=bk[:, 0:n], scalar=self.mcol(i, 5, oc, lc), in1=hz,
                                                                     op0=ALU.mult, op1=ALU.add),
                             reads=[rb, self.r_h[ti][oc], self.r_mod[i]], writes=[self.r_h[ti][oc]])
            if mg is not None:
                for _ in mg:
                    pass
            S.barrier()

    def store_out(self):
        S = self.S
        with ExitStack() as ph:
            os_ = [self.sb([128, D], F32, es=ph) for _ in range(2)]
            r_os = [Res("os0"), Res("os1")]
            nt = 18 if self.dbg else 16
            for t in range(nt):
                st, rs = os_[t % 2], r_os[t % 2]
                ti = min(t // 4, 4)
                for g in range(2):
                    bk, rb = self.bank()
                    for j in range(4):
                        S.op("pe", lambda e: e.transpose(bk[:, j * 128:(j + 1) * 128], self.hT[:, g * 4 + j, t * 128:(t + 1) * 128], self.ident[:]),
                             reads=[self.r_h[ti][g * 4 + j], self.r_ident], writes=[rb], same_ok=True)
                    if g == 0:
                        S.op("dve", lambda e: e.tensor_copy(st[:, 0:512], bk[:, 0:512]), reads=[rb], writes=[rs])
                    else:
                        S.op("act", lambda e: e.copy(st[:, 512:1024], bk[:, 0:512]), reads=[rb], writes=[rs])
                dst = self.out[t * 128:(t + 1) * 128, :] if t < 16 else self.out_c[(t - 16) * 128:(t - 15) * 128, :]
                ev = S.dma("sp", [(dst, st[:])], rs, reads=[rs])
                S.out_events.append(ev)
            S.finish()

    def build(self):
        self.declare()
        self.setup()
        self.mods(0)
        self.modulate_all(0, 0)
        for i in range(self.depth_run):
            if self.mixers:
                self.mixer(i)
            self.layer_norm(i, 0, (i, 1))
            self.mlp(i)
            self.layer_norm(i, 1, (i + 1, 0) if i + 1 < DEPTH else None)
        self.store_out()


    def mixer(self, i):
        if i == 0:
            self.mla(i)
        elif i == 1:
            self.diff(i)
        elif i == 2:
            self.gla(i)
        elif i == 3:
            self.gdn(i)

    def mla(self, i):
        S = self.S
        SCALE = 96.0 ** -0.5
        with ExitStack() as ph:
            kp = [self.sb([128, T], BF16, es=ph) for _ in range(2)]; r_kp = [Res("kp0"), Res("kp1")]
            qtab = self.sb([128, T], F32, es=ph); r_qtab = Res("qtab")
            opad = self.sb([128, 2, 128], BF16, es=ph); r_opad = Res("opad")
            nrm = self.sb([128, 8], F32, es=ph); r_nrm = Res("nrm")
            epsc = self.sb([128, 1], F32, es=ph); r_eps = Res("eps")
            S.dma("sp", [(qtab[:], self.din["mla_qtab"])], r_qtab, writes=[r_qtab])
            S.op("pool", lambda e: e.memset(opad[:], 0.0), writes=[r_opad])
            S.op("pool", lambda e: e.memset(opad[:, 0, 0:64], 1.0), reads=[r_opad], writes=[r_opad])
            S.op("pool", lambda e: e.memset(opad[:, 1, 64:128], 1.0), reads=[r_opad], writes=[r_opad])
            S.op("pool", lambda e: e.memset(epsc[:], EPS), writes=[r_eps])
            self.load_cols(self.din["mla_q_norm"], 6, nrm[:, 0:6], r_nrm)
            self.load_cols(self.din["mla_kv_norm"], 2, nrm[:, 6:8], r_nrm)
            with ExitStack() as p1:
                raw = self.sb([128, 8, 512], F32, es=p1); r_raw = Res("raw")
                sq = [self.sb([128, 512], F32, es=p1) for _ in range(2)]; r_sq = [Res(), Res()]
                rs = self.sb([128, 2, 512], F32, es=p1); r_rs = Res("rs")
                kta = self.sb([128, 512], F32, es=p1); r_kta = Res("kta")
                ktb = self.sb([128, 512], F32, es=p1); r_ktb = Res("ktb")
                t1 = self.sb([128, 512], F32, es=p1); r_t1 = Res("t1")
                t2 = self.sb([128, 512], F32, es=p1); r_t2 = Res("t2")
                wi = []
                for blk in range(2):
                    w_, r_w = self.wnext()
                    S.dma("pool", [(w_[:].rearrange("p (k n) -> p k n", k=8),
                                    self.din["mla_w_in"][:, blk * 512:(blk + 1) * 512].rearrange("(k p) n -> p k n", p=128))], r_w, writes=[r_w])
                    wi.append((w_[:].rearrange("p (k n) -> p k n", k=8), r_w))
                w_, r_wk = self.wnext()
                wkr = w_[:, 0:2048].rearrange("p (k n) -> p k n", k=8)
                S.dma("pool", [(wkr, self.din["mla_wkr"].rearrange("(k p) n -> p k n", p=128))], r_wk, writes=[r_wk])
                for ti, (t0, n, lc) in enumerate(TT):
                    ru = self.r_u[ti]
                    S.dma("sp", [(kta[:, 0:n], self.din["mla_kta"][:, t0:t0 + n])], r_kta, writes=[r_kta])
                    S.dma("sp", [(ktb[:, 0:n], self.din["mla_ktb"][:, t0:t0 + n])], r_ktb, writes=[r_ktb])
                    bA, rbA = self.bank()
                    bB, rbB = self.bank()
                    for kc in range(8):
                        S.op("pe", lambda e: e.matmul(bA[:, 0:n], lhsT=wkr[:, kc, 0:128], rhs=self.uT[:, kc, t0:t0 + n], start=(kc == 0), stop=(kc == 7)),
                             reads=[r_wk, ru], writes=[rbA], same_ok=True)
                    for kc in range(8):
                        S.op("pe", lambda e: e.matmul(bB[:, 0:n], lhsT=wkr[:, kc, 128:256], rhs=self.uT[:, kc, t0:t0 + n], start=(kc == 0), stop=(kc == 7)),
                             reads=[r_wk, ru], writes=[rbB], same_ok=True)
                    S.op("dve", lambda e: e.tensor_tensor(out=t1[64:128, 0:n], in0=bA[64:128, 0:n], in1=kta[64:128, 0:n], op=ALU.mult),
                         reads=[rbA, r_kta], writes=[r_t1])
                    S.op("dve", lambda e: e.tensor_tensor(out=t2[64:128, 0:n], in0=bB[64:128, 0:n], in1=ktb[64:128, 0:n], op=ALU.mult),
                         reads=[rbB, r_ktb], writes=[r_t2])
                    S.op("pool", lambda e: e.tensor_tensor(out=kp[0][64:128, t0:t0 + n], in0=t1[64:128, 0:n], in1=t2[64:128, 0:n], op=ALU.add),
                         reads=[r_t1, r_t2], writes=[r_kp[0]])
                    S.op("pool", lambda e: e.tensor_copy(kp[1][64:128, t0:t0 + n], kp[0][64:128, t0:t0 + n]), reads=[r_kp[0]], writes=[r_kp[1]])
                    for oc in range(8):
                        wv, r_w = wi[oc // 4]
                        bk, rb = self.bank()
                        for kc in range(8):
                            S.op("pe", lambda e: e.matmul(bk[:, 0:n], lhsT=wv[:, kc, (oc % 4) * 128:(oc % 4 + 1) * 128], rhs=self.uT[:, kc, t0:t0 + n],
                                                          start=(kc == 0), stop=(kc == 7)),
                                 reads=[r_w, ru], writes=[rb], same_ok=True)
                        if oc % 2 == 0:
                            S.op("dve", lambda e: e.tensor_copy(raw[:, oc, 0:n], bk[:, 0:n]), reads=[rb], writes=[r_raw])
                        else:
                            S.op("act", lambda e: e.copy(raw[:, oc, 0:n], bk[:, 0:n]), reads=[rb], writes=[r_raw])
                    bq, rbq = self.banks[6], self.bank_res[6]
                    bkv, rbkv = self.banks[7], self.bank_res[7]
                    for oc in range(8):
                        s_, rs_ = sq[oc % 2], r_sq[oc % 2]
                        S.op("act", lambda e: e.activation(out=s_[:, 0:n], in_=raw[:, oc, 0:n], func=AF.Square), reads=[r_raw], writes=[rs_])
                        if oc < 6:
                            S.op("pe", lambda e: e.matmul(bq[:, 0:n], lhsT=self.ones[:], rhs=s_[:, 0:n], start=(oc == 0), stop=(oc == 5)),
                                 reads=[rs_, self.r_ident], writes=[rbq], same_ok=True)
                        else:
                            S.op("pe", lambda e: e.matmul(bkv[:, 0:n], lhsT=self.ones[:], rhs=s_[:, 0:n], start=(oc == 6), stop=(oc == 7)),
                                 reads=[rs_, self.r_ident], writes=[rbkv], same_ok=True)
                    for g, (bb, rbb, dim) in enumerate(((bq, rbq, 768.0), (bkv, rbkv, 256.0))):
                        S.op("act", lambda e: e.activation(out=rs[:, g, 0:n], in_=bb[:, 0:n], func=AF.Sqrt, bias=epsc[:, 0:1], scale=1.0 / dim),
                             reads=[rbb, r_eps], writes=[r_rs])
                        S.op("dve", lambda e: e.reciprocal(out=rs[:, g, 0:n], in_=rs[:, g, 0:n]), reads=[r_rs], writes=[r_rs])
                    for oc in range(8):
                        g = 0 if oc < 6 else 1
                        S.op("dve", lambda e: e.scalar_tensor_tensor(out=self.uT[:, oc, t0:t0 + n], in0=raw[:, oc, 0:n], scalar=nrm[:, oc:oc + 1],
                                                                     in1=rs[:, g, 0:n], op0=ALU.mult, op1=ALU.mult),
                             reads=[r_raw, r_nrm, r_rs], writes=[ru])
                S.barrier()
            with ExitStack() as p2:
                qp = [self.sb([128, T], BF16, es=p2) for _ in range(2)]; r_qp = [Res("qp0"), Res("qp1")]
                vp = [self.sb([128, 18, 128], BF16, es=p2) for _ in range(2)]; r_vp = [Res("vp0"), Res("vp1")]
                pt = [self.sb([128, 512], BF16, es=p2) for _ in range(4)]; r_pt = [Res() for _ in range(4)]
                rden = [self.sb([128, 512], F32, es=p2) for _ in range(2)]; r_rden = [Res("rden0"), Res("rden1")]
                attn_ctr = [0]
                pending = [None]
                self.nrr = 4
                self.bank_i = 0
                opr = [self.sb([128, 512], BF16, es=p2) for _ in range(2)]; r_opr = [Res(), Res()]
                wo = [self.sb([128, D], BF16, es=p2) for _ in range(2)]; r_wo = [Res("wo0"), Res("wo1")]
                for par in range(2):
                    S.op("pool", lambda e: e.memset(vp[par][:], 0.0), writes=[r_vp[par]])
                pti = 0
                wq = wkv = None
                for pair in range(8):
                    S.dma("pool", [(wo[pair % 2][:], self.din["mla_w_out"][pair * 128:(pair + 1) * 128, :])], r_wo[pair % 2], writes=[r_wo[pair % 2]])
                    for par in range(2):
                        h = pair * 2 + par
                        hl = h % 4
                        if hl == 0:
                            w_, r_wq = self.wnext()
                            wq = w_[:, 0:3072].rearrange("p (k n) -> p k n", k=6)
                            wkv = w_[:, 3072:4096].rearrange("p (k n) -> p k n", k=2)
                            S.dma("pool", [(wq, self.din["mla_wqx"][:, h * 128:(h + 4) * 128].rearrange("(k p) n -> p k n", p=128)),
                                           (wkv, self.din["mla_w_kvb"][:, h * 128:(h + 4) * 128].rearrange("(k p) n -> p k n", p=128))],
                                  r_wq, writes=[r_wq])
                        for ti, (t0, n, lc) in enumerate(TT):
                            ru = self.r_u[ti]
                            bk, rb = self.bank()
                            for kc in range(6):
                                S.op("pe", lambda e: e.matmul(bk[:, 0:n], lhsT=wq[:, kc, hl * 128:(hl + 1) * 128], rhs=self.uT[:, kc, t0:t0 + n],
                                                              start=(kc == 0), stop=(kc == 5)),
                                     reads=[r_wq, ru], writes=[rb], same_ok=True)
                            S.op("dve", lambda e: e.tensor_tensor(out=qp[par][:, t0:t0 + n], in0=bk[:, 0:n], in1=qtab[:, t0:t0 + n], op=ALU.mult),
                                 reads=[rb, r_qtab], writes=[r_qp[par]])
                            bk, rb = self.bank()
                            for kc in range(2):
                                S.op("pe", lambda e: e.matmul(bk[0:64, 0:n], lhsT=wkv[:, kc, hl * 128:hl * 128 + 64], rhs=self.uT[:, 6 + kc, t0:t0 + n],
                                                              start=(kc == 0), stop=(kc == 1)),
                                     reads=[r_wq, ru], writes=[rb], same_ok=True)
                            S.op("dve", lambda e: e.tensor_copy(kp[par][0:64, t0:t0 + n], bk[0:64, 0:n]), reads=[rb], writes=[r_kp[par]])
                        for g0 in range(0, 18, 8):
                            ng = min(8, 18 - g0)
                            bk, rb = self.bank()
                            for jt in range(ng):
                                kt = g0 + jt
                                for kc in range(2):
                                    S.op("pe", lambda e: e.matmul(bk[:, jt * 64:(jt + 1) * 64], lhsT=self.uT[:, 6 + kc, kt * 128:(kt + 1) * 128],
                                                                  rhs=wkv[:, kc, hl * 128 + 64:hl * 128 + 128], start=(kc == 0), stop=(kc == 1)),
                                         reads=[r_wq] + self.r_u, writes=[rb], same_ok=True)
                            S.op("dve", lambda e: e.tensor_copy(vp[par][:, g0:g0 + ng, par * 64:par * 64 + 64],
                                                                bk[:, 0:ng * 64].rearrange("p (j d) -> p j d", d=64)), reads=[rb], writes=[r_vp[par]])
                    for ti, (t0, n, lc) in enumerate(TT):
                        kts = list(range(18)) if lc == 0 else [16, 17]
                        items = [(par, kt) for par in range(2) for kt in kts]
                        nb_ = attn_ctr[0] % 2
                        attn_ctr[0] += 1
                        num, r_num = self.banks[4 + 2 * nb_], self.bank_res[4 + 2 * nb_]
                        den, r_den = self.banks[5 + 2 * nb_], self.bank_res[5 + 2 * nb_]
                        sbanks = {}

                        def issue_score(ix):
                            par, kt = items[ix]
                            bk, rb = self.bank()
                            S.op("pe", lambda e: e.matmul(bk[:, 0:n], lhsT=kp[par][:, kt * 128:(kt + 1) * 128], rhs=qp[par][:, t0:t0 + n], start=True, stop=True),
                                 reads=[r_kp[par], r_qp[par]], writes=[rb], same_ok=True)
                            sbanks[ix] = (bk, rb)
                        for ix in range(min(2, len(items))):
                            issue_score(ix)
                        for ix, (par, kt) in enumerate(items):
                            bk, rb = sbanks.pop(ix)
                            p_, rp_ = pt[pti % 4], r_pt[pti % 4]
                            pti += 1
                            S.op("act", lambda e: e.activation(out=p_[:, 0:n], in_=bk[:, 0:n], func=AF.Exp, scale=SCALE), reads=[rb], writes=[rp_])
                            if ix + 2 < len(items):
                                issue_score(ix + 2)
                            first = (ix == 0)
                            last = (ix == len(items) - 1)
                            S.op("pe", lambda e: e.matmul(num[:, 0:n], lhsT=vp[par][:, kt, :], rhs=p_[:, 0:n], start=first, stop=last),
                                 reads=[r_vp[par], rp_], writes=[r_num], same_ok=True)
                            S.op("pe", lambda e: e.matmul(den[:, 0:n], lhsT=opad[:, par, :], rhs=p_[:, 0:n], start=first, stop=last),
                                 reads=[r_opad, rp_], writes=[r_den], same_ok=True)

                        def epilogue(ti=ti, t0=t0, n=n, lc=lc, num=num, den=den, r_num=r_num, r_den=r_den, pair=pair, k=attn_ctr[0]):
                            rd_, rrd_ = rden[k % 2], r_rden[k % 2]
                            S.op("dve", lambda e: e.reciprocal(out=rd_[:, 0:n], in_=den[:, 0:n]), reads=[r_den], writes=[rrd_])
                            o_, ro_ = opr[k % 2], r_opr[k % 2]
                            S.op("dve", lambda e: e.tensor_tensor(out=o_[:, 0:n], in0=num[:, 0:n], in1=rd_[:, 0:n], op=ALU.mult),
                                 reads=[r_num, rrd_], writes=[ro_])
                            for oc in range(8):
                                bk, rb = self.bank()
                                S.op("pe", lambda e: e.matmul(bk[:, 0:n], lhsT=wo[pair % 2][:, oc * 128:(oc + 1) * 128], rhs=o_[:, 0:n], start=True, stop=True),
                                     reads=[r_wo[pair % 2], ro_], writes=[rb], same_ok=True)
                                hz = self.hT[:, oc, t0:t0 + n]
                                S.op("dve", lambda e: e.scalar_tensor_tensor(out=hz, in0=bk[:, 0:n], scalar=self.mcol(i, 2, oc, lc), in1=hz,
                                                                             op0=ALU.mult, op1=ALU.add),
                                     reads=[rb, self.r_h[ti][oc], self.r_mod[i]], writes=[self.r_h[ti][oc]])
                        if pending[0] is not None:
                            pending[0]()
                        pending[0] = epilogue
                if pending[0] is not None:
                    pending[0]()
                S.barrier()
        self.nrr = 6
        self.bank_i = 0

    def diff(self, i):
        S = self.S
        SCALE = 64.0 ** -0.5
        lam_init = 0.8 - 0.6 * math.exp(-0.3 * i)
        self.nrr = 4
        self.bank_i = 0
        with ExitStack() as ph:
            qp = [self.sb([128, T], BF16, es=ph) for _ in range(2)]; r_qp = [Res("qp0"), Res("qp1")]
            kp = [self.sb([128, T], BF16, es=ph) for _ in range(2)]; r_kp = [Res("kp0"), Res("kp1")]
            vp = self.sb([128, 18, 128], BF16, es=ph); r_vp = Res("vp")
            qtab = self.sb([128, 512], F32, es=ph); r_qtab = Res("qtab")
            kta = self.sb([128, 512], F32, es=ph); r_kta = Res("kta")
            ktb = self.sb([128, 512], F32, es=ph); r_ktb = Res("ktb")
            t1 = self.sb([128, 512], F32, es=ph); r_t1 = Res("t1")
            t2 = self.sb([128, 512], F32, es=ph); r_t2 = Res("t2")
            t1s = [self.sb([128, 512], F32, es=ph) for _ in range(2)]; r_t1s = [Res("t1s0"), Res("t1s1")]
            dctr = [0]
            pend1 = [None]
            pt = [self.sb([128, 512], BF16, es=ph) for _ in range(4)]; r_pt = [Res() for _ in range(4)]
            rd = self.sb([128, 2, 512], F32, es=ph); r_rd0 = Res("rd0"); r_rd1 = Res("rd1")
            onb = self.sb([128, 128], BF16, es=ph); r_onb = Res("onb")
            on_ = [self.sb([128, 512], BF16, es=ph) for _ in range(2)]; r_on = [Res(), Res()]
            wo = [self.sb([128, D], BF16, es=ph) for _ in range(2)]; r_wo = [Res("wo0"), Res("wo1")]
            lam = self.sb([128, 4], F32, es=ph); r_lam = Res("lam")
            lv = self.sb([64, 4], F32, es=ph); r_lv = Res("lv")
            sub = self.sb([128, 1], F32, es=ph); r_sub = Res("sub")
            epsc = self.sb([128, 1], F32, es=ph); r_eps = Res("eps")
            S.op("pool", lambda e: e.memset(epsc[:], EPS), writes=[r_eps])
            S.op("pool", lambda e: e.memset(onb[:], 1.0), writes=[r_onb])
            S.dma("sp", [(lv[:], self.din["diff_lam"])], r_lv, writes=[r_lv])
            S.op("dve", lambda e: e.tensor_tensor(out=lv[:, 0:1], in0=lv[:, 0:1], in1=lv[:, 1:2], op=ALU.mult), reads=[r_lv], writes=[r_lv])
            S.op("dve", lambda e: e.tensor_tensor(out=lv[:, 1:2], in0=lv[:, 2:3], in1=lv[:, 3:4], op=ALU.mult), reads=[r_lv], writes=[r_lv])
            bk, rb = self.bank()
            S.op("pe", lambda e: e.matmul(bk[:, 0:2], lhsT=self.ones[0:64, :], rhs=lv[:, 0:2], start=True, stop=True),
                 reads=[r_lv, self.r_ident], writes=[rb], same_ok=True)
            S.op("act", lambda e: e.activation(out=lam[:, 0:2], in_=bk[:, 0:2], func=AF.Exp), reads=[rb], writes=[r_lam])
            S.op("dve", lambda e: e.scalar_tensor_tensor(out=lam[:, 2:3], in0=lam[:, 1:2], scalar=-lam_init, in1=lam[:, 0:1], op0=ALU.add, op1=ALU.subtract),
                 reads=[r_lam], writes=[r_lam])
            self.load_cols(self.din["diff_subln"], 1, sub[:, 0:1], r_sub)
            S.op("dve", lambda e: e.tensor_scalar(out=sub[:], in0=sub[:], scalar1=1.0 - lam_init, scalar2=None, op0=ALU.mult), reads=[r_sub], writes=[r_sub])
            nums = [(self.banks[4], self.bank_res[4]), (self.banks[5], self.bank_res[5])]
            dens = [(self.banks[6], self.bank_res[6]), (self.banks[7], self.bank_res[7])]
            pti = 0
            pti = 0
            for h in range(8):
                S.dma("pool", [(wo[h % 2][:], self.din["diff_w_out"][h * 128:(h + 1) * 128, :])], r_wo[h % 2], writes=[r_wo[h % 2]])
                wv = None
                for m in range(2):
                    mi = h * 2 + m
                    w_, r_w = self.wnext()
                    wx = w_[:, 0:3072].rearrange("p (k n) -> p k n", k=8)
                    prs = [(wx, self.din["diff_wx"][:, mi * 384:(mi + 1) * 384].rearrange("(k p) n -> p k n", p=128))]
                    if m == 0:
                        wv = w_[:, 3072:4096].rearrange("p (k n) -> p k n", k=8)
                        r_wv = r_w
                        prs.append((wv, self.din["diff_wv"][:, h * 128:(h + 1) * 128].rearrange("(k p) n -> p k n", p=128)))
                    S.dma("pool", prs, r_w, writes=[r_w])
                    for ti, (t0, n, lc) in enumerate(TT):
                        ru = self.r_u[ti]
                        S.dma("sp", [(qtab[:, 0:n], self.din["diff_qtab"][:, t0:t0 + n])], r_qtab, writes=[r_qtab])
                        S.dma("sp", [(kta[:, 0:n], self.din["diff_kta"][:, t0:t0 + n])], r_kta, writes=[r_kta])
                        S.dma("sp", [(ktb[:, 0:n], self.din["diff_ktb"][:, t0:t0 + n])], r_ktb, writes=[r_ktb])
                        bq, rbq = self.bank()
                        for kc in range(8):
                            S.op("pe", lambda e: e.matmul(bq[:, 0:n], lhsT=wx[:, kc, 0:128], rhs=self.uT[:, kc, t0:t0 + n], start=(kc == 0), stop=(kc == 7)),
                                 reads=[r_w, ru], writes=[rbq], same_ok=True)
                        S.op("dve", lambda e: e.tensor_tensor(out=qp[m][:, t0:t0 + n], in0=bq[:, 0:n], in1=qtab[:, 0:n], op=ALU.mult),
                             reads=[rbq, r_qtab], writes=[r_qp[m]])
                        bA, rbA = self.bank()
                        for kc in range(8):
                            S.op("pe", lambda e: e.matmul(bA[:, 0:n], lhsT=wx[:, kc, 128:256], rhs=self.uT[:, kc, t0:t0 + n], start=(kc == 0), stop=(kc == 7)),
                                 reads=[r_w, ru], writes=[rbA], same_ok=True)
                        bB, rbB = self.bank()
                        for kc in range(8):
                            S.op("pe", lambda e: e.matmul(bB[:, 0:n], lhsT=wx[:, kc, 256:384], rhs=self.uT[:, kc, t0:t0 + n], start=(kc == 0), stop=(kc == 7)),
                                 reads=[r_w, ru], writes=[rbB], same_ok=True)
                        S.op("dve", lambda e: e.tensor_tensor(out=t1[:, 0:n], in0=bA[:, 0:n], in1=kta[:, 0:n], op=ALU.mult), reads=[rbA, r_kta], writes=[r_t1])
                        S.op("dve", lambda e: e.tensor_tensor(out=t2[:, 0:n], in0=bB[:, 0:n], in1=ktb[:, 0:n], op=ALU.mult), reads=[rbB, r_ktb], writes=[r_t2])
                        S.op("pool", lambda e: e.tensor_tensor(out=kp[m][:, t0:t0 + n], in0=t1[:, 0:n], in1=t2[:, 0:n], op=ALU.add),
                             reads=[r_t1, r_t2], writes=[r_kp[m]])
                for g0 in range(0, 18, 4):
                    ng = min(4, 18 - g0)
                    bk, rb = self.bank()
                    for jt in range(ng):
                        kt = g0 + jt
                        for kc in range(8):
                            S.op("pe", lambda e: e.matmul(bk[:, jt * 128:(jt + 1) * 128], lhsT=self.uT[:, kc, kt * 128:(kt + 1) * 128],
                                                          rhs=wv[:, kc, :], start=(kc == 0), stop=(kc == 7)),
                                 reads=[r_wv] + self.r_u, writes=[rb], same_ok=True)
                    S.op("act", lambda e: e.copy(vp[:, g0:g0 + ng, :], bk[:, 0:ng * 128].rearrange("p (j d) -> p j d", d=128)), reads=[rb], writes=[r_vp])
                for ti, (t0, n, lc) in enumerate(TT):
                    kts = list(range(18)) if lc == 0 else [16, 17]
                    k_ = dctr[0]
                    dctr[0] += 1
                    t1_, rt1_ = t1s[k_ % 2], r_t1s[k_ % 2]

                    def attend(m):
                        nonlocal pti
                        num, r_num = nums[m]
                        den, r_den = dens[m]
                        sbanks = {}

                        def issue_score(ix):
                            kt = kts[ix]
                            bk, rb = self.bank()
                            S.op("pe", lambda e: e.matmul(bk[:, 0:n], lhsT=kp[m][:, kt * 128:(kt + 1) * 128], rhs=qp[m][:, t0:t0 + n], start=True, stop=True),
                                 reads=[r_kp[m], r_qp[m]], writes=[rb], same_ok=True)
                            sbanks[ix] = (bk, rb)
                        for ix in range(min(2, len(kts))):
                            issue_score(ix)
                        for ix, kt in enumerate(kts):
                            bk, rb = sbanks.pop(ix)
                            p_, rp_ = pt[pti % 4], r_pt[pti % 4]
                            pti += 1
                            S.op("act", lambda e: e.activation(out=p_[:, 0:n], in_=bk[:, 0:n], func=AF.Exp, scale=SCALE), reads=[rb], writes=[rp_])
                            if ix + 2 < len(kts):
                                issue_score(ix + 2)
                            first, last = (ix == 0), (ix == len(kts) - 1)
                            S.op("pe", lambda e: e.matmul(num[:, 0:n], lhsT=vp[:, kt, :], rhs=p_[:, 0:n], start=first, stop=last),
                                 reads=[r_vp, rp_], writes=[r_num], same_ok=True)
                            S.op("pe", lambda e: e.matmul(den[:, 0:n], lhsT=onb[:], rhs=p_[:, 0:n], start=first, stop=last),
                                 reads=[r_onb, rp_], writes=[r_den], same_ok=True)

                    def ep0(n=n, t1_=t1_, rt1_=rt1_):
                        S.op("dve", lambda e: e.reciprocal(out=rd[:, 0, 0:n], in_=dens[0][0][:, 0:n]), reads=[dens[0][1]], writes=[r_rd0])
                        S.op("dve", lambda e: e.tensor_tensor(out=t1_[:, 0:n], in0=nums[0][0][:, 0:n], in1=rd[:, 0, 0:n], op=ALU.mult),
                             reads=[nums[0][1], r_rd0], writes=[rt1_])

                    def ep1(ti=ti, t0=t0, n=n, lc=lc, t1_=t1_, rt1_=rt1_, h=h, k_=k_):
                        S.op("dve", lambda e: e.reciprocal(out=rd[:, 1, 0:n], in_=dens[1][0][:, 0:n]), reads=[dens[1][1]], writes=[r_rd1])
                        S.op("dve", lambda e: e.scalar_tensor_tensor(out=t2[:, 0:n], in0=nums[1][0][:, 0:n], scalar=lam[:, 2:3], in1=rd[:, 1, 0:n],
                                                                     op0=ALU.mult, op1=ALU.mult), reads=[nums[1][1], r_rd1, r_lam], writes=[r_t2])
                        S.op("dve", lambda e: e.tensor_tensor(out=t1_[:, 0:n], in0=t1_[:, 0:n], in1=t2[:, 0:n], op=ALU.add), reads=[rt1_, r_t2], writes=[rt1_])
                        S.op("dve", lambda e: e.tensor_tensor(out=t2[:, 0:n], in0=t1_[:, 0:n], in1=t1_[:, 0:n], op=ALU.mult), reads=[rt1_], writes=[r_t2])
                        bk, rb = self.bank()
                        S.op("pe", lambda e: e.matmul(bk[:, 0:n], lhsT=self.ones[:], rhs=t2[:, 0:n], start=True, stop=True),
                             reads=[r_t2, self.r_ident], writes=[rb], same_ok=True)
                        S.op("act", lambda e: e.activation(out=t2[:, 0:n], in_=bk[:, 0:n], func=AF.Sqrt, bias=epsc[:, 0:1], scale=1.0 / 128.0),
                             reads=[rb, r_eps], writes=[r_t2])
                        S.op("dve", lambda e: e.reciprocal(out=t2[:, 0:n], in_=t2[:, 0:n]), reads=[r_t2], writes=[r_t2])
                        o_, ro_ = on_[k_ % 2], r_on[k_ % 2]
                        S.op("dve", lambda e: e.scalar_tensor_tensor(out=o_[:, 0:n], in0=t1_[:, 0:n], scalar=sub[:, 0:1], in1=t2[:, 0:n],
                                                                     op0=ALU.mult, op1=ALU.mult), reads=[rt1_, r_t2, r_sub], writes=[ro_])
                        for oc in range(8):
                            bk, rb = self.bank()
                            S.op("pe", lambda e: e.matmul(bk[:, 0:n], lhsT=wo[h % 2][:, oc * 128:(oc + 1) * 128], rhs=o_[:, 0:n], start=True, stop=True),
                                 reads=[r_wo[h % 2], ro_], writes=[rb], same_ok=True)
                            hz = self.hT[:, oc, t0:t0 + n]
                            S.op("dve", lambda e: e.scalar_tensor_tensor(out=hz, in0=bk[:, 0:n], scalar=self.mcol(i, 2, oc, lc), in1=hz,
                                                                         op0=ALU.mult, op1=ALU.add),
                                 reads=[rb, self.r_h[ti][oc], self.r_mod[i]], writes=[self.r_h[ti][oc]])
                    attend(0)
                    if pend1[0] is not None:
                        pend1[0]()
                    attend(1)
                    ep0()
                    pend1[0] = ep1
            if pend1[0] is not None:
                pend1[0]()
            S.barrier()
        self.nrr = 6
        self.bank_i = 0

    def gla(self, i):
        S = self.S
        QS = 128.0 ** -0.5
        win = self.din["gla_w_in"]
        with ExitStack() as ph:
            arr = [[self.sb([128, T], BF16, es=ph) for _ in range(3)] for _ in range(2)]
            r_arr = [[Res() for _ in range(3)] for _ in range(2)]
            vh = self.sb([128, 18, 256], BF16, es=ph); r_vh = Res("vh")
            oacc = self.sb([128, 2, T], BF16, es=ph); r_oacc = [Res(f"oacc{c}") for c in range(18)]
            rT = self.sb([32, T], BF16, es=ph); r_rT = Res("rT")
            gw = self.sb([32, 2, 512], BF16, es=ph); r_gw = Res("gw")
            gb = self.sb([128, 2, 4], F32, es=ph); r_gb = Res("gb")
            ng = self.sb([128, 2], F32, es=ph); r_ng = Res("ng")
            dec = self.sb([128, 2, 18], F32, es=ph); r_dec = Res("dec")
            cmask = self.sb([128, 512], F32, es=ph); r_cm = Res("cmask")
            msk = [self.sb([128, 128], F32, es=ph) for _ in range(2)]; r_msk = Res("msk")
            bA = self.sb([128, 512], F32, es=ph); r_bA = Res("bA")
            bB = self.sb([128, 512], F32, es=ph); r_bB = Res("bB")
            bC = self.sb([128, 512], F32, es=ph); r_bC = Res("bC")
            bD = self.sb([128, 512], F32, es=ph); r_bD = Res("bD")
            Sf = [self.sb([128, 256], F32, es=ph) for _ in range(2)]; r_Sf = [Res("Sf0"), Res("Sf1")]
            Sb = [self.sb([128, 256], BF16, es=ph) for _ in range(2)]; r_Sb = [Res("Sb0"), Res("Sb1")]
            Am = [self.sb([128, 128], BF16, es=ph) for _ in range(2)]; r_Am = [Res(), Res()]
            keT = [self.sb([128, 128], BF16, es=ph) for _ in range(2)]; r_keT = [Res(), Res()]
            ogn = [self.sb([128, 2, 512], BF16, es=ph) for _ in range(1)]; r_ogn = [Res()]
            epsc = self.sb([128, 1], F32, es=ph); r_eps = Res("eps")
            S.op("pool", lambda e: e.memset(epsc[:], EPS), writes=[r_eps])
            S.op("pool", lambda e: e.memset(cmask[:], 1.0), writes=[r_cm])
            for c in range(4):
                S.op("pool", lambda e: e.memset(cmask[:, c * 128:c * 128 + 1], 0.0), reads=[r_cm], writes=[r_cm])
            for d in range(2):
                S.op("pool", lambda e: e.memset(msk[d][:], 1.0), reads=[r_msk], writes=[r_msk])
                cm, pat = ((-1, [[1, 128]]) if d == 0 else (1, [[-1, 128]]))
                S.op("pool", lambda e: e.affine_select(out=msk[d][:], in_=msk[d][:], compare_op=ALU.is_ge, fill=0.0, base=0,
                                                      pattern=pat, channel_multiplier=cm), reads=[r_msk], writes=[r_msk])
            S.op("pool", lambda e: e.memset(gw[:], 0.0), writes=[r_gw])
            S.dma("pool", [(gw[0:16, 0, :], self.din["gla_gw"][0]), (gw[16:32, 1, :], self.din["gla_gw"][1])], r_gw, reads=[r_gw], writes=[r_gw])
            for d in range(2):
                self.load_cols(self.din["gla_gb"][d], 4, gb[:, d, :], r_gb)
            S.op("dve", lambda e: e.tensor_scalar(out=gb[:], in0=gb[:], scalar1=-1.0, scalar2=None, op0=ALU.mult), reads=[r_gb], writes=[r_gb])
            self.load_cols(self.din["gla_norm"], 2, ng[:, 0:2], r_ng)
            w_, r_w = self.wnext()
            wr = w_[:, 0:256].rearrange("p (k n) -> p k n", k=8)
            S.dma("pool", [(wr, win[:, 3072:3104].rearrange("(k p) n -> p k n", p=128))], r_w, writes=[r_w])
            for ti, (t0, n, lc) in enumerate(TT):
                bk, rb = self.bank()
                for kc in range(8):
                    S.op("pe", lambda e: e.matmul(bk[0:32, 0:n], lhsT=wr[:, kc, :], rhs=self.uT[:, kc, t0:t0 + n], start=(kc == 0), stop=(kc == 7)),
                         reads=[r_w, self.r_u[ti]], writes=[rb], same_ok=True)
                S.op("act", lambda e: e.copy(rT[:, t0:t0 + n], bk[0:32, 0:n]), reads=[rb], writes=[r_rT])

            for h in range(4):
                wA_, r_wA = self.wnext()
                wqk = wA_[:, 0:2048].rearrange("p (k n) -> p k n", k=8)
                S.dma("pool", [(wqk[:, :, 0:128], win[:, h * 128:(h + 1) * 128].rearrange("(k p) n -> p k n", p=128)),
                               (wqk[:, :, 128:256], win[:, 512 + h * 128:512 + (h + 1) * 128].rearrange("(k p) n -> p k n", p=128))],
                      r_wA, writes=[r_wA])
                wB_, r_wB = self.wnext()
                wv = wB_[:, 0:2048].rearrange("p (k n) -> p k n", k=8)
                wg = wB_[:, 2048:4096].rearrange("p (k n) -> p k n", k=8)
                S.dma("pool", [(wv, win[:, 1024 + h * 256:1024 + (h + 1) * 256].rearrange("(k p) n -> p k n", p=128)),
                               (wg, win[:, 2048 + h * 256:2048 + (h + 1) * 256].rearrange("(k p) n -> p k n", p=128))],
                      r_wB, writes=[r_wB])
                wC_, r_wC = self.wnext()
                wo = wC_[:, 0:2048].rearrange("p (k n) -> p k n", k=2)
                S.dma("pool", [(wo, self.din["gla_w_out"][h * 256:(h + 1) * 256, :].rearrange("(k p) n -> p k n", p=128))], r_wC, writes=[r_wC])
                for g0 in range(0, 18, 2):
                    bk, rb = self.bank()
                    for jt in range(2):
                        kt = g0 + jt
                        for kc in range(8):
                            S.op("pe", lambda e: e.matmul(bk[:, jt * 256:(jt + 1) * 256], lhsT=self.uT[:, kc, kt * 128:(kt + 1) * 128], rhs=wv[:, kc, :],
                                                          start=(kc == 0), stop=(kc == 7)), reads=[r_wB] + self.r_u, writes=[rb], same_ok=True)
                    S.op("act", lambda e: e.copy(vh[:, g0:g0 + 2, :], bk[:, 0:512].rearrange("p (j d) -> p j d", d=256)), reads=[rb], writes=[r_vh])
                for ti, (t0, n, lc) in enumerate(TT):
                    ru = self.r_u[ti]
                    nch = n // 128
                    bq, rbq = self.bank()
                    for kc in range(8):
                        S.op("pe", lambda e: e.matmul(bq[:, 0:n], lhsT=wqk[:, kc, 0:128], rhs=self.uT[:, kc, t0:t0 + n], start=(kc == 0), stop=(kc == 7)),
                             reads=[r_wA, ru], writes=[rbq], same_ok=True)
                    bkk, rbk = self.bank()
                    for kc in range(8):
                        S.op("pe", lambda e: e.matmul(bkk[:, 0:n], lhsT=wqk[:, kc, 128:256], rhs=self.uT[:, kc, t0:t0 + n], start=(kc == 0), stop=(kc == 7)),
                             reads=[r_wA, ru], writes=[rbk], same_ok=True)
                    for d in range(2):
                        bx, rbx = self.bank()
                        S.op("pe", lambda e: e.matmul(bx[:, 0:n], lhsT=gw[:, d, h * 128:(h + 1) * 128], rhs=rT[:, t0:t0 + n], start=True, stop=True),
                             reads=[r_gw, r_rT], writes=[rbx], same_ok=True)
                        S.op("act", lambda e: e.activation(out=bA[:, 0:n], in_=bx[:, 0:n], func=AF.Exp, bias=gb[:, d, h:h + 1], scale=-1.0),
                             reads=[rbx, r_gb], writes=[r_bA])
                        S.op("act", lambda e: e.activation(out=bA[:, 0:n], in_=bA[:, 0:n], func=AF.Ln, bias=1.0, scale=1.0), reads=[r_bA], writes=[r_bA])
                        S.op("dve", lambda e: e.tensor_tensor_scan(out=bB[:, 0:n], data0=cmask[:, 0:n], data1=bA[:, 0:n], initial=0.0,
                                                                   op0=ALU.mult, op1=ALU.add), reads=[r_cm, r_bA], writes=[r_bB])
                        for c in range(nch):
                            gc = t0 // 128 + c
                            ce = c * 128 + 127
                            S.op("act", lambda e: e.activation(out=dec[:, d, gc:gc + 1], in_=bB[:, ce:ce + 1], func=AF.Exp, scale=-1.0 / 16), reads=[r_bB], writes=[r_dec])
                            S.op("dve", lambda e: e.tensor_scalar(out=bD[:, c * 128:(c + 1) * 128], in0=bB[:, c * 128:(c + 1) * 128], scalar1=bB[:, ce:ce + 1],
                                                                  scalar2=None, op0=ALU.subtract), reads=[r_bB], writes=[r_bD])
                        if d == 0:
                            S.op("act", lambda e: e.activation(out=bC[:, 0:n], in_=bB[:, 0:n], func=AF.Exp, scale=-1.0 / 16), reads=[r_bB], writes=[r_bC])
                            S.op("dve", lambda e: e.scalar_tensor_tensor(out=arr[d][0][:, t0:t0 + n], in0=bq[:, 0:n], scalar=QS, in1=bC[:, 0:n], op0=ALU.mult, op1=ALU.mult),
                                 reads=[rbq, r_bC], writes=[r_arr[d][0]])
                            S.op("act", lambda e: e.activation(out=bC[:, 0:n], in_=bB[:, 0:n], func=AF.Exp, scale=1.0 / 16), reads=[r_bB], writes=[r_bC])
                            S.op("dve", lambda e: e.tensor_tensor(out=arr[d][1][:, t0:t0 + n], in0=bkk[:, 0:n], in1=bC[:, 0:n], op=ALU.mult),
                                 reads=[rbk, r_bC], writes=[r_arr[d][1]])
                            S.op("act", lambda e: e.activation(out=bC[:, 0:n], in_=bD[:, 0:n], func=AF.Exp, scale=1.0 / 16), reads=[r_bD], writes=[r_bC])
                            S.op("dve", lambda e: e.tensor_tensor(out=arr[d][2][:, t0:t0 + n], in0=bkk[:, 0:n], in1=bC[:, 0:n], op=ALU.mult),
                                 reads=[rbk, r_bC], writes=[r_arr[d][2]])
                        else:
                            S.op("dve", lambda e: e.tensor_tensor(out=bD[:, 0:n], in0=bA[:, 0:n], in1=bD[:, 0:n], op=ALU.subtract), reads=[r_bA, r_bD], writes=[r_bD])
                            S.op("act", lambda e: e.activation(out=bC[:, 0:n], in_=bD[:, 0:n], func=AF.Exp, scale=-1.0 / 16), reads=[r_bD], writes=[r_bC])
                            S.op("dve", lambda e: e.scalar_tensor_tensor(out=arr[d][0][:, t0:t0 + n], in0=bq[:, 0:n], scalar=QS, in1=bC[:, 0:n], op0=ALU.mult, op1=ALU.mult),
                                 reads=[rbq, r_bC], writes=[r_arr[d][0]])
                            S.op("act", lambda e: e.activation(out=bC[:, 0:n], in_=bD[:, 0:n], func=AF.Exp, scale=1.0 / 16), reads=[r_bD], writes=[r_bC])
                            S.op("dve", lambda e: e.tensor_tensor(out=arr[d][1][:, t0:t0 + n], in0=bkk[:, 0:n], in1=bC[:, 0:n], op=ALU.mult),
                                 reads=[rbk, r_bC], writes=[r_arr[d][1]])
                            S.op("dve", lambda e: e.tensor_tensor(out=bD[:, 0:n], in0=bA[:, 0:n], in1=bB[:, 0:n], op=ALU.subtract), reads=[r_bA, r_bB, r_bC], writes=[r_bD])
                            S.op("act", lambda e: e.activation(out=bC[:, 0:n], in_=bD[:, 0:n], func=AF.Exp, scale=1.0 / 16), reads=[r_bD], writes=[r_bC])
                            S.op("dve", lambda e: e.tensor_tensor(out=arr[d][2][:, t0:t0 + n], in0=bkk[:, 0:n], in1=bC[:, 0:n], op=ALU.mult),
                                 reads=[rbk, r_bC], writes=[r_arr[d][2]])
                for d in range(2):
                    S.op("pool", lambda e: e.memset(Sf[d][:], 0.0), reads=[r_Sf[d]], writes=[r_Sf[d]])
                    S.op("pool", lambda e: e.memset(Sb[d][:], 0.0), reads=[r_Sb[d]], writes=[r_Sb[d]])
                order = [[16, 17] + list(range(16)), [17, 16] + list(range(15, -1, -1))]
                written = set()
                for step in range(18):
                    for d in range(2):
                        c = order[d][step]
                        cs = slice(c * 128, (c + 1) * 128)
                        qd, ki, ke = arr[d]
                        ba, rba = self.bank()
                        S.op("pe", lambda e: e.matmul(ba[:, 0:128], lhsT=ki[:, cs], rhs=qd[:, cs], start=True, stop=True),
                             reads=[r_arr[d][1], r_arr[d][0]], writes=[rba], same_ok=True)
                        bke, rbke = self.bank()
                        S.op("pe", lambda e: e.matmul(bke[:, 0:128], lhsT=ke[:, cs], rhs=self.identb[:], start=True, stop=True),
                             reads=[r_arr[d][2], self.r_ident], writes=[rbke], same_ok=True)
                        S.op("dve", lambda e: e.tensor_tensor(out=Am[d][:], in0=ba[:, 0:128], in1=msk[d][:], op=ALU.mult), reads=[rba, r_msk], writes=[r_Am[d]])
                        S.op("act", lambda e: e.copy(keT[d][:], bke[:, 0:128]), reads=[rbke], writes=[r_keT[d]])
                        bo, rbo = self.bank()
                        for j in range(2):
                            S.op("pe", lambda e: e.matmul(bo[:, j * 128:(j + 1) * 128], lhsT=Sb[d][:, j * 128:(j + 1) * 128], rhs=qd[:, cs], start=True, stop=False),
                                 reads=[r_Sb[d], r_arr[d][0]], writes=[rbo], same_ok=True)
                            S.op("pe", lambda e: e.matmul(bo[:, j * 128:(j + 1) * 128], lhsT=vh[:, c, j * 128:(j + 1) * 128], rhs=Am[d][:], start=False, stop=True),
                                 reads=[r_vh, r_Am[d]], writes=[rbo], same_ok=True)
                        ov = oacc[:, :, cs]
                        pv = bo[:, 0:256].rearrange("p (j c) -> p j c", j=2)
                        if c not in written:
                            written.add(c)
                            S.op("act", lambda e: e.copy(ov, pv), reads=[rbo], writes=[r_oacc[c]])
                        else:
                            S.op("dve", lambda e: e.tensor_tensor(out=ov, in0=pv, in1=ov, op=ALU.add), reads=[rbo, r_oacc[c]], writes=[r_oacc[c]])
                        bs, rbs = self.bank()
                        S.op("pe", lambda e: e.matmul(bs[:, 0:256], lhsT=keT[d][:], rhs=vh[:, c, :], start=True, stop=True),
                             reads=[r_keT[d], r_vh], writes=[rbs], same_ok=True)
                        S.op("dve", lambda e: e.scalar_tensor_tensor(out=Sf[d][:], in0=Sf[d][:], scalar=dec[:, d, c:c + 1], in1=bs[:, 0:256], op0=ALU.mult, op1=ALU.add),
                             reads=[r_Sf[d], r_dec, rbs], writes=[r_Sf[d]])
                        S.op("act", lambda e: e.copy(Sb[d][:], Sf[d][:]), reads=[r_Sf[d]], writes=[r_Sb[d]])
                for ti, (t0, n, lc) in enumerate(TT):
                    ru = self.r_u[ti]
                    roa = r_oacc[t0 // 128:(t0 + n) // 128]
                    bss = self.banks[6]; rbss = self.bank_res[6]
                    for j in range(2):
                        S.op("act", lambda e: e.activation(out=bA[:, 0:n], in_=oacc[:, j, t0:t0 + n], func=AF.Square), reads=roa + [r_bA], writes=[r_bA])
                        S.op("pe", lambda e: e.matmul(bss[:, 0:n], lhsT=self.ones[:], rhs=bA[:, 0:n], start=(j == 0), stop=(j == 1)),
                             reads=[r_bA, self.r_ident], writes=[rbss], same_ok=True)
                    S.op("act", lambda e: e.activation(out=bB[:, 0:n], in_=bss[:, 0:n], func=AF.Sqrt, bias=epsc[:, 0:1], scale=1.0 / 256.0), reads=[rbss, r_eps], writes=[r_bB])
                    S.op("dve", lambda e: e.reciprocal(out=bB[:, 0:n], in_=bB[:, 0:n]), reads=[r_bB], writes=[r_bB])
                    og, rog = ogn[0], r_ogn[0]
                    for j in range(2):
                        bg, rbg = self.bank()
                        for kc in range(8):
                            S.op("pe", lambda e: e.matmul(bg[:, 0:n], lhsT=wg[:, kc, j * 128:(j + 1) * 128], rhs=self.uT[:, kc, t0:t0 + n], start=(kc == 0), stop=(kc == 7)),
                                 reads=[r_wB, ru], writes=[rbg], same_ok=True)
                        S.op("act", lambda e: e.activation(out=bC[:, 0:n], in_=bg[:, 0:n], func=AF.Silu), reads=[rbg], writes=[r_bC])
                        S.op("dve", lambda e: e.scalar_tensor_tensor(out=bD[:, 0:n], in0=oacc[:, j, t0:t0 + n], scalar=ng[:, j:j + 1], in1=bB[:, 0:n], op0=ALU.mult, op1=ALU.mult),
                             reads=roa + [r_ng, r_bB], writes=[r_bD])
                        S.op("dve", lambda e: e.tensor_tensor(out=og[:, j, 0:n], in0=bD[:, 0:n], in1=bC[:, 0:n], op=ALU.mult), reads=[r_bD, r_bC], writes=[rog])
                    for oc in range(8):
                        bk, rb = self.bank()
                        for j in range(2):
                            S.op("pe", lambda e: e.matmul(bk[:, 0:n], lhsT=wo[:, j, oc * 128:(oc + 1) * 128], rhs=og[:, j, 0:n], start=(j == 0), stop=(j == 1)),
                                 reads=[r_wC, rog], writes=[rb], same_ok=True)
                        hz = self.hT[:, oc, t0:t0 + n]
                        S.op("dve", lambda e: e.scalar_tensor_tensor(out=hz, in0=bk[:, 0:n], scalar=self.mcol(i, 2, oc, lc), in1=hz, op0=ALU.mult, op1=ALU.add),
                             reads=[rb, self.r_h[ti][oc], self.r_mod[i]], writes=[self.r_h[ti][oc]])
            S.barrier()

    def gdn(self, i):
        S = self.S
        R32 = mybir.dt.float32r
        win = self.din["gdn_w_in"]
        rr = lambda ap: ap.bitcast(R32)
        with ExitStack() as ph:
            sbp = lambda shape, dt: self.sb(shape, dt, es=ph)
            cw = sbp([128, 5, 32], F32); r_cw = Res("cw")
            for j in range(5):
                self.load_cols(self.din["gdn_conv"][j], 32, cw[:, j, :], r_cw)
            r_msk = Res("gmsk")
            inclT = [sbp([128, 128], F32) for _ in range(2)]
            strict2 = sbp([128, 2, 128], BF16)
            inclT2 = sbp([128, 2, 128], BF16)
            bd16 = sbp([128, 128], BF16); off16 = sbp([128, 128], BF16); off32 = sbp([128, 128], BF16); off64 = sbp([128, 128], BF16)
            with ExitStack() as mk:
                strict = [self.sb([128, 128], F32, es=mk) for _ in range(2)]
                specs = [(strict[0], ALU.is_gt, 1, [[-1, 128]]), (strict[1], ALU.is_gt, -1, [[1, 128]]),
                         (inclT[0], ALU.is_ge, -1, [[1, 128]]), (inclT[1], ALU.is_ge, 1, [[-1, 128]])]
                for (t_, cmp_, cm, pat) in specs:
                    S.op("pool", lambda e: e.memset(t_[:], 1.0), reads=[r_msk], writes=[r_msk])
                    S.op("pool", lambda e: e.affine_select(out=t_[:], in_=t_[:], compare_op=cmp_, fill=0.0, base=0, pattern=pat, channel_multiplier=cm),
                         reads=[r_msk], writes=[r_msk])
                for d in range(2):
                    S.op("dve", lambda e: e.tensor_copy(strict2[:, d, :], strict[d][:]), reads=[r_msk], writes=[r_msk])
                    S.op("dve", lambda e: e.tensor_copy(inclT2[:, d, :], inclT[d][:]), reads=[r_msk], writes=[r_msk])
                bsel = self.sb([8, 128], F32, es=mk)
                bd = {}
                for b_ in (16, 32, 64):
                    nb = 128 // b_
                    S.op("pool", lambda e: e.memset(bsel[:], 1.0), reads=[r_msk], writes=[r_msk])
                    S.op("pool", lambda e: e.affine_select(out=bsel[:], in_=bsel[:], compare_op=ALU.is_ge, fill=0.0, base=0, pattern=[[1, 128]], channel_multiplier=-b_),
                         reads=[r_msk], writes=[r_msk])
                    S.op("pool", lambda e: e.affine_select(out=bsel[:], in_=bsel[:], compare_op=ALU.is_ge, fill=0.0, base=b_ - 1, pattern=[[-1, 128]], channel_multiplier=b_),
                         reads=[r_msk], writes=[r_msk])
                    bk, rb = self.bank()
                    S.op("pe", lambda e: e.matmul(bk[:, 0:128], lhsT=bsel[0:nb, :], rhs=bsel[0:nb, :], start=True, stop=True), reads=[r_msk], writes=[rb], same_ok=True)
                    bd[b_] = self.sb([128, 128], F32, es=mk)
                    S.op("dve", lambda e: e.tensor_copy(bd[b_][:], bk[:, 0:128]), reads=[rb, r_msk], writes=[r_msk])
                S.op("dve", lambda e: e.tensor_copy(bd16[:], bd[16][:]), reads=[r_msk], writes=[r_msk])
                S.op("dve", lambda e: e.tensor_tensor(out=off16[:], in0=bd[32][:], in1=bd[16][:], op=ALU.subtract), reads=[r_msk], writes=[r_msk])
                S.op("dve", lambda e: e.tensor_tensor(out=off32[:], in0=bd[64][:], in1=bd[32][:], op=ALU.subtract), reads=[r_msk], writes=[r_msk])
                S.op("dve", lambda e: e.tensor_scalar(out=off64[:], in0=bd[64][:], scalar1=-1.0, scalar2=1.0, op0=ALU.mult, op1=ALU.add), reads=[r_msk], writes=[r_msk])
                S.barrier()
            b4 = lambda m_: m_[:].unsqueeze(1).broadcast_to([128, 4, 128])
            hc = sbp([128, 64], F32); r_hc = Res("hc")
            S.dma("sp", [(hc[:], self.din["gdn_hc"].partition_broadcast(128))], r_hc, writes=[r_hc])
            S.op("act", lambda e: e.activation(out=hc[:, 0:32], in_=hc[:, 0:32], func=AF.Exp), reads=[r_hc], writes=[r_hc])
            S.op("dve", lambda e: e.tensor_scalar(out=hc[:, 0:32], in0=hc[:, 0:32], scalar1=-1.0, scalar2=None, op0=ALU.mult), reads=[r_hc], writes=[r_hc])
            ngrep = sbp([128, 128], F32); r_ngr = Res("ngrep")
            S.dma("sp", [(ngrep[:], self.din["gdn_norm"].partition_broadcast(128))], r_ngr, writes=[r_ngr])
            epsc = sbp([128, 1], F32); r_eps = Res("eps")
            S.op("pool", lambda e: e.memset(epsc[:], EPS), writes=[r_eps])
            qkT = sbp([128, 2, T], BF16); r_qkT = Res("qkT")
            kn = sbp([128, 18, 128], BF16); r_kn = Res("kn")
            vt = sbp([128, 18, 256], BF16); r_vt = Res("vt")
            oacc = sbp([128, 18, 256], BF16); r_oacc = [Res(f"go{c}") for c in range(18)]
            sc_names = ("negb", "gc", "e", "ecoef", "negbe", "dl", "g", "ngc")
            sc = {n_: sbp([128, 18, 4], F32) for n_ in sc_names}
            r_sc = Res("gsc")

            for kh in range(8):
                wA_, r_wA = self.wnext()
                wA = wA_[:].rearrange("p (k n) -> p k n", k=8)
                S.dma("pool", [(wA[:, :, 0:128], win[:, kh * 128:(kh + 1) * 128].rearrange("(k p) n -> p k n", p=128)),
                               (wA[:, :, 128:256], win[:, 1024 + kh * 128:1024 + (kh + 1) * 128].rearrange("(k p) n -> p k n", p=128)),
                               (wA[:, :, 256:512], win[:, 2048 + kh * 256:2048 + (kh + 1) * 256].rearrange("(k p) n -> p k n", p=128))],
                      r_wA, writes=[r_wA])
                wB_, r_wB = self.wnext()
                wz = wB_[:, 0:2048].rearrange("p (k n) -> p k n", k=8)
                wgt = wB_[:, 2048:2112].rearrange("p (k n) -> p k n", k=8)
                S.dma("pool", [(wz, win[:, 4096 + kh * 256:4096 + (kh + 1) * 256].rearrange("(k p) n -> p k n", p=128)),
                               (wgt, self.din["gdn_wg"][:, kh * 8:(kh + 1) * 8].rearrange("(k p) n -> p k n", p=128))], r_wB, writes=[r_wB])
                wC_, r_wC = self.wnext()
                wo = wC_[:, 0:2048].rearrange("p (k n) -> p k n", k=2)
                S.dma("pool", [(wo, self.din["gdn_w_out"][kh * 256:(kh + 1) * 256, :].rearrange("(k p) n -> p k n", p=128))], r_wC, writes=[r_wC])
                with ExitStack() as p1:
                    xpad = self.sb([128, 4, 2312], BF16, es=p1); r_xp = Res("xpad")
                    dg = self.sb([128, 4, 5, 128], BF16, es=p1); r_dg = Res("dg")
                    cvs = [self.sb([128, 512], F32, es=p1) for _ in range(3)]; r_cvs = [Res() for _ in range(3)]
                    junk = self.sb([128, 128], F32, es=p1); r_junk = Res("junk")
                    sss = [self.sb([128, 2], F32, es=p1) for _ in range(3)]; r_sss = [Res() for _ in range(3)]
                    qns = [self.sb([128, 128], BF16, es=p1) for _ in range(3)]; r_qns = [Res() for _ in range(3)]
                    S.op("pool", lambda e: e.memset(xpad[:], 0.0), writes=[r_xp])
                    gch = [kh, 8 + kh, 16 + 2 * kh, 17 + 2 * kh]
                    for ch in range(4):
                        for j in range(5):
                            S.op("pool", lambda e: e.tensor_scalar(out=dg[:, ch, j, :], in0=self.identb[:], scalar1=cw[:, j, gch[ch]:gch[ch] + 1], scalar2=1.0, op0=ALU.mult, op1=ALU.mult),
                                 reads=[r_cw, self.r_ident], writes=[r_dg])
                    for ch in range(4):
                        for ti, (t0, n, lc) in enumerate(TT):
                            bk, rb = self.bank()
                            for kc in range(8):
                                S.op("pe", lambda e: e.matmul(bk[:, 0:n], lhsT=wA[:, kc, ch * 128:(ch + 1) * 128], rhs=self.uT[:, kc, t0:t0 + n], start=(kc == 0), stop=(kc == 7)),
                                     reads=[r_wA, self.r_u[ti]], writes=[rb], same_ok=True)
                            c0 = t0 + 2 if lc == 0 else 2054
                            if (ch + ti) % 2 == 0:
                                S.op("act", lambda e: e.copy(xpad[:, ch, c0:c0 + n], bk[:, 0:n]), reads=[rb], writes=[r_xp])
                            else:
                                S.op("dve", lambda e: e.tensor_copy(xpad[:, ch, c0:c0 + n], bk[:, 0:n]), reads=[rb], writes=[r_xp])
                    def p1A(t):
                        b0 = t * 128 + 2 if t < 16 else 2054 + (t - 16) * 128
                        cv, r_cv = cvs[t % 3], r_cvs[t % 3]
                        ss, r_ss = sss[t % 3], r_sss[t % 3]
                        bk, rb = self.bank()
                        for ch in range(4):
                            for j in range(5):
                                S.op("pe", lambda e: e.matmul(bk[:, ch * 128:(ch + 1) * 128], lhsT=xpad[:, ch, b0 + j - 2:b0 + j - 2 + 128], rhs=dg[:, ch, j, :],
                                                              start=(j == 0), stop=(j == 4)), reads=[r_xp, r_dg], writes=[rb], same_ok=True)
                        S.op("act", lambda e: e.activation(out=cv[:], in_=bk[:, 0:512], func=AF.Silu), reads=[rb], writes=[r_cv])
                        for q_ in range(2):
                            S.op("act", lambda e: e.activation(out=junk[:], in_=cv[:, q_ * 128:(q_ + 1) * 128], func=AF.Square, accum_out=ss[:, q_:q_ + 1]),
                                 reads=[r_cv], writes=[r_ss])
                        S.op("act", lambda e: e.activation(out=ss[:], in_=ss[:], func=AF.Sqrt, bias=epsc[:, 0:1], scale=1.0), reads=[r_ss, r_eps], writes=[r_ss])

                    def p1B(t):
                        cv, r_cv = cvs[t % 3], r_cvs[t % 3]
                        ss, r_ss = sss[t % 3], r_sss[t % 3]
                        qn, r_qn = qns[t % 3], r_qns[t % 3]
                        S.op("dve", lambda e: e.reciprocal(out=ss[:], in_=ss[:]), reads=[r_ss], writes=[r_ss])
                        S.op("dve", lambda e: e.tensor_scalar(out=qn[:], in0=cv[:, 0:128], scalar1=ss[:, 0:1], scalar2=128.0 ** -0.5, op0=ALU.mult, op1=ALU.mult),
                             reads=[r_cv, r_ss], writes=[r_qn])
                        S.op("dve", lambda e: e.tensor_scalar(out=kn[:, t, :], in0=cv[:, 128:256], scalar1=ss[:, 1:2], scalar2=None, op0=ALU.mult),
                             reads=[r_cv, r_ss], writes=[r_kn])
                        S.op("pool", lambda e: e.tensor_copy(vt[:, t, :], cv[:, 256:512]), reads=[r_cv], writes=[r_vt])
                        b2, rb2 = self.bank()
                        S.op("pe", lambda e: e.matmul(b2[:, 0:128], lhsT=qn[:], rhs=self.identb[:], start=True, stop=True), reads=[r_qn, self.r_ident], writes=[rb2], same_ok=True)
                        S.op("pe", lambda e: e.matmul(b2[:, 128:256], lhsT=kn[:, t, :], rhs=self.identb[:], start=True, stop=True), reads=[r_kn, self.r_ident], writes=[rb2], same_ok=True)
                        S.op("act", lambda e: e.copy(qkT[:, :, t * 128:(t + 1) * 128], b2[:, 0:256].rearrange("p (a c) -> p a c", a=2)), reads=[rb2], writes=[r_qkT])
                    p1A(0)
                    for t in range(18):
                        if t + 1 < 18:
                            p1A(t + 1)
                        p1B(t)
                    S.barrier()
                bk, rb = self.bank()
                for t in range(18):
                    for kc in range(8):
                        S.op("pe", lambda e: e.matmul(bk[:, t * 8:(t + 1) * 8], lhsT=self.uT[:, kc, t * 128:(t + 1) * 128], rhs=wgt[:, kc, :], start=(kc == 0), stop=(kc == 7)),
                             reads=[r_wB] + self.r_u, writes=[rb], same_ok=True)
                graw = bk[:, 0:144].rearrange("p (t c) -> p t c", c=8)
                S.op("act", lambda e: e.activation(out=sc["negb"][:], in_=graw[:, :, 0:4], func=AF.Sigmoid), reads=[rb], writes=[r_sc])
                S.op("dve", lambda e: e.tensor_scalar(out=sc["negb"][:], in0=sc["negb"][:], scalar1=-1.0, scalar2=None, op0=ALU.mult), reads=[r_sc], writes=[r_sc])
                for m in range(4):
                    d_, j_ = m // 2, m % 2
                    hidx = d_ * 16 + 2 * kh + j_
                    S.op("act", lambda e: e.activation(out=sc["g"][:, :, m], in_=graw[:, :, 4 + m], func=AF.Exp, bias=hc[:, 32 + hidx:33 + hidx], scale=1.0),
                         reads=[rb, r_hc], writes=[r_sc])
                S.op("act", lambda e: e.activation(out=sc["g"][:], in_=sc["g"][:], func=AF.Ln, bias=1.0, scale=1.0), reads=[r_sc], writes=[r_sc])
                for m in range(4):
                    d_, j_ = m // 2, m % 2
                    hidx = d_ * 16 + 2 * kh + j_
                    S.op("dve", lambda e: e.tensor_scalar(out=sc["g"][:, :, m], in0=sc["g"][:, :, m], scalar1=hc[:, hidx:hidx + 1], scalar2=None, op0=ALU.mult),
                         reads=[r_sc, r_hc], writes=[r_sc])
                bk, rb = self.bank()
                gv = sc["g"][:]
                S.op("pe", lambda e: e.matmul(bk[:, 0:72].rearrange("p (t c) -> p t c", c=4)[:, :, 0:2], lhsT=inclT[0][:], rhs=gv[:, :, 0:2], start=True, stop=True),
                     reads=[r_sc, r_msk], writes=[rb], same_ok=True)
                S.op("pe", lambda e: e.matmul(bk[:, 0:72].rearrange("p (t c) -> p t c", c=4)[:, :, 2:4], lhsT=inclT[1][:], rhs=gv[:, :, 2:4], start=True, stop=True),
                     reads=[r_sc, r_msk], writes=[rb], same_ok=True)
                S.op("pe", lambda e: e.matmul(bk[:, 128:200], lhsT=self.ones[:], rhs=gv.rearrange("p t c -> p (t c)"), start=True, stop=True),
                     reads=[r_sc, self.r_ident], writes=[rb], same_ok=True)
                gcp = bk[:, 0:72].rearrange("p (t c) -> p t c", c=4)
                glp = bk[:, 128:200].rearrange("p (t c) -> p t c", c=4)
                S.op("dve", lambda e: e.tensor_copy(sc["gc"][:], gcp), reads=[rb], writes=[r_sc])
                S.op("dve", lambda e: e.tensor_copy(sc["dl"][:], glp), reads=[rb], writes=[r_sc])
                S.op("dve", lambda e: e.tensor_tensor(out=sc["ecoef"][:], in0=glp, in1=sc["gc"][:], op=ALU.subtract), reads=[rb, r_sc], writes=[r_sc])
                S.op("dve", lambda e: e.tensor_scalar(out=sc["ngc"][:], in0=sc["gc"][:], scalar1=-1.0, scalar2=None, op0=ALU.mult), reads=[r_sc], writes=[r_sc])
                S.op("act", lambda e: e.activation(out=sc["e"][:], in_=sc["gc"][:], func=AF.Exp), reads=[r_sc], writes=[r_sc])
                S.op("act", lambda e: e.activation(out=sc["dl"][:], in_=sc["dl"][:], func=AF.Exp), reads=[r_sc], writes=[r_sc])
                S.op("act", lambda e: e.activation(out=sc["ecoef"][:], in_=sc["ecoef"][:], func=AF.Exp), reads=[r_sc], writes=[r_sc])
                S.op("dve", lambda e: e.tensor_tensor(out=sc["negbe"][:], in0=sc["negb"][:], in1=sc["e"][:], op=ALU.mult), reads=[r_sc], writes=[r_sc])
                with ExitStack() as p3:
                    h4 = lambda: self.sb([128, 4, 128], BF16, es=p3)
                    f4 = lambda: self.sb([128, 4, 128], F32, es=p3)
                    ST = []
                    for st_ in range(2):
                        B = dict(X=[h4(), h4()], Y=[h4(), h4()], W=h4(), Tm=h4(), Y0=h4(), AT=h4(),
                                 Gm=self.sb([128, 2, 128], BF16, es=p3), QKm=self.sb([128, 2, 128], BF16, es=p3), scr=f4(), rscr=Res(),
                                 rX=[Res(), Res()], rY=[Res(), Res()], rW=Res(), rTm=Res(), rY0=Res(), rAT=Res(), rGm=Res(), rQKm=Res())
                        ST.append(B)
                    Rm = h4(); r_Rm = Res("Rm")
                    vn = h4(); r_vn = Res("vn")
                    kdec = h4(); r_kdec = Res("kdec")
                    bv = h4(); r_bv = Res("bv")
                    Sf = f4(); r_Sf = Res("Sf")
                    Sb = h4(); r_Sb = Res("Sb")
                    ot = h4(); r_ot = Res("ot")
                    S.op("pool", lambda e: e.memset(Sf[:], 0.0), writes=[r_Sf])
                    S.op("pool", lambda e: e.memset(Sb[:], 0.0), writes=[r_Sb])
                    order = [[16, 17] + list(range(16)), [17, 16] + list(range(15, -1, -1))]
                    written = set()
                    kT = lambda c: qkT[:, 1, c * 128:(c + 1) * 128]
                    qT = lambda c: qkT[:, 0, c * 128:(c + 1) * 128]
                    pv4 = lambda b_: b_[:, 0:512].rearrange("p (m c) -> p m c", m=4)

                    def pre(step, B):
                        X, Y, W, Tm, Y0, AT, Gm, QKm = B["X"], B["Y"], B["W"], B["Tm"], B["Y0"], B["AT"], B["Gm"], B["QKm"]
                        rX, rY, rW, rTm, rY0, rAT, rGm, rQKm = B["rX"], B["rY"], B["rW"], B["rTm"], B["rY0"], B["rAT"], B["rGm"], B["rQKm"]
                        cc = [order[0][step], order[1][step]]
                        cm_ = [cc[m // 2] for m in range(4)]
                        scrA = scrB = B["scr"]
                        r_scrA = r_scrB = B["rscr"]
                        bk, rb = self.bank()
                        for d in range(2):
                            S.op("pe", lambda e: e.matmul(bk[:, d * 128:(d + 1) * 128], lhsT=kT(cc[d]), rhs=kT(cc[d]), start=True, stop=True), reads=[r_qkT], writes=[rb], same_ok=True)
                            S.op("pe", lambda e: e.matmul(bk[:, 256 + d * 128:256 + (d + 1) * 128], lhsT=kT(cc[d]), rhs=qT(cc[d]), start=True, stop=True), reads=[r_qkT], writes=[rb], same_ok=True)
                        S.op("dve", lambda e: e.tensor_tensor(out=Gm[:], in0=bk[:, 0:256].rearrange("p (d c) -> p d c", d=2), in1=strict2[:], op=ALU.mult), reads=[rb, r_msk], writes=[rGm])
                        S.op("dve", lambda e: e.tensor_tensor(out=QKm[:], in0=bk[:, 256:512].rearrange("p (d c) -> p d c", d=2), in1=inclT2[:], op=ALU.mult), reads=[rb, r_msk], writes=[rQKm])
                        for m in range(4):
                            S.op("pool", lambda e: e.tensor_scalar(out=scrA[:, m, :], in0=self.ident[:], scalar1=sc["gc"][:, cm_[m], m:m + 1], scalar2=1.0, op0=ALU.mult, op1=ALU.mult),
                                 reads=[r_sc, self.r_ident], writes=[r_scrA])
                        yield
                        bb, rbb = self.bank()
                        for m in range(4):
                            S.op("pe", lambda e: e.matmul(bb[:, m * 128:(m + 1) * 128], lhsT=self.ones[:], rhs=scrA[:, m, :], start=True, stop=True),
                                 reads=[r_scrA, self.r_ident], writes=[rbb], same_ok=True)
                        for m in range(4):
                            S.op("act", lambda e: e.activation(out=scrA[:, m, :], in_=bb[:, m * 128:(m + 1) * 128], func=AF.Relu, bias=sc["ngc"][:, cm_[m], m:m + 1], scale=1.0),
                                 reads=[rbb, r_sc], writes=[r_scrA])
                        S.op("act", lambda e: e.activation(out=X[1][:], in_=scrA[:], func=AF.Exp, scale=-1.0), reads=[r_scrA], writes=[rX[1]])
                        for m in range(4):
                            S.op("act", lambda e: e.activation(out=scrB[:, m, :], in_=bb[:, m * 128:(m + 1) * 128], func=AF.Relu, bias=sc["gc"][:, cm_[m], m:m + 1], scale=-1.0),
                                 reads=[rbb, r_sc], writes=[r_scrB])
                        S.op("act", lambda e: e.activation(out=Y[1][:], in_=scrB[:], func=AF.Exp, scale=-1.0), reads=[r_scrB], writes=[rY[1]])
                        yield
                        for m in range(4):
                            d = m // 2
                            S.op("dve", lambda e: e.scalar_tensor_tensor(out=X[0][:, m, :], in0=X[1][:, m, :], scalar=sc["negb"][:, cm_[m], m:m + 1], in1=Gm[:, d, :],
                                                                         op0=ALU.mult, op1=ALU.mult), reads=[rX[1], r_sc, rGm], writes=[rX[0]])
                        for d in range(2):
                            S.op("pool", lambda e: e.tensor_tensor(out=AT[:, 2 * d:2 * d + 2, :], in0=Y[1][:, 2 * d:2 * d + 2, :],
                                                                   in1=QKm[:, d:d + 1, :].broadcast_to([128, 2, 128]), op=ALU.mult), reads=[rQKm, rY[1]], writes=[rAT])
                        yield
                        bk, rb = self.bank()
                        for m in range(4):
                            S.op("pe", lambda e: e.matmul(bk[:, m * 128:(m + 1) * 128], lhsT=X[0][:, m, :], rhs=self.identb[:], start=True, stop=True),
                                 reads=[rX[0], self.r_ident], writes=[rb], same_ok=True)
                        S.op("act", lambda e: e.copy(Y0[:], pv4(bk)), reads=[rb], writes=[rY0])
                        yield
                        S.op("dve", lambda e: e.tensor_tensor(out=X[1][:], in0=X[0][:], in1=b4(bd16), op=ALU.mult), reads=[rX[0], r_msk, rAT], writes=[rX[1]])
                        S.op("dve", lambda e: e.tensor_tensor(out=Y[1][:], in0=Y0[:], in1=b4(bd16), op=ALU.mult), reads=[rY0, r_msk, rAT], writes=[rY[1]])
                        S.op("dve", lambda e: e.tensor_tensor(out=W[:], in0=Y[1][:], in1=b4(self.identb), op=ALU.add), reads=[rY[1], self.r_ident], writes=[rW])
                        yield
                        cur = 1
                        for lev in range(3):
                            nxt = 1 - cur
                            bx, rbx = self.bank()
                            for m in range(4):
                                S.op("pe", lambda e: e.matmul(bx[:, m * 128:(m + 1) * 128], lhsT=Y[cur][:, m, :], rhs=X[cur][:, m, :], start=True, stop=True),
                                     reads=[rY[cur], rX[cur]], writes=[rbx], same_ok=True)
                            if lev < 2:
                                by, rby = self.bank()
                                for m in range(4):
                                    S.op("pe", lambda e: e.matmul(by[:, m * 128:(m + 1) * 128], lhsT=X[cur][:, m, :], rhs=Y[cur][:, m, :], start=True, stop=True),
                                         reads=[rY[cur], rX[cur]], writes=[rby], same_ok=True)
                            S.op("act", lambda e: e.copy(X[nxt][:], pv4(bx)), reads=[rbx], writes=[rX[nxt]])
                            if lev < 2:
                                S.op("dve", lambda e: e.tensor_copy(Y[nxt][:], pv4(by)), reads=[rby], writes=[rY[nxt]])
                            yield
                            bw, rbw = self.bank()
                            for m in range(4):
                                S.op("pe", lambda e: e.matmul(bw[:, m * 128:(m + 1) * 128], lhsT=X[nxt][:, m, :], rhs=W[:, m, :], start=True, stop=True),
                                     reads=[rX[nxt], rW], writes=[rbw], same_ok=True)
                            S.op("dve", lambda e: e.tensor_tensor(out=W[:], in0=pv4(bw), in1=W[:], op=ALU.add), reads=[rbw, rW], writes=[rW])
                            cur = nxt
                            yield
                        bk, rb = self.bank()
                        for m in range(4):
                            S.op("pe", lambda e: e.matmul(bk[:, m * 128:(m + 1) * 128], lhsT=W[:, m, :], rhs=self.identb[:], start=True, stop=True),
                                 reads=[rW, self.r_ident], writes=[rb], same_ok=True)
                        S.op("act", lambda e: e.copy(Tm[:], pv4(bk)), reads=[rb], writes=[rTm])
                        yield
                        for li, offm in enumerate((off16, off32, off64)):
                            S.op("dve", lambda e: e.tensor_tensor(out=X[0][:], in0=Y0[:], in1=b4(offm), op=ALU.mult), reads=[rY0, r_msk], writes=[rX[0]])
                            bz, rbz = self.bank()
                            for m in range(4):
                                S.op("pe", lambda e: e.matmul(bz[:, m * 128:(m + 1) * 128], lhsT=X[0][:, m, :], rhs=Tm[:, m, :], start=True, stop=True),
                                     reads=[rX[0], rTm], writes=[rbz], same_ok=True)
                            S.op("act", lambda e: e.copy(X[1][:], pv4(bz)), reads=[rbz], writes=[rX[1]])
                            yield
                            if li < 2:
                                bt, rbt = self.bank()
                                for m in range(4):
                                    S.op("pe", lambda e: e.matmul(bt[:, m * 128:(m + 1) * 128], lhsT=W[:, m, :], rhs=X[1][:, m, :], start=True, stop=True),
                                         reads=[rW, rX[1]], writes=[rbt], same_ok=True)
                            bw, rbw = self.bank()
                            for m in range(4):
                                S.op("pe", lambda e: e.matmul(bw[:, m * 128:(m + 1) * 128], lhsT=X[1][:, m, :], rhs=W[:, m, :], start=True, stop=True),
                                     reads=[rX[1], rW], writes=[rbw], same_ok=True)
                            if li < 2:
                                S.op("dve", lambda e: e.tensor_tensor(out=Tm[:], in0=pv4(bt), in1=Tm[:], op=ALU.add), reads=[rbt, rTm], writes=[rTm])
                            S.op("dve", lambda e: e.tensor_tensor(out=W[:], in0=pv4(bw), in1=W[:], op=ALU.add), reads=[rbw, rW], writes=[rW])
                            yield

                    def chain(step, B):
                        W, AT, rW, rAT = B["W"], B["AT"], B["rW"], B["rAT"]
                        cc = [order[0][step], order[1][step]]
                        cm_ = [cc[m // 2] for m in range(4)]
                        for m in range(4):
                            S.op("pool", lambda e: e.tensor_scalar(out=kdec[:, m, :], in0=kn[:, cm_[m], :], scalar1=sc["ecoef"][:, cm_[m], m:m + 1], scalar2=1.0, op0=ALU.mult, op1=ALU.mult),
                                 reads=[r_kn, r_sc], writes=[r_kdec])
                            S.op("pool", lambda e: e.tensor_scalar(out=bv[:, m, :], in0=vt[:, cm_[m], (m % 2) * 128:(m % 2 + 1) * 128], scalar1=sc["negb"][:, cm_[m], m:m + 1],
                                                                   scalar2=-1.0, op0=ALU.mult, op1=ALU.mult), reads=[r_vt, r_sc], writes=[r_bv])
                        yield
                        bks, rbks = self.bank()
                        for m in range(4):
                            S.op("pe", lambda e: e.matmul(bks[:, m * 128:(m + 1) * 128], lhsT=kT(cm_[m]), rhs=Sb[:, m, :], start=True, stop=True), reads=[r_qkT, r_Sb], writes=[rbks], same_ok=True)
                        bo1, rbo1 = self.bank()
                        for m in range(4):
                            S.op("pe", lambda e: e.matmul(bo1[:, m * 128:(m + 1) * 128], lhsT=qT(cm_[m]), rhs=Sb[:, m, :], start=True, stop=True), reads=[r_qkT, r_Sb], writes=[rbo1], same_ok=True)
                        for m in range(4):
                            S.op("dve", lambda e: e.scalar_tensor_tensor(out=Rm[:, m, :], in0=bks[:, m * 128:(m + 1) * 128], scalar=sc["negbe"][:, cm_[m], m:m + 1], in1=bv[:, m, :],
                                                                         op0=ALU.mult, op1=ALU.add), reads=[rbks, r_sc, r_bv], writes=[r_Rm])
                            S.op("act", lambda e: e.activation(out=ot[:, m, :], in_=bo1[:, m * 128:(m + 1) * 128], func=AF.Copy, scale=sc["e"][:, cm_[m], m:m + 1]),
                                 reads=[rbo1, r_sc], writes=[r_ot])
                        yield
                        bvn, rbvn = self.bank()
                        for m in range(4):
                            S.op("pe", lambda e: e.matmul(bvn[:, m * 128:(m + 1) * 128], lhsT=W[:, m, :], rhs=Rm[:, m, :], start=True, stop=True), reads=[rW, r_Rm], writes=[rbvn], same_ok=True)
                        S.op("act", lambda e: e.copy(vn[:], pv4(bvn)), reads=[rbvn], writes=[r_vn])
                        yield
                        bo2, rbo2 = self.bank()
                        for m in range(4):
                            S.op("pe", lambda e: e.matmul(bo2[:, m * 128:(m + 1) * 128], lhsT=AT[:, m, :], rhs=vn[:, m, :], start=True, stop=True), reads=[rAT, r_vn], writes=[rbo2], same_ok=True)
                        bst, rbst = self.bank()
                        for m in range(4):
                            S.op("pe", lambda e: e.matmul(bst[:, m * 128:(m + 1) * 128], lhsT=kdec[:, m, :], rhs=vn[:, m, :], start=True, stop=True), reads=[r_kdec, r_vn], writes=[rbst], same_ok=True)
                        yield
                        for m in range(4):
                            S.op("dve", lambda e: e.scalar_tensor_tensor(out=Sf[:, m, :], in0=Sf[:, m, :], scalar=sc["dl"][:, cm_[m], m:m + 1], in1=bst[:, m * 128:(m + 1) * 128],
                                                                         op0=ALU.mult, op1=ALU.add), reads=[r_Sf, r_sc, rbst], writes=[r_Sf])
                        S.op("act", lambda e: e.copy(Sb[:], Sf[:]), reads=[r_Sf], writes=[r_Sb])
                        S.op("dve", lambda e: e.tensor_tensor(out=ot[:], in0=pv4(bo2), in1=ot[:], op=ALU.add), reads=[rbo2, r_ot], writes=[r_ot])
                        for d in range(2):
                            c = cc[d]
                            src = ot[:, 2 * d:2 * d + 2, :]
                            dst = oacc[:, c, :].rearrange("p (j v) -> p j v", j=2)
                            if c not in written:
                                written.add(c)
                                S.op("pool", lambda e: e.tensor_copy(dst, src), reads=[r_ot], writes=[r_oacc[c]])
                            else:
                                S.op("pool", lambda e: e.tensor_tensor(out=dst, in0=dst, in1=src, op=ALU.add), reads=[r_ot, r_oacc[c]], writes=[r_oacc[c]])

                    def run_all(g):
                        for _ in g:
                            pass

                    pend = None
                    for p_ in range(9):
                        ga, gb = pre(2 * p_, ST[0]), pre(2 * p_ + 1, ST[1])
                        next(ga); next(gb)
                        if pend is not None:
                            run_all(chain(pend[0], ST[0]))
                        next(ga); next(gb)
                        if pend is not None:
                            run_all(chain(pend[1], ST[1]))
                        live = [True, True]
                        gens = [ga, gb]
                        while any(live):
                            for gi in range(2):
                                if live[gi]:
                                    try:
                                        next(gens[gi])
                                    except StopIteration:
                                        live[gi] = False
                        pend = (2 * p_, 2 * p_ + 1)
                    run_all(chain(pend[0], ST[0]))
                    run_all(chain(pend[1], ST[1]))
                    S.barrier()
                with ExitStack() as p4:
                    ogT = self.sb([128, 2, 512], BF16, es=p4); r_ogT = Res("ogT")
                    ogs = [self.sb([128, 256], F32, es=p4) for _ in range(2)]; r_ogs = [Res(), Res()]
                    ogbs = [self.sb([128, 256], BF16, es=p4) for _ in range(2)]; r_ogbs = [Res(), Res()]
                    zss = [self.sb([128, 256], F32, es=p4) for _ in range(2)]; r_zss = [Res(), Res()]
                    junk = self.sb([128, 128], F32, es=p4); r_junk = Res("junk")
                    sss4 = [self.sb([128, 2], F32, es=p4) for _ in range(2)]; r_sss4 = [Res(), Res()]
                    ogTs = [ogT, self.sb([128, 2, 512], BF16, es=p4)]; r_ogTs = [r_ogT, Res("ogT1")]
                    tile_of = [min(t // 4, 4) for t in range(18)]
                    zb = {}

                    def p4A(t):
                        ti = tile_of[t]
                        zs, r_zs = zss[t % 2], r_zss[t % 2]
                        ss, r_ss = sss4[t % 2], r_sss4[t % 2]
                        for j in range(2):
                            S.op("act", lambda e: e.activation(out=junk[:], in_=oacc[:, t, j * 128:(j + 1) * 128], func=AF.Square, accum_out=ss[:, j:j + 1]),
                                 reads=[r_oacc[t]], writes=[r_ss])
                        S.op("act", lambda e: e.activation(out=ss[:], in_=ss[:], func=AF.Sqrt, bias=epsc[:, 0:1], scale=1.0 / 128.0), reads=[r_ss, r_eps], writes=[r_ss])
                        bz, rbz = self.bank()
                        for kc in range(8):
                            S.op("pe", lambda e: e.matmul(bz[:, 0:256], lhsT=self.uT[:, kc, t * 128:(t + 1) * 128], rhs=wz[:, kc, :], start=(kc == 0), stop=(kc == 7)),
                                 reads=[r_wB, self.r_u[ti]], writes=[rbz], same_ok=True)
                        S.op("act", lambda e: e.activation(out=zs[:], in_=bz[:, 0:256], func=AF.Silu), reads=[rbz], writes=[r_zs])

                    def p4B(t):
                        ti = tile_of[t]
                        t0, n, lc = TT[ti]
                        tt = t - t0 // 128
                        og, r_og = ogs[t % 2], r_ogs[t % 2]
                        ogb, r_ogb = ogbs[t % 2], r_ogbs[t % 2]
                        zs, r_zs = zss[t % 2], r_zss[t % 2]
                        ss, r_ss = sss4[t % 2], r_sss4[t % 2]
                        ogT_, r_ogT_ = ogTs[ti % 2], r_ogTs[ti % 2]
                        S.op("dve", lambda e: e.reciprocal(out=ss[:], in_=ss[:]), reads=[r_ss], writes=[r_ss])
                        for j in range(2):
                            S.op("dve", lambda e: e.scalar_tensor_tensor(out=og[:, j * 128:(j + 1) * 128], in0=oacc[:, t, j * 128:(j + 1) * 128], scalar=ss[:, j:j + 1], in1=ngrep[:],
                                                                         op0=ALU.mult, op1=ALU.mult), reads=[r_oacc[t], r_ss, r_ngr], writes=[r_og])
                        S.op("dve", lambda e: e.tensor_tensor(out=ogb[:], in0=og[:], in1=zs[:], op=ALU.mult), reads=[r_og, r_zs], writes=[r_ogb])
                        b2, rb2 = self.bank()
                        for j in range(2):
                            S.op("pe", lambda e: e.matmul(b2[:, j * 128:(j + 1) * 128], lhsT=ogb[:, j * 128:(j + 1) * 128], rhs=self.identb[:], start=True, stop=True),
                                 reads=[r_ogb, self.r_ident], writes=[rb2], same_ok=True)
                        S.op("act", lambda e: e.copy(ogT_[:, :, tt * 128:(tt + 1) * 128], b2[:, 0:256].rearrange("p (j c) -> p j c", j=2)), reads=[rb2], writes=[r_ogT_])
                        if tt == n // 128 - 1:
                            for oc in range(8):
                                bk, rb = self.bank()
                                for j in range(2):
                                    S.op("pe", lambda e: e.matmul(bk[:, 0:n], lhsT=wo[:, j, oc * 128:(oc + 1) * 128], rhs=ogT_[:, j, 0:n], start=(j == 0), stop=(j == 1)),
                                         reads=[r_wC, r_ogT_], writes=[rb], same_ok=True)
                                hz = self.hT[:, oc, t0:t0 + n]
                                S.op("dve", lambda e: e.scalar_tensor_tensor(out=hz, in0=bk[:, 0:n], scalar=self.mcol(i, 2, oc, lc), in1=hz, op0=ALU.mult, op1=ALU.add),
                                     reads=[rb, self.r_h[ti][oc], self.r_mod[i]], writes=[self.r_h[ti][oc]])
                    p4A(0)
                    for t in range(18):
                        if t + 1 < 18:
                            p4A(t + 1)
                        p4B(t)
                    S.barrier()


def build_program(depth_run=DEPTH, mixers=True, dbg=False):
    nc = bass.Bass("TRN2", target_bir_lowering=False)
    es = ExitStack()
    with es:
        kb = KB(nc, es, depth_run, mixers, dbg)
        kb.build()
        print("instructions", kb.S.ninst, "sems", kb.S.nsem, flush=True)
    return nc, kb


def _rope_tables(dim):
    n_freq = dim // 4
    inv = (10000.0 ** (-np.arange(n_freq, dtype=np.float32) / n_freq)).astype(np.float32)
    tok = np.arange(TL)
    row = (tok // 64).astype(np.float32)
    col = (tok % 64).astype(np.float32)
    ang = np.concatenate([row[:, None] * inv, col[:, None] * inv], -1).astype(np.float32)
    c = np.ones((dim // 2, T), np.float32)
    s = np.zeros((dim // 2, T), np.float32)
    c[:, :TL] = np.cos(ang).T
    s[:, :TL] = np.sin(ang).T
    return c, s


def _mla_host(inputs, shared, g):
    w_in = g("mla_w_in")[0]
    w_qb = g("mla_w_qb")[0]
    ev = np.arange(0, 32, 2)
    od = ev + 1
    cols = []
    for h in range(16):
        b = h * 96
        cols += list(range(b, b + 64)) + list(b + 64 + ev) + list(b + 64 + od) + list(b + 64 + ev) + list(b + 64 + od)
    shared["mla_wqx"] = np.ascontiguousarray(w_qb[:, cols])
    ia = list(1024 + ev) * 4
    ib = list(1024 + od) * 4
    shared["mla_wkr"] = np.ascontiguousarray(np.concatenate(
        [w_in[:, 0:64], w_in[:, ia], w_in[:, 0:64], w_in[:, ib]], axis=1))
    c, s = _rope_tables(32)
    one = np.ones((64, T), np.float32)
    shared["mla_qtab"] = np.concatenate([one, c, s, s, c], 0)
    shared["mla_kta"] = np.concatenate([one, c, -c, s, s], 0)
    shared["mla_ktb"] = np.concatenate([one, -s, s, c, c], 0)


def _diff_host(inputs, shared, g):
    w_in = g("diff_w_in")[0]
    ev = np.arange(0, 64, 2)
    od = ev + 1
    cols = []
    for h in range(8):
        for m in range(2):
            bq = h * 128 + m * 64
            bk = 1024 + h * 128 + m * 64
            cols += list(bq + ev) + list(bq + od) + list(bq + ev) + list(bq + od)
            cols += list(bk + ev) * 4
            cols += list(bk + od) * 4
    shared["diff_wx"] = np.ascontiguousarray(w_in[:, cols])
    shared["diff_wv"] = np.ascontiguousarray(w_in[:, 2048:3072])
    shared["diff_w_out"] = g("diff_w_out")[0]
    c, s = _rope_tables(64)
    shared["diff_qtab"] = np.concatenate([c, s, s, c], 0)
    shared["diff_kta"] = np.concatenate([c, -c, s, s], 0)
    shared["diff_ktb"] = np.concatenate([-s, s, c, c], 0)
    shared["diff_lam"] = np.ascontiguousarray(np.stack([g("diff_lambda_q1")[0], g("diff_lambda_k1")[0],
                                                        g("diff_lambda_q2")[0], g("diff_lambda_k2")[0]], axis=1))
    shared["diff_subln"] = g("diff_subln")[0].reshape(1, 128)


def _gla_host(inputs, shared, g):
    shared["gla_w_in"] = g("gla_w_in")[0]
    shared["gla_gw"] = np.ascontiguousarray(np.stack([g("gla_gate_w_fwd")[0], g("gla_gate_w_bwd")[0]], 0))
    shared["gla_gb"] = np.ascontiguousarray(np.stack([g("gla_gate_b_fwd")[0].reshape(4, 128), g("gla_gate_b_bwd")[0].reshape(4, 128)], 0))
    shared["gla_norm"] = g("gla_norm")[0].reshape(2, 128)
    shared["gla_w_out"] = g("gla_w_out")[0]


def _gdn_host(inputs, shared, g):
    w_in = g("gdn_w_in")[0]
    shared["gdn_w_in"] = w_in
    cols = []
    for kh in range(8):
        for base in (6144, 6160, 6176, 6192):
            cols += [base + 2 * kh, base + 2 * kh + 1]
    shared["gdn_wg"] = np.ascontiguousarray(w_in[:, cols])
    shared["gdn_conv"] = g("gdn_conv_w")[0].reshape(5, 32, 128)
    shared["gdn_hc"] = np.ascontiguousarray(np.concatenate([g("gdn_a_log_fwd")[0], g("gdn_a_log_bwd")[0],
                                                            g("gdn_dt_bias_fwd")[0], g("gdn_dt_bias_bwd")[0]]).reshape(1, 64))
    shared["gdn_norm"] = g("gdn_norm")[0].reshape(1, 128)
    shared["gdn_w_out"] = g("gdn_w_out")[0]

def make_in_maps(inputs):
    g = lambda k: np.ascontiguousarray(np.asarray(inputs[k], dtype=np.float32))
    shared = {
        "c_ctx": g("c_ctx").reshape(8, 128),
        "ada_w": g("ada_w"), "ada_b": g("ada_b").reshape(DEPTH, 48, 128),
        "ln1_g": g("ln1_g").reshape(DEPTH, 8, 128), "ln1_b": g("ln1_b").reshape(DEPTH, 8, 128),
        "ln2_g": g("ln2_g").reshape(DEPTH, 8, 128), "ln2_b": g("ln2_b").reshape(DEPTH, 8, 128),
        "mlp_w1": g("mlp_w1"), "mlp_w2": g("mlp_w2"),
        "mla_w_in": g("mla_w_in")[0], "mla_q_norm": g("mla_q_norm")[0].reshape(6, 128),
        "mla_kv_norm": g("mla_kv_norm")[0].reshape(2, 128),
        "mla_w_kvb": g("mla_w_kvb")[0], "mla_w_out": g("mla_w_out")[0],
    }
    _mla_host(inputs, shared, g)
    _diff_host(inputs, shared, g)
    _gla_host(inputs, shared, g)
    _gdn_host(inputs, shared, g)
    x, c, ctx = g("x"), g("c"), g("ctx")
    maps = []
    for b in range(8):
        m = dict(shared)
        m["x"] = x[b]
        m["ctx"] = ctx[b]
        m["c"] = c[b].reshape(8, 128)
        maps.append(m)
    return maps


def kernel(**inputs):
    nc, kb = build_program()
    maps = make_in_maps(inputs)
    res = run_bass_kernel_spmd(nc, maps, core_ids=list(range(8)))
    return np.stack([np.asarray(r["out"], dtype=np.float32) for r in res.results], axis=0)
```

```python
import math
import numpy as np
import concourse.bass as bass
import concourse.mybir as mybir
from concourse.bass_utils import run_bass_kernel_spmd
from contextlib import ExitStack

F32 = mybir.dt.float32
BF16 = mybir.dt.bfloat16
ALU = mybir.AluOpType
AF = mybir.ActivationFunctionType

DEPTH = 4
D = 1024
TL = 2048
TC = 256
T = TL + TC
ALPHA = (2 * DEPTH) ** 0.25
EPS = 1e-6
EPS_LN = EPS / (ALPHA * ALPHA)
TT = [(0, 512, 0), (512, 512, 0), (1024, 512, 0), (1536, 512, 0), (2048, 256, 1)]


class Res:
    __slots__ = ("name", "w", "r", "dsem", "dcnt")

    def __init__(self, name=""):
        self.name = name
        self.w = None
        self.r = {}
        self.dsem = None
        self.dcnt = 0


class Sched:
    EPOCH = 30000

    def __init__(self, nc, es):
        self.nc = nc
        self.es = es
        self.eng = {"pe": nc.tensor, "dve": nc.vector, "act": nc.scalar,
                    "pool": nc.gpsimd, "sp": nc.sync}
        self.cnt = {e: 0 for e in self.eng}
        self.cursem = {e: None for e in self.eng}
        self.last = {e: None for e in self.eng}
        self.seen = {e: {} for e in self.eng}
        self.nsem = 0
        self.ninst = 0
        self.owners = []
        self.out_events = []

    def newsem(self, name):
        self.nsem += 1
        return self.es.enter_context(self.nc.semaphore(f"{name}_{self.nsem}"))

    def _wait(self, e, ev):
        sem, val, _ = ev
        k = id(sem)
        if self.seen[e].get(k, 0) >= val:
            return
        self.eng[e].wait_ge(sem, val)
        self.seen[e][k] = val

    def _deps(self, e, reads, writes, same_ok):
        for r in reads:
            if r.w is not None and not (same_ok and r.w[2] == e):
                self._wait(e, r.w)
        for w in writes:
            if w.w is not None and not (same_ok and w.w[2] == e):
                self._wait(e, w.w)
            for ev in w.r.values():
                if not (same_ok and ev[2] == e):
                    self._wait(e, ev)

    def op(self, e, fn, reads=(), writes=(), same_ok=False):
        self._deps(e, reads, writes, same_ok)
        ins = fn(self.eng[e])
        if self.cnt[e] % self.EPOCH == 0:
            self.cursem[e] = self.newsem("c" + e)
        self.cnt[e] += 1
        val = (self.cnt[e] - 1) % self.EPOCH + 1
        sem = self.cursem[e]
        ins.then_inc(sem, 1)
        ev = (sem, val, e)
        self.last[e] = ev
        for r in reads:
            r.r[id(sem)] = ev
        for w in writes:
            w.w = ev
            w.r = {}
        self.ninst += 1
        return ev

    def dma(self, q, pairs, owner, reads=(), writes=(), **kw):
        self._deps(q, reads, writes, False)
        if owner.dsem is None:
            owner.dsem = self.newsem("d")
            self.owners.append(owner)
        if owner.dcnt > 0:
            self._wait(q, (owner.dsem, owner.dcnt, "dma"))
        for (o, i) in pairs:
            self.eng[q].dma_start(out=o, in_=i, **kw).then_inc(owner.dsem, 16)
            owner.dcnt += 16
        ev = (owner.dsem, owner.dcnt, "dma")
        for r in reads:
            r.r[id(owner.dsem)] = ev
        for w in writes:
            w.w = ev
            w.r = {}
        self.ninst += len(pairs)
        return ev

    def barrier(self):
        evs = [ev for ev in self.last.values() if ev is not None]
        evs += [(o.dsem, o.dcnt, "dma") for o in self.owners if o.dcnt > 0]
        for e in ("pe", "dve", "act", "pool", "sp"):
            for ev in evs:
                if ev[2] != e:
                    self._wait(e, ev)

    def finish(self):
        for ev in self.out_events:
            self._wait("sp", ev)


class KB:
    def __init__(self, nc, es, depth_run=DEPTH, mixers=True, dbg=False):
        self.nc, self.es = nc, es
        self.S = Sched(nc, es)
        self.depth_run = depth_run
        self.mixers = mixers
        self.dbg = dbg
        self.din = {}
        self._n = 0
        self.skip_ctx = False

    def sb(self, shape, dt, es=None, name=None):
        self._n += 1
        return (es or self.es).enter_context(self.nc.sbuf_tensor(name or f"t{self._n}", list(shape), dt))

    def dram_in(self, name, shape):
        t = self.nc.dram_tensor(name, list(shape), F32, kind="ExternalInput").ap()
        self.din[name] = t
        return t

    def bank(self):
        i = self.bank_i
        self.bank_i = (i + 1) % self.nrr
        return self.banks[i], self.bank_res[i]

    def declare(self):
        di = self.dram_in
        di("x", [TL, D]); di("ctx", [TC, D]); di("c", [8, 128]); di("c_ctx", [8, 128])
        di("ada_w", [DEPTH, D, 6 * D]); di("ada_b", [DEPTH, 48, 128])
        for n in ("ln1_g", "ln1_b", "ln2_g", "ln2_b"):
            di(n, [DEPTH, 8, 128])
        di("mlp_w1", [DEPTH, D, 4 * D]); di("mlp_w2", [DEPTH, 4 * D, D])
        di("mla_w_in", [D, 1056]); di("mla_q_norm", [6, 128]); di("mla_kv_norm", [2, 128])
        di("mla_w_kvb", [256, 2048]); di("mla_w_out", [D, D])
        di("mla_wqx", [768, 2048]); di("mla_wkr", [D, 256])
        di("mla_qtab", [128, T]); di("mla_kta", [128, T]); di("mla_ktb", [128, T])
        di("diff_wx", [D, 16 * 384]); di("diff_wv", [D, D]); di("diff_w_out", [D, D])
        di("diff_qtab", [128, T]); di("diff_kta", [128, T]); di("diff_ktb", [128, T])
        di("diff_lam", [64, 4]); di("diff_subln", [1, 128])
        di("gla_w_in", [D, 3104]); di("gla_gw", [2, 16, 512]); di("gla_gb", [2, 4, 128])
        di("gla_norm", [2, 128]); di("gla_w_out", [D, D])
        di("gdn_w_in", [D, 6208]); di("gdn_wg", [D, 64]); di("gdn_conv", [5, 32, 128])
        di("gdn_hc", [1, 64]); di("gdn_norm", [1, 128]); di("gdn_w_out", [2 * D, D])
        self.out = self.nc.dram_tensor("out", [TL, D], F32, kind="ExternalOutput").ap()
        if self.dbg:
            self.out_c = self.nc.dram_tensor("out_c", [TC, D], F32, kind="ExternalOutput").ap()

        nc = self.nc
        self.banks = [self.es.enter_context(nc.psum_tensor(f"bank{i}", [128, 512], F32)) for i in range(8)]
        self.bank_res = [Res(f"bank{i}") for i in range(8)]
        self.bank_i = 0
        self.nrr = 6
        self.hT = self.sb([128, 8, T], F32, name="hT")
        self.r_h = [[Res(f"h{t}_{k}") for k in range(8)] for t in range(len(TT))]
        self.uT = self.sb([128, 8, T], BF16, name="uT")
        self.r_u = [Res(f"u{t}") for t in range(len(TT))]
        self.ident = self.sb([128, 128], F32, name="ident"); self.r_ident = Res("ident")
        self.identb = self.sb([128, 128], BF16, name="identb")
        self.ones = self.sb([128, 128], F32, name="ones")
        self.identr = self.sb([128, 128], F32, name="identr")
        self.onesr = self.sb([128, 128], F32, name="onesr")
        self.sT = self.sb([128, 8, 2], F32, name="sT"); self.r_sT = Res("sT")
        self.sTb = self.sb([128, 8, 2], BF16, name="sTb")
        self.mod = [self.sb([128, 48, 2], F32, name=f"mod{i}") for i in range(DEPTH)]
        self.r_mod = [Res(f"mod{i}") for i in range(DEPTH)]
        self.lnp = self.sb([128, 4, DEPTH, 8], F32, name="lnp"); self.r_lnp = Res("lnp")
        self.adab = self.sb([128, DEPTH, 48], F32, name="adab"); self.r_adab = Res("adab")
        self.vst = self.sb([64, 128], F32, name="vst"); self.r_vst = Res("vst")
        self.NW = 3
        self.wslot = [self.sb([128, 4096], BF16, name=f"wslot{i}") for i in range(self.NW)]
        self.r_wslot = [Res(f"wslot{i}") for i in range(self.NW)]
        self.w_i = 0

    def wnext(self):
        i = self.w_i
        self.w_i = (i + 1) % self.NW
        return self.wslot[i], self.r_wslot[i]

    def load_cols(self, src, n, dst, r_dst):
        S = self.S
        S.dma("sp", [(self.vst[0:n, :], src)], self.r_vst, writes=[self.r_vst])
        bk, rb = self.bank()
        S.op("pe", lambda e: e.transpose(bk[:, 0:n], self.vst[0:n, :], self.ident[0:n, 0:n]),
             reads=[self.r_vst, self.r_ident], writes=[rb], same_ok=True)
        S.op("dve", lambda e: e.tensor_copy(dst, bk[:, 0:n]), reads=[rb], writes=[r_dst])

    def setup(self):
        S, nc = self.S, self.nc
        S.op("pool", lambda e: e.memset(self.ident[:], 0.0), writes=[self.r_ident])
        S.op("pool", lambda e: e.affine_select(out=self.ident[:], in_=self.ident[:], compare_op=ALU.not_equal,
                                              fill=1.0, base=0, pattern=[[-1, 128]], channel_multiplier=1),
             reads=[self.r_ident], writes=[self.r_ident])
        S.op("dve", lambda e: e.tensor_copy(self.identb[:], self.ident[:]), reads=[self.r_ident], writes=[self.r_ident])
        S.op("dve", lambda e: e.memset(self.ones[:], 1.0), writes=[self.r_ident])
        S.op("dve", lambda e: e.tensor_copy(self.identr[:].bitcast(mybir.dt.float32r), self.ident[:]), reads=[self.r_ident], writes=[self.r_ident])
        S.op("dve", lambda e: e.tensor_copy(self.onesr[:].bitcast(mybir.dt.float32r), self.ones[:]), reads=[self.r_ident], writes=[self.r_ident])
        for k, n in enumerate(("ln1_g", "ln1_b", "ln2_g", "ln2_b")):
            for i in range(DEPTH):
                self.load_cols(self.din[n][i], 8, self.lnp[:, k, i, :], self.r_lnp)
        for i in range(DEPTH):
            self.load_cols(self.din["ada_b"][i], 48, self.adab[:, i, :], self.r_adab)
        self.load_cols(self.din["c"], 8, self.sT[:, :, 0], self.r_sT)
        self.load_cols(self.din["c_ctx"], 8, self.sT[:, :, 1], self.r_sT)
        S.op("act", lambda e: e.activation(out=self.sT[:], in_=self.sT[:], func=AF.Silu), reads=[self.r_sT], writes=[self.r_sT])
        S.op("dve", lambda e: e.tensor_copy(self.sTb[:], self.sT[:]), reads=[self.r_sT], writes=[self.r_sT])
        with ExitStack() as ph:
            xs = [self.sb([128, D], F32, es=ph) for _ in range(2)]
            r_xs = [Res("xs0"), Res("xs1")]
            for t in range(18):
                src = self.din["x"][t * 128:(t + 1) * 128, :] if t < 16 else self.din["ctx"][(t - 16) * 128:(t - 15) * 128, :]
                st, rs = xs[t % 2], r_xs[t % 2]
                S.dma("sp", [(st[:], src)], rs, writes=[rs])
                ti = min(t // 4, 4)
                for g in range(2):
                    bk, rb = self.bank()
                    for j in range(4):
                        S.op("pe", lambda e: e.transpose(bk[:, j * 128:(j + 1) * 128], st[:, (g * 4 + j) * 128:(g * 4 + j + 1) * 128], self.ident[:]),
                             reads=[rs, self.r_ident], writes=[rb], same_ok=True)
                    dst = self.hT[:, g * 4:(g + 1) * 4, t * 128:(t + 1) * 128]
                    srcp = bk[:, 0:512].rearrange("p (j n) -> p j n", j=4)
                    if g == 0:
                        S.op("dve", lambda e: e.tensor_copy(dst, srcp), reads=[rb], writes=self.r_h[ti][g * 4:(g + 1) * 4])
                    else:
                        S.op("act", lambda e: e.copy(dst, srcp), reads=[rb], writes=self.r_h[ti][g * 4:(g + 1) * 4])
            S.barrier()

    def mods_gen(self, i, stg, r_stg):
        S = self.S
        aw = self.din["ada_w"][i]
        bk, rb = self.banks[7], self.bank_res[7]
        for blk in range(24):
            st, rs = stg[blk % 2], r_stg[blk % 2]
            S.dma("pool", [(st[:], aw[:, blk * 256:(blk + 1) * 256].rearrange("(k p) n -> p k n", p=128))], rs, writes=[rs])
            for cc in range(2):
                c = blk * 2 + cc
                for kc in range(8):
                    S.op("pe", lambda e: e.matmul(bk[:, c * 2:(c + 1) * 2], lhsT=st[:, kc, cc * 128:(cc + 1) * 128], rhs=self.sTb[:, kc, :],
                                                  start=(kc == 0), stop=(kc == 7)),
                         reads=[rs, self.r_sT], writes=[rb], same_ok=True)
            yield
        m, rm = self.mod[i], self.r_mod[i]
        pv = bk[:, 0:96].rearrange("p (c l) -> p c l", l=2)
        for l in range(2):
            S.op("dve", lambda e: e.tensor_tensor(out=m[:, :, l], in0=pv[:, :, l], in1=self.adab[:, i, :], op=ALU.add),
                 reads=[rb, self.r_adab], writes=[rm])
        for c0 in (8, 32):
            S.op("dve", lambda e: e.tensor_scalar(out=m[:, c0:c0 + 8, :], in0=m[:, c0:c0 + 8, :], scalar1=1.0, scalar2=None, op0=ALU.add),
                 reads=[rm], writes=[rm])
        for c0 in (16, 40):
            S.op("dve", lambda e: e.tensor_scalar(out=m[:, c0:c0 + 8, :], in0=m[:, c0:c0 + 8, :], scalar1=1.0 / ALPHA, scalar2=None, op0=ALU.mult),
                 reads=[rm], writes=[rm])

    def mods(self, i):
        with ExitStack() as ph:
            stg = [self.sb([128, 8, 256], BF16, es=ph) for _ in range(2)]
            r_stg = [Res("as0"), Res("as1")]
            for _ in self.mods_gen(i, stg, r_stg):
                pass
            self.S.barrier()

    def mcol(self, i, which, kc, lc):
        return self.mod[i][:, which * 8 + kc, lc:lc + 1]

    def modulate_all(self, i, sub):
        S = self.S
        for ti, (t0, n, lc) in enumerate(TT):
            for kc in range(8):
                S.op("dve", lambda e: e.tensor_scalar(out=self.uT[:, kc, t0:t0 + n], in0=self.hT[:, kc, t0:t0 + n],
                                                      scalar1=self.mcol(i, 3 * sub + 1, kc, lc), scalar2=self.mcol(i, 3 * sub, kc, lc),
                                                      op0=ALU.mult, op1=ALU.add),
                     reads=[self.r_h[ti][kc], self.r_mod[i]], writes=[self.r_u[ti]])

    def layer_norm(self, i, sub, nxt):
        S = self.S
        R32 = mybir.dt.float32r
        with ExitStack() as ph:
            sq = [self.sb([128, 512], F32, es=ph) for _ in range(3)]; r_sq = [Res() for _ in range(3)]
            mean = [self.sb([128, 512], F32, es=ph) for _ in range(2)]; r_mean = [Res(), Res()]
            msq = [self.sb([128, 512], F32, es=ph) for _ in range(2)]; r_msq = [Res(), Res()]
            rstd = [self.sb([128, 512], F32, es=ph) for _ in range(2)]; r_rstd = [Res(), Res()]
            tmp = [self.sb([128, 512], F32, es=ph) for _ in range(3)]; r_tmp = [Res() for _ in range(3)]
            epsc = self.sb([128, 1], F32, es=ph); r_eps = Res()
            S.op("pool", lambda e: e.memset(epsc[:], EPS_LN), writes=[r_eps])
            gsc = self.sb([128, 8, 2], F32, es=ph); bsc = self.sb([128, 8, 2], F32, es=ph); r_gb = Res("gsc")
            if nxt is not None:
                ni, nsub = nxt
                msc = self.mod[ni][:, (3 * nsub + 1) * 8:(3 * nsub + 2) * 8, :]
                msh = self.mod[ni][:, (3 * nsub) * 8:(3 * nsub + 1) * 8, :]
                for l_ in range(2):
                    S.op("dve", lambda e: e.tensor_tensor(out=gsc[:, :, l_], in0=self.lnp[:, 2 * sub, i, :], in1=msc[:, :, l_], op=ALU.mult),
                         reads=[self.r_lnp, self.r_mod[ni]], writes=[r_gb])
                    S.op("dve", lambda e: e.tensor_tensor(out=bsc[:, :, l_], in0=self.lnp[:, 2 * sub + 1, i, :], in1=msc[:, :, l_], op=ALU.mult),
                         reads=[self.r_lnp, self.r_mod[ni]], writes=[r_gb])
                    S.op("dve", lambda e: e.tensor_tensor(out=bsc[:, :, l_], in0=bsc[:, :, l_], in1=msh[:, :, l_], op=ALU.add),
                         reads=[r_gb, self.r_mod[ni]], writes=[r_gb])
            banks = {}
            cnt = [0, 0]

            def stats(ti):
                t0, n, lc = TT[ti]
                rh = self.r_h[ti]
                b1, rb1 = self.bank()
                b2, rb2 = self.bank()
                banks[ti] = (b1, rb1, b2, rb2)
                for kc in range(8):
                    z = self.hT[:, kc, t0:t0 + n]
                    s_, rs_ = sq[cnt[0] % 3], r_sq[cnt[0] % 3]
                    cnt[0] += 1
                    S.op("act", lambda e: e.activation(out=s_[:, 0:n].bitcast(R32), in_=z, func=AF.Square), reads=[rh[kc]], writes=[rs_])
                    S.op("pe", lambda e: e.matmul(b1[:, 0:n], lhsT=self.ones[:], rhs=z, start=(kc == 0), stop=(kc == 7)),
                         reads=[rh[kc], self.r_ident], writes=[rb1], same_ok=True)
                    S.op("pe", lambda e: e.matmul(b2[:, 0:n], lhsT=self.onesr[:].bitcast(R32), rhs=s_[:, 0:n].bitcast(R32), start=(kc == 0), stop=(kc == 7)),
                         reads=[rs_, self.r_ident], writes=[rb2], same_ok=True)

            def finish_stats(ti):
                t0, n, lc = TT[ti]
                b1, rb1, b2, rb2 = banks[ti]
                mn, rmn = mean[ti % 2], r_mean[ti % 2]
                ms, rms = msq[ti % 2], r_msq[ti % 2]
                rs, rrs = rstd[ti % 2], r_rstd[ti % 2]
                S.op("act", lambda e: e.activation(out=mn[:, 0:n], in_=b1[:, 0:n], func=AF.Copy, scale=1.0 / D), reads=[rb1], writes=[rmn])
                S.op("dve", lambda e: e.tensor_tensor(out=ms[:, 0:n], in0=mn[:, 0:n], in1=mn[:, 0:n], op=ALU.mult), reads=[rmn], writes=[rms])
                S.op("dve", lambda e: e.scalar_tensor_tensor(out=rs[:, 0:n], in0=b2[:, 0:n], scalar=1.0 / D, in1=ms[:, 0:n],
                                                             op0=ALU.mult, op1=ALU.subtract), reads=[rb2, rms], writes=[rrs])
                S.op("act", lambda e: e.activation(out=rs[:, 0:n], in_=rs[:, 0:n], func=AF.Sqrt, bias=epsc[:, 0:1], scale=1.0),
                     reads=[rrs, r_eps], writes=[rrs])
                S.op("dve", lambda e: e.reciprocal(out=rs[:, 0:n], in_=rs[:, 0:n]), reads=[rrs], writes=[rrs])

            def normalize(ti):
                t0, n, lc = TT[ti]
                rh = self.r_h[ti]
                mn, rmn = mean[ti % 2], r_mean[ti % 2]
                rs, rrs = rstd[ti % 2], r_rstd[ti % 2]
                for kc in range(8):
                    z = self.hT[:, kc, t0:t0 + n]
                    tp, rtp = tmp[cnt[1] % 3], r_tmp[cnt[1] % 3]
                    cnt[1] += 1
                    S.op("pool", lambda e: e.tensor_tensor(out=tp[:, 0:n], in0=z, in1=mn[:, 0:n], op=ALU.subtract), reads=[rh[kc], rmn], writes=[rtp])
                    S.op("dve", lambda e: e.tensor_tensor(out=tp[:, 0:n], in0=tp[:, 0:n], in1=rs[:, 0:n], op=ALU.mult), reads=[rtp, rrs], writes=[rtp])
                    S.op("act", lambda e: e.activation(out=z, in_=tp[:, 0:n], func=AF.Identity,
                                                       bias=self.lnp[:, 2 * sub + 1, i, kc:kc + 1], scale=self.lnp[:, 2 * sub, i, kc:kc + 1]),
                         reads=[rtp, self.r_lnp], writes=[rh[kc]])
                    if nxt is not None:
                        S.op("dve", lambda e: e.tensor_scalar(out=self.uT[:, kc, t0:t0 + n], in0=tp[:, 0:n],
                                                              scalar1=gsc[:, kc, lc:lc + 1], scalar2=bsc[:, kc, lc:lc + 1],
                                                              op0=ALU.mult, op1=ALU.add),
                             reads=[rtp, r_gb], writes=[self.r_u[ti]])

            nt = len(TT) - 1 if self.skip_ctx else len(TT)
            stats(0)
            finish_stats(0)
            for ti in range(nt):
                if ti + 1 < nt:
                    stats(ti + 1)
                normalize(ti)
                if ti + 1 < nt:
                    finish_stats(ti + 1)
            S.barrier()

    def mlp(self, i):
        S = self.S
        w1 = self.din["mlp_w1"][i]
        w2 = self.din["mlp_w2"][i]
        with ExitStack() as ph:
            mg = None
            if i + 1 < DEPTH:
                mstg = [self.sb([128, 8, 256], BF16, es=ph) for _ in range(2)]
                mg = self.mods_gen(i + 1, mstg, [Res("ms0"), Res("ms1")])
            ab = [self.sb([128, 4, 512], BF16, es=ph) for _ in range(2)]; r_ab = [Res(), Res()]
            rl = [self.sb([128, 512], BF16, es=ph) for _ in range(3)]; r_rl = [Res(), Res(), Res()]
            rli = 0
            step = 0
            for j in range(8):
                wa, r_wa = self.wnext()
                wb, r_wb = self.wnext()
                S.dma("pool", [(wa[:].rearrange("p (k n) -> p k n", k=8), w1[:, j * 512:(j + 1) * 512].rearrange("(k p) n -> p k n", p=128))],
                      r_wa, writes=[r_wa])
                S.dma("pool", [(wb[:].rearrange("p (k n) -> p k n", k=4), w2[j * 512:(j + 1) * 512, :].rearrange("(k p) n -> p k n", p=128))],
                      r_wb, writes=[r_wb])
                wav = wa[:].rearrange("p (k n) -> p k n", k=8)
                wbv = wb[:].rearrange("p (k n) -> p k n", k=4)
                for ti, (t0, n, lc) in enumerate(TT):
                    if self.skip_ctx and lc == 1:
                        continue
                    a_, r_a = ab[step % 2], r_ab[step % 2]
                    step += 1
                    if mg is not None:
                        try:
                            next(mg)
                        except StopIteration:
                            mg = None
                    for hc in range(4):
                        bk, rb = self.bank()
                        for kc in range(8):
                            S.op("pe", lambda e: e.matmul(bk[:, 0:n], lhsT=wav[:, kc, hc * 128:(hc + 1) * 128], rhs=self.uT[:, kc, t0:t0 + n],
                                                          start=(kc == 0), stop=(kc == 7)),
                                 reads=[r_wa, self.r_u[ti]], writes=[rb], same_ok=True)
                        r_, rr_ = rl[rli % 3], r_rl[rli % 3]
                        rli += 1
                        S.op("act", lambda e: e.activation(out=r_[:, 0:n], in_=bk[:, 0:n], func=AF.Relu), reads=[rb], writes=[rr_])
                        S.op("pool", lambda e: e.tensor_tensor(out=a_[:, hc, 0:n], in0=r_[:, 0:n], in1=r_[:, 0:n], op=ALU.mult),
                             reads=[rr_], writes=[r_a])
                    for oc in range(8):
                        bk, rb = self.bank()
                        for kc in range(4):
                            S.op("pe", lambda e: e.matmul(bk[:, 0:n], lhsT=wbv[:, kc, oc * 128:(oc + 1) * 128], rhs=a_[:, kc, 0:n],
                                                          start=(kc == 0), stop=(kc == 3)),
                                 reads=[r_wb, r_a], writes=[rb], same_ok=True)
                        hz = self.hT[:, oc, t0:t0 + n]
                        S.op("dve", lambda e: e.scalar_tensor_tensor(out=hz, in0=bk[:, 0:n], scalar=self.mcol(i, 5, oc, lc), in1=hz,
                                                                     op0=ALU.mult, op1=ALU.add),
                             reads=[rb, self.r_h[ti][oc], self.r_mod[i]], writes=[self.r_h[ti][oc]])
            if mg is not None:
                for _ in mg:
                    pass
            S.barrier()

    def store_out(self):
        S = self.S
        with ExitStack() as ph:
            os_ = [self.sb([128, D], F32, es=ph) for _ in range(2)]
            r_os = [Res("os0"), Res("os1")]
            nt = 18 if self.dbg else 16
            for t in range(nt):
                st, rs = os_[t % 2], r_os[t % 2]
                ti = min(t // 4, 4)
                for g in range(2):
                    bk, rb = self.bank()
                    for j in range(4):
                        S.op("pe", lambda e: e.transpose(bk[:, j * 128:(j + 1) * 128], self.hT[:, g * 4 + j, t * 128:(t + 1) * 128], self.ident[:]),
                             reads=[self.r_h[ti][g * 4 + j], self.r_ident], writes=[rb], same_ok=True)
                    if g == 0:
                        S.op("dve", lambda e: e.tensor_copy(st[:, 0:512], bk[:, 0:512]), reads=[rb], writes=[rs])
                    else:
                        S.op("act", lambda e: e.copy(st[:, 512:1024], bk[:, 0:512]), reads=[rb], writes=[rs])
                dst = self.out[t * 128:(t + 1) * 128, :] if t < 16 else self.out_c[(t - 16) * 128:(t - 15) * 128, :]
                ev = S.dma("sp", [(dst, st[:])], rs, reads=[rs])
                S.out_events.append(ev)
            S.finish()

    def build(self):
        self.declare()
        self.setup()
        self.mods(0)
        self.modulate_all(0, 0)
        for i in range(self.depth_run):
            if i == DEPTH - 1 and not self.dbg:
                self.skip_ctx = True
            if self.mixers:
                self.mixer(i)
            self.layer_norm(i, 0, (i, 1))
            self.mlp(i)
            self.layer_norm(i, 1, (i + 1, 0) if i + 1 < DEPTH else None)
        self.store_out()


    def mixer(self, i):
        if i == 0:
            self.mla(i)
        elif i == 1:
            self.diff(i)
        elif i == 2:
            self.gla(i)
        elif i == 3:
            self.gdn(i)

    def mla(self, i):
        S = self.S
        SCALE = 96.0 ** -0.5
        with ExitStack() as ph:
            kp = [self.sb([128, T], BF16, es=ph) for _ in range(2)]; r_kp = [Res("kp0"), Res("kp1")]
            qtab = self.sb([128, T], F32, es=ph); r_qtab = Res("qtab")
            opad = self.sb([128, 2, 128], BF16, es=ph); r_opad = Res("opad")
            nrm = self.sb([128, 8], F32, es=ph); r_nrm = Res("nrm")
            epsc = self.sb([128, 1], F32, es=ph); r_eps = Res("eps")
            S.dma("sp", [(qtab[:], self.din["mla_qtab"])], r_qtab, writes=[r_qtab])
            S.op("pool", lambda e: e.memset(opad[:], 0.0), writes=[r_opad])
            S.op("pool", lambda e: e.memset(opad[:, 0, 0:64], 1.0), reads=[r_opad], writes=[r_opad])
            S.op("pool", lambda e: e.memset(opad[:, 1, 64:128], 1.0), reads=[r_opad], writes=[r_opad])
            S.op("pool", lambda e: e.memset(epsc[:], EPS), writes=[r_eps])
            self.load_cols(self.din["mla_q_norm"], 6, nrm[:, 0:6], r_nrm)
            self.load_cols(self.din["mla_kv_norm"], 2, nrm[:, 6:8], r_nrm)
            with ExitStack() as p1:
                raw = self.sb([128, 8, 512], F32, es=p1); r_raw = Res("raw")
                sq = [self.sb([128, 512], F32, es=p1) for _ in range(2)]; r_sq = [Res(), Res()]
                rs = self.sb([128, 2, 512], F32, es=p1); r_rs = Res("rs")
                kta = self.sb([128, 512], F32, es=p1); r_kta = Res("kta")
                ktb = self.sb([128, 512], F32, es=p1); r_ktb = Res("ktb")
                t1 = self.sb([128, 512], F32, es=p1); r_t1 = Res("t1")
                t2 = self.sb([128, 512], F32, es=p1); r_t2 = Res("t2")
                wi = []
                for blk in range(2):
                    w_, r_w = self.wnext()
                    S.dma("pool", [(w_[:].rearrange("p (k n) -> p k n", k=8),
                                    self.din["mla_w_in"][:, blk * 512:(blk + 1) * 512].rearrange("(k p) n -> p k n", p=128))], r_w, writes=[r_w])
                    wi.append((w_[:].rearrange("p (k n) -> p k n", k=8), r_w))
                w_, r_wk = self.wnext()
                wkr = w_[:, 0:2048].rearrange("p (k n) -> p k n", k=8)
                S.dma("pool", [(wkr, self.din["mla_wkr"].rearrange("(k p) n -> p k n", p=128))], r_wk, writes=[r_wk])
                for ti, (t0, n, lc) in enumerate(TT):
                    ru = self.r_u[ti]
                    S.dma("sp", [(kta[:, 0:n], self.din["mla_kta"][:, t0:t0 + n])], r_kta, writes=[r_kta])
                    S.dma("sp", [(ktb[:, 0:n], self.din["mla_ktb"][:, t0:t0 + n])], r_ktb, writes=[r_ktb])
                    bA, rbA = self.bank()
                    bB, rbB = self.bank()
                    for kc in range(8):
                        S.op("pe", lambda e: e.matmul(bA[:, 0:n], lhsT=wkr[:, kc, 0:128], rhs=self.uT[:, kc, t0:t0 + n], start=(kc == 0), stop=(kc == 7)),
                             reads=[r_wk, ru], writes=[rbA], same_ok=True)
                    for kc in range(8):
                        S.op("pe", lambda e: e.matmul(bB[:, 0:n], lhsT=wkr[:, kc, 128:256], rhs=self.uT[:, kc, t0:t0 + n], start=(kc == 0), stop=(kc == 7)),
                             reads=[r_wk, ru], writes=[rbB], same_ok=True)
                    S.op("dve", lambda e: e.tensor_tensor(out=t1[64:128, 0:n], in0=bA[64:128, 0:n], in1=kta[64:128, 0:n], op=ALU.mult),
                         reads=[rbA, r_kta], writes=[r_t1])
                    S.op("dve", lambda e: e.tensor_tensor(out=t2[64:128, 0:n], in0=bB[64:128, 0:n], in1=ktb[64:128, 0:n], op=ALU.mult),
                         reads=[rbB, r_ktb], writes=[r_t2])
                    S.op("pool", lambda e: e.tensor_tensor(out=kp[0][64:128, t0:t0 + n], in0=t1[64:128, 0:n], in1=t2[64:128, 0:n], op=ALU.add),
                         reads=[r_t1, r_t2], writes=[r_kp[0]])
                    S.op("pool", lambda e: e.tensor_copy(kp[1][64:128, t0:t0 + n], kp[0][64:128, t0:t0 + n]), reads=[r_kp[0]], writes=[r_kp[1]])
                    for oc in range(8):
                        wv, r_w = wi[oc // 4]
                        bk, rb = self.bank()
                        for kc in range(8):
                            S.op("pe", lambda e: e.matmul(bk[:, 0:n], lhsT=wv[:, kc, (oc % 4) * 128:(oc % 4 + 1) * 128], rhs=self.uT[:, kc, t0:t0 + n],
                                                          start=(kc == 0), stop=(kc == 7)),
                                 reads=[r_w, ru], writes=[rb], same_ok=True)
                        if oc % 2 == 0:
                            S.op("dve", lambda e: e.tensor_copy(raw[:, oc, 0:n], bk[:, 0:n]), reads=[rb], writes=[r_raw])
                        else:
                            S.op("act", lambda e: e.copy(raw[:, oc, 0:n], bk[:, 0:n]), reads=[rb], writes=[r_raw])
                    bq, rbq = self.banks[6], self.bank_res[6]
                    bkv, rbkv = self.banks[7], self.bank_res[7]
                    for oc in range(8):
                        s_, rs_ = sq[oc % 2], r_sq[oc % 2]
                        S.op("act", lambda e: e.activation(out=s_[:, 0:n], in_=raw[:, oc, 0:n], func=AF.Square), reads=[r_raw], writes=[rs_])
                        if oc < 6:
                            S.op("pe", lambda e: e.matmul(bq[:, 0:n], lhsT=self.ones[:], rhs=s_[:, 0:n], start=(oc == 0), stop=(oc == 5)),
                                 reads=[rs_, self.r_ident], writes=[rbq], same_ok=True)
                        else:
                            S.op("pe", lambda e: e.matmul(bkv[:, 0:n], lhsT=self.ones[:], rhs=s_[:, 0:n], start=(oc == 6), stop=(oc == 7)),
                                 reads=[rs_, self.r_ident], writes=[rbkv], same_ok=True)
                    for g, (bb, rbb, dim) in enumerate(((bq, rbq, 768.0), (bkv, rbkv, 256.0))):
                        S.op("act", lambda e: e.activation(out=rs[:, g, 0:n], in_=bb[:, 0:n], func=AF.Sqrt, bias=epsc[:, 0:1], scale=1.0 / dim),
                             reads=[rbb, r_eps], writes=[r_rs])
                        S.op("dve", lambda e: e.reciprocal(out=rs[:, g, 0:n], in_=rs[:, g, 0:n]), reads=[r_rs], writes=[r_rs])
                    for oc in range(8):
                        g = 0 if oc < 6 else 1
                        S.op("dve", lambda e: e.scalar_tensor_tensor(out=self.uT[:, oc, t0:t0 + n], in0=raw[:, oc, 0:n], scalar=nrm[:, oc:oc + 1],
                                                                     in1=rs[:, g, 0:n], op0=ALU.mult, op1=ALU.mult),
                             reads=[r_raw, r_nrm, r_rs], writes=[ru])
                S.barrier()
            with ExitStack() as p2:
                qp = [self.sb([128, T], BF16, es=p2) for _ in range(2)]; r_qp = [Res("qp0"), Res("qp1")]
                vp = [self.sb([128, 18, 128], BF16, es=p2) for _ in range(2)]; r_vp = [Res("vp0"), Res("vp1")]
                pt = [self.sb([128, 512], BF16, es=p2) for _ in range(4)]; r_pt = [Res() for _ in range(4)]
                rden = [self.sb([128, 512], F32, es=p2) for _ in range(2)]; r_rden = [Res("rden0"), Res("rden1")]
                attn_ctr = [0]
                pending = [None]
                self.nrr = 4
                self.bank_i = 0
                opr = [self.sb([128, 512], BF16, es=p2) for _ in range(2)]; r_opr = [Res(), Res()]
                wo = [self.sb([128, D], BF16, es=p2) for _ in range(2)]; r_wo = [Res("wo0"), Res("wo1")]
                for par in range(2):
                    S.op("pool", lambda e: e.memset(vp[par][:], 0.0), writes=[r_vp[par]])
                pti = 0
                wq = wkv = None
                for pair in range(8):
                    S.dma("pool", [(wo[pair % 2][:], self.din["mla_w_out"][pair * 128:(pair + 1) * 128, :])], r_wo[pair % 2], writes=[r_wo[pair % 2]])
                    for par in range(2):
                        h = pair * 2 + par
                        hl = h % 4
                        if hl == 0:
                            w_, r_wq = self.wnext()
                            wq = w_[:, 0:3072].rearrange("p (k n) -> p k n", k=6)
                            wkv = w_[:, 3072:4096].rearrange("p (k n) -> p k n", k=2)
                            S.dma("pool", [(wq, self.din["mla_wqx"][:, h * 128:(h + 4) * 128].rearrange("(k p) n -> p k n", p=128)),
                                           (wkv, self.din["mla_w_kvb"][:, h * 128:(h + 4) * 128].rearrange("(k p) n -> p k n", p=128))],
                                  r_wq, writes=[r_wq])
                        for ti, (t0, n, lc) in enumerate(TT):
                            ru = self.r_u[ti]
                            bk, rb = self.bank()
                            for kc in range(6):
                                S.op("pe", lambda e: e.matmul(bk[:, 0:n], lhsT=wq[:, kc, hl * 128:(hl + 1) * 128], rhs=self.uT[:, kc, t0:t0 + n],
                                                              start=(kc == 0), stop=(kc == 5)),
                                     reads=[r_wq, ru], writes=[rb], same_ok=True)
                            S.op("dve", lambda e: e.tensor_tensor(out=qp[par][:, t0:t0 + n], in0=bk[:, 0:n], in1=qtab[:, t0:t0 + n], op=ALU.mult),
                                 reads=[rb, r_qtab], writes=[r_qp[par]])
                            bk, rb = self.bank()
                            for kc in range(2):
                                S.op("pe", lambda e: e.matmul(bk[0:64, 0:n], lhsT=wkv[:, kc, hl * 128:hl * 128 + 64], rhs=self.uT[:, 6 + kc, t0:t0 + n],
                                                              start=(kc == 0), stop=(kc == 1)),
                                     reads=[r_wq, ru], writes=[rb], same_ok=True)
                            S.op("dve", lambda e: e.tensor_copy(kp[par][0:64, t0:t0 + n], bk[0:64, 0:n]), reads=[rb], writes=[r_kp[par]])
                        for g0 in range(0, 18, 8):
                            ng = min(8, 18 - g0)
                            bk, rb = self.bank()
                            for jt in range(ng):
                                kt = g0 + jt
                                for kc in range(2):
                                    S.op("pe", lambda e: e.matmul(bk[:, jt * 64:(jt + 1) * 64], lhsT=self.uT[:, 6 + kc, kt * 128:(kt + 1) * 128],
                                                                  rhs=wkv[:, kc, hl * 128 + 64:hl * 128 + 128], start=(kc == 0), stop=(kc == 1)),
                                         reads=[r_wq] + self.r_u, writes=[rb], same_ok=True)
                            S.op("dve", lambda e: e.tensor_copy(vp[par][:, g0:g0 + ng, par * 64:par * 64 + 64],
                                                                bk[:, 0:ng * 64].rearrange("p (j d) -> p j d", d=64)), reads=[rb], writes=[r_vp[par]])
                    for ti, (t0, n, lc) in enumerate(TT):
                        kts = list(range(18)) if lc == 0 else [16, 17]
                        items = [(par, kt) for par in range(2) for kt in kts]
                        nb_ = attn_ctr[0] % 2
                        attn_ctr[0] += 1
                        num, r_num = self.banks[4 + 2 * nb_], self.bank_res[4 + 2 * nb_]
                        den, r_den = self.banks[5 + 2 * nb_], self.bank_res[5 + 2 * nb_]
                        sbanks = {}

                        def issue_score(ix):
                            par, kt = items[ix]
                            bk, rb = self.bank()
                            S.op("pe", lambda e: e.matmul(bk[:, 0:n], lhsT=kp[par][:, kt * 128:(kt + 1) * 128], rhs=qp[par][:, t0:t0 + n], start=True, stop=True),
                                 reads=[r_kp[par], r_qp[par]], writes=[rb], same_ok=True)
                            sbanks[ix] = (bk, rb)
                        for ix in range(min(2, len(items))):
                            issue_score(ix)
                        for ix, (par, kt) in enumerate(items):
                            bk, rb = sbanks.pop(ix)
                            p_, rp_ = pt[pti % 4], r_pt[pti % 4]
                            pti += 1
                            S.op("act", lambda e: e.activation(out=p_[:, 0:n], in_=bk[:, 0:n], func=AF.Exp, scale=SCALE), reads=[rb], writes=[rp_])
                            if ix + 2 < len(items):
                                issue_score(ix + 2)
                            first = (ix == 0)
                            last = (ix == len(items) - 1)
                            S.op("pe", lambda e: e.matmul(num[:, 0:n], lhsT=vp[par][:, kt, :], rhs=p_[:, 0:n], start=first, stop=last),
                                 reads=[r_vp[par], rp_], writes=[r_num], same_ok=True)
                            S.op("pe", lambda e: e.matmul(den[:, 0:n], lhsT=opad[:, par, :], rhs=p_[:, 0:n], start=first, stop=last),
                                 reads=[r_opad, rp_], writes=[r_den], same_ok=True)

                        def epilogue(ti=ti, t0=t0, n=n, lc=lc, num=num, den=den, r_num=r_num, r_den=r_den, pair=pair, k=attn_ctr[0]):
                            rd_, rrd_ = rden[k % 2], r_rden[k % 2]
                            S.op("dve", lambda e: e.reciprocal(out=rd_[:, 0:n], in_=den[:, 0:n]), reads=[r_den], writes=[rrd_])
                            o_, ro_ = opr[k % 2], r_opr[k % 2]
                            S.op("dve", lambda e: e.tensor_tensor(out=o_[:, 0:n], in0=num[:, 0:n], in1=rd_[:, 0:n], op=ALU.mult),
                                 reads=[r_num, rrd_], writes=[ro_])
                            for oc in range(8):
                                bk, rb = self.bank()
                                S.op("pe", lambda e: e.matmul(bk[:, 0:n], lhsT=wo[pair % 2][:, oc * 128:(oc + 1) * 128], rhs=o_[:, 0:n], start=True, stop=True),
                                     reads=[r_wo[pair % 2], ro_], writes=[rb], same_ok=True)
                                hz = self.hT[:, oc, t0:t0 + n]
                                S.op("dve", lambda e: e.scalar_tensor_tensor(out=hz, in0=bk[:, 0:n], scalar=self.mcol(i, 2, oc, lc), in1=hz,
                                                                             op0=ALU.mult, op1=ALU.add),
                                     reads=[rb, self.r_h[ti][oc], self.r_mod[i]], writes=[self.r_h[ti][oc]])
                        if pending[0] is not None:
                            pending[0]()
                        pending[0] = epilogue
                if pending[0] is not None:
                    pending[0]()
                S.barrier()
        self.nrr = 6
        self.bank_i = 0

    def diff(self, i):
        S = self.S
        SCALE = 64.0 ** -0.5
        lam_init = 0.8 - 0.6 * math.exp(-0.3 * i)
        self.nrr = 4
        self.bank_i = 0
        with ExitStack() as ph:
            qp = [self.sb([128, T], BF16, es=ph) for _ in range(2)]; r_qp = [Res("qp0"), Res("qp1")]
            kp = [self.sb([128, T], BF16, es=ph) for _ in range(2)]; r_kp = [Res("kp0"), Res("kp1")]
            vp = self.sb([128, 18, 128], BF16, es=ph); r_vp = Res("vp")
            qtab = self.sb([128, 512], F32, es=ph); r_qtab = Res("qtab")
            kta = self.sb([128, 512], F32, es=ph); r_kta = Res("kta")
            ktb = self.sb([128, 512], F32, es=ph); r_ktb = Res("ktb")
            t1 = self.sb([128, 512], F32, es=ph); r_t1 = Res("t1")
            t2 = self.sb([128, 512], F32, es=ph); r_t2 = Res("t2")
            t1s = [self.sb([128, 512], F32, es=ph) for _ in range(2)]; r_t1s = [Res("t1s0"), Res("t1s1")]
            dctr = [0]
            pend1 = [None]
            pt = [self.sb([128, 512], BF16, es=ph) for _ in range(4)]; r_pt = [Res() for _ in range(4)]
            rd = self.sb([128, 2, 512], F32, es=ph); r_rd0 = Res("rd0"); r_rd1 = Res("rd1")
            onb = self.sb([128, 128], BF16, es=ph); r_onb = Res("onb")
            on_ = [self.sb([128, 512], BF16, es=ph) for _ in range(2)]; r_on = [Res(), Res()]
            wo = [self.sb([128, D], BF16, es=ph) for _ in range(2)]; r_wo = [Res("wo0"), Res("wo1")]
            lam = self.sb([128, 4], F32, es=ph); r_lam = Res("lam")
            lv = self.sb([64, 4], F32, es=ph); r_lv = Res("lv")
            sub = self.sb([128, 1], F32, es=ph); r_sub = Res("sub")
            epsc = self.sb([128, 1], F32, es=ph); r_eps = Res("eps")
            S.op("pool", lambda e: e.memset(epsc[:], EPS), writes=[r_eps])
            S.op("pool", lambda e: e.memset(onb[:], 1.0), writes=[r_onb])
            S.dma("sp", [(lv[:], self.din["diff_lam"])], r_lv, writes=[r_lv])
            S.op("dve", lambda e: e.tensor_tensor(out=lv[:, 0:1], in0=lv[:, 0:1], in1=lv[:, 1:2], op=ALU.mult), reads=[r_lv], writes=[r_lv])
            S.op("dve", lambda e: e.tensor_tensor(out=lv[:, 1:2], in0=lv[:, 2:3], in1=lv[:, 3:4], op=ALU.mult), reads=[r_lv], writes=[r_lv])
            bk, rb = self.bank()
            S.op("pe", lambda e: e.matmul(bk[:, 0:2], lhsT=self.ones[0:64, :], rhs=lv[:, 0:2], start=True, stop=True),
                 reads=[r_lv, self.r_ident], writes=[rb], same_ok=True)
            S.op("act", lambda e: e.activation(out=lam[:, 0:2], in_=bk[:, 0:2], func=AF.Exp), reads=[rb], writes=[r_lam])
            S.op("dve", lambda e: e.scalar_tensor_tensor(out=lam[:, 2:3], in0=lam[:, 1:2], scalar=-lam_init, in1=lam[:, 0:1], op0=ALU.add, op1=ALU.subtract),
                 reads=[r_lam], writes=[r_lam])
            self.load_cols(self.din["diff_subln"], 1, sub[:, 0:1], r_sub)
            S.op("dve", lambda e: e.tensor_scalar(out=sub[:], in0=sub[:], scalar1=1.0 - lam_init, scalar2=None, op0=ALU.mult), reads=[r_sub], writes=[r_sub])
            nums = [(self.banks[4], self.bank_res[4]), (self.banks[5], self.bank_res[5])]
            dens = [(self.banks[6], self.bank_res[6]), (self.banks[7], self.bank_res[7])]
            pti = 0
            pti = 0
            for h in range(8):
                S.dma("pool", [(wo[h % 2][:], self.din["diff_w_out"][h * 128:(h + 1) * 128, :])], r_wo[h % 2], writes=[r_wo[h % 2]])
                wv = None
                for m in range(2):
                    mi = h * 2 + m
                    w_, r_w = self.wnext()
                    wx = w_[:, 0:3072].rearrange("p (k n) -> p k n", k=8)
                    prs = [(wx, self.din["diff_wx"][:, mi * 384:(mi + 1) * 384].rearrange("(k p) n -> p k n", p=128))]
                    if m == 0:
                        wv = w_[:, 3072:4096].rearrange("p (k n) -> p k n", k=8)
                        r_wv = r_w
                        prs.append((wv, self.din["diff_wv"][:, h * 128:(h + 1) * 128].rearrange("(k p) n -> p k n", p=128)))
                    S.dma("pool", prs, r_w, writes=[r_w])
                    for ti, (t0, n, lc) in enumerate(TT):
                        ru = self.r_u[ti]
                        S.dma("sp", [(qtab[:, 0:n], self.din["diff_qtab"][:, t0:t0 + n])], r_qtab, writes=[r_qtab])
                        S.dma("sp", [(kta[:, 0:n], self.din["diff_kta"][:, t0:t0 + n])], r_kta, writes=[r_kta])
                        S.dma("sp", [(ktb[:, 0:n], self.din["diff_ktb"][:, t0:t0 + n])], r_ktb, writes=[r_ktb])
                        bq, rbq = self.bank()
                        for kc in range(8):
                            S.op("pe", lambda e: e.matmul(bq[:, 0:n], lhsT=wx[:, kc, 0:128], rhs=self.uT[:, kc, t0:t0 + n], start=(kc == 0), stop=(kc == 7)),
                                 reads=[r_w, ru], writes=[rbq], same_ok=True)
                        S.op("dve", lambda e: e.tensor_tensor(out=qp[m][:, t0:t0 + n], in0=bq[:, 0:n], in1=qtab[:, 0:n], op=ALU.mult),
                             reads=[rbq, r_qtab], writes=[r_qp[m]])
                        bA, rbA = self.bank()
                        for kc in range(8):
                            S.op("pe", lambda e: e.matmul(bA[:, 0:n], lhsT=wx[:, kc, 128:256], rhs=self.uT[:, kc, t0:t0 + n], start=(kc == 0), stop=(kc == 7)),
                                 reads=[r_w, ru], writes=[rbA], same_ok=True)
                        bB, rbB = self.bank()
                        for kc in range(8):
                            S.op("pe", lambda e: e.matmul(bB[:, 0:n], lhsT=wx[:, kc, 256:384], rhs=self.uT[:, kc, t0:t0 + n], start=(kc == 0), stop=(kc == 7)),
                                 reads=[r_w, ru], writes=[rbB], same_ok=True)
                        S.op("dve", lambda e: e.tensor_tensor(out=t1[:, 0:n], in0=bA[:, 0:n], in1=kta[:, 0:n], op=ALU.mult), reads=[rbA, r_kta], writes=[r_t1])
                        S.op("dve", lambda e: e.tensor_tensor(out=t2[:, 0:n], in0=bB[:, 0:n], in1=ktb[:, 0:n], op=ALU.mult), reads=[rbB, r_ktb], writes=[r_t2])
                        S.op("pool", lambda e: e.tensor_tensor(out=kp[m][:, t0:t0 + n], in0=t1[:, 0:n], in1=t2[:, 0:n], op=ALU.add),
                             reads=[r_t1, r_t2], writes=[r_kp[m]])
                for g0 in range(0, 18, 4):
                    ng = min(4, 18 - g0)
                    bk, rb = self.bank()
                    for jt in range(ng):
                        kt = g0 + jt
                        for kc in range(8):
                            S.op("pe", lambda e: e.matmul(bk[:, jt * 128:(jt + 1) * 128], lhsT=self.uT[:, kc, kt * 128:(kt + 1) * 128],
                                                          rhs=wv[:, kc, :], start=(kc == 0), stop=(kc == 7)),
                                 reads=[r_wv] + self.r_u, writes=[rb], same_ok=True)
                    S.op("act", lambda e: e.copy(vp[:, g0:g0 + ng, :], bk[:, 0:ng * 128].rearrange("p (j d) -> p j d", d=128)), reads=[rb], writes=[r_vp])
                for ti, (t0, n, lc) in enumerate(TT):
                    kts = list(range(18)) if lc == 0 else [16, 17]
                    k_ = dctr[0]
                    dctr[0] += 1
                    t1_, rt1_ = t1s[k_ % 2], r_t1s[k_ % 2]

                    def attend(m):
                        nonlocal pti
                        num, r_num = nums[m]
                        den, r_den = dens[m]
                        sbanks = {}

                        def issue_score(ix):
                            kt = kts[ix]
                            bk, rb = self.bank()
                            S.op("pe", lambda e: e.matmul(bk[:, 0:n], lhsT=kp[m][:, kt * 128:(kt + 1) * 128], rhs=qp[m][:, t0:t0 + n], start=True, stop=True),
                                 reads=[r_kp[m], r_qp[m]], writes=[rb], same_ok=True)
                            sbanks[ix] = (bk, rb)
                        for ix in range(min(2, len(kts))):
                            issue_score(ix)
                        for ix, kt in enumerate(kts):
                            bk, rb = sbanks.pop(ix)
                            p_, rp_ = pt[pti % 4], r_pt[pti % 4]
                            pti += 1
                            S.op("act", lambda e: e.activation(out=p_[:, 0:n], in_=bk[:, 0:n], func=AF.Exp, scale=SCALE), reads=[rb], writes=[rp_])
                            if ix + 2 < len(kts):
                                issue_score(ix + 2)
                            first, last = (ix == 0), (ix == len(kts) - 1)
                            S.op("pe", lambda e: e.matmul(num[:, 0:n], lhsT=vp[:, kt, :], rhs=p_[:, 0:n], start=first, stop=last),
                                 reads=[r_vp, rp_], writes=[r_num], same_ok=True)
                            S.op("pe", lambda e: e.matmul(den[:, 0:n], lhsT=onb[:], rhs=p_[:, 0:n], start=first, stop=last),
                                 reads=[r_onb, rp_], writes=[r_den], same_ok=True)

                    def ep0(n=n, t1_=t1_, rt1_=rt1_):
                        S.op("dve", lambda e: e.reciprocal(out=rd[:, 0, 0:n], in_=dens[0][0][:, 0:n]), reads=[dens[0][1]], writes=[r_rd0])
                        S.op("dve", lambda e: e.tensor_tensor(out=t1_[:, 0:n], in0=nums[0][0][:, 0:n], in1=rd[:, 0, 0:n], op=ALU.mult),
                             reads=[nums[0][1], r_rd0], writes=[rt1_])

                    def ep1(ti=ti, t0=t0, n=n, lc=lc, t1_=t1_, rt1_=rt1_, h=h, k_=k_):
                        S.op("dve", lambda e: e.reciprocal(out=rd[:, 1, 0:n], in_=dens[1][0][:, 0:n]), reads=[dens[1][1]], writes=[r_rd1])
                        S.op("dve", lambda e: e.scalar_tensor_tensor(out=t2[:, 0:n], in0=nums[1][0][:, 0:n], scalar=lam[:, 2:3], in1=rd[:, 1, 0:n],
                                                                     op0=ALU.mult, op1=ALU.mult), reads=[nums[1][1], r_rd1, r_lam], writes=[r_t2])
                        S.op("dve", lambda e: e.tensor_tensor(out=t1_[:, 0:n], in0=t1_[:, 0:n], in1=t2[:, 0:n], op=ALU.add), reads=[rt1_, r_t2], writes=[rt1_])
                        S.op("dve", lambda e: e.tensor_tensor(out=t2[:, 0:n], in0=t1_[:, 0:n], in1=t1_[:, 0:n], op=ALU.mult), reads=[rt1_], writes=[r_t2])
                        bk, rb = self.bank()
                        S.op("pe", lambda e: e.matmul(bk[:, 0:n], lhsT=self.ones[:], rhs=t2[:, 0:n], start=True, stop=True),
                             reads=[r_t2, self.r_ident], writes=[rb], same_ok=True)
                        S.op("act", lambda e: e.activation(out=t2[:, 0:n], in_=bk[:, 0:n], func=AF.Sqrt, bias=epsc[:, 0:1], scale=1.0 / 128.0),
                             reads=[rb, r_eps], writes=[r_t2])
                        S.op("dve", lambda e: e.reciprocal(out=t2[:, 0:n], in_=t2[:, 0:n]), reads=[r_t2], writes=[r_t2])
                        o_, ro_ = on_[k_ % 2], r_on[k_ % 2]
                        S.op("dve", lambda e: e.scalar_tensor_tensor(out=o_[:, 0:n], in0=t1_[:, 0:n], scalar=sub[:, 0:1], in1=t2[:, 0:n],
                                                                     op0=ALU.mult, op1=ALU.mult), reads=[rt1_, r_t2, r_sub], writes=[ro_])
                        for oc in range(8):
                            bk, rb = self.bank()
                            S.op("pe", lambda e: e.matmul(bk[:, 0:n], lhsT=wo[h % 2][:, oc * 128:(oc + 1) * 128], rhs=o_[:, 0:n], start=True, stop=True),
                                 reads=[r_wo[h % 2], ro_], writes=[rb], same_ok=True)
                            hz = self.hT[:, oc, t0:t0 + n]
                            S.op("dve", lambda e: e.scalar_tensor_tensor(out=hz, in0=bk[:, 0:n], scalar=self.mcol(i, 2, oc, lc), in1=hz,
                                                                         op0=ALU.mult, op1=ALU.add),
                                 reads=[rb, self.r_h[ti][oc], self.r_mod[i]], writes=[self.r_h[ti][oc]])
                    attend(0)
                    if pend1[0] is not None:
                        pend1[0]()
                    attend(1)
                    ep0()
                    pend1[0] = ep1
            if pend1[0] is not None:
                pend1[0]()
            S.barrier()
        self.nrr = 6
        self.bank_i = 0

    def gla(self, i):
        S = self.S
        QS = 128.0 ** -0.5
        win = self.din["gla_w_in"]
        with ExitStack() as ph:
            arr = [[self.sb([128, T], BF16, es=ph) for _ in range(3)] for _ in range(2)]
            r_arr = [[Res() for _ in range(3)] for _ in range(2)]
            vh = self.sb([128, 18, 256], BF16, es=ph); r_vh = Res("vh")
            oacc = self.sb([128, 2, T], BF16, es=ph); r_oacc = [Res(f"oacc{c}") for c in range(18)]
            rT = self.sb([32, T], BF16, es=ph); r_rT = Res("rT")
            gw = self.sb([32, 2, 512], BF16, es=ph); r_gw = Res("gw")
            gb = self.sb([128, 2, 4], F32, es=ph); r_gb = Res("gb")
            ng = self.sb([128, 2], F32, es=ph); r_ng = Res("ng")
            dec = self.sb([128, 2, 18], F32, es=ph); r_dec = Res("dec")
            cmask = self.sb([128, 512], F32, es=ph); r_cm = Res("cmask")
            msk = [self.sb([128, 128], F32, es=ph) for _ in range(2)]; r_msk = Res("msk")
            bA = self.sb([128, 512], F32, es=ph); r_bA = Res("bA")
            bB = self.sb([128, 512], F32, es=ph); r_bB = Res("bB")
            bC = self.sb([128, 512], F32, es=ph); r_bC = Res("bC")
            bD = self.sb([128, 512], F32, es=ph); r_bD = Res("bD")
            Sf = [self.sb([128, 256], F32, es=ph) for _ in range(2)]; r_Sf = [Res("Sf0"), Res("Sf1")]
            Sb = [self.sb([128, 256], BF16, es=ph) for _ in range(2)]; r_Sb = [Res("Sb0"), Res("Sb1")]
            Am = [self.sb([128, 128], BF16, es=ph) for _ in range(2)]; r_Am = [Res(), Res()]
            keT = [self.sb([128, 128], BF16, es=ph) for _ in range(2)]; r_keT = [Res(), Res()]
            ogn = [self.sb([128, 2, 512], BF16, es=ph) for _ in range(1)]; r_ogn = [Res()]
            epsc = self.sb([128, 1], F32, es=ph); r_eps = Res("eps")
            S.op("pool", lambda e: e.memset(epsc[:], EPS), writes=[r_eps])
            S.op("pool", lambda e: e.memset(cmask[:], 1.0), writes=[r_cm])
            for c in range(4):
                S.op("pool", lambda e: e.memset(cmask[:, c * 128:c * 128 + 1], 0.0), reads=[r_cm], writes=[r_cm])
            for d in range(2):
                S.op("pool", lambda e: e.memset(msk[d][:], 1.0), reads=[r_msk], writes=[r_msk])
                cm, pat = ((-1, [[1, 128]]) if d == 0 else (1, [[-1, 128]]))
                S.op("pool", lambda e: e.affine_select(out=msk[d][:], in_=msk[d][:], compare_op=ALU.is_ge, fill=0.0, base=0,
                                                      pattern=pat, channel_multiplier=cm), reads=[r_msk], writes=[r_msk])
            S.op("pool", lambda e: e.memset(gw[:], 0.0), writes=[r_gw])
            S.dma("pool", [(gw[0:16, 0, :], self.din["gla_gw"][0]), (gw[16:32, 1, :], self.din["gla_gw"][1])], r_gw, reads=[r_gw], writes=[r_gw])
            for d in range(2):
                self.load_cols(self.din["gla_gb"][d], 4, gb[:, d, :], r_gb)
            S.op("dve", lambda e: e.tensor_scalar(out=gb[:], in0=gb[:], scalar1=-1.0, scalar2=None, op0=ALU.mult), reads=[r_gb], writes=[r_gb])
            self.load_cols(self.din["gla_norm"], 2, ng[:, 0:2], r_ng)
            w_, r_w = self.wnext()
            wr = w_[:, 0:256].rearrange("p (k n) -> p k n", k=8)
            S.dma("pool", [(wr, win[:, 3072:3104].rearrange("(k p) n -> p k n", p=128))], r_w, writes=[r_w])
            for ti, (t0, n, lc) in enumerate(TT):
                bk, rb = self.bank()
                for kc in range(8):
                    S.op("pe", lambda e: e.matmul(bk[0:32, 0:n], lhsT=wr[:, kc, :], rhs=self.uT[:, kc, t0:t0 + n], start=(kc == 0), stop=(kc == 7)),
                         reads=[r_w, self.r_u[ti]], writes=[rb], same_ok=True)
                S.op("act", lambda e: e.copy(rT[:, t0:t0 + n], bk[0:32, 0:n]), reads=[rb], writes=[r_rT])

            for h in range(4):
                wA_, r_wA = self.wnext()
                wqk = wA_[:, 0:2048].rearrange("p (k n) -> p k n", k=8)
                S.dma("pool", [(wqk[:, :, 0:128], win[:, h * 128:(h + 1) * 128].rearrange("(k p) n -> p k n", p=128)),
                               (wqk[:, :, 128:256], win[:, 512 + h * 128:512 + (h + 1) * 128].rearrange("(k p) n -> p k n", p=128))],
                      r_wA, writes=[r_wA])
                wB_, r_wB = self.wnext()
                wv = wB_[:, 0:2048].rearrange("p (k n) -> p k n", k=8)
                wg = wB_[:, 2048:4096].rearrange("p (k n) -> p k n", k=8)
                S.dma("pool", [(wv, win[:, 1024 + h * 256:1024 + (h + 1) * 256].rearrange("(k p) n -> p k n", p=128)),
                               (wg, win[:, 2048 + h * 256:2048 + (h + 1) * 256].rearrange("(k p) n -> p k n", p=128))],
                      r_wB, writes=[r_wB])
                wC_, r_wC = self.wnext()
                wo = wC_[:, 0:2048].rearrange("p (k n) -> p k n", k=2)
                S.dma("pool", [(wo, self.din["gla_w_out"][h * 256:(h + 1) * 256, :].rearrange("(k p) n -> p k n", p=128))], r_wC, writes=[r_wC])
                for g0 in range(0, 18, 2):
                    bk, rb = self.bank()
                    for jt in range(2):
                        kt = g0 + jt
                        for kc in range(8):
                            S.op("pe", lambda e: e.matmul(bk[:, jt * 256:(jt + 1) * 256], lhsT=self.uT[:, kc, kt * 128:(kt + 1) * 128], rhs=wv[:, kc, :],
                                                          start=(kc == 0), stop=(kc == 7)), reads=[r_wB] + self.r_u, writes=[rb], same_ok=True)
                    S.op("act", lambda e: e.copy(vh[:, g0:g0 + 2, :], bk[:, 0:512].rearrange("p (j d) -> p j d", d=256)), reads=[rb], writes=[r_vh])
                for ti, (t0, n, lc) in enumerate(TT):
                    ru = self.r_u[ti]
                    nch = n // 128
                    bq, rbq = self.bank()
                    for kc in range(8):
                        S.op("pe", lambda e: e.matmul(bq[:, 0:n], lhsT=wqk[:, kc, 0:128], rhs=self.uT[:, kc, t0:t0 + n], start=(kc == 0), stop=(kc == 7)),
                             reads=[r_wA, ru], writes=[rbq], same_ok=True)
                    bkk, rbk = self.bank()
                    for kc in range(8):
                        S.op("pe", lambda e: e.matmul(bkk[:, 0:n], lhsT=wqk[:, kc, 128:256], rhs=self.uT[:, kc, t0:t0 + n], start=(kc == 0), stop=(kc == 7)),
                             reads=[r_wA, ru], writes=[rbk], same_ok=True)
                    for d in range(2):
                        bx, rbx = self.bank()
                        S.op("pe", lambda e: e.matmul(bx[:, 0:n], lhsT=gw[:, d, h * 128:(h + 1) * 128], rhs=rT[:, t0:t0 + n], start=True, stop=True),
                             reads=[r_gw, r_rT], writes=[rbx], same_ok=True)
                        S.op("act", lambda e: e.activation(out=bA[:, 0:n], in_=bx[:, 0:n], func=AF.Exp, bias=gb[:, d, h:h + 1], scale=-1.0),
                             reads=[rbx, r_gb], writes=[r_bA])
                        S.op("act", lambda e: e.activation(out=bA[:, 0:n], in_=bA[:, 0:n], func=AF.Ln, bias=1.0, scale=1.0), reads=[r_bA], writes=[r_bA])
                        S.op("dve", lambda e: e.tensor_tensor_scan(out=bB[:, 0:n], data0=cmask[:, 0:n], data1=bA[:, 0:n], initial=0.0,
                                                                   op0=ALU.mult, op1=ALU.add), reads=[r_cm, r_bA], writes=[r_bB])
                        for c in range(nch):
                            gc = t0 // 128 + c
                            ce = c * 128 + 127
                            S.op("act", lambda e: e.activation(out=dec[:, d, gc:gc + 1], in_=bB[:, ce:ce + 1], func=AF.Exp, scale=-1.0 / 16), reads=[r_bB], writes=[r_dec])
                            S.op("dve", lambda e: e.tensor_scalar(out=bD[:, c * 128:(c + 1) * 128], in0=bB[:, c * 128:(c + 1) * 128], scalar1=bB[:, ce:ce + 1],
                                                                  scalar2=None, op0=ALU.subtract), reads=[r_bB], writes=[r_bD])
                        if d == 0:
                            S.op("act", lambda e: e.activation(out=bC[:, 0:n], in_=bB[:, 0:n], func=AF.Exp, scale=-1.0 / 16), reads=[r_bB], writes=[r_bC])
                            S.op("dve", lambda e: e.scalar_tensor_tensor(out=arr[d][0][:, t0:t0 + n], in0=bq[:, 0:n], scalar=QS, in1=bC[:, 0:n], op0=ALU.mult, op1=ALU.mult),
                                 reads=[rbq, r_bC], writes=[r_arr[d][0]])
                            S.op("act", lambda e: e.activation(out=bC[:, 0:n], in_=bB[:, 0:n], func=AF.Exp, scale=1.0 / 16), reads=[r_bB], writes=[r_bC])
                            S.op("dve", lambda e: e.tensor_tensor(out=arr[d][1][:, t0:t0 + n], in0=bkk[:, 0:n], in1=bC[:, 0:n], op=ALU.mult),
                                 reads=[rbk, r_bC], writes=[r_arr[d][1]])
                            S.op("act", lambda e: e.activation(out=bC[:, 0:n], in_=bD[:, 0:n], func=AF.Exp, scale=1.0 / 16), reads=[r_bD], writes=[r_bC])
                            S.op("dve", lambda e: e.tensor_tensor(out=arr[d][2][:, t0:t0 + n], in0=bkk[:, 0:n], in1=bC[:, 0:n], op=ALU.mult),
                                 reads=[rbk, r_bC], writes=[r_arr[d][2]])
                        else:
                            S.op("dve", lambda e: e.tensor_tensor(out=bD[:, 0:n], in0=bA[:, 0:n], in1=bD[:, 0:n], op=ALU.subtract), reads=[r_bA, r_bD], writes=[r_bD])
                            S.op("act", lambda e: e.activation(out=bC[:, 0:n], in_=bD[:, 0:n], func=AF.Exp, scale=-1.0 / 16), reads=[r_bD], writes=[r_bC])
                            S.op("dve", lambda e: e.scalar_tensor_tensor(out=arr[d][0][:, t0:t0 + n], in0=bq[:, 0:n], scalar=QS, in1=bC[:, 0:n], op0=ALU.mult, op1=ALU.mult),
                                 reads=[rbq, r_bC], writes=[r_arr[d][0]])
                            S.op("act", lambda e: e.activation(out=bC[:, 0:n], in_=bD[:, 0:n], func=AF.Exp, scale=1.0 / 16), reads=[r_bD], writes=[r_bC])
                            S.op("dve", lambda e: e.tensor_tensor(out=arr[d][1][:, t0:t0 + n], in0=bkk[:, 0:n], in1=bC[:, 0:n], op=ALU.mult),
                                 reads=[rbk, r_bC], writes=[r_arr[d][1]])
                            S.op("dve", lambda e: e.tensor_tensor(out=bD[:, 0:n], in0=bA[:, 0:n], in1=bB[:, 0:n], op=ALU.subtract), reads=[r_bA, r_bB, r_bC], writes=[r_bD])
                            S.op("act", lambda e: e.activation(out=bC[:, 0:n], in_=bD[:, 0:n], func=AF.Exp, scale=1.0 / 16), reads=[r_bD], writes=[r_bC])
                            S.op("dve", lambda e: e.tensor_tensor(out=arr[d][2][:, t0:t0 + n], in0=bkk[:, 0:n], in1=bC[:, 0:n], op=ALU.mult),
                                 reads=[rbk, r_bC], writes=[r_arr[d][2]])
                for d in range(2):
                    S.op("pool", lambda e: e.memset(Sf[d][:], 0.0), reads=[r_Sf[d]], writes=[r_Sf[d]])
                    S.op("pool", lambda e: e.memset(Sb[d][:], 0.0), reads=[r_Sb[d]], writes=[r_Sb[d]])
                order = [[16, 17] + list(range(16)), [17, 16] + list(range(15, -1, -1))]
                written = set()
                for step in range(18):
                    for d in range(2):
                        c = order[d][step]
                        cs = slice(c * 128, (c + 1) * 128)
                        qd, ki, ke = arr[d]
                        ba, rba = self.bank()
                        S.op("pe", lambda e: e.matmul(ba[:, 0:128], lhsT=ki[:, cs], rhs=qd[:, cs], start=True, stop=True),
                             reads=[r_arr[d][1], r_arr[d][0]], writes=[rba], same_ok=True)
                        bke, rbke = self.bank()
                        S.op("pe", lambda e: e.matmul(bke[:, 0:128], lhsT=ke[:, cs], rhs=self.identb[:], start=True, stop=True),
                             reads=[r_arr[d][2], self.r_ident], writes=[rbke], same_ok=True)
                        S.op("dve", lambda e: e.tensor_tensor(out=Am[d][:], in0=ba[:, 0:128], in1=msk[d][:], op=ALU.mult), reads=[rba, r_msk], writes=[r_Am[d]])
                        S.op("act", lambda e: e.copy(keT[d][:], bke[:, 0:128]), reads=[rbke], writes=[r_keT[d]])
                        bo, rbo = self.bank()
                        for j in range(2):
                            S.op("pe", lambda e: e.matmul(bo[:, j * 128:(j + 1) * 128], lhsT=Sb[d][:, j * 128:(j + 1) * 128], rhs=qd[:, cs], start=True, stop=False),
                                 reads=[r_Sb[d], r_arr[d][0]], writes=[rbo], same_ok=True)
                            S.op("pe", lambda e: e.matmul(bo[:, j * 128:(j + 1) * 128], lhsT=vh[:, c, j * 128:(j + 1) * 128], rhs=Am[d][:], start=False, stop=True),
                                 reads=[r_vh, r_Am[d]], writes=[rbo], same_ok=True)
                        ov = oacc[:, :, cs]
                        pv = bo[:, 0:256].rearrange("p (j c) -> p j c", j=2)
                        if c not in written:
                            written.add(c)
                            S.op("act", lambda e: e.copy(ov, pv), reads=[rbo], writes=[r_oacc[c]])
                        else:
                            S.op("dve", lambda e: e.tensor_tensor(out=ov, in0=pv, in1=ov, op=ALU.add), reads=[rbo, r_oacc[c]], writes=[r_oacc[c]])
                        bs, rbs = self.bank()
                        S.op("pe", lambda e: e.matmul(bs[:, 0:256], lhsT=keT[d][:], rhs=vh[:, c, :], start=True, stop=True),
                             reads=[r_keT[d], r_vh], writes=[rbs], same_ok=True)
                        S.op("dve", lambda e: e.scalar_tensor_tensor(out=Sf[d][:], in0=Sf[d][:], scalar=dec[:, d, c:c + 1], in1=bs[:, 0:256], op0=ALU.mult, op1=ALU.add),
                             reads=[r_Sf[d], r_dec, rbs], writes=[r_Sf[d]])
                        S.op("act", lambda e: e.copy(Sb[d][:], Sf[d][:]), reads=[r_Sf[d]], writes=[r_Sb[d]])
                for ti, (t0, n, lc) in enumerate(TT):
                    ru = self.r_u[ti]
                    roa = r_oacc[t0 // 128:(t0 + n) // 128]
                    bss = self.banks[6]; rbss = self.bank_res[6]
                    for j in range(2):
                        S.op("act", lambda e: e.activation(out=bA[:, 0:n], in_=oacc[:, j, t0:t0 + n], func=AF.Square), reads=roa + [r_bA], writes=[r_bA])
                        S.op("pe", lambda e: e.matmul(bss[:, 0:n], lhsT=self.ones[:], rhs=bA[:, 0:n], start=(j == 0), stop=(j == 1)),
                             reads=[r_bA, self.r_ident], writes=[rbss], same_ok=True)
                    S.op("act", lambda e: e.activation(out=bB[:, 0:n], in_=bss[:, 0:n], func=AF.Sqrt, bias=epsc[:, 0:1], scale=1.0 / 256.0), reads=[rbss, r_eps], writes=[r_bB])
                    S.op("dve", lambda e: e.reciprocal(out=bB[:, 0:n], in_=bB[:, 0:n]), reads=[r_bB], writes=[r_bB])
                    og, rog = ogn[0], r_ogn[0]
                    for j in range(2):
                        bg, rbg = self.bank()
                        for kc in range(8):
                            S.op("pe", lambda e: e.matmul(bg[:, 0:n], lhsT=wg[:, kc, j * 128:(j + 1) * 128], rhs=self.uT[:, kc, t0:t0 + n], start=(kc == 0), stop=(kc == 7)),
                                 reads=[r_wB, ru], writes=[rbg], same_ok=True)
                        S.op("act", lambda e: e.activation(out=bC[:, 0:n], in_=bg[:, 0:n], func=AF.Silu), reads=[rbg], writes=[r_bC])
                        S.op("dve", lambda e: e.scalar_tensor_tensor(out=bD[:, 0:n], in0=oacc[:, j, t0:t0 + n], scalar=ng[:, j:j + 1], in1=bB[:, 0:n], op0=ALU.mult, op1=ALU.mult),
                             reads=roa + [r_ng, r_bB], writes=[r_bD])
                        S.op("dve", lambda e: e.tensor_tensor(out=og[:, j, 0:n], in0=bD[:, 0:n], in1=bC[:, 0:n], op=ALU.mult), reads=[r_bD, r_bC], writes=[rog])
                    for oc in range(8):
                        bk, rb = self.bank()
                        for j in range(2):
                            S.op("pe", lambda e: e.matmul(bk[:, 0:n], lhsT=wo[:, j, oc * 128:(oc + 1) * 128], rhs=og[:, j, 0:n], start=(j == 0), stop=(j == 1)),
                                 reads=[r_wC, rog], writes=[rb], same_ok=True)
                        hz = self.hT[:, oc, t0:t0 + n]
                        S.op("dve", lambda e: e.scalar_tensor_tensor(out=hz, in0=bk[:, 0:n], scalar=self.mcol(i, 2, oc, lc), in1=hz, op0=ALU.mult, op1=ALU.add),
                             reads=[rb, self.r_h[ti][oc], self.r_mod[i]], writes=[self.r_h[ti][oc]])
            S.barrier()

    def gdn(self, i):
        S = self.S
        R32 = mybir.dt.float32r
        win = self.din["gdn_w_in"]
        rr = lambda ap: ap.bitcast(R32)
        with ExitStack() as ph:
            sbp = lambda shape, dt: self.sb(shape, dt, es=ph)
            cw = sbp([128, 5, 32], F32); r_cw = Res("cw")
            for j in range(5):
                self.load_cols(self.din["gdn_conv"][j], 32, cw[:, j, :], r_cw)
            r_msk = Res("gmsk")
            inclT = [sbp([128, 128], F32) for _ in range(2)]
            strict2 = sbp([128, 2, 128], BF16)
            inclT2 = sbp([128, 2, 128], BF16)
            bd16 = sbp([128, 128], BF16); off16 = sbp([128, 128], BF16); off32 = sbp([128, 128], BF16); off64 = sbp([128, 128], BF16)
            with ExitStack() as mk:
                strict = [self.sb([128, 128], F32, es=mk) for _ in range(2)]
                specs = [(strict[0], ALU.is_gt, 1, [[-1, 128]]), (strict[1], ALU.is_gt, -1, [[1, 128]]),
                         (inclT[0], ALU.is_ge, -1, [[1, 128]]), (inclT[1], ALU.is_ge, 1, [[-1, 128]])]
                for (t_, cmp_, cm, pat) in specs:
                    S.op("pool", lambda e: e.memset(t_[:], 1.0), reads=[r_msk], writes=[r_msk])
                    S.op("pool", lambda e: e.affine_select(out=t_[:], in_=t_[:], compare_op=cmp_, fill=0.0, base=0, pattern=pat, channel_multiplier=cm),
                         reads=[r_msk], writes=[r_msk])
                for d in range(2):
                    S.op("dve", lambda e: e.tensor_copy(strict2[:, d, :], strict[d][:]), reads=[r_msk], writes=[r_msk])
                    S.op("dve", lambda e: e.tensor_copy(inclT2[:, d, :], inclT[d][:]), reads=[r_msk], writes=[r_msk])
                bsel = self.sb([8, 128], F32, es=mk)
                bd = {}
                for b_ in (16, 32, 64):
                    nb = 128 // b_
                    S.op("pool", lambda e: e.memset(bsel[:], 1.0), reads=[r_msk], writes=[r_msk])
                    S.op("pool", lambda e: e.affine_select(out=bsel[:], in_=bsel[:], compare_op=ALU.is_ge, fill=0.0, base=0, pattern=[[1, 128]], channel_multiplier=-b_),
                         reads=[r_msk], writes=[r_msk])
                    S.op("pool", lambda e: e.affine_select(out=bsel[:], in_=bsel[:], compare_op=ALU.is_ge, fill=0.0, base=b_ - 1, pattern=[[-1, 128]], channel_multiplier=b_),
                         reads=[r_msk], writes=[r_msk])
                    bk, rb = self.bank()
                    S.op("pe", lambda e: e.matmul(bk[:, 0:128], lhsT=bsel[0:nb, :], rhs=bsel[0:nb, :], start=True, stop=True), reads=[r_msk], writes=[rb], same_ok=True)
                    bd[b_] = self.sb([128, 128], F32, es=mk)
                    S.op("dve", lambda e: e.tensor_copy(bd[b_][:], bk[:, 0:128]), reads=[rb, r_msk], writes=[r_msk])
                S.op("dve", lambda e: e.tensor_copy(bd16[:], bd[16][:]), reads=[r_msk], writes=[r_msk])
                S.op("dve", lambda e: e.tensor_tensor(out=off16[:], in0=bd[32][:], in1=bd[16][:], op=ALU.subtract), reads=[r_msk], writes=[r_msk])
                S.op("dve", lambda e: e.tensor_tensor(out=off32[:], in0=bd[64][:], in1=bd[32][:], op=ALU.subtract), reads=[r_msk], writes=[r_msk])
                S.op("dve", lambda e: e.tensor_scalar(out=off64[:], in0=bd[64][:], scalar1=-1.0, scalar2=1.0, op0=ALU.mult, op1=ALU.add), reads=[r_msk], writes=[r_msk])
                S.barrier()
            b4 = lambda m_: m_[:].unsqueeze(1).broadcast_to([128, 4, 128])
            hc = sbp([128, 64], F32); r_hc = Res("hc")
            S.dma("sp", [(hc[:], self.din["gdn_hc"].partition_broadcast(128))], r_hc, writes=[r_hc])
            S.op("act", lambda e: e.activation(out=hc[:, 0:32], in_=hc[:, 0:32], func=AF.Exp), reads=[r_hc], writes=[r_hc])
            S.op("dve", lambda e: e.tensor_scalar(out=hc[:, 0:32], in0=hc[:, 0:32], scalar1=-1.0, scalar2=None, op0=ALU.mult), reads=[r_hc], writes=[r_hc])
            ngrep = sbp([128, 128], F32); r_ngr = Res("ngrep")
            S.dma("sp", [(ngrep[:], self.din["gdn_norm"].partition_broadcast(128))], r_ngr, writes=[r_ngr])
            epsc = sbp([128, 1], F32); r_eps = Res("eps")
            S.op("pool", lambda e: e.memset(epsc[:], EPS), writes=[r_eps])
            qkT = sbp([128, 2, T], BF16); r_qkT = Res("qkT")
            kn = sbp([128, 18, 128], BF16); r_kn = Res("kn")
            vt = sbp([128, 18, 256], BF16); r_vt = Res("vt")
            oacc = sbp([128, 18, 256], BF16); r_oacc = [Res(f"go{c}") for c in range(18)]
            sc_names = ("negb", "gc", "e", "ecoef", "negbe", "dl", "g", "ngc")
            sc = {n_: sbp([128, 18, 4], F32) for n_ in sc_names}
            r_sc = Res("gsc")

            for kh in range(8):
                wA_, r_wA = self.wnext()
                wA = wA_[:].rearrange("p (k n) -> p k n", k=8)
                S.dma("pool", [(wA[:, :, 0:128], win[:, kh * 128:(kh + 1) * 128].rearrange("(k p) n -> p k n", p=128)),
                               (wA[:, :, 128:256], win[:, 1024 + kh * 128:1024 + (kh + 1) * 128].rearrange("(k p) n -> p k n", p=128)),
                               (wA[:, :, 256:512], win[:, 2048 + kh * 256:2048 + (kh + 1) * 256].rearrange("(k p) n -> p k n", p=128))],
                      r_wA, writes=[r_wA])
                wB_, r_wB = self.wnext()
                wz = wB_[:, 0:2048].rearrange("p (k n) -> p k n", k=8)
                wgt = wB_[:, 2048:2112].rearrange("p (k n) -> p k n", k=8)
                S.dma("pool", [(wz, win[:, 4096 + kh * 256:4096 + (kh + 1) * 256].rearrange("(k p) n -> p k n", p=128)),
                               (wgt, self.din["gdn_wg"][:, kh * 8:(kh + 1) * 8].rearrange("(k p) n -> p k n", p=128))], r_wB, writes=[r_wB])
                wC_, r_wC = self.wnext()
                wo = wC_[:, 0:2048].rearrange("p (k n) -> p k n", k=2)
                S.dma("pool", [(wo, self.din["gdn_w_out"][kh * 256:(kh + 1) * 256, :].rearrange("(k p) n -> p k n", p=128))], r_wC, writes=[r_wC])
                with ExitStack() as p1:
                    xpad = self.sb([128, 4, 2312], BF16, es=p1); r_xp = Res("xpad")
                    dg = self.sb([128, 4, 5, 128], BF16, es=p1); r_dg = Res("dg")
                    cvs = [self.sb([128, 512], F32, es=p1) for _ in range(3)]; r_cvs = [Res() for _ in range(3)]
                    junk = self.sb([128, 128], F32, es=p1); r_junk = Res("junk")
                    sss = [self.sb([128, 2], F32, es=p1) for _ in range(3)]; r_sss = [Res() for _ in range(3)]
                    qns = [self.sb([128, 128], BF16, es=p1) for _ in range(3)]; r_qns = [Res() for _ in range(3)]
                    S.op("pool", lambda e: e.memset(xpad[:], 0.0), writes=[r_xp])
                    gch = [kh, 8 + kh, 16 + 2 * kh, 17 + 2 * kh]
                    for ch in range(4):
                        for j in range(5):
                            S.op("pool", lambda e: e.tensor_scalar(out=dg[:, ch, j, :], in0=self.identb[:], scalar1=cw[:, j, gch[ch]:gch[ch] + 1], scalar2=1.0, op0=ALU.mult, op1=ALU.mult),
                                 reads=[r_cw, self.r_ident], writes=[r_dg])
                    for ch in range(4):
                        for ti, (t0, n, lc) in enumerate(TT):
                            bk, rb = self.bank()
                            for kc in range(8):
                                S.op("pe", lambda e: e.matmul(bk[:, 0:n], lhsT=wA[:, kc, ch * 128:(ch + 1) * 128], rhs=self.uT[:, kc, t0:t0 + n], start=(kc == 0), stop=(kc == 7)),
                                     reads=[r_wA, self.r_u[ti]], writes=[rb], same_ok=True)
                            c0 = t0 + 2 if lc == 0 else 2054
                            if (ch + ti) % 2 == 0:
                                S.op("act", lambda e: e.copy(xpad[:, ch, c0:c0 + n], bk[:, 0:n]), reads=[rb], writes=[r_xp])
                            else:
                                S.op("dve", lambda e: e.tensor_copy(xpad[:, ch, c0:c0 + n], bk[:, 0:n]), reads=[rb], writes=[r_xp])
                    def p1A(t):
                        b0 = t * 128 + 2 if t < 16 else 2054 + (t - 16) * 128
                        cv, r_cv = cvs[t % 3], r_cvs[t % 3]
                        ss, r_ss = sss[t % 3], r_sss[t % 3]
                        bk, rb = self.bank()
                        for ch in range(4):
                            for j in range(5):
                                S.op("pe", lambda e: e.matmul(bk[:, ch * 128:(ch + 1) * 128], lhsT=xpad[:, ch, b0 + j - 2:b0 + j - 2 + 128], rhs=dg[:, ch, j, :],
                                                              start=(j == 0), stop=(j == 4)), reads=[r_xp, r_dg], writes=[rb], same_ok=True)
                        S.op("act", lambda e: e.activation(out=cv[:], in_=bk[:, 0:512], func=AF.Silu), reads=[rb], writes=[r_cv])
                        for q_ in range(2):
                            S.op("act", lambda e: e.activation(out=junk[:], in_=cv[:, q_ * 128:(q_ + 1) * 128], func=AF.Square, accum_out=ss[:, q_:q_ + 1]),
                                 reads=[r_cv], writes=[r_ss])
                        S.op("act", lambda e: e.activation(out=ss[:], in_=ss[:], func=AF.Sqrt, bias=epsc[:, 0:1], scale=1.0), reads=[r_ss, r_eps], writes=[r_ss])

                    def p1B(t):
                        cv, r_cv = cvs[t % 3], r_cvs[t % 3]
                        ss, r_ss = sss[t % 3], r_sss[t % 3]
                        qn, r_qn = qns[t % 3], r_qns[t % 3]
                        S.op("dve", lambda e: e.reciprocal(out=ss[:], in_=ss[:]), reads=[r_ss], writes=[r_ss])
                        S.op("dve", lambda e: e.tensor_scalar(out=qn[:], in0=cv[:, 0:128], scalar1=ss[:, 0:1], scalar2=128.0 ** -0.5, op0=ALU.mult, op1=ALU.mult),
                             reads=[r_cv, r_ss], writes=[r_qn])
                        S.op("dve", lambda e: e.tensor_scalar(out=kn[:, t, :], in0=cv[:, 128:256], scalar1=ss[:, 1:2], scalar2=None, op0=ALU.mult),
                             reads=[r_cv, r_ss], writes=[r_kn])
                        S.op("pool", lambda e: e.tensor_copy(vt[:, t, :], cv[:, 256:512]), reads=[r_cv], writes=[r_vt])
                        b2, rb2 = self.bank()
                        S.op("pe", lambda e: e.matmul(b2[:, 0:128], lhsT=qn[:], rhs=self.identb[:], start=True, stop=True), reads=[r_qn, self.r_ident], writes=[rb2], same_ok=True)
                        S.op("pe", lambda e: e.matmul(b2[:, 128:256], lhsT=kn[:, t, :], rhs=self.identb[:], start=True, stop=True), reads=[r_kn, self.r_ident], writes=[rb2], same_ok=True)
                        S.op("act", lambda e: e.copy(qkT[:, :, t * 128:(t + 1) * 128], b2[:, 0:256].rearrange("p (a c) -> p a c", a=2)), reads=[rb2], writes=[r_qkT])
                    p1A(0)
                    for t in range(18):
                        if t + 1 < 18:
                            p1A(t + 1)
                        p1B(t)
                    S.barrier()
                bk, rb = self.bank()
                for t in range(18):
                    for kc in range(8):
                        S.op("pe", lambda e: e.matmul(bk[:, t * 8:(t + 1) * 8], lhsT=self.uT[:, kc, t * 128:(t + 1) * 128], rhs=wgt[:, kc, :], start=(kc == 0), stop=(kc == 7)),
                             reads=[r_wB] + self.r_u, writes=[rb], same_ok=True)
                graw = bk[:, 0:144].rearrange("p (t c) -> p t c", c=8)
                S.op("act", lambda e: e.activation(out=sc["negb"][:], in_=graw[:, :, 0:4], func=AF.Sigmoid), reads=[rb], writes=[r_sc])
                S.op("dve", lambda e: e.tensor_scalar(out=sc["negb"][:], in0=sc["negb"][:], scalar1=-1.0, scalar2=None, op0=ALU.mult), reads=[r_sc], writes=[r_sc])
                for m in range(4):
                    d_, j_ = m // 2, m % 2
                    hidx = d_ * 16 + 2 * kh + j_
                    S.op("act", lambda e: e.activation(out=sc["g"][:, :, m], in_=graw[:, :, 4 + m], func=AF.Exp, bias=hc[:, 32 + hidx:33 + hidx], scale=1.0),
                         reads=[rb, r_hc], writes=[r_sc])
                S.op("act", lambda e: e.activation(out=sc["g"][:], in_=sc["g"][:], func=AF.Ln, bias=1.0, scale=1.0), reads=[r_sc], writes=[r_sc])
                for m in range(4):
                    d_, j_ = m // 2, m % 2
                    hidx = d_ * 16 + 2 * kh + j_
                    S.op("dve", lambda e: e.tensor_scalar(out=sc["g"][:, :, m], in0=sc["g"][:, :, m], scalar1=hc[:, hidx:hidx + 1], scalar2=None, op0=ALU.mult),
                         reads=[r_sc, r_hc], writes=[r_sc])
                bk, rb = self.bank()
                gv = sc["g"][:]
                S.op("pe", lambda e: e.matmul(bk[:, 0:72].rearrange("p (t c) -> p t c", c=4)[:, :, 0:2], lhsT=inclT[0][:], rhs=gv[:, :, 0:2], start=True, stop=True),
                     reads=[r_sc, r_msk], writes=[rb], same_ok=True)
                S.op("pe", lambda e: e.matmul(bk[:, 0:72].rearrange("p (t c) -> p t c", c=4)[:, :, 2:4], lhsT=inclT[1][:], rhs=gv[:, :, 2:4], start=True, stop=True),
                     reads=[r_sc, r_msk], writes=[rb], same_ok=True)
                S.op("pe", lambda e: e.matmul(bk[:, 128:200], lhsT=self.ones[:], rhs=gv.rearrange("p t c -> p (t c)"), start=True, stop=True),
                     reads=[r_sc, self.r_ident], writes=[rb], same_ok=True)
                gcp = bk[:, 0:72].rearrange("p (t c) -> p t c", c=4)
                glp = bk[:, 128:200].rearrange("p (t c) -> p t c", c=4)
                S.op("dve", lambda e: e.tensor_copy(sc["gc"][:], gcp), reads=[rb], writes=[r_sc])
                S.op("dve", lambda e: e.tensor_copy(sc["dl"][:], glp), reads=[rb], writes=[r_sc])
                S.op("dve", lambda e: e.tensor_tensor(out=sc["ecoef"][:], in0=glp, in1=sc["gc"][:], op=ALU.subtract), reads=[rb, r_sc], writes=[r_sc])
                S.op("dve", lambda e: e.tensor_scalar(out=sc["ngc"][:], in0=sc["gc"][:], scalar1=-1.0, scalar2=None, op0=ALU.mult), reads=[r_sc], writes=[r_sc])
                S.op("act", lambda e: e.activation(out=sc["e"][:], in_=sc["gc"][:], func=AF.Exp), reads=[r_sc], writes=[r_sc])
                S.op("act", lambda e: e.activation(out=sc["dl"][:], in_=sc["dl"][:], func=AF.Exp), reads=[r_sc], writes=[r_sc])
                S.op("act", lambda e: e.activation(out=sc["ecoef"][:], in_=sc["ecoef"][:], func=AF.Exp), reads=[r_sc], writes=[r_sc])
                S.op("dve", lambda e: e.tensor_tensor(out=sc["negbe"][:], in0=sc["negb"][:], in1=sc["e"][:], op=ALU.mult), reads=[r_sc], writes=[r_sc])
                with ExitStack() as p3:
                    h4 = lambda: self.sb([128, 4, 128], BF16, es=p3)
                    f4 = lambda: self.sb([128, 4, 128], F32, es=p3)
                    ST = []
                    for st_ in range(2):
                        B = dict(X=[h4(), h4()], Y=[h4(), h4()], W=h4(), Tm=h4(), Y0=h4(), AT=h4(),
                                 Gm=self.sb([128, 2, 128], BF16, es=p3), QKm=self.sb([128, 2, 128], BF16, es=p3), scr=f4(), rscr=Res(),
                                 rX=[Res(), Res()], rY=[Res(), Res()], rW=Res(), rTm=Res(), rY0=Res(), rAT=Res(), rGm=Res(), rQKm=Res())
                        ST.append(B)
                    Rm = h4(); r_Rm = Res("Rm")
                    vn = h4(); r_vn = Res("vn")
                    kdec = h4(); r_kdec = Res("kdec")
                    bv = h4(); r_bv = Res("bv")
                    Sf = f4(); r_Sf = Res("Sf")
                    Sb = h4(); r_Sb = Res("Sb")
                    ot = h4(); r_ot = Res("ot")
                    S.op("pool", lambda e: e.memset(Sf[:], 0.0), writes=[r_Sf])
                    S.op("pool", lambda e: e.memset(Sb[:], 0.0), writes=[r_Sb])
                    order = [[16, 17] + list(range(16)), [17, 16] + list(range(15, -1, -1))]
                    written = set()
                    kT = lambda c: qkT[:, 1, c * 128:(c + 1) * 128]
                    qT = lambda c: qkT[:, 0, c * 128:(c + 1) * 128]
                    pv4 = lambda b_: b_[:, 0:512].rearrange("p (m c) -> p m c", m=4)

                    def pre(step, B):
                        X, Y, W, Tm, Y0, AT, Gm, QKm = B["X"], B["Y"], B["W"], B["Tm"], B["Y0"], B["AT"], B["Gm"], B["QKm"]
                        rX, rY, rW, rTm, rY0, rAT, rGm, rQKm = B["rX"], B["rY"], B["rW"], B["rTm"], B["rY0"], B["rAT"], B["rGm"], B["rQKm"]
                        cc = [order[0][step], order[1][step]]
                        cm_ = [cc[m // 2] for m in range(4)]
                        scrA = scrB = B["scr"]
                        r_scrA = r_scrB = B["rscr"]
                        bk, rb = self.bank()
                        for d in range(2):
                            S.op("pe", lambda e: e.matmul(bk[:, d * 128:(d + 1) * 128], lhsT=kT(cc[d]), rhs=kT(cc[d]), start=True, stop=True), reads=[r_qkT], writes=[rb], same_ok=True)
                            S.op("pe", lambda e: e.matmul(bk[:, 256 + d * 128:256 + (d + 1) * 128], lhsT=kT(cc[d]), rhs=qT(cc[d]), start=True, stop=True), reads=[r_qkT], writes=[rb], same_ok=True)
                        S.op("dve", lambda e: e.tensor_tensor(out=Gm[:], in0=bk[:, 0:256].rearrange("p (d c) -> p d c", d=2), in1=strict2[:], op=ALU.mult), reads=[rb, r_msk], writes=[rGm])
                        S.op("dve", lambda e: e.tensor_tensor(out=QKm[:], in0=bk[:, 256:512].rearrange("p (d c) -> p d c", d=2), in1=inclT2[:], op=ALU.mult), reads=[rb, r_msk], writes=[rQKm])
                        for m in range(4):
                            S.op("pool", lambda e: e.tensor_scalar(out=scrA[:, m, :], in0=self.ident[:], scalar1=sc["gc"][:, cm_[m], m:m + 1], scalar2=1.0, op0=ALU.mult, op1=ALU.mult),
                                 reads=[r_sc, self.r_ident], writes=[r_scrA])
                        yield
                        bb, rbb = self.bank()
                        for m in range(4):
                            S.op("pe", lambda e: e.matmul(bb[:, m * 128:(m + 1) * 128], lhsT=self.ones[:], rhs=scrA[:, m, :], start=True, stop=True),
                                 reads=[r_scrA, self.r_ident], writes=[rbb], same_ok=True)
                        for m in range(4):
                            S.op("act", lambda e: e.activation(out=scrA[:, m, :], in_=bb[:, m * 128:(m + 1) * 128], func=AF.Relu, bias=sc["ngc"][:, cm_[m], m:m + 1], scale=1.0),
                                 reads=[rbb, r_sc], writes=[r_scrA])
                        S.op("act", lambda e: e.activation(out=X[1][:], in_=scrA[:], func=AF.Exp, scale=-1.0), reads=[r_scrA], writes=[rX[1]])
                        for m in range(4):
                            S.op("act", lambda e: e.activation(out=scrB[:, m, :], in_=bb[:, m * 128:(m + 1) * 128], func=AF.Relu, bias=sc["gc"][:, cm_[m], m:m + 1], scale=-1.0),
                                 reads=[rbb, r_sc], writes=[r_scrB])
                        S.op("act", lambda e: e.activation(out=Y[1][:], in_=scrB[:], func=AF.Exp, scale=-1.0), reads=[r_scrB], writes=[rY[1]])
                        yield
                        for m in range(4):
                            d = m // 2
                            S.op("dve", lambda e: e.scalar_tensor_tensor(out=X[0][:, m, :], in0=X[1][:, m, :], scalar=sc["negb"][:, cm_[m], m:m + 1], in1=Gm[:, d, :],
                                                                         op0=ALU.mult, op1=ALU.mult), reads=[rX[1], r_sc, rGm], writes=[rX[0]])
                        for d in range(2):
                            S.op("pool", lambda e: e.tensor_tensor(out=AT[:, 2 * d:2 * d + 2, :], in0=Y[1][:, 2 * d:2 * d + 2, :],
                                                                   in1=QKm[:, d:d + 1, :].broadcast_to([128, 2, 128]), op=ALU.mult), reads=[rQKm, rY[1]], writes=[rAT])
                        yield
                        bk, rb = self.bank()
                        for m in range(4):
                            S.op("pe", lambda e: e.matmul(bk[:, m * 128:(m + 1) * 128], lhsT=X[0][:, m, :], rhs=self.identb[:], start=True, stop=True),
                                 reads=[rX[0], self.r_ident], writes=[rb], same_ok=True)
                        S.op("act", lambda e: e.copy(Y0[:], pv4(bk)), reads=[rb], writes=[rY0])
                        yield
                        S.op("dve", lambda e: e.tensor_tensor(out=X[1][:], in0=X[0][:], in1=b4(bd16), op=ALU.mult), reads=[rX[0], r_msk, rAT], writes=[rX[1]])
                        S.op("dve", lambda e: e.tensor_tensor(out=Y[1][:], in0=Y0[:], in1=b4(bd16), op=ALU.mult), reads=[rY0, r_msk, rAT], writes=[rY[1]])
                        S.op("dve", lambda e: e.tensor_tensor(out=W[:], in0=Y[1][:], in1=b4(self.identb), op=ALU.add), reads=[rY[1], self.r_ident], writes=[rW])
                        yield
                        cur = 1
                        for lev in range(3):
                            nxt = 1 - cur
                            bx, rbx = self.bank()
                            for m in range(4):
                                S.op("pe", lambda e: e.matmul(bx[:, m * 128:(m + 1) * 128], lhsT=Y[cur][:, m, :], rhs=X[cur][:, m, :], start=True, stop=True),
                                     reads=[rY[cur], rX[cur]], writes=[rbx], same_ok=True)
                            if lev < 2:
                                by, rby = self.bank()
                                for m in range(4):
                                    S.op("pe", lambda e: e.matmul(by[:, m * 128:(m + 1) * 128], lhsT=X[cur][:, m, :], rhs=Y[cur][:, m, :], start=True, stop=True),
                                         reads=[rY[cur], rX[cur]], writes=[rby], same_ok=True)
                            S.op("act", lambda e: e.copy(X[nxt][:], pv4(bx)), reads=[rbx], writes=[rX[nxt]])
                            if lev < 2:
                                S.op("dve", lambda e: e.tensor_copy(Y[nxt][:], pv4(by)), reads=[rby], writes=[rY[nxt]])
                            yield
                            bw, rbw = self.bank()
                            for m in range(4):
                                S.op("pe", lambda e: e.matmul(bw[:, m * 128:(m + 1) * 128], lhsT=X[nxt][:, m, :], rhs=W[:, m, :], start=True, stop=True),
                                     reads=[rX[nxt], rW], writes=[rbw], same_ok=True)
                            S.op("dve", lambda e: e.tensor_tensor(out=W[:], in0=pv4(bw), in1=W[:], op=ALU.add), reads=[rbw, rW], writes=[rW])
                            cur = nxt
                            yield
                        bk, rb = self.bank()
                        for m in range(4):
                            S.op("pe", lambda e: e.matmul(bk[:, m * 128:(m + 1) * 128], lhsT=W[:, m, :], rhs=self.identb[:], start=True, stop=True),
                                 reads=[rW, self.r_ident], writes=[rb], same_ok=True)
                        S.op("act", lambda e: e.copy(Tm[:], pv4(bk)), reads=[rb], writes=[rTm])
                        yield
                        for li, offm in enumerate((off16, off32, off64)):
                            S.op("dve", lambda e: e.tensor_tensor(out=X[0][:], in0=Y0[:], in1=b4(offm), op=ALU.mult), reads=[rY0, r_msk], writes=[rX[0]])
                            bz, rbz = self.bank()
                            for m in range(4):
                                S.op("pe", lambda e: e.matmul(bz[:, m * 128:(m + 1) * 128], lhsT=X[0][:, m, :], rhs=Tm[:, m, :], start=True, stop=True),
                                     reads=[rX[0], rTm], writes=[rbz], same_ok=True)
                            S.op("act", lambda e: e.copy(X[1][:], pv4(bz)), reads=[rbz], writes=[rX[1]])
                            yield
                            if li < 2:
                                bt, rbt = self.bank()
                                for m in range(4):
                                    S.op("pe", lambda e: e.matmul(bt[:, m * 128:(m + 1) * 128], lhsT=W[:, m, :], rhs=X[1][:, m, :], start=True, stop=True),
                                         reads=[rW, rX[1]], writes=[rbt], same_ok=True)
                            bw, rbw = self.bank()
                            for m in range(4):
                                S.op("pe", lambda e: e.matmul(bw[:, m * 128:(m + 1) * 128], lhsT=X[1][:, m, :], rhs=W[:, m, :], start=True, stop=True),
                                     reads=[rX[1], rW], writes=[rbw], same_ok=True)
                            if li < 2:
                                S.op("dve", lambda e: e.tensor_tensor(out=Tm[:], in0=pv4(bt), in1=Tm[:], op=ALU.add), reads=[rbt, rTm], writes=[rTm])
                            S.op("dve", lambda e: e.tensor_tensor(out=W[:], in0=pv4(bw), in1=W[:], op=ALU.add), reads=[rbw, rW], writes=[rW])
                            yield

                    def chain(step, B):
                        W, AT, rW, rAT = B["W"], B["AT"], B["rW"], B["rAT"]
                        cc = [order[0][step], order[1][step]]
                        cm_ = [cc[m // 2] for m in range(4)]
                        for m in range(4):
                            S.op("pool", lambda e: e.tensor_scalar(out=kdec[:, m, :], in0=kn[:, cm_[m], :], scalar1=sc["ecoef"][:, cm_[m], m:m + 1], scalar2=1.0, op0=ALU.mult, op1=ALU.mult),
                                 reads=[r_kn, r_sc], writes=[r_kdec])
                            S.op("pool", lambda e: e.tensor_scalar(out=bv[:, m, :], in0=vt[:, cm_[m], (m % 2) * 128:(m % 2 + 1) * 128], scalar1=sc["negb"][:, cm_[m], m:m + 1],
                                                                   scalar2=-1.0, op0=ALU.mult, op1=ALU.mult), reads=[r_vt, r_sc], writes=[r_bv])
                        yield
                        bks, rbks = self.bank()
                        for m in range(4):
                            S.op("pe", lambda e: e.matmul(bks[:, m * 128:(m + 1) * 128], lhsT=kT(cm_[m]), rhs=Sb[:, m, :], start=True, stop=True), reads=[r_qkT, r_Sb], writes=[rbks], same_ok=True)
                        bo1, rbo1 = self.bank()
                        for m in range(4):
                            S.op("pe", lambda e: e.matmul(bo1[:, m * 128:(m + 1) * 128], lhsT=qT(cm_[m]), rhs=Sb[:, m, :], start=True, stop=True), reads=[r_qkT, r_Sb], writes=[rbo1], same_ok=True)
                        for m in range(4):
                            S.op("dve", lambda e: e.scalar_tensor_tensor(out=Rm[:, m, :], in0=bks[:, m * 128:(m + 1) * 128], scalar=sc["negbe"][:, cm_[m], m:m + 1], in1=bv[:, m, :],
                                                                         op0=ALU.mult, op1=ALU.add), reads=[rbks, r_sc, r_bv], writes=[r_Rm])
                            S.op("act", lambda e: e.activation(out=ot[:, m, :], in_=bo1[:, m * 128:(m + 1) * 128], func=AF.Copy, scale=sc["e"][:, cm_[m], m:m + 1]),
                                 reads=[rbo1, r_sc], writes=[r_ot])
                        yield
                        bvn, rbvn = self.bank()
                        for m in range(4):
                            S.op("pe", lambda e: e.matmul(bvn[:, m * 128:(m + 1) * 128], lhsT=W[:, m, :], rhs=Rm[:, m, :], start=True, stop=True), reads=[rW, r_Rm], writes=[rbvn], same_ok=True)
                        S.op("act", lambda e: e.copy(vn[:], pv4(bvn)), reads=[rbvn], writes=[r_vn])
                        yield
                        bo2, rbo2 = self.bank()
                        for m in range(4):
                            S.op("pe", lambda e: e.matmul(bo2[:, m * 128:(m + 1) * 128], lhsT=AT[:, m, :], rhs=vn[:, m, :], start=True, stop=True), reads=[rAT, r_vn], writes=[rbo2], same_ok=True)
                        bst, rbst = self.bank()
                        for m in range(4):
                            S.op("pe", lambda e: e.matmul(bst[:, m * 128:(m + 1) * 128], lhsT=kdec[:, m, :], rhs=vn[:, m, :], start=True, stop=True), reads=[r_kdec, r_vn], writes=[rbst], same_ok=True)
                        yield
                        for m in range(4):
                            S.op("dve", lambda e: e.scalar_tensor_tensor(out=Sf[:, m, :], in0=Sf[:, m, :], scalar=sc["dl"][:, cm_[m], m:m + 1], in1=bst[:, m * 128:(m + 1) * 128],
                                                                         op0=ALU.mult, op1=ALU.add), reads=[r_Sf, r_sc, rbst], writes=[r_Sf])
                        S.op("act", lambda e: e.copy(Sb[:], Sf[:]), reads=[r_Sf], writes=[r_Sb])
                        S.op("dve", lambda e: e.tensor_tensor(out=ot[:], in0=pv4(bo2), in1=ot[:], op=ALU.add), reads=[rbo2, r_ot], writes=[r_ot])
                        for d in range(2):
                            c = cc[d]
                            src = ot[:, 2 * d:2 * d + 2, :]
                            dst = oacc[:, c, :].rearrange("p (j v) -> p j v", j=2)
                            if c not in written:
                                written.add(c)
                                S.op("pool", lambda e: e.tensor_copy(dst, src), reads=[r_ot], writes=[r_oacc[c]])
                            else:
                                S.op("pool", lambda e: e.tensor_tensor(out=dst, in0=dst, in1=src, op=ALU.add), reads=[r_ot, r_oacc[c]], writes=[r_oacc[c]])

                    def run_all(g):
                        for _ in g:
                            pass

                    pend = None
                    for p_ in range(9):
                        ga, gb = pre(2 * p_, ST[0]), pre(2 * p_ + 1, ST[1])
                        next(ga); next(gb)
                        if pend is not None:
                            run_all(chain(pend[0], ST[0]))
                        next(ga); next(gb)
                        if pend is not None:
                            run_all(chain(pend[1], ST[1]))
                        live = [True, True]
                        gens = [ga, gb]
                        while any(live):
                            for gi in range(2):
                                if live[gi]:
                                    try:
                                        next(gens[gi])
                                    except StopIteration:
                                        live[gi] = False
                        pend = (2 * p_, 2 * p_ + 1)
                    run_all(chain(pend[0], ST[0]))
                    run_all(chain(pend[1], ST[1]))
                    S.barrier()
                with ExitStack() as p4:
                    ogT = self.sb([128, 2, 512], BF16, es=p4); r_ogT = Res("ogT")
                    ogs = [self.sb([128, 256], F32, es=p4) for _ in range(2)]; r_ogs = [Res(), Res()]
                    ogbs = [self.sb([128, 256], BF16, es=p4) for _ in range(2)]; r_ogbs = [Res(), Res()]
                    zss = [self.sb([128, 256], F32, es=p4) for _ in range(2)]; r_zss = [Res(), Res()]
                    junk = self.sb([128, 128], F32, es=p4); r_junk = Res("junk")
                    sss4 = [self.sb([128, 2], F32, es=p4) for _ in range(2)]; r_sss4 = [Res(), Res()]
                    ogTs = [ogT, self.sb([128, 2, 512], BF16, es=p4)]; r_ogTs = [r_ogT, Res("ogT1")]
                    tile_of = [min(t // 4, 4) for t in range(18)]
                    zb = {}

                    def p4A(t):
                        ti = tile_of[t]
                        zs, r_zs = zss[t % 2], r_zss[t % 2]
                        ss, r_ss = sss4[t % 2], r_sss4[t % 2]
                        for j in range(2):
                            S.op("act", lambda e: e.activation(out=junk[:], in_=oacc[:, t, j * 128:(j + 1) * 128], func=AF.Square, accum_out=ss[:, j:j + 1]),
                                 reads=[r_oacc[t]], writes=[r_ss])
                        S.op("act", lambda e: e.activation(out=ss[:], in_=ss[:], func=AF.Sqrt, bias=epsc[:, 0:1], scale=1.0 / 128.0), reads=[r_ss, r_eps], writes=[r_ss])
                        bz, rbz = self.bank()
                        for kc in range(8):
                            S.op("pe", lambda e: e.matmul(bz[:, 0:256], lhsT=self.uT[:, kc, t * 128:(t + 1) * 128], rhs=wz[:, kc, :], start=(kc == 0), stop=(kc == 7)),
                                 reads=[r_wB, self.r_u[ti]], writes=[rbz], same_ok=True)
                        S.op("act", lambda e: e.activation(out=zs[:], in_=bz[:, 0:256], func=AF.Silu), reads=[rbz], writes=[r_zs])

                    def p4B(t):
                        ti = tile_of[t]
                        t0, n, lc = TT[ti]
                        tt = t - t0 // 128
                        og, r_og = ogs[t % 2], r_ogs[t % 2]
                        ogb, r_ogb = ogbs[t % 2], r_ogbs[t % 2]
                        zs, r_zs = zss[t % 2], r_zss[t % 2]
                        ss, r_ss = sss4[t % 2], r_sss4[t % 2]
                        ogT_, r_ogT_ = ogTs[ti % 2], r_ogTs[ti % 2]
                        S.op("dve", lambda e: e.reciprocal(out=ss[:], in_=ss[:]), reads=[r_ss], writes=[r_ss])
                        for j in range(2):
                            S.op("dve", lambda e: e.scalar_tensor_tensor(out=og[:, j * 128:(j + 1) * 128], in0=oacc[:, t, j * 128:(j + 1) * 128], scalar=ss[:, j:j + 1], in1=ngrep[:],
                                                                         op0=ALU.mult, op1=ALU.mult), reads=[r_oacc[t], r_ss, r_ngr], writes=[r_og])
                        S.op("dve", lambda e: e.tensor_tensor(out=ogb[:], in0=og[:], in1=zs[:], op=ALU.mult), reads=[r_og, r_zs], writes=[r_ogb])
                        b2, rb2 = self.bank()
                        for j in range(2):
                            S.op("pe", lambda e: e.matmul(b2[:, j * 128:(j + 1) * 128], lhsT=ogb[:, j * 128:(j + 1) * 128], rhs=self.identb[:], start=True, stop=True),
                                 reads=[r_ogb, self.r_ident], writes=[rb2], same_ok=True)
                        S.op("act", lambda e: e.copy(ogT_[:, :, tt * 128:(tt + 1) * 128], b2[:, 0:256].rearrange("p (j c) -> p j c", j=2)), reads=[rb2], writes=[r_ogT_])
                        if tt == n // 128 - 1:
                            for oc in range(8):
                                bk, rb = self.bank()
                                for j in range(2):
                                    S.op("pe", lambda e: e.matmul(bk[:, 0:n], lhsT=wo[:, j, oc * 128:(oc + 1) * 128], rhs=ogT_[:, j, 0:n], start=(j == 0), stop=(j == 1)),
                                         reads=[r_wC, r_ogT_], writes=[rb], same_ok=True)
                                hz = self.hT[:, oc, t0:t0 + n]
                                S.op("dve", lambda e: e.scalar_tensor_tensor(out=hz, in0=bk[:, 0:n], scalar=self.mcol(i, 2, oc, lc), in1=hz, op0=ALU.mult, op1=ALU.add),
                                     reads=[rb, self.r_h[ti][oc], self.r_mod[i]], writes=[self.r_h[ti][oc]])
                    n4 = 16 if self.skip_ctx else 18
                    p4A(0)
                    for t in range(n4):
                        if t + 1 < n4:
                            p4A(t + 1)
                        p4B(t)
                    S.barrier()


def build_program(depth_run=DEPTH, mixers=True, dbg=False):
    nc = bass.Bass("TRN2", target_bir_lowering=False)
    es = ExitStack()
    with es:
        kb = KB(nc, es, depth_run, mixers, dbg)
        kb.build()
        print("instructions", kb.S.ninst, "sems", kb.S.nsem, flush=True)
    return nc, kb


def _rope_tables(dim):
    n_freq = dim // 4
    inv = (10000.0 ** (-np.arange(n_freq, dtype=np.float32) / n_freq)).astype(np.float32)
    tok = np.arange(TL)
    row = (tok // 64).astype(np.float32)
    col = (tok % 64).astype(np.float32)
    ang = np.concatenate([row[:, None] * inv, col[:, None] * inv], -1).astype(np.float32)
    c = np.ones((dim // 2, T), np.float32)
    s = np.zeros((dim // 2, T), np.float32)
    c[:, :TL] = np.cos(ang).T
    s[:, :TL] = np.sin(ang).T
    return c, s


def _mla_host(inputs, shared, g):
    w_in = g("mla_w_in")[0]
    w_qb = g("mla_w_qb")[0]
    ev = np.arange(0, 32, 2)
    od = ev + 1
    cols = []
    for h in range(16):
        b = h * 96
        cols += list(range(b, b + 64)) + list(b + 64 + ev) + list(b + 64 + od) + list(b + 64 + ev) + list(b + 64 + od)
    shared["mla_wqx"] = np.ascontiguousarray(w_qb[:, cols])
    ia = list(1024 + ev) * 4
    ib = list(1024 + od) * 4
    shared["mla_wkr"] = np.ascontiguousarray(np.concatenate(
        [w_in[:, 0:64], w_in[:, ia], w_in[:, 0:64], w_in[:, ib]], axis=1))
    c, s = _rope_tables(32)
    one = np.ones((64, T), np.float32)
    shared["mla_qtab"] = np.concatenate([one, c, s, s, c], 0)
    shared["mla_kta"] = np.concatenate([one, c, -c, s, s], 0)
    shared["mla_ktb"] = np.concatenate([one, -s, s, c, c], 0)


def _diff_host(inputs, shared, g):
    w_in = g("diff_w_in")[0]
    ev = np.arange(0, 64, 2)
    od = ev + 1
    cols = []
    for h in range(8):
        for m in range(2):
            bq = h * 128 + m * 64
            bk = 1024 + h * 128 + m * 64
            cols += list(bq + ev) + list(bq + od) + list(bq + ev) + list(bq + od)
            cols += list(bk + ev) * 4
            cols += list(bk + od) * 4
    shared["diff_wx"] = np.ascontiguousarray(w_in[:, cols])
    shared["diff_wv"] = np.ascontiguousarray(w_in[:, 2048:3072])
    shared["diff_w_out"] = g("diff_w_out")[0]
    c, s = _rope_tables(64)
    shared["diff_qtab"] = np.concatenate([c, s, s, c], 0)
    shared["diff_kta"] = np.concatenate([c, -c, s, s], 0)
    shared["diff_ktb"] = np.concatenate([-s, s, c, c], 0)
    shared["diff_lam"] = np.ascontiguousarray(np.stack([g("diff_lambda_q1")[0], g("diff_lambda_k1")[0],
                                                        g("diff_lambda_q2")[0], g("diff_lambda_k2")[0]], axis=1))
    shared["diff_subln"] = g("diff_subln")[0].reshape(1, 128)


def _gla_host(inputs, shared, g):
    shared["gla_w_in"] = g("gla_w_in")[0]
    shared["gla_gw"] = np.ascontiguousarray(np.stack([g("gla_gate_w_fwd")[0], g("gla_gate_w_bwd")[0]], 0))
    shared["gla_gb"] = np.ascontiguousarray(np.stack([g("gla_gate_b_fwd")[0].reshape(4, 128), g("gla_gate_b_bwd")[0].reshape(4, 128)], 0))
    shared["gla_norm"] = g("gla_norm")[0].reshape(2, 128)
    shared["gla_w_out"] = g("gla_w_out")[0]


def _gdn_host(inputs, shared, g):
    w_in = g("gdn_w_in")[0]
    shared["gdn_w_in"] = w_in
    cols = []
    for kh in range(8):
        for base in (6144, 6160, 6176, 6192):
            cols += [base + 2 * kh, base + 2 * kh + 1]
    shared["gdn_wg"] = np.ascontiguousarray(w_in[:, cols])
    shared["gdn_conv"] = g("gdn_conv_w")[0].reshape(5, 32, 128)
    shared["gdn_hc"] = np.ascontiguousarray(np.concatenate([g("gdn_a_log_fwd")[0], g("gdn_a_log_bwd")[0],
                                                            g("gdn_dt_bias_fwd")[0], g("gdn_dt_bias_bwd")[0]]).reshape(1, 64))
    shared["gdn_norm"] = g("gdn_norm")[0].reshape(1, 128)
    shared["gdn_w_out"] = g("gdn_w_out")[0]

def make_in_maps(inputs):
    g = lambda k: np.ascontiguousarray(np.asarray(inputs[k], dtype=np.float32))
    shared = {
        "c_ctx": g("c_ctx").reshape(8, 128),
        "ada_w": g("ada_w"), "ada_b": g("ada_b").reshape(DEPTH, 48, 128),
        "ln1_g": g("ln1_g").reshape(DEPTH, 8, 128), "ln1_b": g("ln1_b").reshape(DEPTH, 8, 128),
        "ln2_g": g("ln2_g").reshape(DEPTH, 8, 128), "ln2_b": g("ln2_b").reshape(DEPTH, 8, 128),
        "mlp_w1": g("mlp_w1"), "mlp_w2": g("mlp_w2"),
        "mla_w_in": g("mla_w_in")[0], "mla_q_norm": g("mla_q_norm")[0].reshape(6, 128),
        "mla_kv_norm": g("mla_kv_norm")[0].reshape(2, 128),
        "mla_w_kvb": g("mla_w_kvb")[0], "mla_w_out": g("mla_w_out")[0],
    }
    _mla_host(inputs, shared, g)
    _diff_host(inputs, shared, g)
    _gla_host(inputs, shared, g)
    _gdn_host(inputs, shared, g)
    x, c, ctx = g("x"), g("c"), g("ctx")
    maps = []
    for b in range(8):
        m = dict(shared)
        m["x"] = x[b]
        m["ctx"] = ctx[b]
        m["c"] = c[b].reshape(8, 128)
        maps.append(m)
    return maps


def kernel(**inputs):
    nc, kb = build_program()
    maps = make_in_maps(inputs)
    res = run_bass_kernel_spmd(nc, maps, core_ids=list(range(8)))
    return np.stack([np.asarray(r["out"], dtype=np.float32) for r in res.results], axis=0)
```

```python
import math
import numpy as np
import concourse.bass as bass
import concourse.mybir as mybir
from concourse.bass_utils import run_bass_kernel_spmd
from contextlib import ExitStack

F32 = mybir.dt.float32
BF16 = mybir.dt.bfloat16
ALU = mybir.AluOpType
AF = mybir.ActivationFunctionType

DEPTH = 4
D = 1024
TL = 2048
TC = 256
T = TL + TC
ALPHA = (2 * DEPTH) ** 0.25
EPS = 1e-6
EPS_LN = EPS / (ALPHA * ALPHA)
TT = [(0, 512, 0), (512, 512, 0), (1024, 512, 0), (1536, 512, 0), (2048, 256, 1)]


class Res:
    __slots__ = ("name", "w", "r", "dsem", "dcnt")

    def __init__(self, name=""):
        self.name = name
        self.w = None
        self.r = {}
        self.dsem = None
        self.dcnt = 0


class Sched:
    EPOCH = 30000

    def __init__(self, nc, es):
        self.nc = nc
        self.es = es
        self.eng = {"pe": nc.tensor, "dve": nc.vector, "act": nc.scalar,
                    "pool": nc.gpsimd, "sp": nc.sync}
        self.cnt = {e: 0 for e in self.eng}
        self.cursem = {e: None for e in self.eng}
        self.last = {e: None for e in self.eng}
        self.seen = {e: {} for e in self.eng}
        self.nsem = 0
        self.ninst = 0
        self.owners = []
        self.out_events = []

    def newsem(self, name):
        self.nsem += 1
        return self.es.enter_context(self.nc.semaphore(f"{name}_{self.nsem}"))

    def _wait(self, e, ev):
        sem, val, _ = ev
        k = id(sem)
        if self.seen[e].get(k, 0) >= val:
            return
        self.eng[e].wait_ge(sem, val)
        self.seen[e][k] = val

    def _deps(self, e, reads, writes, same_ok):
        for r in reads:
            if r.w is not None and not (same_ok and r.w[2] == e):
                self._wait(e, r.w)
        for w in writes:
            if w.w is not None and not (same_ok and w.w[2] == e):
                self._wait(e, w.w)
            for ev in w.r.values():
                if not (same_ok and ev[2] == e):
                    self._wait(e, ev)

    def op(self, e, fn, reads=(), writes=(), same_ok=False):
        self._deps(e, reads, writes, same_ok)
        ins = fn(self.eng[e])
        if self.cnt[e] % self.EPOCH == 0:
            self.cursem[e] = self.newsem("c" + e)
        self.cnt[e] += 1
        val = (self.cnt[e] - 1) % self.EPOCH + 1
        sem = self.cursem[e]
        ins.then_inc(sem, 1)
        ev = (sem, val, e)
        self.last[e] = ev
        for r in reads:
            r.r[id(sem)] = ev
        for w in writes:
            w.w = ev
            w.r = {}
        self.ninst += 1
        return ev

    def dma(self, q, pairs, owner, reads=(), writes=(), **kw):
        self._deps(q, reads, writes, False)
        if owner.dsem is None:
            owner.dsem = self.newsem("d")
            self.owners.append(owner)
        if owner.dcnt > 0:
            self._wait(q, (owner.dsem, owner.dcnt, "dma"))
        for (o, i) in pairs:
            self.eng[q].dma_start(out=o, in_=i, **kw).then_inc(owner.dsem, 16)
            owner.dcnt += 16
        ev = (owner.dsem, owner.dcnt, "dma")
        for r in reads:
            r.r[id(owner.dsem)] = ev
        for w in writes:
            w.w = ev
            w.r = {}
        self.ninst += len(pairs)
        return ev

    def barrier(self):
        evs = [ev for ev in self.last.values() if ev is not None]
        evs += [(o.dsem, o.dcnt, "dma") for o in self.owners if o.dcnt > 0]
        for e in ("pe", "dve", "act", "pool", "sp"):
            for ev in evs:
                if ev[2] != e:
                    self._wait(e, ev)

    def finish(self):
        for ev in self.out_events:
            self._wait("sp", ev)


class KB:
    def __init__(self, nc, es, depth_run=DEPTH, mixers=True, dbg=False):
        self.nc, self.es = nc, es
        self.S = Sched(nc, es)
        self.depth_run = depth_run
        self.mixers = mixers
        self.dbg = dbg
        self.din = {}
        self._n = 0
        self.skip_ctx = False

    def sb(self, shape, dt, es=None, name=None):
        self._n += 1
        return (es or self.es).enter_context(self.nc.sbuf_tensor(name or f"t{self._n}", list(shape), dt))

    def dram_in(self, name, shape):
        t = self.nc.dram_tensor(name, list(shape), F32, kind="ExternalInput").ap()
        self.din[name] = t
        return t

    def bank(self):
        i = self.bank_i
        self.bank_i = (i + 1) % self.nrr
        return self.banks[i], self.bank_res[i]

    def declare(self):
        di = self.dram_in
        di("x", [TL, D]); di("ctx", [TC, D]); di("c", [8, 128]); di("c_ctx", [8, 128])
        di("ada_w", [DEPTH, D, 6 * D]); di("ada_b", [DEPTH, 48, 128])
        for n in ("ln1_g", "ln1_b", "ln2_g", "ln2_b"):
            di(n, [DEPTH, 8, 128])
        di("mlp_w1", [DEPTH, D, 4 * D]); di("mlp_w2", [DEPTH, 4 * D, D])
        di("mla_w_in", [D, 1056]); di("mla_q_norm", [6, 128]); di("mla_kv_norm", [2, 128])
        di("mla_w_kvb", [256, 2048]); di("mla_w_out", [D, D])
        di("mla_wqx", [768, 2048]); di("mla_wkr", [D, 256])
        di("mla_qtab", [128, T]); di("mla_kta", [128, T]); di("mla_ktb", [128, T])
        di("diff_wx", [D, 16 * 384]); di("diff_wv", [D, D]); di("diff_w_out", [D, D])
        di("diff_qtab", [128, T]); di("diff_kta", [128, T]); di("diff_ktb", [128, T])
        di("diff_lam", [64, 4]); di("diff_subln", [1, 128])
        di("gla_w_in", [D, 3104]); di("gla_gw", [2, 16, 512]); di("gla_gb", [2, 4, 128])
        di("gla_norm", [2, 128]); di("gla_w_out", [D, D])
        di("gdn_w_in", [D, 6208]); di("gdn_wg", [D, 64]); di("gdn_conv", [5, 32, 128])
        di("gdn_hc", [1, 64]); di("gdn_norm", [1, 128]); di("gdn_w_out", [2 * D, D])
        self.out = self.nc.dram_tensor("out", [TL, D], F32, kind="ExternalOutput").ap()
        if self.dbg:
            self.out_c = self.nc.dram_tensor("out_c", [TC, D], F32, kind="ExternalOutput").ap()

        nc = self.nc
        self.banks = [self.es.enter_context(nc.psum_tensor(f"bank{i}", [128, 512], F32)) for i in range(8)]
        self.bank_res = [Res(f"bank{i}") for i in range(8)]
        self.bank_i = 0
        self.nrr = 6
        self.hT = self.sb([128, 8, T], F32, name="hT")
        self.r_h = [[Res(f"h{t}_{k}") for k in range(8)] for t in range(len(TT))]
        self.uT = self.sb([128, 8, T], BF16, name="uT")
        self.r_u = [Res(f"u{t}") for t in range(len(TT))]
        self.ident = self.sb([128, 128], F32, name="ident"); self.r_ident = Res("ident")
        self.identb = self.sb([128, 128], BF16, name="identb")
        self.ones = self.sb([128, 128], F32, name="ones")
        self.identr = self.sb([128, 128], F32, name="identr")
        self.onesr = self.sb([128, 128], F32, name="onesr")
        self.sT = self.sb([128, 8, 2], F32, name="sT"); self.r_sT = Res("sT")
        self.sTb = self.sb([128, 8, 2], BF16, name="sTb")
        self.mod = [self.sb([128, 48, 2], F32, name=f"mod{i}") for i in range(DEPTH)]
        self.r_mod = [Res(f"mod{i}") for i in range(DEPTH)]
        self.lnp = self.sb([128, 4, DEPTH, 8], F32, name="lnp"); self.r_lnp = Res("lnp")
        self.adab = self.sb([128, DEPTH, 48], F32, name="adab"); self.r_adab = Res("adab")
        self.vst = self.sb([64, 128], F32, name="vst"); self.r_vst = Res("vst")
        self.NW = 3
        self.wslot = [self.sb([128, 4096], BF16, name=f"wslot{i}") for i in range(self.NW)]
        self.r_wslot = [Res(f"wslot{i}") for i in range(self.NW)]
        self.w_i = 0

    def wnext(self):
        i = self.w_i
        self.w_i = (i + 1) % self.NW
        return self.wslot[i], self.r_wslot[i]

    def load_cols(self, src, n, dst, r_dst):
        S = self.S
        S.dma("sp", [(self.vst[0:n, :], src)], self.r_vst, writes=[self.r_vst])
        bk, rb = self.bank()
        S.op("pe", lambda e: e.transpose(bk[:, 0:n], self.vst[0:n, :], self.ident[0:n, 0:n]),
             reads=[self.r_vst, self.r_ident], writes=[rb], same_ok=True)
        S.op("dve", lambda e: e.tensor_copy(dst, bk[:, 0:n]), reads=[rb], writes=[r_dst])

    def setup(self):
        S, nc = self.S, self.nc
        S.op("pool", lambda e: e.memset(self.ident[:], 0.0), writes=[self.r_ident])
        S.op("pool", lambda e: e.affine_select(out=self.ident[:], in_=self.ident[:], compare_op=ALU.not_equal,
                                              fill=1.0, base=0, pattern=[[-1, 128]], channel_multiplier=1),
             reads=[self.r_ident], writes=[self.r_ident])
        S.op("dve", lambda e: e.tensor_copy(self.identb[:], self.ident[:]), reads=[self.r_ident], writes=[self.r_ident])
        S.op("dve", lambda e: e.memset(self.ones[:], 1.0), writes=[self.r_ident])
        S.op("dve", lambda e: e.tensor_copy(self.identr[:].bitcast(mybir.dt.float32r), self.ident[:]), reads=[self.r_ident], writes=[self.r_ident])
        S.op("dve", lambda e: e.tensor_copy(self.onesr[:].bitcast(mybir.dt.float32r), self.ones[:]), reads=[self.r_ident], writes=[self.r_ident])
        for k, n in enumerate(("ln1_g", "ln1_b", "ln2_g", "ln2_b")):
            for i in range(DEPTH):
                self.load_cols(self.din[n][i], 8, self.lnp[:, k, i, :], self.r_lnp)
        for i in range(DEPTH):
            self.load_cols(self.din["ada_b"][i], 48, self.adab[:, i, :], self.r_adab)
        self.load_cols(self.din["c"], 8, self.sT[:, :, 0], self.r_sT)
        self.load_cols(self.din["c_ctx"], 8, self.sT[:, :, 1], self.r_sT)
        S.op("act", lambda e: e.activation(out=self.sT[:], in_=self.sT[:], func=AF.Silu), reads=[self.r_sT], writes=[self.r_sT])
        S.op("dve", lambda e: e.tensor_copy(self.sTb[:], self.sT[:]), reads=[self.r_sT], writes=[self.r_sT])
        with ExitStack() as ph:
            xs = [self.sb([128, D], F32, es=ph) for _ in range(2)]
            r_xs = [Res("xs0"), Res("xs1")]
            mstg = [self.sb([128, 8, 256], BF16, es=ph) for _ in range(2)]
            mg = self.mods_gen(0, mstg, [Res("ms0"), Res("ms1")])
            for t in range(18):
                if mg is not None:
                    try:
                        next(mg)
                    except StopIteration:
                        mg = None
                src = self.din["x"][t * 128:(t + 1) * 128, :] if t < 16 else self.din["ctx"][(t - 16) * 128:(t - 15) * 128, :]
                st, rs = xs[t % 2], r_xs[t % 2]
                S.dma("sp", [(st[:], src)], rs, writes=[rs])
                ti = min(t // 4, 4)
                for g in range(2):
                    bk, rb = self.bank()
                    for j in range(4):
                        S.op("pe", lambda e: e.transpose(bk[:, j * 128:(j + 1) * 128], st[:, (g * 4 + j) * 128:(g * 4 + j + 1) * 128], self.ident[:]),
                             reads=[rs, self.r_ident], writes=[rb], same_ok=True)
                    dst = self.hT[:, g * 4:(g + 1) * 4, t * 128:(t + 1) * 128]
                    srcp = bk[:, 0:512].rearrange("p (j n) -> p j n", j=4)
                    if g == 0:
                        S.op("dve", lambda e: e.tensor_copy(dst, srcp), reads=[rb], writes=self.r_h[ti][g * 4:(g + 1) * 4])
                    else:
                        S.op("act", lambda e: e.copy(dst, srcp), reads=[rb], writes=self.r_h[ti][g * 4:(g + 1) * 4])
            if mg is not None:
                for _ in mg:
                    pass
            S.barrier()

    def mods_gen(self, i, stg, r_stg):
        S = self.S
        aw = self.din["ada_w"][i]
        bk, rb = self.banks[7], self.bank_res[7]
        for blk in range(24):
            st, rs = stg[blk % 2], r_stg[blk % 2]
            S.dma("pool", [(st[:], aw[:, blk * 256:(blk + 1) * 256].rearrange("(k p) n -> p k n", p=128))], rs, writes=[rs])
            for cc in range(2):
                c = blk * 2 + cc
                for kc in range(8):
                    S.op("pe", lambda e: e.matmul(bk[:, c * 2:(c + 1) * 2], lhsT=st[:, kc, cc * 128:(cc + 1) * 128], rhs=self.sTb[:, kc, :],
                                                  start=(kc == 0), stop=(kc == 7)),
                         reads=[rs, self.r_sT], writes=[rb], same_ok=True)
            yield
        m, rm = self.mod[i], self.r_mod[i]
        pv = bk[:, 0:96].rearrange("p (c l) -> p c l", l=2)
        for l in range(2):
            S.op("dve", lambda e: e.tensor_tensor(out=m[:, :, l], in0=pv[:, :, l], in1=self.adab[:, i, :], op=ALU.add),
                 reads=[rb, self.r_adab], writes=[rm])
        for c0 in (8, 32):
            S.op("dve", lambda e: e.tensor_scalar(out=m[:, c0:c0 + 8, :], in0=m[:, c0:c0 + 8, :], scalar1=1.0, scalar2=None, op0=ALU.add),
                 reads=[rm], writes=[rm])
        for c0 in (16, 40):
            S.op("dve", lambda e: e.tensor_scalar(out=m[:, c0:c0 + 8, :], in0=m[:, c0:c0 + 8, :], scalar1=1.0 / ALPHA, scalar2=None, op0=ALU.mult),
                 reads=[rm], writes=[rm])

    def mods(self, i):
        with ExitStack() as ph:
            stg = [self.sb([128, 8, 256], BF16, es=ph) for _ in range(2)]
            r_stg = [Res("as0"), Res("as1")]
            for _ in self.mods_gen(i, stg, r_stg):
                pass
            self.S.barrier()

    def mcol(self, i, which, kc, lc):
        return self.mod[i][:, which * 8 + kc, lc:lc + 1]

    def modulate_all(self, i, sub):
        S = self.S
        for ti, (t0, n, lc) in enumerate(TT):
            for kc in range(8):
                S.op("dve", lambda e: e.tensor_scalar(out=self.uT[:, kc, t0:t0 + n], in0=self.hT[:, kc, t0:t0 + n],
                                                      scalar1=self.mcol(i, 3 * sub + 1, kc, lc), scalar2=self.mcol(i, 3 * sub, kc, lc),
                                                      op0=ALU.mult, op1=ALU.add),
                     reads=[self.r_h[ti][kc], self.r_mod[i]], writes=[self.r_u[ti]])

    def layer_norm(self, i, sub, nxt):
        S = self.S
        R32 = mybir.dt.float32r
        with ExitStack() as ph:
            sq = [self.sb([128, 512], F32, es=ph) for _ in range(3)]; r_sq = [Res() for _ in range(3)]
            mean = [self.sb([128, 512], F32, es=ph) for _ in range(2)]; r_mean = [Res(), Res()]
            msq = [self.sb([128, 512], F32, es=ph) for _ in range(2)]; r_msq = [Res(), Res()]
            rstd = [self.sb([128, 512], F32, es=ph) for _ in range(2)]; r_rstd = [Res(), Res()]
            tmp = [self.sb([128, 512], F32, es=ph) for _ in range(3)]; r_tmp = [Res() for _ in range(3)]
            epsc = self.sb([128, 1], F32, es=ph); r_eps = Res()
            S.op("pool", lambda e: e.memset(epsc[:], EPS_LN), writes=[r_eps])
            gsc = self.sb([128, 8, 2], F32, es=ph); bsc = self.sb([128, 8, 2], F32, es=ph); r_gb = Res("gsc")
            if nxt is not None:
                ni, nsub = nxt
                msc = self.mod[ni][:, (3 * nsub + 1) * 8:(3 * nsub + 2) * 8, :]
                msh = self.mod[ni][:, (3 * nsub) * 8:(3 * nsub + 1) * 8, :]
                for l_ in range(2):
                    S.op("dve", lambda e: e.tensor_tensor(out=gsc[:, :, l_], in0=self.lnp[:, 2 * sub, i, :], in1=msc[:, :, l_], op=ALU.mult),
                         reads=[self.r_lnp, self.r_mod[ni]], writes=[r_gb])
                    S.op("dve", lambda e: e.tensor_tensor(out=bsc[:, :, l_], in0=self.lnp[:, 2 * sub + 1, i, :], in1=msc[:, :, l_], op=ALU.mult),
                         reads=[self.r_lnp, self.r_mod[ni]], writes=[r_gb])
                    S.op("dve", lambda e: e.tensor_tensor(out=bsc[:, :, l_], in0=bsc[:, :, l_], in1=msh[:, :, l_], op=ALU.add),
                         reads=[r_gb, self.r_mod[ni]], writes=[r_gb])
            banks = {}
            cnt = [0, 0]

            def stats(ti):
                t0, n, lc = TT[ti]
                rh = self.r_h[ti]
                b1, rb1 = self.bank()
                b2, rb2 = self.bank()
                banks[ti] = (b1, rb1, b2, rb2)
                for kc in range(8):
                    z = self.hT[:, kc, t0:t0 + n]
                    s_, rs_ = sq[cnt[0] % 3], r_sq[cnt[0] % 3]
                    cnt[0] += 1
                    S.op("act", lambda e: e.activation(out=s_[:, 0:n].bitcast(R32), in_=z, func=AF.Square), reads=[rh[kc]], writes=[rs_])
                    S.op("pe", lambda e: e.matmul(b1[:, 0:n], lhsT=self.ones[:], rhs=z, start=(kc == 0), stop=(kc == 7)),
                         reads=[rh[kc], self.r_ident], writes=[rb1], same_ok=True)
                    S.op("pe", lambda e: e.matmul(b2[:, 0:n], lhsT=self.onesr[:].bitcast(R32), rhs=s_[:, 0:n].bitcast(R32), start=(kc == 0), stop=(kc == 7)),
                         reads=[rs_, self.r_ident], writes=[rb2], same_ok=True)

            def finish_stats(ti):
                t0, n, lc = TT[ti]
                b1, rb1, b2, rb2 = banks[ti]
                mn, rmn = mean[ti % 2], r_mean[ti % 2]
                ms, rms = msq[ti % 2], r_msq[ti % 2]
                rs, rrs = rstd[ti % 2], r_rstd[ti % 2]
                S.op("act", lambda e: e.activation(out=mn[:, 0:n], in_=b1[:, 0:n], func=AF.Copy, scale=1.0 / D), reads=[rb1], writes=[rmn])
                S.op("dve", lambda e: e.tensor_tensor(out=ms[:, 0:n], in0=mn[:, 0:n], in1=mn[:, 0:n], op=ALU.mult), reads=[rmn], writes=[rms])
                S.op("dve", lambda e: e.scalar_tensor_tensor(out=rs[:, 0:n], in0=b2[:, 0:n], scalar=1.0 / D, in1=ms[:, 0:n],
                                                             op0=ALU.mult, op1=ALU.subtract), reads=[rb2, rms], writes=[rrs])
                S.op("act", lambda e: e.activation(out=rs[:, 0:n], in_=rs[:, 0:n], func=AF.Sqrt, bias=epsc[:, 0:1], scale=1.0),
                     reads=[rrs, r_eps], writes=[rrs])
                S.op("dve", lambda e: e.reciprocal(out=rs[:, 0:n], in_=rs[:, 0:n]), reads=[rrs], writes=[rrs])

            def normalize(ti):
                t0, n, lc = TT[ti]
                rh = self.r_h[ti]
                mn, rmn = mean[ti % 2], r_mean[ti % 2]
                rs, rrs = rstd[ti % 2], r_rstd[ti % 2]
                for kc in range(8):
                    z = self.hT[:, kc, t0:t0 + n]
                    tp, rtp = tmp[cnt[1] % 3], r_tmp[cnt[1] % 3]
                    cnt[1] += 1
                    S.op("pool", lambda e: e.tensor_tensor(out=tp[:, 0:n], in0=z, in1=mn[:, 0:n], op=ALU.subtract), reads=[rh[kc], rmn], writes=[rtp])
                    S.op("dve", lambda e: e.tensor_tensor(out=tp[:, 0:n], in0=tp[:, 0:n], in1=rs[:, 0:n], op=ALU.mult), reads=[rtp, rrs], writes=[rtp])
                    S.op("act", lambda e: e.activation(out=z, in_=tp[:, 0:n], func=AF.Identity,
                                                       bias=self.lnp[:, 2 * sub + 1, i, kc:kc + 1], scale=self.lnp[:, 2 * sub, i, kc:kc + 1]),
                         reads=[rtp, self.r_lnp], writes=[rh[kc]])
                    if nxt is not None:
                        S.op("dve", lambda e: e.tensor_scalar(out=self.uT[:, kc, t0:t0 + n], in0=tp[:, 0:n],
                                                              scalar1=gsc[:, kc, lc:lc + 1], scalar2=bsc[:, kc, lc:lc + 1],
                                                              op0=ALU.mult, op1=ALU.add),
                             reads=[rtp, r_gb], writes=[self.r_u[ti]])

            nt = len(TT) - 1 if self.skip_ctx else len(TT)
            stats(0)
            finish_stats(0)
            for ti in range(nt):
                if ti + 1 < nt:
                    stats(ti + 1)
                normalize(ti)
                if ti + 1 < nt:
                    finish_stats(ti + 1)
            S.barrier()

    def mlp(self, i):
        S = self.S
        w1 = self.din["mlp_w1"][i]
        w2 = self.din["mlp_w2"][i]
        with ExitStack() as ph:
            mg = None
            if i + 1 < DEPTH:
                mstg = [self.sb([128, 8, 256], BF16, es=ph) for _ in range(2)]
                mg = self.mods_gen(i + 1, mstg, [Res("ms0"), Res("ms1")])
            ab = [self.sb([128, 4, 512], BF16, es=ph) for _ in range(2)]; r_ab = [Res(), Res()]
            rl = [self.sb([128, 512], BF16, es=ph) for _ in range(3)]; r_rl = [Res(), Res(), Res()]
            rli = 0
            step = 0
            for j in range(8):
                wa, r_wa = self.wnext()
                wb, r_wb = self.wnext()
                S.dma("pool", [(wa[:].rearrange("p (k n) -> p k n", k=8), w1[:, j * 512:(j + 1) * 512].rearrange("(k p) n -> p k n", p=128))],
                      r_wa, writes=[r_wa])
                S.dma("pool", [(wb[:].rearrange("p (k n) -> p k n", k=4), w2[j * 512:(j + 1) * 512, :].rearrange("(k p) n -> p k n", p=128))],
                      r_wb, writes=[r_wb])
                wav = wa[:].rearrange("p (k n) -> p k n", k=8)
                wbv = wb[:].rearrange("p (k n) -> p k n", k=4)
                for ti, (t0, n, lc) in enumerate(TT):
                    if self.skip_ctx and lc == 1:
                        continue
                    a_, r_a = ab[step % 2], r_ab[step % 2]
                    step += 1
                    if mg is not None:
                        try:
                            next(mg)
                        except StopIteration:
                            mg = None
                    for hc in range(4):
                        bk, rb = self.bank()
                        for kc in range(8):
                            S.op("pe", lambda e: e.matmul(bk[:, 0:n], lhsT=wav[:, kc, hc * 128:(hc + 1) * 128], rhs=self.uT[:, kc, t0:t0 + n],
                                                          start=(kc == 0), stop=(kc == 7)),
                                 reads=[r_wa, self.r_u[ti]], writes=[rb], same_ok=True)
                        r_, rr_ = rl[rli % 3], r_rl[rli % 3]
                        rli += 1
                        S.op("act", lambda e: e.activation(out=r_[:, 0:n], in_=bk[:, 0:n], func=AF.Relu), reads=[rb], writes=[rr_])
                        S.op("act", lambda e: e.activation(out=a_[:, hc, 0:n], in_=r_[:, 0:n], func=AF.Square),
                             reads=[rr_], writes=[r_a])
                    for oc in range(8):
                        bk, rb = self.bank()
                        for kc in range(4):
                            S.op("pe", lambda e: e.matmul(bk[:, 0:n], lhsT=wbv[:, kc, oc * 128:(oc + 1) * 128], rhs=a_[:, kc, 0:n],
                                                          start=(kc == 0), stop=(kc == 3)),
                                 reads=[r_wb, r_a], writes=[rb], same_ok=True)
                        hz = self.hT[:, oc, t0:t0 + n]
                        S.op("dve", lambda e: e.scalar_tensor_tensor(out=hz, in0=bk[:, 0:n], scalar=self.mcol(i, 5, oc, lc), in1=hz,
                                                                     op0=ALU.mult, op1=ALU.add),
                             reads=[rb, self.r_h[ti][oc], self.r_mod[i]], writes=[self.r_h[ti][oc]])
            if mg is not None:
                for _ in mg:
                    pass
            S.barrier()

    def store_out(self):
        S = self.S
        with ExitStack() as ph:
            os_ = [self.sb([128, D], F32, es=ph) for _ in range(2)]
            r_os = [Res("os0"), Res("os1")]
            nt = 18 if self.dbg else 16
            for t in range(nt):
                st, rs = os_[t % 2], r_os[t % 2]
                ti = min(t // 4, 4)
                for g in range(2):
                    bk, rb = self.bank()
                    for j in range(4):
                        S.op("pe", lambda e: e.transpose(bk[:, j * 128:(j + 1) * 128], self.hT[:, g * 4 + j, t * 128:(t + 1) * 128], self.ident[:]),
                             reads=[self.r_h[ti][g * 4 + j], self.r_ident], writes=[rb], same_ok=True)
                    if g == 0:
                        S.op("dve", lambda e: e.tensor_copy(st[:, 0:512], bk[:, 0:512]), reads=[rb], writes=[rs])
                    else:
                        S.op("act", lambda e: e.copy(st[:, 512:1024], bk[:, 0:512]), reads=[rb], writes=[rs])
                dst = self.out[t * 128:(t + 1) * 128, :] if t < 16 else self.out_c[(t - 16) * 128:(t - 15) * 128, :]
                ev = S.dma("sp", [(dst, st[:])], rs, reads=[rs])
                S.out_events.append(ev)
            S.finish()

    def build(self):
        self.declare()
        self.setup()
        self.modulate_all(0, 0)
        for i in range(self.depth_run):
            if i == DEPTH - 1 and not self.dbg:
                self.skip_ctx = True
            if self.mixers:
                self.mixer(i)
            self.layer_norm(i, 0, (i, 1))
            self.mlp(i)
            self.layer_norm(i, 1, (i + 1, 0) if i + 1 < DEPTH else None)
        self.store_out()


    def mixer(self, i):
        if i == 0:
            self.mla(i)
        elif i == 1:
            self.diff(i)
        elif i == 2:
            self.gla(i)
        elif i == 3:
            self.gdn(i)

    def mla(self, i):
        S = self.S
        SCALE = 96.0 ** -0.5
        with ExitStack() as ph:
            kp = [self.sb([128, T], BF16, es=ph) for _ in range(2)]; r_kp = [Res("kp0"), Res("kp1")]
            qtab = self.sb([128, T], F32, es=ph); r_qtab = Res("qtab")
            opad = self.sb([128, 2, 128], BF16, es=ph); r_opad = Res("opad")
            nrm = self.sb([128, 8], F32, es=ph); r_nrm = Res("nrm")
            epsc = self.sb([128, 1], F32, es=ph); r_eps = Res("eps")
            S.dma("sp", [(qtab[:], self.din["mla_qtab"])], r_qtab, writes=[r_qtab])
            S.op("pool", lambda e: e.memset(opad[:], 0.0), writes=[r_opad])
            S.op("pool", lambda e: e.memset(opad[:, 0, 0:64], 1.0), reads=[r_opad], writes=[r_opad])
            S.op("pool", lambda e: e.memset(opad[:, 1, 64:128], 1.0), reads=[r_opad], writes=[r_opad])
            S.op("pool", lambda e: e.memset(epsc[:], EPS), writes=[r_eps])
            self.load_cols(self.din["mla_q_norm"], 6, nrm[:, 0:6], r_nrm)
            self.load_cols(self.din["mla_kv_norm"], 2, nrm[:, 6:8], r_nrm)
            with ExitStack() as p1:
                raw = self.sb([128, 8, 512], F32, es=p1); r_raw = Res("raw")
                sq = [self.sb([128, 512], F32, es=p1) for _ in range(2)]; r_sq = [Res(), Res()]
                rs = self.sb([128, 2, 512], F32, es=p1); r_rs = Res("rs")
                kta = self.sb([128, 512], F32, es=p1); r_kta = Res("kta")
                ktb = self.sb([128, 512], F32, es=p1); r_ktb = Res("ktb")
                t1 = self.sb([128, 512], F32, es=p1); r_t1 = Res("t1")
                t2 = self.sb([128, 512], F32, es=p1); r_t2 = Res("t2")
                wi = []
                for blk in range(2):
                    w_, r_w = self.wnext()
                    S.dma("pool", [(w_[:].rearrange("p (k n) -> p k n", k=8),
                                    self.din["mla_w_in"][:, blk * 512:(blk + 1) * 512].rearrange("(k p) n -> p k n", p=128))], r_w, writes=[r_w])
                    wi.append((w_[:].rearrange("p (k n) -> p k n", k=8), r_w))
                w_, r_wk = self.wnext()
                wkr = w_[:, 0:2048].rearrange("p (k n) -> p k n", k=8)
                S.dma("pool", [(wkr, self.din["mla_wkr"].rearrange("(k p) n -> p k n", p=128))], r_wk, writes=[r_wk])
                for ti, (t0, n, lc) in enumerate(TT):
                    ru = self.r_u[ti]
                    S.dma("sp", [(kta[:, 0:n], self.din["mla_kta"][:, t0:t0 + n])], r_kta, writes=[r_kta])
                    S.dma("sp", [(ktb[:, 0:n], self.din["mla_ktb"][:, t0:t0 + n])], r_ktb, writes=[r_ktb])
                    bA, rbA = self.bank()
                    bB, rbB = self.bank()
                    for kc in range(8):
                        S.op("pe", lambda e: e.matmul(bA[:, 0:n], lhsT=wkr[:, kc, 0:128], rhs=self.uT[:, kc, t0:t0 + n], start=(kc == 0), stop=(kc == 7)),
                             reads=[r_wk, ru], writes=[rbA], same_ok=True)
                    for kc in range(8):
                        S.op("pe", lambda e: e.matmul(bB[:, 0:n], lhsT=wkr[:, kc, 128:256], rhs=self.uT[:, kc, t0:t0 + n], start=(kc == 0), stop=(kc == 7)),
                             reads=[r_wk, ru], writes=[rbB], same_ok=True)
                    S.op("dve", lambda e: e.tensor_tensor(out=t1[64:128, 0:n], in0=bA[64:128, 0:n], in1=kta[64:128, 0:n], op=ALU.mult),
                         reads=[rbA, r_kta], writes=[r_t1])
                    S.op("dve", lambda e: e.tensor_tensor(out=t2[64:128, 0:n], in0=bB[64:128, 0:n], in1=ktb[64:128, 0:n], op=ALU.mult),
                         reads=[rbB, r_ktb], writes=[r_t2])
                    S.op("pool", lambda e: e.tensor_tensor(out=kp[0][64:128, t0:t0 + n], in0=t1[64:128, 0:n], in1=t2[64:128, 0:n], op=ALU.add),
                         reads=[r_t1, r_t2], writes=[r_kp[0]])
                    S.op("pool", lambda e: e.tensor_copy(kp[1][64:128, t0:t0 + n], kp[0][64:128, t0:t0 + n]), reads=[r_kp[0]], writes=[r_kp[1]])
                    for oc in range(8):
                        wv, r_w = wi[oc // 4]
                        bk, rb = self.bank()
                        for kc in range(8):
                            S.op("pe", lambda e: e.matmul(bk[:, 0:n], lhsT=wv[:, kc, (oc % 4) * 128:(oc % 4 + 1) * 128], rhs=self.uT[:, kc, t0:t0 + n],
                                                          start=(kc == 0), stop=(kc == 7)),
                                 reads=[r_w, ru], writes=[rb], same_ok=True)
                        if oc % 2 == 0:
                            S.op("dve", lambda e: e.tensor_copy(raw[:, oc, 0:n], bk[:, 0:n]), reads=[rb], writes=[r_raw])
                        else:
                            S.op("act", lambda e: e.copy(raw[:, oc, 0:n], bk[:, 0:n]), reads=[rb], writes=[r_raw])
                    bq, rbq = self.banks[6], self.bank_res[6]
                    bkv, rbkv = self.banks[7], self.bank_res[7]
                    for oc in range(8):
                        s_, rs_ = sq[oc % 2], r_sq[oc % 2]
                        S.op("act", lambda e: e.activation(out=s_[:, 0:n], in_=raw[:, oc, 0:n], func=AF.Square), reads=[r_raw], writes=[rs_])
                        if oc < 6:
                            S.op("pe", lambda e: e.matmul(bq[:, 0:n], lhsT=self.ones[:], rhs=s_[:, 0:n], start=(oc == 0), stop=(oc == 5)),
                                 reads=[rs_, self.r_ident], writes=[rbq], same_ok=True)
                        else:
                            S.op("pe", lambda e: e.matmul(bkv[:, 0:n], lhsT=self.ones[:], rhs=s_[:, 0:n], start=(oc == 6), stop=(oc == 7)),
                                 reads=[rs_, self.r_ident], writes=[rbkv], same_ok=True)
                    for g, (bb, rbb, dim) in enumerate(((bq, rbq, 768.0), (bkv, rbkv, 256.0))):
                        S.op("act", lambda e: e.activation(out=rs[:, g, 0:n], in_=bb[:, 0:n], func=AF.Sqrt, bias=epsc[:, 0:1], scale=1.0 / dim),
                             reads=[rbb, r_eps], writes=[r_rs])
                        S.op("dve", lambda e: e.reciprocal(out=rs[:, g, 0:n], in_=rs[:, g, 0:n]), reads=[r_rs], writes=[r_rs])
                    for oc in range(8):
                        g = 0 if oc < 6 else 1
                        S.op("dve", lambda e: e.scalar_tensor_tensor(out=self.uT[:, oc, t0:t0 + n], in0=raw[:, oc, 0:n], scalar=nrm[:, oc:oc + 1],
                                                                     in1=rs[:, g, 0:n], op0=ALU.mult, op1=ALU.mult),
                             reads=[r_raw, r_nrm, r_rs], writes=[ru])
                S.barrier()
            with ExitStack() as p2:
                qp = [self.sb([128, T], BF16, es=p2) for _ in range(2)]; r_qp = [Res("qp0"), Res("qp1")]
                vp = [self.sb([128, 18, 128], BF16, es=p2) for _ in range(2)]; r_vp = [Res("vp0"), Res("vp1")]
                pt = [self.sb([128, 512], BF16, es=p2) for _ in range(4)]; r_pt = [Res() for _ in range(4)]
                rden = [self.sb([128, 512], F32, es=p2) for _ in range(2)]; r_rden = [Res("rden0"), Res("rden1")]
                attn_ctr = [0]
                pending = [None]
                self.nrr = 4
                self.bank_i = 0
                opr = [self.sb([128, 512], BF16, es=p2) for _ in range(2)]; r_opr = [Res(), Res()]
                wo = [self.sb([128, D], BF16, es=p2) for _ in range(2)]; r_wo = [Res("wo0"), Res("wo1")]
                for par in range(2):
                    S.op("pool", lambda e: e.memset(vp[par][:], 0.0), writes=[r_vp[par]])
                pti = 0
                wq = wkv = None
                for pair in range(8):
                    S.dma("pool", [(wo[pair % 2][:], self.din["mla_w_out"][pair * 128:(pair + 1) * 128, :])], r_wo[pair % 2], writes=[r_wo[pair % 2]])
                    for par in range(2):
                        h = pair * 2 + par
                        hl = h % 4
                        if hl == 0:
                            w_, r_wq = self.wnext()
                            wq = w_[:, 0:3072].rearrange("p (k n) -> p k n", k=6)
                            wkv = w_[:, 3072:4096].rearrange("p (k n) -> p k n", k=2)
                            S.dma("pool", [(wq, self.din["mla_wqx"][:, h * 128:(h + 4) * 128].rearrange("(k p) n -> p k n", p=128)),
                                           (wkv, self.din["mla_w_kvb"][:, h * 128:(h + 4) * 128].rearrange("(k p) n -> p k n", p=128))],
                                  r_wq, writes=[r_wq])
                        for ti, (t0, n, lc) in enumerate(TT):
                            ru = self.r_u[ti]
                            bk, rb = self.bank()
                            for kc in range(6):
                                S.op("pe", lambda e: e.matmul(bk[:, 0:n], lhsT=wq[:, kc, hl * 128:(hl + 1) * 128], rhs=self.uT[:, kc, t0:t0 + n],
                                                              start=(kc == 0), stop=(kc == 5)),
                                     reads=[r_wq, ru], writes=[rb], same_ok=True)
                            S.op("dve", lambda e: e.tensor_tensor(out=qp[par][:, t0:t0 + n], in0=bk[:, 0:n], in1=qtab[:, t0:t0 + n], op=ALU.mult),
                                 reads=[rb, r_qtab], writes=[r_qp[par]])
                            bk, rb = self.bank()
                            for kc in range(2):
                                S.op("pe", lambda e: e.matmul(bk[0:64, 0:n], lhsT=wkv[:, kc, hl * 128:hl * 128 + 64], rhs=self.uT[:, 6 + kc, t0:t0 + n],
                                                              start=(kc == 0), stop=(kc == 1)),
                                     reads=[r_wq, ru], writes=[rb], same_ok=True)
                            S.op("dve", lambda e: e.tensor_copy(kp[par][0:64, t0:t0 + n], bk[0:64, 0:n]), reads=[rb], writes=[r_kp[par]])
                        for g0 in range(0, 18, 8):
                            ng = min(8, 18 - g0)
                            bk, rb = self.bank()
                            for jt in range(ng):
                                kt = g0 + jt
                                for kc in range(2):
                                    S.op("pe", lambda e: e.matmul(bk[:, jt * 64:(jt + 1) * 64], lhsT=self.uT[:, 6 + kc, kt * 128:(kt + 1) * 128],
                                                                  rhs=wkv[:, kc, hl * 128 + 64:hl * 128 + 128], start=(kc == 0), stop=(kc == 1)),
                                         reads=[r_wq] + self.r_u, writes=[rb], same_ok=True)
                            S.op("dve", lambda e: e.tensor_copy(vp[par][:, g0:g0 + ng, par * 64:par * 64 + 64],
                                                                bk[:, 0:ng * 64].rearrange("p (j d) -> p j d", d=64)), reads=[rb], writes=[r_vp[par]])
                    for ti, (t0, n, lc) in enumerate(TT):
                        kts = list(range(18)) if lc == 0 else [16, 17]
                        items = [(par, kt) for par in range(2) for kt in kts]
                        nb_ = attn_ctr[0] % 2
                        attn_ctr[0] += 1
                        num, r_num = self.banks[4 + 2 * nb_], self.bank_res[4 + 2 * nb_]
                        den, r_den = self.banks[5 + 2 * nb_], self.bank_res[5 + 2 * nb_]
                        sbanks = {}

                        def issue_score(ix):
                            par, kt = items[ix]
                            bk, rb = self.bank()
                            S.op("pe", lambda e: e.matmul(bk[:, 0:n], lhsT=kp[par][:, kt * 128:(kt + 1) * 128], rhs=qp[par][:, t0:t0 + n], start=True, stop=True),
                                 reads=[r_kp[par], r_qp[par]], writes=[rb], same_ok=True)
                            sbanks[ix] = (bk, rb)
                        for ix in range(min(2, len(items))):
                            issue_score(ix)
                        for ix, (par, kt) in enumerate(items):
                            bk, rb = sbanks.pop(ix)
                            p_, rp_ = pt[pti % 4], r_pt[pti % 4]
                            pti += 1
                            S.op("act", lambda e: e.activation(out=p_[:, 0:n], in_=bk[:, 0:n], func=AF.Exp, scale=SCALE), reads=[rb], writes=[rp_])
                            if ix + 2 < len(items):
                                issue_score(ix + 2)
                            first = (ix == 0)
                            last = (ix == len(items) - 1)
                            S.op("pe", lambda e: e.matmul(num[:, 0:n], lhsT=vp[par][:, kt, :], rhs=p_[:, 0:n], start=first, stop=last),
                                 reads=[r_vp[par], rp_], writes=[r_num], same_ok=True)
                            S.op("pe", lambda e: e.matmul(den[:, 0:n], lhsT=opad[:, par, :], rhs=p_[:, 0:n], start=first, stop=last),
                                 reads=[r_opad, rp_], writes=[r_den], same_ok=True)

                        def epilogue(ti=ti, t0=t0, n=n, lc=lc, num=num, den=den, r_num=r_num, r_den=r_den, pair=pair, k=attn_ctr[0]):
                            rd_, rrd_ = rden[k % 2], r_rden[k % 2]
                            S.op("dve", lambda e: e.reciprocal(out=rd_[:, 0:n], in_=den[:, 0:n]), reads=[r_den], writes=[rrd_])
                            o_, ro_ = opr[k % 2], r_opr[k % 2]
                            S.op("dve", lambda e: e.tensor_tensor(out=o_[:, 0:n], in0=num[:, 0:n], in1=rd_[:, 0:n], op=ALU.mult),
                                 reads=[r_num, rrd_], writes=[ro_])
                            for oc in range(8):
                                bk, rb = self.bank()
                                S.op("pe", lambda e: e.matmul(bk[:, 0:n], lhsT=wo[pair % 2][:, oc * 128:(oc + 1) * 128], rhs=o_[:, 0:n], start=True, stop=True),
                                     reads=[r_wo[pair % 2], ro_], writes=[rb], same_ok=True)
                                hz = self.hT[:, oc, t0:t0 + n]
                                S.op("dve", lambda e: e.scalar_tensor_tensor(out=hz, in0=bk[:, 0:n], scalar=self.mcol(i, 2, oc, lc), in1=hz,
                                                                             op0=ALU.mult, op1=ALU.add),
                                     reads=[rb, self.r_h[ti][oc], self.r_mod[i]], writes=[self.r_h[ti][oc]])
                        if pending[0] is not None:
                            pending[0]()
                        pending[0] = epilogue
                if pending[0] is not None:
                    pending[0]()
                S.barrier()
        self.nrr = 6
        self.bank_i = 0

    def diff(self, i):
        S = self.S
        SCALE = 64.0 ** -0.5
        lam_init = 0.8 - 0.6 * math.exp(-0.3 * i)
        self.nrr = 4
        self.bank_i = 0
        with ExitStack() as ph:
            qp = [self.sb([128, T], BF16, es=ph) for _ in range(2)]; r_qp = [Res("qp0"), Res("qp1")]
            kp = [self.sb([128, T], BF16, es=ph) for _ in range(2)]; r_kp = [Res("kp0"), Res("kp1")]
            vp = self.sb([128, 18, 128], BF16, es=ph); r_vp = Res("vp")
            qtab = self.sb([128, 512], F32, es=ph); r_qtab = Res("qtab")
            kta = self.sb([128, 512], F32, es=ph); r_kta = Res("kta")
            ktb = self.sb([128, 512], F32, es=ph); r_ktb = Res("ktb")
            t1 = self.sb([128, 512], F32, es=ph); r_t1 = Res("t1")
            t2 = self.sb([128, 512], F32, es=ph); r_t2 = Res("t2")
            t1s = [self.sb([128, 512], F32, es=ph) for _ in range(2)]; r_t1s = [Res("t1s0"), Res("t1s1")]
            dctr = [0]
            pend1 = [None]
            pt = [self.sb([128, 512], BF16, es=ph) for _ in range(4)]; r_pt = [Res() for _ in range(4)]
            rd = self.sb([128, 2, 512], F32, es=ph); r_rd0 = Res("rd0"); r_rd1 = Res("rd1")
            onb = self.sb([128, 128], BF16, es=ph); r_onb = Res("onb")
            on_ = [self.sb([128, 512], BF16, es=ph) for _ in range(2)]; r_on = [Res(), Res()]
            wo = [self.sb([128, D], BF16, es=ph) for _ in range(2)]; r_wo = [Res("wo0"), Res("wo1")]
            lam = self.sb([128, 4], F32, es=ph); r_lam = Res("lam")
            lv = self.sb([64, 4], F32, es=ph); r_lv = Res("lv")
            sub = self.sb([128, 1], F32, es=ph); r_sub = Res("sub")
            epsc = self.sb([128, 1], F32, es=ph); r_eps = Res("eps")
            S.op("pool", lambda e: e.memset(epsc[:], EPS), writes=[r_eps])
            S.op("pool", lambda e: e.memset(onb[:], 1.0), writes=[r_onb])
            S.dma("sp", [(lv[:], self.din["diff_lam"])], r_lv, writes=[r_lv])
            S.op("dve", lambda e: e.tensor_tensor(out=lv[:, 0:1], in0=lv[:, 0:1], in1=lv[:, 1:2], op=ALU.mult), reads=[r_lv], writes=[r_lv])
            S.op("dve", lambda e: e.tensor_tensor(out=lv[:, 1:2], in0=lv[:, 2:3], in1=lv[:, 3:4], op=ALU.mult), reads=[r_lv], writes=[r_lv])
            bk, rb = self.bank()
            S.op("pe", lambda e: e.matmul(bk[:, 0:2], lhsT=self.ones[0:64, :], rhs=lv[:, 0:2], start=True, stop=True),
                 reads=[r_lv, self.r_ident], writes=[rb], same_ok=True)
            S.op("act", lambda e: e.activation(out=lam[:, 0:2], in_=bk[:, 0:2], func=AF.Exp), reads=[rb], writes=[r_lam])
            S.op("dve", lambda e: e.scalar_tensor_tensor(out=lam[:, 2:3], in0=lam[:, 1:2], scalar=-lam_init, in1=lam[:, 0:1], op0=ALU.add, op1=ALU.subtract),
                 reads=[r_lam], writes=[r_lam])
            self.load_cols(self.din["diff_subln"], 1, sub[:, 0:1], r_sub)
            S.op("dve", lambda e: e.tensor_scalar(out=sub[:], in0=sub[:], scalar1=1.0 - lam_init, scalar2=None, op0=ALU.mult), reads=[r_sub], writes=[r_sub])
            nums = [(self.banks[4], self.bank_res[4]), (self.banks[5], self.bank_res[5])]
            dens = [(self.banks[6], self.bank_res[6]), (self.banks[7], self.bank_res[7])]
            pti = 0
            pti = 0
            for h in range(8):
                S.dma("pool", [(wo[h % 2][:], self.din["diff_w_out"][h * 128:(h + 1) * 128, :])], r_wo[h % 2], writes=[r_wo[h % 2]])
                wv = None
                for m in range(2):
                    mi = h * 2 + m
                    w_, r_w = self.wnext()
                    wx = w_[:, 0:3072].rearrange("p (k n) -> p k n", k=8)
                    prs = [(wx, self.din["diff_wx"][:, mi * 384:(mi + 1) * 384].rearrange("(k p) n -> p k n", p=128))]
                    if m == 0:
                        wv = w_[:, 3072:4096].rearrange("p (k n) -> p k n", k=8)
                        r_wv = r_w
                        prs.append((wv, self.din["diff_wv"][:, h * 128:(h + 1) * 128].rearrange("(k p) n -> p k n", p=128)))
                    S.dma("pool", prs, r_w, writes=[r_w])
                    for ti, (t0, n, lc) in enumerate(TT):
                        ru = self.r_u[ti]
                        S.dma("sp", [(qtab[:, 0:n], self.din["diff_qtab"][:, t0:t0 + n])], r_qtab, writes=[r_qtab])
                        S.dma("sp", [(kta[:, 0:n], self.din["diff_kta"][:, t0:t0 + n])], r_kta, writes=[r_kta])
                        S.dma("sp", [(ktb[:, 0:n], self.din["diff_ktb"][:, t0:t0 + n])], r_ktb, writes=[r_ktb])
                        bq, rbq = self.bank()
                        for kc in range(8):
                            S.op("pe", lambda e: e.matmul(bq[:, 0:n], lhsT=wx[:, kc, 0:128], rhs=self.uT[:, kc, t0:t0 + n], start=(kc == 0), stop=(kc == 7)),
                                 reads=[r_w, ru], writes=[rbq], same_ok=True)
                        S.op("dve", lambda e: e.tensor_tensor(out=qp[m][:, t0:t0 + n], in0=bq[:, 0:n], in1=qtab[:, 0:n], op=ALU.mult),
                             reads=[rbq, r_qtab], writes=[r_qp[m]])
                        bA, rbA = self.bank()
                        for kc in range(8):
                            S.op("pe", lambda e: e.matmul(bA[:, 0:n], lhsT=wx[:, kc, 128:256], rhs=self.uT[:, kc, t0:t0 + n], start=(kc == 0), stop=(kc == 7)),
                                 reads=[r_w, ru], writes=[rbA], same_ok=True)
                        bB, rbB = self.bank()
                        for kc in range(8):
                            S.op("pe", lambda e: e.matmul(bB[:, 0:n], lhsT=wx[:, kc, 256:384], rhs=self.uT[:, kc, t0:t0 + n], start=(kc == 0), stop=(kc == 7)),
                                 reads=[r_w, ru], writes=[rbB], same_ok=True)
                        S.op("dve", lambda e: e.tensor_tensor(out=t1[:, 0:n], in0=bA[:, 0:n], in1=kta[:, 0:n], op=ALU.mult), reads=[rbA, r_kta], writes=[r_t1])
                        S.op("dve", lambda e: e.tensor_tensor(out=t2[:, 0:n], in0=bB[:, 0:n], in1=ktb[:, 0:n], op=ALU.mult), reads=[rbB, r_ktb], writes=[r_t2])
                        S.op("pool", lambda e: e.tensor_tensor(out=kp[m][:, t0:t0 + n], in0=t1[:, 0:n], in1=t2[:, 0:n], op=ALU.add),
                             reads=[r_t1, r_t2], writes=[r_kp[m]])
                for g0 in range(0, 18, 4):
                    ng = min(4, 18 - g0)
                    bk, rb = self.bank()
                    for jt in range(ng):
                        kt = g0 + jt
                        for kc in range(8):
                            S.op("pe", lambda e: e.matmul(bk[:, jt * 128:(jt + 1) * 128], lhsT=self.uT[:, kc, kt * 128:(kt + 1) * 128],
                                                          rhs=wv[:, kc, :], start=(kc == 0), stop=(kc == 7)),
                                 reads=[r_wv] + self.r_u, writes=[rb], same_ok=True)
                    S.op("act", lambda e: e.copy(vp[:, g0:g0 + ng, :], bk[:, 0:ng * 128].rearrange("p (j d) -> p j d", d=128)), reads=[rb], writes=[r_vp])
                for ti, (t0, n, lc) in enumerate(TT):
                    kts = list(range(18)) if lc == 0 else [16, 17]
                    k_ = dctr[0]
                    dctr[0] += 1
                    t1_, rt1_ = t1s[k_ % 2], r_t1s[k_ % 2]

                    def attend(m):
                        nonlocal pti
                        num, r_num = nums[m]
                        den, r_den = dens[m]
                        sbanks = {}

                        def issue_score(ix):
                            kt = kts[ix]
                            bk, rb = self.bank()
                            S.op("pe", lambda e: e.matmul(bk[:, 0:n], lhsT=kp[m][:, kt * 128:(kt + 1) * 128], rhs=qp[m][:, t0:t0 + n], start=True, stop=True),
                                 reads=[r_kp[m], r_qp[m]], writes=[rb], same_ok=True)
                            sbanks[ix] = (bk, rb)
                        for ix in range(min(2, len(kts))):
                            issue_score(ix)
                        for ix, kt in enumerate(kts):
                            bk, rb = sbanks.pop(ix)
                            p_, rp_ = pt[pti % 4], r_pt[pti % 4]
                            pti += 1
                            S.op("act", lambda e: e.activation(out=p_[:, 0:n], in_=bk[:, 0:n], func=AF.Exp, scale=SCALE), reads=[rb], writes=[rp_])
                            if ix + 2 < len(kts):
                                issue_score(ix + 2)
                            first, last = (ix == 0), (ix == len(kts) - 1)
                            S.op("pe", lambda e: e.matmul(num[:, 0:n], lhsT=vp[:, kt, :], rhs=p_[:, 0:n], start=first, stop=last),
                                 reads=[r_vp, rp_], writes=[r_num], same_ok=True)
                            S.op("pe", lambda e: e.matmul(den[:, 0:n], lhsT=onb[:], rhs=p_[:, 0:n], start=first, stop=last),
                                 reads=[r_onb, rp_], writes=[r_den], same_ok=True)

                    def ep0(n=n, t1_=t1_, rt1_=rt1_):
                        S.op("dve", lambda e: e.reciprocal(out=rd[:, 0, 0:n], in_=dens[0][0][:, 0:n]), reads=[dens[0][1]], writes=[r_rd0])
                        S.op("dve", lambda e: e.tensor_tensor(out=t1_[:, 0:n], in0=nums[0][0][:, 0:n], in1=rd[:, 0, 0:n], op=ALU.mult),
                             reads=[nums[0][1], r_rd0], writes=[rt1_])

                    def ep1(ti=ti, t0=t0, n=n, lc=lc, t1_=t1_, rt1_=rt1_, h=h, k_=k_):
                        S.op("dve", lambda e: e.reciprocal(out=rd[:, 1, 0:n], in_=dens[1][0][:, 0:n]), reads=[dens[1][1]], writes=[r_rd1])
                        S.op("dve", lambda e: e.scalar_tensor_tensor(out=t2[:, 0:n], in0=nums[1][0][:, 0:n], scalar=lam[:, 2:3], in1=rd[:, 1, 0:n],
                                                                     op0=ALU.mult, op1=ALU.mult), reads=[nums[1][1], r_rd1, r_lam], writes=[r_t2])
                        S.op("dve", lambda e: e.tensor_tensor(out=t1_[:, 0:n], in0=t1_[:, 0:n], in1=t2[:, 0:n], op=ALU.add), reads=[rt1_, r_t2], writes=[rt1_])
                        S.op("dve", lambda e: e.tensor_tensor(out=t2[:, 0:n], in0=t1_[:, 0:n], in1=t1_[:, 0:n], op=ALU.mult), reads=[rt1_], writes=[r_t2])
                        bk, rb = self.bank()
                        S.op("pe", lambda e: e.matmul(bk[:, 0:n], lhsT=self.ones[:], rhs=t2[:, 0:n], start=True, stop=True),
                             reads=[r_t2, self.r_ident], writes=[rb], same_ok=True)
                        S.op("act", lambda e: e.activation(out=t2[:, 0:n], in_=bk[:, 0:n], func=AF.Sqrt, bias=epsc[:, 0:1], scale=1.0 / 128.0),
                             reads=[rb, r_eps], writes=[r_t2])
                        S.op("dve", lambda e: e.reciprocal(out=t2[:, 0:n], in_=t2[:, 0:n]), reads=[r_t2], writes=[r_t2])
                        o_, ro_ = on_[k_ % 2], r_on[k_ % 2]
                        S.op("dve", lambda e: e.scalar_tensor_tensor(out=o_[:, 0:n], in0=t1_[:, 0:n], scalar=sub[:, 0:1], in1=t2[:, 0:n],
                                                                     op0=ALU.mult, op1=ALU.mult), reads=[rt1_, r_t2, r_sub], writes=[ro_])
                        for oc in range(8):
                            bk, rb = self.bank()
                            S.op("pe", lambda e: e.matmul(bk[:, 0:n], lhsT=wo[h % 2][:, oc * 128:(oc + 1) * 128], rhs=o_[:, 0:n], start=True, stop=True),
                                 reads=[r_wo[h % 2], ro_], writes=[rb], same_ok=True)
                            hz = self.hT[:, oc, t0:t0 + n]
                            S.op("dve", lambda e: e.scalar_tensor_tensor(out=hz, in0=bk[:, 0:n], scalar=self.mcol(i, 2, oc, lc), in1=hz,
                                                                         op0=ALU.mult, op1=ALU.add),
                                 reads=[rb, self.r_h[ti][oc], self.r_mod[i]], writes=[self.r_h[ti][oc]])
                    attend(0)
                    if pend1[0] is not None:
                        pend1[0]()
                    attend(1)
                    ep0()
                    pend1[0] = ep1
            if pend1[0] is not None:
                pend1[0]()
            S.barrier()
        self.nrr = 6
        self.bank_i = 0

    def gla(self, i):
        S = self.S
        QS = 128.0 ** -0.5
        win = self.din["gla_w_in"]
        with ExitStack() as ph:
            arr = [[self.sb([128, T], BF16, es=ph) for _ in range(3)] for _ in range(2)]
            r_arr = [[Res() for _ in range(3)] for _ in range(2)]
            vh = self.sb([128, 18, 256], BF16, es=ph); r_vh = Res("vh")
            oacc = self.sb([128, 2, T], BF16, es=ph); r_oacc = [Res(f"oacc{c}") for c in range(18)]
            rT = self.sb([32, T], BF16, es=ph); r_rT = Res("rT")
            gw = self.sb([32, 2, 512], BF16, es=ph); r_gw = Res("gw")
            gb = self.sb([128, 2, 4], F32, es=ph); r_gb = Res("gb")
            ng = self.sb([128, 2], F32, es=ph); r_ng = Res("ng")
            dec = self.sb([128, 2, 18], F32, es=ph); r_dec = Res("dec")
            cmask = self.sb([128, 512], F32, es=ph); r_cm = Res("cmask")
            msk = [self.sb([128, 128], F32, es=ph) for _ in range(2)]; r_msk = Res("msk")
            bA = self.sb([128, 512], F32, es=ph); r_bA = Res("bA")
            bB = self.sb([128, 512], F32, es=ph); r_bB = Res("bB")
            bC = self.sb([128, 512], F32, es=ph); r_bC = Res("bC")
            bD = self.sb([128, 512], F32, es=ph); r_bD = Res("bD")
            Sf = [self.sb([128, 256], F32, es=ph) for _ in range(2)]; r_Sf = [Res("Sf0"), Res("Sf1")]
            Sb = [self.sb([128, 256], BF16, es=ph) for _ in range(2)]; r_Sb = [Res("Sb0"), Res("Sb1")]
            Am = [self.sb([128, 128], BF16, es=ph) for _ in range(2)]; r_Am = [Res(), Res()]
            keT = [self.sb([128, 128], BF16, es=ph) for _ in range(2)]; r_keT = [Res(), Res()]
            ogn = [self.sb([128, 2, 512], BF16, es=ph) for _ in range(1)]; r_ogn = [Res()]
            epsc = self.sb([128, 1], F32, es=ph); r_eps = Res("eps")
            S.op("pool", lambda e: e.memset(epsc[:], EPS), writes=[r_eps])
            S.op("pool", lambda e: e.memset(cmask[:], 1.0), writes=[r_cm])
            for c in range(4):
                S.op("pool", lambda e: e.memset(cmask[:, c * 128:c * 128 + 1], 0.0), reads=[r_cm], writes=[r_cm])
            for d in range(2):
                S.op("pool", lambda e: e.memset(msk[d][:], 1.0), reads=[r_msk], writes=[r_msk])
                cm, pat = ((-1, [[1, 128]]) if d == 0 else (1, [[-1, 128]]))
                S.op("pool", lambda e: e.affine_select(out=msk[d][:], in_=msk[d][:], compare_op=ALU.is_ge, fill=0.0, base=0,
                                                      pattern=pat, channel_multiplier=cm), reads=[r_msk], writes=[r_msk])
            S.op("pool", lambda e: e.memset(gw[:], 0.0), writes=[r_gw])
            S.dma("pool", [(gw[0:16, 0, :], self.din["gla_gw"][0]), (gw[16:32, 1, :], self.din["gla_gw"][1])], r_gw, reads=[r_gw], writes=[r_gw])
            for d in range(2):
                self.load_cols(self.din["gla_gb"][d], 4, gb[:, d, :], r_gb)
            S.op("dve", lambda e: e.tensor_scalar(out=gb[:], in0=gb[:], scalar1=-1.0, scalar2=None, op0=ALU.mult), reads=[r_gb], writes=[r_gb])
            self.load_cols(self.din["gla_norm"], 2, ng[:, 0:2], r_ng)
            w_, r_w = self.wnext()
            wr = w_[:, 0:256].rearrange("p (k n) -> p k n", k=8)
            S.dma("pool", [(wr, win[:, 3072:3104].rearrange("(k p) n -> p k n", p=128))], r_w, writes=[r_w])
            for ti, (t0, n, lc) in enumerate(TT):
                bk, rb = self.bank()
                for kc in range(8):
                    S.op("pe", lambda e: e.matmul(bk[0:32, 0:n], lhsT=wr[:, kc, :], rhs=self.uT[:, kc, t0:t0 + n], start=(kc == 0), stop=(kc == 7)),
                         reads=[r_w, self.r_u[ti]], writes=[rb], same_ok=True)
                S.op("act", lambda e: e.copy(rT[:, t0:t0 + n], bk[0:32, 0:n]), reads=[rb], writes=[r_rT])

            for h in range(4):
                wA_, r_wA = self.wnext()
                wqk = wA_[:, 0:2048].rearrange("p (k n) -> p k n", k=8)
                S.dma("pool", [(wqk[:, :, 0:128], win[:, h * 128:(h + 1) * 128].rearrange("(k p) n -> p k n", p=128)),
                               (wqk[:, :, 128:256], win[:, 512 + h * 128:512 + (h + 1) * 128].rearrange("(k p) n -> p k n", p=128))],
                      r_wA, writes=[r_wA])
                wB_, r_wB = self.wnext()
                wv = wB_[:, 0:2048].rearrange("p (k n) -> p k n", k=8)
                wg = wB_[:, 2048:4096].rearrange("p (k n) -> p k n", k=8)
                S.dma("pool", [(wv, win[:, 1024 + h * 256:1024 + (h + 1) * 256].rearrange("(k p) n -> p k n", p=128)),
                               (wg, win[:, 2048 + h * 256:2048 + (h + 1) * 256].rearrange("(k p) n -> p k n", p=128))],
                      r_wB, writes=[r_wB])
                wC_, r_wC = self.wnext()
                wo = wC_[:, 0:2048].rearrange("p (k n) -> p k n", k=2)
                S.dma("pool", [(wo, self.din["gla_w_out"][h * 256:(h + 1) * 256, :].rearrange("(k p) n -> p k n", p=128))], r_wC, writes=[r_wC])
                for g0 in range(0, 18, 2):
                    bk, rb = self.bank()
                    for jt in range(2):
                        kt = g0 + jt
                        for kc in range(8):
                            S.op("pe", lambda e: e.matmul(bk[:, jt * 256:(jt + 1) * 256], lhsT=self.uT[:, kc, kt * 128:(kt + 1) * 128], rhs=wv[:, kc, :],
                                                          start=(kc == 0), stop=(kc == 7)), reads=[r_wB] + self.r_u, writes=[rb], same_ok=True)
                    S.op("act", lambda e: e.copy(vh[:, g0:g0 + 2, :], bk[:, 0:512].rearrange("p (j d) -> p j d", d=256)), reads=[rb], writes=[r_vh])
                for ti, (t0, n, lc) in enumerate(TT):
                    ru = self.r_u[ti]
                    nch = n // 128
                    bq, rbq = self.bank()
                    for kc in range(8):
                        S.op("pe", lambda e: e.matmul(bq[:, 0:n], lhsT=wqk[:, kc, 0:128], rhs=self.uT[:, kc, t0:t0 + n], start=(kc == 0), stop=(kc == 7)),
                             reads=[r_wA, ru], writes=[rbq], same_ok=True)
                    bkk, rbk = self.bank()
                    for kc in range(8):
                        S.op("pe", lambda e: e.matmul(bkk[:, 0:n], lhsT=wqk[:, kc, 128:256], rhs=self.uT[:, kc, t0:t0 + n], start=(kc == 0), stop=(kc == 7)),
                             reads=[r_wA, ru], writes=[rbk], same_ok=True)
                    for d in range(2):
                        bx, rbx = self.bank()
                        S.op("pe", lambda e: e.matmul(bx[:, 0:n], lhsT=gw[:, d, h * 128:(h + 1) * 128], rhs=rT[:, t0:t0 + n], start=True, stop=True),
                             reads=[r_gw, r_rT], writes=[rbx], same_ok=True)
                        S.op("act", lambda e: e.activation(out=bA[:, 0:n], in_=bx[:, 0:n], func=AF.Exp, bias=gb[:, d, h:h + 1], scale=-1.0),
                             reads=[rbx, r_gb], writes=[r_bA])
                        S.op("act", lambda e: e.activation(out=bA[:, 0:n], in_=bA[:, 0:n], func=AF.Ln, bias=1.0, scale=1.0), reads=[r_bA], writes=[r_bA])
                        S.op("dve", lambda e: e.tensor_tensor_scan(out=bB[:, 0:n], data0=cmask[:, 0:n], data1=bA[:, 0:n], initial=0.0,
                                                                   op0=ALU.mult, op1=ALU.add), reads=[r_cm, r_bA], writes=[r_bB])
                        for c in range(nch):
                            gc = t0 // 128 + c
                            ce = c * 128 + 127
                            S.op("act", lambda e: e.activation(out=dec[:, d, gc:gc + 1], in_=bB[:, ce:ce + 1], func=AF.Exp, scale=-1.0 / 16), reads=[r_bB], writes=[r_dec])
                            S.op("dve", lambda e: e.tensor_scalar(out=bD[:, c * 128:(c + 1) * 128], in0=bB[:, c * 128:(c + 1) * 128], scalar1=bB[:, ce:ce + 1],
                                                                  scalar2=None, op0=ALU.subtract), reads=[r_bB], writes=[r_bD])
                        if d == 0:
                            S.op("act", lambda e: e.activation(out=bC[:, 0:n], in_=bB[:, 0:n], func=AF.Exp, scale=-1.0 / 16), reads=[r_bB], writes=[r_bC])
                            S.op("dve", lambda e: e.scalar_tensor_tensor(out=arr[d][0][:, t0:t0 + n], in0=bq[:, 0:n], scalar=QS, in1=bC[:, 0:n], op0=ALU.mult, op1=ALU.mult),
                                 reads=[rbq, r_bC], writes=[r_arr[d][0]])
                            S.op("act", lambda e: e.activation(out=bC[:, 0:n], in_=bB[:, 0:n], func=AF.Exp, scale=1.0 / 16), reads=[r_bB], writes=[r_bC])
                            S.op("dve", lambda e: e.tensor_tensor(out=arr[d][1][:, t0:t0 + n], in0=bkk[:, 0:n], in1=bC[:, 0:n], op=ALU.mult),
                                 reads=[rbk, r_bC], writes=[r_arr[d][1]])
                            S.op("act", lambda e: e.activation(out=bC[:, 0:n], in_=bD[:, 0:n], func=AF.Exp, scale=1.0 / 16), reads=[r_bD], writes=[r_bC])
                            S.op("dve", lambda e: e.tensor_tensor(out=arr[d][2][:, t0:t0 + n], in0=bkk[:, 0:n], in1=bC[:, 0:n], op=ALU.mult),
                                 reads=[rbk, r_bC], writes=[r_arr[d][2]])
                        else:
                            S.op("dve", lambda e: e.tensor_tensor(out=bD[:, 0:n], in0=bA[:, 0:n], in1=bD[:, 0:n], op=ALU.subtract), reads=[r_bA, r_bD], writes=[r_bD])
                            S.op("act", lambda e: e.activation(out=bC[:, 0:n], in_=bD[:, 0:n], func=AF.Exp, scale=-1.0 / 16), reads=[r_bD], writes=[r_bC])
                            S.op("dve", lambda e: e.scalar_tensor_tensor(out=arr[d][0][:, t0:t0 + n], in0=bq[:, 0:n], scalar=QS, in1=bC[:, 0:n], op0=ALU.mult, op1=ALU.mult),
                                 reads=[rbq, r_bC], writes=[r_arr[d][0]])
                            S.op("act", lambda e: e.activation(out=bC[:, 0:n], in_=bD[:, 0:n], func=AF.Exp, scale=1.0 / 16), reads=[r_bD], writes=[r_bC])
                            S.op("dve", lambda e: e.tensor_tensor(out=arr[d][1][:, t0:t0 + n], in0=bkk[:, 0:n], in1=bC[:, 0:n], op=ALU.mult),
                                 reads=[rbk, r_bC], writes=[r_arr[d][1]])
                            S.op("dve", lambda e: e.tensor_tensor(out=bD[:, 0:n], in0=bA[:, 0:n], in1=bB[:, 0:n], op=ALU.subtract), reads=[r_bA, r_bB, r_bC], writes=[r_bD])
                            S.op("act", lambda e: e.activation(out=bC[:, 0:n], in_=bD[:, 0:n], func=AF.Exp, scale=1.0 / 16), reads=[r_bD], writes=[r_bC])
                            S.op("dve", lambda e: e.tensor_tensor(out=arr[d][2][:, t0:t0 + n], in0=bkk[:, 0:n], in1=bC[:, 0:n], op=ALU.mult),
                                 reads=[rbk, r_bC], writes=[r_arr[d][2]])
                for d in range(2):
                    S.op("pool", lambda e: e.memset(Sf[d][:], 0.0), reads=[r_Sf[d]], writes=[r_Sf[d]])
                    S.op("pool", lambda e: e.memset(Sb[d][:], 0.0), reads=[r_Sb[d]], writes=[r_Sb[d]])
                order = [[16, 17] + list(range(16)), [17, 16] + list(range(15, -1, -1))]
                written = set()
                for step in range(18):
                    for d in range(2):
                        c = order[d][step]
                        cs = slice(c * 128, (c + 1) * 128)
                        qd, ki, ke = arr[d]
                        ba, rba = self.bank()
                        S.op("pe", lambda e: e.matmul(ba[:, 0:128], lhsT=ki[:, cs], rhs=qd[:, cs], start=True, stop=True),
                             reads=[r_arr[d][1], r_arr[d][0]], writes=[rba], same_ok=True)
                        bke, rbke = self.bank()
                        S.op("pe", lambda e: e.matmul(bke[:, 0:128], lhsT=ke[:, cs], rhs=self.identb[:], start=True, stop=True),
                             reads=[r_arr[d][2], self.r_ident], writes=[rbke], same_ok=True)
                        S.op("dve", lambda e: e.tensor_tensor(out=Am[d][:], in0=ba[:, 0:128], in1=msk[d][:], op=ALU.mult), reads=[rba, r_msk], writes=[r_Am[d]])
                        S.op("act", lambda e: e.copy(keT[d][:], bke[:, 0:128]), reads=[rbke], writes=[r_keT[d]])
                        bo, rbo = self.bank()
                        for j in range(2):
                            S.op("pe", lambda e: e.matmul(bo[:, j * 128:(j + 1) * 128], lhsT=Sb[d][:, j * 128:(j + 1) * 128], rhs=qd[:, cs], start=True, stop=False),
                                 reads=[r_Sb[d], r_arr[d][0]], writes=[rbo], same_ok=True)
                            S.op("pe", lambda e: e.matmul(bo[:, j * 128:(j + 1) * 128], lhsT=vh[:, c, j * 128:(j + 1) * 128], rhs=Am[d][:], start=False, stop=True),
                                 reads=[r_vh, r_Am[d]], writes=[rbo], same_ok=True)
                        ov = oacc[:, :, cs]
                        pv = bo[:, 0:256].rearrange("p (j c) -> p j c", j=2)
                        if c not in written:
                            written.add(c)
                            S.op("act", lambda e: e.copy(ov, pv), reads=[rbo], writes=[r_oacc[c]])
                        else:
                            S.op("dve", lambda e: e.tensor_tensor(out=ov, in0=pv, in1=ov, op=ALU.add), reads=[rbo, r_oacc[c]], writes=[r_oacc[c]])
                        bs, rbs = self.bank()
                        S.op("pe", lambda e: e.matmul(bs[:, 0:256], lhsT=keT[d][:], rhs=vh[:, c, :], start=True, stop=True),
                             reads=[r_keT[d], r_vh], writes=[rbs], same_ok=True)
                        S.op("dve", lambda e: e.scalar_tensor_tensor(out=Sf[d][:], in0=Sf[d][:], scalar=dec[:, d, c:c + 1], in1=bs[:, 0:256], op0=ALU.mult, op1=ALU.add),
                             reads=[r_Sf[d], r_dec, rbs], writes=[r_Sf[d]])
                        S.op("act", lambda e: e.copy(Sb[d][:], Sf[d][:]), reads=[r_Sf[d]], writes=[r_Sb[d]])
                for ti, (t0, n, lc) in enumerate(TT):
                    ru = self.r_u[ti]
                    roa = r_oacc[t0 // 128:(t0 + n) // 128]
                    bss = self.banks[6]; rbss = self.bank_res[6]
                    for j in range(2):
                        S.op("act", lambda e: e.activation(out=bA[:, 0:n], in_=oacc[:, j, t0:t0 + n], func=AF.Square), reads=roa + [r_bA], writes=[r_bA])
                        S.op("pe", lambda e: e.matmul(bss[:, 0:n], lhsT=self.ones[:], rhs=bA[:, 0:n], start=(j == 0), stop=(j == 1)),
                             reads=[r_bA, self.r_ident], writes=[rbss], same_ok=True)
                    S.op("act", lambda e: e.activation(out=bB[:, 0:n], in_=bss[:, 0:n], func=AF.Sqrt, bias=epsc[:, 0:1], scale=1.0 / 256.0), reads=[rbss, r_eps], writes=[r_bB])
                    S.op("dve", lambda e: e.reciprocal(out=bB[:, 0:n], in_=bB[:, 0:n]), reads=[r_bB], writes=[r_bB])
                    og, rog = ogn[0], r_ogn[0]
                    for j in range(2):
                        bg, rbg = self.bank()
                        for kc in range(8):
                            S.op("pe", lambda e: e.matmul(bg[:, 0:n], lhsT=wg[:, kc, j * 128:(j + 1) * 128], rhs=self.uT[:, kc, t0:t0 + n], start=(kc == 0), stop=(kc == 7)),
                                 reads=[r_wB, ru], writes=[rbg], same_ok=True)
                        S.op("act", lambda e: e.activation(out=bC[:, 0:n], in_=bg[:, 0:n], func=AF.Silu), reads=[rbg], writes=[r_bC])
                        S.op("dve", lambda e: e.scalar_tensor_tensor(out=bD[:, 0:n], in0=oacc[:, j, t0:t0 + n], scalar=ng[:, j:j + 1], in1=bB[:, 0:n], op0=ALU.mult, op1=ALU.mult),
                             reads=roa + [r_ng, r_bB], writes=[r_bD])
                        S.op("dve", lambda e: e.tensor_tensor(out=og[:, j, 0:n], in0=bD[:, 0:n], in1=bC[:, 0:n], op=ALU.mult), reads=[r_bD, r_bC], writes=[rog])
                    for oc in range(8):
                        bk, rb = self.bank()
                        for j in range(2):
                            S.op("pe", lambda e: e.matmul(bk[:, 0:n], lhsT=wo[:, j, oc * 128:(oc + 1) * 128], rhs=og[:, j, 0:n], start=(j == 0), stop=(j == 1)),
                                 reads=[r_wC, rog], writes=[rb], same_ok=True)
                        hz = self.hT[:, oc, t0:t0 + n]
                        S.op("dve", lambda e: e.scalar_tensor_tensor(out=hz, in0=bk[:, 0:n], scalar=self.mcol(i, 2, oc, lc), in1=hz, op0=ALU.mult, op1=ALU.add),
                             reads=[rb, self.r_h[ti][oc], self.r_mod[i]], writes=[self.r_h[ti][oc]])
            S.barrier()

    def gdn(self, i):
        S = self.S
        R32 = mybir.dt.float32r
        win = self.din["gdn_w_in"]
        rr = lambda ap: ap.bitcast(R32)
        with ExitStack() as ph:
            sbp = lambda shape, dt: self.sb(shape, dt, es=ph)
            cw = sbp([128, 5, 32], F32); r_cw = Res("cw")
            for j in range(5):
                self.load_cols(self.din["gdn_conv"][j], 32, cw[:, j, :], r_cw)
            r_msk = Res("gmsk")
            inclT = [sbp([128, 128], F32) for _ in range(2)]
            strict2 = sbp([128, 2, 128], BF16)
            inclT2 = sbp([128, 2, 128], BF16)
            bd16 = sbp([128, 128], BF16); off16 = sbp([128, 128], BF16); off32 = sbp([128, 128], BF16); off64 = sbp([128, 128], BF16)
            with ExitStack() as mk:
                strict = [self.sb([128, 128], F32, es=mk) for _ in range(2)]
                specs = [(strict[0], ALU.is_gt, 1, [[-1, 128]]), (strict[1], ALU.is_gt, -1, [[1, 128]]),
                         (inclT[0], ALU.is_ge, -1, [[1, 128]]), (inclT[1], ALU.is_ge, 1, [[-1, 128]])]
                for (t_, cmp_, cm, pat) in specs:
                    S.op("pool", lambda e: e.memset(t_[:], 1.0), reads=[r_msk], writes=[r_msk])
                    S.op("pool", lambda e: e.affine_select(out=t_[:], in_=t_[:], compare_op=cmp_, fill=0.0, base=0, pattern=pat, channel_multiplier=cm),
                         reads=[r_msk], writes=[r_msk])
                for d in range(2):
                    S.op("dve", lambda e: e.tensor_copy(strict2[:, d, :], strict[d][:]), reads=[r_msk], writes=[r_msk])
                    S.op("dve", lambda e: e.tensor_copy(inclT2[:, d, :], inclT[d][:]), reads=[r_msk], writes=[r_msk])
                bsel = self.sb([8, 128], F32, es=mk)
                bd = {}
                for b_ in (16, 32, 64):
                    nb = 128 // b_
                    S.op("pool", lambda e: e.memset(bsel[:], 1.0), reads=[r_msk], writes=[r_msk])
                    S.op("pool", lambda e: e.affine_select(out=bsel[:], in_=bsel[:], compare_op=ALU.is_ge, fill=0.0, base=0, pattern=[[1, 128]], channel_multiplier=-b_),
                         reads=[r_msk], writes=[r_msk])
                    S.op("pool", lambda e: e.affine_select(out=bsel[:], in_=bsel[:], compare_op=ALU.is_ge, fill=0.0, base=b_ - 1, pattern=[[-1, 128]], channel_multiplier=b_),
                         reads=[r_msk], writes=[r_msk])
                    bk, rb = self.bank()
                    S.op("pe", lambda e: e.matmul(bk[:, 0:128], lhsT=bsel[0:nb, :], rhs=bsel[0:nb, :], start=True, stop=True), reads=[r_msk], writes=[rb], same_ok=True)
                    bd[b_] = self.sb([128, 128], F32, es=mk)
                    S.op("dve", lambda e: e.tensor_copy(bd[b_][:], bk[:, 0:128]), reads=[rb, r_msk], writes=[r_msk])
                S.op("dve", lambda e: e.tensor_copy(bd16[:], bd[16][:]), reads=[r_msk], writes=[r_msk])
                S.op("dve", lambda e: e.tensor_tensor(out=off16[:], in0=bd[32][:], in1=bd[16][:], op=ALU.subtract), reads=[r_msk], writes=[r_msk])
                S.op("dve", lambda e: e.tensor_tensor(out=off32[:], in0=bd[64][:], in1=bd[32][:], op=ALU.subtract), reads=[r_msk], writes=[r_msk])
                S.op("dve", lambda e: e.tensor_scalar(out=off64[:], in0=bd[64][:], scalar1=-1.0, scalar2=1.0, op0=ALU.mult, op1=ALU.add), reads=[r_msk], writes=[r_msk])
                S.barrier()
            b4 = lambda m_: m_[:].unsqueeze(1).broadcast_to([128, 4, 128])
            hc = sbp([128, 64], F32); r_hc = Res("hc")
            S.dma("sp", [(hc[:], self.din["gdn_hc"].partition_broadcast(128))], r_hc, writes=[r_hc])
            S.op("act", lambda e: e.activation(out=hc[:, 0:32], in_=hc[:, 0:32], func=AF.Exp), reads=[r_hc], writes=[r_hc])
            S.op("dve", lambda e: e.tensor_scalar(out=hc[:, 0:32], in0=hc[:, 0:32], scalar1=-1.0, scalar2=None, op0=ALU.mult), reads=[r_hc], writes=[r_hc])
            ngrep = sbp([128, 128], F32); r_ngr = Res("ngrep")
            S.dma("sp", [(ngrep[:], self.din["gdn_norm"].partition_broadcast(128))], r_ngr, writes=[r_ngr])
            epsc = sbp([128, 1], F32); r_eps = Res("eps")
            S.op("pool", lambda e: e.memset(epsc[:], EPS), writes=[r_eps])
            qkT = sbp([128, 2, T], BF16); r_qkT = Res("qkT")
            kn = sbp([128, 18, 128], BF16); r_kn = Res("kn")
            vt = sbp([128, 18, 256], BF16); r_vt = Res("vt")
            oacc = sbp([128, 18, 256], BF16); r_oacc = [Res(f"go{c}") for c in range(18)]
            sc_names = ("negb", "gc", "e", "ecoef", "negbe", "dl", "g", "ngc")
            sc = {n_: sbp([128, 18, 4], F32) for n_ in sc_names}
            r_sc = Res("gsc")

            for kh in range(8):
                wA_, r_wA = self.wnext()
                wA = wA_[:].rearrange("p (k n) -> p k n", k=8)
                S.dma("pool", [(wA[:, :, 0:128], win[:, kh * 128:(kh + 1) * 128].rearrange("(k p) n -> p k n", p=128)),
                               (wA[:, :, 128:256], win[:, 1024 + kh * 128:1024 + (kh + 1) * 128].rearrange("(k p) n -> p k n", p=128)),
                               (wA[:, :, 256:512], win[:, 2048 + kh * 256:2048 + (kh + 1) * 256].rearrange("(k p) n -> p k n", p=128))],
                      r_wA, writes=[r_wA])
                wB_, r_wB = self.wnext()
                wz = wB_[:, 0:2048].rearrange("p (k n) -> p k n", k=8)
                wgt = wB_[:, 2048:2112].rearrange("p (k n) -> p k n", k=8)
                S.dma("pool", [(wz, win[:, 4096 + kh * 256:4096 + (kh + 1) * 256].rearrange("(k p) n -> p k n", p=128)),
                               (wgt, self.din["gdn_wg"][:, kh * 8:(kh + 1) * 8].rearrange("(k p) n -> p k n", p=128))], r_wB, writes=[r_wB])
                wC_, r_wC = self.wnext()
                wo = wC_[:, 0:2048].rearrange("p (k n) -> p k n", k=2)
                S.dma("pool", [(wo, self.din["gdn_w_out"][kh * 256:(kh + 1) * 256, :].rearrange("(k p) n -> p k n", p=128))], r_wC, writes=[r_wC])
                with ExitStack() as p1:
                    xpad = self.sb([128, 4, 2312], BF16, es=p1); r_xp = Res("xpad")
                    dg = self.sb([128, 4, 5, 128], BF16, es=p1); r_dg = Res("dg")
                    cvs = [self.sb([128, 512], F32, es=p1) for _ in range(3)]; r_cvs = [Res() for _ in range(3)]
                    junk = self.sb([128, 128], F32, es=p1); r_junk = Res("junk")
                    sss = [self.sb([128, 2], F32, es=p1) for _ in range(3)]; r_sss = [Res() for _ in range(3)]
                    qns = [self.sb([128, 128], BF16, es=p1) for _ in range(3)]; r_qns = [Res() for _ in range(3)]
                    S.op("pool", lambda e: e.memset(xpad[:], 0.0), writes=[r_xp])
                    gch = [kh, 8 + kh, 16 + 2 * kh, 17 + 2 * kh]
                    for ch in range(4):
                        for j in range(5):
                            S.op("pool", lambda e: e.tensor_scalar(out=dg[:, ch, j, :], in0=self.identb[:], scalar1=cw[:, j, gch[ch]:gch[ch] + 1], scalar2=1.0, op0=ALU.mult, op1=ALU.mult),
                                 reads=[r_cw, self.r_ident], writes=[r_dg])
                    for ch in range(4):
                        for ti, (t0, n, lc) in enumerate(TT):
                            bk, rb = self.bank()
                            for kc in range(8):
                                S.op("pe", lambda e: e.matmul(bk[:, 0:n], lhsT=wA[:, kc, ch * 128:(ch + 1) * 128], rhs=self.uT[:, kc, t0:t0 + n], start=(kc == 0), stop=(kc == 7)),
                                     reads=[r_wA, self.r_u[ti]], writes=[rb], same_ok=True)
                            c0 = t0 + 2 if lc == 0 else 2054
                            if (ch + ti) % 2 == 0:
                                S.op("act", lambda e: e.copy(xpad[:, ch, c0:c0 + n], bk[:, 0:n]), reads=[rb], writes=[r_xp])
                            else:
                                S.op("dve", lambda e: e.tensor_copy(xpad[:, ch, c0:c0 + n], bk[:, 0:n]), reads=[rb], writes=[r_xp])
                    def p1A(t):
                        b0 = t * 128 + 2 if t < 16 else 2054 + (t - 16) * 128
                        cv, r_cv = cvs[t % 3], r_cvs[t % 3]
                        ss, r_ss = sss[t % 3], r_sss[t % 3]
                        bk, rb = self.bank()
                        for ch in range(4):
                            for j in range(5):
                                S.op("pe", lambda e: e.matmul(bk[:, ch * 128:(ch + 1) * 128], lhsT=xpad[:, ch, b0 + j - 2:b0 + j - 2 + 128], rhs=dg[:, ch, j, :],
                                                              start=(j == 0), stop=(j == 4)), reads=[r_xp, r_dg], writes=[rb], same_ok=True)
                        S.op("act", lambda e: e.activation(out=cv[:], in_=bk[:, 0:512], func=AF.Silu), reads=[rb], writes=[r_cv])
                        for q_ in range(2):
                            S.op("act", lambda e: e.activation(out=junk[:], in_=cv[:, q_ * 128:(q_ + 1) * 128], func=AF.Square, accum_out=ss[:, q_:q_ + 1]),
                                 reads=[r_cv], writes=[r_ss])
                        S.op("act", lambda e: e.activation(out=ss[:], in_=ss[:], func=AF.Sqrt, bias=epsc[:, 0:1], scale=1.0), reads=[r_ss, r_eps], writes=[r_ss])

                    def p1B(t):
                        cv, r_cv = cvs[t % 3], r_cvs[t % 3]
                        ss, r_ss = sss[t % 3], r_sss[t % 3]
                        qn, r_qn = qns[t % 3], r_qns[t % 3]
                        S.op("dve", lambda e: e.reciprocal(out=ss[:], in_=ss[:]), reads=[r_ss], writes=[r_ss])
                        S.op("dve", lambda e: e.tensor_scalar(out=qn[:], in0=cv[:, 0:128], scalar1=ss[:, 0:1], scalar2=128.0 ** -0.5, op0=ALU.mult, op1=ALU.mult),
                             reads=[r_cv, r_ss], writes=[r_qn])
                        S.op("dve", lambda e: e.tensor_scalar(out=kn[:, t, :], in0=cv[:, 128:256], scalar1=ss[:, 1:2], scalar2=None, op0=ALU.mult),
                             reads=[r_cv, r_ss], writes=[r_kn])
                        S.op("pool", lambda e: e.tensor_copy(vt[:, t, :], cv[:, 256:512]), reads=[r_cv], writes=[r_vt])
                        b2, rb2 = self.bank()
                        S.op("pe", lambda e: e.matmul(b2[:, 0:128], lhsT=qn[:], rhs=self.identb[:], start=True, stop=True), reads=[r_qn, self.r_ident], writes=[rb2], same_ok=True)
                        S.op("pe", lambda e: e.matmul(b2[:, 128:256], lhsT=kn[:, t, :], rhs=self.identb[:], start=True, stop=True), reads=[r_kn, self.r_ident], writes=[rb2], same_ok=True)
                        S.op("act", lambda e: e.copy(qkT[:, :, t * 128:(t + 1) * 128], b2[:, 0:256].rearrange("p (a c) -> p a c", a=2)), reads=[rb2], writes=[r_qkT])
                    p1A(0)
                    for t in range(18):
                        if t + 1 < 18:
                            p1A(t + 1)
                        p1B(t)
                    S.barrier()
                bk, rb = self.bank()
                for t in range(18):
                    for kc in range(8):
                        S.op("pe", lambda e: e.matmul(bk[:, t * 8:(t + 1) * 8], lhsT=self.uT[:, kc, t * 128:(t + 1) * 128], rhs=wgt[:, kc, :], start=(kc == 0), stop=(kc == 7)),
                             reads=[r_wB] + self.r_u, writes=[rb], same_ok=True)
                graw = bk[:, 0:144].rearrange("p (t c) -> p t c", c=8)
                S.op("act", lambda e: e.activation(out=sc["negb"][:], in_=graw[:, :, 0:4], func=AF.Sigmoid), reads=[rb], writes=[r_sc])
                S.op("dve", lambda e: e.tensor_scalar(out=sc["negb"][:], in0=sc["negb"][:], scalar1=-1.0, scalar2=None, op0=ALU.mult), reads=[r_sc], writes=[r_sc])
                for m in range(4):
                    d_, j_ = m // 2, m % 2
                    hidx = d_ * 16 + 2 * kh + j_
                    S.op("act", lambda e: e.activation(out=sc["g"][:, :, m], in_=graw[:, :, 4 + m], func=AF.Exp, bias=hc[:, 32 + hidx:33 + hidx], scale=1.0),
                         reads=[rb, r_hc], writes=[r_sc])
                S.op("act", lambda e: e.activation(out=sc["g"][:], in_=sc["g"][:], func=AF.Ln, bias=1.0, scale=1.0), reads=[r_sc], writes=[r_sc])
                for m in range(4):
                    d_, j_ = m // 2, m % 2
                    hidx = d_ * 16 + 2 * kh + j_
                    S.op("dve", lambda e: e.tensor_scalar(out=sc["g"][:, :, m], in0=sc["g"][:, :, m], scalar1=hc[:, hidx:hidx + 1], scalar2=None, op0=ALU.mult),
                         reads=[r_sc, r_hc], writes=[r_sc])
                bk, rb = self.bank()
                gv = sc["g"][:]
                S.op("pe", lambda e: e.matmul(bk[:, 0:72].rearrange("p (t c) -> p t c", c=4)[:, :, 0:2], lhsT=inclT[0][:], rhs=gv[:, :, 0:2], start=True, stop=True),
                     reads=[r_sc, r_msk], writes=[rb], same_ok=True)
                S.op("pe", lambda e: e.matmul(bk[:, 0:72].rearrange("p (t c) -> p t c", c=4)[:, :, 2:4], lhsT=inclT[1][:], rhs=gv[:, :, 2:4], start=True, stop=True),
                     reads=[r_sc, r_msk], writes=[rb], same_ok=True)
                S.op("pe", lambda e: e.matmul(bk[:, 128:200], lhsT=self.ones[:], rhs=gv.rearrange("p t c -> p (t c)"), start=True, stop=True),
                     reads=[r_sc, self.r_ident], writes=[rb], same_ok=True)
                gcp = bk[:, 0:72].rearrange("p (t c) -> p t c", c=4)
                glp = bk[:, 128:200].rearrange("p (t c) -> p t c", c=4)
                S.op("dve", lambda e: e.tensor_copy(sc["gc"][:], gcp), reads=[rb], writes=[r_sc])
                S.op("dve", lambda e: e.tensor_copy(sc["dl"][:], glp), reads=[rb], writes=[r_sc])
                S.op("dve", lambda e: e.tensor_tensor(out=sc["ecoef"][:], in0=glp, in1=sc["gc"][:], op=ALU.subtract), reads=[rb, r_sc], writes=[r_sc])
                S.op("dve", lambda e: e.tensor_scalar(out=sc["ngc"][:], in0=sc["gc"][:], scalar1=-1.0, scalar2=None, op0=ALU.mult), reads=[r_sc], writes=[r_sc])
                S.op("act", lambda e: e.activation(out=sc["e"][:], in_=sc["gc"][:], func=AF.Exp), reads=[r_sc], writes=[r_sc])
                S.op("act", lambda e: e.activation(out=sc["dl"][:], in_=sc["dl"][:], func=AF.Exp), reads=[r_sc], writes=[r_sc])
                S.op("act", lambda e: e.activation(out=sc["ecoef"][:], in_=sc["ecoef"][:], func=AF.Exp), reads=[r_sc], writes=[r_sc])
                S.op("dve", lambda e: e.tensor_tensor(out=sc["negbe"][:], in0=sc["negb"][:], in1=sc["e"][:], op=ALU.mult), reads=[r_sc], writes=[r_sc])
                with ExitStack() as p3:
                    h4 = lambda: self.sb([128, 4, 128], BF16, es=p3)
                    f4 = lambda: self.sb([128, 4, 128], F32, es=p3)
                    ST = []
                    for st_ in range(2):
                        B = dict(X=[h4(), h4()], Y=[h4(), h4()], W=h4(), Tm=h4(), Y0=h4(), AT=h4(),
                                 Gm=self.sb([128, 2, 128], BF16, es=p3), QKm=self.sb([128, 2, 128], BF16, es=p3), scr=f4(), rscr=Res(),
                                 rX=[Res(), Res()], rY=[Res(), Res()], rW=Res(), rTm=Res(), rY0=Res(), rAT=Res(), rGm=Res(), rQKm=Res())
                        ST.append(B)
                    Rm = h4(); r_Rm = Res("Rm")
                    vn = h4(); r_vn = Res("vn")
                    kdec = h4(); r_kdec = Res("kdec")
                    bv = h4(); r_bv = Res("bv")
                    Sf = f4(); r_Sf = Res("Sf")
                    Sb = h4(); r_Sb = Res("Sb")
                    ot = h4(); r_ot = Res("ot")
                    S.op("pool", lambda e: e.memset(Sf[:], 0.0), writes=[r_Sf])
                    S.op("pool", lambda e: e.memset(Sb[:], 0.0), writes=[r_Sb])
                    order = [[16, 17] + list(range(16)), [17, 16] + list(range(15, -1, -1))]
                    written = set()
                    kT = lambda c: qkT[:, 1, c * 128:(c + 1) * 128]
                    qT = lambda c: qkT[:, 0, c * 128:(c + 1) * 128]
                    pv4 = lambda b_: b_[:, 0:512].rearrange("p (m c) -> p m c", m=4)

                    def pre(step, B):
                        X, Y, W, Tm, Y0, AT, Gm, QKm = B["X"], B["Y"], B["W"], B["Tm"], B["Y0"], B["AT"], B["Gm"], B["QKm"]
                        rX, rY, rW, rTm, rY0, rAT, rGm, rQKm = B["rX"], B["rY"], B["rW"], B["rTm"], B["rY0"], B["rAT"], B["rGm"], B["rQKm"]
                        cc = [order[0][step], order[1][step]]
                        cm_ = [cc[m // 2] for m in range(4)]
                        scrA = scrB = B["scr"]
                        r_scrA = r_scrB = B["rscr"]
                        bk, rb = self.bank()
                        for d in range(2):
                            S.op("pe", lambda e: e.matmul(bk[:, d * 128:(d + 1) * 128], lhsT=kT(cc[d]), rhs=kT(cc[d]), start=True, stop=True), reads=[r_qkT], writes=[rb], same_ok=True)
                            S.op("pe", lambda e: e.matmul(bk[:, 256 + d * 128:256 + (d + 1) * 128], lhsT=kT(cc[d]), rhs=qT(cc[d]), start=True, stop=True), reads=[r_qkT], writes=[rb], same_ok=True)
                        S.op("dve", lambda e: e.tensor_tensor(out=Gm[:], in0=bk[:, 0:256].rearrange("p (d c) -> p d c", d=2), in1=strict2[:], op=ALU.mult), reads=[rb, r_msk], writes=[rGm])
                        S.op("dve", lambda e: e.tensor_tensor(out=QKm[:], in0=bk[:, 256:512].rearrange("p (d c) -> p d c", d=2), in1=inclT2[:], op=ALU.mult), reads=[rb, r_msk], writes=[rQKm])
                        for m in range(4):
                            S.op("pool", lambda e: e.tensor_scalar(out=scrA[:, m, :], in0=self.ident[:], scalar1=sc["gc"][:, cm_[m], m:m + 1], scalar2=1.0, op0=ALU.mult, op1=ALU.mult),
                                 reads=[r_sc, self.r_ident], writes=[r_scrA])
                        yield
                        bb, rbb = self.bank()
                        for m in range(4):
                            S.op("pe", lambda e: e.matmul(bb[:, m * 128:(m + 1) * 128], lhsT=self.ones[:], rhs=scrA[:, m, :], start=True, stop=True),
                                 reads=[r_scrA, self.r_ident], writes=[rbb], same_ok=True)
                        for m in range(4):
                            S.op("act", lambda e: e.activation(out=scrA[:, m, :], in_=bb[:, m * 128:(m + 1) * 128], func=AF.Relu, bias=sc["ngc"][:, cm_[m], m:m + 1], scale=1.0),
                                 reads=[rbb, r_sc], writes=[r_scrA])
                        S.op("act", lambda e: e.activation(out=X[1][:], in_=scrA[:], func=AF.Exp, scale=-1.0), reads=[r_scrA], writes=[rX[1]])
                        for m in range(4):
                            S.op("act", lambda e: e.activation(out=scrB[:, m, :], in_=bb[:, m * 128:(m + 1) * 128], func=AF.Relu, bias=sc["gc"][:, cm_[m], m:m + 1], scale=-1.0),
                                 reads=[rbb, r_sc], writes=[r_scrB])
                        S.op("act", lambda e: e.activation(out=Y[1][:], in_=scrB[:], func=AF.Exp, scale=-1.0), reads=[r_scrB], writes=[rY[1]])
                        yield
                        for m in range(4):
                            d = m // 2
                            S.op("dve", lambda e: e.scalar_tensor_tensor(out=X[0][:, m, :], in0=X[1][:, m, :], scalar=sc["negb"][:, cm_[m], m:m + 1], in1=Gm[:, d, :],
                                                                         op0=ALU.mult, op1=ALU.mult), reads=[rX[1], r_sc, rGm], writes=[rX[0]])
                        for d in range(2):
                            S.op("pool", lambda e: e.tensor_tensor(out=AT[:, 2 * d:2 * d + 2, :], in0=Y[1][:, 2 * d:2 * d + 2, :],
                                                                   in1=QKm[:, d:d + 1, :].broadcast_to([128, 2, 128]), op=ALU.mult), reads=[rQKm, rY[1]], writes=[rAT])
                        yield
                        bk, rb = self.bank()
                        for m in range(4):
                            S.op("pe", lambda e: e.matmul(bk[:, m * 128:(m + 1) * 128], lhsT=X[0][:, m, :], rhs=self.identb[:], start=True, stop=True),
                                 reads=[rX[0], self.r_ident], writes=[rb], same_ok=True)
                        S.op("act", lambda e: e.copy(Y0[:], pv4(bk)), reads=[rb], writes=[rY0])
                        yield
                        S.op("dve", lambda e: e.tensor_tensor(out=X[1][:], in0=X[0][:], in1=b4(bd16), op=ALU.mult), reads=[rX[0], r_msk, rAT], writes=[rX[1]])
                        S.op("dve", lambda e: e.tensor_tensor(out=Y[1][:], in0=Y0[:], in1=b4(bd16), op=ALU.mult), reads=[rY0, r_msk, rAT], writes=[rY[1]])
                        S.op("dve", lambda e: e.tensor_tensor(out=W[:], in0=Y[1][:], in1=b4(self.identb), op=ALU.add), reads=[rY[1], self.r_ident], writes=[rW])
                        yield
                        cur = 1
                        for lev in range(3):
                            nxt = 1 - cur
                            bx, rbx = self.bank()
                            for m in range(4):
                                S.op("pe", lambda e: e.matmul(bx[:, m * 128:(m + 1) * 128], lhsT=Y[cur][:, m, :], rhs=X[cur][:, m, :], start=True, stop=True),
                                     reads=[rY[cur], rX[cur]], writes=[rbx], same_ok=True)
                            if lev < 2:
                                by, rby = self.bank()
                                for m in range(4):
                                    S.op("pe", lambda e: e.matmul(by[:, m * 128:(m + 1) * 128], lhsT=X[cur][:, m, :], rhs=Y[cur][:, m, :], start=True, stop=True),
                                         reads=[rY[cur], rX[cur]], writes=[rby], same_ok=True)
                            S.op("act", lambda e: e.copy(X[nxt][:], pv4(bx)), reads=[rbx], writes=[rX[nxt]])
                            if lev < 2:
                                S.op("dve", lambda e: e.tensor_copy(Y[nxt][:], pv4(by)), reads=[rby], writes=[rY[nxt]])
                            yield
                            bw, rbw = self.bank()
                            for m in range(4):
                                S.op("pe", lambda e: e.matmul(bw[:, m * 128:(m + 1) * 128], lhsT=X[nxt][:, m, :], rhs=W[:, m, :], start=True, stop=True),
                                     reads=[rX[nxt], rW], writes=[rbw], same_ok=True)
                            S.op("dve", lambda e: e.tensor_tensor(out=W[:], in0=pv4(bw), in1=W[:], op=ALU.add), reads=[rbw, rW], writes=[rW])
                            cur = nxt
                            yield
                        bk, rb = self.bank()
                        for m in range(4):
                            S.op("pe", lambda e: e.matmul(bk[:, m * 128:(m + 1) * 128], lhsT=W[:, m, :], rhs=self.identb[:], start=True, stop=True),
                                 reads=[rW, self.r_ident], writes=[rb], same_ok=True)
                        S.op("act", lambda e: e.copy(Tm[:], pv4(bk)), reads=[rb], writes=[rTm])
                        yield
                        for li, offm in enumerate((off16, off32, off64)):
                            S.op("dve", lambda e: e.tensor_tensor(out=X[0][:], in0=Y0[:], in1=b4(offm), op=ALU.mult), reads=[rY0, r_msk], writes=[rX[0]])
                            bz, rbz = self.bank()
                            for m in range(4):
                                S.op("pe", lambda e: e.matmul(bz[:, m * 128:(m + 1) * 128], lhsT=X[0][:, m, :], rhs=Tm[:, m, :], start=True, stop=True),
                                     reads=[rX[0], rTm], writes=[rbz], same_ok=True)
                            S.op("act", lambda e: e.copy(X[1][:], pv4(bz)), reads=[rbz], writes=[rX[1]])
                            yield
                            if li < 2:
                                bt, rbt = self.bank()
                                for m in range(4):
                                    S.op("pe", lambda e: e.matmul(bt[:, m * 128:(m + 1) * 128], lhsT=W[:, m, :], rhs=X[1][:, m, :], start=True, stop=True),
                                         reads=[rW, rX[1]], writes=[rbt], same_ok=True)
                            bw, rbw = self.bank()
                            for m in range(4):
                                S.op("pe", lambda e: e.matmul(bw[:, m * 128:(m + 1) * 128], lhsT=X[1][:, m, :], rhs=W[:, m, :], start=True, stop=True),
                                     reads=[rX[1], rW], writes=[rbw], same_ok=True)
                            if li < 2:
                                S.op("dve", lambda e: e.tensor_tensor(out=Tm[:], in0=pv4(bt), in1=Tm[:], op=ALU.add), reads=[rbt, rTm], writes=[rTm])
                            S.op("dve", lambda e: e.tensor_tensor(out=W[:], in0=pv4(bw), in1=W[:], op=ALU.add), reads=[rbw, rW], writes=[rW])
                            yield

                    def chain(step, B):
                        W, AT, rW, rAT = B["W"], B["AT"], B["rW"], B["rAT"]
                        cc = [order[0][step], order[1][step]]
                        cm_ = [cc[m // 2] for m in range(4)]
                        for m in range(4):
                            S.op("pool", lambda e: e.tensor_scalar(out=kdec[:, m, :], in0=kn[:, cm_[m], :], scalar1=sc["ecoef"][:, cm_[m], m:m + 1], scalar2=1.0, op0=ALU.mult, op1=ALU.mult),
                                 reads=[r_kn, r_sc], writes=[r_kdec])
                            S.op("pool", lambda e: e.tensor_scalar(out=bv[:, m, :], in0=vt[:, cm_[m], (m % 2) * 128:(m % 2 + 1) * 128], scalar1=sc["negb"][:, cm_[m], m:m + 1],
                                                                   scalar2=-1.0, op0=ALU.mult, op1=ALU.mult), reads=[r_vt, r_sc], writes=[r_bv])
                        yield
                        bks, rbks = self.bank()
                        for m in range(4):
                            S.op("pe", lambda e: e.matmul(bks[:, m * 128:(m + 1) * 128], lhsT=kT(cm_[m]), rhs=Sb[:, m, :], start=True, stop=True), reads=[r_qkT, r_Sb], writes=[rbks], same_ok=True)
                        bo1, rbo1 = self.bank()
                        for m in range(4):
                            S.op("pe", lambda e: e.matmul(bo1[:, m * 128:(m + 1) * 128], lhsT=qT(cm_[m]), rhs=Sb[:, m, :], start=True, stop=True), reads=[r_qkT, r_Sb], writes=[rbo1], same_ok=True)
                        for m in range(4):
                            S.op("dve", lambda e: e.scalar_tensor_tensor(out=Rm[:, m, :], in0=bks[:, m * 128:(m + 1) * 128], scalar=sc["negbe"][:, cm_[m], m:m + 1], in1=bv[:, m, :],
                                                                         op0=ALU.mult, op1=ALU.add), reads=[rbks, r_sc, r_bv], writes=[r_Rm])
                            S.op("act", lambda e: e.activation(out=ot[:, m, :], in_=bo1[:, m * 128:(m + 1) * 128], func=AF.Copy, scale=sc["e"][:, cm_[m], m:m + 1]),
                                 reads=[rbo1, r_sc], writes=[r_ot])
                        yield
                        bvn, rbvn = self.bank()
                        for m in range(4):
                            S.op("pe", lambda e: e.matmul(bvn[:, m * 128:(m + 1) * 128], lhsT=W[:, m, :], rhs=Rm[:, m, :], start=True, stop=True), reads=[rW, r_Rm], writes=[rbvn], same_ok=True)
                        S.op("act", lambda e: e.copy(vn[:], pv4(bvn)), reads=[rbvn], writes=[r_vn])
                        yield
                        bo2, rbo2 = self.bank()
                        for m in range(4):
                            S.op("pe", lambda e: e.matmul(bo2[:, m * 128:(m + 1) * 128], lhsT=AT[:, m, :], rhs=vn[:, m, :], start=True, stop=True), reads=[rAT, r_vn], writes=[rbo2], same_ok=True)
                        bst, rbst = self.bank()
                        for m in range(4):
                            S.op("pe", lambda e: e.matmul(bst[:, m * 128:(m + 1) * 128], lhsT=kdec[:, m, :], rhs=vn[:, m, :], start=True, stop=True), reads=[r_kdec, r_vn], writes=[rbst], same_ok=True)
                        yield
                        for m in range(4):
                            S.op("dve", lambda e: e.scalar_tensor_tensor(out=Sf[:, m, :], in0=Sf[:, m, :], scalar=sc["dl"][:, cm_[m], m:m + 1], in1=bst[:, m * 128:(m + 1) * 128],
                                                                         op0=ALU.mult, op1=ALU.add), reads=[r_Sf, r_sc, rbst], writes=[r_Sf])
                        S.op("act", lambda e: e.copy(Sb[:], Sf[:]), reads=[r_Sf], writes=[r_Sb])
                        S.op("dve", lambda e: e.tensor_tensor(out=ot[:], in0=pv4(bo2), in1=ot[:], op=ALU.add), reads=[rbo2, r_ot], writes=[r_ot])
                        for d in range(2):
                            c = cc[d]
                            src = ot[:, 2 * d:2 * d + 2, :]
                            dst = oacc[:, c, :].rearrange("p (j v) -> p j v", j=2)
                            if c not in written:
                                written.add(c)
                                S.op("pool", lambda e: e.tensor_copy(dst, src), reads=[r_ot], writes=[r_oacc[c]])
                            else:
                                S.op("pool", lambda e: e.tensor_tensor(out=dst, in0=dst, in1=src, op=ALU.add), reads=[r_ot, r_oacc[c]], writes=[r_oacc[c]])

                    def run_all(g):
                        for _ in g:
                            pass

                    pend = None
                    for p_ in range(9):
                        ga, gb = pre(2 * p_, ST[0]), pre(2 * p_ + 1, ST[1])
                        next(ga); next(gb)
                        if pend is not None:
                            run_all(chain(pend[0], ST[0]))
                        next(ga); next(gb)
                        if pend is not None:
                            run_all(chain(pend[1], ST[1]))
                        live = [True, True]
                        gens = [ga, gb]
                        while any(live):
                            for gi in range(2):
                                if live[gi]:
                                    try:
                                        next(gens[gi])
                                    except StopIteration:
                                        live[gi] = False
                        pend = (2 * p_, 2 * p_ + 1)
                    run_all(chain(pend[0], ST[0]))
                    run_all(chain(pend[1], ST[1]))
                    S.barrier()
                with ExitStack() as p4:
                    ogT = self.sb([128, 2, 512], BF16, es=p4); r_ogT = Res("ogT")
                    ogs = [self.sb([128, 256], F32, es=p4) for _ in range(2)]; r_ogs = [Res(), Res()]
                    ogbs = [self.sb([128, 256], BF16, es=p4) for _ in range(2)]; r_ogbs = [Res(), Res()]
                    zss = [self.sb([128, 256], F32, es=p4) for _ in range(2)]; r_zss = [Res(), Res()]
                    junk = self.sb([128, 128], F32, es=p4); r_junk = Res("junk")
                    sss4 = [self.sb([128, 2], F32, es=p4) for _ in range(2)]; r_sss4 = [Res(), Res()]
                    ogTs = [ogT, self.sb([128, 2, 512], BF16, es=p4)]; r_ogTs = [r_ogT, Res("ogT1")]
                    tile_of = [min(t // 4, 4) for t in range(18)]
                    zb = {}

                    def p4A(t):
                        ti = tile_of[t]
                        zs, r_zs = zss[t % 2], r_zss[t % 2]
                        ss, r_ss = sss4[t % 2], r_sss4[t % 2]
                        for j in range(2):
                            S.op("act", lambda e: e.activation(out=junk[:], in_=oacc[:, t, j * 128:(j + 1) * 128], func=AF.Square, accum_out=ss[:, j:j + 1]),
                                 reads=[r_oacc[t]], writes=[r_ss])
                        S.op("act", lambda e: e.activation(out=ss[:], in_=ss[:], func=AF.Sqrt, bias=epsc[:, 0:1], scale=1.0 / 128.0), reads=[r_ss, r_eps], writes=[r_ss])
                        bz, rbz = self.bank()
                        for kc in range(8):
                            S.op("pe", lambda e: e.matmul(bz[:, 0:256], lhsT=self.uT[:, kc, t * 128:(t + 1) * 128], rhs=wz[:, kc, :], start=(kc == 0), stop=(kc == 7)),
                                 reads=[r_wB, self.r_u[ti]], writes=[rbz], same_ok=True)
                        S.op("act", lambda e: e.activation(out=zs[:], in_=bz[:, 0:256], func=AF.Silu), reads=[rbz], writes=[r_zs])

                    def p4B(t):
                        ti = tile_of[t]
                        t0, n, lc = TT[ti]
                        tt = t - t0 // 128
                        og, r_og = ogs[t % 2], r_ogs[t % 2]
                        ogb, r_ogb = ogbs[t % 2], r_ogbs[t % 2]
                        zs, r_zs = zss[t % 2], r_zss[t % 2]
                        ss, r_ss = sss4[t % 2], r_sss4[t % 2]
                        ogT_, r_ogT_ = ogTs[ti % 2], r_ogTs[ti % 2]
                        S.op("dve", lambda e: e.reciprocal(out=ss[:], in_=ss[:]), reads=[r_ss], writes=[r_ss])
                        for j in range(2):
                            S.op("dve", lambda e: e.scalar_tensor_tensor(out=og[:, j * 128:(j + 1) * 128], in0=oacc[:, t, j * 128:(j + 1) * 128], scalar=ss[:, j:j + 1], in1=ngrep[:],
                                                                         op0=ALU.mult, op1=ALU.mult), reads=[r_oacc[t], r_ss, r_ngr], writes=[r_og])
                        S.op("dve", lambda e: e.tensor_tensor(out=ogb[:], in0=og[:], in1=zs[:], op=ALU.mult), reads=[r_og, r_zs], writes=[r_ogb])
                        b2, rb2 = self.bank()
                        for j in range(2):
                            S.op("pe", lambda e: e.matmul(b2[:, j * 128:(j + 1) * 128], lhsT=ogb[:, j * 128:(j + 1) * 128], rhs=self.identb[:], start=True, stop=True),
                                 reads=[r_ogb, self.r_ident], writes=[rb2], same_ok=True)
                        S.op("act", lambda e: e.copy(ogT_[:, :, tt * 128:(tt + 1) * 128], b2[:, 0:256].rearrange("p (j c) -> p j c", j=2)), reads=[rb2], writes=[r_ogT_])
                        if tt == n // 128 - 1:
                            for oc in range(8):
                                bk, rb = self.bank()
                                for j in range(2):
                                    S.op("pe", lambda e: e.matmul(bk[:, 0:n], lhsT=wo[:, j, oc * 128:(oc + 1) * 128], rhs=ogT_[:, j, 0:n], start=(j == 0), stop=(j == 1)),
                                         reads=[r_wC, r_ogT_], writes=[rb], same_ok=True)
                                hz = self.hT[:, oc, t0:t0 + n]
                                S.op("dve", lambda e: e.scalar_tensor_tensor(out=hz, in0=bk[:, 0:n], scalar=self.mcol(i, 2, oc, lc), in1=hz, op0=ALU.mult, op1=ALU.add),
                                     reads=[rb, self.r_h[ti][oc], self.r_mod[i]], writes=[self.r_h[ti][oc]])
                    n4 = 16 if self.skip_ctx else 18
                    p4A(0)
                    for t in range(n4):
                        if t + 1 < n4:
                            p4A(t + 1)
                        p4B(t)
                    S.barrier()


def build_program(depth_run=DEPTH, mixers=True, dbg=False):
    nc = bass.Bass("TRN2", target_bir_lowering=False)
    es = ExitStack()
    with es:
        kb = KB(nc, es, depth_run, mixers, dbg)
        kb.build()
        print("instructions", kb.S.ninst, "sems", kb.S.nsem, flush=True)
    return nc, kb


def _rope_tables(dim):
    n_freq = dim // 4
    inv = (10000.0 ** (-np.arange(n_freq, dtype=np.float32) / n_freq)).astype(np.float32)
    tok = np.arange(TL)
    row = (tok // 64).astype(np.float32)
    col = (tok % 64).astype(np.float32)
    ang = np.concatenate([row[:, None] * inv, col[:, None] * inv], -1).astype(np.float32)
    c = np.ones((dim // 2, T), np.float32)
    s = np.zeros((dim // 2, T), np.float32)
    c[:, :TL] = np.cos(ang).T
    s[:, :TL] = np.sin(ang).T
    return c, s


def _mla_host(inputs, shared, g):
    w_in = g("mla_w_in")[0]
    w_qb = g("mla_w_qb")[0]
    ev = np.arange(0, 32, 2)
    od = ev + 1
    cols = []
    for h in range(16):
        b = h * 96
        cols += list(range(b, b + 64)) + list(b + 64 + ev) + list(b + 64 + od) + list(b + 64 + ev) + list(b + 64 + od)
    shared["mla_wqx"] = np.ascontiguousarray(w_qb[:, cols])
    ia = list(1024 + ev) * 4
    ib = list(1024 + od) * 4
    shared["mla_wkr"] = np.ascontiguousarray(np.concatenate(
        [w_in[:, 0:64], w_in[:, ia], w_in[:, 0:64], w_in[:, ib]], axis=1))
    c, s = _rope_tables(32)
    one = np.ones((64, T), np.float32)
    shared["mla_qtab"] = np.concatenate([one, c, s, s, c], 0)
    shared["mla_kta"] = np.concatenate([one, c, -c, s, s], 0)
    shared["mla_ktb"] = np.concatenate([one, -s, s, c, c], 0)


def _diff_host(inputs, shared, g):
    w_in = g("diff_w_in")[0]
    ev = np.arange(0, 64, 2)
    od = ev + 1
    cols = []
    for h in range(8):
        for m in range(2):
            bq = h * 128 + m * 64
            bk = 1024 + h * 128 + m * 64
            cols += list(bq + ev) + list(bq + od) + list(bq + ev) + list(bq + od)
            cols += list(bk + ev) * 4
            cols += list(bk + od) * 4
    shared["diff_wx"] = np.ascontiguousarray(w_in[:, cols])
    shared["diff_wv"] = np.ascontiguousarray(w_in[:, 2048:3072])
    shared["diff_w_out"] = g("diff_w_out")[0]
    c, s = _rope_tables(64)
    shared["diff_qtab"] = np.concatenate([c, s, s, c], 0)
    shared["diff_kta"] = np.concatenate([c, -c, s, s], 0)
    shared["diff_ktb"] = np.concatenate([-s, s, c, c], 0)
    shared["diff_lam"] = np.ascontiguousarray(np.stack([g("diff_lambda_q1")[0], g("diff_lambda_k1")[0],
                                                        g("diff_lambda_q2")[0], g("diff_lambda_k2")[0]], axis=1))
    shared["diff_subln"] = g("diff_subln")[0].reshape(1, 128)


def _gla_host(inputs, shared, g):
    shared["gla_w_in"] = g("gla_w_in")[0]
    shared["gla_gw"] = np.ascontiguousarray(np.stack([g("gla_gate_w_fwd")[0], g("gla_gate_w_bwd")[0]], 0))
    shared["gla_gb"] = np.ascontiguousarray(np.stack([g("gla_gate_b_fwd")[0].reshape(4, 128), g("gla_gate_b_bwd")[0].reshape(4, 128)], 0))
    shared["gla_norm"] = g("gla_norm")[0].reshape(2, 128)
    shared["gla_w_out"] = g("gla_w_out")[0]


def _gdn_host(inputs, shared, g):
    w_in = g("gdn_w_in")[0]
    shared["gdn_w_in"] = w_in
    cols = []
    for kh in range(8):
        for base in (6144, 6160, 6176, 6192):
            cols += [base + 2 * kh, base + 2 * kh + 1]
    shared["gdn_wg"] = np.ascontiguousarray(w_in[:, cols])
    shared["gdn_conv"] = g("gdn_conv_w")[0].reshape(5, 32, 128)
    shared["gdn_hc"] = np.ascontiguousarray(np.concatenate([g("gdn_a_log_fwd")[0], g("gdn_a_log_bwd")[0],
                                                            g("gdn_dt_bias_fwd")[0], g("gdn_dt_bias_bwd")[0]]).reshape(1, 64))
    shared["gdn_norm"] = g("gdn_norm")[0].reshape(1, 128)
    shared["gdn_w_out"] = g("gdn_w_out")[0]

def make_in_maps(inputs):
    g = lambda k: np.ascontiguousarray(np.asarray(inputs[k], dtype=np.float32))
    shared = {
        "c_ctx": g("c_ctx").reshape(8, 128),
        "ada_w": g("ada_w"), "ada_b": g("ada_b").reshape(DEPTH, 48, 128),
        "ln1_g": g("ln1_g").reshape(DEPTH, 8, 128), "ln1_b": g("ln1_b").reshape(DEPTH, 8, 128),
        "ln2_g": g("ln2_g").reshape(DEPTH, 8, 128), "ln2_b": g("ln2_b").reshape(DEPTH, 8, 128),
        "mlp_w1": g("mlp_w1"), "mlp_w2": g("mlp_w2"),
        "mla_w_in": g("mla_w_in")[0], "mla_q_norm": g("mla_q_norm")[0].reshape(6, 128),
        "mla_kv_norm": g("mla_kv_norm")[0].reshape(2, 128),
        "mla_w_kvb": g("mla_w_kvb")[0], "mla_w_out": g("mla_w_out")[0],
    }
    _mla_host(inputs, shared, g)
    _diff_host(inputs, shared, g)
    _gla_host(inputs, shared, g)
    _gdn_host(inputs, shared, g)
    x, c, ctx = g("x"), g("c"), g("ctx")
    maps = []
    for b in range(8):
        m = dict(shared)
        m["x"] = x[b]
        m["ctx"] = ctx[b]
        m["c"] = c[b].reshape(8, 128)
        maps.append(m)
    return maps


def kernel(**inputs):
    nc, kb = build_program()
    maps = make_in_maps(inputs)
    res = run_bass_kernel_spmd(nc, maps, core_ids=list(range(8)))
    return np.stack([np.asarray(r["out"], dtype=np.float32) for r in res.results], axis=0)
```

```python
import math
import numpy as np
import concourse.bass as bass
import concourse.mybir as mybir
from concourse.bass_utils import run_bass_kernel_spmd
from contextlib import ExitStack

F32 = mybir.dt.float32
BF16 = mybir.dt.bfloat16
ALU = mybir.AluOpType
AF = mybir.ActivationFunctionType

DEPTH = 4
D = 1024
TL = 2048
TC = 256
T = TL + TC
ALPHA = (2 * DEPTH) ** 0.25
EPS = 1e-6
EPS_LN = EPS / (ALPHA * ALPHA)
TT = [(0, 512, 0), (512, 512, 0), (1024, 512, 0), (1536, 512, 0), (2048, 256, 1)]


class Res:
    __slots__ = ("name", "w", "r", "dsem", "dcnt")

    def __init__(self, name=""):
        self.name = name
        self.w = None
        self.r = {}
        self.dsem = None
        self.dcnt = 0


class Sched:
    EPOCH = 30000

    def __init__(self, nc, es):
        self.nc = nc
        self.es = es
        self.eng = {"pe": nc.tensor, "dve": nc.vector, "act": nc.scalar,
                    "pool": nc.gpsimd, "sp": nc.sync}
        self.cnt = {e: 0 for e in self.eng}
        self.cursem = {e: None for e in self.eng}
        self.last = {e: None for e in self.eng}
        self.seen = {e: {} for e in self.eng}
        self.nsem = 0
        self.ninst = 0
        self.owners = []
        self.out_events = []

    def newsem(self, name):
        self.nsem += 1
        return self.es.enter_context(self.nc.semaphore(f"{name}_{self.nsem}"))

    def _wait(self, e, ev):
        sem, val, _ = ev
        k = id(sem)
        if self.seen[e].get(k, 0) >= val:
            return
        self.eng[e].wait_ge(sem, val)
        self.seen[e][k] = val

    def _deps(self, e, reads, writes, same_ok):
        for r in reads:
            if r.w is not None and not (same_ok and r.w[2] == e):
                self._wait(e, r.w)
        for w in writes:
            if w.w is not None and not (same_ok and w.w[2] == e):
                self._wait(e, w.w)
            for ev in w.r.values():
                if not (same_ok and ev[2] == e):
                    self._wait(e, ev)

    def op(self, e, fn, reads=(), writes=(), same_ok=False):
        self._deps(e, reads, writes, same_ok)
        ins = fn(self.eng[e])
        if self.cnt[e] % self.EPOCH == 0:
            self.cursem[e] = self.newsem("c" + e)
        self.cnt[e] += 1
        val = (self.cnt[e] - 1) % self.EPOCH + 1
        sem = self.cursem[e]
        ins.then_inc(sem, 1)
        ev = (sem, val, e)
        self.last[e] = ev
        for r in reads:
            r.r[id(sem)] = ev
        for w in writes:
            w.w = ev
            w.r = {}
        self.ninst += 1
        return ev

    def dma(self, q, pairs, owner, reads=(), writes=(), **kw):
        self._deps(q, reads, writes, False)
        if owner.dsem is None:
            owner.dsem = self.newsem("d")
            self.owners.append(owner)
        if owner.dcnt > 0:
            self._wait(q, (owner.dsem, owner.dcnt, "dma"))
        for (o, i) in pairs:
            self.eng[q].dma_start(out=o, in_=i, **kw).then_inc(owner.dsem, 16)
            owner.dcnt += 16
        ev = (owner.dsem, owner.dcnt, "dma")
        for r in reads:
            r.r[id(owner.dsem)] = ev
        for w in writes:
            w.w = ev
            w.r = {}
        self.ninst += len(pairs)
        return ev

    def barrier(self):
        evs = [ev for ev in self.last.values() if ev is not None]
        evs += [(o.dsem, o.dcnt, "dma") for o in self.owners if o.dcnt > 0]
        for e in ("pe", "dve", "act", "pool", "sp"):
            for ev in evs:
                if ev[2] != e:
                    self._wait(e, ev)

    def finish(self):
        for ev in self.out_events:
            self._wait("sp", ev)


class KB:
    def __init__(self, nc, es, depth_run=DEPTH, mixers=True, dbg=False):
        self.nc, self.es = nc, es
        self.S = Sched(nc, es)
        self.depth_run = depth_run
        self.mixers = mixers
        self.dbg = dbg
        self.din = {}
        self._n = 0
        self.skip_ctx = False

    def sb(self, shape, dt, es=None, name=None):
        self._n += 1
        return (es or self.es).enter_context(self.nc.sbuf_tensor(name or f"t{self._n}", list(shape), dt))

    def dram_in(self, name, shape):
        t = self.nc.dram_tensor(name, list(shape), F32, kind="ExternalInput").ap()
        self.din[name] = t
        return t

    def bank(self):
        i = self.bank_i
        self.bank_i = (i + 1) % self.nrr
        return self.banks[i], self.bank_res[i]

    def declare(self):
        di = self.dram_in
        di("x", [TL, D]); di("ctx", [TC, D]); di("c", [8, 128]); di("c_ctx", [8, 128])
        di("ada_w", [DEPTH, D, 6 * D]); di("ada_b", [DEPTH, 48, 128])
        for n in ("ln1_g", "ln1_b", "ln2_g", "ln2_b"):
            di(n, [DEPTH, 8, 128])
        di("mlp_w1", [DEPTH, D, 4 * D]); di("mlp_w2", [DEPTH, 4 * D, D])
        di("mla_w_in", [D, 1056]); di("mla_q_norm", [6, 128]); di("mla_kv_norm", [2, 128])
        di("mla_w_kvb", [256, 2048]); di("mla_w_out", [D, D])
        di("mla_wqx", [768, 2048]); di("mla_wkr", [D, 256])
        di("mla_qtab", [128, T]); di("mla_kta", [128, T]); di("mla_ktb", [128, T])
        di("diff_wx", [D, 16 * 384]); di("diff_wv", [D, D]); di("diff_w_out", [D, D])
        di("diff_qtab", [128, T]); di("diff_kta", [128, T]); di("diff_ktb", [128, T])
        di("diff_lam", [64, 4]); di("diff_subln", [1, 128])
        di("gla_w_in", [D, 3104]); di("gla_gw", [2, 16, 512]); di("gla_gb", [2, 4, 128])
        di("gla_norm", [2, 128]); di("gla_w_out", [D, D])
        di("gdn_w_in", [D, 6208]); di("gdn_wg", [D, 64]); di("gdn_conv", [5, 32, 128])
        di("gdn_hc", [1, 64]); di("gdn_norm", [1, 128]); di("gdn_w_out", [2 * D, D])
        self.out = self.nc.dram_tensor("out", [TL, D], F32, kind="ExternalOutput").ap()
        if self.dbg:
            self.out_c = self.nc.dram_tensor("out_c", [TC, D], F32, kind="ExternalOutput").ap()

        nc = self.nc
        self.banks = [self.es.enter_context(nc.psum_tensor(f"bank{i}", [128, 512], F32)) for i in range(8)]
        self.bank_res = [Res(f"bank{i}") for i in range(8)]
        self.bank_i = 0
        self.nrr = 6
        self.hT = self.sb([128, 8, T], F32, name="hT")
        self.r_h = [[Res(f"h{t}_{k}") for k in range(8)] for t in range(len(TT))]
        self.uT = self.sb([128, 8, T], BF16, name="uT")
        self.r_u = [Res(f"u{t}") for t in range(len(TT))]
        self.ident = self.sb([128, 128], F32, name="ident"); self.r_ident = Res("ident")
        self.identb = self.sb([128, 128], BF16, name="identb")
        self.ones = self.sb([128, 128], F32, name="ones")
        self.identr = self.sb([128, 128], F32, name="identr")
        self.onesr = self.sb([128, 128], F32, name="onesr")
        self.sT = self.sb([128, 8, 2], F32, name="sT"); self.r_sT = Res("sT")
        self.sTb = self.sb([128, 8, 2], BF16, name="sTb")
        self.mod = [self.sb([128, 48, 2], F32, name=f"mod{i}") for i in range(DEPTH)]
        self.r_mod = [Res(f"mod{i}") for i in range(DEPTH)]
        self.lnp = self.sb([128, 4, DEPTH, 8], F32, name="lnp"); self.r_lnp = Res("lnp")
        self.adab = self.sb([128, DEPTH, 48], F32, name="adab"); self.r_adab = Res("adab")
        self.vst = self.sb([64, 128], F32, name="vst"); self.r_vst = Res("vst")
        self.NW = 3
        self.wslot = [self.sb([128, 4096], BF16, name=f"wslot{i}") for i in range(self.NW)]
        self.r_wslot = [Res(f"wslot{i}") for i in range(self.NW)]
        self.w_i = 0

    def wnext(self):
        i = self.w_i
        self.w_i = (i + 1) % self.NW
        return self.wslot[i], self.r_wslot[i]

    def load_cols(self, src, n, dst, r_dst):
        S = self.S
        S.dma("sp", [(self.vst[0:n, :], src)], self.r_vst, writes=[self.r_vst])
        bk, rb = self.bank()
        S.op("pe", lambda e: e.transpose(bk[:, 0:n], self.vst[0:n, :], self.ident[0:n, 0:n]),
             reads=[self.r_vst, self.r_ident], writes=[rb], same_ok=True)
        S.op("dve", lambda e: e.tensor_copy(dst, bk[:, 0:n]), reads=[rb], writes=[r_dst])

    def setup(self):
        S, nc = self.S, self.nc
        S.op("pool", lambda e: e.memset(self.ident[:], 0.0), writes=[self.r_ident])
        S.op("pool", lambda e: e.affine_select(out=self.ident[:], in_=self.ident[:], compare_op=ALU.not_equal,
                                              fill=1.0, base=0, pattern=[[-1, 128]], channel_multiplier=1),
             reads=[self.r_ident], writes=[self.r_ident])
        S.op("dve", lambda e: e.tensor_copy(self.identb[:], self.ident[:]), reads=[self.r_ident], writes=[self.r_ident])
        S.op("dve", lambda e: e.memset(self.ones[:], 1.0), writes=[self.r_ident])
        S.op("dve", lambda e: e.tensor_copy(self.identr[:].bitcast(mybir.dt.float32r), self.ident[:]), reads=[self.r_ident], writes=[self.r_ident])
        S.op("dve", lambda e: e.tensor_copy(self.onesr[:].bitcast(mybir.dt.float32r), self.ones[:]), reads=[self.r_ident], writes=[self.r_ident])
        for k, n in enumerate(("ln1_g", "ln1_b", "ln2_g", "ln2_b")):
            for i in range(DEPTH):
                self.load_cols(self.din[n][i], 8, self.lnp[:, k, i, :], self.r_lnp)
        for i in range(DEPTH):
            self.load_cols(self.din["ada_b"][i], 48, self.adab[:, i, :], self.r_adab)
        self.load_cols(self.din["c"], 8, self.sT[:, :, 0], self.r_sT)
        self.load_cols(self.din["c_ctx"], 8, self.sT[:, :, 1], self.r_sT)
        S.op("act", lambda e: e.activation(out=self.sT[:], in_=self.sT[:], func=AF.Silu), reads=[self.r_sT], writes=[self.r_sT])
        S.op("dve", lambda e: e.tensor_copy(self.sTb[:], self.sT[:]), reads=[self.r_sT], writes=[self.r_sT])
        with ExitStack() as ph:
            xs = [self.sb([128, D], F32, es=ph) for _ in range(2)]
            r_xs = [Res("xs0"), Res("xs1")]
            mstg = [self.sb([128, 8, 256], BF16, es=ph) for _ in range(2)]
            mg = self.mods_gen(0, mstg, [Res("ms0"), Res("ms1")])
            for t in range(18):
                if mg is not None:
                    try:
                        next(mg)
                    except StopIteration:
                        mg = None
                src = self.din["x"][t * 128:(t + 1) * 128, :] if t < 16 else self.din["ctx"][(t - 16) * 128:(t - 15) * 128, :]
                st, rs = xs[t % 2], r_xs[t % 2]
                S.dma("sp", [(st[:], src)], rs, writes=[rs])
                ti = min(t // 4, 4)
                for g in range(2):
                    bk, rb = self.bank()
                    for j in range(4):
                        S.op("pe", lambda e: e.transpose(bk[:, j * 128:(j + 1) * 128], st[:, (g * 4 + j) * 128:(g * 4 + j + 1) * 128], self.ident[:]),
                             reads=[rs, self.r_ident], writes=[rb], same_ok=True)
                    dst = self.hT[:, g * 4:(g + 1) * 4, t * 128:(t + 1) * 128]
                    srcp = bk[:, 0:512].rearrange("p (j n) -> p j n", j=4)
                    if g == 0:
                        S.op("dve", lambda e: e.tensor_copy(dst, srcp), reads=[rb], writes=self.r_h[ti][g * 4:(g + 1) * 4])
                    else:
                        S.op("act", lambda e: e.copy(dst, srcp), reads=[rb], writes=self.r_h[ti][g * 4:(g + 1) * 4])
            if mg is not None:
                for _ in mg:
                    pass
            S.barrier()

    def mods_gen(self, i, stg, r_stg):
        S = self.S
        aw = self.din["ada_w"][i]
        bk, rb = self.banks[7], self.bank_res[7]
        for blk in range(24):
            st, rs = stg[blk % 2], r_stg[blk % 2]
            S.dma("pool", [(st[:], aw[:, blk * 256:(blk + 1) * 256].rearrange("(k p) n -> p k n", p=128))], rs, writes=[rs])
            for cc in range(2):
                c = blk * 2 + cc
                for kc in range(8):
                    S.op("pe", lambda e: e.matmul(bk[:, c * 2:(c + 1) * 2], lhsT=st[:, kc, cc * 128:(cc + 1) * 128], rhs=self.sTb[:, kc, :],
                                                  start=(kc == 0), stop=(kc == 7)),
                         reads=[rs, self.r_sT], writes=[rb], same_ok=True)
            yield
        m, rm = self.mod[i], self.r_mod[i]
        pv = bk[:, 0:96].rearrange("p (c l) -> p c l", l=2)
        for l in range(2):
            S.op("dve", lambda e: e.tensor_tensor(out=m[:, :, l], in0=pv[:, :, l], in1=self.adab[:, i, :], op=ALU.add),
                 reads=[rb, self.r_adab], writes=[rm])
        for c0 in (8, 32):
            S.op("dve", lambda e: e.tensor_scalar(out=m[:, c0:c0 + 8, :], in0=m[:, c0:c0 + 8, :], scalar1=1.0, scalar2=None, op0=ALU.add),
                 reads=[rm], writes=[rm])
        for c0 in (16, 40):
            S.op("dve", lambda e: e.tensor_scalar(out=m[:, c0:c0 + 8, :], in0=m[:, c0:c0 + 8, :], scalar1=1.0 / ALPHA, scalar2=None, op0=ALU.mult),
                 reads=[rm], writes=[rm])

    def mods(self, i):
        with ExitStack() as ph:
            stg = [self.sb([128, 8, 256], BF16, es=ph) for _ in range(2)]
            r_stg = [Res("as0"), Res("as1")]
            for _ in self.mods_gen(i, stg, r_stg):
                pass
            self.S.barrier()

    def mcol(self, i, which, kc, lc):
        return self.mod[i][:, which * 8 + kc, lc:lc + 1]

    def modulate_all(self, i, sub):
        S = self.S
        for ti, (t0, n, lc) in enumerate(TT):
            for kc in range(8):
                S.op("dve", lambda e: e.tensor_scalar(out=self.uT[:, kc, t0:t0 + n], in0=self.hT[:, kc, t0:t0 + n],
                                                      scalar1=self.mcol(i, 3 * sub + 1, kc, lc), scalar2=self.mcol(i, 3 * sub, kc, lc),
                                                      op0=ALU.mult, op1=ALU.add),
                     reads=[self.r_h[ti][kc], self.r_mod[i]], writes=[self.r_u[ti]])

    def layer_norm(self, i, sub, nxt):
        S = self.S
        R32 = mybir.dt.float32r
        with ExitStack() as ph:
            sq = [self.sb([128, 512], F32, es=ph) for _ in range(3)]; r_sq = [Res() for _ in range(3)]
            mean = [self.sb([128, 512], F32, es=ph) for _ in range(2)]; r_mean = [Res(), Res()]
            msq = [self.sb([128, 512], F32, es=ph) for _ in range(2)]; r_msq = [Res(), Res()]
            rstd = [self.sb([128, 512], F32, es=ph) for _ in range(2)]; r_rstd = [Res(), Res()]
            tmp = [self.sb([128, 512], F32, es=ph) for _ in range(3)]; r_tmp = [Res() for _ in range(3)]
            epsc = self.sb([128, 1], F32, es=ph); r_eps = Res()
            S.op("pool", lambda e: e.memset(epsc[:], EPS_LN), writes=[r_eps])
            gsc = self.sb([128, 8, 2], F32, es=ph); bsc = self.sb([128, 8, 2], F32, es=ph); r_gb = Res("gsc")
            if nxt is not None:
                ni, nsub = nxt
                msc = self.mod[ni][:, (3 * nsub + 1) * 8:(3 * nsub + 2) * 8, :]
                msh = self.mod[ni][:, (3 * nsub) * 8:(3 * nsub + 1) * 8, :]
                for l_ in range(2):
                    S.op("dve", lambda e: e.tensor_tensor(out=gsc[:, :, l_], in0=self.lnp[:, 2 * sub, i, :], in1=msc[:, :, l_], op=ALU.mult),
                         reads=[self.r_lnp, self.r_mod[ni]], writes=[r_gb])
                    S.op("dve", lambda e: e.tensor_tensor(out=bsc[:, :, l_], in0=self.lnp[:, 2 * sub + 1, i, :], in1=msc[:, :, l_], op=ALU.mult),
                         reads=[self.r_lnp, self.r_mod[ni]], writes=[r_gb])
                    S.op("dve", lambda e: e.tensor_tensor(out=bsc[:, :, l_], in0=bsc[:, :, l_], in1=msh[:, :, l_], op=ALU.add),
                         reads=[r_gb, self.r_mod[ni]], writes=[r_gb])
            banks = {}
            cnt = [0, 0]

            def stats(ti):
                t0, n, lc = TT[ti]
                rh = self.r_h[ti]
                b1, rb1 = self.bank()
                b2, rb2 = self.bank()
                banks[ti] = (b1, rb1, b2, rb2)
                for kc in range(8):
                    z = self.hT[:, kc, t0:t0 + n]
                    s_, rs_ = sq[cnt[0] % 3], r_sq[cnt[0] % 3]
                    cnt[0] += 1
                    S.op("act", lambda e: e.activation(out=s_[:, 0:n].bitcast(R32), in_=z, func=AF.Square), reads=[rh[kc]], writes=[rs_])
                    S.op("pe", lambda e: e.matmul(b1[:, 0:n], lhsT=self.ones[:], rhs=z, start=(kc == 0), stop=(kc == 7)),
                         reads=[rh[kc], self.r_ident], writes=[rb1], same_ok=True)
                    S.op("pe", lambda e: e.matmul(b2[:, 0:n], lhsT=self.onesr[:].bitcast(R32), rhs=s_[:, 0:n].bitcast(R32), start=(kc == 0), stop=(kc == 7)),
                         reads=[rs_, self.r_ident], writes=[rb2], same_ok=True)

            def finish_stats(ti):
                t0, n, lc = TT[ti]
                b1, rb1, b2, rb2 = banks[ti]
                mn, rmn = mean[ti % 2], r_mean[ti % 2]
                ms, rms = msq[ti % 2], r_msq[ti % 2]
                rs, rrs = rstd[ti % 2], r_rstd[ti % 2]
                S.op("act", lambda e: e.activation(out=mn[:, 0:n], in_=b1[:, 0:n], func=AF.Copy, scale=1.0 / D), reads=[rb1], writes=[rmn])
                S.op("dve", lambda e: e.tensor_tensor(out=ms[:, 0:n], in0=mn[:, 0:n], in1=mn[:, 0:n], op=ALU.mult), reads=[rmn], writes=[rms])
                S.op("dve", lambda e: e.scalar_tensor_tensor(out=rs[:, 0:n], in0=b2[:, 0:n], scalar=1.0 / D, in1=ms[:, 0:n],
                                                             op0=ALU.mult, op1=ALU.subtract), reads=[rb2, rms], writes=[rrs])
                S.op("act", lambda e: e.activation(out=rs[:, 0:n], in_=rs[:, 0:n], func=AF.Sqrt, bias=epsc[:, 0:1], scale=1.0),
                     reads=[rrs, r_eps], writes=[rrs])
                S.op("dve", lambda e: e.reciprocal(out=rs[:, 0:n], in_=rs[:, 0:n]), reads=[rrs], writes=[rrs])

            def normalize(ti):
                t0, n, lc = TT[ti]
                rh = self.r_h[ti]
                mn, rmn = mean[ti % 2], r_mean[ti % 2]
                rs, rrs = rstd[ti % 2], r_rstd[ti % 2]
                for kc in range(8):
                    z = self.hT[:, kc, t0:t0 + n]
                    tp, rtp = tmp[cnt[1] % 3], r_tmp[cnt[1] % 3]
                    cnt[1] += 1
                    S.op("pool", lambda e: e.tensor_tensor(out=tp[:, 0:n], in0=z, in1=mn[:, 0:n], op=ALU.subtract), reads=[rh[kc], rmn], writes=[rtp])
                    S.op("dve", lambda e: e.tensor_tensor(out=tp[:, 0:n], in0=tp[:, 0:n], in1=rs[:, 0:n], op=ALU.mult), reads=[rtp, rrs], writes=[rtp])
                    S.op("act", lambda e: e.activation(out=z, in_=tp[:, 0:n], func=AF.Identity,
                                                       bias=self.lnp[:, 2 * sub + 1, i, kc:kc + 1], scale=self.lnp[:, 2 * sub, i, kc:kc + 1]),
                         reads=[rtp, self.r_lnp], writes=[rh[kc]])
                    if nxt is not None:
                        if kc % 2 == 0:
                            S.op("dve", lambda e: e.tensor_scalar(out=self.uT[:, kc, t0:t0 + n], in0=tp[:, 0:n],
                                                                  scalar1=gsc[:, kc, lc:lc + 1], scalar2=bsc[:, kc, lc:lc + 1],
                                                                  op0=ALU.mult, op1=ALU.add),
                                 reads=[rtp, r_gb], writes=[self.r_u[ti]])
                        else:
                            S.op("act", lambda e: e.activation(out=self.uT[:, kc, t0:t0 + n], in_=tp[:, 0:n], func=AF.Identity,
                                                               bias=bsc[:, kc, lc:lc + 1], scale=gsc[:, kc, lc:lc + 1]),
                                 reads=[rtp, r_gb], writes=[self.r_u[ti]])

            nt = len(TT) - 1 if self.skip_ctx else len(TT)
            stats(0)
            finish_stats(0)
            for ti in range(nt):
                if ti + 1 < nt:
                    stats(ti + 1)
                normalize(ti)
                if ti + 1 < nt:
                    finish_stats(ti + 1)
            S.barrier()

    def mlp(self, i):
        S = self.S
        w1 = self.din["mlp_w1"][i]
        w2 = self.din["mlp_w2"][i]
        with ExitStack() as ph:
            mg = None
            if i + 1 < DEPTH:
                mstg = [self.sb([128, 8, 256], BF16, es=ph) for _ in range(2)]
                mg = self.mods_gen(i + 1, mstg, [Res("ms0"), Res("ms1")])
            ab = [self.sb([128, 4, 512], BF16, es=ph) for _ in range(2)]; r_ab = [Res(), Res()]
            rl = [self.sb([128, 512], BF16, es=ph) for _ in range(3)]; r_rl = [Res(), Res(), Res()]
            rli = 0
            step = 0
            for j in range(8):
                wa, r_wa = self.wnext()
                wb, r_wb = self.wnext()
                S.dma("pool", [(wa[:].rearrange("p (k n) -> p k n", k=8), w1[:, j * 512:(j + 1) * 512].rearrange("(k p) n -> p k n", p=128))],
                      r_wa, writes=[r_wa])
                S.dma("pool", [(wb[:].rearrange("p (k n) -> p k n", k=4), w2[j * 512:(j + 1) * 512, :].rearrange("(k p) n -> p k n", p=128))],
                      r_wb, writes=[r_wb])
                wav = wa[:].rearrange("p (k n) -> p k n", k=8)
                wbv = wb[:].rearrange("p (k n) -> p k n", k=4)
                for ti, (t0, n, lc) in enumerate(TT):
                    if self.skip_ctx and lc == 1:
                        continue
                    a_, r_a = ab[step % 2], r_ab[step % 2]
                    step += 1
                    if mg is not None:
                        try:
                            next(mg)
                        except StopIteration:
                            mg = None
                    for hc in range(4):
                        bk, rb = self.bank()
                        for kc in range(8):
                            S.op("pe", lambda e: e.matmul(bk[:, 0:n], lhsT=wav[:, kc, hc * 128:(hc + 1) * 128], rhs=self.uT[:, kc, t0:t0 + n],
                                                          start=(kc == 0), stop=(kc == 7)),
                                 reads=[r_wa, self.r_u[ti]], writes=[rb], same_ok=True)
                        r_, rr_ = rl[rli % 3], r_rl[rli % 3]
                        rli += 1
                        S.op("act", lambda e: e.activation(out=r_[:, 0:n], in_=bk[:, 0:n], func=AF.Relu), reads=[rb], writes=[rr_])
                        S.op("act", lambda e: e.activation(out=a_[:, hc, 0:n], in_=r_[:, 0:n], func=AF.Square),
                             reads=[rr_], writes=[r_a])
                    for oc in range(8):
                        bk, rb = self.bank()
                        for kc in range(4):
                            S.op("pe", lambda e: e.matmul(bk[:, 0:n], lhsT=wbv[:, kc, oc * 128:(oc + 1) * 128], rhs=a_[:, kc, 0:n],
                                                          start=(kc == 0), stop=(kc == 3)),
                                 reads=[r_wb, r_a], writes=[rb], same_ok=True)
                        hz = self.hT[:, oc, t0:t0 + n]
                        S.op("dve", lambda e: e.scalar_tensor_tensor(out=hz, in0=bk[:, 0:n], scalar=self.mcol(i, 5, oc, lc), in1=hz,
                                                                     op0=ALU.mult, op1=ALU.add),
                             reads=[rb, self.r_h[ti][oc], self.r_mod[i]], writes=[self.r_h[ti][oc]])
            if mg is not None:
                for _ in mg:
                    pass
            S.barrier()

    def store_out(self):
        S = self.S
        with ExitStack() as ph:
            os_ = [self.sb([128, D], F32, es=ph) for _ in range(2)]
            r_os = [Res("os0"), Res("os1")]
            nt = 18 if self.dbg else 16
            for t in range(nt):
                st, rs = os_[t % 2], r_os[t % 2]
                ti = min(t // 4, 4)
                for g in range(2):
                    bk, rb = self.bank()
                    for j in range(4):
                        S.op("pe", lambda e: e.transpose(bk[:, j * 128:(j + 1) * 128], self.hT[:, g * 4 + j, t * 128:(t + 1) * 128], self.ident[:]),
                             reads=[self.r_h[ti][g * 4 + j], self.r_ident], writes=[rb], same_ok=True)
                    if g == 0:
                        S.op("dve", lambda e: e.tensor_copy(st[:, 0:512], bk[:, 0:512]), reads=[rb], writes=[rs])
                    else:
                        S.op("act", lambda e: e.copy(st[:, 512:1024], bk[:, 0:512]), reads=[rb], writes=[rs])
                dst = self.out[t * 128:(t + 1) * 128, :] if t < 16 else self.out_c[(t - 16) * 128:(t - 15) * 128, :]
                ev = S.dma("sp", [(dst, st[:])], rs, reads=[rs])
                S.out_events.append(ev)
            S.finish()

    def build(self):
        self.declare()
        self.setup()
        self.modulate_all(0, 0)
        for i in range(self.depth_run):
            if i == DEPTH - 1 and not self.dbg:
                self.skip_ctx = True
            if self.mixers:
                self.mixer(i)
            self.layer_norm(i, 0, (i, 1))
            self.mlp(i)
            self.layer_norm(i, 1, (i + 1, 0) if i + 1 < DEPTH else None)
        self.store_out()


    def mixer(self, i):
        if i == 0:
            self.mla(i)
        elif i == 1:
            self.diff(i)
        elif i == 2:
            self.gla(i)
        elif i == 3:
            self.gdn(i)

    def mla(self, i):
        S = self.S
        SCALE = 96.0 ** -0.5
        with ExitStack() as ph:
            kp = [self.sb([128, T], BF16, es=ph) for _ in range(2)]; r_kp = [Res("kp0"), Res("kp1")]
            qtab = self.sb([128, T], F32, es=ph); r_qtab = Res("qtab")
            opad = self.sb([128, 2, 128], BF16, es=ph); r_opad = Res("opad")
            nrm = self.sb([128, 8], F32, es=ph); r_nrm = Res("nrm")
            epsc = self.sb([128, 1], F32, es=ph); r_eps = Res("eps")
            S.dma("sp", [(qtab[:], self.din["mla_qtab"])], r_qtab, writes=[r_qtab])
            S.op("pool", lambda e: e.memset(opad[:], 0.0), writes=[r_opad])
            S.op("pool", lambda e: e.memset(opad[:, 0, 0:64], 1.0), reads=[r_opad], writes=[r_opad])
            S.op("pool", lambda e: e.memset(opad[:, 1, 64:128], 1.0), reads=[r_opad], writes=[r_opad])
            S.op("pool", lambda e: e.memset(epsc[:], EPS), writes=[r_eps])
            self.load_cols(self.din["mla_q_norm"], 6, nrm[:, 0:6], r_nrm)
            self.load_cols(self.din["mla_kv_norm"], 2, nrm[:, 6:8], r_nrm)
            with ExitStack() as p1:
                raw = self.sb([128, 8, 512], F32, es=p1); r_raw = Res("raw")
                sq = [self.sb([128, 512], F32, es=p1) for _ in range(2)]; r_sq = [Res(), Res()]
                rs = self.sb([128, 2, 512], F32, es=p1); r_rs = Res("rs")
                kta = self.sb([128, 512], F32, es=p1); r_kta = Res("kta")
                ktb = self.sb([128, 512], F32, es=p1); r_ktb = Res("ktb")
                t1 = self.sb([128, 512], F32, es=p1); r_t1 = Res("t1")
                t2 = self.sb([128, 512], F32, es=p1); r_t2 = Res("t2")
                wi = []
                for blk in range(2):
                    w_, r_w = self.wnext()
                    S.dma("pool", [(w_[:].rearrange("p (k n) -> p k n", k=8),
                                    self.din["mla_w_in"][:, blk * 512:(blk + 1) * 512].rearrange("(k p) n -> p k n", p=128))], r_w, writes=[r_w])
                    wi.append((w_[:].rearrange("p (k n) -> p k n", k=8), r_w))
                w_, r_wk = self.wnext()
                wkr = w_[:, 0:2048].rearrange("p (k n) -> p k n", k=8)
                S.dma("pool", [(wkr, self.din["mla_wkr"].rearrange("(k p) n -> p k n", p=128))], r_wk, writes=[r_wk])
                for ti, (t0, n, lc) in enumerate(TT):
                    ru = self.r_u[ti]
                    S.dma("sp", [(kta[:, 0:n], self.din["mla_kta"][:, t0:t0 + n])], r_kta, writes=[r_kta])
                    S.dma("sp", [(ktb[:, 0:n], self.din["mla_ktb"][:, t0:t0 + n])], r_ktb, writes=[r_ktb])
                    bA, rbA = self.bank()
                    bB, rbB = self.bank()
                    for kc in range(8):
                        S.op("pe", lambda e: e.matmul(bA[:, 0:n], lhsT=wkr[:, kc, 0:128], rhs=self.uT[:, kc, t0:t0 + n], start=(kc == 0), stop=(kc == 7)),
                             reads=[r_wk, ru], writes=[rbA], same_ok=True)
                    for kc in range(8):
                        S.op("pe", lambda e: e.matmul(bB[:, 0:n], lhsT=wkr[:, kc, 128:256], rhs=self.uT[:, kc, t0:t0 + n], start=(kc == 0), stop=(kc == 7)),
                             reads=[r_wk, ru], writes=[rbB], same_ok=True)
                    S.op("dve", lambda e: e.tensor_tensor(out=t1[64:128, 0:n], in0=bA[64:128, 0:n], in1=kta[64:128, 0:n], op=ALU.mult),
                         reads=[rbA, r_kta], writes=[r_t1])
                    S.op("dve", lambda e: e.tensor_tensor(out=t2[64:128, 0:n], in0=bB[64:128, 0:n], in1=ktb[64:128, 0:n], op=ALU.mult),
                         reads=[rbB, r_ktb], writes=[r_t2])
                    S.op("pool", lambda e: e.tensor_tensor(out=kp[0][64:128, t0:t0 + n], in0=t1[64:128, 0:n], in1=t2[64:128, 0:n], op=ALU.add),
                         reads=[r_t1, r_t2], writes=[r_kp[0]])
                    S.op("pool", lambda e: e.tensor_copy(kp[1][64:128, t0:t0 + n], kp[0][64:128, t0:t0 + n]), reads=[r_kp[0]], writes=[r_kp[1]])
                    for oc in range(8):
                        wv, r_w = wi[oc // 4]
                        bk, rb = self.bank()
                        for kc in range(8):
                            S.op("pe", lambda e: e.matmul(bk[:, 0:n], lhsT=wv[:, kc, (oc % 4) * 128:(oc % 4 + 1) * 128], rhs=self.uT[:, kc, t0:t0 + n],
                                                          start=(kc == 0), stop=(kc == 7)),
                                 reads=[r_w, ru], writes=[rb], same_ok=True)
                        if oc % 2 == 0:
                            S.op("dve", lambda e: e.tensor_copy(raw[:, oc, 0:n], bk[:, 0:n]), reads=[rb], writes=[r_raw])
                        else:
                            S.op("act", lambda e: e.copy(raw[:, oc, 0:n], bk[:, 0:n]), reads=[rb], writes=[r_raw])
                    bq, rbq = self.banks[6], self.bank_res[6]
                    bkv, rbkv = self.banks[7], self.bank_res[7]
                    for oc in range(8):
                        s_, rs_ = sq[oc % 2], r_sq[oc % 2]
                        S.op("act", lambda e: e.activation(out=s_[:, 0:n], in_=raw[:, oc, 0:n], func=AF.Square), reads=[r_raw], writes=[rs_])
                        if oc < 6:
                            S.op("pe", lambda e: e.matmul(bq[:, 0:n], lhsT=self.ones[:], rhs=s_[:, 0:n], start=(oc == 0), stop=(oc == 5)),
                                 reads=[rs_, self.r_ident], writes=[rbq], same_ok=True)
                        else:
                            S.op("pe", lambda e: e.matmul(bkv[:, 0:n], lhsT=self.ones[:], rhs=s_[:, 0:n], start=(oc == 6), stop=(oc == 7)),
                                 reads=[rs_, self.r_ident], writes=[rbkv], same_ok=True)
                    for g, (bb, rbb, dim) in enumerate(((bq, rbq, 768.0), (bkv, rbkv, 256.0))):
                        S.op("act", lambda e: e.activation(out=rs[:, g, 0:n], in_=bb[:, 0:n], func=AF.Sqrt, bias=epsc[:, 0:1], scale=1.0 / dim),
                             reads=[rbb, r_eps], writes=[r_rs])
                        S.op("dve", lambda e: e.reciprocal(out=rs[:, g, 0:n], in_=rs[:, g, 0:n]), reads=[r_rs], writes=[r_rs])
                    for oc in range(8):
                        g = 0 if oc < 6 else 1
                        S.op("dve", lambda e: e.scalar_tensor_tensor(out=self.uT[:, oc, t0:t0 + n], in0=raw[:, oc, 0:n], scalar=nrm[:, oc:oc + 1],
                                                                     in1=rs[:, g, 0:n], op0=ALU.mult, op1=ALU.mult),
                             reads=[r_raw, r_nrm, r_rs], writes=[ru])
                S.barrier()
            with ExitStack() as p2:
                qp = [self.sb([128, T], BF16, es=p2) for _ in range(2)]; r_qp = [Res("qp0"), Res("qp1")]
                vp = [self.sb([128, 18, 128], BF16, es=p2) for _ in range(2)]; r_vp = [Res("vp0"), Res("vp1")]
                pt = [self.sb([128, 512], BF16, es=p2) for _ in range(4)]; r_pt = [Res() for _ in range(4)]
                rden = [self.sb([128, 512], F32, es=p2) for _ in range(2)]; r_rden = [Res("rden0"), Res("rden1")]
                attn_ctr = [0]
                pending = [None]
                self.nrr = 4
                self.bank_i = 0
                opr = [self.sb([128, 512], BF16, es=p2) for _ in range(2)]; r_opr = [Res(), Res()]
                wo = [self.sb([128, D], BF16, es=p2) for _ in range(2)]; r_wo = [Res("wo0"), Res("wo1")]
                for par in range(2):
                    S.op("pool", lambda e: e.memset(vp[par][:], 0.0), writes=[r_vp[par]])
                pti = 0
                wq = wkv = None
                for pair in range(8):
                    S.dma("pool", [(wo[pair % 2][:], self.din["mla_w_out"][pair * 128:(pair + 1) * 128, :])], r_wo[pair % 2], writes=[r_wo[pair % 2]])
                    for par in range(2):
                        h = pair * 2 + par
                        hl = h % 4
                        if hl == 0:
                            w_, r_wq = self.wnext()
                            wq = w_[:, 0:3072].rearrange("p (k n) -> p k n", k=6)
                            wkv = w_[:, 3072:4096].rearrange("p (k n) -> p k n", k=2)
                            S.dma("pool", [(wq, self.din["mla_wqx"][:, h * 128:(h + 4) * 128].rearrange("(k p) n -> p k n", p=128)),
                                           (wkv, self.din["mla_w_kvb"][:, h * 128:(h + 4) * 128].rearrange("(k p) n -> p k n", p=128))],
                                  r_wq, writes=[r_wq])
                        for ti, (t0, n, lc) in enumerate(TT):
                            ru = self.r_u[ti]
                            bk, rb = self.bank()
                            for kc in range(6):
                                S.op("pe", lambda e: e.matmul(bk[:, 0:n], lhsT=wq[:, kc, hl * 128:(hl + 1) * 128], rhs=self.uT[:, kc, t0:t0 + n],
                                                              start=(kc == 0), stop=(kc == 5)),
                                     reads=[r_wq, ru], writes=[rb], same_ok=True)
                            S.op("dve", lambda e: e.tensor_tensor(out=qp[par][:, t0:t0 + n], in0=bk[:, 0:n], in1=qtab[:, t0:t0 + n], op=ALU.mult),
                                 reads=[rb, r_qtab], writes=[r_qp[par]])
                            bk, rb = self.bank()
                            for kc in range(2):
                                S.op("pe", lambda e: e.matmul(bk[0:64, 0:n], lhsT=wkv[:, kc, hl * 128:hl * 128 + 64], rhs=self.uT[:, 6 + kc, t0:t0 + n],
                                                              start=(kc == 0), stop=(kc == 1)),
                                     reads=[r_wq, ru], writes=[rb], same_ok=True)
                            S.op("dve", lambda e: e.tensor_copy(kp[par][0:64, t0:t0 + n], bk[0:64, 0:n]), reads=[rb], writes=[r_kp[par]])
                        for g0 in range(0, 18, 8):
                            ng = min(8, 18 - g0)
                            bk, rb = self.bank()
                            for jt in range(ng):
                                kt = g0 + jt
                                for kc in range(2):
                                    S.op("pe", lambda e: e.matmul(bk[:, jt * 64:(jt + 1) * 64], lhsT=self.uT[:, 6 + kc, kt * 128:(kt + 1) * 128],
                                                                  rhs=wkv[:, kc, hl * 128 + 64:hl * 128 + 128], start=(kc == 0), stop=(kc == 1)),
                                         reads=[r_wq] + self.r_u, writes=[rb], same_ok=True)
                            S.op("dve", lambda e: e.tensor_copy(vp[par][:, g0:g0 + ng, par * 64:par * 64 + 64],
                                                                bk[:, 0:ng * 64].rearrange("p (j d) -> p j d", d=64)), reads=[rb], writes=[r_vp[par]])
                    for ti, (t0, n, lc) in enumerate(TT):
                        kts = list(range(18)) if lc == 0 else [16, 17]
                        items = [(par, kt) for par in range(2) for kt in kts]
                        nb_ = attn_ctr[0] % 2
                        attn_ctr[0] += 1
                        num, r_num = self.banks[4 + 2 * nb_], self.bank_res[4 + 2 * nb_]
                        den, r_den = self.banks[5 + 2 * nb_], self.bank_res[5 + 2 * nb_]
                        sbanks = {}

                        def issue_score(ix):
                            par, kt = items[ix]
                            bk, rb = self.bank()
                            S.op("pe", lambda e: e.matmul(bk[:, 0:n], lhsT=kp[par][:, kt * 128:(kt + 1) * 128], rhs=qp[par][:, t0:t0 + n], start=True, stop=True),
                                 reads=[r_kp[par], r_qp[par]], writes=[rb], same_ok=True)
                            sbanks[ix] = (bk, rb)
                        for ix in range(min(2, len(items))):
                            issue_score(ix)
                        for ix, (par, kt) in enumerate(items):
                            bk, rb = sbanks.pop(ix)
                            p_, rp_ = pt[pti % 4], r_pt[pti % 4]
                            pti += 1
                            S.op("act", lambda e: e.activation(out=p_[:, 0:n], in_=bk[:, 0:n], func=AF.Exp, scale=SCALE), reads=[rb], writes=[rp_])
                            if ix + 2 < len(items):
                                issue_score(ix + 2)
                            first = (ix == 0)
                            last = (ix == len(items) - 1)
                            S.op("pe", lambda e: e.matmul(num[:, 0:n], lhsT=vp[par][:, kt, :], rhs=p_[:, 0:n], start=first, stop=last),
                                 reads=[r_vp[par], rp_], writes=[r_num], same_ok=True)
                            S.op("pe", lambda e: e.matmul(den[:, 0:n], lhsT=opad[:, par, :], rhs=p_[:, 0:n], start=first, stop=last),
                                 reads=[r_opad, rp_], writes=[r_den], same_ok=True)

                        def epilogue(ti=ti, t0=t0, n=n, lc=lc, num=num, den=den, r_num=r_num, r_den=r_den, pair=pair, k=attn_ctr[0]):
                            rd_, rrd_ = rden[k % 2], r_rden[k % 2]
                            S.op("dve", lambda e: e.reciprocal(out=rd_[:, 0:n], in_=den[:, 0:n]), reads=[r_den], writes=[rrd_])
                            o_, ro_ = opr[k % 2], r_opr[k % 2]
                            S.op("dve", lambda e: e.tensor_tensor(out=o_[:, 0:n], in0=num[:, 0:n], in1=rd_[:, 0:n], op=ALU.mult),
                                 reads=[r_num, rrd_], writes=[ro_])
                            for oc in range(8):
                                bk, rb = self.bank()
                                S.op("pe", lambda e: e.matmul(bk[:, 0:n], lhsT=wo[pair % 2][:, oc * 128:(oc + 1) * 128], rhs=o_[:, 0:n], start=True, stop=True),
                                     reads=[r_wo[pair % 2], ro_], writes=[rb], same_ok=True)
                                hz = self.hT[:, oc, t0:t0 + n]
                                S.op("dve", lambda e: e.scalar_tensor_tensor(out=hz, in0=bk[:, 0:n], scalar=self.mcol(i, 2, oc, lc), in1=hz,
                                                                             op0=ALU.mult, op1=ALU.add),
                                     reads=[rb, self.r_h[ti][oc], self.r_mod[i]], writes=[self.r_h[ti][oc]])
                        if pending[0] is not None:
                            pending[0]()
                        pending[0] = epilogue
                if pending[0] is not None:
                    pending[0]()
                S.barrier()
        self.nrr = 6
        self.bank_i = 0

    def diff(self, i):
        S = self.S
        SCALE = 64.0 ** -0.5
        lam_init = 0.8 - 0.6 * math.exp(-0.3 * i)
        self.nrr = 4
        self.bank_i = 0
        with ExitStack() as ph:
            qp = [self.sb([128, T], BF16, es=ph) for _ in range(2)]; r_qp = [Res("qp0"), Res("qp1")]
            kp = [self.sb([128, T], BF16, es=ph) for _ in range(2)]; r_kp = [Res("kp0"), Res("kp1")]
            vp = self.sb([128, 18, 128], BF16, es=ph); r_vp = Res("vp")
            qtab = self.sb([128, 512], F32, es=ph); r_qtab = Res("qtab")
            kta = self.sb([128, 512], F32, es=ph); r_kta = Res("kta")
            ktb = self.sb([128, 512], F32, es=ph); r_ktb = Res("ktb")
            t1 = self.sb([128, 512], F32, es=ph); r_t1 = Res("t1")
            t2 = self.sb([128, 512], F32, es=ph); r_t2 = Res("t2")
            t1s = [self.sb([128, 512], F32, es=ph) for _ in range(2)]; r_t1s = [Res("t1s0"), Res("t1s1")]
            dctr = [0]
            pend1 = [None]
            pt = [self.sb([128, 512], BF16, es=ph) for _ in range(4)]; r_pt = [Res() for _ in range(4)]
            rd = self.sb([128, 2, 512], F32, es=ph); r_rd0 = Res("rd0"); r_rd1 = Res("rd1")
            onb = self.sb([128, 128], BF16, es=ph); r_onb = Res("onb")
            on_ = [self.sb([128, 512], BF16, es=ph) for _ in range(2)]; r_on = [Res(), Res()]
            wo = [self.sb([128, D], BF16, es=ph) for _ in range(2)]; r_wo = [Res("wo0"), Res("wo1")]
            lam = self.sb([128, 4], F32, es=ph); r_lam = Res("lam")
            lv = self.sb([64, 4], F32, es=ph); r_lv = Res("lv")
            sub = self.sb([128, 1], F32, es=ph); r_sub = Res("sub")
            epsc = self.sb([128, 1], F32, es=ph); r_eps = Res("eps")
            S.op("pool", lambda e: e.memset(epsc[:], EPS), writes=[r_eps])
            S.op("pool", lambda e: e.memset(onb[:], 1.0), writes=[r_onb])
            S.dma("sp", [(lv[:], self.din["diff_lam"])], r_lv, writes=[r_lv])
            S.op("dve", lambda e: e.tensor_tensor(out=lv[:, 0:1], in0=lv[:, 0:1], in1=lv[:, 1:2], op=ALU.mult), reads=[r_lv], writes=[r_lv])
            S.op("dve", lambda e: e.tensor_tensor(out=lv[:, 1:2], in0=lv[:, 2:3], in1=lv[:, 3:4], op=ALU.mult), reads=[r_lv], writes=[r_lv])
            bk, rb = self.bank()
            S.op("pe", lambda e: e.matmul(bk[:, 0:2], lhsT=self.ones[0:64, :], rhs=lv[:, 0:2], start=True, stop=True),
                 reads=[r_lv, self.r_ident], writes=[rb], same_ok=True)
            S.op("act", lambda e: e.activation(out=lam[:, 0:2], in_=bk[:, 0:2], func=AF.Exp), reads=[rb], writes=[r_lam])
            S.op("dve", lambda e: e.scalar_tensor_tensor(out=lam[:, 2:3], in0=lam[:, 1:2], scalar=-lam_init, in1=lam[:, 0:1], op0=ALU.add, op1=ALU.subtract),
                 reads=[r_lam], writes=[r_lam])
            self.load_cols(self.din["diff_subln"], 1, sub[:, 0:1], r_sub)
            S.op("dve", lambda e: e.tensor_scalar(out=sub[:], in0=sub[:], scalar1=1.0 - lam_init, scalar2=None, op0=ALU.mult), reads=[r_sub], writes=[r_sub])
            nums = [(self.banks[4], self.bank_res[4]), (self.banks[5], self.bank_res[5])]
            dens = [(self.banks[6], self.bank_res[6]), (self.banks[7], self.bank_res[7])]
            pti = 0
            pti = 0
            for h in range(8):
                S.dma("pool", [(wo[h % 2][:], self.din["diff_w_out"][h * 128:(h + 1) * 128, :])], r_wo[h % 2], writes=[r_wo[h % 2]])
                wv = None
                for m in range(2):
                    mi = h * 2 + m
                    w_, r_w = self.wnext()
                    wx = w_[:, 0:3072].rearrange("p (k n) -> p k n", k=8)
                    prs = [(wx, self.din["diff_wx"][:, mi * 384:(mi + 1) * 384].rearrange("(k p) n -> p k n", p=128))]
                    if m == 0:
                        wv = w_[:, 3072:4096].rearrange("p (k n) -> p k n", k=8)
                        r_wv = r_w
                        prs.append((wv, self.din["diff_wv"][:, h * 128:(h + 1) * 128].rearrange("(k p) n -> p k n", p=128)))
                    S.dma("pool", prs, r_w, writes=[r_w])
                    for ti, (t0, n, lc) in enumerate(TT):
                        ru = self.r_u[ti]
                        S.dma("sp", [(qtab[:, 0:n], self.din["diff_qtab"][:, t0:t0 + n])], r_qtab, writes=[r_qtab])
                        S.dma("sp", [(kta[:, 0:n], self.din["diff_kta"][:, t0:t0 + n])], r_kta, writes=[r_kta])
                        S.dma("sp", [(ktb[:, 0:n], self.din["diff_ktb"][:, t0:t0 + n])], r_ktb, writes=[r_ktb])
                        bq, rbq = self.bank()
                        for kc in range(8):
                            S.op("pe", lambda e: e.matmul(bq[:, 0:n], lhsT=wx[:, kc, 0:128], rhs=self.uT[:, kc, t0:t0 + n], start=(kc == 0), stop=(kc == 7)),
                                 reads=[r_w, ru], writes=[rbq], same_ok=True)
                        S.op("dve", lambda e: e.tensor_tensor(out=qp[m][:, t0:t0 + n], in0=bq[:, 0:n], in1=qtab[:, 0:n], op=ALU.mult),
                             reads=[rbq, r_qtab], writes=[r_qp[m]])
                        bA, rbA = self.bank()
                        for kc in range(8):
                            S.op("pe", lambda e: e.matmul(bA[:, 0:n], lhsT=wx[:, kc, 128:256], rhs=self.uT[:, kc, t0:t0 + n], start=(kc == 0), stop=(kc == 7)),
                                 reads=[r_w, ru], writes=[rbA], same_ok=True)
                        bB, rbB = self.bank()
                        for kc in range(8):
                            S.op("pe", lambda e: e.matmul(bB[:, 0:n], lhsT=wx[:, kc, 256:384], rhs=self.uT[:, kc, t0:t0 + n], start=(kc == 0), stop=(kc == 7)),
                                 reads=[r_w, ru], writes=[rbB], same_ok=True)
                        S.op("dve", lambda e: e.tensor_tensor(out=t1[:, 0:n], in0=bA[:, 0:n], in1=kta[:, 0:n], op=ALU.mult), reads=[rbA, r_kta], writes=[r_t1])
                        S.op("dve", lambda e: e.tensor_tensor(out=t2[:, 0:n], in0=bB[:, 0:n], in1=ktb[:, 0:n], op=ALU.mult), reads=[rbB, r_ktb], writes=[r_t2])
                        S.op("pool", lambda e: e.tensor_tensor(out=kp[m][:, t0:t0 + n], in0=t1[:, 0:n], in1=t2[:, 0:n], op=ALU.add),
                             reads=[r_t1, r_t2], writes=[r_kp[m]])
                for g0 in range(0, 18, 4):
                    ng = min(4, 18 - g0)
                    bk, rb = self.bank()
                    for jt in range(ng):
                        kt = g0 + jt
                        for kc in range(8):
                            S.op("pe", lambda e: e.matmul(bk[:, jt * 128:(jt + 1) * 128], lhsT=self.uT[:, kc, kt * 128:(kt + 1) * 128],
                                                          rhs=wv[:, kc, :], start=(kc == 0), stop=(kc == 7)),
                                 reads=[r_wv] + self.r_u, writes=[rb], same_ok=True)
                    S.op("act", lambda e: e.copy(vp[:, g0:g0 + ng, :], bk[:, 0:ng * 128].rearrange("p (j d) -> p j d", d=128)), reads=[rb], writes=[r_vp])
                for ti, (t0, n, lc) in enumerate(TT):
                    kts = list(range(18)) if lc == 0 else [16, 17]
                    k_ = dctr[0]
                    dctr[0] += 1
                    t1_, rt1_ = t1s[k_ % 2], r_t1s[k_ % 2]

                    def attend(m):
                        nonlocal pti
                        num, r_num = nums[m]
                        den, r_den = dens[m]
                        sbanks = {}

                        def issue_score(ix):
                            kt = kts[ix]
                            bk, rb = self.bank()
                            S.op("pe", lambda e: e.matmul(bk[:, 0:n], lhsT=kp[m][:, kt * 128:(kt + 1) * 128], rhs=qp[m][:, t0:t0 + n], start=True, stop=True),
                                 reads=[r_kp[m], r_qp[m]], writes=[rb], same_ok=True)
                            sbanks[ix] = (bk, rb)
                        for ix in range(min(2, len(kts))):
                            issue_score(ix)
                        for ix, kt in enumerate(kts):
                            bk, rb = sbanks.pop(ix)
                            p_, rp_ = pt[pti % 4], r_pt[pti % 4]
                            pti += 1
                            S.op("act", lambda e: e.activation(out=p_[:, 0:n], in_=bk[:, 0:n], func=AF.Exp, scale=SCALE), reads=[rb], writes=[rp_])
                            if ix + 2 < len(kts):
                                issue_score(ix + 2)
                            first, last = (ix == 0), (ix == len(kts) - 1)
                            S.op("pe", lambda e: e.matmul(num[:, 0:n], lhsT=vp[:, kt, :], rhs=p_[:, 0:n], start=first, stop=last),
                                 reads=[r_vp, rp_], writes=[r_num], same_ok=True)
                            S.op("pe", lambda e: e.matmul(den[:, 0:n], lhsT=onb[:], rhs=p_[:, 0:n], start=first, stop=last),
                                 reads=[r_onb, rp_], writes=[r_den], same_ok=True)

                    def ep0(n=n, t1_=t1_, rt1_=rt1_):
                        S.op("dve", lambda e: e.reciprocal(out=rd[:, 0, 0:n], in_=dens[0][0][:, 0:n]), reads=[dens[0][1]], writes=[r_rd0])
                        S.op("dve", lambda e: e.tensor_tensor(out=t1_[:, 0:n], in0=nums[0][0][:, 0:n], in1=rd[:, 0, 0:n], op=ALU.mult),
                             reads=[nums[0][1], r_rd0], writes=[rt1_])

                    def ep1(ti=ti, t0=t0, n=n, lc=lc, t1_=t1_, rt1_=rt1_, h=h, k_=k_):
                        S.op("dve", lambda e: e.reciprocal(out=rd[:, 1, 0:n], in_=dens[1][0][:, 0:n]), reads=[dens[1][1]], writes=[r_rd1])
                        S.op("dve", lambda e: e.scalar_tensor_tensor(out=t2[:, 0:n], in0=nums[1][0][:, 0:n], scalar=lam[:, 2:3], in1=rd[:, 1, 0:n],
                                                                     op0=ALU.mult, op1=ALU.mult), reads=[nums[1][1], r_rd1, r_lam], writes=[r_t2])
                        S.op("dve", lambda e: e.tensor_tensor(out=t1_[:, 0:n], in0=t1_[:, 0:n], in1=t2[:, 0:n], op=ALU.add), reads=[rt1_, r_t2], writes=[rt1_])
                        S.op("dve", lambda e: e.tensor_tensor(out=t2[:, 0:n], in0=t1_[:, 0:n], in1=t1_[:, 0:n], op=ALU.mult), reads=[rt1_], writes=[r_t2])
                        bk, rb = self.bank()
                        S.op("pe", lambda e: e.matmul(bk[:, 0:n], lhsT=self.ones[:], rhs=t2[:, 0:n], start=True, stop=True),
                             reads=[r_t2, self.r_ident], writes=[rb], same_ok=True)
                        S.op("act", lambda e: e.activation(out=t2[:, 0:n], in_=bk[:, 0:n], func=AF.Sqrt, bias=epsc[:, 0:1], scale=1.0 / 128.0),
                             reads=[rb, r_eps], writes=[r_t2])
                        S.op("dve", lambda e: e.reciprocal(out=t2[:, 0:n], in_=t2[:, 0:n]), reads=[r_t2], writes=[r_t2])
                        o_, ro_ = on_[k_ % 2], r_on[k_ % 2]
                        S.op("dve", lambda e: e.scalar_tensor_tensor(out=o_[:, 0:n], in0=t1_[:, 0:n], scalar=sub[:, 0:1], in1=t2[:, 0:n],
                                                                     op0=ALU.mult, op1=ALU.mult), reads=[rt1_, r_t2, r_sub], writes=[ro_])
                        for oc in range(8):
                            bk, rb = self.bank()
                            S.op("pe", lambda e: e.matmul(bk[:, 0:n], lhsT=wo[h % 2][:, oc * 128:(oc + 1) * 128], rhs=o_[:, 0:n], start=True, stop=True),
                                 reads=[r_wo[h % 2], ro_], writes=[rb], same_ok=True)
                            hz = self.hT[:, oc, t0:t0 + n]
                            S.op("dve", lambda e: e.scalar_tensor_tensor(out=hz, in0=bk[:, 0:n], scalar=self.mcol(i, 2, oc, lc), in1=hz,
                                                                         op0=ALU.mult, op1=ALU.add),
                                 reads=[rb, self.r_h[ti][oc], self.r_mod[i]], writes=[self.r_h[ti][oc]])
                    attend(0)
                    if pend1[0] is not None:
                        pend1[0]()
                    attend(1)
                    ep0()
                    pend1[0] = ep1
            if pend1[0] is not None:
                pend1[0]()
            S.barrier()
        self.nrr = 6
        self.bank_i = 0

    def gla(self, i):
        S = self.S
        QS = 128.0 ** -0.5
        win = self.din["gla_w_in"]
        with ExitStack() as ph:
            arr = [[self.sb([128, T], BF16, es=ph) for _ in range(3)] for _ in range(2)]
            r_arr = [[Res() for _ in range(3)] for _ in range(2)]
            vh = self.sb([128, 18, 256], BF16, es=ph); r_vh = Res("vh")
            oacc = self.sb([128, 2, T], BF16, es=ph); r_oacc = [Res(f"oacc{c}") for c in range(18)]
            rT = self.sb([32, T], BF16, es=ph); r_rT = Res("rT")
            gw = self.sb([32, 2, 512], BF16, es=ph); r_gw = Res("gw")
            gb = self.sb([128, 2, 4], F32, es=ph); r_gb = Res("gb")
            ng = self.sb([128, 2], F32, es=ph); r_ng = Res("ng")
            dec = self.sb([128, 2, 18], F32, es=ph); r_dec = Res("dec")
            cmask = self.sb([128, 512], F32, es=ph); r_cm = Res("cmask")
            msk = [self.sb([128, 128], F32, es=ph) for _ in range(2)]; r_msk = Res("msk")
            bA = self.sb([128, 512], F32, es=ph); r_bA = Res("bA")
            bB = self.sb([128, 512], F32, es=ph); r_bB = Res("bB")
            bC = self.sb([128, 512], F32, es=ph); r_bC = Res("bC")
            bD = self.sb([128, 512], F32, es=ph); r_bD = Res("bD")
            Sf = [self.sb([128, 256], F32, es=ph) for _ in range(2)]; r_Sf = [Res("Sf0"), Res("Sf1")]
            Sb = [self.sb([128, 256], BF16, es=ph) for _ in range(2)]; r_Sb = [Res("Sb0"), Res("Sb1")]
            Am = [self.sb([128, 128], BF16, es=ph) for _ in range(2)]; r_Am = [Res(), Res()]
            keT = [self.sb([128, 128], BF16, es=ph) for _ in range(2)]; r_keT = [Res(), Res()]
            ogn = [self.sb([128, 2, 512], BF16, es=ph) for _ in range(1)]; r_ogn = [Res()]
            epsc = self.sb([128, 1], F32, es=ph); r_eps = Res("eps")
            S.op("pool", lambda e: e.memset(epsc[:], EPS), writes=[r_eps])
            S.op("pool", lambda e: e.memset(cmask[:], 1.0), writes=[r_cm])
            for c in range(4):
                S.op("pool", lambda e: e.memset(cmask[:, c * 128:c * 128 + 1], 0.0), reads=[r_cm], writes=[r_cm])
            for d in range(2):
                S.op("pool", lambda e: e.memset(msk[d][:], 1.0), reads=[r_msk], writes=[r_msk])
                cm, pat = ((-1, [[1, 128]]) if d == 0 else (1, [[-1, 128]]))
                S.op("pool", lambda e: e.affine_select(out=msk[d][:], in_=msk[d][:], compare_op=ALU.is_ge, fill=0.0, base=0,
                                                      pattern=pat, channel_multiplier=cm), reads=[r_msk], writes=[r_msk])
            S.op("pool", lambda e: e.memset(gw[:], 0.0), writes=[r_gw])
            S.dma("pool", [(gw[0:16, 0, :], self.din["gla_gw"][0]), (gw[16:32, 1, :], self.din["gla_gw"][1])], r_gw, reads=[r_gw], writes=[r_gw])
            for d in range(2):
                self.load_cols(self.din["gla_gb"][d], 4, gb[:, d, :], r_gb)
            S.op("dve", lambda e: e.tensor_scalar(out=gb[:], in0=gb[:], scalar1=-1.0, scalar2=None, op0=ALU.mult), reads=[r_gb], writes=[r_gb])
            self.load_cols(self.din["gla_norm"], 2, ng[:, 0:2], r_ng)
            w_, r_w = self.wnext()
            wr = w_[:, 0:256].rearrange("p (k n) -> p k n", k=8)
            S.dma("pool", [(wr, win[:, 3072:3104].rearrange("(k p) n -> p k n", p=128))], r_w, writes=[r_w])
            for ti, (t0, n, lc) in enumerate(TT):
                bk, rb = self.bank()
                for kc in range(8):
                    S.op("pe", lambda e: e.matmul(bk[0:32, 0:n], lhsT=wr[:, kc, :], rhs=self.uT[:, kc, t0:t0 + n], start=(kc == 0), stop=(kc == 7)),
                         reads=[r_w, self.r_u[ti]], writes=[rb], same_ok=True)
                S.op("act", lambda e: e.copy(rT[:, t0:t0 + n], bk[0:32, 0:n]), reads=[rb], writes=[r_rT])

            for h in range(4):
                wA_, r_wA = self.wnext()
                wqk = wA_[:, 0:2048].rearrange("p (k n) -> p k n", k=8)
                S.dma("pool", [(wqk[:, :, 0:128], win[:, h * 128:(h + 1) * 128].rearrange("(k p) n -> p k n", p=128)),
                               (wqk[:, :, 128:256], win[:, 512 + h * 128:512 + (h + 1) * 128].rearrange("(k p) n -> p k n", p=128))],
                      r_wA, writes=[r_wA])
                wB_, r_wB = self.wnext()
                wv = wB_[:, 0:2048].rearrange("p (k n) -> p k n", k=8)
                wg = wB_[:, 2048:4096].rearrange("p (k n) -> p k n", k=8)
                S.dma("pool", [(wv, win[:, 1024 + h * 256:1024 + (h + 1) * 256].rearrange("(k p) n -> p k n", p=128)),
                               (wg, win[:, 2048 + h * 256:2048 + (h + 1) * 256].rearrange("(k p) n -> p k n", p=128))],
                      r_wB, writes=[r_wB])
                wC_, r_wC = self.wnext()
                wo = wC_[:, 0:2048].rearrange("p (k n) -> p k n", k=2)
                S.dma("pool", [(wo, self.din["gla_w_out"][h * 256:(h + 1) * 256, :].rearrange("(k p) n -> p k n", p=128))], r_wC, writes=[r_wC])
                for g0 in range(0, 18, 2):
                    bk, rb = self.bank()
                    for jt in range(2):
                        kt = g0 + jt
                        for kc in range(8):
                            S.op("pe", lambda e: e.matmul(bk[:, jt * 256:(jt + 1) * 256], lhsT=self.uT[:, kc, kt * 128:(kt + 1) * 128], rhs=wv[:, kc, :],
                                                          start=(kc == 0), stop=(kc == 7)), reads=[r_wB] + self.r_u, writes=[rb], same_ok=True)
                    S.op("act", lambda e: e.copy(vh[:, g0:g0 + 2, :], bk[:, 0:512].rearrange("p (j d) -> p j d", d=256)), reads=[rb], writes=[r_vh])
                for ti, (t0, n, lc) in enumerate(TT):
                    ru = self.r_u[ti]
                    nch = n // 128
                    bq, rbq = self.bank()
                    for kc in range(8):
                        S.op("pe", lambda e: e.matmul(bq[:, 0:n], lhsT=wqk[:, kc, 0:128], rhs=self.uT[:, kc, t0:t0 + n], start=(kc == 0), stop=(kc == 7)),
                             reads=[r_wA, ru], writes=[rbq], same_ok=True)
                    bkk, rbk = self.bank()
                    for kc in range(8):
                        S.op("pe", lambda e: e.matmul(bkk[:, 0:n], lhsT=wqk[:, kc, 128:256], rhs=self.uT[:, kc, t0:t0 + n], start=(kc == 0), stop=(kc == 7)),
                             reads=[r_wA, ru], writes=[rbk], same_ok=True)
                    for d in range(2):
                        bx, rbx = self.bank()
                        S.op("pe", lambda e: e.matmul(bx[:, 0:n], lhsT=gw[:, d, h * 128:(h + 1) * 128], rhs=rT[:, t0:t0 + n], start=True, stop=True),
                             reads=[r_gw, r_rT], writes=[rbx], same_ok=True)
                        S.op("act", lambda e: e.activation(out=bA[:, 0:n], in_=bx[:, 0:n], func=AF.Exp, bias=gb[:, d, h:h + 1], scale=-1.0),
                             reads=[rbx, r_gb], writes=[r_bA])
                        S.op("act", lambda e: e.activation(out=bA[:, 0:n], in_=bA[:, 0:n], func=AF.Ln, bias=1.0, scale=1.0), reads=[r_bA], writes=[r_bA])
                        S.op("dve", lambda e: e.tensor_tensor_scan(out=bB[:, 0:n], data0=cmask[:, 0:n], data1=bA[:, 0:n], initial=0.0,
                                                                   op0=ALU.mult, op1=ALU.add), reads=[r_cm, r_bA], writes=[r_bB])
                        for c in range(nch):
                            gc = t0 // 128 + c
                            ce = c * 128 + 127
                            S.op("act", lambda e: e.activation(out=dec[:, d, gc:gc + 1], in_=bB[:, ce:ce + 1], func=AF.Exp, scale=-1.0 / 16), reads=[r_bB], writes=[r_dec])
                            S.op("dve", lambda e: e.tensor_scalar(out=bD[:, c * 128:(c + 1) * 128], in0=bB[:, c * 128:(c + 1) * 128], scalar1=bB[:, ce:ce + 1],
                                                                  scalar2=None, op0=ALU.subtract), reads=[r_bB], writes=[r_bD])
                        if d == 0:
                            S.op("act", lambda e: e.activation(out=bC[:, 0:n], in_=bB[:, 0:n], func=AF.Exp, scale=-1.0 / 16), reads=[r_bB], writes=[r_bC])
                            S.op("dve", lambda e: e.scalar_tensor_tensor(out=arr[d][0][:, t0:t0 + n], in0=bq[:, 0:n], scalar=QS, in1=bC[:, 0:n], op0=ALU.mult, op1=ALU.mult),
                                 reads=[rbq, r_bC], writes=[r_arr[d][0]])
                            S.op("act", lambda e: e.activation(out=bC[:, 0:n], in_=bB[:, 0:n], func=AF.Exp, scale=1.0 / 16), reads=[r_bB], writes=[r_bC])
                            S.op("dve", lambda e: e.tensor_tensor(out=arr[d][1][:, t0:t0 + n], in0=bkk[:, 0:n], in1=bC[:, 0:n], op=ALU.mult),
                                 reads=[rbk, r_bC], writes=[r_arr[d][1]])
                            S.op("act", lambda e: e.activation(out=bC[:, 0:n], in_=bD[:, 0:n], func=AF.Exp, scale=1.0 / 16), reads=[r_bD], writes=[r_bC])
                            S.op("dve", lambda e: e.tensor_tensor(out=arr[d][2][:, t0:t0 + n], in0=bkk[:, 0:n], in1=bC[:, 0:n], op=ALU.mult),
                                 reads=[rbk, r_bC], writes=[r_arr[d][2]])
                        else:
                            S.op("dve", lambda e: e.tensor_tensor(out=bD[:, 0:n], in0=bA[:, 0:n], in1=bD[:, 0:n], op=ALU.subtract), reads=[r_bA, r_bD], writes=[r_bD])
                            S.op("act", lambda e: e.activation(out=bC[:, 0:n], in_=bD[:, 0:n], func=AF.Exp, scale=-1.0 / 16), reads=[r_bD], writes=[r_bC])
                            S.op("dve", lambda e: e.scalar_tensor_tensor(out=arr[d][0][:, t0:t0 + n], in0=bq[:, 0:n], scalar=QS, in1=bC[:, 0:n], op0=ALU.mult, op1=ALU.mult),
                                 reads=[rbq, r_bC], writes=[r_arr[d][0]])
                            S.op("act", lambda e: e.activation(out=bC[:, 0:n], in_=bD[:, 0:n], func=AF.Exp, scale=1.0 / 16), reads=[r_bD], writes=[r_bC])
                            S.op("dve", lambda e: e.tensor_tensor(out=arr[d][1][:, t0:t0 + n], in0=bkk[:, 0:n], in1=bC[:, 0:n], op=ALU.mult),
                                 reads=[rbk, r_bC], writes=[r_arr[d][1]])
                            S.op("dve", lambda e: e.tensor_tensor(out=bD[:, 0:n], in0=bA[:, 0:n], in1=bB[:, 0:n], op=ALU.subtract), reads=[r_bA, r_bB, r_bC], writes=[r_bD])
                            S.op("act", lambda e: e.activation(out=bC[:, 0:n], in_=bD[:, 0:n], func=AF.Exp, scale=1.0 / 16), reads=[r_bD], writes=[r_bC])
                            S.op("dve", lambda e: e.tensor_tensor(out=arr[d][2][:, t0:t0 + n], in0=bkk[:, 0:n], in1=bC[:, 0:n], op=ALU.mult),
                                 reads=[rbk, r_bC], writes=[r_arr[d][2]])
                for d in range(2):
                    S.op("pool", lambda e: e.memset(Sf[d][:], 0.0), reads=[r_Sf[d]], writes=[r_Sf[d]])
                    S.op("pool", lambda e: e.memset(Sb[d][:], 0.0), reads=[r_Sb[d]], writes=[r_Sb[d]])
                order = [[16, 17] + list(range(16)), [17, 16] + list(range(15, -1, -1))]
                written = set()
                for step in range(18):
                    for d in range(2):
                        c = order[d][step]
                        cs = slice(c * 128, (c + 1) * 128)
                        qd, ki, ke = arr[d]
                        ba, rba = self.bank()
                        S.op("pe", lambda e: e.matmul(ba[:, 0:128], lhsT=ki[:, cs], rhs=qd[:, cs], start=True, stop=True),
                             reads=[r_arr[d][1], r_arr[d][0]], writes=[rba], same_ok=True)
                        bke, rbke = self.bank()
                        S.op("pe", lambda e: e.matmul(bke[:, 0:128], lhsT=ke[:, cs], rhs=self.identb[:], start=True, stop=True),
                             reads=[r_arr[d][2], self.r_ident], writes=[rbke], same_ok=True)
                        S.op("dve", lambda e: e.tensor_tensor(out=Am[d][:], in0=ba[:, 0:128], in1=msk[d][:], op=ALU.mult), reads=[rba, r_msk], writes=[r_Am[d]])
                        S.op("act", lambda e: e.copy(keT[d][:], bke[:, 0:128]), reads=[rbke], writes=[r_keT[d]])
                        bo, rbo = self.bank()
                        for j in range(2):
                            S.op("pe", lambda e: e.matmul(bo[:, j * 128:(j + 1) * 128], lhsT=Sb[d][:, j * 128:(j + 1) * 128], rhs=qd[:, cs], start=True, stop=False),
                                 reads=[r_Sb[d], r_arr[d][0]], writes=[rbo], same_ok=True)
                            S.op("pe", lambda e: e.matmul(bo[:, j * 128:(j + 1) * 128], lhsT=vh[:, c, j * 128:(j + 1) * 128], rhs=Am[d][:], start=False, stop=True),
                                 reads=[r_vh, r_Am[d]], writes=[rbo], same_ok=True)
                        ov = oacc[:, :, cs]
                        pv = bo[:, 0:256].rearrange("p (j c) -> p j c", j=2)
                        if c not in written:
                            written.add(c)
                            S.op("act", lambda e: e.copy(ov, pv), reads=[rbo], writes=[r_oacc[c]])
                        else:
                            S.op("dve", lambda e: e.tensor_tensor(out=ov, in0=pv, in1=ov, op=ALU.add), reads=[rbo, r_oacc[c]], writes=[r_oacc[c]])
                        bs, rbs = self.bank()
                        S.op("pe", lambda e: e.matmul(bs[:, 0:256], lhsT=keT[d][:], rhs=vh[:, c, :], start=True, stop=True),
                             reads=[r_keT[d], r_vh], writes=[rbs], same_ok=True)
                        S.op("dve", lambda e: e.scalar_tensor_tensor(out=Sf[d][:], in0=Sf[d][:], scalar=dec[:, d, c:c + 1], in1=bs[:, 0:256], op0=ALU.mult, op1=ALU.add),
                             reads=[r_Sf[d], r_dec, rbs], writes=[r_Sf[d]])
                        S.op("act", lambda e: e.copy(Sb[d][:], Sf[d][:]), reads=[r_Sf[d]], writes=[r_Sb[d]])
                for ti, (t0, n, lc) in enumerate(TT):
                    ru = self.r_u[ti]
                    roa = r_oacc[t0 // 128:(t0 + n) // 128]
                    bss = self.banks[6]; rbss = self.bank_res[6]
                    for j in range(2):
                        S.op("act", lambda e: e.activation(out=bA[:, 0:n], in_=oacc[:, j, t0:t0 + n], func=AF.Square), reads=roa + [r_bA], writes=[r_bA])
                        S.op("pe", lambda e: e.matmul(bss[:, 0:n], lhsT=self.ones[:], rhs=bA[:, 0:n], start=(j == 0), stop=(j == 1)),
                             reads=[r_bA, self.r_ident], writes=[rbss], same_ok=True)
                    S.op("act", lambda e: e.activation(out=bB[:, 0:n], in_=bss[:, 0:n], func=AF.Sqrt, bias=epsc[:, 0:1], scale=1.0 / 256.0), reads=[rbss, r_eps], writes=[r_bB])
                    S.op("dve", lambda e: e.reciprocal(out=bB[:, 0:n], in_=bB[:, 0:n]), reads=[r_bB], writes=[r_bB])
                    og, rog = ogn[0], r_ogn[0]
                    for j in range(2):
                        bg, rbg = self.bank()
                        for kc in range(8):
                            S.op("pe", lambda e: e.matmul(bg[:, 0:n], lhsT=wg[:, kc, j * 128:(j + 1) * 128], rhs=self.uT[:, kc, t0:t0 + n], start=(kc == 0), stop=(kc == 7)),
                                 reads=[r_wB, ru], writes=[rbg], same_ok=True)
                        S.op("act", lambda e: e.activation(out=bC[:, 0:n], in_=bg[:, 0:n], func=AF.Silu), reads=[rbg], writes=[r_bC])
                        S.op("dve", lambda e: e.scalar_tensor_tensor(out=bD[:, 0:n], in0=oacc[:, j, t0:t0 + n], scalar=ng[:, j:j + 1], in1=bB[:, 0:n], op0=ALU.mult, op1=ALU.mult),
                             reads=roa + [r_ng, r_bB], writes=[r_bD])
                        S.op("dve", lambda e: e.tensor_tensor(out=og[:, j, 0:n], in0=bD[:, 0:n], in1=bC[:, 0:n], op=ALU.mult), reads=[r_bD, r_bC], writes=[rog])
                    for oc in range(8):
                        bk, rb = self.bank()
                        for j in range(2):
                            S.op("pe", lambda e: e.matmul(bk[:, 0:n], lhsT=wo[:, j, oc * 128:(oc + 1) * 128], rhs=og[:, j, 0:n], start=(j == 0), stop=(j == 1)),
                                 reads=[r_wC, rog], writes=[rb], same_ok=True)
                        hz = self.hT[:, oc, t0:t0 + n]
                        S.op("dve", lambda e: e.scalar_tensor_tensor(out=hz, in0=bk[:, 0:n], scalar=self.mcol(i, 2, oc, lc), in1=hz, op0=ALU.mult, op1=ALU.add),
                             reads=[rb, self.r_h[ti][oc], self.r_mod[i]], writes=[self.r_h[ti][oc]])
            S.barrier()

    def gdn(self, i):
        S = self.S
        R32 = mybir.dt.float32r
        win = self.din["gdn_w_in"]
        rr = lambda ap: ap.bitcast(R32)
        with ExitStack() as ph:
            sbp = lambda shape, dt: self.sb(shape, dt, es=ph)
            cw = sbp([128, 5, 32], F32); r_cw = Res("cw")
            for j in range(5):
                self.load_cols(self.din["gdn_conv"][j], 32, cw[:, j, :], r_cw)
            r_msk = Res("gmsk")
            inclT = [sbp([128, 128], F32) for _ in range(2)]
            strict2 = sbp([128, 2, 128], BF16)
            inclT2 = sbp([128, 2, 128], BF16)
            bd16 = sbp([128, 128], BF16); off16 = sbp([128, 128], BF16); off32 = sbp([128, 128], BF16); off64 = sbp([128, 128], BF16)
            with ExitStack() as mk:
                strict = [self.sb([128, 128], F32, es=mk) for _ in range(2)]
                specs = [(strict[0], ALU.is_gt, 1, [[-1, 128]]), (strict[1], ALU.is_gt, -1, [[1, 128]]),
                         (inclT[0], ALU.is_ge, -1, [[1, 128]]), (inclT[1], ALU.is_ge, 1, [[-1, 128]])]
                for (t_, cmp_, cm, pat) in specs:
                    S.op("pool", lambda e: e.memset(t_[:], 1.0), reads=[r_msk], writes=[r_msk])
                    S.op("pool", lambda e: e.affine_select(out=t_[:], in_=t_[:], compare_op=cmp_, fill=0.0, base=0, pattern=pat, channel_multiplier=cm),
                         reads=[r_msk], writes=[r_msk])
                for d in range(2):
                    S.op("dve", lambda e: e.tensor_copy(strict2[:, d, :], strict[d][:]), reads=[r_msk], writes=[r_msk])
                    S.op("dve", lambda e: e.tensor_copy(inclT2[:, d, :], inclT[d][:]), reads=[r_msk], writes=[r_msk])
                bsel = self.sb([8, 128], F32, es=mk)
                bd = {}
                for b_ in (16, 32, 64):
                    nb = 128 // b_
                    S.op("pool", lambda e: e.memset(bsel[:], 1.0), reads=[r_msk], writes=[r_msk])
                    S.op("pool", lambda e: e.affine_select(out=bsel[:], in_=bsel[:], compare_op=ALU.is_ge, fill=0.0, base=0, pattern=[[1, 128]], channel_multiplier=-b_),
                         reads=[r_msk], writes=[r_msk])
                    S.op("pool", lambda e: e.affine_select(out=bsel[:], in_=bsel[:], compare_op=ALU.is_ge, fill=0.0, base=b_ - 1, pattern=[[-1, 128]], channel_multiplier=b_),
                         reads=[r_msk], writes=[r_msk])
                    bk, rb = self.bank()
                    S.op("pe", lambda e: e.matmul(bk[:, 0:128], lhsT=bsel[0:nb, :], rhs=bsel[0:nb, :], start=True, stop=True), reads=[r_msk], writes=[rb], same_ok=True)
                    bd[b_] = self.sb([128, 128], F32, es=mk)
                    S.op("dve", lambda e: e.tensor_copy(bd[b_][:], bk[:, 0:128]), reads=[rb, r_msk], writes=[r_msk])
                S.op("dve", lambda e: e.tensor_copy(bd16[:], bd[16][:]), reads=[r_msk], writes=[r_msk])
                S.op("dve", lambda e: e.tensor_tensor(out=off16[:], in0=bd[32][:], in1=bd[16][:], op=ALU.subtract), reads=[r_msk], writes=[r_msk])
                S.op("dve", lambda e: e.tensor_tensor(out=off32[:], in0=bd[64][:], in1=bd[32][:], op=ALU.subtract), reads=[r_msk], writes=[r_msk])
                S.op("dve", lambda e: e.tensor_scalar(out=off64[:], in0=bd[64][:], scalar1=-1.0, scalar2=1.0, op0=ALU.mult, op1=ALU.add), reads=[r_msk], writes=[r_msk])
                S.barrier()
            b4 = lambda m_: m_[:].unsqueeze(1).broadcast_to([128, 4, 128])
            hc = sbp([128, 64], F32); r_hc = Res("hc")
            S.dma("sp", [(hc[:], self.din["gdn_hc"].partition_broadcast(128))], r_hc, writes=[r_hc])
            S.op("act", lambda e: e.activation(out=hc[:, 0:32], in_=hc[:, 0:32], func=AF.Exp), reads=[r_hc], writes=[r_hc])
            S.op("dve", lambda e: e.tensor_scalar(out=hc[:, 0:32], in0=hc[:, 0:32], scalar1=-1.0, scalar2=None, op0=ALU.mult), reads=[r_hc], writes=[r_hc])
            ngrep = sbp([128, 128], F32); r_ngr = Res("ngrep")
            S.dma("sp", [(ngrep[:], self.din["gdn_norm"].partition_broadcast(128))], r_ngr, writes=[r_ngr])
            epsc = sbp([128, 1], F32); r_eps = Res("eps")
            S.op("pool", lambda e: e.memset(epsc[:], EPS), writes=[r_eps])
            qkT = sbp([128, 2, T], BF16); r_qkT = Res("qkT")
            kn = sbp([128, 18, 128], BF16); r_kn = Res("kn")
            vt = sbp([128, 18, 256], BF16); r_vt = Res("vt")
            oacc = sbp([128, 18, 256], BF16); r_oacc = [Res(f"go{c}") for c in range(18)]
            sc_names = ("negb", "gc", "e", "ecoef", "negbe", "dl", "g", "ngc")
            sc = {n_: sbp([128, 18, 4], F32) for n_ in sc_names}
            r_sc = Res("gsc")

            for kh in range(8):
                wA_, r_wA = self.wnext()
                wA = wA_[:].rearrange("p (k n) -> p k n", k=8)
                S.dma("pool", [(wA[:, :, 0:128], win[:, kh * 128:(kh + 1) * 128].rearrange("(k p) n -> p k n", p=128)),
                               (wA[:, :, 128:256], win[:, 1024 + kh * 128:1024 + (kh + 1) * 128].rearrange("(k p) n -> p k n", p=128)),
                               (wA[:, :, 256:512], win[:, 2048 + kh * 256:2048 + (kh + 1) * 256].rearrange("(k p) n -> p k n", p=128))],
                      r_wA, writes=[r_wA])
                wB_, r_wB = self.wnext()
                wz = wB_[:, 0:2048].rearrange("p (k n) -> p k n", k=8)
                wgt = wB_[:, 2048:2112].rearrange("p (k n) -> p k n", k=8)
                S.dma("pool", [(wz, win[:, 4096 + kh * 256:4096 + (kh + 1) * 256].rearrange("(k p) n -> p k n", p=128)),
                               (wgt, self.din["gdn_wg"][:, kh * 8:(kh + 1) * 8].rearrange("(k p) n -> p k n", p=128))], r_wB, writes=[r_wB])
                wC_, r_wC = self.wnext()
                wo = wC_[:, 0:2048].rearrange("p (k n) -> p k n", k=2)
                S.dma("pool", [(wo, self.din["gdn_w_out"][kh * 256:(kh + 1) * 256, :].rearrange("(k p) n -> p k n", p=128))], r_wC, writes=[r_wC])
                with ExitStack() as p1:
                    xpad = self.sb([128, 4, 2312], BF16, es=p1); r_xp = Res("xpad")
                    dg = self.sb([128, 4, 5, 128], BF16, es=p1); r_dg = Res("dg")
                    cvs = [self.sb([128, 512], F32, es=p1) for _ in range(3)]; r_cvs = [Res() for _ in range(3)]
                    junk = self.sb([128, 128], F32, es=p1); r_junk = Res("junk")
                    sss = [self.sb([128, 2], F32, es=p1) for _ in range(3)]; r_sss = [Res() for _ in range(3)]
                    qns = [self.sb([128, 128], BF16, es=p1) for _ in range(3)]; r_qns = [Res() for _ in range(3)]
                    S.op("pool", lambda e: e.memset(xpad[:], 0.0), writes=[r_xp])
                    gch = [kh, 8 + kh, 16 + 2 * kh, 17 + 2 * kh]
                    for ch in range(4):
                        for j in range(5):
                            S.op("pool", lambda e: e.tensor_scalar(out=dg[:, ch, j, :], in0=self.identb[:], scalar1=cw[:, j, gch[ch]:gch[ch] + 1], scalar2=1.0, op0=ALU.mult, op1=ALU.mult),
                                 reads=[r_cw, self.r_ident], writes=[r_dg])
                    for ch in range(4):
                        for ti, (t0, n, lc) in enumerate(TT):
                            bk, rb = self.bank()
                            for kc in range(8):
                                S.op("pe", lambda e: e.matmul(bk[:, 0:n], lhsT=wA[:, kc, ch * 128:(ch + 1) * 128], rhs=self.uT[:, kc, t0:t0 + n], start=(kc == 0), stop=(kc == 7)),
                                     reads=[r_wA, self.r_u[ti]], writes=[rb], same_ok=True)
                            c0 = t0 + 2 if lc == 0 else 2054
                            if (ch + ti) % 2 == 0:
                                S.op("act", lambda e: e.copy(xpad[:, ch, c0:c0 + n], bk[:, 0:n]), reads=[rb], writes=[r_xp])
                            else:
                                S.op("dve", lambda e: e.tensor_copy(xpad[:, ch, c0:c0 + n], bk[:, 0:n]), reads=[rb], writes=[r_xp])
                    def p1A(t):
                        b0 = t * 128 + 2 if t < 16 else 2054 + (t - 16) * 128
                        cv, r_cv = cvs[t % 3], r_cvs[t % 3]
                        ss, r_ss = sss[t % 3], r_sss[t % 3]
                        bk, rb = self.bank()
                        for ch in range(4):
                            for j in range(5):
                                S.op("pe", lambda e: e.matmul(bk[:, ch * 128:(ch + 1) * 128], lhsT=xpad[:, ch, b0 + j - 2:b0 + j - 2 + 128], rhs=dg[:, ch, j, :],
                                                              start=(j == 0), stop=(j == 4)), reads=[r_xp, r_dg], writes=[rb], same_ok=True)
                        S.op("act", lambda e: e.activation(out=cv[:], in_=bk[:, 0:512], func=AF.Silu), reads=[rb], writes=[r_cv])
                        for q_ in range(2):
                            S.op("act", lambda e: e.activation(out=junk[:], in_=cv[:, q_ * 128:(q_ + 1) * 128], func=AF.Square, accum_out=ss[:, q_:q_ + 1]),
                                 reads=[r_cv], writes=[r_ss])
                        S.op("act", lambda e: e.activation(out=ss[:], in_=ss[:], func=AF.Sqrt, bias=epsc[:, 0:1], scale=1.0), reads=[r_ss, r_eps], writes=[r_ss])

                    def p1B(t):
                        cv, r_cv = cvs[t % 3], r_cvs[t % 3]
                        ss, r_ss = sss[t % 3], r_sss[t % 3]
                        qn, r_qn = qns[t % 3], r_qns[t % 3]
                        S.op("dve", lambda e: e.reciprocal(out=ss[:], in_=ss[:]), reads=[r_ss], writes=[r_ss])
                        S.op("dve", lambda e: e.tensor_scalar(out=qn[:], in0=cv[:, 0:128], scalar1=ss[:, 0:1], scalar2=128.0 ** -0.5, op0=ALU.mult, op1=ALU.mult),
                             reads=[r_cv, r_ss], writes=[r_qn])
                        S.op("dve", lambda e: e.tensor_scalar(out=kn[:, t, :], in0=cv[:, 128:256], scalar1=ss[:, 1:2], scalar2=None, op0=ALU.mult),
                             reads=[r_cv, r_ss], writes=[r_kn])
                        S.op("pool", lambda e: e.tensor_copy(vt[:, t, :], cv[:, 256:512]), reads=[r_cv], writes=[r_vt])
                        b2, rb2 = self.bank()
                        S.op("pe", lambda e: e.matmul(b2[:, 0:128], lhsT=qn[:], rhs=self.identb[:], start=True, stop=True), reads=[r_qn, self.r_ident], writes=[rb2], same_ok=True)
                        S.op("pe", lambda e: e.matmul(b2[:, 128:256], lhsT=kn[:, t, :], rhs=self.identb[:], start=True, stop=True), reads=[r_kn, self.r_ident], writes=[rb2], same_ok=True)
                        S.op("act", lambda e: e.copy(qkT[:, :, t * 128:(t + 1) * 128], b2[:, 0:256].rearrange("p (a c) -> p a c", a=2)), reads=[rb2], writes=[r_qkT])
                    p1A(0)
                    for t in range(18):
                        if t + 1 < 18:
                            p1A(t + 1)
                        p1B(t)
                    S.barrier()
                bk, rb = self.bank()
                for t in range(18):
                    for kc in range(8):
                        S.op("pe", lambda e: e.matmul(bk[:, t * 8:(t + 1) * 8], lhsT=self.uT[:, kc, t * 128:(t + 1) * 128], rhs=wgt[:, kc, :], start=(kc == 0), stop=(kc == 7)),
                             reads=[r_wB] + self.r_u, writes=[rb], same_ok=True)
                graw = bk[:, 0:144].rearrange("p (t c) -> p t c", c=8)
                S.op("act", lambda e: e.activation(out=sc["negb"][:], in_=graw[:, :, 0:4], func=AF.Sigmoid), reads=[rb], writes=[r_sc])
                S.op("dve", lambda e: e.tensor_scalar(out=sc["negb"][:], in0=sc["negb"][:], scalar1=-1.0, scalar2=None, op0=ALU.mult), reads=[r_sc], writes=[r_sc])
                for m in range(4):
                    d_, j_ = m // 2, m % 2
                    hidx = d_ * 16 + 2 * kh + j_
                    S.op("act", lambda e: e.activation(out=sc["g"][:, :, m], in_=graw[:, :, 4 + m], func=AF.Exp, bias=hc[:, 32 + hidx:33 + hidx], scale=1.0),
                         reads=[rb, r_hc], writes=[r_sc])
                S.op("act", lambda e: e.activation(out=sc["g"][:], in_=sc["g"][:], func=AF.Ln, bias=1.0, scale=1.0), reads=[r_sc], writes=[r_sc])
                for m in range(4):
                    d_, j_ = m // 2, m % 2
                    hidx = d_ * 16 + 2 * kh + j_
                    S.op("dve", lambda e: e.tensor_scalar(out=sc["g"][:, :, m], in0=sc["g"][:, :, m], scalar1=hc[:, hidx:hidx + 1], scalar2=None, op0=ALU.mult),
                         reads=[r_sc, r_hc], writes=[r_sc])
                bk, rb = self.bank()
                gv = sc["g"][:]
                S.op("pe", lambda e: e.matmul(bk[:, 0:72].rearrange("p (t c) -> p t c", c=4)[:, :, 0:2], lhsT=inclT[0][:], rhs=gv[:, :, 0:2], start=True, stop=True),
                     reads=[r_sc, r_msk], writes=[rb], same_ok=True)
                S.op("pe", lambda e: e.matmul(bk[:, 0:72].rearrange("p (t c) -> p t c", c=4)[:, :, 2:4], lhsT=inclT[1][:], rhs=gv[:, :, 2:4], start=True, stop=True),
                     reads=[r_sc, r_msk], writes=[rb], same_ok=True)
                S.op("pe", lambda e: e.matmul(bk[:, 128:200], lhsT=self.ones[:], rhs=gv.rearrange("p t c -> p (t c)"), start=True, stop=True),
                     reads=[r_sc, self.r_ident], writes=[rb], same_ok=True)
                gcp = bk[:, 0:72].rearrange("p (t c) -> p t c", c=4)
                glp = bk[:, 128:200].rearrange("p (t c) -> p t c", c=4)
                S.op("dve", lambda e: e.tensor_copy(sc["gc"][:], gcp), reads=[rb], writes=[r_sc])
                S.op("dve", lambda e: e.tensor_copy(sc["dl"][:], glp), reads=[rb], writes=[r_sc])
                S.op("dve", lambda e: e.tensor_tensor(out=sc["ecoef"][:], in0=glp, in1=sc["gc"][:], op=ALU.subtract), reads=[rb, r_sc], writes=[r_sc])
                S.op("dve", lambda e: e.tensor_scalar(out=sc["ngc"][:], in0=sc["gc"][:], scalar1=-1.0, scalar2=None, op0=ALU.mult), reads=[r_sc], writes=[r_sc])
                S.op("act", lambda e: e.activation(out=sc["e"][:], in_=sc["gc"][:], func=AF.Exp), reads=[r_sc], writes=[r_sc])
                S.op("act", lambda e: e.activation(out=sc["dl"][:], in_=sc["dl"][:], func=AF.Exp), reads=[r_sc], writes=[r_sc])
                S.op("act", lambda e: e.activation(out=sc["ecoef"][:], in_=sc["ecoef"][:], func=AF.Exp), reads=[r_sc], writes=[r_sc])
                S.op("dve", lambda e: e.tensor_tensor(out=sc["negbe"][:], in0=sc["negb"][:], in1=sc["e"][:], op=ALU.mult), reads=[r_sc], writes=[r_sc])
                with ExitStack() as p3:
                    h4 = lambda: self.sb([128, 4, 128], BF16, es=p3)
                    f4 = lambda: self.sb([128, 4, 128], F32, es=p3)
                    ST = []
                    for st_ in range(2):
                        B = dict(X=[h4(), h4()], Y=[h4(), h4()], W=h4(), Tm=h4(), Y0=h4(), AT=h4(),
                                 Gm=self.sb([128, 2, 128], BF16, es=p3), QKm=self.sb([128, 2, 128], BF16, es=p3), scr=f4(), rscr=Res(),
                                 rX=[Res(), Res()], rY=[Res(), Res()], rW=Res(), rTm=Res(), rY0=Res(), rAT=Res(), rGm=Res(), rQKm=Res())
                        ST.append(B)
                    Rm = h4(); r_Rm = Res("Rm")
                    vn = h4(); r_vn = Res("vn")
                    kdec = h4(); r_kdec = Res("kdec")
                    bv = h4(); r_bv = Res("bv")
                    Sf = f4(); r_Sf = Res("Sf")
                    Sb = h4(); r_Sb = Res("Sb")
                    ot = h4(); r_ot = Res("ot")
                    S.op("pool", lambda e: e.memset(Sf[:], 0.0), writes=[r_Sf])
                    S.op("pool", lambda e: e.memset(Sb[:], 0.0), writes=[r_Sb])
                    order = [[16, 17] + list(range(16)), [17, 16] + list(range(15, -1, -1))]
                    written = set()
                    kT = lambda c: qkT[:, 1, c * 128:(c + 1) * 128]
                    qT = lambda c: qkT[:, 0, c * 128:(c + 1) * 128]
                    pv4 = lambda b_: b_[:, 0:512].rearrange("p (m c) -> p m c", m=4)

                    def pre(step, B):
                        X, Y, W, Tm, Y0, AT, Gm, QKm = B["X"], B["Y"], B["W"], B["Tm"], B["Y0"], B["AT"], B["Gm"], B["QKm"]
                        rX, rY, rW, rTm, rY0, rAT, rGm, rQKm = B["rX"], B["rY"], B["rW"], B["rTm"], B["rY0"], B["rAT"], B["rGm"], B["rQKm"]
                        cc = [order[0][step], order[1][step]]
                        cm_ = [cc[m // 2] for m in range(4)]
                        scrA = scrB = B["scr"]
                        r_scrA = r_scrB = B["rscr"]
                        bk, rb = self.bank()
                        for d in range(2):
                            S.op("pe", lambda e: e.matmul(bk[:, d * 128:(d + 1) * 128], lhsT=kT(cc[d]), rhs=kT(cc[d]), start=True, stop=True), reads=[r_qkT], writes=[rb], same_ok=True)
                            S.op("pe", lambda e: e.matmul(bk[:, 256 + d * 128:256 + (d + 1) * 128], lhsT=kT(cc[d]), rhs=qT(cc[d]), start=True, stop=True), reads=[r_qkT], writes=[rb], same_ok=True)
                        S.op("dve", lambda e: e.tensor_tensor(out=Gm[:], in0=bk[:, 0:256].rearrange("p (d c) -> p d c", d=2), in1=strict2[:], op=ALU.mult), reads=[rb, r_msk], writes=[rGm])
                        S.op("dve", lambda e: e.tensor_tensor(out=QKm[:], in0=bk[:, 256:512].rearrange("p (d c) -> p d c", d=2), in1=inclT2[:], op=ALU.mult), reads=[rb, r_msk], writes=[rQKm])
                        for m in range(4):
                            S.op("pool", lambda e: e.tensor_scalar(out=scrA[:, m, :], in0=self.ident[:], scalar1=sc["gc"][:, cm_[m], m:m + 1], scalar2=1.0, op0=ALU.mult, op1=ALU.mult),
                                 reads=[r_sc, self.r_ident], writes=[r_scrA])
                        yield
                        bb, rbb = self.bank()
                        for m in range(4):
                            S.op("pe", lambda e: e.matmul(bb[:, m * 128:(m + 1) * 128], lhsT=self.ones[:], rhs=scrA[:, m, :], start=True, stop=True),
                                 reads=[r_scrA, self.r_ident], writes=[rbb], same_ok=True)
                        for m in range(4):
                            S.op("act", lambda e: e.activation(out=scrA[:, m, :], in_=bb[:, m * 128:(m + 1) * 128], func=AF.Relu, bias=sc["ngc"][:, cm_[m], m:m + 1], scale=1.0),
                                 reads=[rbb, r_sc], writes=[r_scrA])
                        S.op("act", lambda e: e.activation(out=X[1][:], in_=scrA[:], func=AF.Exp, scale=-1.0), reads=[r_scrA], writes=[rX[1]])
                        for m in range(4):
                            S.op("act", lambda e: e.activation(out=scrB[:, m, :], in_=bb[:, m * 128:(m + 1) * 128], func=AF.Relu, bias=sc["gc"][:, cm_[m], m:m + 1], scale=-1.0),
                                 reads=[rbb, r_sc], writes=[r_scrB])
                        S.op("act", lambda e: e.activation(out=Y[1][:], in_=scrB[:], func=AF.Exp, scale=-1.0), reads=[r_scrB], writes=[rY[1]])
                        yield
                        for m in range(4):
                            d = m // 2
                            S.op("dve", lambda e: e.scalar_tensor_tensor(out=X[0][:, m, :], in0=X[1][:, m, :], scalar=sc["negb"][:, cm_[m], m:m + 1], in1=Gm[:, d, :],
                                                                         op0=ALU.mult, op1=ALU.mult), reads=[rX[1], r_sc, rGm], writes=[rX[0]])
                        for d in range(2):
                            S.op("pool", lambda e: e.tensor_tensor(out=AT[:, 2 * d:2 * d + 2, :], in0=Y[1][:, 2 * d:2 * d + 2, :],
                                                                   in1=QKm[:, d:d + 1, :].broadcast_to([128, 2, 128]), op=ALU.mult), reads=[rQKm, rY[1]], writes=[rAT])
                        yield
                        bk, rb = self.bank()
                        for m in range(4):
                            S.op("pe", lambda e: e.matmul(bk[:, m * 128:(m + 1) * 128], lhsT=X[0][:, m, :], rhs=self.identb[:], start=True, stop=True),
                                 reads=[rX[0], self.r_ident], writes=[rb], same_ok=True)
                        S.op("act", lambda e: e.copy(Y0[:], pv4(bk)), reads=[rb], writes=[rY0])
                        yield
                        S.op("dve", lambda e: e.tensor_tensor(out=X[1][:], in0=X[0][:], in1=b4(bd16), op=ALU.mult), reads=[rX[0], r_msk, rAT], writes=[rX[1]])
                        S.op("dve", lambda e: e.tensor_tensor(out=Y[1][:], in0=Y0[:], in1=b4(bd16), op=ALU.mult), reads=[rY0, r_msk, rAT], writes=[rY[1]])
                        S.op("dve", lambda e: e.tensor_tensor(out=W[:], in0=Y[1][:], in1=b4(self.identb), op=ALU.add), reads=[rY[1], self.r_ident], writes=[rW])
                        yield
                        cur = 1
                        for lev in range(3):
                            nxt = 1 - cur
                            bx, rbx = self.bank()
                            for m in range(4):
                                S.op("pe", lambda e: e.matmul(bx[:, m * 128:(m + 1) * 128], lhsT=Y[cur][:, m, :], rhs=X[cur][:, m, :], start=True, stop=True),
                                     reads=[rY[cur], rX[cur]], writes=[rbx], same_ok=True)
                            if lev < 2:
                                by, rby = self.bank()
                                for m in range(4):
                                    S.op("pe", lambda e: e.matmul(by[:, m * 128:(m + 1) * 128], lhsT=X[cur][:, m, :], rhs=Y[cur][:, m, :], start=True, stop=True),
                                         reads=[rY[cur], rX[cur]], writes=[rby], same_ok=True)
                            S.op("act", lambda e: e.copy(X[nxt][:], pv4(bx)), reads=[rbx], writes=[rX[nxt]])
                            if lev < 2:
                                S.op("dve", lambda e: e.tensor_copy(Y[nxt][:], pv4(by)), reads=[rby], writes=[rY[nxt]])
                            yield
                            bw, rbw = self.bank()
                            for m in range(4):
                                S.op("pe", lambda e: e.matmul(bw[:, m * 128:(m + 1) * 128], lhsT=X[nxt][:, m, :], rhs=W[:, m, :], start=True, stop=True),
                                     reads=[rX[nxt], rW], writes=[rbw], same_ok=True)
                            S.op("dve", lambda e: e.tensor_tensor(out=W[:], in0=pv4(bw), in1=W[:], op=ALU.add), reads=[rbw, rW], writes=[rW])
                            cur = nxt
                            yield
                        bk, rb = self.bank()
                        for m in range(4):
                            S.op("pe", lambda e: e.matmul(bk[:, m * 128:(m + 1) * 128], lhsT=W[:, m, :], rhs=self.identb[:], start=True, stop=True),
                                 reads=[rW, self.r_ident], writes=[rb], same_ok=True)
                        S.op("act", lambda e: e.copy(Tm[:], pv4(bk)), reads=[rb], writes=[rTm])
                        yield
                        for li, offm in enumerate((off16, off32, off64)):
                            S.op("dve", lambda e: e.tensor_tensor(out=X[0][:], in0=Y0[:], in1=b4(offm), op=ALU.mult), reads=[rY0, r_msk], writes=[rX[0]])
                            bz, rbz = self.bank()
                            for m in range(4):
                                S.op("pe", lambda e: e.matmul(bz[:, m * 128:(m + 1) * 128], lhsT=X[0][:, m, :], rhs=Tm[:, m, :], start=True, stop=True),
                                     reads=[rX[0], rTm], writes=[rbz], same_ok=True)
                            S.op("act", lambda e: e.copy(X[1][:], pv4(bz)), reads=[rbz], writes=[rX[1]])
                            yield
                            if li < 2:
                                bt, rbt = self.bank()
                                for m in range(4):
                                    S.op("pe", lambda e: e.matmul(bt[:, m * 128:(m + 1) * 128], lhsT=W[:, m, :], rhs=X[1][:, m, :], start=True, stop=True),
                                         reads=[rW, rX[1]], writes=[rbt], same_ok=True)
                            bw, rbw = self.bank()
                            for m in range(4):
                                S.op("pe", lambda e: e.matmul(bw[:, m * 128:(m + 1) * 128], lhsT=X[1][:, m, :], rhs=W[:, m, :], start=True, stop=True),
                                     reads=[rX[1], rW], writes=[rbw], same_ok=True)
                            if li < 2:
                                S.op("dve", lambda e: e.tensor_tensor(out=Tm[:], in0=pv4(bt), in1=Tm[:], op=ALU.add), reads=[rbt, rTm], writes=[rTm])
                            S.op("dve", lambda e: e.tensor_tensor(out=W[:], in0=pv4(bw), in1=W[:], op=ALU.add), reads=[rbw, rW], writes=[rW])
                            yield

                    def chain(step, B):
                        W, AT, rW, rAT = B["W"], B["AT"], B["rW"], B["rAT"]
                        cc = [order[0][step], order[1][step]]
                        cm_ = [cc[m // 2] for m in range(4)]
                        for m in range(4):
                            S.op("pool", lambda e: e.tensor_scalar(out=kdec[:, m, :], in0=kn[:, cm_[m], :], scalar1=sc["ecoef"][:, cm_[m], m:m + 1], scalar2=1.0, op0=ALU.mult, op1=ALU.mult),
                                 reads=[r_kn, r_sc], writes=[r_kdec])
                            S.op("pool", lambda e: e.tensor_scalar(out=bv[:, m, :], in0=vt[:, cm_[m], (m % 2) * 128:(m % 2 + 1) * 128], scalar1=sc["negb"][:, cm_[m], m:m + 1],
                                                                   scalar2=-1.0, op0=ALU.mult, op1=ALU.mult), reads=[r_vt, r_sc], writes=[r_bv])
                        yield
                        bks, rbks = self.bank()
                        for m in range(4):
                            S.op("pe", lambda e: e.matmul(bks[:, m * 128:(m + 1) * 128], lhsT=kT(cm_[m]), rhs=Sb[:, m, :], start=True, stop=True), reads=[r_qkT, r_Sb], writes=[rbks], same_ok=True)
                        bo1, rbo1 = self.bank()
                        for m in range(4):
                            S.op("pe", lambda e: e.matmul(bo1[:, m * 128:(m + 1) * 128], lhsT=qT(cm_[m]), rhs=Sb[:, m, :], start=True, stop=True), reads=[r_qkT, r_Sb], writes=[rbo1], same_ok=True)
                        for m in range(4):
                            S.op("dve", lambda e: e.scalar_tensor_tensor(out=Rm[:, m, :], in0=bks[:, m * 128:(m + 1) * 128], scalar=sc["negbe"][:, cm_[m], m:m + 1], in1=bv[:, m, :],
                                                                         op0=ALU.mult, op1=ALU.add), reads=[rbks, r_sc, r_bv], writes=[r_Rm])
                            S.op("act", lambda e: e.activation(out=ot[:, m, :], in_=bo1[:, m * 128:(m + 1) * 128], func=AF.Copy, scale=sc["e"][:, cm_[m], m:m + 1]),
                                 reads=[rbo1, r_sc], writes=[r_ot])
                        yield
                        bvn, rbvn = self.bank()
                        for m in range(4):
                            S.op("pe", lambda e: e.matmul(bvn[:, m * 128:(m + 1) * 128], lhsT=W[:, m, :], rhs=Rm[:, m, :], start=True, stop=True), reads=[rW, r_Rm], writes=[rbvn], same_ok=True)
                        S.op("act", lambda e: e.copy(vn[:], pv4(bvn)), reads=[rbvn], writes=[r_vn])
                        yield
                        bo2, rbo2 = self.bank()
                        for m in range(4):
                            S.op("pe", lambda e: e.matmul(bo2[:, m * 128:(m + 1) * 128], lhsT=AT[:, m, :], rhs=vn[:, m, :], start=True, stop=True), reads=[rAT, r_vn], writes=[rbo2], same_ok=True)
                        bst, rbst = self.bank()
                        for m in range(4):
                            S.op("pe", lambda e: e.matmul(bst[:, m * 128:(m + 1) * 128], lhsT=kdec[:, m, :], rhs=vn[:, m, :], start=True, stop=True), reads=[r_kdec, r_vn], writes=[rbst], same_ok=True)
                        yield
                        for m in range(4):
                            S.op("dve", lambda e: e.scalar_tensor_tensor(out=Sf[:, m, :], in0=Sf[:, m, :], scalar=sc["dl"][:, cm_[m], m:m + 1], in1=bst[:, m * 128:(m + 1) * 128],
                                                                         op0=ALU.mult, op1=ALU.add), reads=[r_Sf, r_sc, rbst], writes=[r_Sf])
                        S.op("act", lambda e: e.copy(Sb[:], Sf[:]), reads=[r_Sf], writes=[r_Sb])
                        S.op("dve", lambda e: e.tensor_tensor(out=ot[:], in0=pv4(bo2), in1=ot[:], op=ALU.add), reads=[rbo2, r_ot], writes=[r_ot])
                        for d in range(2):
                            c = cc[d]
                            src = ot[:, 2 * d:2 * d + 2, :]
                            dst = oacc[:, c, :].rearrange("p (j v) -> p j v", j=2)
                            if c not in written:
                                written.add(c)
                                S.op("pool", lambda e: e.tensor_copy(dst, src), reads=[r_ot], writes=[r_oacc[c]])
                            else:
                                S.op("pool", lambda e: e.tensor_tensor(out=dst, in0=dst, in1=src, op=ALU.add), reads=[r_ot, r_oacc[c]], writes=[r_oacc[c]])

                    def run_all(g):
                        for _ in g:
                            pass

                    pend = None
                    for p_ in range(9):
                        ga, gb = pre(2 * p_, ST[0]), pre(2 * p_ + 1, ST[1])
                        next(ga); next(gb)
                        if pend is not None:
                            run_all(chain(pend[0], ST[0]))
                        next(ga); next(gb)
                        if pend is not None:
                            run_all(chain(pend[1], ST[1]))
                        live = [True, True]
                        gens = [ga, gb]
                        while any(live):
                            for gi in range(2):
                                if live[gi]:
                                    try:
                                        next(gens[gi])
                                    except StopIteration:
                                        live[gi] = False
                        pend = (2 * p_, 2 * p_ + 1)
                    run_all(chain(pend[0], ST[0]))
                    run_all(chain(pend[1], ST[1]))
                    S.barrier()
                with ExitStack() as p4:
                    ogT = self.sb([128, 2, 512], BF16, es=p4); r_ogT = Res("ogT")
                    ogs = [self.sb([128, 256], F32, es=p4) for _ in range(2)]; r_ogs = [Res(), Res()]
                    ogbs = [self.sb([128, 256], BF16, es=p4) for _ in range(2)]; r_ogbs = [Res(), Res()]
                    zss = [self.sb([128, 256], F32, es=p4) for _ in range(2)]; r_zss = [Res(), Res()]
                    junk = self.sb([128, 128], F32, es=p4); r_junk = Res("junk")
                    sss4 = [self.sb([128, 2], F32, es=p4) for _ in range(2)]; r_sss4 = [Res(), Res()]
                    ogTs = [ogT, self.sb([128, 2, 512], BF16, es=p4)]; r_ogTs = [r_ogT, Res("ogT1")]
                    tile_of = [min(t // 4, 4) for t in range(18)]
                    zb = {}

                    def p4A(t):
                        ti = tile_of[t]
                        zs, r_zs = zss[t % 2], r_zss[t % 2]
                        ss, r_ss = sss4[t % 2], r_sss4[t % 2]
                        for j in range(2):
                            S.op("act", lambda e: e.activation(out=junk[:], in_=oacc[:, t, j * 128:(j + 1) * 128], func=AF.Square, accum_out=ss[:, j:j + 1]),
                                 reads=[r_oacc[t]], writes=[r_ss])
                        S.op("act", lambda e: e.activation(out=ss[:], in_=ss[:], func=AF.Sqrt, bias=epsc[:, 0:1], scale=1.0 / 128.0), reads=[r_ss, r_eps], writes=[r_ss])
                        bz, rbz = self.bank()
                        for kc in range(8):
                            S.op("pe", lambda e: e.matmul(bz[:, 0:256], lhsT=self.uT[:, kc, t * 128:(t + 1) * 128], rhs=wz[:, kc, :], start=(kc == 0), stop=(kc == 7)),
                                 reads=[r_wB, self.r_u[ti]], writes=[rbz], same_ok=True)
                        S.op("act", lambda e: e.activation(out=zs[:], in_=bz[:, 0:256], func=AF.Silu), reads=[rbz], writes=[r_zs])

                    def p4B(t):
                        ti = tile_of[t]
                        t0, n, lc = TT[ti]
                        tt = t - t0 // 128
                        og, r_og = ogs[t % 2], r_ogs[t % 2]
                        ogb, r_ogb = ogbs[t % 2], r_ogbs[t % 2]
                        zs, r_zs = zss[t % 2], r_zss[t % 2]
                        ss, r_ss = sss4[t % 2], r_sss4[t % 2]
                        ogT_, r_ogT_ = ogTs[ti % 2], r_ogTs[ti % 2]
                        S.op("dve", lambda e: e.reciprocal(out=ss[:], in_=ss[:]), reads=[r_ss], writes=[r_ss])
                        for j in range(2):
                            S.op("dve", lambda e: e.scalar_tensor_tensor(out=og[:, j * 128:(j + 1) * 128], in0=oacc[:, t, j * 128:(j + 1) * 128], scalar=ss[:, j:j + 1], in1=ngrep[:],
                                                                         op0=ALU.mult, op1=ALU.mult), reads=[r_oacc[t], r_ss, r_ngr], writes=[r_og])
                        S.op("dve", lambda e: e.tensor_tensor(out=ogb[:], in0=og[:], in1=zs[:], op=ALU.mult), reads=[r_og, r_zs], writes=[r_ogb])
                        b2, rb2 = self.bank()
                        for j in range(2):
                            S.op("pe", lambda e: e.matmul(b2[:, j * 128:(j + 1) * 128], lhsT=ogb[:, j * 128:(j + 1) * 128], rhs=self.identb[:], start=True, stop=True),
                                 reads=[r_ogb, self.r_ident], writes=[rb2], same_ok=True)
                        S.op("act", lambda e: e.copy(ogT_[:, :, tt * 128:(tt + 1) * 128], b2[:, 0:256].rearrange("p (j c) -> p j c", j=2)), reads=[rb2], writes=[r_ogT_])
                        if tt == n // 128 - 1:
                            for oc in range(8):
                                bk, rb = self.bank()
                                for j in range(2):
                                    S.op("pe", lambda e: e.matmul(bk[:, 0:n], lhsT=wo[:, j, oc * 128:(oc + 1) * 128], rhs=ogT_[:, j, 0:n], start=(j == 0), stop=(j == 1)),
                                         reads=[r_wC, r_ogT_], writes=[rb], same_ok=True)
                                hz = self.hT[:, oc, t0:t0 + n]
                                S.op("dve", lambda e: e.scalar_tensor_tensor(out=hz, in0=bk[:, 0:n], scalar=self.mcol(i, 2, oc, lc), in1=hz, op0=ALU.mult, op1=ALU.add),
                                     reads=[rb, self.r_h[ti][oc], self.r_mod[i]], writes=[self.r_h[ti][oc]])
                    n4 = 16 if self.skip_ctx else 18
                    p4A(0)
                    for t in range(n4):
                        if t + 1 < n4:
                            p4A(t + 1)
                        p4B(t)
                    S.barrier()


def build_program(depth_run=DEPTH, mixers=True, dbg=False):
    nc = bass.Bass("TRN2", target_bir_lowering=False)
    es = ExitStack()
    with es:
        kb = KB(nc, es, depth_run, mixers, dbg)
        kb.build()
        print("instructions", kb.S.ninst, "sems", kb.S.nsem, flush=True)
    return nc, kb


def _rope_tables(dim):
    n_freq = dim // 4
    inv = (10000.0 ** (-np.arange(n_freq, dtype=np.float32) / n_freq)).astype(np.float32)
    tok = np.arange(TL)
    row = (tok // 64).astype(np.float32)
    col = (tok % 64).astype(np.float32)
    ang = np.concatenate([row[:, None] * inv, col[:, None] * inv], -1).astype(np.float32)
    c = np.ones((dim // 2, T), np.float32)
    s = np.zeros((dim // 2, T), np.float32)
    c[:, :TL] = np.cos(ang).T
    s[:, :TL] = np.sin(ang).T
    return c, s


def _mla_host(inputs, shared, g):
    w_in = g("mla_w_in")[0]
    w_qb = g("mla_w_qb")[0]
    ev = np.arange(0, 32, 2)
    od = ev + 1
    cols = []
    for h in range(16):
        b = h * 96
        cols += list(range(b, b + 64)) + list(b + 64 + ev) + list(b + 64 + od) + list(b + 64 + ev) + list(b + 64 + od)
    shared["mla_wqx"] = np.ascontiguousarray(w_qb[:, cols])
    ia = list(1024 + ev) * 4
    ib = list(1024 + od) * 4
    shared["mla_wkr"] = np.ascontiguousarray(np.concatenate(
        [w_in[:, 0:64], w_in[:, ia], w_in[:, 0:64], w_in[:, ib]], axis=1))
    c, s = _rope_tables(32)
    one = np.ones((64, T), np.float32)
    shared["mla_qtab"] = np.concatenate([one, c, s, s, c], 0)
    shared["mla_kta"] = np.concatenate([one, c, -c, s, s], 0)
    shared["mla_ktb"] = np.concatenate([one, -s, s, c, c], 0)


def _diff_host(inputs, shared, g):
    w_in = g("diff_w_in")[0]
    ev = np.arange(0, 64, 2)
    od = ev + 1
    cols = []
    for h in range(8):
        for m in range(2):
            bq = h * 128 + m * 64
            bk = 1024 + h * 128 + m * 64
            cols += list(bq + ev) + list(bq + od) + list(bq + ev) + list(bq + od)
            cols += list(bk + ev) * 4
            cols += list(bk + od) * 4
    shared["diff_wx"] = np.ascontiguousarray(w_in[:, cols])
    shared["diff_wv"] = np.ascontiguousarray(w_in[:, 2048:3072])
    shared["diff_w_out"] = g("diff_w_out")[0]
    c, s = _rope_tables(64)
    shared["diff_qtab"] = np.concatenate([c, s, s, c], 0)
    shared["diff_kta"] = np.concatenate([c, -c, s, s], 0)
    shared["diff_ktb"] = np.concatenate([-s, s, c, c], 0)
    shared["diff_lam"] = np.ascontiguousarray(np.stack([g("diff_lambda_q1")[0], g("diff_lambda_k1")[0],
                                                        g("diff_lambda_q2")[0], g("diff_lambda_k2")[0]], axis=1))
    shared["diff_subln"] = g("diff_subln")[0].reshape(1, 128)


def _gla_host(inputs, shared, g):
    shared["gla_w_in"] = g("gla_w_in")[0]
    shared["gla_gw"] = np.ascontiguousarray(np.stack([g("gla_gate_w_fwd")[0], g("gla_gate_w_bwd")[0]], 0))
    shared["gla_gb"] = np.ascontiguousarray(np.stack([g("gla_gate_b_fwd")[0].reshape(4, 128), g("gla_gate_b_bwd")[0].reshape(4, 128)], 0))
    shared["gla_norm"] = g("gla_norm")[0].reshape(2, 128)
    shared["gla_w_out"] = g("gla_w_out")[0]


def _gdn_host(inputs, shared, g):
    w_in = g("gdn_w_in")[0]
    shared["gdn_w_in"] = w_in
    cols = []
    for kh in range(8):
        for base in (6144, 6160, 6176, 6192):
            cols += [base + 2 * kh, base + 2 * kh + 1]
    shared["gdn_wg"] = np.ascontiguousarray(w_in[:, cols])
    shared["gdn_conv"] = g("gdn_conv_w")[0].reshape(5, 32, 128)
    shared["gdn_hc"] = np.ascontiguousarray(np.concatenate([g("gdn_a_log_fwd")[0], g("gdn_a_log_bwd")[0],
                                                            g("gdn_dt_bias_fwd")[0], g("gdn_dt_bias_bwd")[0]]).reshape(1, 64))
    shared["gdn_norm"] = g("gdn_norm")[0].reshape(1, 128)
    shared["gdn_w_out"] = g("gdn_w_out")[0]

def make_in_maps(inputs):
    g = lambda k: np.ascontiguousarray(np.asarray(inputs[k], dtype=np.float32))
    shared = {
        "c_ctx": g("c_ctx").reshape(8, 128),
        "ada_w": g("ada_w"), "ada_b": g("ada_b").reshape(DEPTH, 48, 128),
        "ln1_g": g("ln1_g").reshape(DEPTH, 8, 128), "ln1_b": g("ln1_b").reshape(DEPTH, 8, 128),
        "ln2_g": g("ln2_g").reshape(DEPTH, 8, 128), "ln2_b": g("ln2_b").reshape(DEPTH, 8, 128),
        "mlp_w1": g("mlp_w1"), "mlp_w2": g("mlp_w2"),
        "mla_w_in": g("mla_w_in")[0], "mla_q_norm": g("mla_q_norm")[0].reshape(6, 128),
        "mla_kv_norm": g("mla_kv_norm")[0].reshape(2, 128),
        "mla_w_kvb": g("mla_w_kvb")[0], "mla_w_out": g("mla_w_out")[0],
    }
    _mla_host(inputs, shared, g)
    _diff_host(inputs, shared, g)
    _gla_host(inputs, shared, g)
    _gdn_host(inputs, shared, g)
    x, c, ctx = g("x"), g("c"), g("ctx")
    maps = []
    for b in range(8):
        m = dict(shared)
        m["x"] = x[b]
        m["ctx"] = ctx[b]
        m["c"] = c[b].reshape(8, 128)
        maps.append(m)
    return maps


def kernel(**inputs):
    nc, kb = build_program()
    maps = make_in_maps(inputs)
    res = run_bass_kernel_spmd(nc, maps, core_ids=list(range(8)))
    return np.stack([np.asarray(r["out"], dtype=np.float32) for r in res.results], axis=0)
```
